# Optimizing a Trainium2 kernel written in Bass

```python
import math
import numpy as np
import jax
import jax.numpy as jnp
from jax import lax


D_MODEL = 2048
BATCH = 2
SEQ = 4096
DEPTH = 1

D_MIX = D_MODEL
SSD_WIDTH = D_MIX // 2
ATT_WIDTH = D_MIX - SSD_WIDTH

SSD_HEAD_DIM = 64
SSD_HEADS = SSD_WIDTH // SSD_HEAD_DIM
SSD_GROUPS = 2
SSD_STATE = 128
SSD_CHUNK = 128
CONV_WIDTH = 4
CONV_CH = SSD_WIDTH + 2 * SSD_GROUPS * SSD_STATE

ATT_HEAD_DIM = 64
ATT_HEADS = ATT_WIDTH // ATT_HEAD_DIM
ATT_KV_HEADS = 4
ATT_GROUP = ATT_HEADS // ATT_KV_HEADS
KV_WIDTH = ATT_KV_HEADS * ATT_HEAD_DIM
CMP_BLOCK = 32
CMP_STRIDE = 16
CMP_HIDDEN = 256
SEL_BLOCK = 64
N_SELECT = 16
WINDOW = 512
Q_BLOCK = 128
N_BRANCH = 3

ROPE_THETA = 500000.0
ROPE_DIM = ATT_HEAD_DIM // 4

D_FF = -((-8 * D_MODEL) // (3 * 256)) * 256

NORM_EPS = 1e-6
NEG_INF = -1e30
FORCE_SCORE = 1e4

IN_WIDTHS = (SSD_WIDTH, CONV_CH, SSD_HEADS, ATT_WIDTH, KV_WIDTH, KV_WIDTH, KV_WIDTH, KV_WIDTH, KV_WIDTH, KV_WIDTH, ATT_HEADS * N_BRANCH)
D_IN = sum(IN_WIDTHS)

kernel_name = 'hymba_ssd_nsa_hybrid_block'


def rmsnorm(x, w):
    xf = x.astype(jnp.float32)
    xf = xf * lax.rsqrt(jnp.mean(xf * xf, axis=-1, keepdims=True) + NORM_EPS)
    return xf.astype(x.dtype) * w


def rope_tables(seq, dtype):
    inv = 1.0 / (ROPE_THETA ** (jnp.arange(0, ROPE_DIM, 2, dtype=jnp.float32) / ROPE_DIM))
    ang = jnp.arange(seq, dtype=jnp.float32)[:, None] * inv[None, :]
    return jnp.cos(ang).astype(dtype), jnp.sin(ang).astype(dtype)


def partial_rope(x, cos, sin):
    half = ROPE_DIM // 2
    c = cos[None, :, None, :]
    s = sin[None, :, None, :]
    x1 = x[..., :half]
    x2 = x[..., half:ROPE_DIM]
    return jnp.concatenate([x1 * c - x2 * s, x2 * c + x1 * s, x[..., ROPE_DIM:]], axis=-1)


def causal_depthwise_conv(u, w, b):
    y = lax.conv_general_dilated(u, w[:, None, :].astype(u.dtype), window_strides=(1,), padding=((CONV_WIDTH - 1, 0),), dimension_numbers=('NWC', 'WIO', 'NWC'), feature_group_count=u.shape[-1])
    return y + b


def ssd_chunked(xdt, adt, bm, cm):
    bsz, seq = xdt.shape[:2]
    nc = seq // SSD_CHUNK
    hg = SSD_HEADS // SSD_GROUPS
    x = xdt.reshape(bsz, nc, SSD_CHUNK, SSD_GROUPS, hg, SSD_HEAD_DIM)
    a = adt.reshape(bsz, nc, SSD_CHUNK, SSD_GROUPS, hg)
    b = bm.reshape(bsz, nc, SSD_CHUNK, SSD_GROUPS, SSD_STATE)
    c = cm.reshape(bsz, nc, SSD_CHUNK, SSD_GROUPS, SSD_STATE)
    a_cum = jnp.cumsum(a, axis=2)
    causal = jnp.tril(jnp.ones((SSD_CHUNK, SSD_CHUNK), dtype=bool))[:, :, None, None]
    seg = a_cum[:, :, :, None] - a_cum[:, :, None, :]
    decay = jnp.exp(jnp.where(causal, seg, -jnp.inf))
    cb = jnp.einsum('bclgn,bcsgn->bclsg', c, b)
    y_diag = jnp.einsum('bclsgh,bcsghp->bclghp', cb[..., None] * decay, x)
    decay_states = jnp.exp(a_cum[:, :, -1:] - a_cum)
    states = jnp.einsum('bclgn,bclghp->bcghpn', b, x * decay_states[..., None])
    chunk_decay = jnp.exp(a_cum[:, :, -1])

    def step(h, inp):
        st, dec = inp
        return h * dec[..., None, None] + st, h

    h0 = jnp.zeros_like(states[:, 0])
    _, prev = lax.scan(step, h0, (jnp.moveaxis(states, 1, 0), jnp.moveaxis(chunk_decay, 1, 0)))
    prev = jnp.moveaxis(prev, 0, 1)
    y_off = jnp.einsum('bclgn,bcghpn->bclghp', c, prev) * jnp.exp(a_cum)[..., None]
    return (y_diag + y_off).reshape(bsz, seq, SSD_HEADS, SSD_HEAD_DIM)


def ssd_mixer(z, xbc, dt_raw, conv_w, conv_b, dt_bias, a_log, d_skip, norm_w):
    bsz, seq = z.shape[:2]
    xbc = jax.nn.silu(causal_depthwise_conv(xbc, conv_w, conv_b))
    xs, bm, cm = jnp.split(xbc, [SSD_WIDTH, SSD_WIDTH + SSD_GROUPS * SSD_STATE], axis=-1)
    xh = xs.reshape(bsz, seq, SSD_HEADS, SSD_HEAD_DIM)
    dt = jax.nn.softplus(dt_raw.astype(jnp.float32) + dt_bias.astype(jnp.float32))
    a = -jnp.exp(a_log.astype(jnp.float32))
    y = ssd_chunked(xh * dt[..., None], a * dt, bm.reshape(bsz, seq, SSD_GROUPS, SSD_STATE), cm.reshape(bsz, seq, SSD_GROUPS, SSD_STATE))
    y = y + d_skip[:, None] * xh
    y = y.reshape(bsz, seq, SSD_WIDTH).astype(z.dtype)
    return rmsnorm(y * jax.nn.silu(z), norm_w)


def selection_overlap(n_cmp, n_sel):
    cs = np.arange(n_cmp)[:, None] * CMP_STRIDE
    ce = cs + CMP_BLOCK
    ss = np.arange(n_sel)[None, :] * SEL_BLOCK
    se = ss + SEL_BLOCK
    ov = np.clip(np.minimum(ce, se) - np.maximum(cs, ss), 0, None)
    return (ov / CMP_BLOCK).astype(np.float32)


def compress_tokens(kv, w1, w2, pe):
    bsz, seq = kv.shape[:2]
    n_cmp = (seq - CMP_BLOCK) // CMP_STRIDE + 1
    idx = np.arange(n_cmp)[:, None] * CMP_STRIDE + np.arange(CMP_BLOCK)[None, :]
    blocks = kv[:, idx] + pe[None, None, :, None, :]
    flat = jnp.swapaxes(blocks, 2, 3).reshape(bsz, n_cmp, ATT_KV_HEADS, CMP_BLOCK * ATT_HEAD_DIM)
    return jax.nn.silu(flat @ w1) @ w2


def nsa_mixer(q, kc, vc, ks, vs, kw, vw, gate_raw, cmp_w1_k, cmp_w2_k, cmp_w1_v, cmp_w2_v, cmp_pe_k, cmp_pe_v, cos, sin):
    bsz, seq = q.shape[:2]
    hd = ATT_HEAD_DIM
    scale = hd ** -0.5
    q = q.reshape(bsz, seq, ATT_HEADS, hd)
    kc, vc, ks, vs, kw, vw = [t.reshape(bsz, seq, ATT_KV_HEADS, hd) for t in (kc, vc, ks, vs, kw, vw)]
    q_grp = q.reshape(bsz, seq, ATT_KV_HEADS, ATT_GROUP, hd)
    q_rot = partial_rope(q, cos, sin).reshape(bsz, seq, ATT_KV_HEADS, ATT_GROUP, hd)
    ks = partial_rope(ks, cos, sin)
    kw = partial_rope(kw, cos, sin)
    t_pos = jnp.arange(seq)

    k_cmp = compress_tokens(kc, cmp_w1_k, cmp_w2_k, cmp_pe_k)
    v_cmp = compress_tokens(vc, cmp_w1_v, cmp_w2_v, cmp_pe_v)
    n_cmp = k_cmp.shape[1]
    cmp_end = jnp.arange(n_cmp) * CMP_STRIDE + CMP_BLOCK - 1
    cmp_mask = cmp_end[None, :] <= t_pos[:, None]
    s_cmp = jnp.einsum('bshgd,bihd->bhgsi', q_grp, k_cmp).astype(jnp.float32) * scale
    p_cmp = jax.nn.softmax(jnp.where(cmp_mask, s_cmp, NEG_INF), axis=-1) * cmp_mask
    o_cmp = jnp.einsum('bhgsi,bihd->bshgd', p_cmp.astype(v_cmp.dtype), v_cmp)

    n_sel = seq // SEL_BLOCK
    k_top = min(N_SELECT, n_sel)
    overlap = jnp.asarray(selection_overlap(n_cmp, n_sel))
    imp = jnp.einsum('bhgsi,ij->bhsj', p_cmp, overlap)
    blk = jnp.arange(n_sel)[None, :]
    cur = (t_pos // SEL_BLOCK)[:, None]
    imp = jnp.where((blk == 0) | (blk == cur) | (blk == cur - 1), FORCE_SCORE, imp)
    imp = jnp.where(blk <= cur, imp, -1.0)
    top_val, top_idx = lax.top_k(imp, k_top)
    top_valid = top_val >= 0.0

    ks_blocks = ks.reshape(bsz, n_sel, SEL_BLOCK, ATT_KV_HEADS, hd).transpose(0, 3, 1, 2, 4)
    vs_blocks = vs.reshape(bsz, n_sel, SEL_BLOCK, ATT_KV_HEADS, hd).transpose(0, 3, 1, 2, 4)
    kw_pad = jnp.pad(kw, ((0, 0), (WINDOW, 0), (0, 0), (0, 0)))
    vw_pad = jnp.pad(vw, ((0, 0), (WINDOW, 0), (0, 0), (0, 0)))
    gather_blocks = jax.vmap(jax.vmap(lambda blocks, ix: blocks[ix]))
    q_offs = jnp.arange(Q_BLOCK)
    w_offs = jnp.arange(WINDOW + Q_BLOCK)
    s_offs = jnp.arange(SEL_BLOCK)

    def query_block(qb):
        s0 = qb * Q_BLOCK
        tq = s0 + q_offs
        qblk = lax.dynamic_slice_in_dim(q_rot, s0, Q_BLOCK, axis=1)
        ib = lax.dynamic_slice_in_dim(top_idx, s0, Q_BLOCK, axis=2)
        vb = lax.dynamic_slice_in_dim(top_valid, s0, Q_BLOCK, axis=2)
        kg = gather_blocks(ks_blocks, ib)
        vg = gather_blocks(vs_blocks, ib)
        kpos = ib[..., None] * SEL_BLOCK + s_offs
        m_sel = (kpos <= tq[:, None, None]) & vb[..., None]
        s_sel = jnp.einsum('bqhgd,bhqjkd->bhgqjk', qblk, kg).astype(jnp.float32) * scale
        s_sel = jnp.where(m_sel[:, :, None], s_sel, NEG_INF).reshape(bsz, ATT_KV_HEADS, ATT_GROUP, Q_BLOCK, k_top * SEL_BLOCK)
        p_sel = jax.nn.softmax(s_sel, axis=-1).astype(vg.dtype)
        o_sel = jnp.einsum('bhgqn,bhqnd->bqhgd', p_sel, vg.reshape(bsz, ATT_KV_HEADS, Q_BLOCK, k_top * SEL_BLOCK, hd))
        kwb = lax.dynamic_slice_in_dim(kw_pad, s0, WINDOW + Q_BLOCK, axis=1)
        vwb = lax.dynamic_slice_in_dim(vw_pad, s0, WINDOW + Q_BLOCK, axis=1)
        kp = s0 - WINDOW + w_offs
        dist = tq[:, None] - kp[None, :]
        m_win = (dist >= 0) & (dist < WINDOW) & (kp[None, :] >= 0)
        s_win = jnp.einsum('bqhgd,bkhd->bhgqk', qblk, kwb).astype(jnp.float32) * scale
        p_win = jax.nn.softmax(jnp.where(m_win, s_win, NEG_INF), axis=-1).astype(vwb.dtype)
        o_win = jnp.einsum('bhgqk,bkhd->bqhgd', p_win, vwb)
        return o_sel, o_win

    o_sel, o_win = lax.map(query_block, jnp.arange(seq // Q_BLOCK))
    o_sel = jnp.moveaxis(o_sel, 0, 1).reshape(bsz, seq, ATT_KV_HEADS, ATT_GROUP, hd)
    o_win = jnp.moveaxis(o_win, 0, 1).reshape(bsz, seq, ATT_KV_HEADS, ATT_GROUP, hd)
    g = jax.nn.sigmoid(gate_raw.astype(jnp.float32)).reshape(bsz, seq, ATT_KV_HEADS, ATT_GROUP, N_BRANCH, 1).astype(q.dtype)
    o = g[..., 0, :] * o_cmp + g[..., 1, :] * o_sel + g[..., 2, :] * o_win
    return o.reshape(bsz, seq, ATT_WIDTH)


def setup_inputs(seed: int = 0) -> dict:
    key = jax.random.key(seed)
    k = jax.random.split(key, 24)
    f32 = jnp.float32
    L = DEPTH

    def dense(kk, shape, fan_in):
        return jax.random.normal(kk, shape, f32) * fan_in ** -0.5

    def gain(kk, shape):
        return 1.0 + 0.05 * jax.random.normal(kk, shape, f32)

    dt = jnp.exp(jax.random.uniform(k[5], (L, SSD_HEADS), f32, math.log(1e-3), math.log(1e-1)))
    return {
        'x': jax.random.normal(k[0], (BATCH, SEQ, D_MODEL), f32),
        'attn_norm_w': gain(k[1], (L, D_MODEL)),
        'w_in': dense(k[2], (L, D_MODEL, D_IN), D_MODEL),
        'conv_w': dense(k[3], (L, CONV_WIDTH, CONV_CH), CONV_WIDTH),
        'conv_b': 0.01 * jax.random.normal(k[4], (L, CONV_CH), f32),
        'dt_bias': dt + jnp.log(-jnp.expm1(-dt)),
        'a_log': jnp.log(jax.random.uniform(k[6], (L, SSD_HEADS), f32, 1.0, 16.0)),
        'd_skip': 1.0 + 0.1 * jax.random.normal(k[7], (L, SSD_HEADS), f32),
        'ssd_norm_w': gain(k[8], (L, SSD_WIDTH)),
        'cmp_w1_k': dense(k[9], (L, CMP_BLOCK * ATT_HEAD_DIM, CMP_HIDDEN), CMP_BLOCK * ATT_HEAD_DIM),
        'cmp_w2_k': dense(k[10], (L, CMP_HIDDEN, ATT_HEAD_DIM), CMP_HIDDEN),
        'cmp_w1_v': dense(k[11], (L, CMP_BLOCK * ATT_HEAD_DIM, CMP_HIDDEN), CMP_BLOCK * ATT_HEAD_DIM),
        'cmp_w2_v': dense(k[12], (L, CMP_HIDDEN, ATT_HEAD_DIM), CMP_HIDDEN),
        'cmp_pe_k': 0.1 * jax.random.normal(k[13], (L, CMP_BLOCK, ATT_HEAD_DIM), f32),
        'cmp_pe_v': 0.1 * jax.random.normal(k[14], (L, CMP_BLOCK, ATT_HEAD_DIM), f32),
        'w_out': dense(k[15], (L, D_MIX, D_MODEL), D_MIX),
        'ffn_norm_w': gain(k[16], (L, D_MODEL)),
        'w_gate': dense(k[17], (L, D_MODEL, D_FF), D_MODEL),
        'w_up': dense(k[18], (L, D_MODEL, D_FF), D_MODEL),
        'w_down': dense(k[19], (L, D_FF, D_MODEL), D_FF),
        'final_norm_w': gain(k[20], (D_MODEL,)),
    }


def reference(x, attn_norm_w, w_in, conv_w, conv_b, dt_bias, a_log, d_skip, ssd_norm_w, cmp_w1_k, cmp_w2_k, cmp_w1_v, cmp_w2_v, cmp_pe_k, cmp_pe_v, w_out, ffn_norm_w, w_gate, w_up, w_down, final_norm_w):
    cos, sin = rope_tables(x.shape[1], x.dtype)
    split_at = np.cumsum(IN_WIDTHS)[:-1].tolist()
    h = x
    for l in range(DEPTH):
        u = rmsnorm(h, attn_norm_w[l])
        z, xbc, dt_raw, q, kc, vc, ks, vs, kw, vw, gate_raw = jnp.split(u @ w_in[l], split_at, axis=-1)
        y_ssd = ssd_mixer(z, xbc, dt_raw, conv_w[l], conv_b[l], dt_bias[l], a_log[l], d_skip[l], ssd_norm_w[l])
        y_att = nsa_mixer(q, kc, vc, ks, vs, kw, vw, gate_raw, cmp_w1_k[l], cmp_w2_k[l], cmp_w1_v[l], cmp_w2_v[l], cmp_pe_k[l], cmp_pe_v[l], cos, sin)
        mixed = jnp.concatenate([y_ssd.astype(h.dtype), y_att.astype(h.dtype)], axis=-1)
        h = h + mixed @ w_out[l]
        v = rmsnorm(h, ffn_norm_w[l])
        h = h + (jax.nn.silu(v @ w_gate[l]) * (v @ w_up[l])) @ w_down[l]
    return rmsnorm(h, final_norm_w)
```

```python
import contextlib
import os
import numpy as np
import ml_dtypes
import concourse.bass as bass
import concourse.mybir as mybir
from concourse.bass_utils import run_bass_kernel_spmd

F32 = mybir.dt.float32
BF16 = mybir.dt.bfloat16
U8 = mybir.dt.uint8
ALU = mybir.AluOpType
AF = mybir.ActivationFunctionType
AX = mybir.AxisListType

ENGS = ("pe", "act", "dve", "pool", "sp")
EPS = 1e-6


class _Op:
    __slots__ = ("eng", "fn", "dma", "deps", "idx", "signal", "val", "sem")


class Sched:
    def __init__(self, nc, n_dma_sems=8):
        self.nc = nc
        self.ops = []
        self.last_w = {}
        self.readers = {}
        self.n_dma_sems = n_dma_sems
        self.bar = None
        self.bank_of = {}

    def op(self, eng, fn, reads=(), writes=(), dma=False):
        o = _Op()
        o.eng, o.fn, o.dma = eng, fn, dma
        o.idx = len(self.ops)
        o.signal = False
        deps = {}
        if self.bar is not None:
            deps[self.bar] = "raw"
        for r in reads:
            w = self.last_w.get(r)
            if w is not None:
                deps[w] = "raw"
        for w_ in writes:
            w = self.last_w.get(w_)
            if w is not None:
                deps[w] = "raw"
            for r in self.readers.get(w_, ()):
                if r not in deps:
                    deps[r] = "war"
        for r in reads:
            self.readers.setdefault(r, []).append(o.idx)
        for w_ in writes:
            self.last_w[w_] = o.idx
            self.readers[w_] = []
        banks = set()
        for r in tuple(reads) + tuple(writes):
            banks.update(self.bank_of.get(r, ()))
        for b in banks:
            key = ("bank", b)
            w = self.last_w.get(key)
            if w is not None and w not in deps:
                deps[w] = "bank"
            self.last_w[key] = o.idx
        deps.pop(o.idx, None)
        o.deps = deps
        self.ops.append(o)
        return o

    def pe(self, fn, reads=(), writes=()):
        return self.op("pe", fn, reads, writes)

    def act(self, fn, reads=(), writes=()):
        return self.op("act", fn, reads, writes)

    def dve(self, fn, reads=(), writes=()):
        return self.op("dve", fn, reads, writes)

    def pool(self, fn, reads=(), writes=()):
        return self.op("pool", fn, reads, writes)

    def dma(self, q, out, in_, reads=(), writes=()):
        return self.op(q, lambda e: e.dma_start(out=out, in_=in_), reads, writes, dma=True)

    def barrier(self, out, in_):
        allres = set(self.last_w.keys()) | set(self.readers.keys())
        o = self.op("sp", lambda e: e.dma_start(out=out, in_=in_), reads=(), writes=tuple(allres), dma=True)
        self.bar = o.idx
        self.last_w = {}
        self.readers = {}
        return o

    def emit(self, sems, block, final_wait_eng="sp"):
        ops = self.ops
        need = [False] * len(ops)
        for o in ops:
            for d, kind in o.deps.items():
                do = ops[d]
                if do.dma:
                    continue
                if do.eng == o.eng and not o.dma:
                    if do.eng == "pe" or kind in ("war", "bank"):
                        continue
                need[d] = True
        cnt = {e: 0 for e in ENGS}
        dcnt = {}
        dval = {}
        per_eng = {e: [] for e in ENGS}
        for o in ops:
            per_eng[o.eng].append(o)
            if o.dma:
                k = dcnt.get(o.eng, 0)
                dcnt[o.eng] = k + 1
                key = ("dma", o.eng, k % self.n_dma_sems)
                o.sem = key
                dval[key] = dval.get(key, 0) + 16
                o.val = dval[key]
            elif need[o.idx]:
                cnt[o.eng] += 1
                o.val = cnt[o.eng]
                o.sem = o.eng
                o.signal = True
        self.stats = {e: len(per_eng[e]) for e in ENGS}
        self.stats["signals"] = dict(cnt)

        def run(engname, e):
            waited = {}
            for o in per_eng[engname]:
                wl = {}
                for d, kind in o.deps.items():
                    do = ops[d]
                    if do.dma:
                        wl[do.sem] = max(wl.get(do.sem, 0), do.val)
                        continue
                    if do.eng == o.eng and not o.dma:
                        if do.eng == "pe" or kind in ("war", "bank"):
                            continue
                    wl[do.sem] = max(wl.get(do.sem, 0), do.val)
                if o.dma and o.val > 16:
                    wl[o.sem] = max(wl.get(o.sem, 0), o.val - 16)
                for s, v in wl.items():
                    if waited.get(s, 0) >= v:
                        continue
                    waited[s] = v
                    e.wait_ge(sems[s], v)
                ins = o.fn(e)
                if o.dma:
                    ins.then_inc(sems[o.sem], 16)
                elif o.signal:
                    ins.then_inc(sems[o.sem], 1)
            if engname == final_wait_eng:
                for key, v in dval.items():
                    if waited.get(key, 0) < v:
                        e.wait_ge(sems[key], v)
                for en in ("pe", "act", "dve", "pool"):
                    if cnt[en] > 0 and waited.get(en, 0) < cnt[en]:
                        e.wait_ge(sems[en], cnt[en])

        @block.tensor
        def _(e):
            run("pe", e)

        @block.scalar
        def _(e):
            run("act", e)

        @block.vector
        def _(e):
            run("dve", e)

        @block.gpsimd
        def _(e):
            run("pool", e)

        @block.sync
        def _(e):
            run("sp", e)


def make_sems(nc, stack, n_dma_sems=8, queues=("sp", "pool", "act")):
    sems = {}
    for e in ("pe", "act", "dve", "pool"):
        sems[e] = stack.enter_context(nc.semaphore("s_" + e))
    for q in queues:
        for i in range(n_dma_sems):
            sems[("dma", q, i)] = stack.enter_context(nc.semaphore("d_%s_%d" % (q, i)))
    return sems


_DTSZ = {F32: 4, BF16: 2, U8: 1}


class Arena:
    def __init__(self, ar, size):
        self.ar, self.size, self.off = ar, size, 0

    def reset(self, off=0):
        self.off = off

    def alloc(self, shape, dtype, parts=128):
        n = int(np.prod(shape)) * _DTSZ[dtype]
        off = (self.off + 63) // 64 * 64
        assert off + n <= self.size, ("arena overflow", off, n, self.size)
        self.off = off + n
        ap = self.ar[0:parts, off:off + n].bitcast(dtype)
        if len(shape) > 1:
            names = [chr(ord("a") + i) for i in range(len(shape))]
            pat = "p (%s) -> p %s" % (" ".join(names), " ".join(names))
            ap = ap.rearrange(pat, **{nm: int(s) for nm, s in zip(names, shape)})
        return ap


D = 2048
FF = 5632
TOK = 1024
NT = TOK // 128
KC = D // 128
FC = FF // 128
SSDW = 1024


def build_tail():
    nc = bass.Bass("TRN2", target_bir_lowering=False)
    x = nc.dram_tensor("x_own", [TOK, D], F32, kind="ExternalInput").ap()
    mix = nc.dram_tensor("mix", [TOK, D], F32, kind="ExternalInput").ap()
    w_out = nc.dram_tensor("w_out", [D, D], F32, kind="ExternalInput").ap()
    w_gate = nc.dram_tensor("w_gate", [D, FF], F32, kind="ExternalInput").ap()
    w_up = nc.dram_tensor("w_up", [D, FF], F32, kind="ExternalInput").ap()
    w_down = nc.dram_tensor("w_down", [FF, D], F32, kind="ExternalInput").ap()
    nw = nc.dram_tensor("nw", [3, D], F32, kind="ExternalInput").ap()
    ident = nc.dram_tensor("ident", [128, 128], F32, kind="ExternalInput").ap()
    out = nc.dram_tensor("out", [TOK, D], F32, kind="ExternalOutput").ap()
    h_d = nc.dram_tensor("h_d", [TOK, D], F32, kind="Internal").ap()
    dummy = nc.dram_tensor("dummy_bar", [2, 64], F32, kind="Internal").ap()
    with contextlib.ExitStack() as st:
        ASZ = 200 * 1024
        ar = st.enter_context(nc.sbuf_tensor("arena", [128, ASZ], U8))
        A = Arena(ar, ASZ)
        pbig = st.enter_context(nc.psum_tensor("pbig", [128, 8, 512], F32))
        sems = make_sems(nc, st)
        block = st.enter_context(nc.Block())
        S = Sched(nc)
        tail_body(nc, S, A, pbig, x, mix, w_out, w_gate, w_up, w_down, nw, ident, out, h_d, dummy)
        S.emit(sems, block)
    return nc


def rms_rstd(S, src, n, ss, sq, tag, rd, wr_extra=()):
    S.act(lambda e: e.activation(sq, src, AF.Square, accum_out=ss), reads=rd, writes=[tag + "ss", tag + "sq"])
    S.act(lambda e: e.activation(ss, ss, AF.Sqrt, scale=1.0 / n, bias=EPS_AP[0]), reads=[tag + "ss"], writes=[tag + "ss"])
    S.dve(lambda e: e.reciprocal(ss, ss), reads=[tag + "ss"], writes=[tag + "ss"])


EPS_AP = [None]


def tail_body(nc, S, A, pbig, x, mix, w_out, w_gate, w_up, w_down, nw, ident, out, h_d, dummy):
    bk = {"ptr": (4, 5)}
    for i_ in range(8):
        bk["pacc%d" % i_] = (i_,)
        bk["pd%d" % i_] = (i_,)
    for i_ in range(2):
        bk["pg%d" % i_] = (i_ * 4, i_ * 4 + 1)
        bk["pu%d" % i_] = (i_ * 4 + 2, i_ * 4 + 3)
    S.bank_of = bk
    identb = A.alloc([128], BF16)
    nwb = A.alloc([3, D], F32)
    epsb = A.alloc([1], F32)
    ss = A.alloc([4], F32)
    EPS_AP[0] = epsb
    S.pool(lambda e: e.memset(epsb, EPS), writes=["eps"])
    S.dma("pool", identb, ident, writes=["ident"])
    S.dma("sp", nwb, nw.rearrange("a b -> (a b)").partition_broadcast(128).rearrange("p (a b) -> p a b", a=3), writes=["nwb"])
    vT = A.alloc([KC, TOK], BF16)
    base_persist = A.off

    wo = A.alloc([KC, D], BF16)
    for half in range(2):
        S.dma("pool", wo[:, half * 8:(half + 1) * 8, :],
              w_out[half * 1024:(half + 1) * 1024, :].rearrange("(k p) n -> p k n", p=128), writes=["wo%d" % half])
    xt = [A.alloc([D], F32) for _ in range(2)]
    mt = [A.alloc([D], F32) for _ in range(2)]
    sq = A.alloc([D], F32)
    mb = A.alloc([D], BF16)
    mT = A.alloc([KC, 128], BF16)
    hs = [A.alloc([D], F32) for _ in range(2)]
    vb = A.alloc([D], BF16)
    pacc = pbig[:, 0:4, :]
    ptr_all = pbig[:, 4:6, :].rearrange("p a b -> p (a b)").bitcast(BF16)
    ptr = ptr_all.rearrange("p (k n) -> p k n", k=KC)

    def loads(tt):
        b = tt % 2
        S.dma("sp", xt[b], x[tt * 128:(tt + 1) * 128, :], writes=["xt%d" % b])
        S.dma("sp", mt[b], mix[tt * 128:(tt + 1) * 128, :], writes=["mt%d" % b])

    loads(0)
    for tt in range(NT):
        b = tt % 2
        if tt + 1 < NT:
            loads(tt + 1)
        rms_rstd(S, mt[b][:, 0:SSDW], SSDW, ss[:, 0:1], sq[:, 0:SSDW], "a", ["mt%d" % b, "eps"])
        S.dve(lambda e, b=b: e.scalar_tensor_tensor(mb[:, 0:SSDW], mt[b][:, 0:SSDW], ss[:, 0:1], nwb[:, 0, 0:SSDW], ALU.mult, ALU.mult),
              reads=["mt%d" % b, "ass", "nwb"], writes=["mb0"])
        S.pool(lambda e, b=b: e.tensor_copy(mb[:, SSDW:D], mt[b][:, SSDW:D]), reads=["mt%d" % b], writes=["mb1"])
        for kc in range(KC):
            S.pe(lambda e, kc=kc: e.transpose(ptr[:, kc, :], mb[:, kc * 128:(kc + 1) * 128], identb),
                 reads=["mb0", "mb1", "ident"], writes=["ptr"])
        S.act(lambda e: e.copy(mT[:, 0:8, :], ptr[:, 0:8, :]), reads=["ptr"], writes=["mTa"])
        S.dve(lambda e: e.tensor_copy(mT[:, 8:16, :], ptr[:, 8:16, :]), reads=["ptr"], writes=["mTb"])
        for cb in range(4):
            for kc in range(KC):
                S.pe(lambda e, cb=cb, kc=kc: e.matmul(pacc[:, cb, :], mT[:, kc, :], wo[:, kc, cb * 512:(cb + 1) * 512],
                                                       start=(kc == 0), stop=(kc == KC - 1)),
                     reads=["mTa", "mTb", "wo0", "wo1"], writes=["pacc%d" % cb])
            S.dve(lambda e, cb=cb, b=b: e.tensor_tensor(hs[b][:, cb * 512:(cb + 1) * 512], pacc[:, cb, :], xt[b][:, cb * 512:(cb + 1) * 512], ALU.add),
                  reads=["pacc%d" % cb, "xt%d" % b], writes=["hs%d_%d" % (b, cb)])
        hres = ["hs%d_%d" % (b, cb) for cb in range(4)]
        S.dma("sp", h_d[tt * 128:(tt + 1) * 128, :], hs[b], reads=hres, writes=["h_d%d" % tt])
        rms_rstd(S, hs[b], D, ss[:, 1:2], sq, "b", hres + ["eps"])
        S.dve(lambda e, b=b: e.scalar_tensor_tensor(vb, hs[b], ss[:, 1:2], nwb[:, 1, :], ALU.mult, ALU.mult),
              reads=hres + ["bss", "nwb"], writes=["vb"])
        for kc in range(KC):
            S.pe(lambda e, kc=kc: e.transpose(ptr[:, kc, :], vb[:, kc * 128:(kc + 1) * 128], identb),
                 reads=["vb", "ident"], writes=["ptr"])
        S.act(lambda e, tt=tt: e.copy(vT[:, 0:8, tt * 128:(tt + 1) * 128], ptr[:, 0:8, :]), reads=["ptr"], writes=["vT%da" % tt])
        S.dve(lambda e, tt=tt: e.tensor_copy(vT[:, 8:16, tt * 128:(tt + 1) * 128], ptr[:, 8:16, :]), reads=["ptr"], writes=["vT%db" % tt])

    S.barrier(dummy[1:2, :], ident[0:1, 0:64])
    A.reset(base_persist)
    hT = A.alloc([FC, TOK], BF16)
    base_b = A.off
    WB = 256
    NB = FF // WB
    wg = [A.alloc([KC, WB], BF16) for _ in range(2)]
    wu = [A.alloc([KC, WB], BF16) for _ in range(2)]
    sg = [A.alloc([TOK], F32) for _ in range(2)]
    for blk in range(NB):
        b = blk % 2
        S.dma("pool", wg[b], w_gate[:, blk * WB:(blk + 1) * WB].rearrange("(k p) n -> p k n", p=128), writes=["wg%d" % b])
        S.dma("pool", wu[b], w_up[:, blk * WB:(blk + 1) * WB].rearrange("(k p) n -> p k n", p=128), writes=["wu%d" % b])
        for j in range(WB // 128):
            fc = blk * (WB // 128) + j
            pb = fc % 2
            pg = pbig[:, pb * 4:pb * 4 + 2, :]
            pu = pbig[:, pb * 4 + 2:pb * 4 + 4, :]
            for hf in range(2):
                for kc in range(KC):
                    S.pe(lambda e, b=b, j=j, hf=hf, kc=kc, pg=pg: e.matmul(pg[:, hf, :], wg[b][:, kc, j * 128:(j + 1) * 128], vT[:, kc, hf * 512:(hf + 1) * 512],
                                                                         start=(kc == 0), stop=(kc == KC - 1)),
                         reads=["wg%d" % b, "vT"], writes=["pg%d" % pb])
            for hf in range(2):
                for kc in range(KC):
                    S.pe(lambda e, b=b, j=j, hf=hf, kc=kc, pu=pu: e.matmul(pu[:, hf, :], wu[b][:, kc, j * 128:(j + 1) * 128], vT[:, kc, hf * 512:(hf + 1) * 512],
                                                                         start=(kc == 0), stop=(kc == KC - 1)),
                         reads=["wu%d" % b, "vT"], writes=["pu%d" % pb])
            S.act(lambda e, pb=pb, pg=pg: e.activation(sg[pb], pg.rearrange("p a b -> p (a b)"), AF.Silu), reads=["pg%d" % pb], writes=["sg%d" % pb])
            S.dve(lambda e, pb=pb, pu=pu, fc=fc: e.tensor_tensor(hT[:, fc, :], sg[pb], pu.rearrange("p a b -> p (a b)"), ALU.mult),
                  reads=["sg%d" % pb, "pu%d" % pb], writes=["hT%d" % fc])

    S.barrier(dummy[1:2, :], ident[0:1, 0:64])
    A.reset(base_b)
    HK = 22
    wd = [A.alloc([HK, 512], BF16) for _ in range(2)]
    hl = [A.alloc([512], F32) for _ in range(2)]
    ys = [A.alloc([512], F32) for _ in range(2)]
    it = 0
    for r in range(4):
        for hf in range(2):
            b = (r * 2 + hf) % 2
            S.dma("pool", wd[b], w_down[hf * HK * 128:(hf + 1) * HK * 128, r * 512:(r + 1) * 512].rearrange("(k p) n -> p k n", p=128), writes=["wd%d" % b])
            for tt in range(NT):
                for k in range(HK):
                    kk = hf * HK + k
                    S.pe(lambda e, b=b, tt=tt, k=k, kk=kk: e.matmul(pbig[:, tt, :], hT[:, kk, tt * 128:(tt + 1) * 128], wd[b][:, k, :],
                                                                    start=(kk == 0), stop=(kk == FC - 1)),
                         reads=["wd%d" % b, "hT"], writes=["pd%d" % tt])
        for tt in range(NT):
            b = it % 2
            it += 1
            S.dma("sp", hl[b], h_d[tt * 128:(tt + 1) * 128, r * 512:(r + 1) * 512], reads=["h_d%d_%d" % (tt, r)], writes=["hl%d" % b])
            S.dve(lambda e, b=b, tt=tt: e.tensor_tensor(ys[b], pbig[:, tt, :], hl[b], ALU.add), reads=["pd%d" % tt, "hl%d" % b], writes=["ys%d" % b])
            S.dma("sp", h_d[tt * 128:(tt + 1) * 128, r * 512:(r + 1) * 512], ys[b], reads=["ys%d" % b], writes=["h_d%d_%d" % (tt, r)])

    S.barrier(dummy[1:2, :], ident[0:1, 0:64])
    A.reset(base_persist)
    yt = [A.alloc([D], F32) for _ in range(2)]
    ot = [A.alloc([D], F32) for _ in range(2)]
    sq2 = A.alloc([D], F32)
    for tt in range(NT):
        b = tt % 2
        S.dma("sp", yt[b], h_d[tt * 128:(tt + 1) * 128, :], writes=["yt%d" % b])
        rms_rstd(S, yt[b], D, ss[:, 2:3], sq2, "c", ["yt%d" % b, "eps"])
        S.dve(lambda e, b=b: e.scalar_tensor_tensor(ot[b], yt[b], ss[:, 2:3], nwb[:, 2, :], ALU.mult, ALU.mult),
              reads=["yt%d" % b, "css", "nwb"], writes=["ot%d" % b])
        S.dma("sp", out[tt * 128:(tt + 1) * 128, :], ot[b], reads=["ot%d" % b])


_CACHE = {}


def run_tail(x, mixed, ssd_norm_w, w_out, ffn_norm_w, w_gate, w_up, w_down, final_norm_w):
    if "tail" not in _CACHE:
        _CACHE["tail"] = build_tail()
    nc = _CACHE["tail"]
    nwv = np.ones((3, D), np.float32)
    nwv[0, :SSDW] = ssd_norm_w
    nwv[1] = ffn_norm_w
    nwv[2] = final_norm_w
    ident = np.eye(128, dtype=np.float32)
    in_maps = []
    for c in range(8):
        b, g = c // 4, c % 4
        in_maps.append({
            "x_own": np.ascontiguousarray(x[b, g * TOK:(g + 1) * TOK]),
            "mix": np.ascontiguousarray(mixed[b, g * TOK:(g + 1) * TOK]),
            "w_out": w_out, "w_gate": w_gate, "w_up": w_up, "w_down": w_down,
            "nw": nwv, "ident": ident,
        })
    res = run_bass_kernel_spmd(nc, in_maps, core_ids=list(range(8)))
    outp = np.empty((2, 4096, D), np.float32)
    for c in range(8):
        b, g = c // 4, c % 4
        outp[b, g * TOK:(g + 1) * TOK] = res.results[c]["out"]
    return outp


SEQ = 4096
NTT = SEQ // 128
NEG = -30000.0
NA = 400
NB_ = 384
NFM = 640
SCALE = 0.125


def build_mix():
    nc = bass.Bass("TRN2", target_bir_lowering=False)
    di = lambda n, s, d=F32: nc.dram_tensor(n, s, d, kind="ExternalInput").ap()
    T = dict(
        xb=di("xb", [SEQ, D]), anw=di("anw", [D]), w_tm=di("w_tm", [D, NA + NB_]), w_fm=di("w_fm", [D, NFM]),
        convw=di("convw", [128, 16]), convb=di("convb", [128, 4]), hp=di("hp", [12]), rope=di("rope", [128, NTT * 16]),
        w1k=di("w1k", [64, 32 * 256]), w1v=di("w1v", [64, 32 * 256]), w2k=di("w2k", [256, 64]), w2v=di("w2v", [256, 64]),
        pek=di("pek", [64, 32]), pev=di("pev", [64, 32]), ident=di("ident", [128, 128]), emat=di("emat", [64, SEQ]),
        tric=di("tric", [128, 128]), tria=di("tria", [128, 128]), cmask=di("cmask", [256, SEQ]), ovl=di("ovl", [256, 64]),
        utri=di("utri", [128, 128]),
    )
    T["mixo"] = nc.dram_tensor("mixo", [SEQ, 512], F32, kind="ExternalOutput").ap()
    T["qr_d"] = nc.dram_tensor("qr_d", [64, 4, SEQ], BF16, kind="Internal").ap()
    T["qu_d"] = nc.dram_tensor("qu_d", [64, 4, SEQ], BF16, kind="Internal").ap()
    T["dummy"] = nc.dram_tensor("dummy_bar", [2, 64], F32, kind="Internal").ap()
    with contextlib.ExitStack() as st:
        ASZ = 207 * 1024
        ar = st.enter_context(nc.sbuf_tensor("arena", [128, ASZ], U8))
        A = Arena(ar, ASZ)
        pbig = st.enter_context(nc.psum_tensor("pbig", [128, 8, 512], F32))
        sems = make_sems(nc, st)
        block = st.enter_context(nc.Block())
        S = Sched(nc)
        mix_body(nc, S, A, pbig, T)
        S.emit(sems, block)
    return nc


def pbf(pb, lo, hi):
    return pb[:, lo:hi, :].rearrange("p a b -> p (a b)").bitcast(BF16)


def mix_body(nc, S, A, pbig, T):
    bar = lambda: S.barrier(T["dummy"][1:2, :], T["ident"][0:1, 0:64])
    bk = {"ptr": (0,), "ptr2": (6,), "psA": (1,), "psB": (2,), "psF0": (3,), "psF1": (4,), "psS_a": (5,), "psS_r": (5,), "psS_c": (5,),
          "psN": (6,), "psY_o": (7,), "psY_d": (7,), "pk": (5,), "pv0": (6,), "pv1": (7,), "pnt": (7,), "posel": (6,), "powin": (7,)}
    for i_ in range(4):
        bk["pbias%d" % i_] = (4,)
        bk["phid%d" % i_] = (i_,)
        bk["psw%d" % i_] = (3 + i_ // 2,)
    for a_ in range(2):
        bk["po%d" % a_] = (4 + a_,)
        for b_ in range(2):
            bk["psc%d_%d" % (a_, b_)] = (a_ * 2 + b_,)
    for i_ in range(3):
        bk["pss%d" % i_] = (i_,)
    S.bank_of = bk
    identb = A.alloc([128], BF16)
    identf = A.alloc([128], F32)
    utri = A.alloc([128], F32)
    tricb = A.alloc([128], BF16)
    triab = A.alloc([128], BF16)
    epsb = A.alloc([1], F32)
    oneb = A.alloc([1], F32)
    EPS_AP[0] = epsb
    KsA = A.alloc([SEQ], BF16)
    KwT = A.alloc([SEQ], BF16)
    kcvcT = A.alloc([SEQ], BF16)
    VsA = A.alloc([NTT, 65], BF16)
    VwA = A.alloc([NTT, 65], BF16)
    gate = A.alloc([NTT, 12], F32)
    attO = A.alloc([NTT, 256], F32)
    nmT = A.alloc([SEQ], BF16)
    hpb = A.alloc([12], F32)
    base_persist = A.off
    S.pool(lambda e: e.memset(epsb, EPS), writes=["eps"])
    S.pool(lambda e: e.memset(oneb, 1.0), writes=["one"])
    S.pool(lambda e: e.memset(VsA, 1.0), writes=["VsA"])
    S.pool(lambda e: e.memset(VwA, 1.0), writes=["VwA"])
    S.dma("pool", identb, T["ident"], writes=["ident"])
    S.dma("sp", identf, T["ident"], writes=["identf"])
    S.dma("sp", utri, T["utri"], writes=["utri"])
    S.dma("pool", tricb, T["tric"], writes=["tric"])
    S.dma("pool", triab, T["tria"], writes=["tria"])
    S.dma("pool", KsA[64:128, :], T["emat"], writes=["KsE"])
    S.dma("sp", hpb, T["hp"].partition_broadcast(128), writes=["hpb"])

    wtm = A.alloc([KC, NA + NB_], BF16)
    wfm = A.alloc([KC, NFM], BF16)
    S.dma("pool", wtm, T["w_tm"].rearrange("(k p) n -> p k n", p=128), writes=["wtm"])
    S.dma("pool", wfm, T["w_fm"].rearrange("(k p) n -> p k n", p=128), writes=["wfm"])
    anwb = A.alloc([D], F32)
    S.dma("sp", anwb, T["anw"].partition_broadcast(128), writes=["anwb"])
    ropet = A.alloc([NTT, 16], F32)
    S.dma("sp", ropet, T["rope"].rearrange("p (t c) -> p t c", c=16), writes=["ropet"])
    convw = A.alloc([16], F32)
    convb = A.alloc([4], F32)
    S.dma("sp", convw, T["convw"], writes=["convw"])
    S.dma("sp", convb, T["convb"], writes=["convb"])
    xt = [A.alloc([D], F32) for _ in range(2)]
    sq = A.alloc([D], BF16)
    ss = A.alloc([4], F32)
    ub = A.alloc([D], BF16)
    uT = A.alloc([KC, 512], BF16)
    cbuf = A.alloc([4, 515], F32)
    cacc = A.alloc([512], F32)
    xbcT = A.alloc([4, 512], BF16)
    zs = A.alloc([256], BF16)
    dtt = A.alloc([4], F32)
    qk = A.alloc([6, 64], F32)
    qkr = A.alloc([6, 64], BF16)
    qkb = A.alloc([4, 64], BF16)
    rt = A.alloc([4, 6, 8], F32)
    qst = [A.alloc([4, 128], BF16) for _ in range(2)]
    qut = [A.alloc([4, 128], BF16) for _ in range(2)]
    xtm = A.alloc([256], BF16)
    btm = A.alloc([128], BF16)
    hst = A.alloc([256], F32)
    hstb = A.alloc([256], BF16)
    aneg = A.alloc([4], F32)
    adt = A.alloc([4], F32)
    adtrep = A.alloc([128], F32)
    acol = A.alloc([4], F32)
    nacol = A.alloc([4], F32)
    seg = A.alloc([128], F32)
    dec = A.alloc([128], F32)
    cbm = A.alloc([128], F32)
    MT = A.alloc([4, 128], BF16)
    dsv = A.alloc([4], F32)
    wsc = A.alloc([4], F32)
    cdv = A.alloc([4], F32)
    eac = A.alloc([4], F32)
    alast = A.alloc([4], F32)
    xw = A.alloc([256], BF16)
    xdt = A.alloc([256], BF16)
    ydg = A.alloc([256], F32)
    yy = A.alloc([256], F32)
    yo = [A.alloc([256], F32) for _ in range(2)]
    S.pool(lambda e: e.memset(cbuf, 0.0), writes=["cbuf", "cbuf0", "cbuf1", "cbuf2", "cbuf3"])
    S.pool(lambda e: e.memset(hst, 0.0), writes=["hst"])
    S.act(lambda e: e.activation(aneg, hpb[:, 4:8], AF.Exp), reads=["hpb"], writes=["aneg"])
    S.dve(lambda e: e.tensor_scalar(aneg, aneg, -1.0, None, ALU.mult), reads=["aneg"], writes=["aneg"])

    ptr = pbf(pbig, 0, 1)
    psA = pbig[:, 1, 0:NA]
    psB = pbig[:, 2, 0:NB_]
    psF = [pbig[:, 3, :], pbig[:, 4, :]]
    psS = pbig[:, 5, :]
    psN = pbig[:, 6, 0:256]
    psY = pbig[:, 7, :]

    def load_x(tt):
        S.dma("sp", xt[tt % 2], T["xb"][tt * 128:(tt + 1) * 128, :], writes=["xt%d" % (tt % 2)])

    load_x(0)
    for G in range(int(os.environ.get('MIX_G', '8'))):
        for j in range(4):
            tt = G * 4 + j
            b = tt % 2
            if tt + 1 < NTT:
                load_x(tt + 1)
            rms_rstd(S, xt[b], D, ss[:, 0:1], sq, "n", ["xt%d" % b, "eps"])
            S.dve(lambda e, b=b: e.scalar_tensor_tensor(ub, xt[b], ss[:, 0:1], anwb, ALU.mult, ALU.mult), reads=["xt%d" % b, "nss", "anwb"], writes=["ub"])
            for half in range(2):
                for k8 in range(8):
                    kc = half * 8 + k8
                    S.pe(lambda e, kc=kc, k8=k8: e.transpose(ptr[:, k8 * 128:(k8 + 1) * 128], ub[:, kc * 128:(kc + 1) * 128], identb),
                         reads=["ub", "ident"], writes=["ptr"])
                eng = S.act if half == 0 else S.dve
                if half == 0:
                    S.act(lambda e, j=j: e.copy(uT[:, 0:8, j * 128:(j + 1) * 128], ptr.rearrange("p (k n) -> p k n", k=8)), reads=["ptr"], writes=["uT%d_0" % j])
                else:
                    S.dve(lambda e, j=j: e.tensor_copy(uT[:, 8:16, j * 128:(j + 1) * 128], ptr.rearrange("p (k n) -> p k n", k=8)), reads=["ptr"], writes=["uT%d_1" % j])
        uTr = ["uT%d_%d" % (j, h) for j in range(4) for h in range(2)]
        if int(os.environ.get('MIX_P1', '15')) & 1:
            for c in range(5):
                pf = psF[c % 2]
                for kc in range(KC):
                    S.pe(lambda e, c=c, kc=kc, pf=pf: e.matmul(pf, wfm[:, kc, c * 128:(c + 1) * 128], uT[:, kc, :], start=(kc == 0), stop=(kc == KC - 1)),
                         reads=uTr + ["wfm"], writes=["psF%d" % (c % 2)])
                if c < 4:
                    S.act(lambda e, c=c, pf=pf: e.copy(cbuf[:, c, 3:515], pf), reads=["psF%d" % (c % 2)], writes=["cbuf%d" % c])
                    S.dve(lambda e, c=c: e.tensor_scalar(cacc, cbuf[:, c, 0:512], convw[:, c * 4:c * 4 + 1], None, ALU.mult), reads=["cbuf%d" % c, "cbuf", "convw"], writes=["cacc"])
                    for k in range(1, 4):
                        S.dve(lambda e, c=c, k=k: e.scalar_tensor_tensor(cacc, cbuf[:, c, k:k + 512], convw[:, c * 4 + k:c * 4 + k + 1], cacc, ALU.mult, ALU.add),
                              reads=["cbuf%d" % c, "cbuf", "cacc", "convw"], writes=["cacc"])
                    S.act(lambda e, c=c: e.activation(xbcT[:, c, :], cacc, AF.Silu, bias=convb[:, c:c + 1]), reads=["cacc", "convb"], writes=["xbcT%d" % c])
                    S.pool(lambda e, c=c: e.tensor_copy(cbuf[:, c, 0:3], cbuf[:, c, 512:515]), reads=["cbuf%d" % c], writes=["cbuf%d" % c])
                else:
                    S.act(lambda e, G=G, pf=pf: e.copy(kcvcT[:, G * 512:(G + 1) * 512], pf), reads=["psF%d" % (c % 2)], writes=["kcvcT"])
        for j in range(4):
            tt = G * 4 + j
            tok = slice(tt * 128, (tt + 1) * 128)
            if int(os.environ.get('MIX_P1', '15')) & 2:
                for kc in range(KC):
                    S.pe(lambda e, j=j, kc=kc: e.matmul(psA, uT[:, kc, j * 128:(j + 1) * 128], wtm[:, kc, 0:NA], start=(kc == 0), stop=(kc == KC - 1)),
                         reads=uTr + ["wtm"], writes=["psA"])
                for kc in range(KC):
                    S.pe(lambda e, j=j, kc=kc: e.matmul(psB, uT[:, kc, j * 128:(j + 1) * 128], wtm[:, kc, NA:NA + NB_], start=(kc == 0), stop=(kc == KC - 1)),
                         reads=uTr + ["wtm"], writes=["psB"])
                S.act(lambda e: e.activation(zs, psA[:, 0:256], AF.Silu), reads=["psA"], writes=["zs"])
                S.dve(lambda e: e.tensor_tensor(dtt, psA[:, 256:260], hpb[:, 0:4], ALU.add), reads=["psA", "hpb"], writes=["dtt"])
                S.act(lambda e: e.activation(dtt, dtt, AF.Exp), reads=["dtt"], writes=["dtt"])
                S.act(lambda e: e.activation(dtt, dtt, AF.Ln, bias=oneb), reads=["dtt", "one"], writes=["dtt"])
                S.act(lambda e, tt=tt: e.activation(gate[:, tt, :], psA[:, 260:272], AF.Sigmoid), reads=["psA"], writes=["gate"])
                S.dve(lambda e, tt=tt: e.tensor_copy(VsA[:, tt, 0:64], psA[:, 272:336]), reads=["psA", "VsA"], writes=["VsA"])
                S.dve(lambda e, tt=tt: e.tensor_copy(VwA[:, tt, 0:64], psA[:, 336:400]), reads=["psA", "VwA"], writes=["VwA"])
            if int(os.environ.get('MIX_P1', '15')) & 4:
                S.act(lambda e: e.copy(qk, psB.rearrange("p (a b) -> p a b", a=6)), reads=["psB"], writes=["qk"])
                S.pool(lambda e: e.tensor_copy(qkb, qk[:, 0:4, :]), reads=["qk"], writes=["qkb"])
                S.pool(lambda e: e.tensor_copy(qkr, qk), reads=["qk"], writes=["qkr"])
                cosb = ropet[:, tt, 0:8].unsqueeze(1).to_broadcast([128, 6, 8])
                sinb = ropet[:, tt, 8:16].unsqueeze(1).to_broadcast([128, 6, 8])
                S.dve(lambda e, cosb=cosb: e.tensor_tensor(rt[:, 0], qk[:, :, 0:8], cosb, ALU.mult), reads=["qk", "ropet"], writes=["rt0"])
                S.dve(lambda e, sinb=sinb: e.tensor_tensor(rt[:, 1], qk[:, :, 8:16], sinb, ALU.mult), reads=["qk", "ropet"], writes=["rt1"])
                S.dve(lambda e, cosb=cosb: e.tensor_tensor(rt[:, 2], qk[:, :, 8:16], cosb, ALU.mult), reads=["qk", "ropet"], writes=["rt2"])
                S.dve(lambda e, sinb=sinb: e.tensor_tensor(rt[:, 3], qk[:, :, 0:8], sinb, ALU.mult), reads=["qk", "ropet"], writes=["rt3"])
                S.dve(lambda e: e.tensor_tensor(qkr[:, :, 0:8], rt[:, 0], rt[:, 1], ALU.subtract), reads=["rt0", "rt1", "qkr"], writes=["qkr"])
                S.dve(lambda e: e.tensor_tensor(qkr[:, :, 8:16], rt[:, 2], rt[:, 3], ALU.add), reads=["rt2", "rt3", "qkr"], writes=["qkr"])
                ptq = ptr[0:64, 0:768].rearrange("p (a b) -> p a b", a=6)
                ptu = pbf(pbig, 6, 7)[0:64, 512:1024].rearrange("p (a b) -> p a b", a=4)
                for a in range(6):
                    S.pe(lambda e, a=a: e.transpose(ptq[:, a, :], qkr[:, a, :], identb), reads=["qkr", "ident"], writes=["ptr"])
                for a in range(4):
                    S.pe(lambda e, a=a: e.transpose(ptu[:, a, :], qkb[:, a, :], identb), reads=["qkb", "ident"], writes=["ptr2"])
                qb = tt % 2
                S.act(lambda e, qb=qb: e.copy(qst[qb][0:64], ptq[:, 0:4, :]), reads=["ptr"], writes=["qst%d" % qb])
                S.dve(lambda e, qb=qb: e.tensor_copy(qut[qb][0:64], ptu), reads=["ptr2"], writes=["qut%d" % qb])
                S.act(lambda e, tok=tok: e.copy(KsA[0:64, tok], ptq[:, 4, :]), reads=["ptr"], writes=["KsA"])
                S.dve(lambda e, tok=tok: e.tensor_copy(KwT[0:64, tok], ptq[:, 5, :]), reads=["ptr"], writes=["KwT"])
                S.dma("sp", T["qr_d"][:, :, tok], qst[qb][0:64], reads=["qst%d" % qb], writes=["qr_d"])
                S.dma("sp", T["qu_d"][:, :, tok], qut[qb][0:64], reads=["qut%d" % qb], writes=["qu_d"])
            if int(os.environ.get('MIX_P1', '15')) & 8:
                ptx = ptr[:, 0:384]
                for c in range(3):
                    S.pe(lambda e, c=c, j=j: e.transpose(ptx[:, c * 128:(c + 1) * 128], xbcT[:, c, j * 128:(j + 1) * 128], identb),
                         reads=["xbcT%d" % c, "ident"], writes=["ptr"])
                if int(os.environ.get('MIX_X', '3')) & 1:
                    S.act(lambda e: e.copy(xtm, ptx[:, 0:256]), reads=["ptr"], writes=["xtm"])
                if int(os.environ.get('MIX_X', '3')) & 2:
                    S.dve(lambda e: e.tensor_copy(btm, ptx[:, 256:384]), reads=["ptr"], writes=["btm"])
                if int(os.environ.get('MIX_SSD', '9')) < 1:
                    continue
                S.dve(lambda e: e.tensor_tensor(adt, dtt, aneg, ALU.mult), reads=["dtt", "aneg"], writes=["adt"])
                S.pe(lambda e: e.matmul(psS[:, 0:4], utri, adt, start=True, stop=True), reads=["utri", "adt"], writes=["psS_a"])
                S.act(lambda e: e.copy(acol, psS[:, 0:4]), reads=["psS_a"], writes=["acol"])
                S.dve(lambda e: e.tensor_scalar(nacol, psS[:, 0:4], -1.0, None, ALU.mult), reads=["psS_a"], writes=["nacol"])
                S.act(lambda e: e.activation(eac, acol, AF.Exp), reads=["acol"], writes=["eac"])
                if int(os.environ.get('MIX_SSD', '9')) < 2:
                    continue
                S.pe(lambda e, j=j: e.matmul(psS[:, 256:384], xbcT[:, 2, j * 128:(j + 1) * 128], xbcT[:, 3, j * 128:(j + 1) * 128], start=True, stop=True),
                     reads=["xbcT2", "xbcT3"], writes=["psS_c"])
                S.dve(lambda e: e.tensor_tensor(cbm, psS[:, 256:384], utri, ALU.mult), reads=["psS_c", "utri"], writes=["cbm"])
                if int(os.environ.get('MIX_SSD', '9')) < 3:
                    continue
                for h in range(4):
                    S.dve(lambda e, h=h: e.tensor_scalar(adtrep, utri, 0.0, adt[:, h:h + 1], ALU.mult, ALU.add), reads=["utri", "adt"], writes=["adtrep"])
                    S.pe(lambda e: e.matmul(psS[:, 128:256], adtrep, utri, start=True, stop=True), reads=["adtrep", "utri"], writes=["psS_r"])
                    S.dve(lambda e, h=h: e.tensor_scalar(seg, psS[:, 128:256], acol[:, h:h + 1], 0.0, ALU.subtract, ALU.min), reads=["psS_r", "acol"], writes=["seg"])
                    S.act(lambda e: e.activation(dec, seg, AF.Exp), reads=["seg"], writes=["dec"])
                    S.dve(lambda e, h=h: e.tensor_tensor(MT[:, h, :], dec, cbm, ALU.mult), reads=["dec", "cbm"], writes=["MT%d" % h])
                    S.dve(lambda e, h=h: e.tensor_copy(alast[:, h:h + 1], psS[:, 255:256]), reads=["psS_r"], writes=["alast%d" % h])
                    S.act(lambda e, h=h: e.activation(dsv[:, h:h + 1], nacol[:, h:h + 1], AF.Exp, bias=alast[:, h:h + 1]), reads=["nacol", "alast%d" % h], writes=["dsv%d" % h])
                    S.act(lambda e, h=h: e.activation(cdv[:, h:h + 1], alast[:, h:h + 1], AF.Exp), reads=["alast%d" % h], writes=["cdv%d" % h])
                if int(os.environ.get('MIX_SSD', '9')) < 4:
                    continue
                dsr = ["dsv%d" % h for h in range(4)]
                S.dve(lambda e: e.tensor_tensor(wsc, dtt, dsv, ALU.mult), reads=["dtt"] + dsr, writes=["wsc"])
                xv = xtm.rearrange("p (h d) -> p h d", h=4)
                S.dve(lambda e, xv=xv: e.tensor_tensor(xw.rearrange("p (h d) -> p h d", h=4), xv, wsc.unsqueeze(2).to_broadcast([128, 4, 64]), ALU.mult),
                      reads=["xtm", "wsc"], writes=["xw"])
                S.dve(lambda e, xv=xv: e.tensor_tensor(xdt.rearrange("p (h d) -> p h d", h=4), xv, dtt.unsqueeze(2).to_broadcast([128, 4, 64]), ALU.mult),
                      reads=["xtm", "dtt"], writes=["xdt"])
                if int(os.environ.get('MIX_SSD', '9')) < 5:
                    continue
                S.pool(lambda e: e.tensor_copy(hstb, hst), reads=["hst"], writes=["hstb"])
                S.pe(lambda e, j=j: e.matmul(psY[:, 0:256], xbcT[:, 3, j * 128:(j + 1) * 128], hstb, start=True, stop=True), reads=["xbcT3", "hstb"], writes=["psY_o"])
                for h in range(4):
                    S.pe(lambda e, h=h: e.matmul(psY[:, 256 + h * 64:256 + (h + 1) * 64], MT[:, h, :], xdt[:, h * 64:(h + 1) * 64], start=True, stop=True),
                         reads=["MT%d" % h, "xdt"], writes=["psY_d"])
                S.pe(lambda e: e.matmul(psN, btm, xw, start=True, stop=True), reads=["btm", "xw"], writes=["psN"])
                cdr = ["cdv%d" % h for h in range(4)]
                for h in range(4):
                    S.dve(lambda e, h=h: e.scalar_tensor_tensor(hst[:, h * 64:(h + 1) * 64], hst[:, h * 64:(h + 1) * 64], cdv[:, h:h + 1], psN[:, h * 64:(h + 1) * 64], ALU.mult, ALU.add),
                          reads=["hst", "psN"] + cdr, writes=["hst"])
                S.act(lambda e: e.copy(ydg, psY[:, 256:512]), reads=["psY_d"], writes=["ydg"])
                for h in range(4):
                    hs_ = slice(h * 64, (h + 1) * 64)
                    S.dve(lambda e, h=h, hs_=hs_: e.scalar_tensor_tensor(yy[:, hs_], psY[:, hs_], eac[:, h:h + 1], ydg[:, hs_], ALU.mult, ALU.add),
                          reads=["psY_o", "eac", "ydg"], writes=["yy"])
                    S.dve(lambda e, h=h, hs_=hs_: e.scalar_tensor_tensor(yy[:, hs_], xtm[:, hs_], hpb[:, 8 + h:9 + h], yy[:, hs_], ALU.mult, ALU.add),
                          reads=["xtm", "hpb", "yy"], writes=["yy"])
                ob = tt % 2
                S.dve(lambda e, ob=ob: e.tensor_tensor(yo[ob], yy, zs, ALU.mult), reads=["yy", "zs"], writes=["yo%d" % ob])
                S.dma("sp", T["mixo"][tok, 0:256], yo[ob], reads=["yo%d" % ob], writes=["mixo_s%d" % tt])

    if int(os.environ.get('MIX_STOP', '9')) <= 1:
        return
    bar()
    A.reset(base_persist)
    w1 = A.alloc([32, 256], BF16)
    S.dma("pool", w1[0:64], T["w1k"].rearrange("d (l h) -> d l h", l=32), writes=["w1k"])
    S.dma("pool", w1[64:128], T["w1v"].rearrange("d (l h) -> d l h", l=32), writes=["w1v"])
    pe_ = A.alloc([32], BF16)
    S.dma("pool", pe_[0:64], T["pek"], writes=["pek"])
    S.dma("pool", pe_[64:128], T["pev"], writes=["pev"])
    w2 = A.alloc([2, 2, 64], BF16)
    S.dma("pool", w2[:, 0], T["w2k"].rearrange("(c p) d -> p c d", p=128), writes=["w2k"])
    S.dma("pool", w2[:, 1], T["w2v"].rearrange("(c p) d -> p c d", p=128), writes=["w2v"])
    cbias = A.alloc([4], F32)
    hsb = A.alloc([4, 256], BF16)
    KcT = A.alloc([256], BF16)
    VcA = A.alloc([2, 129], BF16)
    S.pool(lambda e: e.memset(hsb, 0.0), writes=["hsb"])
    S.pool(lambda e: e.memset(VcA, 0.0), writes=["VcA"])
    S.pool(lambda e: e.memset(KcT, 0.0), writes=["KcT"])
    S.pool(lambda e: e.memset(VcA[:, :, 64:65], 1.0), reads=["VcA"], writes=["VcA"])
    S.dma("pool", VcA[:, :, 65:129], T["ovl"].rearrange("(c p) j -> p c j", p=128), reads=["VcA"], writes=["VcA"])
    for kv in range(2):
        rows = slice(kv * 64, (kv + 1) * 64)
        for hc in range(2):
            idx = kv * 2 + hc
            pb_ = pbig[:, idx, 0:255]
            pbias = pbig[:, 4, idx:idx + 1]
            for l in range(32):
                S.pe(lambda e, rows=rows, hc=hc, l=l, pbias=pbias: e.matmul(pbias, w1[rows, l, hc * 128:(hc + 1) * 128], pe_[rows, l:l + 1], start=(l == 0), stop=(l == 31)),
                     reads=["w1k", "w1v", "pek", "pev"], writes=["pbias%d" % idx])
            S.act(lambda e, idx=idx, pbias=pbias: e.copy(cbias[:, idx:idx + 1], pbias), reads=["pbias%d" % idx], writes=["cbias%d" % idx])
            for l in range(32):
                S.pe(lambda e, rows=rows, hc=hc, l=l, pb_=pb_: e.matmul(pb_, w1[rows, l, hc * 128:(hc + 1) * 128], kcvcT[rows, l:l + 16 * 254 + 1:16], start=(l == 0), stop=(l == 31)),
                     reads=["w1k", "w1v", "kcvcT"], writes=["phid%d" % idx])
            S.act(lambda e, idx=idx, pb_=pb_: e.activation(hsb[:, idx, 0:255], pb_, AF.Silu, bias=cbias[:, idx:idx + 1]), reads=["phid%d" % idx, "cbias%d" % idx, "hsb"], writes=["hsb%d" % idx])
    pk = pbig[0:64, 5, 0:255]
    for hc in range(2):
        S.pe(lambda e, hc=hc: e.matmul(pk, w2[:, 0, hc, :], hsb[:, hc, 0:255], start=(hc == 0), stop=(hc == 1)), reads=["w2k", "hsb0", "hsb1"], writes=["pk"])
    S.act(lambda e: e.copy(KcT[0:64, 0:255], pk), reads=["pk", "KcT"], writes=["KcT"])
    for it in range(2):
        m = 128 if it == 0 else 127
        pv = pbig[0:m, 6 + it, 0:64]
        for hc in range(2):
            S.pe(lambda e, it=it, hc=hc, m=m, pv=pv: e.matmul(pv, hsb[:, 2 + hc, it * 128:it * 128 + m], w2[:, 1, hc, :], start=(hc == 0), stop=(hc == 1)),
                 reads=["w2v", "hsb2", "hsb3"], writes=["pv%d" % it])
        S.act(lambda e, it=it, m=m, pv=pv: e.copy(VcA[0:m, it, 0:64], pv), reads=["pv%d" % it, "VcA"], writes=["VcA"])

    if int(os.environ.get('MIX_STOP', '9')) <= 2:
        return
    bar()
    base3 = A.off
    qu = [A.alloc([4, 512], BF16) for _ in range(2)]
    cmk = [A.alloc([2, 512], BF16) for _ in range(2)]
    PcT = [A.alloc([2, 512], BF16) for _ in range(2)]
    imp = A.alloc([4, 64], F32)
    imp2 = A.alloc([64], F32)
    m8 = A.alloc([16], F32)
    thr = A.alloc([1], F32)
    rr = A.alloc([2], F32)
    nmb = A.alloc([128], BF16)
    S.pool(lambda e: e.memset(nmb, 0.0), writes=["nmb"])
    pnt = pbf(pbig, 7, 8)[:, 0:128]
    for Q in range(8):
        qb = Q % 2
        qs = slice(Q * 512, (Q + 1) * 512)
        S.dma("sp", qu[qb][0:64], T["qu_d"][:, :, qs], writes=["qu%d" % qb])
        S.dma("pool", cmk[qb], T["cmask"][:, qs].rearrange("(c p) t -> p c t", p=128), writes=["cmk%d" % qb])
        for h in range(4):
            pb2 = h % 2
            for it in range(2):
                ps_ = pbig[:, pb2 * 2 + it, :]
                S.pe(lambda e, it=it, h=h, qb=qb, ps_=ps_: e.matmul(ps_, KcT[0:64, it * 128:(it + 1) * 128], qu[qb][0:64, h, :], start=True, stop=False),
                     reads=["KcT", "qu%d" % qb], writes=["psc%d_%d" % (pb2, it)])
                S.pe(lambda e, it=it, qb=qb, ps_=ps_: e.matmul(ps_, identb, cmk[qb][:, it, :], start=False, stop=True),
                     reads=["ident", "cmk%d" % qb], writes=["psc%d_%d" % (pb2, it)])
                S.act(lambda e, it=it, pb2=pb2, ps_=ps_: e.activation(PcT[pb2][:, it, :], ps_, AF.Exp, scale=SCALE), reads=["psc%d_%d" % (pb2, it)], writes=["PcT%d_%d" % (pb2, it)])
            for sub in range(4):
                tt = Q * 4 + sub
                po = pbig[:, 4 + (sub % 2), 0:129]
                for it in range(2):
                    S.pe(lambda e, it=it, pb2=pb2, sub=sub, po=po: e.matmul(po, PcT[pb2][:, it, sub * 128:(sub + 1) * 128], VcA[:, it, :], start=(it == 0), stop=(it == 1)),
                         reads=["PcT%d_0" % pb2, "PcT%d_1" % pb2, "VcA"], writes=["po%d" % (sub % 2)])
                pr = ["po%d" % (sub % 2)]
                S.dve(lambda e, po=po: e.tensor_scalar(rr[:, 0:1], po[:, 64:65], 1e-30, None, ALU.add), reads=pr, writes=["rr0"])
                S.dve(lambda e: e.reciprocal(rr[:, 0:1], rr[:, 0:1]), reads=["rr0"], writes=["rr0"])
                if h == 0:
                    S.dve(lambda e, po=po, sub=sub: e.tensor_scalar(imp[:, sub, :], po[:, 65:129], rr[:, 0:1], None, ALU.mult), reads=pr + ["rr0"], writes=["imp%d" % sub])
                else:
                    S.dve(lambda e, po=po, sub=sub: e.scalar_tensor_tensor(imp[:, sub, :], po[:, 65:129], rr[:, 0:1], imp[:, sub, :], ALU.mult, ALU.add),
                          reads=pr + ["rr0", "imp%d" % sub], writes=["imp%d" % sub])
                S.dve(lambda e, tt=tt, h=h: e.tensor_tensor(rr[:, 1:2], rr[:, 0:1], gate[:, tt, h * 3:h * 3 + 1], ALU.mult), reads=["rr0", "gate"], writes=["rr1"])
                S.dve(lambda e, po=po, tt=tt, h=h: e.tensor_scalar(attO[:, tt, h * 64:(h + 1) * 64], po[:, 0:64], rr[:, 1:2], None, ALU.mult), reads=pr + ["rr1"], writes=["attO%d" % tt])
        for sub in range(4):
            tt = Q * 4 + sub
            ir = ["imp%d" % sub]
            im = imp[:, sub, :]
            S.pool(lambda e, im=im: e.memset(im[:, 0:1], 1e4), reads=ir, writes=ir)
            lo = max(2 * tt - 1, 0)
            S.pool(lambda e, im=im, lo=lo, tt=tt: e.memset(im[0:64, lo:2 * tt + 1], 1e4), reads=ir, writes=ir)
            S.pool(lambda e, im=im, tt=tt: e.memset(im[64:128, 2 * tt:2 * tt + 2], 1e4), reads=ir, writes=ir)
            if 2 * tt + 1 < 64:
                S.pool(lambda e, im=im, tt=tt: e.memset(im[0:64, 2 * tt + 1:64], -1.0), reads=ir, writes=ir)
            if 2 * tt + 2 < 64:
                S.pool(lambda e, im=im, tt=tt: e.memset(im[64:128, 2 * tt + 2:64], -1.0), reads=ir, writes=ir)
            S.dve(lambda e, im=im: e.max(m8[:, 0:8], im), reads=ir, writes=["m8a"])
            S.dve(lambda e, im=im: e.match_replace(imp2, m8[:, 0:8], im, -1e30), reads=ir + ["m8a"], writes=["imp2"])
            S.dve(lambda e: e.max(m8[:, 8:16], imp2), reads=["imp2"], writes=["m8b"])
            S.dve(lambda e: e.tensor_scalar(thr, m8[:, 15:16], 0.0, None, ALU.max), reads=["m8b"], writes=["thr"])
            S.dve(lambda e, im=im: e.tensor_scalar(nmb[:, 64:128], im, thr, NEG, ALU.is_lt, ALU.mult), reads=ir + ["thr", "nmb"], writes=["nmb"])
            S.pe(lambda e: e.transpose(pnt, nmb, identb), reads=["nmb", "ident"], writes=["pnt"])
            S.act(lambda e, tt=tt: e.copy(nmT[64:128, tt * 128:(tt + 1) * 128], pnt[64:128, :]), reads=["pnt"], writes=["nmT"])

    if int(os.environ.get('MIX_STOP', '9')) <= 3:
        return
    bar()
    A.reset(base3)
    Qa = [A.alloc([4, 512], BF16) for _ in range(2)]
    PT = [A.alloc([512], BF16) for _ in range(3)]
    PW = [A.alloc([128], BF16) for _ in range(3)]
    r4 = A.alloc([2], F32)
    pti = 0
    pwi = 0
    for Q in range(8):
        qb = Q % 2
        qs = slice(Q * 512, (Q + 1) * 512)
        S.dma("sp", Qa[qb][0:64], T["qr_d"][:, :, qs], writes=["Qa%d" % qb])
        for h in range(4):
            S.pool(lambda e, qb=qb, h=h, qs=qs: e.tensor_copy(Qa[qb][64:128, h, :], nmT[64:128, qs]), reads=["nmT"], writes=["Qm%d_%d" % (qb, h)])
        for h in range(4):
            qr_ = ["Qa%d" % qb, "Qm%d_%d" % (qb, h)]
            posel = pbig[:, 6, 0:260].rearrange("p (s c) -> p s c", s=4)
            powin = pbig[:, 7, 0:260].rearrange("p (s c) -> p s c", s=4)
            S.dve(lambda e: e.memset(pbig[:, 6, 0:260], 0.0), writes=["posel"])
            for kt in range(4 * Q + 4):
                ks_ = slice(kt * 128, (kt + 1) * 128)
                sb_ = kt % 3
                ps_ = pbig[:, sb_, :]
                pres = "pss%d" % sb_
                o = kt - 4 * Q
                if o < 0:
                    S.pe(lambda e, ks_=ks_, qb=qb, h=h, ps_=ps_: e.matmul(ps_, KsA[:, ks_], Qa[qb][:, h, :], start=True, stop=True),
                         reads=["KsA", "KsE"] + qr_, writes=[pres])
                    lo = 0
                else:
                    lo = o * 128
                    S.pe(lambda e, ks_=ks_, qb=qb, h=h, ps_=ps_, lo=lo: e.matmul(ps_[:, lo:lo + 128], KsA[:, ks_], Qa[qb][:, h, lo:lo + 128], start=True, stop=False),
                         reads=["KsA", "KsE"] + qr_, writes=[pres])
                    S.pe(lambda e, ps_=ps_, lo=lo: e.matmul(ps_[:, lo:lo + 128], identb, tricb, start=False, stop=True), reads=["ident", "tric"], writes=[pres])
                    if o < 3:
                        S.pe(lambda e, ks_=ks_, qb=qb, h=h, ps_=ps_, lo=lo: e.matmul(ps_[:, lo + 128:512], KsA[:, ks_], Qa[qb][:, h, lo + 128:512], start=True, stop=True),
                             reads=["KsA", "KsE"] + qr_, writes=[pres])
                pt_ = PT[pti % 3]
                ptres = "PT%d" % (pti % 3)
                pti += 1
                S.act(lambda e, ps_=ps_, pt_=pt_, lo=lo: e.activation(pt_[:, lo:512], ps_[:, lo:512], AF.Exp, scale=SCALE), reads=[pres], writes=[ptres])
                for sub in range(max(o, 0), 4):
                    S.pe(lambda e, pt_=pt_, sub=sub, kt=kt, Q=Q, posel=posel: e.matmul(posel[:, sub, :], pt_[:, sub * 128:(sub + 1) * 128], VsA[:, kt, :],
                                                                                 start=False, stop=False, skip_group_check=True),
                         reads=[ptres, "VsA"], writes=["posel"])
            for sub in range(4):
                tt = 4 * Q + sub
                kts = [k for k in range(tt - 4, tt + 1) if k >= 0]
                for kt in kts:
                    ks_ = slice(kt * 128, (kt + 1) * 128)
                    wsl = pwi % 4
                    psw = pbig[:, 3 + wsl // 2, (wsl % 2) * 128:(wsl % 2) * 128 + 128]
                    pwres = "psw%d" % wsl
                    msk = triab if kt == tt - 4 else (tricb if kt == tt else None)
                    S.pe(lambda e, ks_=ks_, qb=qb, h=h, sub=sub, psw=psw, msk=msk: e.matmul(psw, KwT[0:64, ks_], Qa[qb][0:64, h, sub * 128:(sub + 1) * 128], start=True, stop=(msk is None)),
                         reads=["KwT", "Qa%d" % qb], writes=[pwres])
                    if msk is not None:
                        S.pe(lambda e, psw=psw, msk=msk: e.matmul(psw, identb, msk, start=False, stop=True), reads=["ident", "tric", "tria"], writes=[pwres])
                    pw_ = PW[pwi % 3]
                    pwr = "PW%d" % (pwi % 3)
                    pwi += 1
                    S.act(lambda e, psw=psw, pw_=pw_: e.activation(pw_, psw, AF.Exp, scale=SCALE), reads=[pwres], writes=[pwr])
                    S.pe(lambda e, pw_=pw_, sub=sub, kt=kt, kts=kts, powin=powin: e.matmul(powin[:, sub, :], pw_, VwA[:, kt, :], start=(kt == kts[0]), stop=(kt == kts[-1])),
                         reads=[pwr, "VwA"], writes=["powin"])
            for sub in range(4):
                tt = 4 * Q + sub
                for br, (po_, pres) in enumerate(((posel, "posel"), (powin, "powin"))):
                    S.dve(lambda e, po_=po_, sub=sub, br=br: e.reciprocal(r4[:, br:br + 1], po_[:, sub, 64:65]), reads=[pres], writes=["r4_%d" % br])
                    S.dve(lambda e, tt=tt, h=h, br=br: e.tensor_tensor(r4[:, br:br + 1], r4[:, br:br + 1], gate[:, tt, h * 3 + 1 + br:h * 3 + 2 + br], ALU.mult),
                          reads=["r4_%d" % br, "gate"], writes=["r4_%d" % br])
                    S.dve(lambda e, po_=po_, sub=sub, tt=tt, h=h, br=br: e.scalar_tensor_tensor(attO[:, tt, h * 64:(h + 1) * 64], po_[:, sub, 0:64], r4[:, br:br + 1],
                                                                                                  attO[:, tt, h * 64:(h + 1) * 64], ALU.mult, ALU.add),
                          reads=[pres, "r4_%d" % br, "attO%d" % tt], writes=["attO%d" % tt])
        for sub in range(4):
            tt = 4 * Q + sub
            S.dma("sp", T["mixo"][tt * 128:(tt + 1) * 128, 256:512], attO[:, tt, :], reads=["attO%d" % tt])


def _perm_cols():
    return None


def run_mix(inputs):
    if "mix" not in _CACHE:
        _CACHE["mix"] = build_mix()
    nc = _CACHE["mix"]
    x = inputs["x"]
    w_in = inputs["w_in"][0]
    offs = np.cumsum([0, 1024, 1536, 16, 1024, 256, 256, 256, 256, 256, 256, 48])
    oz, oxbc, odt, oq, okc, ovc, oks, ovs, okw, ovw, ogate = offs[:11]
    conv_w = inputs["conv_w"][0]
    conv_b = inputs["conv_b"][0]
    t = np.arange(SEQ, dtype=np.float32)
    inv = (1.0 / (500000.0 ** (np.arange(0, 16, 2, dtype=np.float32) / np.float32(16)))).astype(np.float32)
    ang = (t[:, None] * inv[None, :]).astype(np.float32)
    rope = np.concatenate([np.cos(ang), np.sin(ang)], 1).astype(np.float32)
    rope = np.ascontiguousarray(rope.reshape(NTT, 128, 16).transpose(1, 0, 2).reshape(128, NTT * 16))
    ident = np.eye(128, dtype=np.float32)
    kk = np.arange(128)[:, None]
    qq = np.arange(128)[None, :]
    tric = np.where(kk <= qq, 0.0, NEG).astype(np.float32)
    tria = np.where(kk > qq, 0.0, NEG).astype(np.float32)
    utri = (kk <= qq).astype(np.float32)
    emat = (np.arange(SEQ)[None, :] // 64 == np.arange(64)[:, None]).astype(np.float32)
    ii = np.arange(256)[:, None]
    cmask = np.where((16 * ii + 31 <= np.arange(SEQ)[None, :]) & (ii < 255), 0.0, NEG).astype(np.float32)
    cs = np.arange(255)[:, None] * 16
    ss_ = np.arange(64)[None, :] * 64
    ov = np.clip(np.minimum(cs + 32, ss_ + 64) - np.maximum(cs, ss_), 0, None) / 32.0
    ovl = np.zeros((256, 64), np.float32)
    ovl[:255] = ov
    in_maps = []
    for c in range(8):
        b, g = c // 4, c % 4
        grp = g // 2
        ar = np.arange
        tm_cols = np.concatenate([oz + 256 * g + ar(256), odt + 4 * g + ar(4), ogate + 12 * g + ar(12), ovs + 64 * g + ar(64), ovw + 64 * g + ar(64),
                                  oq + 256 * g + ar(256), oks + 64 * g + ar(64), okw + 64 * g + ar(64)])
        xcols = np.concatenate([256 * g + ar(256), 1024 + 128 * grp + ar(128), 1280 + 128 * grp + ar(128)])
        fm_cols = np.concatenate([oxbc + xcols, okc + 64 * g + ar(64), ovc + 64 * g + ar(64)])
        convw = np.ascontiguousarray(conv_w[:, xcols].T.reshape(4, 128, 4).transpose(1, 0, 2).reshape(128, 16))
        convb = np.ascontiguousarray(conv_b[xcols].reshape(4, 128).T)
        hp = np.concatenate([inputs["dt_bias"][0][4 * g:4 * g + 4], inputs["a_log"][0][4 * g:4 * g + 4], inputs["d_skip"][0][4 * g:4 * g + 4]]).astype(np.float32)
        in_maps.append(dict(
            xb=np.ascontiguousarray(x[b]), anw=inputs["attn_norm_w"][0], w_tm=np.ascontiguousarray(w_in[:, tm_cols]), w_fm=np.ascontiguousarray(w_in[:, fm_cols]),
            convw=convw, convb=convb, hp=hp, rope=rope,
            w1k=np.ascontiguousarray(inputs["cmp_w1_k"][0].reshape(32, 64, 256).transpose(1, 0, 2).reshape(64, 32 * 256)),
            w1v=np.ascontiguousarray(inputs["cmp_w1_v"][0].reshape(32, 64, 256).transpose(1, 0, 2).reshape(64, 32 * 256)),
            w2k=inputs["cmp_w2_k"][0], w2v=inputs["cmp_w2_v"][0],
            pek=np.ascontiguousarray(inputs["cmp_pe_k"][0].T), pev=np.ascontiguousarray(inputs["cmp_pe_v"][0].T),
            ident=ident, emat=emat, tric=tric, tria=tria, cmask=cmask, ovl=ovl, utri=utri,
        ))
    res = run_bass_kernel_spmd(nc, in_maps, core_ids=list(range(8)))
    mixed = np.empty((2, SEQ, D), np.float32)
    for c in range(8):
        b, g = c // 4, c % 4
        m = res.results[c]["mixo"]
        mixed[b, :, 256 * g:256 * (g + 1)] = m[:, 0:256]
        mixed[b, :, 1024 + 256 * g:1024 + 256 * (g + 1)] = m[:, 256:512]
    return mixed


def kernel(**inputs):
    inputs = {k: np.asarray(v) for k, v in inputs.items()}
    mixed = run_mix(inputs)
    return run_tail(inputs["x"], mixed, inputs["ssd_norm_w"][0], inputs["w_out"][0], inputs["ffn_norm_w"][0],
                    inputs["w_gate"][0], inputs["w_up"][0], inputs["w_down"][0], inputs["final_norm_w"])
```

```python
import contextlib
import os
import numpy as np
import ml_dtypes
import concourse.bass as bass
import concourse.mybir as mybir
from concourse.bass_utils import run_bass_kernel_spmd

F32 = mybir.dt.float32
BF16 = mybir.dt.bfloat16
U8 = mybir.dt.uint8
ALU = mybir.AluOpType
AF = mybir.ActivationFunctionType
AX = mybir.AxisListType

ENGS = ("pe", "act", "dve", "pool", "sp")
EPS = 1e-6


class _Op:
    __slots__ = ("eng", "fn", "dma", "deps", "idx", "signal", "val", "sem", "cc", "cost")


class Sched:
    def __init__(self, nc, n_dma_sems=8):
        self.nc = nc
        self.ops = []
        self.last_w = {}
        self.readers = {}
        self.n_dma_sems = n_dma_sems
        self.bar = None
        self.bank_of = {}

    def op(self, eng, fn, reads=(), writes=(), dma=False, c=None):
        o = _Op()
        o.cost = c
        o.eng, o.fn, o.dma = eng, fn, dma
        o.cc = False
        o.idx = len(self.ops)
        o.signal = False
        deps = {}
        if self.bar is not None:
            deps[self.bar] = "raw"
        for r in reads:
            w = self.last_w.get(r)
            if w is not None:
                deps[w] = "raw"
        for w_ in writes:
            w = self.last_w.get(w_)
            if w is not None:
                deps[w] = "raw"
            for r in self.readers.get(w_, ()):
                if r not in deps:
                    deps[r] = "war"
        for r in reads:
            self.readers.setdefault(r, []).append(o.idx)
        for w_ in writes:
            self.last_w[w_] = o.idx
            self.readers[w_] = []
        banks = set()
        for r in tuple(reads) + tuple(writes):
            banks.update(self.bank_of.get(r, ()))
        for b in banks:
            key = ("bank", b)
            w = self.last_w.get(key)
            if w is not None and w not in deps:
                deps[w] = "bank"
            self.last_w[key] = o.idx
        deps.pop(o.idx, None)
        o.deps = deps
        self.ops.append(o)
        return o

    def pe(self, fn, reads=(), writes=(), c=None):
        return self.op("pe", fn, reads, writes, c=c)

    def act(self, fn, reads=(), writes=(), c=None):
        return self.op("act", fn, reads, writes, c=c)

    def dve(self, fn, reads=(), writes=(), c=None):
        return self.op("dve", fn, reads, writes, c=c)

    def pool(self, fn, reads=(), writes=(), c=None):
        return self.op("pool", fn, reads, writes, c=c)

    DEF_COST = {"pe": 0.12, "act": 0.35, "dve": 0.25, "pool": 0.35}

    def reorder(self, window=40):
        ops = self.ops
        n = len(ops)
        queues = {e: [] for e in ENGS}
        for o in ops:
            queues[o.eng].append(o.idx)
        head = {e: 0 for e in ENGS}
        sched = [False] * n
        fin = [0.0] * n
        etime = {e: 0.0 for e in ENGS}
        order = []
        left = n
        while left:
            best = None
            for e in ENGS:
                q = queues[e]
                h = head[e]
                while h < len(q) and sched[q[h]]:
                    h += 1
                head[e] = h
                if h >= len(q):
                    continue
                seen = 0
                i = h
                et = etime[e]
                while i < len(q) and seen < window:
                    k = q[i]
                    i += 1
                    if sched[k]:
                        continue
                    seen += 1
                    o = ops[k]
                    rdy = 0.0
                    ok = True
                    for d in o.deps:
                        if not sched[d]:
                            ok = False
                            break
                        f = fin[d] + (0.0 if ops[d].eng == e else 0.15)
                        if f > rdy:
                            rdy = f
                    if not ok:
                        continue
                    st = rdy if rdy > et else et
                    key = (st, k)
                    if best is None or key < best[0]:
                        best = (key, e, k)
                    if st <= et:
                        break
            assert best is not None, "scheduler stuck"
            (st, k), e, _ = best
            o = ops[k]
            if o.dma:
                etime[e] = st + 0.06
                fin[k] = st + (o.cost if o.cost is not None else 3.0)
            else:
                c = o.cost if o.cost is not None else self.DEF_COST[e]
                etime[e] = st + c
                fin[k] = st + c
            sched[k] = True
            order.append(k)
            left -= 1
        self.order = order
        self.est_time = max(fin) if fin else 0.0

    def dma(self, q, out, in_, reads=(), writes=()):
        return self.op(q, lambda e: e.dma_start(out=out, in_=in_), reads, writes, dma=True)

    def cc(self, fn, reads=(), writes=()):
        o = self.op("pool", fn, reads, writes, dma=True)
        o.cc = True
        return o

    def barrier(self, out, in_):
        allres = set(self.last_w.keys()) | set(self.readers.keys())
        o = self.op("sp", lambda e: e.dma_start(out=out, in_=in_), reads=(), writes=tuple(allres), dma=True)
        self.bar = o.idx
        self.last_w = {}
        self.readers = {}
        return o

    def emit(self, sems, block, final_wait_eng="sp"):
        ops = self.ops
        need = [False] * len(ops)
        for o in ops:
            for d, kind in o.deps.items():
                do = ops[d]
                if do.dma:
                    continue
                if do.eng == o.eng and not o.dma:
                    if do.eng == "pe" or kind == "bank":
                        continue
                need[d] = True
        cnt = {e: 0 for e in ENGS}
        dcnt = {}
        dval = {}
        per_eng = {e: [] for e in ENGS}
        order = getattr(self, "order", None) or list(range(len(ops)))
        for k_ in order:
            o = ops[k_]
            per_eng[o.eng].append(o)
            if o.dma and o.cc:
                o.sem = "cc"
                dval["cc"] = dval.get("cc", 0) + 1
                o.val = dval["cc"]
            elif o.dma:
                k = dcnt.get(o.eng, 0)
                dcnt[o.eng] = k + 1
                key = ("dma", o.eng, k % self.n_dma_sems)
                o.sem = key
                dval[key] = dval.get(key, 0) + 16
                o.val = dval[key]
            elif need[o.idx]:
                cnt[o.eng] += 1
                o.val = cnt[o.eng]
                o.sem = o.eng
                o.signal = True
        self.stats = {e: len(per_eng[e]) for e in ENGS}
        self.stats["signals"] = dict(cnt)

        def run(engname, e):
            waited = {}
            for o in per_eng[engname]:
                wl = {}
                for d, kind in o.deps.items():
                    do = ops[d]
                    if do.dma:
                        wl[do.sem] = max(wl.get(do.sem, 0), do.val)
                        continue
                    if do.eng == o.eng and not o.dma:
                        if do.eng == "pe" or kind == "bank":
                            continue
                    wl[do.sem] = max(wl.get(do.sem, 0), do.val)
                if o.dma and o.cc and o.val > 1:
                    wl[o.sem] = max(wl.get(o.sem, 0), o.val - 1)
                elif o.dma and not o.cc and o.val > 16:
                    wl[o.sem] = max(wl.get(o.sem, 0), o.val - 16)
                for s, v in wl.items():
                    if waited.get(s, 0) >= v:
                        continue
                    waited[s] = v
                    e.wait_ge(sems[s], v)
                ins = o.fn(e)
                if o.dma and o.cc:
                    ins.then_inc(sems[o.sem])
                elif o.dma:
                    ins.then_inc(sems[o.sem], 16)
                elif o.signal:
                    ins.then_inc(sems[o.sem], 1)
            if engname == final_wait_eng:
                for key, v in dval.items():
                    if waited.get(key, 0) < v:
                        e.wait_ge(sems[key], v)
                for en in ("pe", "act", "dve", "pool"):
                    if cnt[en] > 0 and waited.get(en, 0) < cnt[en]:
                        e.wait_ge(sems[en], cnt[en])

        @block.tensor
        def _(e):
            run("pe", e)

        @block.scalar
        def _(e):
            run("act", e)

        @block.vector
        def _(e):
            run("dve", e)

        @block.gpsimd
        def _(e):
            run("pool", e)

        @block.sync
        def _(e):
            run("sp", e)


def make_sems(nc, stack, n_dma_sems=8, queues=("sp", "pool", "act")):
    sems = {}
    for e in ("pe", "act", "dve", "pool", "cc"):
        sems[e] = stack.enter_context(nc.semaphore("s_" + e))
    for q in queues:
        for i in range(n_dma_sems):
            sems[("dma", q, i)] = stack.enter_context(nc.semaphore("d_%s_%d" % (q, i)))
    return sems


_DTSZ = {F32: 4, BF16: 2, U8: 1}


class Arena:
    def __init__(self, ar, size):
        self.ar, self.size, self.off = ar, size, 0

    def reset(self, off=0):
        self.off = off

    def alloc(self, shape, dtype, parts=128):
        n = int(np.prod(shape)) * _DTSZ[dtype]
        off = (self.off + 63) // 64 * 64
        assert off + n <= self.size, ("arena overflow", off, n, self.size)
        self.off = off + n
        ap = self.ar[0:parts, off:off + n].bitcast(dtype)
        if len(shape) > 1:
            names = [chr(ord("a") + i) for i in range(len(shape))]
            pat = "p (%s) -> p %s" % (" ".join(names), " ".join(names))
            ap = ap.rearrange(pat, **{nm: int(s) for nm, s in zip(names, shape)})
        return ap


D = 2048
FF = 5632
TOK = 1024
NT = TOK // 128
KC = D // 128
FC = FF // 128
SSDW = 1024


def build_tail():
    nc = bass.Bass("TRN2", target_bir_lowering=False)
    x = nc.dram_tensor("x_own", [TOK, D], F32, kind="ExternalInput").ap()
    mix = nc.dram_tensor("mix", [TOK, D], F32, kind="ExternalInput").ap()
    w_out = nc.dram_tensor("w_out", [D, D], F32, kind="ExternalInput").ap()
    w_gate = nc.dram_tensor("w_gate", [D, FF], F32, kind="ExternalInput").ap()
    w_up = nc.dram_tensor("w_up", [D, FF], F32, kind="ExternalInput").ap()
    w_down = nc.dram_tensor("w_down", [FF, D], F32, kind="ExternalInput").ap()
    nw = nc.dram_tensor("nw", [3, D], F32, kind="ExternalInput").ap()
    ident = nc.dram_tensor("ident", [128, 128], F32, kind="ExternalInput").ap()
    out = nc.dram_tensor("out", [TOK, D], F32, kind="ExternalOutput").ap()
    h_d = nc.dram_tensor("h_d", [TOK, D], F32, kind="Internal").ap()
    dummy = nc.dram_tensor("dummy_bar", [2, 64], F32, kind="Internal").ap()
    with contextlib.ExitStack() as st:
        ASZ = 200 * 1024
        ar = st.enter_context(nc.sbuf_tensor("arena", [128, ASZ], U8))
        A = Arena(ar, ASZ)
        pbig = st.enter_context(nc.psum_tensor("pbig", [128, 8, 512], F32))
        sems = make_sems(nc, st)
        block = st.enter_context(nc.Block())
        S = Sched(nc)
        tail_body(nc, S, A, pbig, x, mix, w_out, w_gate, w_up, w_down, nw, ident, out, h_d, dummy)
        if os.environ.get('NO_REORDER') is None:
            S.reorder()
        S.emit(sems, block)
    return nc


def rms_rstd(S, src, n, ss, sq, tag, rd, wr_extra=()):
    S.act(lambda e: e.activation(sq, src, AF.Square, accum_out=ss), reads=rd, writes=[tag + "ss", tag + "sq"])
    S.act(lambda e: e.activation(ss, ss, AF.Sqrt, scale=1.0 / n, bias=EPS_AP[0]), reads=[tag + "ss"], writes=[tag + "ss"])
    S.dve(lambda e: e.reciprocal(ss, ss), reads=[tag + "ss"], writes=[tag + "ss"])


EPS_AP = [None]


def tail_body(nc, S, A, pbig, x, mix, w_out, w_gate, w_up, w_down, nw, ident, out, h_d, dummy):
    bk = {"ptr": (4, 5)}
    for i_ in range(8):
        bk["pacc%d" % i_] = (i_,)
        bk["pd%d" % i_] = (i_,)
    for i_ in range(2):
        bk["pg%d" % i_] = (i_ * 4, i_ * 4 + 1)
        bk["pu%d" % i_] = (i_ * 4 + 2, i_ * 4 + 3)
    S.bank_of = bk
    identb = A.alloc([128], BF16)
    nwb = A.alloc([3, D], F32)
    epsb = A.alloc([1], F32)
    ss = A.alloc([4], F32)
    EPS_AP[0] = epsb
    S.pool(lambda e: e.memset(epsb, EPS), writes=["eps"])
    S.dma("pool", identb, ident, writes=["ident"])
    S.dma("sp", nwb, nw.rearrange("a b -> (a b)").partition_broadcast(128).rearrange("p (a b) -> p a b", a=3), writes=["nwb"])
    vT = A.alloc([KC, TOK], BF16)
    base_persist = A.off

    wo = A.alloc([KC, D], BF16)
    for half in range(2):
        S.dma("pool", wo[:, half * 8:(half + 1) * 8, :],
              w_out[half * 1024:(half + 1) * 1024, :].rearrange("(k p) n -> p k n", p=128), writes=["wo%d" % half])
    xt = [A.alloc([D], F32) for _ in range(2)]
    mt = [A.alloc([D], F32) for _ in range(2)]
    sq = A.alloc([D], F32)
    mb = A.alloc([D], BF16)
    mT = A.alloc([KC, 128], BF16)
    hs = [A.alloc([D], F32) for _ in range(2)]
    vb = A.alloc([D], BF16)
    pacc = pbig[:, 0:4, :]
    ptr_all = pbig[:, 4:6, :].rearrange("p a b -> p (a b)").bitcast(BF16)
    ptr = ptr_all.rearrange("p (k n) -> p k n", k=KC)

    def loads(tt):
        b = tt % 2
        S.dma("sp", xt[b], x[tt * 128:(tt + 1) * 128, :], writes=["xt%d" % b])
        S.dma("sp", mt[b], mix[tt * 128:(tt + 1) * 128, :], writes=["mt%d" % b])

    loads(0)
    for tt in range(NT):
        b = tt % 2
        if tt + 1 < NT:
            loads(tt + 1)
        rms_rstd(S, mt[b][:, 0:SSDW], SSDW, ss[:, 0:1], sq[:, 0:SSDW], "a", ["mt%d" % b, "eps"])
        S.dve(lambda e, b=b: e.scalar_tensor_tensor(mb[:, 0:SSDW], mt[b][:, 0:SSDW], ss[:, 0:1], nwb[:, 0, 0:SSDW], ALU.mult, ALU.mult),
              reads=["mt%d" % b, "ass", "nwb"], writes=["mb0"])
        S.pool(lambda e, b=b: e.tensor_copy(mb[:, SSDW:D], mt[b][:, SSDW:D]), reads=["mt%d" % b], writes=["mb1"])
        for kc in range(KC):
            S.pe(lambda e, kc=kc: e.transpose(ptr[:, kc, :], mb[:, kc * 128:(kc + 1) * 128], identb),
                 reads=["mb0", "mb1", "ident"], writes=["ptr"])
        S.act(lambda e: e.copy(mT[:, 0:8, :], ptr[:, 0:8, :]), reads=["ptr"], writes=["mTa"])
        S.dve(lambda e: e.tensor_copy(mT[:, 8:16, :], ptr[:, 8:16, :]), reads=["ptr"], writes=["mTb"])
        for cb in range(4):
            for kc in range(KC):
                S.pe(lambda e, cb=cb, kc=kc: e.matmul(pacc[:, cb, :], mT[:, kc, :], wo[:, kc, cb * 512:(cb + 1) * 512],
                                                       start=(kc == 0), stop=(kc == KC - 1)),
                     reads=["mTa", "mTb", "wo0", "wo1"], writes=["pacc%d" % cb])
            S.dve(lambda e, cb=cb, b=b: e.tensor_tensor(hs[b][:, cb * 512:(cb + 1) * 512], pacc[:, cb, :], xt[b][:, cb * 512:(cb + 1) * 512], ALU.add),
                  reads=["pacc%d" % cb, "xt%d" % b], writes=["hs%d_%d" % (b, cb)])
        hres = ["hs%d_%d" % (b, cb) for cb in range(4)]
        S.dma("sp", h_d[tt * 128:(tt + 1) * 128, :], hs[b], reads=hres, writes=["h_d%d" % tt])
        rms_rstd(S, hs[b], D, ss[:, 1:2], sq, "b", hres + ["eps"])
        S.dve(lambda e, b=b: e.scalar_tensor_tensor(vb, hs[b], ss[:, 1:2], nwb[:, 1, :], ALU.mult, ALU.mult),
              reads=hres + ["bss", "nwb"], writes=["vb"])
        for kc in range(KC):
            S.pe(lambda e, kc=kc: e.transpose(ptr[:, kc, :], vb[:, kc * 128:(kc + 1) * 128], identb),
                 reads=["vb", "ident"], writes=["ptr"])
        S.act(lambda e, tt=tt: e.copy(vT[:, 0:8, tt * 128:(tt + 1) * 128], ptr[:, 0:8, :]), reads=["ptr"], writes=["vT%da" % tt])
        S.dve(lambda e, tt=tt: e.tensor_copy(vT[:, 8:16, tt * 128:(tt + 1) * 128], ptr[:, 8:16, :]), reads=["ptr"], writes=["vT%db" % tt])

    S.barrier(dummy[1:2, :], ident[0:1, 0:64])
    A.reset(base_persist)
    hT = A.alloc([FC, TOK], BF16)
    base_b = A.off
    WB = 256
    NB = FF // WB
    wg = [A.alloc([KC, WB], BF16) for _ in range(2)]
    wu = [A.alloc([KC, WB], BF16) for _ in range(2)]
    sg = [A.alloc([TOK], F32) for _ in range(2)]
    for blk in range(NB):
        b = blk % 2
        S.dma("pool", wg[b], w_gate[:, blk * WB:(blk + 1) * WB].rearrange("(k p) n -> p k n", p=128), writes=["wg%d" % b])
        S.dma("pool", wu[b], w_up[:, blk * WB:(blk + 1) * WB].rearrange("(k p) n -> p k n", p=128), writes=["wu%d" % b])
        for j in range(WB // 128):
            fc = blk * (WB // 128) + j
            pb = fc % 2
            pg = pbig[:, pb * 4:pb * 4 + 2, :]
            pu = pbig[:, pb * 4 + 2:pb * 4 + 4, :]
            for hf in range(2):
                for kc in range(KC):
                    S.pe(lambda e, b=b, j=j, hf=hf, kc=kc, pg=pg: e.matmul(pg[:, hf, :], wg[b][:, kc, j * 128:(j + 1) * 128], vT[:, kc, hf * 512:(hf + 1) * 512],
                                                                         start=(kc == 0), stop=(kc == KC - 1)),
                         reads=["wg%d" % b, "vT"], writes=["pg%d" % pb])
            for hf in range(2):
                for kc in range(KC):
                    S.pe(lambda e, b=b, j=j, hf=hf, kc=kc, pu=pu: e.matmul(pu[:, hf, :], wu[b][:, kc, j * 128:(j + 1) * 128], vT[:, kc, hf * 512:(hf + 1) * 512],
                                                                         start=(kc == 0), stop=(kc == KC - 1)),
                         reads=["wu%d" % b, "vT"], writes=["pu%d" % pb])
            S.act(lambda e, pb=pb, pg=pg: e.activation(sg[pb], pg.rearrange("p a b -> p (a b)"), AF.Silu), reads=["pg%d" % pb], writes=["sg%d" % pb])
            S.dve(lambda e, pb=pb, pu=pu, fc=fc: e.tensor_tensor(hT[:, fc, :], sg[pb], pu.rearrange("p a b -> p (a b)"), ALU.mult),
                  reads=["sg%d" % pb, "pu%d" % pb], writes=["hT%d" % fc])

    S.barrier(dummy[1:2, :], ident[0:1, 0:64])
    A.reset(base_b)
    HK = 22
    wd = [A.alloc([HK, 512], BF16) for _ in range(2)]
    hl = [A.alloc([512], F32) for _ in range(2)]
    ys = [A.alloc([512], F32) for _ in range(2)]
    it = 0
    for r in range(4):
        for hf in range(2):
            b = (r * 2 + hf) % 2
            S.dma("pool", wd[b], w_down[hf * HK * 128:(hf + 1) * HK * 128, r * 512:(r + 1) * 512].rearrange("(k p) n -> p k n", p=128), writes=["wd%d" % b])
            for tt in range(NT):
                for k in range(HK):
                    kk = hf * HK + k
                    S.pe(lambda e, b=b, tt=tt, k=k, kk=kk: e.matmul(pbig[:, tt, :], hT[:, kk, tt * 128:(tt + 1) * 128], wd[b][:, k, :],
                                                                    start=(kk == 0), stop=(kk == FC - 1)),
                         reads=["wd%d" % b, "hT"], writes=["pd%d" % tt])
        for tt in range(NT):
            b = it % 2
            it += 1
            S.dma("sp", hl[b], h_d[tt * 128:(tt + 1) * 128, r * 512:(r + 1) * 512], reads=["h_d%d_%d" % (tt, r)], writes=["hl%d" % b])
            S.dve(lambda e, b=b, tt=tt: e.tensor_tensor(ys[b], pbig[:, tt, :], hl[b], ALU.add), reads=["pd%d" % tt, "hl%d" % b], writes=["ys%d" % b])
            S.dma("sp", h_d[tt * 128:(tt + 1) * 128, r * 512:(r + 1) * 512], ys[b], reads=["ys%d" % b], writes=["h_d%d_%d" % (tt, r)])

    S.barrier(dummy[1:2, :], ident[0:1, 0:64])
    A.reset(base_persist)
    yt = [A.alloc([D], F32) for _ in range(2)]
    ot = [A.alloc([D], F32) for _ in range(2)]
    sq2 = A.alloc([D], F32)
    for tt in range(NT):
        b = tt % 2
        S.dma("sp", yt[b], h_d[tt * 128:(tt + 1) * 128, :], writes=["yt%d" % b])
        rms_rstd(S, yt[b], D, ss[:, 2:3], sq2, "c", ["yt%d" % b, "eps"])
        S.dve(lambda e, b=b: e.scalar_tensor_tensor(ot[b], yt[b], ss[:, 2:3], nwb[:, 2, :], ALU.mult, ALU.mult),
              reads=["yt%d" % b, "css", "nwb"], writes=["ot%d" % b])
        S.dma("sp", out[tt * 128:(tt + 1) * 128, :], ot[b], reads=["ot%d" % b])


_CACHE = {}


def run_tail(x, mixed, ssd_norm_w, w_out, ffn_norm_w, w_gate, w_up, w_down, final_norm_w):
    if "tail" not in _CACHE:
        _CACHE["tail"] = build_tail()
    nc = _CACHE["tail"]
    nwv = np.ones((3, D), np.float32)
    nwv[0, :SSDW] = ssd_norm_w
    nwv[1] = ffn_norm_w
    nwv[2] = final_norm_w
    ident = np.eye(128, dtype=np.float32)
    in_maps = []
    for c in range(8):
        b, g = c // 4, c % 4
        in_maps.append({
            "x_own": np.ascontiguousarray(x[b, g * TOK:(g + 1) * TOK]),
            "mix": np.ascontiguousarray(mixed[b, g * TOK:(g + 1) * TOK]),
            "w_out": w_out, "w_gate": w_gate, "w_up": w_up, "w_down": w_down,
            "nw": nwv, "ident": ident,
        })
    res = run_bass_kernel_spmd(nc, in_maps, core_ids=list(range(8)))
    outp = np.empty((2, 4096, D), np.float32)
    for c in range(8):
        b, g = c // 4, c % 4
        outp[b, g * TOK:(g + 1) * TOK] = res.results[c]["out"]
    return outp


SEQ = 4096
NTT = SEQ // 128
NEG = -30000.0
NA = 400
NB_ = 384
NFM = 640
SCALE = 0.125


def build_mix():
    nc = bass.Bass("TRN2", target_bir_lowering=False)
    di = lambda n, s, d=F32: nc.dram_tensor(n, s, d, kind="ExternalInput").ap()
    T = dict(
        xb=di("xb", [SEQ, D]), anw=di("anw", [D]), w_tm=di("w_tm", [D, NA + NB_]), w_fm=di("w_fm", [D, NFM]),
        convw=di("convw", [128, 16]), convb=di("convb", [128, 4]), hp=di("hp", [12]), rope=di("rope", [128, NTT * 16]),
        w1k=di("w1k", [64, 32 * 256]), w1v=di("w1v", [64, 32 * 256]), w2k=di("w2k", [256, 64]), w2v=di("w2v", [256, 64]),
        pek=di("pek", [64, 32]), pev=di("pev", [64, 32]), ident=di("ident", [128, 128]), emat=di("emat", [64, SEQ]),
        tric=di("tric", [128, 128]), tria=di("tria", [128, 128]), cmask=di("cmask", [256, SEQ]), ovl=di("ovl", [256, 64]),
        utri=di("utri", [128, 128]),
    )
    T["mixo"] = nc.dram_tensor("mixo", [SEQ, 512], F32, kind="ExternalOutput").ap()
    T["qr_d"] = nc.dram_tensor("qr_d", [64, 4, SEQ], BF16, kind="Internal").ap()
    T["qu_d"] = nc.dram_tensor("qu_d", [64, 4, SEQ], BF16, kind="Internal").ap()
    T["dummy"] = nc.dram_tensor("dummy_bar", [2, 64], F32, kind="Internal").ap()
    with contextlib.ExitStack() as st:
        ASZ = 207 * 1024
        ar = st.enter_context(nc.sbuf_tensor("arena", [128, ASZ], U8))
        A = Arena(ar, ASZ)
        pbig = st.enter_context(nc.psum_tensor("pbig", [128, 8, 512], F32))
        sems = make_sems(nc, st)
        block = st.enter_context(nc.Block())
        S = Sched(nc)
        mix_body(nc, S, A, pbig, T)
        if os.environ.get('NO_REORDER') is None:
            S.reorder()
        S.emit(sems, block)
    return nc


def pbf(pb, lo, hi):
    return pb[:, lo:hi, :].rearrange("p a b -> p (a b)").bitcast(BF16)


def mix_body(nc, S, A, pbig, T):
    bar = lambda: S.barrier(T["dummy"][1:2, :], T["ident"][0:1, 0:64])
    bk = {"ptr": (0,), "ptr2": (6,), "psA": (1,), "psB": (2,), "psF0": (3,), "psF1": (4,), "psS_a": (5,), "psS_r": (5,), "psS_c": (5,),
          "psN": (6,), "psY_o": (7,), "psY_d": (7,), "pk": (5,), "pv0": (6,), "pv1": (7,), "pnt": (7,), "posel": (6,), "powin": (7,)}
    for i_ in range(4):
        bk["pbias%d" % i_] = (4,)
        bk["phid%d" % i_] = (i_,)
        bk["psw%d" % i_] = (3 + i_ // 2,)
    for a_ in range(2):
        bk["po%d" % a_] = (4 + a_,)
        for b_ in range(2):
            bk["psc%d_%d" % (a_, b_)] = (a_ * 2 + b_,)
    for i_ in range(3):
        bk["pss%d" % i_] = (i_,)
    S.bank_of = bk
    identb = A.alloc([128], BF16)
    identf = A.alloc([128], F32)
    utri = A.alloc([128], F32)
    tricb = A.alloc([128], BF16)
    triab = A.alloc([128], BF16)
    epsb = A.alloc([1], F32)
    oneb = A.alloc([1], F32)
    EPS_AP[0] = epsb
    KsA = A.alloc([SEQ], BF16)
    KwT = A.alloc([SEQ], BF16)
    kcvcT = A.alloc([SEQ], BF16)
    VsA = A.alloc([NTT, 65], BF16)
    VwA = A.alloc([NTT, 65], BF16)
    gate = A.alloc([NTT, 12], F32)
    attO = A.alloc([NTT, 256], F32)
    nmT = A.alloc([SEQ], BF16)
    hpb = A.alloc([12], F32)
    base_persist = A.off
    S.pool(lambda e: e.memset(epsb, EPS), writes=["eps"])
    S.pool(lambda e: e.memset(oneb, 1.0), writes=["one"])
    S.pool(lambda e: e.memset(VsA, 1.0), writes=["VsA"])
    S.pool(lambda e: e.memset(VwA, 1.0), writes=["VwA"])
    S.dma("pool", identb, T["ident"], writes=["ident"])
    S.dma("sp", identf, T["ident"], writes=["identf"])
    S.dma("sp", utri, T["utri"], writes=["utri"])
    S.dma("pool", tricb, T["tric"], writes=["tric"])
    S.dma("pool", triab, T["tria"], writes=["tria"])
    S.dma("pool", KsA[64:128, :], T["emat"], writes=["KsE"])
    S.dma("sp", hpb, T["hp"].partition_broadcast(128), writes=["hpb"])

    wtm = A.alloc([KC, NA + NB_], BF16)
    wfm = A.alloc([KC, NFM], BF16)
    S.dma("pool", wtm, T["w_tm"].rearrange("(k p) n -> p k n", p=128), writes=["wtm"])
    S.dma("pool", wfm, T["w_fm"].rearrange("(k p) n -> p k n", p=128), writes=["wfm"])
    anwb = A.alloc([D], F32)
    S.dma("sp", anwb, T["anw"].partition_broadcast(128), writes=["anwb"])
    ropet = A.alloc([NTT, 16], F32)
    S.dma("sp", ropet, T["rope"].rearrange("p (t c) -> p t c", c=16), writes=["ropet"])
    convw = A.alloc([16], F32)
    convb = A.alloc([4], F32)
    S.dma("sp", convw, T["convw"], writes=["convw"])
    S.dma("sp", convb, T["convb"], writes=["convb"])
    xt = [A.alloc([D], F32) for _ in range(2)]
    sq = A.alloc([D], BF16)
    ss = A.alloc([4], F32)
    ub = A.alloc([D], BF16)
    uT = A.alloc([KC, 512], BF16)
    cbuf = A.alloc([4, 515], F32)
    cacc = A.alloc([512], F32)
    xbcT = A.alloc([4, 512], BF16)
    zs = A.alloc([256], BF16)
    dtt = A.alloc([4], F32)
    qk = A.alloc([6, 64], F32)
    qkr = A.alloc([6, 64], BF16)
    qkb = A.alloc([4, 64], BF16)
    rt = A.alloc([4, 6, 8], F32)
    qst = [A.alloc([4, 128], BF16) for _ in range(2)]
    qut = [A.alloc([4, 128], BF16) for _ in range(2)]
    xtm = A.alloc([256], BF16)
    btm = A.alloc([128], BF16)
    hst = A.alloc([256], F32)
    hstb = A.alloc([256], BF16)
    aneg = A.alloc([4], F32)
    adt = A.alloc([4], F32)
    adtrep = A.alloc([128], F32)
    acol = A.alloc([4], F32)
    nacol = A.alloc([4], F32)
    seg = A.alloc([128], F32)
    dec = A.alloc([128], F32)
    cbm = A.alloc([128], F32)
    MT = A.alloc([4, 128], BF16)
    dsv = A.alloc([4], F32)
    wsc = A.alloc([4], F32)
    cdv = A.alloc([4], F32)
    eac = A.alloc([4], F32)
    alast = A.alloc([4], F32)
    xw = A.alloc([256], BF16)
    xdt = A.alloc([256], BF16)
    ydg = A.alloc([256], F32)
    yy = A.alloc([256], F32)
    yo = [A.alloc([256], F32) for _ in range(2)]
    S.pool(lambda e: e.memset(cbuf, 0.0), writes=["cbuf", "cbuf0", "cbuf1", "cbuf2", "cbuf3"])
    S.pool(lambda e: e.memset(hst, 0.0), writes=["hst"])
    S.act(lambda e: e.activation(aneg, hpb[:, 4:8], AF.Exp), reads=["hpb"], writes=["aneg"])
    S.dve(lambda e: e.tensor_scalar(aneg, aneg, -1.0, None, ALU.mult), reads=["aneg"], writes=["aneg"])

    ptr = pbf(pbig, 0, 1)
    psA = pbig[:, 1, 0:NA]
    psB = pbig[:, 2, 0:NB_]
    psF = [pbig[:, 3, :], pbig[:, 4, :]]
    psS = pbig[:, 5, :]
    psN = pbig[:, 6, 0:256]
    psY = pbig[:, 7, :]

    def load_x(tt):
        S.dma("sp", xt[tt % 2], T["xb"][tt * 128:(tt + 1) * 128, :], writes=["xt%d" % (tt % 2)])

    load_x(0)
    for G in range(int(os.environ.get('MIX_G', '8'))):
        for j in range(4):
            tt = G * 4 + j
            b = tt % 2
            if tt + 1 < NTT:
                load_x(tt + 1)
            rms_rstd(S, xt[b], D, ss[:, 0:1], sq, "n", ["xt%d" % b, "eps"])
            S.dve(lambda e, b=b: e.scalar_tensor_tensor(ub, xt[b], ss[:, 0:1], anwb, ALU.mult, ALU.mult), reads=["xt%d" % b, "nss", "anwb"], writes=["ub"])
            for half in range(2):
                for k8 in range(8):
                    kc = half * 8 + k8
                    S.pe(lambda e, kc=kc, k8=k8: e.transpose(ptr[:, k8 * 128:(k8 + 1) * 128], ub[:, kc * 128:(kc + 1) * 128], identb),
                         reads=["ub", "ident"], writes=["ptr"])
                eng = S.act if half == 0 else S.dve
                if half == 0:
                    S.act(lambda e, j=j: e.copy(uT[:, 0:8, j * 128:(j + 1) * 128], ptr.rearrange("p (k n) -> p k n", k=8)), reads=["ptr"], writes=["uT%d_0" % j])
                else:
                    S.dve(lambda e, j=j: e.tensor_copy(uT[:, 8:16, j * 128:(j + 1) * 128], ptr.rearrange("p (k n) -> p k n", k=8)), reads=["ptr"], writes=["uT%d_1" % j])
        uTr = ["uT%d_%d" % (j, h) for j in range(4) for h in range(2)]
        if int(os.environ.get('MIX_P1', '15')) & 1:
            for c in range(5):
                pf = psF[c % 2]
                for kc in range(KC):
                    S.pe(lambda e, c=c, kc=kc, pf=pf: e.matmul(pf, wfm[:, kc, c * 128:(c + 1) * 128], uT[:, kc, :], start=(kc == 0), stop=(kc == KC - 1)),
                         reads=uTr + ["wfm"], writes=["psF%d" % (c % 2)])
                if c < 4:
                    S.act(lambda e, c=c, pf=pf: e.copy(cbuf[:, c, 3:515], pf), reads=["psF%d" % (c % 2)], writes=["cbuf%d" % c])
                    S.dve(lambda e, c=c: e.tensor_scalar(cacc, cbuf[:, c, 0:512], convw[:, c * 4:c * 4 + 1], None, ALU.mult), reads=["cbuf%d" % c, "cbuf", "convw"], writes=["cacc"])
                    for k in range(1, 4):
                        S.dve(lambda e, c=c, k=k: e.scalar_tensor_tensor(cacc, cbuf[:, c, k:k + 512], convw[:, c * 4 + k:c * 4 + k + 1], cacc, ALU.mult, ALU.add),
                              reads=["cbuf%d" % c, "cbuf", "cacc", "convw"], writes=["cacc"])
                    S.act(lambda e, c=c: e.activation(xbcT[:, c, :], cacc, AF.Silu, bias=convb[:, c:c + 1]), reads=["cacc", "convb"], writes=["xbcT%d" % c])
                    S.pool(lambda e, c=c: e.tensor_copy(cbuf[:, c, 0:3], cbuf[:, c, 512:515]), reads=["cbuf%d" % c], writes=["cbuf%d" % c])
                else:
                    S.act(lambda e, G=G, pf=pf: e.copy(kcvcT[:, G * 512:(G + 1) * 512], pf), reads=["psF%d" % (c % 2)], writes=["kcvcT"])
        for j in range(4):
            tt = G * 4 + j
            tok = slice(tt * 128, (tt + 1) * 128)
            if int(os.environ.get('MIX_P1', '15')) & 2:
                for kc in range(KC):
                    S.pe(lambda e, j=j, kc=kc: e.matmul(psA, uT[:, kc, j * 128:(j + 1) * 128], wtm[:, kc, 0:NA], start=(kc == 0), stop=(kc == KC - 1)),
                         reads=uTr + ["wtm"], writes=["psA"])
                for kc in range(KC):
                    S.pe(lambda e, j=j, kc=kc: e.matmul(psB, uT[:, kc, j * 128:(j + 1) * 128], wtm[:, kc, NA:NA + NB_], start=(kc == 0), stop=(kc == KC - 1)),
                         reads=uTr + ["wtm"], writes=["psB"])
                S.act(lambda e: e.activation(zs, psA[:, 0:256], AF.Silu), reads=["psA"], writes=["zs"])
                S.dve(lambda e: e.tensor_tensor(dtt, psA[:, 256:260], hpb[:, 0:4], ALU.add), reads=["psA", "hpb"], writes=["dtt"])
                S.act(lambda e: e.activation(dtt, dtt, AF.Exp), reads=["dtt"], writes=["dtt"])
                S.act(lambda e: e.activation(dtt, dtt, AF.Ln, bias=oneb), reads=["dtt", "one"], writes=["dtt"])
                S.act(lambda e, tt=tt: e.activation(gate[:, tt, :], psA[:, 260:272], AF.Sigmoid), reads=["psA"], writes=["gate"])
                S.dve(lambda e, tt=tt: e.tensor_copy(VsA[:, tt, 0:64], psA[:, 272:336]), reads=["psA", "VsA"], writes=["VsA"])
                S.dve(lambda e, tt=tt: e.tensor_copy(VwA[:, tt, 0:64], psA[:, 336:400]), reads=["psA", "VwA"], writes=["VwA"])
            if int(os.environ.get('MIX_P1', '15')) & 4:
                S.act(lambda e: e.copy(qk, psB.rearrange("p (a b) -> p a b", a=6)), reads=["psB"], writes=["qk"])
                S.pool(lambda e: e.tensor_copy(qkb, qk[:, 0:4, :]), reads=["qk"], writes=["qkb"])
                S.pool(lambda e: e.tensor_copy(qkr, qk), reads=["qk"], writes=["qkr"])
                cosb = ropet[:, tt, 0:8].unsqueeze(1).to_broadcast([128, 6, 8])
                sinb = ropet[:, tt, 8:16].unsqueeze(1).to_broadcast([128, 6, 8])
                S.dve(lambda e, cosb=cosb: e.tensor_tensor(rt[:, 0], qk[:, :, 0:8], cosb, ALU.mult), reads=["qk", "ropet"], writes=["rt0"])
                S.dve(lambda e, sinb=sinb: e.tensor_tensor(rt[:, 1], qk[:, :, 8:16], sinb, ALU.mult), reads=["qk", "ropet"], writes=["rt1"])
                S.dve(lambda e, cosb=cosb: e.tensor_tensor(rt[:, 2], qk[:, :, 8:16], cosb, ALU.mult), reads=["qk", "ropet"], writes=["rt2"])
                S.dve(lambda e, sinb=sinb: e.tensor_tensor(rt[:, 3], qk[:, :, 0:8], sinb, ALU.mult), reads=["qk", "ropet"], writes=["rt3"])
                S.dve(lambda e: e.tensor_tensor(qkr[:, :, 0:8], rt[:, 0], rt[:, 1], ALU.subtract), reads=["rt0", "rt1", "qkr"], writes=["qkr"])
                S.dve(lambda e: e.tensor_tensor(qkr[:, :, 8:16], rt[:, 2], rt[:, 3], ALU.add), reads=["rt2", "rt3", "qkr"], writes=["qkr"])
                ptq = ptr[0:64, 0:768].rearrange("p (a b) -> p a b", a=6)
                ptu = pbf(pbig, 6, 7)[0:64, 512:1024].rearrange("p (a b) -> p a b", a=4)
                for a in range(6):
                    S.pe(lambda e, a=a: e.transpose(ptq[:, a, :], qkr[:, a, :], identb), reads=["qkr", "ident"], writes=["ptr"])
                for a in range(4):
                    S.pe(lambda e, a=a: e.transpose(ptu[:, a, :], qkb[:, a, :], identb), reads=["qkb", "ident"], writes=["ptr2"])
                qb = tt % 2
                S.act(lambda e, qb=qb: e.copy(qst[qb][0:64], ptq[:, 0:4, :]), reads=["ptr"], writes=["qst%d" % qb])
                S.dve(lambda e, qb=qb: e.tensor_copy(qut[qb][0:64], ptu), reads=["ptr2"], writes=["qut%d" % qb])
                S.act(lambda e, tok=tok: e.copy(KsA[0:64, tok], ptq[:, 4, :]), reads=["ptr"], writes=["KsA"])
                S.dve(lambda e, tok=tok: e.tensor_copy(KwT[0:64, tok], ptq[:, 5, :]), reads=["ptr"], writes=["KwT"])
                S.dma("sp", T["qr_d"][:, :, tok], qst[qb][0:64], reads=["qst%d" % qb], writes=["qr_d"])
                S.dma("sp", T["qu_d"][:, :, tok], qut[qb][0:64], reads=["qut%d" % qb], writes=["qu_d"])
            if int(os.environ.get('MIX_P1', '15')) & 8:
                ptx = ptr[:, 0:384]
                for c in range(3):
                    S.pe(lambda e, c=c, j=j: e.transpose(ptx[:, c * 128:(c + 1) * 128], xbcT[:, c, j * 128:(j + 1) * 128], identb),
                         reads=["xbcT%d" % c, "ident"], writes=["ptr"])
                if int(os.environ.get('MIX_X', '3')) & 1:
                    S.act(lambda e: e.copy(xtm, ptx[:, 0:256]), reads=["ptr"], writes=["xtm"])
                if int(os.environ.get('MIX_X', '3')) & 2:
                    S.dve(lambda e: e.tensor_copy(btm, ptx[:, 256:384]), reads=["ptr"], writes=["btm"])
                if int(os.environ.get('MIX_SSD', '9')) < 1:
                    continue
                S.dve(lambda e: e.tensor_tensor(adt, dtt, aneg, ALU.mult), reads=["dtt", "aneg"], writes=["adt"])
                S.pe(lambda e: e.matmul(psS[:, 0:4], utri, adt, start=True, stop=True), reads=["utri", "adt"], writes=["psS_a"])
                S.act(lambda e: e.copy(acol, psS[:, 0:4]), reads=["psS_a"], writes=["acol"])
                S.dve(lambda e: e.tensor_scalar(nacol, psS[:, 0:4], -1.0, None, ALU.mult), reads=["psS_a"], writes=["nacol"])
                S.act(lambda e: e.activation(eac, acol, AF.Exp), reads=["acol"], writes=["eac"])
                if int(os.environ.get('MIX_SSD', '9')) < 2:
                    continue
                S.pe(lambda e, j=j: e.matmul(psS[:, 256:384], xbcT[:, 2, j * 128:(j + 1) * 128], xbcT[:, 3, j * 128:(j + 1) * 128], start=True, stop=True),
                     reads=["xbcT2", "xbcT3"], writes=["psS_c"])
                S.dve(lambda e: e.tensor_tensor(cbm, psS[:, 256:384], utri, ALU.mult), reads=["psS_c", "utri"], writes=["cbm"])
                if int(os.environ.get('MIX_SSD', '9')) < 3:
                    continue
                for h in range(4):
                    S.dve(lambda e, h=h: e.tensor_scalar(adtrep, utri, 0.0, adt[:, h:h + 1], ALU.mult, ALU.add), reads=["utri", "adt"], writes=["adtrep"])
                    S.pe(lambda e: e.matmul(psS[:, 128:256], adtrep, utri, start=True, stop=True), reads=["adtrep", "utri"], writes=["psS_r"])
                    S.dve(lambda e, h=h: e.tensor_scalar(seg, psS[:, 128:256], acol[:, h:h + 1], 0.0, ALU.subtract, ALU.min), reads=["psS_r", "acol"], writes=["seg"])
                    S.act(lambda e: e.activation(dec, seg, AF.Exp), reads=["seg"], writes=["dec"])
                    S.dve(lambda e, h=h: e.tensor_tensor(MT[:, h, :], dec, cbm, ALU.mult), reads=["dec", "cbm"], writes=["MT%d" % h])
                    S.dve(lambda e, h=h: e.tensor_copy(alast[:, h:h + 1], psS[:, 255:256]), reads=["psS_r"], writes=["alast%d" % h])
                    S.act(lambda e, h=h: e.activation(dsv[:, h:h + 1], nacol[:, h:h + 1], AF.Exp, bias=alast[:, h:h + 1]), reads=["nacol", "alast%d" % h], writes=["dsv%d" % h])
                    S.act(lambda e, h=h: e.activation(cdv[:, h:h + 1], alast[:, h:h + 1], AF.Exp), reads=["alast%d" % h], writes=["cdv%d" % h])
                if int(os.environ.get('MIX_SSD', '9')) < 4:
                    continue
                dsr = ["dsv%d" % h for h in range(4)]
                S.dve(lambda e: e.tensor_tensor(wsc, dtt, dsv, ALU.mult), reads=["dtt"] + dsr, writes=["wsc"])
                xv = xtm.rearrange("p (h d) -> p h d", h=4)
                S.dve(lambda e, xv=xv: e.tensor_tensor(xw.rearrange("p (h d) -> p h d", h=4), xv, wsc.unsqueeze(2).to_broadcast([128, 4, 64]), ALU.mult),
                      reads=["xtm", "wsc"], writes=["xw"])
                S.dve(lambda e, xv=xv: e.tensor_tensor(xdt.rearrange("p (h d) -> p h d", h=4), xv, dtt.unsqueeze(2).to_broadcast([128, 4, 64]), ALU.mult),
                      reads=["xtm", "dtt"], writes=["xdt"])
                if int(os.environ.get('MIX_SSD', '9')) < 5:
                    continue
                S.pool(lambda e: e.tensor_copy(hstb, hst), reads=["hst"], writes=["hstb"])
                S.pe(lambda e, j=j: e.matmul(psY[:, 0:256], xbcT[:, 3, j * 128:(j + 1) * 128], hstb, start=True, stop=True), reads=["xbcT3", "hstb"], writes=["psY_o"])
                for h in range(4):
                    S.pe(lambda e, h=h: e.matmul(psY[:, 256 + h * 64:256 + (h + 1) * 64], MT[:, h, :], xdt[:, h * 64:(h + 1) * 64], start=True, stop=True),
                         reads=["MT%d" % h, "xdt"], writes=["psY_d"])
                S.pe(lambda e: e.matmul(psN, btm, xw, start=True, stop=True), reads=["btm", "xw"], writes=["psN"])
                cdr = ["cdv%d" % h for h in range(4)]
                for h in range(4):
                    S.dve(lambda e, h=h: e.scalar_tensor_tensor(hst[:, h * 64:(h + 1) * 64], hst[:, h * 64:(h + 1) * 64], cdv[:, h:h + 1], psN[:, h * 64:(h + 1) * 64], ALU.mult, ALU.add),
                          reads=["hst", "psN"] + cdr, writes=["hst"])
                S.act(lambda e: e.copy(ydg, psY[:, 256:512]), reads=["psY_d"], writes=["ydg"])
                for h in range(4):
                    hs_ = slice(h * 64, (h + 1) * 64)
                    S.dve(lambda e, h=h, hs_=hs_: e.scalar_tensor_tensor(yy[:, hs_], psY[:, hs_], eac[:, h:h + 1], ydg[:, hs_], ALU.mult, ALU.add),
                          reads=["psY_o", "eac", "ydg"], writes=["yy"])
                    S.dve(lambda e, h=h, hs_=hs_: e.scalar_tensor_tensor(yy[:, hs_], xtm[:, hs_], hpb[:, 8 + h:9 + h], yy[:, hs_], ALU.mult, ALU.add),
                          reads=["xtm", "hpb", "yy"], writes=["yy"])
                ob = tt % 2
                S.dve(lambda e, ob=ob: e.tensor_tensor(yo[ob], yy, zs, ALU.mult), reads=["yy", "zs"], writes=["yo%d" % ob])
                S.dma("sp", T["mixo"][tok, 0:256], yo[ob], reads=["yo%d" % ob], writes=["mixo_s%d" % tt])

    if int(os.environ.get('MIX_STOP', '9')) <= 1:
        return
    bar()
    A.reset(base_persist)
    w1 = A.alloc([32, 256], BF16)
    S.dma("pool", w1[0:64], T["w1k"].rearrange("d (l h) -> d l h", l=32), writes=["w1k"])
    S.dma("pool", w1[64:128], T["w1v"].rearrange("d (l h) -> d l h", l=32), writes=["w1v"])
    pe_ = A.alloc([32], BF16)
    S.dma("pool", pe_[0:64], T["pek"], writes=["pek"])
    S.dma("pool", pe_[64:128], T["pev"], writes=["pev"])
    w2 = A.alloc([2, 2, 64], BF16)
    S.dma("pool", w2[:, 0], T["w2k"].rearrange("(c p) d -> p c d", p=128), writes=["w2k"])
    S.dma("pool", w2[:, 1], T["w2v"].rearrange("(c p) d -> p c d", p=128), writes=["w2v"])
    cbias = A.alloc([4], F32)
    hsb = A.alloc([4, 256], BF16)
    KcT = A.alloc([256], BF16)
    VcA = A.alloc([2, 129], BF16)
    S.pool(lambda e: e.memset(hsb, 0.0), writes=["hsb"])
    S.pool(lambda e: e.memset(VcA, 0.0), writes=["VcA"])
    S.pool(lambda e: e.memset(KcT, 0.0), writes=["KcT"])
    S.pool(lambda e: e.memset(VcA[:, :, 64:65], 1.0), reads=["VcA"], writes=["VcA"])
    S.dma("pool", VcA[:, :, 65:129], T["ovl"].rearrange("(c p) j -> p c j", p=128), reads=["VcA"], writes=["VcA"])
    for kv in range(2):
        rows = slice(kv * 64, (kv + 1) * 64)
        for hc in range(2):
            idx = kv * 2 + hc
            pb_ = pbig[:, idx, 0:255]
            pbias = pbig[:, 4, idx:idx + 1]
            for l in range(32):
                S.pe(lambda e, rows=rows, hc=hc, l=l, pbias=pbias: e.matmul(pbias, w1[rows, l, hc * 128:(hc + 1) * 128], pe_[rows, l:l + 1], start=(l == 0), stop=(l == 31)),
                     reads=["w1k", "w1v", "pek", "pev"], writes=["pbias%d" % idx])
            S.act(lambda e, idx=idx, pbias=pbias: e.copy(cbias[:, idx:idx + 1], pbias), reads=["pbias%d" % idx], writes=["cbias%d" % idx])
            for l in range(32):
                S.pe(lambda e, rows=rows, hc=hc, l=l, pb_=pb_: e.matmul(pb_, w1[rows, l, hc * 128:(hc + 1) * 128], kcvcT[rows, l:l + 16 * 254 + 1:16], start=(l == 0), stop=(l == 31)),
                     reads=["w1k", "w1v", "kcvcT"], writes=["phid%d" % idx])
            S.act(lambda e, idx=idx, pb_=pb_: e.activation(hsb[:, idx, 0:255], pb_, AF.Silu, bias=cbias[:, idx:idx + 1]), reads=["phid%d" % idx, "cbias%d" % idx, "hsb"], writes=["hsb%d" % idx])
    pk = pbig[0:64, 5, 0:255]
    for hc in range(2):
        S.pe(lambda e, hc=hc: e.matmul(pk, w2[:, 0, hc, :], hsb[:, hc, 0:255], start=(hc == 0), stop=(hc == 1)), reads=["w2k", "hsb0", "hsb1"], writes=["pk"])
    S.act(lambda e: e.copy(KcT[0:64, 0:255], pk), reads=["pk", "KcT"], writes=["KcT"])
    for it in range(2):
        m = 128 if it == 0 else 127
        pv = pbig[0:m, 6 + it, 0:64]
        for hc in range(2):
            S.pe(lambda e, it=it, hc=hc, m=m, pv=pv: e.matmul(pv, hsb[:, 2 + hc, it * 128:it * 128 + m], w2[:, 1, hc, :], start=(hc == 0), stop=(hc == 1)),
                 reads=["w2v", "hsb2", "hsb3"], writes=["pv%d" % it])
        S.act(lambda e, it=it, m=m, pv=pv: e.copy(VcA[0:m, it, 0:64], pv), reads=["pv%d" % it, "VcA"], writes=["VcA"])

    if int(os.environ.get('MIX_STOP', '9')) <= 2:
        return
    bar()
    base3 = A.off
    qu = [A.alloc([4, 512], BF16) for _ in range(2)]
    cmk = [A.alloc([2, 512], BF16) for _ in range(2)]
    PcT = [A.alloc([2, 512], BF16) for _ in range(2)]
    imp = A.alloc([4, 64], F32)
    imp2 = A.alloc([64], F32)
    m8 = A.alloc([16], F32)
    thr = A.alloc([1], F32)
    rr = A.alloc([2], F32)
    nmb = A.alloc([128], BF16)
    S.pool(lambda e: e.memset(nmb, 0.0), writes=["nmb"])
    pnt = pbf(pbig, 7, 8)[:, 0:128]
    for Q in range(8):
        qb = Q % 2
        qs = slice(Q * 512, (Q + 1) * 512)
        S.dma("sp", qu[qb][0:64], T["qu_d"][:, :, qs], writes=["qu%d" % qb])
        S.dma("pool", cmk[qb], T["cmask"][:, qs].rearrange("(c p) t -> p c t", p=128), writes=["cmk%d" % qb])
        for h in range(4):
            pb2 = h % 2
            for it in range(2):
                ps_ = pbig[:, pb2 * 2 + it, :]
                S.pe(lambda e, it=it, h=h, qb=qb, ps_=ps_: e.matmul(ps_, KcT[0:64, it * 128:(it + 1) * 128], qu[qb][0:64, h, :], start=True, stop=False),
                     reads=["KcT", "qu%d" % qb], writes=["psc%d_%d" % (pb2, it)])
                S.pe(lambda e, it=it, qb=qb, ps_=ps_: e.matmul(ps_, identb, cmk[qb][:, it, :], start=False, stop=True),
                     reads=["ident", "cmk%d" % qb], writes=["psc%d_%d" % (pb2, it)])
                S.act(lambda e, it=it, pb2=pb2, ps_=ps_: e.activation(PcT[pb2][:, it, :], ps_, AF.Exp, scale=SCALE), reads=["psc%d_%d" % (pb2, it)], writes=["PcT%d_%d" % (pb2, it)])
            for sub in range(4):
                tt = Q * 4 + sub
                po = pbig[:, 4 + (sub % 2), 0:129]
                for it in range(2):
                    S.pe(lambda e, it=it, pb2=pb2, sub=sub, po=po: e.matmul(po, PcT[pb2][:, it, sub * 128:(sub + 1) * 128], VcA[:, it, :], start=(it == 0), stop=(it == 1)),
                         reads=["PcT%d_0" % pb2, "PcT%d_1" % pb2, "VcA"], writes=["po%d" % (sub % 2)])
                pr = ["po%d" % (sub % 2)]
                S.dve(lambda e, po=po: e.tensor_scalar(rr[:, 0:1], po[:, 64:65], 1e-30, None, ALU.add), reads=pr, writes=["rr0"])
                S.dve(lambda e: e.reciprocal(rr[:, 0:1], rr[:, 0:1]), reads=["rr0"], writes=["rr0"])
                if h == 0:
                    S.dve(lambda e, po=po, sub=sub: e.tensor_scalar(imp[:, sub, :], po[:, 65:129], rr[:, 0:1], None, ALU.mult), reads=pr + ["rr0"], writes=["imp%d" % sub])
                else:
                    S.dve(lambda e, po=po, sub=sub: e.scalar_tensor_tensor(imp[:, sub, :], po[:, 65:129], rr[:, 0:1], imp[:, sub, :], ALU.mult, ALU.add),
                          reads=pr + ["rr0", "imp%d" % sub], writes=["imp%d" % sub])
                S.dve(lambda e, tt=tt, h=h: e.tensor_tensor(rr[:, 1:2], rr[:, 0:1], gate[:, tt, h * 3:h * 3 + 1], ALU.mult), reads=["rr0", "gate"], writes=["rr1"])
                S.dve(lambda e, po=po, tt=tt, h=h: e.tensor_scalar(attO[:, tt, h * 64:(h + 1) * 64], po[:, 0:64], rr[:, 1:2], None, ALU.mult), reads=pr + ["rr1"], writes=["attO%d" % tt])
        for sub in range(4):
            tt = Q * 4 + sub
            ir = ["imp%d" % sub]
            im = imp[:, sub, :]
            S.pool(lambda e, im=im: e.memset(im[:, 0:1], 1e4), reads=ir, writes=ir)
            lo = max(2 * tt - 1, 0)
            S.pool(lambda e, im=im, lo=lo, tt=tt: e.memset(im[0:64, lo:2 * tt + 1], 1e4), reads=ir, writes=ir)
            S.pool(lambda e, im=im, tt=tt: e.memset(im[64:128, 2 * tt:2 * tt + 2], 1e4), reads=ir, writes=ir)
            if 2 * tt + 1 < 64:
                S.pool(lambda e, im=im, tt=tt: e.memset(im[0:64, 2 * tt + 1:64], -1.0), reads=ir, writes=ir)
            if 2 * tt + 2 < 64:
                S.pool(lambda e, im=im, tt=tt: e.memset(im[64:128, 2 * tt + 2:64], -1.0), reads=ir, writes=ir)
            S.dve(lambda e, im=im: e.max(m8[:, 0:8], im), reads=ir, writes=["m8a"])
            S.dve(lambda e, im=im: e.match_replace(imp2, m8[:, 0:8], im, -1e30), reads=ir + ["m8a"], writes=["imp2"])
            S.dve(lambda e: e.max(m8[:, 8:16], imp2), reads=["imp2"], writes=["m8b"])
            S.dve(lambda e: e.tensor_scalar(thr, m8[:, 15:16], 0.0, None, ALU.max), reads=["m8b"], writes=["thr"])
            S.dve(lambda e, im=im: e.tensor_scalar(nmb[:, 64:128], im, thr, NEG, ALU.is_lt, ALU.mult), reads=ir + ["thr", "nmb"], writes=["nmb"])
            S.pe(lambda e: e.transpose(pnt, nmb, identb), reads=["nmb", "ident"], writes=["pnt"])
            S.act(lambda e, tt=tt: e.copy(nmT[64:128, tt * 128:(tt + 1) * 128], pnt[64:128, :]), reads=["pnt"], writes=["nmT"])

    if int(os.environ.get('MIX_STOP', '9')) <= 3:
        return
    bar()
    A.reset(base3)
    Qa = [A.alloc([4, 512], BF16) for _ in range(2)]
    PT = [A.alloc([512], BF16) for _ in range(3)]
    PW = [A.alloc([128], BF16) for _ in range(3)]
    r4 = A.alloc([2], F32)
    pti = 0
    pwi = 0
    for Q in range(8):
        qb = Q % 2
        qs = slice(Q * 512, (Q + 1) * 512)
        S.dma("sp", Qa[qb][0:64], T["qr_d"][:, :, qs], writes=["Qa%d" % qb])
        for h in range(4):
            S.pool(lambda e, qb=qb, h=h, qs=qs: e.tensor_copy(Qa[qb][64:128, h, :], nmT[64:128, qs]), reads=["nmT"], writes=["Qm%d_%d" % (qb, h)])
        for h in range(4):
            qr_ = ["Qa%d" % qb, "Qm%d_%d" % (qb, h)]
            posel = pbig[:, 6, 0:260].rearrange("p (s c) -> p s c", s=4)
            powin = pbig[:, 7, 0:260].rearrange("p (s c) -> p s c", s=4)
            S.dve(lambda e: e.memset(pbig[:, 6, 0:260], 0.0), writes=["posel"])
            for kt in range(4 * Q + 4):
                ks_ = slice(kt * 128, (kt + 1) * 128)
                sb_ = kt % 3
                ps_ = pbig[:, sb_, :]
                pres = "pss%d" % sb_
                o = kt - 4 * Q
                if o < 0:
                    S.pe(lambda e, ks_=ks_, qb=qb, h=h, ps_=ps_: e.matmul(ps_, KsA[:, ks_], Qa[qb][:, h, :], start=True, stop=True),
                         reads=["KsA", "KsE"] + qr_, writes=[pres])
                    lo = 0
                else:
                    lo = o * 128
                    S.pe(lambda e, ks_=ks_, qb=qb, h=h, ps_=ps_, lo=lo: e.matmul(ps_[:, lo:lo + 128], KsA[:, ks_], Qa[qb][:, h, lo:lo + 128], start=True, stop=False),
                         reads=["KsA", "KsE"] + qr_, writes=[pres])
                    S.pe(lambda e, ps_=ps_, lo=lo: e.matmul(ps_[:, lo:lo + 128], identb, tricb, start=False, stop=True), reads=["ident", "tric"], writes=[pres])
                    if o < 3:
                        S.pe(lambda e, ks_=ks_, qb=qb, h=h, ps_=ps_, lo=lo: e.matmul(ps_[:, lo + 128:512], KsA[:, ks_], Qa[qb][:, h, lo + 128:512], start=True, stop=True),
                             reads=["KsA", "KsE"] + qr_, writes=[pres])
                pt_ = PT[pti % 3]
                ptres = "PT%d" % (pti % 3)
                pti += 1
                S.act(lambda e, ps_=ps_, pt_=pt_, lo=lo: e.activation(pt_[:, lo:512], ps_[:, lo:512], AF.Exp, scale=SCALE), reads=[pres], writes=[ptres])
                for sub in range(max(o, 0), 4):
                    S.pe(lambda e, pt_=pt_, sub=sub, kt=kt, Q=Q, posel=posel: e.matmul(posel[:, sub, :], pt_[:, sub * 128:(sub + 1) * 128], VsA[:, kt, :],
                                                                                 start=False, stop=False, skip_group_check=True),
                         reads=[ptres, "VsA"], writes=["posel"])
            for sub in range(4):
                tt = 4 * Q + sub
                kts = [k for k in range(tt - 4, tt + 1) if k >= 0]
                for kt in kts:
                    ks_ = slice(kt * 128, (kt + 1) * 128)
                    wsl = pwi % 4
                    psw = pbig[:, 3 + wsl // 2, (wsl % 2) * 128:(wsl % 2) * 128 + 128]
                    pwres = "psw%d" % wsl
                    msk = triab if kt == tt - 4 else (tricb if kt == tt else None)
                    S.pe(lambda e, ks_=ks_, qb=qb, h=h, sub=sub, psw=psw, msk=msk: e.matmul(psw, KwT[0:64, ks_], Qa[qb][0:64, h, sub * 128:(sub + 1) * 128], start=True, stop=(msk is None)),
                         reads=["KwT", "Qa%d" % qb], writes=[pwres])
                    if msk is not None:
                        S.pe(lambda e, psw=psw, msk=msk: e.matmul(psw, identb, msk, start=False, stop=True), reads=["ident", "tric", "tria"], writes=[pwres])
                    pw_ = PW[pwi % 3]
                    pwr = "PW%d" % (pwi % 3)
                    pwi += 1
                    S.act(lambda e, psw=psw, pw_=pw_: e.activation(pw_, psw, AF.Exp, scale=SCALE), reads=[pwres], writes=[pwr])
                    S.pe(lambda e, pw_=pw_, sub=sub, kt=kt, kts=kts, powin=powin: e.matmul(powin[:, sub, :], pw_, VwA[:, kt, :], start=(kt == kts[0]), stop=(kt == kts[-1])),
                         reads=[pwr, "VwA"], writes=["powin"])
            for sub in range(4):
                tt = 4 * Q + sub
                for br, (po_, pres) in enumerate(((posel, "posel"), (powin, "powin"))):
                    S.dve(lambda e, po_=po_, sub=sub, br=br: e.reciprocal(r4[:, br:br + 1], po_[:, sub, 64:65]), reads=[pres], writes=["r4_%d" % br])
                    S.dve(lambda e, tt=tt, h=h, br=br: e.tensor_tensor(r4[:, br:br + 1], r4[:, br:br + 1], gate[:, tt, h * 3 + 1 + br:h * 3 + 2 + br], ALU.mult),
                          reads=["r4_%d" % br, "gate"], writes=["r4_%d" % br])
                    S.dve(lambda e, po_=po_, sub=sub, tt=tt, h=h, br=br: e.scalar_tensor_tensor(attO[:, tt, h * 64:(h + 1) * 64], po_[:, sub, 0:64], r4[:, br:br + 1],
                                                                                                  attO[:, tt, h * 64:(h + 1) * 64], ALU.mult, ALU.add),
                          reads=[pres, "r4_%d" % br, "attO%d" % tt], writes=["attO%d" % tt])
        for sub in range(4):
            tt = 4 * Q + sub
            S.dma("sp", T["mixo"][tt * 128:(tt + 1) * 128, 256:512], attO[:, tt, :], reads=["attO%d" % tt])


def _perm_cols():
    return None


def run_mix(inputs):
    if "mix" not in _CACHE:
        _CACHE["mix"] = build_mix()
    nc = _CACHE["mix"]
    x = inputs["x"]
    w_in = inputs["w_in"][0]
    offs = np.cumsum([0, 1024, 1536, 16, 1024, 256, 256, 256, 256, 256, 256, 48])
    oz, oxbc, odt, oq, okc, ovc, oks, ovs, okw, ovw, ogate = offs[:11]
    conv_w = inputs["conv_w"][0]
    conv_b = inputs["conv_b"][0]
    t = np.arange(SEQ, dtype=np.float32)
    inv = (1.0 / (500000.0 ** (np.arange(0, 16, 2, dtype=np.float32) / np.float32(16)))).astype(np.float32)
    ang = (t[:, None] * inv[None, :]).astype(np.float32)
    rope = np.concatenate([np.cos(ang), np.sin(ang)], 1).astype(np.float32)
    rope = np.ascontiguousarray(rope.reshape(NTT, 128, 16).transpose(1, 0, 2).reshape(128, NTT * 16))
    ident = np.eye(128, dtype=np.float32)
    kk = np.arange(128)[:, None]
    qq = np.arange(128)[None, :]
    tric = np.where(kk <= qq, 0.0, NEG).astype(np.float32)
    tria = np.where(kk > qq, 0.0, NEG).astype(np.float32)
    utri = (kk <= qq).astype(np.float32)
    emat = (np.arange(SEQ)[None, :] // 64 == np.arange(64)[:, None]).astype(np.float32)
    ii = np.arange(256)[:, None]
    cmask = np.where((16 * ii + 31 <= np.arange(SEQ)[None, :]) & (ii < 255), 0.0, NEG).astype(np.float32)
    cs = np.arange(255)[:, None] * 16
    ss_ = np.arange(64)[None, :] * 64
    ov = np.clip(np.minimum(cs + 32, ss_ + 64) - np.maximum(cs, ss_), 0, None) / 32.0
    ovl = np.zeros((256, 64), np.float32)
    ovl[:255] = ov
    in_maps = []
    for c in range(8):
        b, g = c // 4, c % 4
        grp = g // 2
        ar = np.arange
        tm_cols = np.concatenate([oz + 256 * g + ar(256), odt + 4 * g + ar(4), ogate + 12 * g + ar(12), ovs + 64 * g + ar(64), ovw + 64 * g + ar(64),
                                  oq + 256 * g + ar(256), oks + 64 * g + ar(64), okw + 64 * g + ar(64)])
        xcols = np.concatenate([256 * g + ar(256), 1024 + 128 * grp + ar(128), 1280 + 128 * grp + ar(128)])
        fm_cols = np.concatenate([oxbc + xcols, okc + 64 * g + ar(64), ovc + 64 * g + ar(64)])
        convw = np.ascontiguousarray(conv_w[:, xcols].T.reshape(4, 128, 4).transpose(1, 0, 2).reshape(128, 16))
        convb = np.ascontiguousarray(conv_b[xcols].reshape(4, 128).T)
        hp = np.concatenate([inputs["dt_bias"][0][4 * g:4 * g + 4], inputs["a_log"][0][4 * g:4 * g + 4], inputs["d_skip"][0][4 * g:4 * g + 4]]).astype(np.float32)
        in_maps.append(dict(
            xb=np.ascontiguousarray(x[b]), anw=inputs["attn_norm_w"][0], w_tm=np.ascontiguousarray(w_in[:, tm_cols]), w_fm=np.ascontiguousarray(w_in[:, fm_cols]),
            convw=convw, convb=convb, hp=hp, rope=rope,
            w1k=np.ascontiguousarray(inputs["cmp_w1_k"][0].reshape(32, 64, 256).transpose(1, 0, 2).reshape(64, 32 * 256)),
            w1v=np.ascontiguousarray(inputs["cmp_w1_v"][0].reshape(32, 64, 256).transpose(1, 0, 2).reshape(64, 32 * 256)),
            w2k=inputs["cmp_w2_k"][0], w2v=inputs["cmp_w2_v"][0],
            pek=np.ascontiguousarray(inputs["cmp_pe_k"][0].T), pev=np.ascontiguousarray(inputs["cmp_pe_v"][0].T),
            ident=ident, emat=emat, tric=tric, tria=tria, cmask=cmask, ovl=ovl, utri=utri,
        ))
    res = run_bass_kernel_spmd(nc, in_maps, core_ids=list(range(8)))
    mixed = np.empty((2, SEQ, D), np.float32)
    for c in range(8):
        b, g = c // 4, c % 4
        m = res.results[c]["mixo"]
        mixed[b, :, 256 * g:256 * (g + 1)] = m[:, 0:256]
        mixed[b, :, 1024 + 256 * g:1024 + 256 * (g + 1)] = m[:, 256:512]
    return mixed


def kernel(**inputs):
    inputs = {k: np.asarray(v) for k, v in inputs.items()}
    mixed = run_mix(inputs)
    return run_tail(inputs["x"], mixed, inputs["ssd_norm_w"][0], inputs["w_out"][0], inputs["ffn_norm_w"][0],
                    inputs["w_gate"][0], inputs["w_up"][0], inputs["w_down"][0], inputs["final_norm_w"])
```

```python
import contextlib
import os
import numpy as np
import ml_dtypes
import concourse.bass as bass
import concourse.mybir as mybir
from concourse.bass_utils import run_bass_kernel_spmd

F32 = mybir.dt.float32
BF16 = mybir.dt.bfloat16
U8 = mybir.dt.uint8
ALU = mybir.AluOpType
AF = mybir.ActivationFunctionType
AX = mybir.AxisListType

ENGS = ("pe", "act", "dve", "pool", "sp")
EPS = 1e-6


class _Op:
    __slots__ = ("eng", "fn", "dma", "deps", "idx", "signal", "val", "sem", "cc", "cost")


class Sched:
    def __init__(self, nc, n_dma_sems=8):
        self.nc = nc
        self.ops = []
        self.last_w = {}
        self.readers = {}
        self.n_dma_sems = n_dma_sems
        self.bar = None
        self.bank_of = {}

    def op(self, eng, fn, reads=(), writes=(), dma=False, c=None):
        o = _Op()
        o.cost = c
        o.eng, o.fn, o.dma = eng, fn, dma
        o.cc = False
        o.idx = len(self.ops)
        o.signal = False
        deps = {}
        if self.bar is not None:
            deps[self.bar] = "raw"
        for r in reads:
            w = self.last_w.get(r)
            if w is not None:
                deps[w] = "raw"
        for w_ in writes:
            w = self.last_w.get(w_)
            if w is not None:
                deps[w] = "raw"
            for r in self.readers.get(w_, ()):
                if r not in deps:
                    deps[r] = "war"
        for r in reads:
            self.readers.setdefault(r, []).append(o.idx)
        for w_ in writes:
            self.last_w[w_] = o.idx
            self.readers[w_] = []
        banks = set()
        for r in tuple(reads) + tuple(writes):
            banks.update(self.bank_of.get(r, ()))
        for b in banks:
            key = ("bank", b)
            w = self.last_w.get(key)
            if w is not None and w not in deps:
                deps[w] = "bank"
            self.last_w[key] = o.idx
        deps.pop(o.idx, None)
        o.deps = deps
        self.ops.append(o)
        return o

    def pe(self, fn, reads=(), writes=(), c=None):
        return self.op("pe", fn, reads, writes, c=c)

    def act(self, fn, reads=(), writes=(), c=None):
        return self.op("act", fn, reads, writes, c=c)

    def dve(self, fn, reads=(), writes=(), c=None):
        return self.op("dve", fn, reads, writes, c=c)

    def pool(self, fn, reads=(), writes=(), c=None):
        return self.op("pool", fn, reads, writes, c=c)

    DEF_COST = {"pe": 0.12, "act": 0.35, "dve": 0.25, "pool": 0.35}

    def reorder(self, window=40):
        ops = self.ops
        n = len(ops)
        queues = {e: [] for e in ENGS}
        for o in ops:
            queues[o.eng].append(o.idx)
        head = {e: 0 for e in ENGS}
        sched = [False] * n
        fin = [0.0] * n
        etime = {e: 0.0 for e in ENGS}
        order = []
        left = n
        while left:
            best = None
            for e in ENGS:
                q = queues[e]
                h = head[e]
                while h < len(q) and sched[q[h]]:
                    h += 1
                head[e] = h
                if h >= len(q):
                    continue
                seen = 0
                i = h
                et = etime[e]
                while i < len(q) and seen < window:
                    k = q[i]
                    i += 1
                    if sched[k]:
                        continue
                    seen += 1
                    o = ops[k]
                    rdy = 0.0
                    ok = True
                    for d in o.deps:
                        if not sched[d]:
                            ok = False
                            break
                        f = fin[d] + (0.0 if ops[d].eng == e else 0.15)
                        if f > rdy:
                            rdy = f
                    if not ok:
                        continue
                    st = rdy if rdy > et else et
                    key = (st, k)
                    if best is None or key < best[0]:
                        best = (key, e, k)
                    if st <= et:
                        break
            assert best is not None, "scheduler stuck"
            (st, k), e, _ = best
            o = ops[k]
            if o.dma:
                etime[e] = st + 0.06
                fin[k] = st + (o.cost if o.cost is not None else 3.0)
            else:
                c = o.cost if o.cost is not None else self.DEF_COST[e]
                etime[e] = st + c
                fin[k] = st + c
            sched[k] = True
            order.append(k)
            left -= 1
        self.order = order
        self.est_time = max(fin) if fin else 0.0

    def dma(self, q, out, in_, reads=(), writes=(), c=None):
        return self.op(q, lambda e: e.dma_start(out=out, in_=in_), reads, writes, dma=True, c=c)

    def cc(self, fn, reads=(), writes=()):
        o = self.op("pool", fn, reads, writes, dma=True)
        o.cc = True
        return o

    def barrier(self, out, in_):
        allres = set(self.last_w.keys()) | set(self.readers.keys())
        o = self.op("sp", lambda e: e.dma_start(out=out, in_=in_), reads=(), writes=tuple(allres), dma=True)
        self.bar = o.idx
        self.last_w = {}
        self.readers = {}
        return o

    def emit(self, sems, block, final_wait_eng="sp"):
        ops = self.ops
        need = [False] * len(ops)
        for o in ops:
            for d, kind in o.deps.items():
                do = ops[d]
                if do.dma:
                    continue
                if do.eng == o.eng and not o.dma:
                    if do.eng == "pe" or kind == "bank":
                        continue
                need[d] = True
        cnt = {e: 0 for e in ENGS}
        dcnt = {}
        dval = {}
        per_eng = {e: [] for e in ENGS}
        order = getattr(self, "order", None) or list(range(len(ops)))
        for k_ in order:
            o = ops[k_]
            per_eng[o.eng].append(o)
            if o.dma and o.cc:
                o.sem = "cc"
                dval["cc"] = dval.get("cc", 0) + 1
                o.val = dval["cc"]
            elif o.dma:
                k = dcnt.get(o.eng, 0)
                dcnt[o.eng] = k + 1
                key = ("dma", o.eng, k % self.n_dma_sems)
                o.sem = key
                dval[key] = dval.get(key, 0) + 16
                o.val = dval[key]
            elif need[o.idx]:
                cnt[o.eng] += 1
                o.val = cnt[o.eng]
                o.sem = o.eng
                o.signal = True
        self.stats = {e: len(per_eng[e]) for e in ENGS}
        self.stats["signals"] = dict(cnt)

        def run(engname, e):
            waited = {}
            for o in per_eng[engname]:
                wl = {}
                for d, kind in o.deps.items():
                    do = ops[d]
                    if do.dma:
                        wl[do.sem] = max(wl.get(do.sem, 0), do.val)
                        continue
                    if do.eng == o.eng and not o.dma:
                        if do.eng == "pe" or kind == "bank":
                            continue
                    wl[do.sem] = max(wl.get(do.sem, 0), do.val)
                if o.dma and o.cc and o.val > 1:
                    wl[o.sem] = max(wl.get(o.sem, 0), o.val - 1)
                elif o.dma and not o.cc and o.val > 16:
                    wl[o.sem] = max(wl.get(o.sem, 0), o.val - 16)
                for s, v in wl.items():
                    if waited.get(s, 0) >= v:
                        continue
                    waited[s] = v
                    e.wait_ge(sems[s], v)
                ins = o.fn(e)
                if o.dma and o.cc:
                    ins.then_inc(sems[o.sem])
                elif o.dma:
                    ins.then_inc(sems[o.sem], 16)
                elif o.signal:
                    ins.then_inc(sems[o.sem], 1)
            if engname == final_wait_eng:
                for key, v in dval.items():
                    if waited.get(key, 0) < v:
                        e.wait_ge(sems[key], v)
                for en in ("pe", "act", "dve", "pool"):
                    if cnt[en] > 0 and waited.get(en, 0) < cnt[en]:
                        e.wait_ge(sems[en], cnt[en])

        @block.tensor
        def _(e):
            run("pe", e)

        @block.scalar
        def _(e):
            run("act", e)

        @block.vector
        def _(e):
            run("dve", e)

        @block.gpsimd
        def _(e):
            run("pool", e)

        @block.sync
        def _(e):
            run("sp", e)


def make_sems(nc, stack, n_dma_sems=8, queues=("sp", "pool", "act")):
    sems = {}
    for e in ("pe", "act", "dve", "pool", "cc"):
        sems[e] = stack.enter_context(nc.semaphore("s_" + e))
    for q in queues:
        for i in range(n_dma_sems):
            sems[("dma", q, i)] = stack.enter_context(nc.semaphore("d_%s_%d" % (q, i)))
    return sems


_DTSZ = {F32: 4, BF16: 2, U8: 1}


class Arena:
    def __init__(self, ar, size):
        self.ar, self.size, self.off = ar, size, 0

    def reset(self, off=0):
        self.off = off

    def alloc(self, shape, dtype, parts=128):
        n = int(np.prod(shape)) * _DTSZ[dtype]
        off = (self.off + 63) // 64 * 64
        assert off + n <= self.size, ("arena overflow", off, n, self.size)
        self.off = off + n
        ap = self.ar[0:parts, off:off + n].bitcast(dtype)
        if len(shape) > 1:
            names = [chr(ord("a") + i) for i in range(len(shape))]
            pat = "p (%s) -> p %s" % (" ".join(names), " ".join(names))
            ap = ap.rearrange(pat, **{nm: int(s) for nm, s in zip(names, shape)})
        return ap


D = 2048
FF = 5632
TOK = 1024
NT = TOK // 128
KC = D // 128
FC = FF // 128
SSDW = 1024


def build_tail():
    nc = bass.Bass("TRN2", target_bir_lowering=False)
    x = nc.dram_tensor("x_own", [TOK, D], F32, kind="ExternalInput").ap()
    mix = nc.dram_tensor("mix", [TOK, D], F32, kind="ExternalInput").ap()
    w_out = nc.dram_tensor("w_out", [D, D], F32, kind="ExternalInput").ap()
    w_gate = nc.dram_tensor("w_gate", [D, FF], F32, kind="ExternalInput").ap()
    w_up = nc.dram_tensor("w_up", [D, FF], F32, kind="ExternalInput").ap()
    w_down = nc.dram_tensor("w_down", [FF, D], F32, kind="ExternalInput").ap()
    nw = nc.dram_tensor("nw", [3, D], F32, kind="ExternalInput").ap()
    ident = nc.dram_tensor("ident", [128, 128], F32, kind="ExternalInput").ap()
    out = nc.dram_tensor("out", [TOK, D], F32, kind="ExternalOutput").ap()
    h_d = nc.dram_tensor("h_d", [TOK, D], F32, kind="Internal").ap()
    dummy = nc.dram_tensor("dummy_bar", [2, 64], F32, kind="Internal").ap()
    with contextlib.ExitStack() as st:
        ASZ = 200 * 1024
        ar = st.enter_context(nc.sbuf_tensor("arena", [128, ASZ], U8))
        A = Arena(ar, ASZ)
        pbig = st.enter_context(nc.psum_tensor("pbig", [128, 8, 512], F32))
        sems = make_sems(nc, st)
        block = st.enter_context(nc.Block())
        S = Sched(nc)
        tail_body(nc, S, A, pbig, x, mix, w_out, w_gate, w_up, w_down, nw, ident, out, h_d, dummy)
        if os.environ.get('NO_REORDER') is None:
            S.reorder()
        S.emit(sems, block)
    return nc


def rms_rstd(S, src, n, ss, sq, tag, rd, wr_extra=(), c=None):
    S.act(lambda e: e.activation(sq, src, AF.Square, accum_out=ss), reads=rd, writes=[tag + "ss", tag + "sq"], c=c)
    S.act(lambda e: e.activation(ss, ss, AF.Sqrt, scale=1.0 / n, bias=EPS_AP[0]), reads=[tag + "ss"], writes=[tag + "ss"])
    S.dve(lambda e: e.reciprocal(ss, ss), reads=[tag + "ss"], writes=[tag + "ss"])


EPS_AP = [None]


def tail_body(nc, S, A, pbig, x, mix, w_out, w_gate, w_up, w_down, nw, ident, out, h_d, dummy):
    bk = {"ptr": (4, 5)}
    for i_ in range(8):
        bk["pacc%d" % i_] = (i_,)
        bk["pd%d" % i_] = (i_,)
    for i_ in range(2):
        bk["pg%d" % i_] = (i_ * 4, i_ * 4 + 1)
        bk["pu%d" % i_] = (i_ * 4 + 2, i_ * 4 + 3)
    S.bank_of = bk
    identb = A.alloc([128], BF16)
    nwb = A.alloc([3, D], F32)
    epsb = A.alloc([1], F32)
    ss = A.alloc([4], F32)
    EPS_AP[0] = epsb
    S.pool(lambda e: e.memset(epsb, EPS), writes=["eps"])
    S.dma("pool", identb, ident, writes=["ident"])
    S.dma("sp", nwb, nw.rearrange("a b -> (a b)").partition_broadcast(128).rearrange("p (a b) -> p a b", a=3), writes=["nwb"])
    vT = A.alloc([KC, TOK], BF16)
    base_persist = A.off

    wo = A.alloc([KC, D], BF16)
    for half in range(2):
        S.dma("pool", wo[:, half * 8:(half + 1) * 8, :],
              w_out[half * 1024:(half + 1) * 1024, :].rearrange("(k p) n -> p k n", p=128), writes=["wo%d" % half])
    xt = [A.alloc([D], F32) for _ in range(2)]
    mt = [A.alloc([D], F32) for _ in range(2)]
    sq = A.alloc([D], F32)
    mb = A.alloc([D], BF16)
    mT = A.alloc([KC, 128], BF16)
    hs = [A.alloc([D], F32) for _ in range(2)]
    vb = A.alloc([D], BF16)
    pacc = pbig[:, 0:4, :]
    ptr_all = pbig[:, 4:6, :].rearrange("p a b -> p (a b)").bitcast(BF16)
    ptr = ptr_all.rearrange("p (k n) -> p k n", k=KC)

    def loads(tt):
        b = tt % 2
        S.dma("sp", xt[b], x[tt * 128:(tt + 1) * 128, :], writes=["xt%d" % b])
        S.dma("sp", mt[b], mix[tt * 128:(tt + 1) * 128, :], writes=["mt%d" % b])

    loads(0)
    for tt in range(NT):
        b = tt % 2
        if tt + 1 < NT:
            loads(tt + 1)
        rms_rstd(S, mt[b][:, 0:SSDW], SSDW, ss[:, 0:1], sq[:, 0:SSDW], "a", ["mt%d" % b, "eps"])
        S.dve(lambda e, b=b: e.scalar_tensor_tensor(mb[:, 0:SSDW], mt[b][:, 0:SSDW], ss[:, 0:1], nwb[:, 0, 0:SSDW], ALU.mult, ALU.mult),
              reads=["mt%d" % b, "ass", "nwb"], writes=["mb0"])
        S.pool(lambda e, b=b: e.tensor_copy(mb[:, SSDW:D], mt[b][:, SSDW:D]), reads=["mt%d" % b], writes=["mb1"])
        for kc in range(KC):
            S.pe(lambda e, kc=kc: e.transpose(ptr[:, kc, :], mb[:, kc * 128:(kc + 1) * 128], identb),
                 reads=["mb0", "mb1", "ident"], writes=["ptr"])
        S.act(lambda e: e.copy(mT[:, 0:8, :], ptr[:, 0:8, :]), reads=["ptr"], writes=["mTa"])
        S.dve(lambda e: e.tensor_copy(mT[:, 8:16, :], ptr[:, 8:16, :]), reads=["ptr"], writes=["mTb"])
        for cb in range(4):
            for kc in range(KC):
                S.pe(lambda e, cb=cb, kc=kc: e.matmul(pacc[:, cb, :], mT[:, kc, :], wo[:, kc, cb * 512:(cb + 1) * 512],
                                                       start=(kc == 0), stop=(kc == KC - 1)),
                     reads=["mTa", "mTb", "wo0", "wo1"], writes=["pacc%d" % cb])
            S.dve(lambda e, cb=cb, b=b: e.tensor_tensor(hs[b][:, cb * 512:(cb + 1) * 512], pacc[:, cb, :], xt[b][:, cb * 512:(cb + 1) * 512], ALU.add),
                  reads=["pacc%d" % cb, "xt%d" % b], writes=["hs%d_%d" % (b, cb)])
        hres = ["hs%d_%d" % (b, cb) for cb in range(4)]
        S.dma("sp", h_d[tt * 128:(tt + 1) * 128, :], hs[b], reads=hres, writes=["h_d%d" % tt])
        rms_rstd(S, hs[b], D, ss[:, 1:2], sq, "b", hres + ["eps"])
        S.dve(lambda e, b=b: e.scalar_tensor_tensor(vb, hs[b], ss[:, 1:2], nwb[:, 1, :], ALU.mult, ALU.mult),
              reads=hres + ["bss", "nwb"], writes=["vb"])
        for kc in range(KC):
            S.pe(lambda e, kc=kc: e.transpose(ptr[:, kc, :], vb[:, kc * 128:(kc + 1) * 128], identb),
                 reads=["vb", "ident"], writes=["ptr"])
        S.act(lambda e, tt=tt: e.copy(vT[:, 0:8, tt * 128:(tt + 1) * 128], ptr[:, 0:8, :]), reads=["ptr"], writes=["vT%da" % tt])
        S.dve(lambda e, tt=tt: e.tensor_copy(vT[:, 8:16, tt * 128:(tt + 1) * 128], ptr[:, 8:16, :]), reads=["ptr"], writes=["vT%db" % tt])

    S.barrier(dummy[1:2, :], ident[0:1, 0:64])
    A.reset(base_persist)
    hT = A.alloc([FC, TOK], BF16)
    base_b = A.off
    WB = 256
    NB = FF // WB
    wg = [A.alloc([KC, WB], BF16) for _ in range(2)]
    wu = [A.alloc([KC, WB], BF16) for _ in range(2)]
    sg = [A.alloc([TOK], F32) for _ in range(2)]
    for blk in range(NB):
        b = blk % 2
        S.dma("pool", wg[b], w_gate[:, blk * WB:(blk + 1) * WB].rearrange("(k p) n -> p k n", p=128), writes=["wg%d" % b])
        S.dma("pool", wu[b], w_up[:, blk * WB:(blk + 1) * WB].rearrange("(k p) n -> p k n", p=128), writes=["wu%d" % b])
        for j in range(WB // 128):
            fc = blk * (WB // 128) + j
            pb = fc % 2
            pg = pbig[:, pb * 4:pb * 4 + 2, :]
            pu = pbig[:, pb * 4 + 2:pb * 4 + 4, :]
            for hf in range(2):
                for kc in range(KC):
                    S.pe(lambda e, b=b, j=j, hf=hf, kc=kc, pg=pg: e.matmul(pg[:, hf, :], wg[b][:, kc, j * 128:(j + 1) * 128], vT[:, kc, hf * 512:(hf + 1) * 512],
                                                                         start=(kc == 0), stop=(kc == KC - 1)),
                         reads=["wg%d" % b, "vT"], writes=["pg%d" % pb])
            for hf in range(2):
                for kc in range(KC):
                    S.pe(lambda e, b=b, j=j, hf=hf, kc=kc, pu=pu: e.matmul(pu[:, hf, :], wu[b][:, kc, j * 128:(j + 1) * 128], vT[:, kc, hf * 512:(hf + 1) * 512],
                                                                         start=(kc == 0), stop=(kc == KC - 1)),
                         reads=["wu%d" % b, "vT"], writes=["pu%d" % pb])
            S.act(lambda e, pb=pb, pg=pg: e.activation(sg[pb], pg.rearrange("p a b -> p (a b)"), AF.Silu), reads=["pg%d" % pb], writes=["sg%d" % pb])
            S.dve(lambda e, pb=pb, pu=pu, fc=fc: e.tensor_tensor(hT[:, fc, :], sg[pb], pu.rearrange("p a b -> p (a b)"), ALU.mult),
                  reads=["sg%d" % pb, "pu%d" % pb], writes=["hT%d" % fc])

    S.barrier(dummy[1:2, :], ident[0:1, 0:64])
    A.reset(base_b)
    HK = 22
    wd = [A.alloc([HK, 512], BF16) for _ in range(2)]
    hl = [A.alloc([512], F32) for _ in range(2)]
    ys = [A.alloc([512], F32) for _ in range(2)]
    it = 0
    for r in range(4):
        for hf in range(2):
            b = (r * 2 + hf) % 2
            S.dma("pool", wd[b], w_down[hf * HK * 128:(hf + 1) * HK * 128, r * 512:(r + 1) * 512].rearrange("(k p) n -> p k n", p=128), writes=["wd%d" % b])
            for tt in range(NT):
                for k in range(HK):
                    kk = hf * HK + k
                    S.pe(lambda e, b=b, tt=tt, k=k, kk=kk: e.matmul(pbig[:, tt, :], hT[:, kk, tt * 128:(tt + 1) * 128], wd[b][:, k, :],
                                                                    start=(kk == 0), stop=(kk == FC - 1)),
                         reads=["wd%d" % b, "hT"], writes=["pd%d" % tt])
        for tt in range(NT):
            b = it % 2
            it += 1
            S.dma("sp", hl[b], h_d[tt * 128:(tt + 1) * 128, r * 512:(r + 1) * 512], reads=["h_d%d_%d" % (tt, r)], writes=["hl%d" % b])
            S.dve(lambda e, b=b, tt=tt: e.tensor_tensor(ys[b], pbig[:, tt, :], hl[b], ALU.add), reads=["pd%d" % tt, "hl%d" % b], writes=["ys%d" % b])
            S.dma("sp", h_d[tt * 128:(tt + 1) * 128, r * 512:(r + 1) * 512], ys[b], reads=["ys%d" % b], writes=["h_d%d_%d" % (tt, r)])

    S.barrier(dummy[1:2, :], ident[0:1, 0:64])
    A.reset(base_persist)
    yt = [A.alloc([D], F32) for _ in range(2)]
    ot = [A.alloc([D], F32) for _ in range(2)]
    sq2 = A.alloc([D], F32)
    for tt in range(NT):
        b = tt % 2
        S.dma("sp", yt[b], h_d[tt * 128:(tt + 1) * 128, :], writes=["yt%d" % b])
        rms_rstd(S, yt[b], D, ss[:, 2:3], sq2, "c", ["yt%d" % b, "eps"])
        S.dve(lambda e, b=b: e.scalar_tensor_tensor(ot[b], yt[b], ss[:, 2:3], nwb[:, 2, :], ALU.mult, ALU.mult),
              reads=["yt%d" % b, "css", "nwb"], writes=["ot%d" % b])
        S.dma("sp", out[tt * 128:(tt + 1) * 128, :], ot[b], reads=["ot%d" % b])


_CACHE = {}


def run_tail(x, mixed, ssd_norm_w, w_out, ffn_norm_w, w_gate, w_up, w_down, final_norm_w):
    if "tail" not in _CACHE:
        _CACHE["tail"] = build_tail()
    nc = _CACHE["tail"]
    nwv = np.ones((3, D), np.float32)
    nwv[0, :SSDW] = ssd_norm_w
    nwv[1] = ffn_norm_w
    nwv[2] = final_norm_w
    ident = np.eye(128, dtype=np.float32)
    in_maps = []
    for c in range(8):
        b, g = c // 4, c % 4
        in_maps.append({
            "x_own": np.ascontiguousarray(x[b, g * TOK:(g + 1) * TOK]),
            "mix": np.ascontiguousarray(mixed[b, g * TOK:(g + 1) * TOK]),
            "w_out": w_out, "w_gate": w_gate, "w_up": w_up, "w_down": w_down,
            "nw": nwv, "ident": ident,
        })
    res = run_bass_kernel_spmd(nc, in_maps, core_ids=list(range(8)))
    outp = np.empty((2, 4096, D), np.float32)
    for c in range(8):
        b, g = c // 4, c % 4
        outp[b, g * TOK:(g + 1) * TOK] = res.results[c]["out"]
    return outp


SEQ = 4096
NTT = SEQ // 128
NEG = -30000.0
NA = 400
NB_ = 384
NFM = 640
SCALE = 0.125


def build_mix():
    nc = bass.Bass("TRN2", target_bir_lowering=False)
    di = lambda n, s, d=F32: nc.dram_tensor(n, s, d, kind="ExternalInput").ap()
    T = dict(
        xb=di("xb", [SEQ, D]), anw=di("anw", [D]), w_tm=di("w_tm", [D, NA + NB_]), w_fm=di("w_fm", [D, NFM]),
        convw=di("convw", [128, 16]), convb=di("convb", [128, 4]), hp=di("hp", [12]), rope=di("rope", [128, NTT * 16]),
        w1k=di("w1k", [64, 32 * 256]), w1v=di("w1v", [64, 32 * 256]), w2k=di("w2k", [256, 64]), w2v=di("w2v", [256, 64]),
        pek=di("pek", [64, 32]), pev=di("pev", [64, 32]), ident=di("ident", [128, 128]), emat=di("emat", [64, SEQ]),
        tric=di("tric", [128, 128]), tria=di("tria", [128, 128]), cmask=di("cmask", [256, SEQ]), ovl=di("ovl", [256, 64]),
        utri=di("utri", [128, 128]),
    )
    T["mixo"] = nc.dram_tensor("mixo", [SEQ, 512], F32, kind="ExternalOutput").ap()
    T["qr_d"] = nc.dram_tensor("qr_d", [64, 4, SEQ], BF16, kind="Internal").ap()
    T["qu_d"] = nc.dram_tensor("qu_d", [64, 4, SEQ], BF16, kind="Internal").ap()
    T["dummy"] = nc.dram_tensor("dummy_bar", [2, 64], F32, kind="Internal").ap()
    with contextlib.ExitStack() as st:
        ASZ = 207 * 1024
        ar = st.enter_context(nc.sbuf_tensor("arena", [128, ASZ], U8))
        A = Arena(ar, ASZ)
        pbig = st.enter_context(nc.psum_tensor("pbig", [128, 8, 512], F32))
        sems = make_sems(nc, st)
        block = st.enter_context(nc.Block())
        S = Sched(nc)
        mix_body(nc, S, A, pbig, T)
        if os.environ.get('NO_REORDER') is None:
            S.reorder()
        S.emit(sems, block)
    return nc


def pbf(pb, lo, hi):
    return pb[:, lo:hi, :].rearrange("p a b -> p (a b)").bitcast(BF16)


def mix_body(nc, S, A, pbig, T):
    bar = lambda: S.barrier(T["dummy"][1:2, :], T["ident"][0:1, 0:64])
    bk = {"ptr": (0,), "ptr2": (6,), "psA": (1,), "psB": (2,), "psF0": (3,), "psR": (4,), "psS_a": (5,), "psS_c": (5,),
          "psN": (6,), "psY_o": (7,), "psY_d": (7,), "pk": (5,), "pv0": (6,), "pv1": (7,), "pnt": (7,), "posel": (6,), "powin": (7,)}
    for i_ in range(4):
        bk["pbias%d" % i_] = (4,)
        bk["phid%d" % i_] = (i_,)
        bk["psw%d" % i_] = (3 + i_ // 2,)
    for a_ in range(2):
        bk["po%d" % a_] = (4 + a_,)
        for b_ in range(2):
            bk["psc%d_%d" % (a_, b_)] = (a_ * 2 + b_,)
    for i_ in range(3):
        bk["pss%d" % i_] = (i_,)
    S.bank_of = bk
    identb = A.alloc([128], BF16)
    identf = A.alloc([128], F32)
    utri = A.alloc([128], F32)
    tricb = A.alloc([128], BF16)
    triab = A.alloc([128], BF16)
    epsb = A.alloc([1], F32)
    oneb = A.alloc([1], F32)
    EPS_AP[0] = epsb
    KsA = A.alloc([SEQ], BF16)
    KwT = A.alloc([SEQ], BF16)
    kcvcT = A.alloc([SEQ], BF16)
    VsA = A.alloc([NTT, 65], BF16)
    VwA = A.alloc([NTT, 65], BF16)
    gate = A.alloc([NTT, 12], F32)
    hpb = A.alloc([12], F32)
    base_persist = A.off
    S.pool(lambda e: e.memset(epsb, EPS), writes=["eps"])
    S.pool(lambda e: e.memset(oneb, 1.0), writes=["one"])
    S.pool(lambda e: e.memset(VsA, 1.0), writes=["VsA"])
    S.pool(lambda e: e.memset(VwA, 1.0), writes=["VwA"])
    S.dma("pool", identb, T["ident"], writes=["ident"])
    S.dma("sp", identf, T["ident"], writes=["identf"])
    S.dma("sp", utri, T["utri"], writes=["utri"])
    S.dma("pool", tricb, T["tric"], writes=["tric"])
    S.dma("pool", triab, T["tria"], writes=["tria"])
    S.dma("pool", KsA[64:128, :], T["emat"], writes=["KsE"])
    S.dma("sp", hpb, T["hp"].partition_broadcast(128), writes=["hpb"])

    wtm = A.alloc([KC, NA + NB_], BF16)
    wfm = A.alloc([KC, NFM], BF16)
    S.dma("pool", wtm, T["w_tm"].rearrange("(k p) n -> p k n", p=128), writes=["wtm"])
    S.dma("pool", wfm, T["w_fm"].rearrange("(k p) n -> p k n", p=128), writes=["wfm"])
    anwb = A.alloc([D], F32)
    S.dma("sp", anwb, T["anw"].partition_broadcast(128), writes=["anwb"])
    ropet = A.alloc([NTT, 16], F32)
    S.dma("sp", ropet, T["rope"].rearrange("p (t c) -> p t c", c=16), writes=["ropet"])
    convw = A.alloc([16], F32)
    convb = A.alloc([4], F32)
    S.dma("sp", convw, T["convw"], writes=["convw"])
    S.dma("sp", convb, T["convb"], writes=["convb"])
    onesf = A.alloc([128], F32)
    S.pool(lambda e: e.memset(onesf, 1.0), writes=["onesf"])
    two = lambda shape, dt: [A.alloc(shape, dt) for _ in range(2)]
    xt = two([D], F32)
    sq = two([D], BF16)
    ss = A.alloc([8], F32)
    ub = two([D], BF16)
    uT = A.alloc([KC, 512], BF16)
    cbuf = A.alloc([4, 515], F32)
    cacc = two([512], F32)
    xbcT = two([4, 512], BF16)
    zs = two([256], BF16)
    dtt = two([4], F32)
    qk = two([6, 64], F32)
    qkr = two([6, 64], BF16)
    qkb = two([4, 64], BF16)
    rt = two([4, 6, 8], F32)
    qst = two([4, 128], BF16)
    qut = two([4, 128], BF16)
    xtm = two([256], BF16)
    btm = two([128], BF16)
    hst = A.alloc([256], F32)
    hstb = two([256], BF16)
    aneg = A.alloc([4], F32)
    adt = two([4], F32)
    acol = two([4], F32)
    nacol = two([4], F32)
    eac = two([4], F32)
    rhs4 = two([4, 128], F32)
    seg4 = two([4, 128], F32)
    cbm = two([128], F32)
    MT = two([4, 128], BF16)
    alast = two([4], F32)
    dsv = two([4], F32)
    cdv = two([4], F32)
    wsc = two([4], F32)
    xw = two([256], BF16)
    xdt = two([256], BF16)
    ydg = two([256], F32)
    yy = two([256], F32)
    tmpd = two([256], F32)
    yo = two([256], F32)
    S.pool(lambda e: e.memset(cbuf, 0.0), writes=["cbuf", "cbuf0", "cbuf1", "cbuf2", "cbuf3"])
    S.pool(lambda e: e.memset(hst, 0.0), writes=["hst"])
    S.act(lambda e: e.activation(aneg, hpb[:, 4:8], AF.Exp), reads=["hpb"], writes=["aneg"])
    S.dve(lambda e: e.tensor_scalar(aneg, aneg, -1.0, None, ALU.mult), reads=["aneg"], writes=["aneg"])

    ptr = pbf(pbig, 0, 1)
    psA = pbig[:, 1, 0:NA]
    psB = pbig[:, 2, 0:NB_]
    psF = pbig[:, 3, :]
    psR = pbig[:, 4, :]
    psS = pbig[:, 5, :]
    psN = pbig[:, 6, 0:256]
    psY = pbig[:, 7, :]

    def load_x(tt):
        S.dma("sp", xt[tt % 2], T["xb"][tt * 128:(tt + 1) * 128, :], writes=["xt%d" % (tt % 2)], c=5.0)

    def b4(ap, n):
        return ap.unsqueeze(2).to_broadcast([128, 4, n])

    load_x(0)
    for G in range(SEQ // 512):
        gp = G % 2
        for j in range(4):
            tt = G * 4 + j
            b = tt % 2
            if tt + 1 < NTT:
                load_x(tt + 1)
            rms_rstd(S, xt[b], D, ss[:, b:b + 1], sq[b], "n%d" % b, ["xt%d" % b, "eps"], c=1.9)
            S.dve(lambda e, b=b: e.scalar_tensor_tensor(ub[b], xt[b], ss[:, b:b + 1], anwb, ALU.mult, ALU.mult),
                  reads=["xt%d" % b, "n%dss" % b, "anwb"], writes=["ub%d" % b], c=2.2)
            for half in range(2):
                for k8 in range(8):
                    kc = half * 8 + k8
                    S.pe(lambda e, kc=kc, k8=k8, b=b: e.transpose(ptr[:, k8 * 128:(k8 + 1) * 128], ub[b][:, kc * 128:(kc + 1) * 128], identb),
                         reads=["ub%d" % b, "ident"], writes=["ptr"])
                if half == 0:
                    S.act(lambda e, j=j: e.copy(uT[:, 0:8, j * 128:(j + 1) * 128], ptr.rearrange("p (k n) -> p k n", k=8)), reads=["ptr"], writes=["uT%d_0" % j], c=0.9)
                else:
                    S.dve(lambda e, j=j: e.tensor_copy(uT[:, 8:16, j * 128:(j + 1) * 128], ptr.rearrange("p (k n) -> p k n", k=8)), reads=["ptr"], writes=["uT%d_1" % j], c=0.7)
        uTr = ["uT%d_%d" % (j, h) for j in range(4) for h in range(2)]
        for c in range(5):
            for kc in range(KC):
                S.pe(lambda e, c=c, kc=kc: e.matmul(psF, wfm[:, kc, c * 128:(c + 1) * 128], uT[:, kc, :], start=(kc == 0), stop=(kc == KC - 1)),
                     reads=uTr + ["wfm"], writes=["psF0"], c=0.22)
            if c < 4:
                ca = cacc[c % 2]
                car = "cacc%d" % (c % 2)
                S.act(lambda e, c=c: e.copy(cbuf[:, c, 3:515], psF), reads=["psF0"], writes=["cbuf%d" % c], c=0.6)
                S.dve(lambda e, c=c, ca=ca: e.tensor_scalar(ca, cbuf[:, c, 0:512], convw[:, c * 4:c * 4 + 1], None, ALU.mult), reads=["cbuf%d" % c, "convw"], writes=[car], c=0.4)
                for k in range(1, 4):
                    S.dve(lambda e, c=c, k=k, ca=ca: e.scalar_tensor_tensor(ca, cbuf[:, c, k:k + 512], convw[:, c * 4 + k:c * 4 + k + 1], ca, ALU.mult, ALU.add),
                          reads=["cbuf%d" % c, car, "convw"], writes=[car], c=0.65)
                S.act(lambda e, c=c, ca=ca, gp=gp: e.activation(xbcT[gp][:, c, :], ca, AF.Silu, bias=convb[:, c:c + 1]), reads=[car, "convb"], writes=["xbcT%d_%d" % (gp, c)], c=0.6)
                S.pool(lambda e, c=c: e.tensor_copy(cbuf[:, c, 0:3], cbuf[:, c, 512:515]), reads=["cbuf%d" % c], writes=["cbuf%d" % c])
            else:
                S.act(lambda e, G=G: e.copy(kcvcT[:, G * 512:(G + 1) * 512], psF), reads=["psF0"], writes=["kcvcT"], c=0.6)
        xr = lambda c: "xbcT%d_%d" % (gp, c)
        for j in range(4):
            tt = G * 4 + j
            p = tt % 2
            P = str(p)
            tok = slice(tt * 128, (tt + 1) * 128)
            js = slice(j * 128, (j + 1) * 128)
            for kc in range(KC):
                S.pe(lambda e, js=js, kc=kc: e.matmul(psA, uT[:, kc, js], wtm[:, kc, 0:NA], start=(kc == 0), stop=(kc == KC - 1)),
                     reads=uTr + ["wtm"], writes=["psA"], c=0.19)
            for kc in range(KC):
                S.pe(lambda e, js=js, kc=kc: e.matmul(psB, uT[:, kc, js], wtm[:, kc, NA:NA + NB_], start=(kc == 0), stop=(kc == KC - 1)),
                     reads=uTr + ["wtm"], writes=["psB"], c=0.18)
            S.act(lambda e, p=p: e.activation(zs[p], psA[:, 0:256], AF.Silu), reads=["psA"], writes=["zs" + P])
            S.dve(lambda e, p=p: e.tensor_tensor(dtt[p], psA[:, 256:260], hpb[:, 0:4], ALU.add), reads=["psA", "hpb"], writes=["dtt" + P])
            S.act(lambda e, p=p: e.activation(dtt[p], dtt[p], AF.Exp), reads=["dtt" + P], writes=["dtt" + P])
            S.act(lambda e, p=p: e.activation(dtt[p], dtt[p], AF.Ln, bias=oneb), reads=["dtt" + P, "one"], writes=["dtt" + P])
            S.act(lambda e, tt=tt: e.activation(gate[:, tt, :], psA[:, 260:272], AF.Sigmoid), reads=["psA"], writes=["gate%d" % tt])
            S.dve(lambda e, tt=tt: e.tensor_copy(VsA[:, tt, 0:64], psA[:, 272:336]), reads=["psA", "VsA"], writes=["VsA%d" % tt])
            S.dve(lambda e, tt=tt: e.tensor_copy(VwA[:, tt, 0:64], psA[:, 336:400]), reads=["psA", "VwA"], writes=["VwA%d" % tt])
            S.act(lambda e, p=p: e.copy(qk[p], psB.rearrange("p (a b) -> p a b", a=6)), reads=["psB"], writes=["qk" + P], c=0.5)
            S.pool(lambda e, p=p: e.tensor_copy(qkb[p], qk[p][:, 0:4, :]), reads=["qk" + P], writes=["qkb" + P])
            S.pool(lambda e, p=p: e.tensor_copy(qkr[p], qk[p]), reads=["qk" + P], writes=["qkr" + P])
            cosb = ropet[:, tt, 0:8].unsqueeze(1).to_broadcast([128, 6, 8])
            sinb = ropet[:, tt, 8:16].unsqueeze(1).to_broadcast([128, 6, 8])
            S.dve(lambda e, cosb=cosb, p=p: e.tensor_tensor(rt[p][:, 0], qk[p][:, :, 0:8], cosb, ALU.mult), reads=["qk" + P, "ropet"], writes=["rt0" + P])
            S.dve(lambda e, sinb=sinb, p=p: e.tensor_tensor(rt[p][:, 1], qk[p][:, :, 8:16], sinb, ALU.mult), reads=["qk" + P, "ropet"], writes=["rt1" + P])
            S.dve(lambda e, cosb=cosb, p=p: e.tensor_tensor(rt[p][:, 2], qk[p][:, :, 8:16], cosb, ALU.mult), reads=["qk" + P, "ropet"], writes=["rt2" + P])
            S.dve(lambda e, sinb=sinb, p=p: e.tensor_tensor(rt[p][:, 3], qk[p][:, :, 0:8], sinb, ALU.mult), reads=["qk" + P, "ropet"], writes=["rt3" + P])
            S.dve(lambda e, p=p: e.tensor_tensor(qkr[p][:, :, 0:8], rt[p][:, 0], rt[p][:, 1], ALU.subtract), reads=["rt0" + P, "rt1" + P, "qkr" + P], writes=["qkr" + P])
            S.dve(lambda e, p=p: e.tensor_tensor(qkr[p][:, :, 8:16], rt[p][:, 2], rt[p][:, 3], ALU.add), reads=["rt2" + P, "rt3" + P, "qkr" + P], writes=["qkr" + P])
            ptq = ptr[0:64, 0:768].rearrange("p (a b) -> p a b", a=6)
            ptu = pbf(pbig, 6, 7)[0:64, 512:1024].rearrange("p (a b) -> p a b", a=4)
            for a in range(6):
                S.pe(lambda e, a=a, p=p: e.transpose(ptq[:, a, :], qkr[p][:, a, :], identb), reads=["qkr" + P, "ident"], writes=["ptr"])
            for a in range(4):
                S.pe(lambda e, a=a, p=p: e.transpose(ptu[:, a, :], qkb[p][:, a, :], identb), reads=["qkb" + P, "ident"], writes=["ptr2"])
            S.act(lambda e, p=p: e.copy(qst[p][0:64], ptq[:, 0:4, :]), reads=["ptr"], writes=["qst" + P])
            S.dve(lambda e, p=p: e.tensor_copy(qut[p][0:64], ptu), reads=["ptr2"], writes=["qut" + P])
            S.act(lambda e, tok=tok: e.copy(KsA[0:64, tok], ptq[:, 4, :]), reads=["ptr"], writes=["KsA%d" % tt])
            S.dve(lambda e, tok=tok: e.tensor_copy(KwT[0:64, tok], ptq[:, 5, :]), reads=["ptr"], writes=["KwT%d" % tt])
            S.dma("sp", T["qr_d"][:, :, tok], qst[p][0:64], reads=["qst" + P], writes=["qr_d%d" % tt])
            S.dma("sp", T["qu_d"][:, :, tok], qut[p][0:64], reads=["qut" + P], writes=["qu_d%d" % tt])
            ptx = ptr[:, 0:384]
            for c in range(3):
                S.pe(lambda e, c=c, js=js, gp=gp: e.transpose(ptx[:, c * 128:(c + 1) * 128], xbcT[gp][:, c, js], identb),
                     reads=[xr(c), "ident"], writes=["ptr"])
            S.act(lambda e, p=p: e.copy(xtm[p], ptx[:, 0:256]), reads=["ptr"], writes=["xtm" + P])
            S.act(lambda e, p=p: e.copy(btm[p], ptx[:, 256:384]), reads=["ptr"], writes=["btm" + P])
            S.dve(lambda e, p=p: e.tensor_tensor(adt[p], dtt[p], aneg, ALU.mult), reads=["dtt" + P, "aneg"], writes=["adt" + P])
            S.pe(lambda e, p=p: e.matmul(psS[:, 0:4], utri, adt[p], start=True, stop=True), reads=["utri", "adt" + P], writes=["psS_a"])
            S.act(lambda e, p=p: e.copy(acol[p], psS[:, 0:4]), reads=["psS_a"], writes=["acol" + P])
            S.dve(lambda e, p=p: e.tensor_scalar(nacol[p], psS[:, 0:4], -1.0, None, ALU.mult), reads=["psS_a"], writes=["nacol" + P])
            S.act(lambda e, p=p: e.activation(eac[p], acol[p], AF.Exp), reads=["acol" + P], writes=["eac" + P])
            S.pe(lambda e, js=js, gp=gp: e.matmul(psS[:, 256:384], xbcT[gp][:, 2, js], xbcT[gp][:, 3, js], start=True, stop=True),
                 reads=[xr(2), xr(3)], writes=["psS_c"])
            S.dve(lambda e, p=p: e.tensor_tensor(cbm[p], psS[:, 256:384], utri, ALU.mult), reads=["psS_c", "utri"], writes=["cbm" + P])
            S.dve(lambda e, p=p: e.tensor_tensor(rhs4[p], utri.unsqueeze(1).to_broadcast([128, 4, 128]), b4(adt[p], 128), ALU.mult),
                  reads=["utri", "adt" + P], writes=["rhs4" + P], c=0.6)
            S.pe(lambda e, p=p: e.matmul(psR, onesf, rhs4[p].rearrange("p a b -> p (a b)"), start=True, stop=True), reads=["onesf", "rhs4" + P], writes=["psR"], c=0.9)
            psR4 = psR.rearrange("p (a b) -> p a b", a=4)
            S.dve(lambda e, p=p: e.tensor_tensor(seg4[p], psR4, b4(acol[p], 128), ALU.subtract), reads=["psR", "acol" + P], writes=["seg4" + P], c=0.7)
            S.dve(lambda e, p=p: e.tensor_scalar(seg4[p], seg4[p], 0.0, None, ALU.min), reads=["seg4" + P], writes=["seg4" + P], c=0.35)
            S.act(lambda e, p=p: e.activation(seg4[p], seg4[p], AF.Exp), reads=["seg4" + P], writes=["seg4" + P], c=0.6)
            S.dve(lambda e, p=p: e.tensor_tensor(MT[p], seg4[p], cbm[p].unsqueeze(1).to_broadcast([128, 4, 128]), ALU.mult),
                  reads=["seg4" + P, "cbm" + P], writes=["MT" + P], c=0.6)
            S.dve(lambda e, p=p: e.tensor_copy(alast[p], psR4[:, :, 127]), reads=["psR"], writes=["alast" + P])
            S.dve(lambda e, p=p: e.tensor_tensor(dsv[p], nacol[p], alast[p], ALU.add), reads=["nacol" + P, "alast" + P], writes=["dsv" + P])
            S.act(lambda e, p=p: e.activation(dsv[p], dsv[p], AF.Exp), reads=["dsv" + P], writes=["dsv" + P])
            S.act(lambda e, p=p: e.activation(cdv[p], alast[p], AF.Exp), reads=["alast" + P], writes=["cdv" + P])
            S.dve(lambda e, p=p: e.tensor_tensor(wsc[p], dtt[p], dsv[p], ALU.mult), reads=["dtt" + P, "dsv" + P], writes=["wsc" + P])
            v4 = lambda ap: ap.rearrange("p (h d) -> p h d", h=4)
            S.dve(lambda e, p=p: e.tensor_tensor(v4(xw[p]), v4(xtm[p]), b4(wsc[p], 64), ALU.mult), reads=["xtm" + P, "wsc" + P], writes=["xw" + P])
            S.dve(lambda e, p=p: e.tensor_tensor(v4(xdt[p]), v4(xtm[p]), b4(dtt[p], 64), ALU.mult), reads=["xtm" + P, "dtt" + P], writes=["xdt" + P])
            S.pool(lambda e, p=p: e.tensor_copy(hstb[p], hst), reads=["hst"], writes=["hstb" + P])
            S.pe(lambda e, js=js, gp=gp, p=p: e.matmul(psY[:, 0:256], xbcT[gp][:, 3, js], hstb[p], start=True, stop=True), reads=[xr(3), "hstb" + P], writes=["psY_o"])
            for h in range(4):
                S.pe(lambda e, h=h, p=p: e.matmul(psY[:, 256 + h * 64:256 + (h + 1) * 64], MT[p][:, h, :], xdt[p][:, h * 64:(h + 1) * 64], start=True, stop=True),
                     reads=["MT" + P, "xdt" + P], writes=["psY_d"])
            S.pe(lambda e, p=p: e.matmul(psN, btm[p], xw[p], start=True, stop=True), reads=["btm" + P, "xw" + P], writes=["psN"])
            S.dve(lambda e, p=p: e.tensor_tensor(v4(hst), v4(hst), b4(cdv[p], 64), ALU.mult), reads=["hst", "cdv" + P], writes=["hst"])
            S.dve(lambda e: e.tensor_tensor(hst, hst, psN, ALU.add), reads=["hst", "psN"], writes=["hst"])
            S.act(lambda e, p=p: e.copy(ydg[p], psY[:, 256:512]), reads=["psY_d"], writes=["ydg" + P])
            S.dve(lambda e, p=p: e.tensor_tensor(v4(yy[p]), v4(psY[:, 0:256]), b4(eac[p], 64), ALU.mult), reads=["psY_o", "eac" + P], writes=["yy" + P])
            S.pool(lambda e, p=p: e.tensor_tensor(v4(tmpd[p]), v4(xtm[p]), b4(hpb[:, 8:12], 64), ALU.mult), reads=["xtm" + P, "hpb"], writes=["tmpd" + P])
            S.dve(lambda e, p=p: e.tensor_tensor(yy[p], yy[p], ydg[p], ALU.add), reads=["yy" + P, "ydg" + P], writes=["yy" + P])
            S.dve(lambda e, p=p: e.tensor_tensor(yy[p], yy[p], tmpd[p], ALU.add), reads=["yy" + P, "tmpd" + P], writes=["yy" + P])
            S.dve(lambda e, p=p: e.tensor_tensor(yo[p], yy[p], zs[p], ALU.mult), reads=["yy" + P, "zs" + P], writes=["yo" + P])
            S.dma("sp", T["mixo"][tok, 0:256], yo[p], reads=["yo" + P], writes=["mixo_s%d" % tt])

    if int(os.environ.get('MIX_STOP', '9')) <= 1:
        return
    bar()
    A.reset(base_persist)
    attO = A.alloc([NTT, 256], F32)
    nmT = A.alloc([SEQ], BF16)
    w1 = A.alloc([32, 256], BF16)
    S.dma("pool", w1[0:64], T["w1k"].rearrange("d (l h) -> d l h", l=32), writes=["w1k"])
    S.dma("pool", w1[64:128], T["w1v"].rearrange("d (l h) -> d l h", l=32), writes=["w1v"])
    pe_ = A.alloc([32], BF16)
    S.dma("pool", pe_[0:64], T["pek"], writes=["pek"])
    S.dma("pool", pe_[64:128], T["pev"], writes=["pev"])
    w2 = A.alloc([2, 2, 64], BF16)
    S.dma("pool", w2[:, 0], T["w2k"].rearrange("(c p) d -> p c d", p=128), writes=["w2k"])
    S.dma("pool", w2[:, 1], T["w2v"].rearrange("(c p) d -> p c d", p=128), writes=["w2v"])
    cbias = A.alloc([4], F32)
    hsb = A.alloc([4, 256], BF16)
    KcT = A.alloc([256], BF16)
    VcA = A.alloc([2, 129], BF16)
    S.pool(lambda e: e.memset(hsb, 0.0), writes=["hsb"])
    S.pool(lambda e: e.memset(VcA, 0.0), writes=["VcA"])
    S.pool(lambda e: e.memset(KcT, 0.0), writes=["KcT"])
    S.pool(lambda e: e.memset(VcA[:, :, 64:65], 1.0), reads=["VcA"], writes=["VcA"])
    S.dma("pool", VcA[:, :, 65:129], T["ovl"].rearrange("(c p) j -> p c j", p=128), reads=["VcA"], writes=["VcA"])
    for kv in range(2):
        rows = slice(kv * 64, (kv + 1) * 64)
        for hc in range(2):
            idx = kv * 2 + hc
            pb_ = pbig[:, idx, 0:255]
            pbias = pbig[:, 4, idx:idx + 1]
            for l in range(32):
                S.pe(lambda e, rows=rows, hc=hc, l=l, pbias=pbias: e.matmul(pbias, w1[rows, l, hc * 128:(hc + 1) * 128], pe_[rows, l:l + 1], start=(l == 0), stop=(l == 31)),
                     reads=["w1k", "w1v", "pek", "pev"], writes=["pbias%d" % idx])
            S.act(lambda e, idx=idx, pbias=pbias: e.copy(cbias[:, idx:idx + 1], pbias), reads=["pbias%d" % idx], writes=["cbias%d" % idx])
            for l in range(32):
                S.pe(lambda e, rows=rows, hc=hc, l=l, pb_=pb_: e.matmul(pb_, w1[rows, l, hc * 128:(hc + 1) * 128], kcvcT[rows, l:l + 16 * 254 + 1:16], start=(l == 0), stop=(l == 31)),
                     reads=["w1k", "w1v", "kcvcT"], writes=["phid%d" % idx])
            S.act(lambda e, idx=idx, pb_=pb_: e.activation(hsb[:, idx, 0:255], pb_, AF.Silu, bias=cbias[:, idx:idx + 1]), reads=["phid%d" % idx, "cbias%d" % idx, "hsb"], writes=["hsb%d" % idx])
    pk = pbig[0:64, 5, 0:255]
    for hc in range(2):
        S.pe(lambda e, hc=hc: e.matmul(pk, w2[:, 0, hc, :], hsb[:, hc, 0:255], start=(hc == 0), stop=(hc == 1)), reads=["w2k", "hsb0", "hsb1"], writes=["pk"])
    S.act(lambda e: e.copy(KcT[0:64, 0:255], pk), reads=["pk", "KcT"], writes=["KcT"])
    for it in range(2):
        m = 128 if it == 0 else 127
        pv = pbig[0:m, 6 + it, 0:64]
        for hc in range(2):
            S.pe(lambda e, it=it, hc=hc, m=m, pv=pv: e.matmul(pv, hsb[:, 2 + hc, it * 128:it * 128 + m], w2[:, 1, hc, :], start=(hc == 0), stop=(hc == 1)),
                 reads=["w2v", "hsb2", "hsb3"], writes=["pv%d" % it])
        S.act(lambda e, it=it, m=m, pv=pv: e.copy(VcA[0:m, it, 0:64], pv), reads=["pv%d" % it, "VcA"], writes=["VcA"])

    if int(os.environ.get('MIX_STOP', '9')) <= 2:
        return
    bar()
    base3 = A.off
    qu = [A.alloc([4, 512], BF16) for _ in range(2)]
    cmk = [A.alloc([2, 512], BF16) for _ in range(2)]
    PcT = [A.alloc([2, 512], BF16) for _ in range(2)]
    imp = A.alloc([4, 64], F32)
    imp2 = A.alloc([64], F32)
    m8 = A.alloc([16], F32)
    thr = A.alloc([1], F32)
    rr = A.alloc([2], F32)
    nmb = A.alloc([128], BF16)
    S.pool(lambda e: e.memset(nmb, 0.0), writes=["nmb"])
    pnt = pbf(pbig, 7, 8)[:, 0:128]
    for Q in range(8):
        qb = Q % 2
        qs = slice(Q * 512, (Q + 1) * 512)
        S.dma("sp", qu[qb][0:64], T["qu_d"][:, :, qs], writes=["qu%d" % qb])
        S.dma("pool", cmk[qb], T["cmask"][:, qs].rearrange("(c p) t -> p c t", p=128), writes=["cmk%d" % qb])
        for h in range(4):
            pb2 = h % 2
            for it in range(2):
                ps_ = pbig[:, pb2 * 2 + it, :]
                S.pe(lambda e, it=it, h=h, qb=qb, ps_=ps_: e.matmul(ps_, KcT[0:64, it * 128:(it + 1) * 128], qu[qb][0:64, h, :], start=True, stop=False),
                     reads=["KcT", "qu%d" % qb], writes=["psc%d_%d" % (pb2, it)])
                S.pe(lambda e, it=it, qb=qb, ps_=ps_: e.matmul(ps_, identb, cmk[qb][:, it, :], start=False, stop=True),
                     reads=["ident", "cmk%d" % qb], writes=["psc%d_%d" % (pb2, it)])
                S.act(lambda e, it=it, pb2=pb2, ps_=ps_: e.activation(PcT[pb2][:, it, :], ps_, AF.Exp, scale=SCALE), reads=["psc%d_%d" % (pb2, it)], writes=["PcT%d_%d" % (pb2, it)])
            for sub in range(4):
                tt = Q * 4 + sub
                po = pbig[:, 4 + (sub % 2), 0:129]
                for it in range(2):
                    S.pe(lambda e, it=it, pb2=pb2, sub=sub, po=po: e.matmul(po, PcT[pb2][:, it, sub * 128:(sub + 1) * 128], VcA[:, it, :], start=(it == 0), stop=(it == 1)),
                         reads=["PcT%d_0" % pb2, "PcT%d_1" % pb2, "VcA"], writes=["po%d" % (sub % 2)])
                pr = ["po%d" % (sub % 2)]
                S.dve(lambda e, po=po: e.tensor_scalar(rr[:, 0:1], po[:, 64:65], 1e-30, None, ALU.add), reads=pr, writes=["rr0"])
                S.dve(lambda e: e.reciprocal(rr[:, 0:1], rr[:, 0:1]), reads=["rr0"], writes=["rr0"])
                if h == 0:
                    S.dve(lambda e, po=po, sub=sub: e.tensor_scalar(imp[:, sub, :], po[:, 65:129], rr[:, 0:1], None, ALU.mult), reads=pr + ["rr0"], writes=["imp%d" % sub])
                else:
                    S.dve(lambda e, po=po, sub=sub: e.scalar_tensor_tensor(imp[:, sub, :], po[:, 65:129], rr[:, 0:1], imp[:, sub, :], ALU.mult, ALU.add),
                          reads=pr + ["rr0", "imp%d" % sub], writes=["imp%d" % sub])
                S.dve(lambda e, tt=tt, h=h: e.tensor_tensor(rr[:, 1:2], rr[:, 0:1], gate[:, tt, h * 3:h * 3 + 1], ALU.mult), reads=["rr0", "gate"], writes=["rr1"])
                S.dve(lambda e, po=po, tt=tt, h=h: e.tensor_scalar(attO[:, tt, h * 64:(h + 1) * 64], po[:, 0:64], rr[:, 1:2], None, ALU.mult), reads=pr + ["rr1"], writes=["attO%d" % tt])
        for sub in range(4):
            tt = Q * 4 + sub
            ir = ["imp%d" % sub]
            im = imp[:, sub, :]
            S.pool(lambda e, im=im: e.memset(im[:, 0:1], 1e4), reads=ir, writes=ir)
            lo = max(2 * tt - 1, 0)
            S.pool(lambda e, im=im, lo=lo, tt=tt: e.memset(im[0:64, lo:2 * tt + 1], 1e4), reads=ir, writes=ir)
            S.pool(lambda e, im=im, tt=tt: e.memset(im[64:128, 2 * tt:2 * tt + 2], 1e4), reads=ir, writes=ir)
            if 2 * tt + 1 < 64:
                S.pool(lambda e, im=im, tt=tt: e.memset(im[0:64, 2 * tt + 1:64], -1.0), reads=ir, writes=ir)
            if 2 * tt + 2 < 64:
                S.pool(lambda e, im=im, tt=tt: e.memset(im[64:128, 2 * tt + 2:64], -1.0), reads=ir, writes=ir)
            S.dve(lambda e, im=im: e.max(m8[:, 0:8], im), reads=ir, writes=["m8a"])
            S.dve(lambda e, im=im: e.match_replace(imp2, m8[:, 0:8], im, -1e30), reads=ir + ["m8a"], writes=["imp2"])
            S.dve(lambda e: e.max(m8[:, 8:16], imp2), reads=["imp2"], writes=["m8b"])
            S.dve(lambda e: e.tensor_scalar(thr, m8[:, 15:16], 0.0, None, ALU.max), reads=["m8b"], writes=["thr"])
            S.dve(lambda e, im=im: e.tensor_scalar(nmb[:, 64:128], im, thr, NEG, ALU.is_lt, ALU.mult), reads=ir + ["thr", "nmb"], writes=["nmb"])
            S.pe(lambda e: e.transpose(pnt, nmb, identb), reads=["nmb", "ident"], writes=["pnt"])
            S.act(lambda e, tt=tt: e.copy(nmT[64:128, tt * 128:(tt + 1) * 128], pnt[64:128, :]), reads=["pnt"], writes=["nmT"])

    if int(os.environ.get('MIX_STOP', '9')) <= 3:
        return
    bar()
    A.reset(base3)
    Qa = [A.alloc([4, 512], BF16) for _ in range(2)]
    PT = [A.alloc([512], BF16) for _ in range(3)]
    PW = [A.alloc([128], BF16) for _ in range(3)]
    r4 = A.alloc([2], F32)
    pti = 0
    pwi = 0
    for Q in range(8):
        qb = Q % 2
        qs = slice(Q * 512, (Q + 1) * 512)
        S.dma("sp", Qa[qb][0:64], T["qr_d"][:, :, qs], writes=["Qa%d" % qb])
        for h in range(4):
            S.pool(lambda e, qb=qb, h=h, qs=qs: e.tensor_copy(Qa[qb][64:128, h, :], nmT[64:128, qs]), reads=["nmT"], writes=["Qm%d_%d" % (qb, h)])
        for h in range(4):
            qr_ = ["Qa%d" % qb, "Qm%d_%d" % (qb, h)]
            posel = pbig[:, 6, 0:260].rearrange("p (s c) -> p s c", s=4)
            powin = pbig[:, 7, 0:260].rearrange("p (s c) -> p s c", s=4)
            S.dve(lambda e: e.memset(pbig[:, 6, 0:260], 0.0), writes=["posel"])
            for kt in range(4 * Q + 4):
                ks_ = slice(kt * 128, (kt + 1) * 128)
                sb_ = kt % 3
                ps_ = pbig[:, sb_, :]
                pres = "pss%d" % sb_
                o = kt - 4 * Q
                if o < 0:
                    S.pe(lambda e, ks_=ks_, qb=qb, h=h, ps_=ps_: e.matmul(ps_, KsA[:, ks_], Qa[qb][:, h, :], start=True, stop=True),
                         reads=["KsA", "KsE"] + qr_, writes=[pres])
                    lo = 0
                else:
                    lo = o * 128
                    S.pe(lambda e, ks_=ks_, qb=qb, h=h, ps_=ps_, lo=lo: e.matmul(ps_[:, lo:lo + 128], KsA[:, ks_], Qa[qb][:, h, lo:lo + 128], start=True, stop=False),
                         reads=["KsA", "KsE"] + qr_, writes=[pres])
                    S.pe(lambda e, ps_=ps_, lo=lo: e.matmul(ps_[:, lo:lo + 128], identb, tricb, start=False, stop=True), reads=["ident", "tric"], writes=[pres])
                    if o < 3:
                        S.pe(lambda e, ks_=ks_, qb=qb, h=h, ps_=ps_, lo=lo: e.matmul(ps_[:, lo + 128:512], KsA[:, ks_], Qa[qb][:, h, lo + 128:512], start=True, stop=True),
                             reads=["KsA", "KsE"] + qr_, writes=[pres])
                pt_ = PT[pti % 3]
                ptres = "PT%d" % (pti % 3)
                pti += 1
                S.act(lambda e, ps_=ps_, pt_=pt_, lo=lo: e.activation(pt_[:, lo:512], ps_[:, lo:512], AF.Exp, scale=SCALE), reads=[pres], writes=[ptres])
                for sub in range(max(o, 0), 4):
                    S.pe(lambda e, pt_=pt_, sub=sub, kt=kt, Q=Q, posel=posel: e.matmul(posel[:, sub, :], pt_[:, sub * 128:(sub + 1) * 128], VsA[:, kt, :],
                                                                                 start=False, stop=False, skip_group_check=True),
                         reads=[ptres, "VsA"], writes=["posel"])
            for sub in range(4):
                tt = 4 * Q + sub
                kts = [k for k in range(tt - 4, tt + 1) if k >= 0]
                for kt in kts:
                    ks_ = slice(kt * 128, (kt + 1) * 128)
                    wsl = pwi % 4
                    psw = pbig[:, 3 + wsl // 2, (wsl % 2) * 128:(wsl % 2) * 128 + 128]
                    pwres = "psw%d" % wsl
                    msk = triab if kt == tt - 4 else (tricb if kt == tt else None)
                    S.pe(lambda e, ks_=ks_, qb=qb, h=h, sub=sub, psw=psw, msk=msk: e.matmul(psw, KwT[0:64, ks_], Qa[qb][0:64, h, sub * 128:(sub + 1) * 128], start=True, stop=(msk is None)),
                         reads=["KwT", "Qa%d" % qb], writes=[pwres])
                    if msk is not None:
                        S.pe(lambda e, psw=psw, msk=msk: e.matmul(psw, identb, msk, start=False, stop=True), reads=["ident", "tric", "tria"], writes=[pwres])
                    pw_ = PW[pwi % 3]
                    pwr = "PW%d" % (pwi % 3)
                    pwi += 1
                    S.act(lambda e, psw=psw, pw_=pw_: e.activation(pw_, psw, AF.Exp, scale=SCALE), reads=[pwres], writes=[pwr])
                    S.pe(lambda e, pw_=pw_, sub=sub, kt=kt, kts=kts, powin=powin: e.matmul(powin[:, sub, :], pw_, VwA[:, kt, :], start=(kt == kts[0]), stop=(kt == kts[-1])),
                         reads=[pwr, "VwA"], writes=["powin"])
            for sub in range(4):
                tt = 4 * Q + sub
                for br, (po_, pres) in enumerate(((posel, "posel"), (powin, "powin"))):
                    S.dve(lambda e, po_=po_, sub=sub, br=br: e.reciprocal(r4[:, br:br + 1], po_[:, sub, 64:65]), reads=[pres], writes=["r4_%d" % br])
                    S.dve(lambda e, tt=tt, h=h, br=br: e.tensor_tensor(r4[:, br:br + 1], r4[:, br:br + 1], gate[:, tt, h * 3 + 1 + br:h * 3 + 2 + br], ALU.mult),
                          reads=["r4_%d" % br, "gate"], writes=["r4_%d" % br])
                    S.dve(lambda e, po_=po_, sub=sub, tt=tt, h=h, br=br: e.scalar_tensor_tensor(attO[:, tt, h * 64:(h + 1) * 64], po_[:, sub, 0:64], r4[:, br:br + 1],
                                                                                                  attO[:, tt, h * 64:(h + 1) * 64], ALU.mult, ALU.add),
                          reads=[pres, "r4_%d" % br, "attO%d" % tt], writes=["attO%d" % tt])
        for sub in range(4):
            tt = 4 * Q + sub
            S.dma("sp", T["mixo"][tt * 128:(tt + 1) * 128, 256:512], attO[:, tt, :], reads=["attO%d" % tt])


def _perm_cols():
    return None


def run_mix(inputs):
    if "mix" not in _CACHE:
        _CACHE["mix"] = build_mix()
    nc = _CACHE["mix"]
    x = inputs["x"]
    w_in = inputs["w_in"][0]
    offs = np.cumsum([0, 1024, 1536, 16, 1024, 256, 256, 256, 256, 256, 256, 48])
    oz, oxbc, odt, oq, okc, ovc, oks, ovs, okw, ovw, ogate = offs[:11]
    conv_w = inputs["conv_w"][0]
    conv_b = inputs["conv_b"][0]
    t = np.arange(SEQ, dtype=np.float32)
    inv = (1.0 / (500000.0 ** (np.arange(0, 16, 2, dtype=np.float32) / np.float32(16)))).astype(np.float32)
    ang = (t[:, None] * inv[None, :]).astype(np.float32)
    rope = np.concatenate([np.cos(ang), np.sin(ang)], 1).astype(np.float32)
    rope = np.ascontiguousarray(rope.reshape(NTT, 128, 16).transpose(1, 0, 2).reshape(128, NTT * 16))
    ident = np.eye(128, dtype=np.float32)
    kk = np.arange(128)[:, None]
    qq = np.arange(128)[None, :]
    tric = np.where(kk <= qq, 0.0, NEG).astype(np.float32)
    tria = np.where(kk > qq, 0.0, NEG).astype(np.float32)
    utri = (kk <= qq).astype(np.float32)
    emat = (np.arange(SEQ)[None, :] // 64 == np.arange(64)[:, None]).astype(np.float32)
    ii = np.arange(256)[:, None]
    cmask = np.where((16 * ii + 31 <= np.arange(SEQ)[None, :]) & (ii < 255), 0.0, NEG).astype(np.float32)
    cs = np.arange(255)[:, None] * 16
    ss_ = np.arange(64)[None, :] * 64
    ov = np.clip(np.minimum(cs + 32, ss_ + 64) - np.maximum(cs, ss_), 0, None) / 32.0
    ovl = np.zeros((256, 64), np.float32)
    ovl[:255] = ov
    in_maps = []
    for c in range(8):
        b, g = c // 4, c % 4
        grp = g // 2
        ar = np.arange
        tm_cols = np.concatenate([oz + 256 * g + ar(256), odt + 4 * g + ar(4), ogate + 12 * g + ar(12), ovs + 64 * g + ar(64), ovw + 64 * g + ar(64),
                                  oq + 256 * g + ar(256), oks + 64 * g + ar(64), okw + 64 * g + ar(64)])
        xcols = np.concatenate([256 * g + ar(256), 1024 + 128 * grp + ar(128), 1280 + 128 * grp + ar(128)])
        fm_cols = np.concatenate([oxbc + xcols, okc + 64 * g + ar(64), ovc + 64 * g + ar(64)])
        convw = np.ascontiguousarray(conv_w[:, xcols].T.reshape(4, 128, 4).transpose(1, 0, 2).reshape(128, 16))
        convb = np.ascontiguousarray(conv_b[xcols].reshape(4, 128).T)
        hp = np.concatenate([inputs["dt_bias"][0][4 * g:4 * g + 4], inputs["a_log"][0][4 * g:4 * g + 4], inputs["d_skip"][0][4 * g:4 * g + 4]]).astype(np.float32)
        in_maps.append(dict(
            xb=np.ascontiguousarray(x[b]), anw=inputs["attn_norm_w"][0], w_tm=np.ascontiguousarray(w_in[:, tm_cols]), w_fm=np.ascontiguousarray(w_in[:, fm_cols]),
            convw=convw, convb=convb, hp=hp, rope=rope,
            w1k=np.ascontiguousarray(inputs["cmp_w1_k"][0].reshape(32, 64, 256).transpose(1, 0, 2).reshape(64, 32 * 256)),
            w1v=np.ascontiguousarray(inputs["cmp_w1_v"][0].reshape(32, 64, 256).transpose(1, 0, 2).reshape(64, 32 * 256)),
            w2k=inputs["cmp_w2_k"][0], w2v=inputs["cmp_w2_v"][0],
            pek=np.ascontiguousarray(inputs["cmp_pe_k"][0].T), pev=np.ascontiguousarray(inputs["cmp_pe_v"][0].T),
            ident=ident, emat=emat, tric=tric, tria=tria, cmask=cmask, ovl=ovl, utri=utri,
        ))
    res = run_bass_kernel_spmd(nc, in_maps, core_ids=list(range(8)))
    mixed = np.empty((2, SEQ, D), np.float32)
    for c in range(8):
        b, g = c // 4, c % 4
        m = res.results[c]["mixo"]
        mixed[b, :, 256 * g:256 * (g + 1)] = m[:, 0:256]
        mixed[b, :, 1024 + 256 * g:1024 + 256 * (g + 1)] = m[:, 256:512]
    return mixed


def kernel(**inputs):
    inputs = {k: np.asarray(v) for k, v in inputs.items()}
    mixed = run_mix(inputs)
    return run_tail(inputs["x"], mixed, inputs["ssd_norm_w"][0], inputs["w_out"][0], inputs["ffn_norm_w"][0],
                    inputs["w_gate"][0], inputs["w_up"][0], inputs["w_down"][0], inputs["final_norm_w"])
```

```python
import contextlib
import os
import numpy as np
import ml_dtypes
import concourse.bass as bass
import concourse.mybir as mybir
from concourse.bass_utils import run_bass_kernel_spmd

F32 = mybir.dt.float32
BF16 = mybir.dt.bfloat16
U8 = mybir.dt.uint8
ALU = mybir.AluOpType
AF = mybir.ActivationFunctionType
AX = mybir.AxisListType

ENGS = ("pe", "act", "dve", "pool", "sp")
EPS = 1e-6


class _Op:
    __slots__ = ("eng", "fn", "dma", "deps", "idx", "signal", "val", "sem", "cc", "cost")


class Sched:
    def __init__(self, nc, n_dma_sems=8):
        self.nc = nc
        self.ops = []
        self.last_w = {}
        self.readers = {}
        self.n_dma_sems = n_dma_sems
        self.bar = None
        self.bank_of = {}

    def op(self, eng, fn, reads=(), writes=(), dma=False, c=None):
        o = _Op()
        o.cost = c
        o.eng, o.fn, o.dma = eng, fn, dma
        o.cc = False
        o.idx = len(self.ops)
        o.signal = False
        deps = {}
        if self.bar is not None:
            deps[self.bar] = "raw"
        for r in reads:
            w = self.last_w.get(r)
            if w is not None:
                deps[w] = "raw"
        for w_ in writes:
            w = self.last_w.get(w_)
            if w is not None:
                deps[w] = "raw"
            for r in self.readers.get(w_, ()):
                if r not in deps:
                    deps[r] = "war"
        for r in reads:
            self.readers.setdefault(r, []).append(o.idx)
        for w_ in writes:
            self.last_w[w_] = o.idx
            self.readers[w_] = []
        banks = set()
        for r in tuple(reads) + tuple(writes):
            banks.update(self.bank_of.get(r, ()))
        for b in banks:
            key = ("bank", b)
            w = self.last_w.get(key)
            if w is not None and w not in deps:
                deps[w] = "bank"
            self.last_w[key] = o.idx
        deps.pop(o.idx, None)
        o.deps = deps
        self.ops.append(o)
        return o

    def pe(self, fn, reads=(), writes=(), c=None):
        return self.op("pe", fn, reads, writes, c=c)

    def act(self, fn, reads=(), writes=(), c=None):
        return self.op("act", fn, reads, writes, c=c)

    def dve(self, fn, reads=(), writes=(), c=None):
        return self.op("dve", fn, reads, writes, c=c)

    def pool(self, fn, reads=(), writes=(), c=None):
        return self.op("pool", fn, reads, writes, c=c)

    DEF_COST = {"pe": 0.12, "act": 0.35, "dve": 0.25, "pool": 0.35}

    def reorder(self, window=40):
        ops = self.ops
        n = len(ops)
        queues = {e: [] for e in ENGS}
        for o in ops:
            queues[o.eng].append(o.idx)
        head = {e: 0 for e in ENGS}
        sched = [False] * n
        fin = [0.0] * n
        etime = {e: 0.0 for e in ENGS}
        order = []
        left = n
        while left:
            best = None
            for e in ENGS:
                q = queues[e]
                h = head[e]
                while h < len(q) and sched[q[h]]:
                    h += 1
                head[e] = h
                if h >= len(q):
                    continue
                seen = 0
                i = h
                et = etime[e]
                while i < len(q) and seen < window:
                    k = q[i]
                    i += 1
                    if sched[k]:
                        continue
                    seen += 1
                    o = ops[k]
                    rdy = 0.0
                    ok = True
                    for d in o.deps:
                        if not sched[d]:
                            ok = False
                            break
                        f = fin[d] + (0.0 if ops[d].eng == e else 0.15)
                        if f > rdy:
                            rdy = f
                    if not ok:
                        continue
                    st = rdy if rdy > et else et
                    key = (st, k)
                    if best is None or key < best[0]:
                        best = (key, e, k)
                    if st <= et:
                        break
            assert best is not None, "scheduler stuck"
            (st, k), e, _ = best
            o = ops[k]
            if o.dma:
                etime[e] = st + 0.06
                fin[k] = st + (o.cost if o.cost is not None else 3.0)
            else:
                c = o.cost if o.cost is not None else self.DEF_COST[e]
                etime[e] = st + c
                fin[k] = st + c
            sched[k] = True
            order.append(k)
            left -= 1
        self.order = order
        self.est_time = max(fin) if fin else 0.0

    def dma(self, q, out, in_, reads=(), writes=(), c=None):
        return self.op(q, lambda e: e.dma_start(out=out, in_=in_), reads, writes, dma=True, c=c)

    def cc(self, fn, reads=(), writes=()):
        o = self.op("pool", fn, reads, writes, dma=True)
        o.cc = True
        return o

    def barrier(self, out, in_):
        allres = set(self.last_w.keys()) | set(self.readers.keys())
        o = self.op("sp", lambda e: e.dma_start(out=out, in_=in_), reads=(), writes=tuple(allres), dma=True)
        self.bar = o.idx
        self.last_w = {}
        self.readers = {}
        return o

    def emit(self, sems, block, final_wait_eng="sp"):
        ops = self.ops
        need = [False] * len(ops)
        for o in ops:
            for d, kind in o.deps.items():
                do = ops[d]
                if do.dma:
                    continue
                if do.eng == o.eng and not o.dma:
                    if do.eng == "pe" or kind == "bank":
                        continue
                need[d] = True
        cnt = {e: 0 for e in ENGS}
        dcnt = {}
        dval = {}
        per_eng = {e: [] for e in ENGS}
        order = getattr(self, "order", None) or list(range(len(ops)))
        for k_ in order:
            o = ops[k_]
            per_eng[o.eng].append(o)
            if o.dma and o.cc:
                o.sem = "cc"
                dval["cc"] = dval.get("cc", 0) + 1
                o.val = dval["cc"]
            elif o.dma:
                k = dcnt.get(o.eng, 0)
                dcnt[o.eng] = k + 1
                key = ("dma", o.eng, k % self.n_dma_sems)
                o.sem = key
                dval[key] = dval.get(key, 0) + 16
                o.val = dval[key]
            elif need[o.idx]:
                cnt[o.eng] += 1
                o.val = cnt[o.eng]
                o.sem = o.eng
                o.signal = True
        self.stats = {e: len(per_eng[e]) for e in ENGS}
        self.stats["signals"] = dict(cnt)

        def run(engname, e):
            waited = {}
            for o in per_eng[engname]:
                wl = {}
                for d, kind in o.deps.items():
                    do = ops[d]
                    if do.dma:
                        wl[do.sem] = max(wl.get(do.sem, 0), do.val)
                        continue
                    if do.eng == o.eng and not o.dma:
                        if do.eng == "pe" or kind == "bank":
                            continue
                    wl[do.sem] = max(wl.get(do.sem, 0), do.val)
                if o.dma and o.cc and o.val > 1:
                    wl[o.sem] = max(wl.get(o.sem, 0), o.val - 1)
                elif o.dma and not o.cc and o.val > 16:
                    wl[o.sem] = max(wl.get(o.sem, 0), o.val - 16)
                for s, v in wl.items():
                    if waited.get(s, 0) >= v:
                        continue
                    waited[s] = v
                    e.wait_ge(sems[s], v)
                ins = o.fn(e)
                if o.dma and o.cc:
                    ins.then_inc(sems[o.sem])
                elif o.dma:
                    ins.then_inc(sems[o.sem], 16)
                elif o.signal:
                    ins.then_inc(sems[o.sem], 1)
            if engname == final_wait_eng:
                for key, v in dval.items():
                    if waited.get(key, 0) < v:
                        e.wait_ge(sems[key], v)
                for en in ("pe", "act", "dve", "pool"):
                    if cnt[en] > 0 and waited.get(en, 0) < cnt[en]:
                        e.wait_ge(sems[en], cnt[en])

        @block.tensor
        def _(e):
            run("pe", e)

        @block.scalar
        def _(e):
            run("act", e)

        @block.vector
        def _(e):
            run("dve", e)

        @block.gpsimd
        def _(e):
            run("pool", e)

        @block.sync
        def _(e):
            run("sp", e)


def make_sems(nc, stack, n_dma_sems=8, queues=("sp", "pool", "act")):
    sems = {}
    for e in ("pe", "act", "dve", "pool", "cc"):
        sems[e] = stack.enter_context(nc.semaphore("s_" + e))
    for q in queues:
        for i in range(n_dma_sems):
            sems[("dma", q, i)] = stack.enter_context(nc.semaphore("d_%s_%d" % (q, i)))
    return sems


_DTSZ = {F32: 4, BF16: 2, U8: 1}


class Arena:
    def __init__(self, ar, size):
        self.ar, self.size, self.off = ar, size, 0

    def reset(self, off=0):
        self.off = off

    def alloc(self, shape, dtype, parts=128):
        n = int(np.prod(shape)) * _DTSZ[dtype]
        off = (self.off + 63) // 64 * 64
        assert off + n <= self.size, ("arena overflow", off, n, self.size)
        self.off = off + n
        ap = self.ar[0:parts, off:off + n].bitcast(dtype)
        if len(shape) > 1:
            names = [chr(ord("a") + i) for i in range(len(shape))]
            pat = "p (%s) -> p %s" % (" ".join(names), " ".join(names))
            ap = ap.rearrange(pat, **{nm: int(s) for nm, s in zip(names, shape)})
        return ap


D = 2048
FF = 5632
TOK = 1024
NT = TOK // 128
KC = D // 128
FC = FF // 128
SSDW = 1024


def build_tail():
    nc = bass.Bass("TRN2", target_bir_lowering=False)
    x = nc.dram_tensor("x_own", [TOK, D], F32, kind="ExternalInput").ap()
    mix = nc.dram_tensor("mix", [TOK, D], F32, kind="ExternalInput").ap()
    w_out = nc.dram_tensor("w_out", [D, D], F32, kind="ExternalInput").ap()
    w_gate = nc.dram_tensor("w_gate", [D, FF], F32, kind="ExternalInput").ap()
    w_up = nc.dram_tensor("w_up", [D, FF], F32, kind="ExternalInput").ap()
    w_down = nc.dram_tensor("w_down", [FF, D], F32, kind="ExternalInput").ap()
    nw = nc.dram_tensor("nw", [3, D], F32, kind="ExternalInput").ap()
    ident = nc.dram_tensor("ident", [128, 128], F32, kind="ExternalInput").ap()
    out = nc.dram_tensor("out", [TOK, D], F32, kind="ExternalOutput").ap()
    h_d = nc.dram_tensor("h_d", [TOK, D], F32, kind="Internal").ap()
    dummy = nc.dram_tensor("dummy_bar", [2, 64], F32, kind="Internal").ap()
    with contextlib.ExitStack() as st:
        ASZ = 200 * 1024
        ar = st.enter_context(nc.sbuf_tensor("arena", [128, ASZ], U8))
        A = Arena(ar, ASZ)
        pbig = st.enter_context(nc.psum_tensor("pbig", [128, 8, 512], F32))
        sems = make_sems(nc, st)
        block = st.enter_context(nc.Block())
        S = Sched(nc)
        tail_body(nc, S, A, pbig, x, mix, w_out, w_gate, w_up, w_down, nw, ident, out, h_d, dummy)
        if os.environ.get('NO_REORDER') is None:
            S.reorder()
        S.emit(sems, block)
    return nc


def rms_rstd(S, src, n, ss, sq, tag, rd, wr_extra=(), c=None):
    S.act(lambda e: e.activation(sq, src, AF.Square, accum_out=ss), reads=rd, writes=[tag + "ss", tag + "sq"], c=c)
    S.act(lambda e: e.activation(ss, ss, AF.Sqrt, scale=1.0 / n, bias=EPS_AP[0]), reads=[tag + "ss"], writes=[tag + "ss"])
    S.dve(lambda e: e.reciprocal(ss, ss), reads=[tag + "ss"], writes=[tag + "ss"])


EPS_AP = [None]


def tail_body(nc, S, A, pbig, x, mix, w_out, w_gate, w_up, w_down, nw, ident, out, h_d, dummy):
    bk = {"ptr": (4, 5)}
    for i_ in range(8):
        bk["pacc%d" % i_] = (i_,)
        bk["pd%d" % i_] = (i_,)
    for i_ in range(2):
        bk["pg%d" % i_] = (i_ * 4, i_ * 4 + 1)
        bk["pu%d" % i_] = (i_ * 4 + 2, i_ * 4 + 3)
    S.bank_of = bk
    identb = A.alloc([128], BF16)
    nwb = A.alloc([3, D], F32)
    epsb = A.alloc([1], F32)
    ss = A.alloc([4], F32)
    EPS_AP[0] = epsb
    S.pool(lambda e: e.memset(epsb, EPS), writes=["eps"])
    S.dma("pool", identb, ident, writes=["ident"])
    S.dma("sp", nwb, nw.rearrange("a b -> (a b)").partition_broadcast(128).rearrange("p (a b) -> p a b", a=3), writes=["nwb"])
    vT = A.alloc([KC, TOK], BF16)
    base_persist = A.off

    wo = A.alloc([KC, D], BF16)
    for half in range(2):
        S.dma("pool", wo[:, half * 8:(half + 1) * 8, :],
              w_out[half * 1024:(half + 1) * 1024, :].rearrange("(k p) n -> p k n", p=128), writes=["wo%d" % half])
    xt = [A.alloc([D], F32) for _ in range(2)]
    mt = [A.alloc([D], F32) for _ in range(2)]
    sq = A.alloc([D], F32)
    mb = A.alloc([D], BF16)
    mT = A.alloc([KC, 128], BF16)
    hs = [A.alloc([D], F32) for _ in range(2)]
    vb = A.alloc([D], BF16)
    pacc = pbig[:, 0:4, :]
    ptr_all = pbig[:, 4:6, :].rearrange("p a b -> p (a b)").bitcast(BF16)
    ptr = ptr_all.rearrange("p (k n) -> p k n", k=KC)

    def loads(tt):
        b = tt % 2
        S.dma("sp", xt[b], x[tt * 128:(tt + 1) * 128, :], writes=["xt%d" % b])
        S.dma("sp", mt[b], mix[tt * 128:(tt + 1) * 128, :], writes=["mt%d" % b])

    loads(0)
    for tt in range(NT):
        b = tt % 2
        if tt + 1 < NT:
            loads(tt + 1)
        rms_rstd(S, mt[b][:, 0:SSDW], SSDW, ss[:, 0:1], sq[:, 0:SSDW], "a", ["mt%d" % b, "eps"])
        S.dve(lambda e, b=b: e.scalar_tensor_tensor(mb[:, 0:SSDW], mt[b][:, 0:SSDW], ss[:, 0:1], nwb[:, 0, 0:SSDW], ALU.mult, ALU.mult),
              reads=["mt%d" % b, "ass", "nwb"], writes=["mb0"])
        S.pool(lambda e, b=b: e.tensor_copy(mb[:, SSDW:D], mt[b][:, SSDW:D]), reads=["mt%d" % b], writes=["mb1"])
        for kc in range(KC):
            S.pe(lambda e, kc=kc: e.transpose(ptr[:, kc, :], mb[:, kc * 128:(kc + 1) * 128], identb),
                 reads=["mb0", "mb1", "ident"], writes=["ptr"])
        S.act(lambda e: e.copy(mT[:, 0:8, :], ptr[:, 0:8, :]), reads=["ptr"], writes=["mTa"])
        S.dve(lambda e: e.tensor_copy(mT[:, 8:16, :], ptr[:, 8:16, :]), reads=["ptr"], writes=["mTb"])
        for cb in range(4):
            for kc in range(KC):
                S.pe(lambda e, cb=cb, kc=kc: e.matmul(pacc[:, cb, :], mT[:, kc, :], wo[:, kc, cb * 512:(cb + 1) * 512],
                                                       start=(kc == 0), stop=(kc == KC - 1)),
                     reads=["mTa", "mTb", "wo0", "wo1"], writes=["pacc%d" % cb])
            S.dve(lambda e, cb=cb, b=b: e.tensor_tensor(hs[b][:, cb * 512:(cb + 1) * 512], pacc[:, cb, :], xt[b][:, cb * 512:(cb + 1) * 512], ALU.add),
                  reads=["pacc%d" % cb, "xt%d" % b], writes=["hs%d_%d" % (b, cb)])
        hres = ["hs%d_%d" % (b, cb) for cb in range(4)]
        S.dma("sp", h_d[tt * 128:(tt + 1) * 128, :], hs[b], reads=hres, writes=["h_d%d" % tt])
        rms_rstd(S, hs[b], D, ss[:, 1:2], sq, "b", hres + ["eps"])
        S.dve(lambda e, b=b: e.scalar_tensor_tensor(vb, hs[b], ss[:, 1:2], nwb[:, 1, :], ALU.mult, ALU.mult),
              reads=hres + ["bss", "nwb"], writes=["vb"])
        for kc in range(KC):
            S.pe(lambda e, kc=kc: e.transpose(ptr[:, kc, :], vb[:, kc * 128:(kc + 1) * 128], identb),
                 reads=["vb", "ident"], writes=["ptr"])
        S.act(lambda e, tt=tt: e.copy(vT[:, 0:8, tt * 128:(tt + 1) * 128], ptr[:, 0:8, :]), reads=["ptr"], writes=["vT%da" % tt])
        S.dve(lambda e, tt=tt: e.tensor_copy(vT[:, 8:16, tt * 128:(tt + 1) * 128], ptr[:, 8:16, :]), reads=["ptr"], writes=["vT%db" % tt])

    S.barrier(dummy[1:2, :], ident[0:1, 0:64])
    A.reset(base_persist)
    hT = A.alloc([FC, TOK], BF16)
    base_b = A.off
    WB = 256
    NB = FF // WB
    wg = [A.alloc([KC, WB], BF16) for _ in range(2)]
    wu = [A.alloc([KC, WB], BF16) for _ in range(2)]
    sg = [A.alloc([TOK], F32) for _ in range(2)]
    for blk in range(NB):
        b = blk % 2
        S.dma("pool", wg[b], w_gate[:, blk * WB:(blk + 1) * WB].rearrange("(k p) n -> p k n", p=128), writes=["wg%d" % b])
        S.dma("pool", wu[b], w_up[:, blk * WB:(blk + 1) * WB].rearrange("(k p) n -> p k n", p=128), writes=["wu%d" % b])
        for j in range(WB // 128):
            fc = blk * (WB // 128) + j
            pb = fc % 2
            pg = pbig[:, pb * 4:pb * 4 + 2, :]
            pu = pbig[:, pb * 4 + 2:pb * 4 + 4, :]
            for hf in range(2):
                for kc in range(KC):
                    S.pe(lambda e, b=b, j=j, hf=hf, kc=kc, pg=pg: e.matmul(pg[:, hf, :], wg[b][:, kc, j * 128:(j + 1) * 128], vT[:, kc, hf * 512:(hf + 1) * 512],
                                                                         start=(kc == 0), stop=(kc == KC - 1)),
                         reads=["wg%d" % b, "vT"], writes=["pg%d" % pb])
            for hf in range(2):
                for kc in range(KC):
                    S.pe(lambda e, b=b, j=j, hf=hf, kc=kc, pu=pu: e.matmul(pu[:, hf, :], wu[b][:, kc, j * 128:(j + 1) * 128], vT[:, kc, hf * 512:(hf + 1) * 512],
                                                                         start=(kc == 0), stop=(kc == KC - 1)),
                         reads=["wu%d" % b, "vT"], writes=["pu%d" % pb])
            S.act(lambda e, pb=pb, pg=pg: e.activation(sg[pb], pg.rearrange("p a b -> p (a b)"), AF.Silu), reads=["pg%d" % pb], writes=["sg%d" % pb])
            S.dve(lambda e, pb=pb, pu=pu, fc=fc: e.tensor_tensor(hT[:, fc, :], sg[pb], pu.rearrange("p a b -> p (a b)"), ALU.mult),
                  reads=["sg%d" % pb, "pu%d" % pb], writes=["hT%d" % fc])

    S.barrier(dummy[1:2, :], ident[0:1, 0:64])
    A.reset(base_b)
    HK = 22
    wd = [A.alloc([HK, 512], BF16) for _ in range(2)]
    hl = [A.alloc([512], F32) for _ in range(2)]
    ys = [A.alloc([512], F32) for _ in range(2)]
    it = 0
    for r in range(4):
        for hf in range(2):
            b = (r * 2 + hf) % 2
            S.dma("pool", wd[b], w_down[hf * HK * 128:(hf + 1) * HK * 128, r * 512:(r + 1) * 512].rearrange("(k p) n -> p k n", p=128), writes=["wd%d" % b])
            for tt in range(NT):
                for k in range(HK):
                    kk = hf * HK + k
                    S.pe(lambda e, b=b, tt=tt, k=k, kk=kk: e.matmul(pbig[:, tt, :], hT[:, kk, tt * 128:(tt + 1) * 128], wd[b][:, k, :],
                                                                    start=(kk == 0), stop=(kk == FC - 1)),
                         reads=["wd%d" % b, "hT"], writes=["pd%d" % tt])
        for tt in range(NT):
            b = it % 2
            it += 1
            S.dma("sp", hl[b], h_d[tt * 128:(tt + 1) * 128, r * 512:(r + 1) * 512], reads=["h_d%d_%d" % (tt, r)], writes=["hl%d" % b])
            S.dve(lambda e, b=b, tt=tt: e.tensor_tensor(ys[b], pbig[:, tt, :], hl[b], ALU.add), reads=["pd%d" % tt, "hl%d" % b], writes=["ys%d" % b])
            S.dma("sp", h_d[tt * 128:(tt + 1) * 128, r * 512:(r + 1) * 512], ys[b], reads=["ys%d" % b], writes=["h_d%d_%d" % (tt, r)])

    S.barrier(dummy[1:2, :], ident[0:1, 0:64])
    A.reset(base_persist)
    yt = [A.alloc([D], F32) for _ in range(2)]
    ot = [A.alloc([D], F32) for _ in range(2)]
    sq2 = A.alloc([D], F32)
    for tt in range(NT):
        b = tt % 2
        S.dma("sp", yt[b], h_d[tt * 128:(tt + 1) * 128, :], writes=["yt%d" % b])
        rms_rstd(S, yt[b], D, ss[:, 2:3], sq2, "c", ["yt%d" % b, "eps"])
        S.dve(lambda e, b=b: e.scalar_tensor_tensor(ot[b], yt[b], ss[:, 2:3], nwb[:, 2, :], ALU.mult, ALU.mult),
              reads=["yt%d" % b, "css", "nwb"], writes=["ot%d" % b])
        S.dma("sp", out[tt * 128:(tt + 1) * 128, :], ot[b], reads=["ot%d" % b])


_CACHE = {}


def run_tail(x, mixed, ssd_norm_w, w_out, ffn_norm_w, w_gate, w_up, w_down, final_norm_w):
    if "tail" not in _CACHE:
        _CACHE["tail"] = build_tail()
    nc = _CACHE["tail"]
    nwv = np.ones((3, D), np.float32)
    nwv[0, :SSDW] = ssd_norm_w
    nwv[1] = ffn_norm_w
    nwv[2] = final_norm_w
    ident = np.eye(128, dtype=np.float32)
    in_maps = []
    for c in range(8):
        b, g = c // 4, c % 4
        in_maps.append({
            "x_own": np.ascontiguousarray(x[b, g * TOK:(g + 1) * TOK]),
            "mix": np.ascontiguousarray(mixed[b, g * TOK:(g + 1) * TOK]),
            "w_out": w_out, "w_gate": w_gate, "w_up": w_up, "w_down": w_down,
            "nw": nwv, "ident": ident,
        })
    res = run_bass_kernel_spmd(nc, in_maps, core_ids=list(range(8)))
    outp = np.empty((2, 4096, D), np.float32)
    for c in range(8):
        b, g = c // 4, c % 4
        outp[b, g * TOK:(g + 1) * TOK] = res.results[c]["out"]
    return outp


SEQ = 4096
NTT = SEQ // 128
NEG = -30000.0
NA = 400
NB_ = 384
NFM = 640
SCALE = 0.125


def build_mix():
    nc = bass.Bass("TRN2", target_bir_lowering=False)
    di = lambda n, s, d=F32: nc.dram_tensor(n, s, d, kind="ExternalInput").ap()
    T = dict(
        xb=di("xb", [SEQ, D]), anw=di("anw", [D]), w_tm=di("w_tm", [D, NA + NB_]), w_fm=di("w_fm", [D, NFM]),
        convw=di("convw", [128, 16]), convb=di("convb", [128, 4]), hp=di("hp", [12]), rope=di("rope", [128, NTT * 16]),
        w1k=di("w1k", [64, 32 * 256]), w1v=di("w1v", [64, 32 * 256]), w2k=di("w2k", [256, 64]), w2v=di("w2v", [256, 64]),
        pek=di("pek", [64, 32]), pev=di("pev", [64, 32]), ident=di("ident", [128, 128]), emat=di("emat", [64, SEQ]),
        tric=di("tric", [128, 128]), tria=di("tria", [128, 128]), cmask=di("cmask", [256, SEQ]), ovl=di("ovl", [256, 64]),
        utri=di("utri", [128, 128]),
    )
    T["mixo"] = nc.dram_tensor("mixo", [SEQ, 512], F32, kind="ExternalOutput").ap()
    T["qr_d"] = nc.dram_tensor("qr_d", [64, 4, SEQ], BF16, kind="Internal").ap()
    T["qu_d"] = nc.dram_tensor("qu_d", [64, 4, SEQ], BF16, kind="Internal").ap()
    T["dummy"] = nc.dram_tensor("dummy_bar", [2, 64], F32, kind="Internal").ap()
    with contextlib.ExitStack() as st:
        ASZ = 207 * 1024
        ar = st.enter_context(nc.sbuf_tensor("arena", [128, ASZ], U8))
        A = Arena(ar, ASZ)
        pbig = st.enter_context(nc.psum_tensor("pbig", [128, 8, 512], F32))
        sems = make_sems(nc, st)
        block = st.enter_context(nc.Block())
        S = Sched(nc)
        mix_body(nc, S, A, pbig, T)
        if os.environ.get('NO_REORDER') is None:
            S.reorder()
        S.emit(sems, block)
    return nc


def pbf(pb, lo, hi):
    return pb[:, lo:hi, :].rearrange("p a b -> p (a b)").bitcast(BF16)


def mix_body(nc, S, A, pbig, T):
    bar = lambda: S.barrier(T["dummy"][1:2, :], T["ident"][0:1, 0:64])
    bk = {"ptr": (0,), "ptr2": (6,), "psA": (1,), "psB": (2,), "psF0": (3,), "psR": (4,), "psS_a": (5,), "psS_c": (5,),
          "psN": (6,), "psY_o": (7,), "psY_d": (7,), "pk": (5,), "pv0": (6,), "pv1": (7,), "pnt": (7,), "posel": (6,), "powin": (7,)}
    for i_ in range(4):
        bk["pbias%d" % i_] = (4,)
        bk["phid%d" % i_] = (i_,)
        bk["psw%d" % i_] = (3 + i_ // 2,)
    for a_ in range(2):
        bk["po%d" % a_] = (4 + a_,)
        for b_ in range(2):
            bk["psc%d_%d" % (a_, b_)] = (a_ * 2 + b_,)
    for i_ in range(3):
        bk["pss%d" % i_] = (i_,)
    S.bank_of = bk
    identb = A.alloc([128], BF16)
    identf = A.alloc([128], F32)
    utri = A.alloc([128], F32)
    tricb = A.alloc([128], BF16)
    triab = A.alloc([128], BF16)
    epsb = A.alloc([1], F32)
    oneb = A.alloc([1], F32)
    EPS_AP[0] = epsb
    KsA = A.alloc([SEQ], BF16)
    KwT = A.alloc([SEQ], BF16)
    kcvcT = A.alloc([SEQ], BF16)
    VsA = A.alloc([NTT, 65], BF16)
    VwA = A.alloc([NTT, 65], BF16)
    gate = A.alloc([NTT, 12], F32)
    hpb = A.alloc([12], F32)
    base_persist = A.off
    S.pool(lambda e: e.memset(epsb, EPS), writes=["eps"])
    S.pool(lambda e: e.memset(oneb, 1.0), writes=["one"])
    S.pool(lambda e: e.memset(VsA, 1.0), writes=["VsA"])
    S.pool(lambda e: e.memset(VwA, 1.0), writes=["VwA"])
    S.dma("pool", identb, T["ident"], writes=["ident"])
    S.dma("sp", identf, T["ident"], writes=["identf"])
    S.dma("sp", utri, T["utri"], writes=["utri"])
    S.dma("pool", tricb, T["tric"], writes=["tric"])
    S.dma("pool", triab, T["tria"], writes=["tria"])
    S.dma("pool", KsA[64:128, :], T["emat"], writes=["KsE"])
    S.dma("sp", hpb, T["hp"].partition_broadcast(128), writes=["hpb"])

    wtm = A.alloc([KC, NA + NB_], BF16)
    wfm = A.alloc([KC, NFM], BF16)
    S.dma("pool", wtm, T["w_tm"].rearrange("(k p) n -> p k n", p=128), writes=["wtm"])
    S.dma("pool", wfm, T["w_fm"].rearrange("(k p) n -> p k n", p=128), writes=["wfm"])
    anwb = A.alloc([D], F32)
    S.dma("sp", anwb, T["anw"].partition_broadcast(128), writes=["anwb"])
    ropet = A.alloc([NTT, 16], F32)
    S.dma("sp", ropet, T["rope"].rearrange("p (t c) -> p t c", c=16), writes=["ropet"])
    convw = A.alloc([16], F32)
    convb = A.alloc([4], F32)
    S.dma("sp", convw, T["convw"], writes=["convw"])
    S.dma("sp", convb, T["convb"], writes=["convb"])
    onesf = A.alloc([128], F32)
    S.pool(lambda e: e.memset(onesf, 1.0), writes=["onesf"])
    two = lambda shape, dt: [A.alloc(shape, dt) for _ in range(2)]
    xt = two([D], F32)
    sq = two([D], BF16)
    ss = A.alloc([8], F32)
    ub = two([D], BF16)
    uT = A.alloc([KC, 512], BF16)
    cbuf = A.alloc([4, 515], F32)
    cacc = two([512], F32)
    xbcT = two([4, 512], BF16)
    zs = two([256], BF16)
    ez = two([256], F32)
    ec = two([512], F32)
    dtt = two([4], F32)
    qk = two([6, 64], F32)
    qkr = two([6, 64], BF16)
    qkb = two([4, 64], BF16)
    rt = two([4, 6, 8], F32)
    qst = two([4, 128], BF16)
    qut = two([4, 128], BF16)
    xtm = two([256], BF16)
    btm = two([128], BF16)
    hst = A.alloc([256], F32)
    hstb = two([256], BF16)
    aneg = A.alloc([4], F32)
    adt = two([4], F32)
    acol = two([4], F32)
    nacol = two([4], F32)
    eac = two([4], F32)
    rhs4 = two([4, 128], F32)
    seg4 = two([4, 128], F32)
    cbm = two([128], F32)
    MT = two([4, 128], BF16)
    alast = two([4], F32)
    dsv = two([4], F32)
    cdv = two([4], F32)
    wsc = two([4], F32)
    xw = two([256], BF16)
    xdt = two([256], BF16)
    ydg = two([256], F32)
    yy = two([256], F32)
    tmpd = two([256], F32)
    yo = two([256], F32)
    S.pool(lambda e: e.memset(cbuf, 0.0), writes=["cbuf", "cbuf0", "cbuf1", "cbuf2", "cbuf3"])
    S.pool(lambda e: e.memset(hst, 0.0), writes=["hst"])
    S.act(lambda e: e.activation(aneg, hpb[:, 4:8], AF.Exp), reads=["hpb"], writes=["aneg"])
    S.dve(lambda e: e.tensor_scalar(aneg, aneg, -1.0, None, ALU.mult), reads=["aneg"], writes=["aneg"])

    ptr = pbf(pbig, 0, 1)
    psA = pbig[:, 1, 0:NA]
    psB = pbig[:, 2, 0:NB_]
    psF = pbig[:, 3, :]
    psR = pbig[:, 4, :]
    psS = pbig[:, 5, :]
    psN = pbig[:, 6, 0:256]
    psY = pbig[:, 7, :]

    def load_x(tt):
        S.dma("sp", xt[tt % 2], T["xb"][tt * 128:(tt + 1) * 128, :], writes=["xt%d" % (tt % 2)], c=5.0)

    def b4(ap, n):
        return ap.unsqueeze(2).to_broadcast([128, 4, n])

    load_x(0)
    for G in range(SEQ // 512):
        gp = G % 2
        for j in range(4):
            tt = G * 4 + j
            b = tt % 2
            if tt + 1 < NTT:
                load_x(tt + 1)
            ssb = ss[:, b:b + 1]
            S.act(lambda e, b=b, ssb=ssb: e.activation(sq[b], xt[b], AF.Square, accum_out=ssb), reads=["xt%d" % b], writes=["n%dss" % b, "n%dsq" % b], c=1.9)
            S.act(lambda e, ssb=ssb: e.activation(ssb, ssb, AF.Ln, scale=1.0 / D, bias=epsb), reads=["n%dss" % b, "eps"], writes=["n%dss" % b])
            S.act(lambda e, ssb=ssb: e.activation(ssb, ssb, AF.Exp, scale=-0.5), reads=["n%dss" % b], writes=["n%dss" % b])
            S.dve(lambda e, b=b: e.scalar_tensor_tensor(ub[b], xt[b], ss[:, b:b + 1], anwb, ALU.mult, ALU.mult),
                  reads=["xt%d" % b, "n%dss" % b, "anwb"], writes=["ub%d" % b], c=2.2)
            for half in range(2):
                for k8 in range(8):
                    kc = half * 8 + k8
                    S.pe(lambda e, kc=kc, k8=k8, b=b: e.transpose(ptr[:, k8 * 128:(k8 + 1) * 128], ub[b][:, kc * 128:(kc + 1) * 128], identb),
                         reads=["ub%d" % b, "ident"], writes=["ptr"])
                if half == 0:
                    S.act(lambda e, j=j: e.copy(uT[:, 0:8, j * 128:(j + 1) * 128], ptr.rearrange("p (k n) -> p k n", k=8)), reads=["ptr"], writes=["uT%d_0" % j], c=0.9)
                else:
                    S.dve(lambda e, j=j: e.tensor_copy(uT[:, 8:16, j * 128:(j + 1) * 128], ptr.rearrange("p (k n) -> p k n", k=8)), reads=["ptr"], writes=["uT%d_1" % j], c=0.7)
        uTr = ["uT%d_%d" % (j, h) for j in range(4) for h in range(2)]
        for c in range(5):
            for kc in range(KC):
                S.pe(lambda e, c=c, kc=kc: e.matmul(psF, wfm[:, kc, c * 128:(c + 1) * 128], uT[:, kc, :], start=(kc == 0), stop=(kc == KC - 1)),
                     reads=uTr + ["wfm"], writes=["psF0"], c=0.22)
            if c < 4:
                ca = cacc[c % 2]
                car = "cacc%d" % (c % 2)
                S.act(lambda e, c=c: e.copy(cbuf[:, c, 3:515], psF), reads=["psF0"], writes=["cbuf%d" % c], c=0.6)
                S.dve(lambda e, c=c, ca=ca: e.tensor_scalar(ca, cbuf[:, c, 0:512], convw[:, c * 4:c * 4 + 1], convb[:, c:c + 1], ALU.mult, ALU.add), reads=["cbuf%d" % c, "convw", "convb"], writes=[car], c=0.4)
                for k in range(1, 4):
                    S.dve(lambda e, c=c, k=k, ca=ca: e.scalar_tensor_tensor(ca, cbuf[:, c, k:k + 512], convw[:, c * 4 + k:c * 4 + k + 1], ca, ALU.mult, ALU.add),
                          reads=["cbuf%d" % c, car, "convw"], writes=[car], c=0.65)
                ece = ec[c % 2]
                ecr = "ec%d" % (c % 2)
                S.act(lambda e, ca=ca, ece=ece: e.activation(ece, ca, AF.Exp, scale=-1.0), reads=[car], writes=[ecr], c=0.6)
                S.act(lambda e, ece=ece: e.activation(ece, ece, AF.Ln, bias=oneb), reads=[ecr, "one"], writes=[ecr], c=0.6)
                S.act(lambda e, ece=ece: e.activation(ece, ece, AF.Exp, scale=-1.0), reads=[ecr], writes=[ecr], c=0.6)
                S.pool(lambda e, c=c, ca=ca, ece=ece, gp=gp: e.tensor_tensor(xbcT[gp][:, c, :], ca, ece, ALU.mult), reads=[car, ecr], writes=["xbcT%d_%d" % (gp, c)], c=2.0)
                S.pool(lambda e, c=c: e.tensor_copy(cbuf[:, c, 0:3], cbuf[:, c, 512:515]), reads=["cbuf%d" % c], writes=["cbuf%d" % c])
            else:
                S.act(lambda e, G=G: e.copy(kcvcT[:, G * 512:(G + 1) * 512], psF), reads=["psF0"], writes=["kcvcT"], c=0.6)
        xr = lambda c: "xbcT%d_%d" % (gp, c)
        for j in range(4):
            tt = G * 4 + j
            p = tt % 2
            P = str(p)
            tok = slice(tt * 128, (tt + 1) * 128)
            js = slice(j * 128, (j + 1) * 128)
            for kc in range(KC):
                S.pe(lambda e, js=js, kc=kc: e.matmul(psA, uT[:, kc, js], wtm[:, kc, 0:NA], start=(kc == 0), stop=(kc == KC - 1)),
                     reads=uTr + ["wtm"], writes=["psA"], c=0.19)
            for kc in range(KC):
                S.pe(lambda e, js=js, kc=kc: e.matmul(psB, uT[:, kc, js], wtm[:, kc, NA:NA + NB_], start=(kc == 0), stop=(kc == KC - 1)),
                     reads=uTr + ["wtm"], writes=["psB"], c=0.18)
            S.act(lambda e, p=p: e.activation(ez[p], psA[:, 0:256], AF.Exp, scale=-1.0), reads=["psA"], writes=["ez" + P])
            S.act(lambda e, p=p: e.activation(ez[p], ez[p], AF.Ln, bias=oneb), reads=["ez" + P, "one"], writes=["ez" + P])
            S.act(lambda e, p=p: e.activation(ez[p], ez[p], AF.Exp, scale=-1.0), reads=["ez" + P], writes=["ez" + P])
            S.dve(lambda e, p=p: e.tensor_tensor(zs[p], psA[:, 0:256], ez[p], ALU.mult), reads=["psA", "ez" + P], writes=["zs" + P])
            S.dve(lambda e, p=p: e.tensor_tensor(dtt[p], psA[:, 256:260], hpb[:, 0:4], ALU.add), reads=["psA", "hpb"], writes=["dtt" + P])
            S.act(lambda e, p=p: e.activation(dtt[p], dtt[p], AF.Exp), reads=["dtt" + P], writes=["dtt" + P])
            S.act(lambda e, p=p: e.activation(dtt[p], dtt[p], AF.Ln, bias=oneb), reads=["dtt" + P, "one"], writes=["dtt" + P])
            S.act(lambda e, tt=tt: e.activation(gate[:, tt, :], psA[:, 260:272], AF.Exp, scale=-1.0), reads=["psA"], writes=["gate%d" % tt])
            S.dve(lambda e, tt=tt: e.tensor_scalar(gate[:, tt, :], gate[:, tt, :], 1.0, None, ALU.add), reads=["gate%d" % tt], writes=["gate%d" % tt])
            S.dve(lambda e, tt=tt: e.reciprocal(gate[:, tt, :], gate[:, tt, :]), reads=["gate%d" % tt], writes=["gate%d" % tt])
            S.dve(lambda e, tt=tt: e.tensor_copy(VsA[:, tt, 0:64], psA[:, 272:336]), reads=["psA", "VsA"], writes=["VsA%d" % tt])
            S.dve(lambda e, tt=tt: e.tensor_copy(VwA[:, tt, 0:64], psA[:, 336:400]), reads=["psA", "VwA"], writes=["VwA%d" % tt])
            S.act(lambda e, p=p: e.copy(qk[p], psB.rearrange("p (a b) -> p a b", a=6)), reads=["psB"], writes=["qk" + P], c=0.5)
            S.pool(lambda e, p=p: e.tensor_copy(qkb[p], qk[p][:, 0:4, :]), reads=["qk" + P], writes=["qkb" + P])
            S.pool(lambda e, p=p: e.tensor_copy(qkr[p], qk[p]), reads=["qk" + P], writes=["qkr" + P])
            cosb = ropet[:, tt, 0:8].unsqueeze(1).to_broadcast([128, 6, 8])
            sinb = ropet[:, tt, 8:16].unsqueeze(1).to_broadcast([128, 6, 8])
            S.dve(lambda e, cosb=cosb, p=p: e.tensor_tensor(rt[p][:, 0], qk[p][:, :, 0:8], cosb, ALU.mult), reads=["qk" + P, "ropet"], writes=["rt0" + P])
            S.dve(lambda e, sinb=sinb, p=p: e.tensor_tensor(rt[p][:, 1], qk[p][:, :, 8:16], sinb, ALU.mult), reads=["qk" + P, "ropet"], writes=["rt1" + P])
            S.dve(lambda e, cosb=cosb, p=p: e.tensor_tensor(rt[p][:, 2], qk[p][:, :, 8:16], cosb, ALU.mult), reads=["qk" + P, "ropet"], writes=["rt2" + P])
            S.dve(lambda e, sinb=sinb, p=p: e.tensor_tensor(rt[p][:, 3], qk[p][:, :, 0:8], sinb, ALU.mult), reads=["qk" + P, "ropet"], writes=["rt3" + P])
            S.dve(lambda e, p=p: e.tensor_tensor(qkr[p][:, :, 0:8], rt[p][:, 0], rt[p][:, 1], ALU.subtract), reads=["rt0" + P, "rt1" + P, "qkr" + P], writes=["qkr" + P])
            S.dve(lambda e, p=p: e.tensor_tensor(qkr[p][:, :, 8:16], rt[p][:, 2], rt[p][:, 3], ALU.add), reads=["rt2" + P, "rt3" + P, "qkr" + P], writes=["qkr" + P])
            ptq = ptr[0:64, 0:768].rearrange("p (a b) -> p a b", a=6)
            ptu = pbf(pbig, 6, 7)[0:64, 512:1024].rearrange("p (a b) -> p a b", a=4)
            for a in range(6):
                S.pe(lambda e, a=a, p=p: e.transpose(ptq[:, a, :], qkr[p][:, a, :], identb), reads=["qkr" + P, "ident"], writes=["ptr"])
            for a in range(4):
                S.pe(lambda e, a=a, p=p: e.transpose(ptu[:, a, :], qkb[p][:, a, :], identb), reads=["qkb" + P, "ident"], writes=["ptr2"])
            S.act(lambda e, p=p: e.copy(qst[p][0:64], ptq[:, 0:4, :]), reads=["ptr"], writes=["qst" + P])
            S.dve(lambda e, p=p: e.tensor_copy(qut[p][0:64], ptu), reads=["ptr2"], writes=["qut" + P])
            S.act(lambda e, tok=tok: e.copy(KsA[0:64, tok], ptq[:, 4, :]), reads=["ptr"], writes=["KsA%d" % tt])
            S.dve(lambda e, tok=tok: e.tensor_copy(KwT[0:64, tok], ptq[:, 5, :]), reads=["ptr"], writes=["KwT%d" % tt])
            S.dma("sp", T["qr_d"][:, :, tok], qst[p][0:64], reads=["qst" + P], writes=["qr_d%d" % tt])
            S.dma("sp", T["qu_d"][:, :, tok], qut[p][0:64], reads=["qut" + P], writes=["qu_d%d" % tt])
            ptx = ptr[:, 0:384]
            for c in range(3):
                S.pe(lambda e, c=c, js=js, gp=gp: e.transpose(ptx[:, c * 128:(c + 1) * 128], xbcT[gp][:, c, js], identb),
                     reads=[xr(c), "ident"], writes=["ptr"])
            S.act(lambda e, p=p: e.copy(xtm[p], ptx[:, 0:256]), reads=["ptr"], writes=["xtm" + P])
            S.act(lambda e, p=p: e.copy(btm[p], ptx[:, 256:384]), reads=["ptr"], writes=["btm" + P])
            S.dve(lambda e, p=p: e.tensor_tensor(adt[p], dtt[p], aneg, ALU.mult), reads=["dtt" + P, "aneg"], writes=["adt" + P])
            S.pe(lambda e, p=p: e.matmul(psS[:, 0:4], utri, adt[p], start=True, stop=True), reads=["utri", "adt" + P], writes=["psS_a"])
            S.act(lambda e, p=p: e.copy(acol[p], psS[:, 0:4]), reads=["psS_a"], writes=["acol" + P])
            S.dve(lambda e, p=p: e.tensor_scalar(nacol[p], psS[:, 0:4], -1.0, None, ALU.mult), reads=["psS_a"], writes=["nacol" + P])
            S.act(lambda e, p=p: e.activation(eac[p], acol[p], AF.Exp), reads=["acol" + P], writes=["eac" + P])
            S.pe(lambda e, js=js, gp=gp: e.matmul(psS[:, 256:384], xbcT[gp][:, 2, js], xbcT[gp][:, 3, js], start=True, stop=True),
                 reads=[xr(2), xr(3)], writes=["psS_c"])
            S.dve(lambda e, p=p: e.tensor_tensor(cbm[p], psS[:, 256:384], utri, ALU.mult), reads=["psS_c", "utri"], writes=["cbm" + P])
            S.dve(lambda e, p=p: e.tensor_tensor(rhs4[p], utri.unsqueeze(1).to_broadcast([128, 4, 128]), b4(adt[p], 128), ALU.mult),
                  reads=["utri", "adt" + P], writes=["rhs4" + P], c=0.6)
            S.pe(lambda e, p=p: e.matmul(psR, onesf, rhs4[p].rearrange("p a b -> p (a b)"), start=True, stop=True), reads=["onesf", "rhs4" + P], writes=["psR"], c=0.9)
            psR4 = psR.rearrange("p (a b) -> p a b", a=4)
            S.dve(lambda e, p=p: e.tensor_tensor(seg4[p], psR4, b4(acol[p], 128), ALU.subtract), reads=["psR", "acol" + P], writes=["seg4" + P], c=0.7)
            S.dve(lambda e, p=p: e.tensor_scalar(seg4[p], seg4[p], 0.0, None, ALU.min), reads=["seg4" + P], writes=["seg4" + P], c=0.35)
            S.act(lambda e, p=p: e.activation(seg4[p], seg4[p], AF.Exp), reads=["seg4" + P], writes=["seg4" + P], c=0.6)
            S.dve(lambda e, p=p: e.tensor_tensor(MT[p], seg4[p], cbm[p].unsqueeze(1).to_broadcast([128, 4, 128]), ALU.mult),
                  reads=["seg4" + P, "cbm" + P], writes=["MT" + P], c=0.6)
            S.dve(lambda e, p=p: e.tensor_copy(alast[p], psR4[:, :, 127]), reads=["psR"], writes=["alast" + P])
            S.dve(lambda e, p=p: e.tensor_tensor(dsv[p], nacol[p], alast[p], ALU.add), reads=["nacol" + P, "alast" + P], writes=["dsv" + P])
            S.act(lambda e, p=p: e.activation(dsv[p], dsv[p], AF.Exp), reads=["dsv" + P], writes=["dsv" + P])
            S.act(lambda e, p=p: e.activation(cdv[p], alast[p], AF.Exp), reads=["alast" + P], writes=["cdv" + P])
            S.dve(lambda e, p=p: e.tensor_tensor(wsc[p], dtt[p], dsv[p], ALU.mult), reads=["dtt" + P, "dsv" + P], writes=["wsc" + P])
            v4 = lambda ap: ap.rearrange("p (h d) -> p h d", h=4)
            S.dve(lambda e, p=p: e.tensor_tensor(v4(xw[p]), v4(xtm[p]), b4(wsc[p], 64), ALU.mult), reads=["xtm" + P, "wsc" + P], writes=["xw" + P])
            S.dve(lambda e, p=p: e.tensor_tensor(v4(xdt[p]), v4(xtm[p]), b4(dtt[p], 64), ALU.mult), reads=["xtm" + P, "dtt" + P], writes=["xdt" + P])
            S.pool(lambda e, p=p: e.tensor_copy(hstb[p], hst), reads=["hst"], writes=["hstb" + P])
            S.pe(lambda e, js=js, gp=gp, p=p: e.matmul(psY[:, 0:256], xbcT[gp][:, 3, js], hstb[p], start=True, stop=True), reads=[xr(3), "hstb" + P], writes=["psY_o"])
            for h in range(4):
                S.pe(lambda e, h=h, p=p: e.matmul(psY[:, 256 + h * 64:256 + (h + 1) * 64], MT[p][:, h, :], xdt[p][:, h * 64:(h + 1) * 64], start=True, stop=True),
                     reads=["MT" + P, "xdt" + P], writes=["psY_d"])
            S.pe(lambda e, p=p: e.matmul(psN, btm[p], xw[p], start=True, stop=True), reads=["btm" + P, "xw" + P], writes=["psN"])
            S.dve(lambda e, p=p: e.tensor_tensor(v4(hst), v4(hst), b4(cdv[p], 64), ALU.mult), reads=["hst", "cdv" + P], writes=["hst"])
            S.dve(lambda e: e.tensor_tensor(hst, hst, psN, ALU.add), reads=["hst", "psN"], writes=["hst"])
            S.act(lambda e, p=p: e.copy(ydg[p], psY[:, 256:512]), reads=["psY_d"], writes=["ydg" + P])
            S.dve(lambda e, p=p: e.tensor_tensor(v4(yy[p]), v4(psY[:, 0:256]), b4(eac[p], 64), ALU.mult), reads=["psY_o", "eac" + P], writes=["yy" + P])
            S.pool(lambda e, p=p: e.tensor_tensor(v4(tmpd[p]), v4(xtm[p]), b4(hpb[:, 8:12], 64), ALU.mult), reads=["xtm" + P, "hpb"], writes=["tmpd" + P])
            S.dve(lambda e, p=p: e.tensor_tensor(yy[p], yy[p], ydg[p], ALU.add), reads=["yy" + P, "ydg" + P], writes=["yy" + P])
            S.dve(lambda e, p=p: e.tensor_tensor(yy[p], yy[p], tmpd[p], ALU.add), reads=["yy" + P, "tmpd" + P], writes=["yy" + P])
            S.dve(lambda e, p=p: e.tensor_tensor(yo[p], yy[p], zs[p], ALU.mult), reads=["yy" + P, "zs" + P], writes=["yo" + P])
            S.dma("sp", T["mixo"][tok, 0:256], yo[p], reads=["yo" + P], writes=["mixo_s%d" % tt])

    if int(os.environ.get('MIX_STOP', '9')) <= 1:
        return
    bar()
    A.reset(base_persist)
    attO = A.alloc([NTT, 256], F32)
    nmT = A.alloc([SEQ], BF16)
    w1 = A.alloc([32, 256], BF16)
    S.dma("pool", w1[0:64], T["w1k"].rearrange("d (l h) -> d l h", l=32), writes=["w1k"])
    S.dma("pool", w1[64:128], T["w1v"].rearrange("d (l h) -> d l h", l=32), writes=["w1v"])
    pe_ = A.alloc([32], BF16)
    S.dma("pool", pe_[0:64], T["pek"], writes=["pek"])
    S.dma("pool", pe_[64:128], T["pev"], writes=["pev"])
    w2 = A.alloc([2, 2, 64], BF16)
    S.dma("pool", w2[:, 0], T["w2k"].rearrange("(c p) d -> p c d", p=128), writes=["w2k"])
    S.dma("pool", w2[:, 1], T["w2v"].rearrange("(c p) d -> p c d", p=128), writes=["w2v"])
    cbias = A.alloc([4], F32)
    hsb = A.alloc([4, 256], BF16)
    KcT = A.alloc([256], BF16)
    VcA = A.alloc([2, 129], BF16)
    S.pool(lambda e: e.memset(hsb, 0.0), writes=["hsb"])
    S.pool(lambda e: e.memset(VcA, 0.0), writes=["VcA"])
    S.pool(lambda e: e.memset(KcT, 0.0), writes=["KcT"])
    S.pool(lambda e: e.memset(VcA[:, :, 64:65], 1.0), reads=["VcA"], writes=["VcA"])
    S.dma("pool", VcA[:, :, 65:129], T["ovl"].rearrange("(c p) j -> p c j", p=128), reads=["VcA"], writes=["VcA"])
    for kv in range(2):
        rows = slice(kv * 64, (kv + 1) * 64)
        for hc in range(2):
            idx = kv * 2 + hc
            pb_ = pbig[:, idx, 0:255]
            pbias = pbig[:, 4, idx:idx + 1]
            for l in range(32):
                S.pe(lambda e, rows=rows, hc=hc, l=l, pbias=pbias: e.matmul(pbias, w1[rows, l, hc * 128:(hc + 1) * 128], pe_[rows, l:l + 1], start=(l == 0), stop=(l == 31)),
                     reads=["w1k", "w1v", "pek", "pev"], writes=["pbias%d" % idx])
            S.act(lambda e, idx=idx, pbias=pbias: e.copy(cbias[:, idx:idx + 1], pbias), reads=["pbias%d" % idx], writes=["cbias%d" % idx])
            for l in range(32):
                S.pe(lambda e, rows=rows, hc=hc, l=l, pb_=pb_: e.matmul(pb_, w1[rows, l, hc * 128:(hc + 1) * 128], kcvcT[rows, l:l + 16 * 254 + 1:16], start=(l == 0), stop=(l == 31)),
                     reads=["w1k", "w1v", "kcvcT"], writes=["phid%d" % idx])
            S.act(lambda e, idx=idx, pb_=pb_: e.activation(hsb[:, idx, 0:255], pb_, AF.Silu, bias=cbias[:, idx:idx + 1]), reads=["phid%d" % idx, "cbias%d" % idx, "hsb"], writes=["hsb%d" % idx])
    pk = pbig[0:64, 5, 0:255]
    for hc in range(2):
        S.pe(lambda e, hc=hc: e.matmul(pk, w2[:, 0, hc, :], hsb[:, hc, 0:255], start=(hc == 0), stop=(hc == 1)), reads=["w2k", "hsb0", "hsb1"], writes=["pk"])
    S.act(lambda e: e.copy(KcT[0:64, 0:255], pk), reads=["pk", "KcT"], writes=["KcT"])
    for it in range(2):
        m = 128 if it == 0 else 127
        pv = pbig[0:m, 6 + it, 0:64]
        for hc in range(2):
            S.pe(lambda e, it=it, hc=hc, m=m, pv=pv: e.matmul(pv, hsb[:, 2 + hc, it * 128:it * 128 + m], w2[:, 1, hc, :], start=(hc == 0), stop=(hc == 1)),
                 reads=["w2v", "hsb2", "hsb3"], writes=["pv%d" % it])
        S.act(lambda e, it=it, m=m, pv=pv: e.copy(VcA[0:m, it, 0:64], pv), reads=["pv%d" % it, "VcA"], writes=["VcA"])

    if int(os.environ.get('MIX_STOP', '9')) <= 2:
        return
    bar()
    base3 = A.off
    qu = [A.alloc([4, 512], BF16) for _ in range(2)]
    cmk = [A.alloc([2, 512], BF16) for _ in range(2)]
    PcT = [A.alloc([2, 512], BF16) for _ in range(2)]
    imp = A.alloc([4, 64], F32)
    imp2 = A.alloc([64], F32)
    m8 = A.alloc([16], F32)
    thr = A.alloc([1], F32)
    rr = A.alloc([2], F32)
    nmb = A.alloc([128], BF16)
    S.pool(lambda e: e.memset(nmb, 0.0), writes=["nmb"])
    pnt = pbf(pbig, 7, 8)[:, 0:128]
    for Q in range(8):
        qb = Q % 2
        qs = slice(Q * 512, (Q + 1) * 512)
        S.dma("sp", qu[qb][0:64], T["qu_d"][:, :, qs], writes=["qu%d" % qb])
        S.dma("pool", cmk[qb], T["cmask"][:, qs].rearrange("(c p) t -> p c t", p=128), writes=["cmk%d" % qb])
        for h in range(4):
            pb2 = h % 2
            for it in range(2):
                ps_ = pbig[:, pb2 * 2 + it, :]
                S.pe(lambda e, it=it, h=h, qb=qb, ps_=ps_: e.matmul(ps_, KcT[0:64, it * 128:(it + 1) * 128], qu[qb][0:64, h, :], start=True, stop=False),
                     reads=["KcT", "qu%d" % qb], writes=["psc%d_%d" % (pb2, it)])
                S.pe(lambda e, it=it, qb=qb, ps_=ps_: e.matmul(ps_, identb, cmk[qb][:, it, :], start=False, stop=True),
                     reads=["ident", "cmk%d" % qb], writes=["psc%d_%d" % (pb2, it)])
                S.act(lambda e, it=it, pb2=pb2, ps_=ps_: e.activation(PcT[pb2][:, it, :], ps_, AF.Exp, scale=SCALE), reads=["psc%d_%d" % (pb2, it)], writes=["PcT%d_%d" % (pb2, it)])
            for sub in range(4):
                tt = Q * 4 + sub
                po = pbig[:, 4 + (sub % 2), 0:129]
                for it in range(2):
                    S.pe(lambda e, it=it, pb2=pb2, sub=sub, po=po: e.matmul(po, PcT[pb2][:, it, sub * 128:(sub + 1) * 128], VcA[:, it, :], start=(it == 0), stop=(it == 1)),
                         reads=["PcT%d_0" % pb2, "PcT%d_1" % pb2, "VcA"], writes=["po%d" % (sub % 2)])
                pr = ["po%d" % (sub % 2)]
                S.dve(lambda e, po=po: e.tensor_scalar(rr[:, 0:1], po[:, 64:65], 1e-30, None, ALU.add), reads=pr, writes=["rr0"])
                S.dve(lambda e: e.reciprocal(rr[:, 0:1], rr[:, 0:1]), reads=["rr0"], writes=["rr0"])
                if h == 0:
                    S.dve(lambda e, po=po, sub=sub: e.tensor_scalar(imp[:, sub, :], po[:, 65:129], rr[:, 0:1], None, ALU.mult), reads=pr + ["rr0"], writes=["imp%d" % sub])
                else:
                    S.dve(lambda e, po=po, sub=sub: e.scalar_tensor_tensor(imp[:, sub, :], po[:, 65:129], rr[:, 0:1], imp[:, sub, :], ALU.mult, ALU.add),
                          reads=pr + ["rr0", "imp%d" % sub], writes=["imp%d" % sub])
                S.dve(lambda e, tt=tt, h=h: e.tensor_tensor(rr[:, 1:2], rr[:, 0:1], gate[:, tt, h * 3:h * 3 + 1], ALU.mult), reads=["rr0", "gate"], writes=["rr1"])
                S.dve(lambda e, po=po, tt=tt, h=h: e.tensor_scalar(attO[:, tt, h * 64:(h + 1) * 64], po[:, 0:64], rr[:, 1:2], None, ALU.mult), reads=pr + ["rr1"], writes=["attO%d" % tt])
        for sub in range(4):
            tt = Q * 4 + sub
            ir = ["imp%d" % sub]
            im = imp[:, sub, :]
            S.pool(lambda e, im=im: e.memset(im[:, 0:1], 1e4), reads=ir, writes=ir)
            lo = max(2 * tt - 1, 0)
            S.pool(lambda e, im=im, lo=lo, tt=tt: e.memset(im[0:64, lo:2 * tt + 1], 1e4), reads=ir, writes=ir)
            S.pool(lambda e, im=im, tt=tt: e.memset(im[64:128, 2 * tt:2 * tt + 2], 1e4), reads=ir, writes=ir)
            if 2 * tt + 1 < 64:
                S.pool(lambda e, im=im, tt=tt: e.memset(im[0:64, 2 * tt + 1:64], -1.0), reads=ir, writes=ir)
            if 2 * tt + 2 < 64:
                S.pool(lambda e, im=im, tt=tt: e.memset(im[64:128, 2 * tt + 2:64], -1.0), reads=ir, writes=ir)
            S.dve(lambda e, im=im: e.max(m8[:, 0:8], im), reads=ir, writes=["m8a"])
            S.dve(lambda e, im=im: e.match_replace(imp2, m8[:, 0:8], im, -1e30), reads=ir + ["m8a"], writes=["imp2"])
            S.dve(lambda e: e.max(m8[:, 8:16], imp2), reads=["imp2"], writes=["m8b"])
            S.dve(lambda e: e.tensor_scalar(thr, m8[:, 15:16], 0.0, None, ALU.max), reads=["m8b"], writes=["thr"])
            S.dve(lambda e, im=im: e.tensor_scalar(nmb[:, 64:128], im, thr, NEG, ALU.is_lt, ALU.mult), reads=ir + ["thr", "nmb"], writes=["nmb"])
            S.pe(lambda e: e.transpose(pnt, nmb, identb), reads=["nmb", "ident"], writes=["pnt"])
            S.act(lambda e, tt=tt: e.copy(nmT[64:128, tt * 128:(tt + 1) * 128], pnt[64:128, :]), reads=["pnt"], writes=["nmT"])

    if int(os.environ.get('MIX_STOP', '9')) <= 3:
        return
    bar()
    A.reset(base3)
    Qa = [A.alloc([4, 512], BF16) for _ in range(2)]
    PT = [A.alloc([512], BF16) for _ in range(3)]
    PW = [A.alloc([128], BF16) for _ in range(3)]
    r4 = A.alloc([2], F32)
    pti = 0
    pwi = 0
    for Q in range(8):
        qb = Q % 2
        qs = slice(Q * 512, (Q + 1) * 512)
        S.dma("sp", Qa[qb][0:64], T["qr_d"][:, :, qs], writes=["Qa%d" % qb])
        for h in range(4):
            S.pool(lambda e, qb=qb, h=h, qs=qs: e.tensor_copy(Qa[qb][64:128, h, :], nmT[64:128, qs]), reads=["nmT"], writes=["Qm%d_%d" % (qb, h)])
        for h in range(4):
            qr_ = ["Qa%d" % qb, "Qm%d_%d" % (qb, h)]
            posel = pbig[:, 6, 0:260].rearrange("p (s c) -> p s c", s=4)
            powin = pbig[:, 7, 0:260].rearrange("p (s c) -> p s c", s=4)
            S.dve(lambda e: e.memset(pbig[:, 6, 0:260], 0.0), writes=["posel"])
            for kt in range(4 * Q + 4):
                ks_ = slice(kt * 128, (kt + 1) * 128)
                sb_ = kt % 3
                ps_ = pbig[:, sb_, :]
                pres = "pss%d" % sb_
                o = kt - 4 * Q
                if o < 0:
                    S.pe(lambda e, ks_=ks_, qb=qb, h=h, ps_=ps_: e.matmul(ps_, KsA[:, ks_], Qa[qb][:, h, :], start=True, stop=True),
                         reads=["KsA", "KsE"] + qr_, writes=[pres])
                    lo = 0
                else:
                    lo = o * 128
                    S.pe(lambda e, ks_=ks_, qb=qb, h=h, ps_=ps_, lo=lo: e.matmul(ps_[:, lo:lo + 128], KsA[:, ks_], Qa[qb][:, h, lo:lo + 128], start=True, stop=False),
                         reads=["KsA", "KsE"] + qr_, writes=[pres])
                    S.pe(lambda e, ps_=ps_, lo=lo: e.matmul(ps_[:, lo:lo + 128], identb, tricb, start=False, stop=True), reads=["ident", "tric"], writes=[pres])
                    if o < 3:
                        S.pe(lambda e, ks_=ks_, qb=qb, h=h, ps_=ps_, lo=lo: e.matmul(ps_[:, lo + 128:512], KsA[:, ks_], Qa[qb][:, h, lo + 128:512], start=True, stop=True),
                             reads=["KsA", "KsE"] + qr_, writes=[pres])
                pt_ = PT[pti % 3]
                ptres = "PT%d" % (pti % 3)
                pti += 1
                S.act(lambda e, ps_=ps_, pt_=pt_, lo=lo: e.activation(pt_[:, lo:512], ps_[:, lo:512], AF.Exp, scale=SCALE), reads=[pres], writes=[ptres])
                for sub in range(max(o, 0), 4):
                    S.pe(lambda e, pt_=pt_, sub=sub, kt=kt, Q=Q, posel=posel: e.matmul(posel[:, sub, :], pt_[:, sub * 128:(sub + 1) * 128], VsA[:, kt, :],
                                                                                 start=False, stop=False, skip_group_check=True),
                         reads=[ptres, "VsA"], writes=["posel"])
            for sub in range(4):
                tt = 4 * Q + sub
                kts = [k for k in range(tt - 4, tt + 1) if k >= 0]
                for kt in kts:
                    ks_ = slice(kt * 128, (kt + 1) * 128)
                    wsl = pwi % 4
                    psw = pbig[:, 3 + wsl // 2, (wsl % 2) * 128:(wsl % 2) * 128 + 128]
                    pwres = "psw%d" % wsl
                    msk = triab if kt == tt - 4 else (tricb if kt == tt else None)
                    S.pe(lambda e, ks_=ks_, qb=qb, h=h, sub=sub, psw=psw, msk=msk: e.matmul(psw, KwT[0:64, ks_], Qa[qb][0:64, h, sub * 128:(sub + 1) * 128], start=True, stop=(msk is None)),
                         reads=["KwT", "Qa%d" % qb], writes=[pwres])
                    if msk is not None:
                        S.pe(lambda e, psw=psw, msk=msk: e.matmul(psw, identb, msk, start=False, stop=True), reads=["ident", "tric", "tria"], writes=[pwres])
                    pw_ = PW[pwi % 3]
                    pwr = "PW%d" % (pwi % 3)
                    pwi += 1
                    S.act(lambda e, psw=psw, pw_=pw_: e.activation(pw_, psw, AF.Exp, scale=SCALE), reads=[pwres], writes=[pwr])
                    S.pe(lambda e, pw_=pw_, sub=sub, kt=kt, kts=kts, powin=powin: e.matmul(powin[:, sub, :], pw_, VwA[:, kt, :], start=(kt == kts[0]), stop=(kt == kts[-1])),
                         reads=[pwr, "VwA"], writes=["powin"])
            for sub in range(4):
                tt = 4 * Q + sub
                for br, (po_, pres) in enumerate(((posel, "posel"), (powin, "powin"))):
                    S.dve(lambda e, po_=po_, sub=sub, br=br: e.reciprocal(r4[:, br:br + 1], po_[:, sub, 64:65]), reads=[pres], writes=["r4_%d" % br])
                    S.dve(lambda e, tt=tt, h=h, br=br: e.tensor_tensor(r4[:, br:br + 1], r4[:, br:br + 1], gate[:, tt, h * 3 + 1 + br:h * 3 + 2 + br], ALU.mult),
                          reads=["r4_%d" % br, "gate"], writes=["r4_%d" % br])
                    S.dve(lambda e, po_=po_, sub=sub, tt=tt, h=h, br=br: e.scalar_tensor_tensor(attO[:, tt, h * 64:(h + 1) * 64], po_[:, sub, 0:64], r4[:, br:br + 1],
                                                                                                  attO[:, tt, h * 64:(h + 1) * 64], ALU.mult, ALU.add),
                          reads=[pres, "r4_%d" % br, "attO%d" % tt], writes=["attO%d" % tt])
        for sub in range(4):
            tt = 4 * Q + sub
            S.dma("sp", T["mixo"][tt * 128:(tt + 1) * 128, 256:512], attO[:, tt, :], reads=["attO%d" % tt])


def _perm_cols():
    return None


def run_mix(inputs):
    if "mix" not in _CACHE:
        _CACHE["mix"] = build_mix()
    nc = _CACHE["mix"]
    x = inputs["x"]
    w_in = inputs["w_in"][0]
    offs = np.cumsum([0, 1024, 1536, 16, 1024, 256, 256, 256, 256, 256, 256, 48])
    oz, oxbc, odt, oq, okc, ovc, oks, ovs, okw, ovw, ogate = offs[:11]
    conv_w = inputs["conv_w"][0]
    conv_b = inputs["conv_b"][0]
    t = np.arange(SEQ, dtype=np.float32)
    inv = (1.0 / (500000.0 ** (np.arange(0, 16, 2, dtype=np.float32) / np.float32(16)))).astype(np.float32)
    ang = (t[:, None] * inv[None, :]).astype(np.float32)
    rope = np.concatenate([np.cos(ang), np.sin(ang)], 1).astype(np.float32)
    rope = np.ascontiguousarray(rope.reshape(NTT, 128, 16).transpose(1, 0, 2).reshape(128, NTT * 16))
    ident = np.eye(128, dtype=np.float32)
    kk = np.arange(128)[:, None]
    qq = np.arange(128)[None, :]
    tric = np.where(kk <= qq, 0.0, NEG).astype(np.float32)
    tria = np.where(kk > qq, 0.0, NEG).astype(np.float32)
    utri = (kk <= qq).astype(np.float32)
    emat = (np.arange(SEQ)[None, :] // 64 == np.arange(64)[:, None]).astype(np.float32)
    ii = np.arange(256)[:, None]
    cmask = np.where((16 * ii + 31 <= np.arange(SEQ)[None, :]) & (ii < 255), 0.0, NEG).astype(np.float32)
    cs = np.arange(255)[:, None] * 16
    ss_ = np.arange(64)[None, :] * 64
    ov = np.clip(np.minimum(cs + 32, ss_ + 64) - np.maximum(cs, ss_), 0, None) / 32.0
    ovl = np.zeros((256, 64), np.float32)
    ovl[:255] = ov
    in_maps = []
    for c in range(8):
        b, g = c // 4, c % 4
        grp = g // 2
        ar = np.arange
        tm_cols = np.concatenate([oz + 256 * g + ar(256), odt + 4 * g + ar(4), ogate + 12 * g + ar(12), ovs + 64 * g + ar(64), ovw + 64 * g + ar(64),
                                  oq + 256 * g + ar(256), oks + 64 * g + ar(64), okw + 64 * g + ar(64)])
        xcols = np.concatenate([256 * g + ar(256), 1024 + 128 * grp + ar(128), 1280 + 128 * grp + ar(128)])
        fm_cols = np.concatenate([oxbc + xcols, okc + 64 * g + ar(64), ovc + 64 * g + ar(64)])
        convw = np.ascontiguousarray(conv_w[:, xcols].T.reshape(4, 128, 4).transpose(1, 0, 2).reshape(128, 16))
        convb = np.ascontiguousarray(conv_b[xcols].reshape(4, 128).T)
        hp = np.concatenate([inputs["dt_bias"][0][4 * g:4 * g + 4], inputs["a_log"][0][4 * g:4 * g + 4], inputs["d_skip"][0][4 * g:4 * g + 4]]).astype(np.float32)
        in_maps.append(dict(
            xb=np.ascontiguousarray(x[b]), anw=inputs["attn_norm_w"][0], w_tm=np.ascontiguousarray(w_in[:, tm_cols]), w_fm=np.ascontiguousarray(w_in[:, fm_cols]),
            convw=convw, convb=convb, hp=hp, rope=rope,
            w1k=np.ascontiguousarray(inputs["cmp_w1_k"][0].reshape(32, 64, 256).transpose(1, 0, 2).reshape(64, 32 * 256)),
            w1v=np.ascontiguousarray(inputs["cmp_w1_v"][0].reshape(32, 64, 256).transpose(1, 0, 2).reshape(64, 32 * 256)),
            w2k=inputs["cmp_w2_k"][0], w2v=inputs["cmp_w2_v"][0],
            pek=np.ascontiguousarray(inputs["cmp_pe_k"][0].T), pev=np.ascontiguousarray(inputs["cmp_pe_v"][0].T),
            ident=ident, emat=emat, tric=tric, tria=tria, cmask=cmask, ovl=ovl, utri=utri,
        ))
    res = run_bass_kernel_spmd(nc, in_maps, core_ids=list(range(8)))
    mixed = np.empty((2, SEQ, D), np.float32)
    for c in range(8):
        b, g = c // 4, c % 4
        m = res.results[c]["mixo"]
        mixed[b, :, 256 * g:256 * (g + 1)] = m[:, 0:256]
        mixed[b, :, 1024 + 256 * g:1024 + 256 * (g + 1)] = m[:, 256:512]
    return mixed


def kernel(**inputs):
    inputs = {k: np.asarray(v) for k, v in inputs.items()}
    mixed = run_mix(inputs)
    return run_tail(inputs["x"], mixed, inputs["ssd_norm_w"][0], inputs["w_out"][0], inputs["ffn_norm_w"][0],
                    inputs["w_gate"][0], inputs["w_up"][0], inputs["w_down"][0], inputs["final_norm_w"])
```

```python
import contextlib
import os
import numpy as np
import ml_dtypes
import concourse.bass as bass
import concourse.mybir as mybir
from concourse.bass_utils import run_bass_kernel_spmd

F32 = mybir.dt.float32
BF16 = mybir.dt.bfloat16
U8 = mybir.dt.uint8
ALU = mybir.AluOpType
AF = mybir.ActivationFunctionType
AX = mybir.AxisListType

ENGS = ("pe", "act", "dve", "pool", "sp")
EPS = 1e-6


class _Op:
    __slots__ = ("eng", "fn", "dma", "deps", "idx", "signal", "val", "sem", "cc", "cost")


class Sched:
    def __init__(self, nc, n_dma_sems=8):
        self.nc = nc
        self.ops = []
        self.last_w = {}
        self.readers = {}
        self.n_dma_sems = n_dma_sems
        self.bar = None
        self.bank_of = {}

    def op(self, eng, fn, reads=(), writes=(), dma=False, c=None):
        o = _Op()
        o.cost = c
        o.eng, o.fn, o.dma = eng, fn, dma
        o.cc = False
        o.idx = len(self.ops)
        o.signal = False
        deps = {}
        if self.bar is not None:
            deps[self.bar] = "raw"
        for r in reads:
            w = self.last_w.get(r)
            if w is not None:
                deps[w] = "raw"
        for w_ in writes:
            w = self.last_w.get(w_)
            if w is not None:
                deps[w] = "raw"
            for r in self.readers.get(w_, ()):
                if r not in deps:
                    deps[r] = "war"
        for r in reads:
            self.readers.setdefault(r, []).append(o.idx)
        for w_ in writes:
            self.last_w[w_] = o.idx
            self.readers[w_] = []
        banks = set()
        for r in tuple(reads) + tuple(writes):
            banks.update(self.bank_of.get(r, ()))
        for b in banks:
            key = ("bank", b)
            w = self.last_w.get(key)
            if w is not None and w not in deps:
                deps[w] = "bank"
            self.last_w[key] = o.idx
        deps.pop(o.idx, None)
        o.deps = deps
        self.ops.append(o)
        return o

    def pe(self, fn, reads=(), writes=(), c=None):
        return self.op("pe", fn, reads, writes, c=c)

    def act(self, fn, reads=(), writes=(), c=None):
        return self.op("act", fn, reads, writes, c=c)

    def dve(self, fn, reads=(), writes=(), c=None):
        return self.op("dve", fn, reads, writes, c=c)

    def pool(self, fn, reads=(), writes=(), c=None):
        return self.op("pool", fn, reads, writes, c=c)

    DEF_COST = {"pe": 0.12, "act": 0.35, "dve": 0.25, "pool": 0.35}

    def reorder(self, window=40):
        ops = self.ops
        n = len(ops)
        queues = {e: [] for e in ENGS}
        for o in ops:
            queues[o.eng].append(o.idx)
        head = {e: 0 for e in ENGS}
        sched = [False] * n
        fin = [0.0] * n
        etime = {e: 0.0 for e in ENGS}
        order = []
        left = n
        while left:
            best = None
            for e in ENGS:
                q = queues[e]
                h = head[e]
                while h < len(q) and sched[q[h]]:
                    h += 1
                head[e] = h
                if h >= len(q):
                    continue
                seen = 0
                i = h
                et = etime[e]
                while i < len(q) and seen < window:
                    k = q[i]
                    i += 1
                    if sched[k]:
                        continue
                    seen += 1
                    o = ops[k]
                    rdy = 0.0
                    ok = True
                    for d in o.deps:
                        if not sched[d]:
                            ok = False
                            break
                        f = fin[d] + (0.0 if ops[d].eng == e else 0.15)
                        if f > rdy:
                            rdy = f
                    if not ok:
                        continue
                    st = rdy if rdy > et else et
                    key = (st, k)
                    if best is None or key < best[0]:
                        best = (key, e, k)
                    if st <= et:
                        break
            assert best is not None, "scheduler stuck"
            (st, k), e, _ = best
            o = ops[k]
            if o.dma:
                etime[e] = st + 0.06
                fin[k] = st + (o.cost if o.cost is not None else 3.0)
            else:
                c = o.cost if o.cost is not None else self.DEF_COST[e]
                etime[e] = st + c
                fin[k] = st + c
            sched[k] = True
            order.append(k)
            left -= 1
        self.order = order
        self.est_time = max(fin) if fin else 0.0

    def dma(self, q, out, in_, reads=(), writes=(), c=None):
        return self.op(q, lambda e: e.dma_start(out=out, in_=in_), reads, writes, dma=True, c=c)

    def cc(self, fn, reads=(), writes=()):
        o = self.op("pool", fn, reads, writes, dma=True)
        o.cc = True
        return o

    def barrier(self, out, in_):
        allres = set(self.last_w.keys()) | set(self.readers.keys())
        o = self.op("sp", lambda e: e.dma_start(out=out, in_=in_), reads=(), writes=tuple(allres), dma=True)
        self.bar = o.idx
        self.last_w = {}
        self.readers = {}
        return o

    def emit(self, sems, block, final_wait_eng="sp"):
        ops = self.ops
        need = [False] * len(ops)
        for o in ops:
            for d, kind in o.deps.items():
                do = ops[d]
                if do.dma:
                    continue
                if do.eng == o.eng and not o.dma:
                    if do.eng == "pe" or kind == "bank":
                        continue
                need[d] = True
        cnt = {e: 0 for e in ENGS}
        dcnt = {}
        dval = {}
        per_eng = {e: [] for e in ENGS}
        order = getattr(self, "order", None) or list(range(len(ops)))
        for k_ in order:
            o = ops[k_]
            per_eng[o.eng].append(o)
            if o.dma and o.cc:
                o.sem = "cc"
                dval["cc"] = dval.get("cc", 0) + 1
                o.val = dval["cc"]
            elif o.dma:
                k = dcnt.get(o.eng, 0)
                dcnt[o.eng] = k + 1
                key = ("dma", o.eng, k % self.n_dma_sems)
                o.sem = key
                dval[key] = dval.get(key, 0) + 16
                o.val = dval[key]
            elif need[o.idx]:
                cnt[o.eng] += 1
                o.val = cnt[o.eng]
                o.sem = o.eng
                o.signal = True
        self.stats = {e: len(per_eng[e]) for e in ENGS}
        self.stats["signals"] = dict(cnt)

        def run(engname, e):
            waited = {}
            for o in per_eng[engname]:
                wl = {}
                for d, kind in o.deps.items():
                    do = ops[d]
                    if do.dma:
                        wl[do.sem] = max(wl.get(do.sem, 0), do.val)
                        continue
                    if do.eng == o.eng and not o.dma:
                        if do.eng == "pe" or kind == "bank":
                            continue
                    wl[do.sem] = max(wl.get(do.sem, 0), do.val)
                if o.dma and o.cc and o.val > 1:
                    wl[o.sem] = max(wl.get(o.sem, 0), o.val - 1)
                elif o.dma and not o.cc and o.val > 16:
                    wl[o.sem] = max(wl.get(o.sem, 0), o.val - 16)
                for s, v in wl.items():
                    if waited.get(s, 0) >= v:
                        continue
                    waited[s] = v
                    e.wait_ge(sems[s], v)
                ins = o.fn(e)
                if o.dma and o.cc:
                    ins.then_inc(sems[o.sem])
                elif o.dma:
                    ins.then_inc(sems[o.sem], 16)
                elif o.signal:
                    ins.then_inc(sems[o.sem], 1)
            if engname == final_wait_eng:
                for key, v in dval.items():
                    if waited.get(key, 0) < v:
                        e.wait_ge(sems[key], v)
                for en in ("pe", "act", "dve", "pool"):
                    if cnt[en] > 0 and waited.get(en, 0) < cnt[en]:
                        e.wait_ge(sems[en], cnt[en])

        @block.tensor
        def _(e):
            run("pe", e)

        @block.scalar
        def _(e):
            run("act", e)

        @block.vector
        def _(e):
            run("dve", e)

        @block.gpsimd
        def _(e):
            run("pool", e)

        @block.sync
        def _(e):
            run("sp", e)


def make_sems(nc, stack, n_dma_sems=8, queues=("sp", "pool", "act")):
    sems = {}
    for e in ("pe", "act", "dve", "pool", "cc"):
        sems[e] = stack.enter_context(nc.semaphore("s_" + e))
    for q in queues:
        for i in range(n_dma_sems):
            sems[("dma", q, i)] = stack.enter_context(nc.semaphore("d_%s_%d" % (q, i)))
    return sems


_DTSZ = {F32: 4, BF16: 2, U8: 1}


class Arena:
    def __init__(self, ar, size):
        self.ar, self.size, self.off = ar, size, 0

    def reset(self, off=0):
        self.off = off

    def alloc(self, shape, dtype, parts=128):
        n = int(np.prod(shape)) * _DTSZ[dtype]
        off = (self.off + 63) // 64 * 64
        assert off + n <= self.size, ("arena overflow", off, n, self.size)
        self.off = off + n
        ap = self.ar[0:parts, off:off + n].bitcast(dtype)
        if len(shape) > 1:
            names = [chr(ord("a") + i) for i in range(len(shape))]
            pat = "p (%s) -> p %s" % (" ".join(names), " ".join(names))
            ap = ap.rearrange(pat, **{nm: int(s) for nm, s in zip(names, shape)})
        return ap


D = 2048
FF = 5632
TOK = 1024
NT = TOK // 128
KC = D // 128
FC = FF // 128
SSDW = 1024


def build_tail():
    nc = bass.Bass("TRN2", target_bir_lowering=False)
    x = nc.dram_tensor("x_own", [TOK, D], F32, kind="ExternalInput").ap()
    mix = nc.dram_tensor("mix", [TOK, D], F32, kind="ExternalInput").ap()
    w_out = nc.dram_tensor("w_out", [D, D], F32, kind="ExternalInput").ap()
    w_gate = nc.dram_tensor("w_gate", [D, FF], F32, kind="ExternalInput").ap()
    w_up = nc.dram_tensor("w_up", [D, FF], F32, kind="ExternalInput").ap()
    w_down = nc.dram_tensor("w_down", [FF, D], F32, kind="ExternalInput").ap()
    nw = nc.dram_tensor("nw", [3, D], F32, kind="ExternalInput").ap()
    ident = nc.dram_tensor("ident", [128, 128], F32, kind="ExternalInput").ap()
    out = nc.dram_tensor("out", [TOK, D], F32, kind="ExternalOutput").ap()
    h_d = nc.dram_tensor("h_d", [TOK, D], F32, kind="Internal").ap()
    dummy = nc.dram_tensor("dummy_bar", [2, 64], F32, kind="Internal").ap()
    with contextlib.ExitStack() as st:
        ASZ = 207 * 1024
        ar = st.enter_context(nc.sbuf_tensor("arena", [128, ASZ], U8))
        A = Arena(ar, ASZ)
        pbig = st.enter_context(nc.psum_tensor("pbig", [128, 8, 512], F32))
        sems = make_sems(nc, st)
        block = st.enter_context(nc.Block())
        S = Sched(nc)
        tail_body(nc, S, A, pbig, x, mix, w_out, w_gate, w_up, w_down, nw, ident, out, h_d, dummy)
        if os.environ.get('NO_REORDER') is None:
            S.reorder()
        S.emit(sems, block)
    return nc


def rms_rstd(S, src, n, ss, sq, tag, rd, wr_extra=(), c=None):
    S.act(lambda e: e.activation(sq, src, AF.Square, accum_out=ss), reads=rd, writes=[tag + "ss", tag + "sq"], c=c)
    S.act(lambda e: e.activation(ss, ss, AF.Sqrt, scale=1.0 / n, bias=EPS_AP[0]), reads=[tag + "ss"], writes=[tag + "ss"])
    S.dve(lambda e: e.reciprocal(ss, ss), reads=[tag + "ss"], writes=[tag + "ss"])


EPS_AP = [None]


def tail_body(nc, S, A, pbig, x, mix, w_out, w_gate, w_up, w_down, nw, ident, out, h_d, dummy):
    bk = {"ptr": (4, 5)}
    for i_ in range(8):
        bk["pacc%d" % i_] = (i_,)
        bk["pd%d" % i_] = (i_,)
    for i_ in range(2):
        bk["pg%d" % i_] = (i_ * 4, i_ * 4 + 1)
        bk["pu%d" % i_] = (i_ * 4 + 2, i_ * 4 + 3)
    S.bank_of = bk
    identb = A.alloc([128], BF16)
    nwb_flat = A.alloc([2 * D + SSDW], F32)
    epsb = A.alloc([1], F32)
    ss = A.alloc([4], F32)
    EPS_AP[0] = epsb
    S.pool(lambda e: e.memset(epsb, EPS), writes=["eps"])
    S.dma("pool", identb, ident, writes=["ident"])
    S.dma("sp", nwb_flat, nw.rearrange("a b -> (a b)")[0:2 * D + SSDW].partition_broadcast(128), writes=["nwb"])

    class _NW:
        def __getitem__(self, key):
            _, row, cols = key
            base = {1: 0, 2: D, 0: 2 * D}[row]
            lo = cols.start or 0
            hi = cols.stop if cols.stop is not None else (SSDW if row == 0 else D)
            return nwb_flat[:, base + lo:base + hi]
    nwb = _NW()
    vT = A.alloc([KC, TOK], BF16)
    base_persist = A.off

    wo = A.alloc([KC, D], BF16)
    for cb in range(4):
        S.dma("pool", wo[:, :, cb * 512:(cb + 1) * 512],
              w_out[:, cb * 512:(cb + 1) * 512].rearrange("(k p) n -> p k n", p=128), writes=["wo%d" % cb], c=25.0)
    xt = [A.alloc([D], F32) for _ in range(2)]
    mt = [A.alloc([D], F32) for _ in range(2)]
    sq = A.alloc([D], BF16)
    mb = A.alloc([D], BF16)
    mT = A.alloc([KC, 128], BF16)
    hs = [A.alloc([D], F32) for _ in range(2)]
    vb = A.alloc([D], BF16)
    WB = 256
    A.alloc([1024], F32)
    pre_lo = (A.off + 63) // 64 * 64
    wg_pre = A.alloc([KC, WB], BF16)
    wu_pre = A.alloc([KC, WB], BF16)
    S.dma("pool", wg_pre, w_gate[:, 0:WB].rearrange("(k p) n -> p k n", p=128), writes=["wg0"], c=15.0)
    S.dma("pool", wu_pre, w_up[:, 0:WB].rearrange("(k p) n -> p k n", p=128), writes=["wu0"], c=15.0)
    pacc = pbig[:, 0:4, :]
    ptr_all = pbig[:, 4:6, :].rearrange("p a b -> p (a b)").bitcast(BF16)
    ptr = ptr_all.rearrange("p (k n) -> p k n", k=KC)

    def loads(tt):
        b = tt % 2
        S.dma("sp", xt[b], x[tt * 128:(tt + 1) * 128, :], writes=["xt%d" % b])
        S.dma("sp", mt[b], mix[tt * 128:(tt + 1) * 128, :], writes=["mt%d" % b])

    loads(0)
    for tt in range(NT):
        b = tt % 2
        if tt + 1 < NT:
            loads(tt + 1)
        rms_rstd(S, mt[b][:, 0:SSDW], SSDW, ss[:, 0:1], sq[:, 0:SSDW], "a", ["mt%d" % b, "eps"])
        S.dve(lambda e, b=b: e.scalar_tensor_tensor(mb[:, 0:SSDW], mt[b][:, 0:SSDW], ss[:, 0:1], nwb[:, 0, 0:SSDW], ALU.mult, ALU.mult),
              reads=["mt%d" % b, "ass", "nwb"], writes=["mb0"])
        S.pool(lambda e, b=b: e.tensor_copy(mb[:, SSDW:D], mt[b][:, SSDW:D]), reads=["mt%d" % b], writes=["mb1"])
        for kc in range(KC):
            S.pe(lambda e, kc=kc: e.transpose(ptr[:, kc, :], mb[:, kc * 128:(kc + 1) * 128], identb),
                 reads=["mb0", "mb1", "ident"], writes=["ptr"])
        S.act(lambda e: e.copy(mT[:, 0:8, :], ptr[:, 0:8, :]), reads=["ptr"], writes=["mTa"])
        S.dve(lambda e: e.tensor_copy(mT[:, 8:16, :], ptr[:, 8:16, :]), reads=["ptr"], writes=["mTb"])
        for cb in range(4):
            for kc in range(KC):
                S.pe(lambda e, cb=cb, kc=kc: e.matmul(pacc[:, cb, :], mT[:, kc, :], wo[:, kc, cb * 512:(cb + 1) * 512],
                                                       start=(kc == 0), stop=(kc == KC - 1)),
                     reads=["mTa", "mTb", "wo%d" % cb], writes=["pacc%d" % cb], c=0.22)
            S.dve(lambda e, cb=cb, b=b: e.tensor_tensor(hs[b][:, cb * 512:(cb + 1) * 512], pacc[:, cb, :], xt[b][:, cb * 512:(cb + 1) * 512], ALU.add),
                  reads=["pacc%d" % cb, "xt%d" % b], writes=["hs%d_%d" % (b, cb)])
        hres = ["hs%d_%d" % (b, cb) for cb in range(4)]
        S.dma("sp", h_d[tt * 128:(tt + 1) * 128, :], hs[b], reads=hres, writes=["h_d%d" % tt])
        rms_rstd(S, hs[b], D, ss[:, 1:2], sq, "b", hres + ["eps"])
        S.dve(lambda e, b=b: e.scalar_tensor_tensor(vb, hs[b], ss[:, 1:2], nwb[:, 1, :], ALU.mult, ALU.mult),
              reads=hres + ["bss", "nwb"], writes=["vb"])
        for kc in range(KC):
            S.pe(lambda e, kc=kc: e.transpose(ptr[:, kc, :], vb[:, kc * 128:(kc + 1) * 128], identb),
                 reads=["vb", "ident"], writes=["ptr"])
        S.act(lambda e, tt=tt: e.copy(vT[:, 0:8, tt * 128:(tt + 1) * 128], ptr[:, 0:8, :]), reads=["ptr"], writes=["vT%da" % tt])
        S.dve(lambda e, tt=tt: e.tensor_copy(vT[:, 8:16, tt * 128:(tt + 1) * 128], ptr[:, 8:16, :]), reads=["ptr"], writes=["vT%db" % tt])

    S.barrier(dummy[1:2, :], ident[0:1, 0:64])
    A.reset(base_persist)
    hT = A.alloc([FC, TOK], BF16)
    HK = 22
    wd_pre = A.alloc([HK, 512], BF16)
    base_b = A.off
    NB = FF // WB
    wg = [wg_pre, A.alloc([KC, WB], BF16)]
    wu = [wu_pre, A.alloc([KC, WB], BF16)]
    sg = [A.alloc([TOK], BF16) for _ in range(2)]
    assert A.off <= pre_lo, (A.off, pre_lo)
    for blk in range(NB):
        b = blk % 2
        if blk > 0:
            S.dma("pool", wg[b], w_gate[:, blk * WB:(blk + 1) * WB].rearrange("(k p) n -> p k n", p=128), writes=["wg%d" % b], c=15.0)
            S.dma("pool", wu[b], w_up[:, blk * WB:(blk + 1) * WB].rearrange("(k p) n -> p k n", p=128), writes=["wu%d" % b], c=15.0)
        if blk == NB - 2:
            S.dma("pool", wd_pre, w_down[0:HK * 128, 0:512].rearrange("(k p) n -> p k n", p=128), writes=["wd0"], c=15.0)
        for j in range(WB // 128):
            fc = blk * (WB // 128) + j
            pb = fc % 2
            pg = pbig[:, pb * 4:pb * 4 + 2, :]
            pu = pbig[:, pb * 4 + 2:pb * 4 + 4, :]
            for hf in range(2):
                for kc in range(KC):
                    S.pe(lambda e, b=b, j=j, hf=hf, kc=kc, pg=pg: e.matmul(pg[:, hf, :], wg[b][:, kc, j * 128:(j + 1) * 128], vT[:, kc, hf * 512:(hf + 1) * 512],
                                                                         start=(kc == 0), stop=(kc == KC - 1)),
                         reads=["wg%d" % b, "vT"], writes=["pg%d" % pb], c=0.22)
            for hf in range(2):
                for kc in range(KC):
                    S.pe(lambda e, b=b, j=j, hf=hf, kc=kc, pu=pu: e.matmul(pu[:, hf, :], wu[b][:, kc, j * 128:(j + 1) * 128], vT[:, kc, hf * 512:(hf + 1) * 512],
                                                                         start=(kc == 0), stop=(kc == KC - 1)),
                         reads=["wu%d" % b, "vT"], writes=["pu%d" % pb], c=0.22)
            S.act(lambda e, pb=pb, pg=pg: e.activation(sg[pb], pg.rearrange("p a b -> p (a b)"), AF.Silu), reads=["pg%d" % pb], writes=["sg%d" % pb])
            S.dve(lambda e, pb=pb, pu=pu, fc=fc: e.tensor_tensor(hT[:, fc, :], sg[pb], pu.rearrange("p a b -> p (a b)"), ALU.mult),
                  reads=["sg%d" % pb, "pu%d" % pb], writes=["hT%d" % fc])

    S.barrier(dummy[1:2, :], ident[0:1, 0:64])
    A.reset(base_b)
    wd = [wd_pre, A.alloc([HK, 512], BF16)]
    hl = [A.alloc([512], F32) for _ in range(2)]
    ys = [A.alloc([512], F32) for _ in range(2)]
    it = 0
    for r in range(4):
        for hf in range(2):
            b = (r * 2 + hf) % 2
            if r * 2 + hf > 0:
                S.dma("pool", wd[b], w_down[hf * HK * 128:(hf + 1) * HK * 128, r * 512:(r + 1) * 512].rearrange("(k p) n -> p k n", p=128), writes=["wd%d" % b], c=15.0)
            for tt in range(NT):
                for k in range(HK):
                    kk = hf * HK + k
                    S.pe(lambda e, b=b, tt=tt, k=k, kk=kk: e.matmul(pbig[:, tt, :], hT[:, kk, tt * 128:(tt + 1) * 128], wd[b][:, k, :],
                                                                    start=(kk == 0), stop=(kk == FC - 1)),
                         reads=["wd%d" % b, "hT"], writes=["pd%d" % tt], c=0.22)
        for tt in range(NT):
            b = it % 2
            it += 1
            S.dma("sp", hl[b], h_d[tt * 128:(tt + 1) * 128, r * 512:(r + 1) * 512], reads=["h_d%d_%d" % (tt, r)], writes=["hl%d" % b])
            S.dve(lambda e, b=b, tt=tt: e.tensor_tensor(ys[b], pbig[:, tt, :], hl[b], ALU.add), reads=["pd%d" % tt, "hl%d" % b], writes=["ys%d" % b])
            S.dma("sp", h_d[tt * 128:(tt + 1) * 128, r * 512:(r + 1) * 512], ys[b], reads=["ys%d" % b], writes=["h_d%d_%d" % (tt, r)])

    S.barrier(dummy[1:2, :], ident[0:1, 0:64])
    A.reset(base_persist)
    yt = [A.alloc([D], F32) for _ in range(2)]
    ot = [A.alloc([D], F32) for _ in range(2)]
    sq2 = A.alloc([D], F32)
    for tt in range(NT):
        b = tt % 2
        S.dma("sp", yt[b], h_d[tt * 128:(tt + 1) * 128, :], writes=["yt%d" % b])
        rms_rstd(S, yt[b], D, ss[:, 2:3], sq2, "c", ["yt%d" % b, "eps"])
        S.dve(lambda e, b=b: e.scalar_tensor_tensor(ot[b], yt[b], ss[:, 2:3], nwb[:, 2, :], ALU.mult, ALU.mult),
              reads=["yt%d" % b, "css", "nwb"], writes=["ot%d" % b])
        S.dma("sp", out[tt * 128:(tt + 1) * 128, :], ot[b], reads=["ot%d" % b])


_CACHE = {}


def run_tail(x, mixed, ssd_norm_w, w_out, ffn_norm_w, w_gate, w_up, w_down, final_norm_w):
    if "tail" not in _CACHE:
        _CACHE["tail"] = build_tail()
    nc = _CACHE["tail"]
    nwv = np.ones((3, D), np.float32)
    nwv[0] = ffn_norm_w
    nwv[1] = final_norm_w
    nwv[2, :SSDW] = ssd_norm_w
    ident = np.eye(128, dtype=np.float32)
    in_maps = []
    for c in range(8):
        b, g = c // 4, c % 4
        in_maps.append({
            "x_own": np.ascontiguousarray(x[b, g * TOK:(g + 1) * TOK]),
            "mix": np.ascontiguousarray(mixed[b, g * TOK:(g + 1) * TOK]),
            "w_out": w_out, "w_gate": w_gate, "w_up": w_up, "w_down": w_down,
            "nw": nwv, "ident": ident,
        })
    res = run_bass_kernel_spmd(nc, in_maps, core_ids=list(range(8)))
    outp = np.empty((2, 4096, D), np.float32)
    for c in range(8):
        b, g = c // 4, c % 4
        outp[b, g * TOK:(g + 1) * TOK] = res.results[c]["out"]
    return outp


SEQ = 4096
NTT = SEQ // 128
NEG = -30000.0
NA = 400
NB_ = 384
NFM = 640
SCALE = 0.125


def build_mix():
    nc = bass.Bass("TRN2", target_bir_lowering=False)
    di = lambda n, s, d=F32: nc.dram_tensor(n, s, d, kind="ExternalInput").ap()
    T = dict(
        xb=di("xb", [SEQ, D]), anw=di("anw", [D]), w_tm=di("w_tm", [D, NA + NB_]), w_fm=di("w_fm", [D, NFM]),
        convw=di("convw", [128, 16]), convb=di("convb", [128, 4]), hp=di("hp", [12]), rope=di("rope", [128, NTT * 16]),
        w1k=di("w1k", [64, 32 * 256]), w1v=di("w1v", [64, 32 * 256]), w2k=di("w2k", [256, 64]), w2v=di("w2v", [256, 64]),
        pek=di("pek", [64, 32]), pev=di("pev", [64, 32]), ident=di("ident", [128, 128]), emat=di("emat", [64, SEQ]),
        tric=di("tric", [128, 128]), tria=di("tria", [128, 128]), cmask=di("cmask", [256, SEQ]), ovl=di("ovl", [256, 64]),
        utri=di("utri", [128, 128]),
    )
    T["mixo"] = nc.dram_tensor("mixo", [SEQ, 512], F32, kind="ExternalOutput").ap()
    T["qr_d"] = nc.dram_tensor("qr_d", [64, 4, SEQ], BF16, kind="Internal").ap()
    T["qu_d"] = nc.dram_tensor("qu_d", [64, 4, SEQ], BF16, kind="Internal").ap()
    T["dummy"] = nc.dram_tensor("dummy_bar", [2, 64], F32, kind="Internal").ap()
    with contextlib.ExitStack() as st:
        ASZ = 207 * 1024
        ar = st.enter_context(nc.sbuf_tensor("arena", [128, ASZ], U8))
        A = Arena(ar, ASZ)
        pbig = st.enter_context(nc.psum_tensor("pbig", [128, 8, 512], F32))
        sems = make_sems(nc, st)
        block = st.enter_context(nc.Block())
        S = Sched(nc)
        mix_body(nc, S, A, pbig, T)
        if os.environ.get('NO_REORDER') is None:
            S.reorder()
        S.emit(sems, block)
    return nc


def pbf(pb, lo, hi):
    return pb[:, lo:hi, :].rearrange("p a b -> p (a b)").bitcast(BF16)


def mix_body(nc, S, A, pbig, T):
    bar = lambda: S.barrier(T["dummy"][1:2, :], T["ident"][0:1, 0:64])
    bk = {"ptr": (0,), "ptr2": (6,), "psA": (1,), "psB": (2,), "psF0": (3,), "psR": (4,), "psS_a": (5,), "psS_c": (5,),
          "psN": (6,), "psY_o": (7,), "psY_d": (7,), "pk": (5,), "pv0": (6,), "pv1": (7,), "pnt": (7,), "posel": (6,), "powin": (7,)}
    for i_ in range(4):
        bk["pbias%d" % i_] = (4,)
        bk["phid%d" % i_] = (i_,)
        bk["psw%d" % i_] = (3 + i_ // 2,)
    for a_ in range(2):
        bk["po%d" % a_] = (4 + a_,)
        for b_ in range(2):
            bk["psc%d_%d" % (a_, b_)] = (a_ * 2 + b_,)
    for i_ in range(3):
        bk["pss%d" % i_] = (i_,)
    S.bank_of = bk
    identb = A.alloc([128], BF16)
    utri = A.alloc([128], F32)
    tricb = A.alloc([128], BF16)
    triab = A.alloc([128], BF16)
    epsb = A.alloc([1], F32)
    oneb = A.alloc([1], F32)
    EPS_AP[0] = epsb
    KsA = A.alloc([SEQ], BF16)
    KwT = A.alloc([SEQ], BF16)
    kcvcT = A.alloc([SEQ], BF16)
    VsA = A.alloc([NTT, 65], BF16)
    VwA = A.alloc([NTT, 65], BF16)
    gate = A.alloc([NTT, 12], F32)
    hpb = A.alloc([12], F32)
    base_persist = A.off
    S.pool(lambda e: e.memset(epsb, EPS), writes=["eps"])
    S.pool(lambda e: e.memset(oneb, 1.0), writes=["one"])
    S.pool(lambda e: e.memset(VsA, 1.0), writes=["VsA"])
    S.pool(lambda e: e.memset(VwA, 1.0), writes=["VwA"])
    S.dma("pool", identb, T["ident"], writes=["ident"])
    S.dma("sp", utri, T["utri"], writes=["utri"])
    S.dma("pool", tricb, T["tric"], writes=["tric"])
    S.dma("pool", triab, T["tria"], writes=["tria"])
    S.dma("pool", KsA[64:128, :], T["emat"], writes=["KsE"])
    S.dma("sp", hpb, T["hp"].partition_broadcast(128), writes=["hpb"])

    wtm = A.alloc([KC, NA + NB_], BF16)
    wfm = A.alloc([KC, NFM], BF16)
    S.dma("pool", wfm, T["w_fm"].rearrange("(k p) n -> p k n", p=128), writes=["wfm"], c=30.0)
    S.dma("pool", wtm, T["w_tm"].rearrange("(k p) n -> p k n", p=128), writes=["wtm"], c=35.0)
    anwb = A.alloc([D], F32)
    S.dma("sp", anwb, T["anw"].partition_broadcast(128), writes=["anwb"])
    ropet = A.alloc([NTT, 16], F32)
    S.dma("sp", ropet, T["rope"].rearrange("p (t c) -> p t c", c=16), writes=["ropet"])
    convw = A.alloc([16], F32)
    convb = A.alloc([4], F32)
    S.dma("sp", convw, T["convw"], writes=["convw"])
    S.dma("sp", convb, T["convb"], writes=["convb"])
    onesf = A.alloc([128], F32)
    S.pool(lambda e: e.memset(onesf, 1.0), writes=["onesf"])
    two = lambda shape, dt: [A.alloc(shape, dt) for _ in range(2)]
    xt = two([D], F32)
    sq1 = A.alloc([D], BF16)
    sq = [sq1, sq1]
    ss = A.alloc([8], F32)
    ub = two([D], BF16)
    uT2 = two([KC, 512], BF16)
    cbuf = A.alloc([4, 515], F32)
    cacc_1 = A.alloc([512], F32)
    cacc = [cacc_1, cacc_1]
    xbcT = two([4, 512], BF16)
    zs = two([256], BF16)
    ez_1 = A.alloc([256], F32)
    ez = [ez_1, ez_1]
    ec = two([512], F32)
    dtt = two([4], F32)
    qk = two([6, 64], F32)
    qkr = two([6, 64], BF16)
    qkb = two([4, 64], BF16)
    rt = two([4, 6, 8], F32)
    qst = two([4, 128], BF16)
    qut = two([4, 128], BF16)
    xtm = two([256], BF16)
    btm = two([128], BF16)
    hst = A.alloc([256], F32)
    hstb = two([256], BF16)
    aneg = A.alloc([4], F32)
    adt = two([4], F32)
    acol = two([4], F32)
    nacol = two([4], F32)
    eac = two([4], F32)
    rhs4_1 = A.alloc([4, 128], F32)
    rhs4 = [rhs4_1, rhs4_1]
    seg4_1 = A.alloc([4, 128], F32)
    seg4 = [seg4_1, seg4_1]
    cbm = two([128], F32)
    MT = two([4, 128], BF16)
    alast = two([4], F32)
    dsv = two([4], F32)
    cdv = two([4], F32)
    wsc = two([4], F32)
    xw = two([256], BF16)
    xdt = two([256], BF16)
    ydg_1 = A.alloc([256], F32)
    ydg = [ydg_1, ydg_1]
    yy_1 = A.alloc([256], F32)
    yy = [yy_1, yy_1]
    tmpd_1 = A.alloc([256], F32)
    tmpd = [tmpd_1, tmpd_1]
    yo = two([256], F32)
    S.pool(lambda e: e.memset(cbuf, 0.0), writes=["cbuf", "cbuf0", "cbuf1", "cbuf2", "cbuf3"])
    S.pool(lambda e: e.memset(hst, 0.0), writes=["hst"])
    S.act(lambda e: e.activation(aneg, hpb[:, 4:8], AF.Exp), reads=["hpb"], writes=["aneg"])
    S.dve(lambda e: e.tensor_scalar(aneg, aneg, -1.0, None, ALU.mult), reads=["aneg"], writes=["aneg"])

    ptr = pbf(pbig, 0, 1)
    psA = pbig[:, 1, 0:NA]
    psB = pbig[:, 2, 0:NB_]
    psF = pbig[:, 3, :]
    psR = pbig[:, 4, :]
    psS = pbig[:, 5, :]
    psN = pbig[:, 6, 0:256]
    psY = pbig[:, 7, :]

    def load_x(tt):
        S.dma("sp", xt[tt % 2], T["xb"][tt * 128:(tt + 1) * 128, :], writes=["xt%d" % (tt % 2)], c=5.0)

    def b4(ap, n):
        return ap.unsqueeze(2).to_broadcast([128, 4, n])

    load_x(0)
    for G in range(SEQ // 512):
        gp = G % 2
        uT = uT2[gp]
        for j in range(4):
            tt = G * 4 + j
            b = tt % 2
            if tt + 1 < NTT:
                load_x(tt + 1)
            ssb = ss[:, b:b + 1]
            S.act(lambda e, b=b, ssb=ssb: e.activation(sq[b], xt[b], AF.Square, accum_out=ssb), reads=["xt%d" % b], writes=["n%dss" % b, "nsq"], c=1.9)
            S.act(lambda e, ssb=ssb: e.activation(ssb, ssb, AF.Ln, scale=1.0 / D, bias=epsb), reads=["n%dss" % b, "eps"], writes=["n%dss" % b])
            S.act(lambda e, ssb=ssb: e.activation(ssb, ssb, AF.Exp, scale=-0.5), reads=["n%dss" % b], writes=["n%dss" % b])
            S.dve(lambda e, b=b: e.scalar_tensor_tensor(ub[b], xt[b], ss[:, b:b + 1], anwb, ALU.mult, ALU.mult),
                  reads=["xt%d" % b, "n%dss" % b, "anwb"], writes=["ub%d" % b], c=2.2)
            for half in range(2):
                for k8 in range(8):
                    kc = half * 8 + k8
                    S.pe(lambda e, kc=kc, k8=k8, b=b: e.transpose(ptr[:, k8 * 128:(k8 + 1) * 128], ub[b][:, kc * 128:(kc + 1) * 128], identb),
                         reads=["ub%d" % b, "ident"], writes=["ptr"])
                if half == 0:
                    S.act(lambda e, j=j, uT=uT: e.copy(uT[:, 0:8, j * 128:(j + 1) * 128], ptr.rearrange("p (k n) -> p k n", k=8)), reads=["ptr"], writes=["uT%d_%d_0" % (gp, j)], c=0.9)
                else:
                    S.dve(lambda e, j=j, uT=uT: e.tensor_copy(uT[:, 8:16, j * 128:(j + 1) * 128], ptr.rearrange("p (k n) -> p k n", k=8)), reads=["ptr"], writes=["uT%d_%d_1" % (gp, j)], c=0.7)
        uTr = ["uT%d_%d_%d" % (gp, j, h) for j in range(4) for h in range(2)]
        for c in range(5):
            for kc in range(KC):
                S.pe(lambda e, c=c, kc=kc, uT=uT: e.matmul(psF, wfm[:, kc, c * 128:(c + 1) * 128], uT[:, kc, :], start=(kc == 0), stop=(kc == KC - 1)),
                     reads=uTr + ["wfm"], writes=["psF0"], c=0.22)
            if c < 4:
                ca = cacc[c % 2]
                car = "cacc"
                S.act(lambda e, c=c: e.copy(cbuf[:, c, 3:515], psF), reads=["psF0"], writes=["cbuf%d" % c], c=0.6)
                S.dve(lambda e, c=c, ca=ca: e.tensor_scalar(ca, cbuf[:, c, 0:512], convw[:, c * 4:c * 4 + 1], convb[:, c:c + 1], ALU.mult, ALU.add), reads=["cbuf%d" % c, "convw", "convb"], writes=[car], c=0.4)
                for k in range(1, 4):
                    S.dve(lambda e, c=c, k=k, ca=ca: e.scalar_tensor_tensor(ca, cbuf[:, c, k:k + 512], convw[:, c * 4 + k:c * 4 + k + 1], ca, ALU.mult, ALU.add),
                          reads=["cbuf%d" % c, car, "convw"], writes=[car], c=0.65)
                ece = ec[c % 2]
                ecr = "ec%d" % (c % 2)
                S.act(lambda e, ca=ca, ece=ece: e.activation(ece, ca, AF.Exp, scale=-1.0), reads=[car], writes=[ecr], c=0.6)
                S.act(lambda e, ece=ece: e.activation(ece, ece, AF.Ln, bias=oneb), reads=[ecr, "one"], writes=[ecr], c=0.6)
                S.act(lambda e, ece=ece: e.activation(ece, ece, AF.Exp, scale=-1.0), reads=[ecr], writes=[ecr], c=0.6)
                S.pool(lambda e, c=c, ca=ca, ece=ece, gp=gp: e.tensor_tensor(xbcT[gp][:, c, :], ca, ece, ALU.mult), reads=[car, ecr], writes=["xbcT%d_%d" % (gp, c)], c=2.0)
                S.pool(lambda e, c=c: e.tensor_copy(cbuf[:, c, 0:3], cbuf[:, c, 512:515]), reads=["cbuf%d" % c], writes=["cbuf%d" % c])
            else:
                S.act(lambda e, G=G: e.copy(kcvcT[:, G * 512:(G + 1) * 512], psF), reads=["psF0"], writes=["kcvcT"], c=0.6)
        xr = lambda c: "xbcT%d_%d" % (gp, c)
        for j in range(4):
            tt = G * 4 + j
            p = tt % 2
            P = str(p)
            tok = slice(tt * 128, (tt + 1) * 128)
            js = slice(j * 128, (j + 1) * 128)
            for kc in range(KC):
                S.pe(lambda e, js=js, kc=kc, uT=uT: e.matmul(psA, uT[:, kc, js], wtm[:, kc, 0:NA], start=(kc == 0), stop=(kc == KC - 1)),
                     reads=uTr + ["wtm"], writes=["psA"], c=0.19)
            for kc in range(KC):
                S.pe(lambda e, js=js, kc=kc, uT=uT: e.matmul(psB, uT[:, kc, js], wtm[:, kc, NA:NA + NB_], start=(kc == 0), stop=(kc == KC - 1)),
                     reads=uTr + ["wtm"], writes=["psB"], c=0.18)
            S.act(lambda e, p=p: e.activation(ez[p], psA[:, 0:256], AF.Exp, scale=-1.0), reads=["psA"], writes=["ez"])
            S.act(lambda e, p=p: e.activation(ez[p], ez[p], AF.Ln, bias=oneb), reads=["ez", "one"], writes=["ez"])
            S.act(lambda e, p=p: e.activation(ez[p], ez[p], AF.Exp, scale=-1.0), reads=["ez"], writes=["ez"])
            S.dve(lambda e, p=p: e.tensor_tensor(zs[p], psA[:, 0:256], ez[p], ALU.mult), reads=["psA", "ez"], writes=["zs" + P])
            S.dve(lambda e, p=p: e.tensor_tensor(dtt[p], psA[:, 256:260], hpb[:, 0:4], ALU.add), reads=["psA", "hpb"], writes=["dtt" + P])
            S.act(lambda e, p=p: e.activation(dtt[p], dtt[p], AF.Exp), reads=["dtt" + P], writes=["dtt" + P])
            S.act(lambda e, p=p: e.activation(dtt[p], dtt[p], AF.Ln, bias=oneb), reads=["dtt" + P, "one"], writes=["dtt" + P])
            S.act(lambda e, tt=tt: e.activation(gate[:, tt, :], psA[:, 260:272], AF.Exp, scale=-1.0), reads=["psA"], writes=["gate%d" % tt])
            S.dve(lambda e, tt=tt: e.tensor_scalar(gate[:, tt, :], gate[:, tt, :], 1.0, None, ALU.add), reads=["gate%d" % tt], writes=["gate%d" % tt])
            S.dve(lambda e, tt=tt: e.reciprocal(gate[:, tt, :], gate[:, tt, :]), reads=["gate%d" % tt], writes=["gate%d" % tt])
            S.dve(lambda e, tt=tt: e.tensor_copy(VsA[:, tt, 0:64], psA[:, 272:336]), reads=["psA", "VsA"], writes=["VsA%d" % tt])
            S.dve(lambda e, tt=tt: e.tensor_copy(VwA[:, tt, 0:64], psA[:, 336:400]), reads=["psA", "VwA"], writes=["VwA%d" % tt])
            S.act(lambda e, p=p: e.copy(qk[p], psB.rearrange("p (a b) -> p a b", a=6)), reads=["psB"], writes=["qk" + P], c=0.5)
            S.pool(lambda e, p=p: e.tensor_copy(qkb[p], qk[p][:, 0:4, :]), reads=["qk" + P], writes=["qkb" + P])
            S.pool(lambda e, p=p: e.tensor_copy(qkr[p], qk[p]), reads=["qk" + P], writes=["qkr" + P])
            cosb = ropet[:, tt, 0:8].unsqueeze(1).to_broadcast([128, 6, 8])
            sinb = ropet[:, tt, 8:16].unsqueeze(1).to_broadcast([128, 6, 8])
            S.dve(lambda e, cosb=cosb, p=p: e.tensor_tensor(rt[p][:, 0], qk[p][:, :, 0:8], cosb, ALU.mult), reads=["qk" + P, "ropet"], writes=["rt0" + P])
            S.dve(lambda e, sinb=sinb, p=p: e.tensor_tensor(rt[p][:, 1], qk[p][:, :, 8:16], sinb, ALU.mult), reads=["qk" + P, "ropet"], writes=["rt1" + P])
            S.dve(lambda e, cosb=cosb, p=p: e.tensor_tensor(rt[p][:, 2], qk[p][:, :, 8:16], cosb, ALU.mult), reads=["qk" + P, "ropet"], writes=["rt2" + P])
            S.dve(lambda e, sinb=sinb, p=p: e.tensor_tensor(rt[p][:, 3], qk[p][:, :, 0:8], sinb, ALU.mult), reads=["qk" + P, "ropet"], writes=["rt3" + P])
            S.dve(lambda e, p=p: e.tensor_tensor(qkr[p][:, :, 0:8], rt[p][:, 0], rt[p][:, 1], ALU.subtract), reads=["rt0" + P, "rt1" + P, "qkr" + P], writes=["qkr" + P])
            S.dve(lambda e, p=p: e.tensor_tensor(qkr[p][:, :, 8:16], rt[p][:, 2], rt[p][:, 3], ALU.add), reads=["rt2" + P, "rt3" + P, "qkr" + P], writes=["qkr" + P])
            ptq = ptr[0:64, 0:768].rearrange("p (a b) -> p a b", a=6)
            ptu = pbf(pbig, 6, 7)[0:64, 512:1024].rearrange("p (a b) -> p a b", a=4)
            for a in range(6):
                S.pe(lambda e, a=a, p=p: e.transpose(ptq[:, a, :], qkr[p][:, a, :], identb), reads=["qkr" + P, "ident"], writes=["ptr"])
            for a in range(4):
                S.pe(lambda e, a=a, p=p: e.transpose(ptu[:, a, :], qkb[p][:, a, :], identb), reads=["qkb" + P, "ident"], writes=["ptr2"])
            S.act(lambda e, p=p: e.copy(qst[p][0:64], ptq[:, 0:4, :]), reads=["ptr"], writes=["qst" + P])
            S.dve(lambda e, p=p: e.tensor_copy(qut[p][0:64], ptu), reads=["ptr2"], writes=["qut" + P])
            S.act(lambda e, tok=tok: e.copy(KsA[0:64, tok], ptq[:, 4, :]), reads=["ptr"], writes=["KsA%d" % tt])
            S.dve(lambda e, tok=tok: e.tensor_copy(KwT[0:64, tok], ptq[:, 5, :]), reads=["ptr"], writes=["KwT%d" % tt])
            S.dma("sp", T["qr_d"][:, :, tok], qst[p][0:64], reads=["qst" + P], writes=["qr_d%d" % tt])
            S.dma("sp", T["qu_d"][:, :, tok], qut[p][0:64], reads=["qut" + P], writes=["qu_d%d" % tt])
            ptx = ptr[:, 0:384]
            for c in range(3):
                S.pe(lambda e, c=c, js=js, gp=gp: e.transpose(ptx[:, c * 128:(c + 1) * 128], xbcT[gp][:, c, js], identb),
                     reads=[xr(c), "ident"], writes=["ptr"])
            S.act(lambda e, p=p: e.copy(xtm[p], ptx[:, 0:256]), reads=["ptr"], writes=["xtm" + P])
            S.act(lambda e, p=p: e.copy(btm[p], ptx[:, 256:384]), reads=["ptr"], writes=["btm" + P])
            S.dve(lambda e, p=p: e.tensor_tensor(adt[p], dtt[p], aneg, ALU.mult), reads=["dtt" + P, "aneg"], writes=["adt" + P])
            S.pe(lambda e, p=p: e.matmul(psS[:, 0:4], utri, adt[p], start=True, stop=True), reads=["utri", "adt" + P], writes=["psS_a"])
            S.act(lambda e, p=p: e.copy(acol[p], psS[:, 0:4]), reads=["psS_a"], writes=["acol" + P])
            S.dve(lambda e, p=p: e.tensor_scalar(nacol[p], psS[:, 0:4], -1.0, None, ALU.mult), reads=["psS_a"], writes=["nacol" + P])
            S.act(lambda e, p=p: e.activation(eac[p], acol[p], AF.Exp), reads=["acol" + P], writes=["eac" + P])
            S.pe(lambda e, js=js, gp=gp: e.matmul(psS[:, 256:384], xbcT[gp][:, 2, js], xbcT[gp][:, 3, js], start=True, stop=True),
                 reads=[xr(2), xr(3)], writes=["psS_c"])
            S.dve(lambda e, p=p: e.tensor_tensor(cbm[p], psS[:, 256:384], utri, ALU.mult), reads=["psS_c", "utri"], writes=["cbm" + P])
            S.dve(lambda e, p=p: e.tensor_tensor(rhs4[p], utri.unsqueeze(1).to_broadcast([128, 4, 128]), b4(adt[p], 128), ALU.mult),
                  reads=["utri", "adt" + P], writes=["rhs4"], c=0.6)
            S.pe(lambda e, p=p: e.matmul(psR, onesf, rhs4[p].rearrange("p a b -> p (a b)"), start=True, stop=True), reads=["onesf", "rhs4"], writes=["psR"], c=0.9)
            psR4 = psR.rearrange("p (a b) -> p a b", a=4)
            S.dve(lambda e, p=p: e.tensor_tensor(seg4[p], psR4, b4(acol[p], 128), ALU.subtract), reads=["psR", "acol" + P], writes=["seg4"], c=0.7)
            S.dve(lambda e, p=p: e.tensor_scalar(seg4[p], seg4[p], 0.0, None, ALU.min), reads=["seg4"], writes=["seg4"], c=0.35)
            S.act(lambda e, p=p: e.activation(seg4[p], seg4[p], AF.Exp), reads=["seg4"], writes=["seg4"], c=0.6)
            S.dve(lambda e, p=p: e.tensor_tensor(MT[p], seg4[p], cbm[p].unsqueeze(1).to_broadcast([128, 4, 128]), ALU.mult),
                  reads=["seg4", "cbm" + P], writes=["MT" + P], c=0.6)
            S.dve(lambda e, p=p: e.tensor_copy(alast[p], psR4[:, :, 127]), reads=["psR"], writes=["alast" + P])
            S.dve(lambda e, p=p: e.tensor_tensor(dsv[p], nacol[p], alast[p], ALU.add), reads=["nacol" + P, "alast" + P], writes=["dsv" + P])
            S.act(lambda e, p=p: e.activation(dsv[p], dsv[p], AF.Exp), reads=["dsv" + P], writes=["dsv" + P])
            S.act(lambda e, p=p: e.activation(cdv[p], alast[p], AF.Exp), reads=["alast" + P], writes=["cdv" + P])
            S.dve(lambda e, p=p: e.tensor_tensor(wsc[p], dtt[p], dsv[p], ALU.mult), reads=["dtt" + P, "dsv" + P], writes=["wsc" + P])
            v4 = lambda ap: ap.rearrange("p (h d) -> p h d", h=4)
            S.dve(lambda e, p=p: e.tensor_tensor(v4(xw[p]), v4(xtm[p]), b4(wsc[p], 64), ALU.mult), reads=["xtm" + P, "wsc" + P], writes=["xw" + P])
            S.dve(lambda e, p=p: e.tensor_tensor(v4(xdt[p]), v4(xtm[p]), b4(dtt[p], 64), ALU.mult), reads=["xtm" + P, "dtt" + P], writes=["xdt" + P])
            S.pool(lambda e, p=p: e.tensor_copy(hstb[p], hst), reads=["hst"], writes=["hstb" + P])
            S.pe(lambda e, js=js, gp=gp, p=p: e.matmul(psY[:, 0:256], xbcT[gp][:, 3, js], hstb[p], start=True, stop=True), reads=[xr(3), "hstb" + P], writes=["psY_o"])
            for h in range(4):
                S.pe(lambda e, h=h, p=p: e.matmul(psY[:, 256 + h * 64:256 + (h + 1) * 64], MT[p][:, h, :], xdt[p][:, h * 64:(h + 1) * 64], start=True, stop=True),
                     reads=["MT" + P, "xdt" + P], writes=["psY_d"])
            S.pe(lambda e, p=p: e.matmul(psN, btm[p], xw[p], start=True, stop=True), reads=["btm" + P, "xw" + P], writes=["psN"])
            S.dve(lambda e, p=p: e.tensor_tensor(v4(hst), v4(hst), b4(cdv[p], 64), ALU.mult), reads=["hst", "cdv" + P], writes=["hst"])
            S.dve(lambda e: e.tensor_tensor(hst, hst, psN, ALU.add), reads=["hst", "psN"], writes=["hst"])
            S.act(lambda e, p=p: e.copy(ydg[p], psY[:, 256:512]), reads=["psY_d"], writes=["ydg"])
            S.dve(lambda e, p=p: e.tensor_tensor(v4(yy[p]), v4(psY[:, 0:256]), b4(eac[p], 64), ALU.mult), reads=["psY_o", "eac" + P], writes=["yy"])
            S.pool(lambda e, p=p: e.tensor_tensor(v4(tmpd[p]), v4(xtm[p]), b4(hpb[:, 8:12], 64), ALU.mult), reads=["xtm" + P, "hpb"], writes=["tmpd"])
            S.dve(lambda e, p=p: e.tensor_tensor(yy[p], yy[p], ydg[p], ALU.add), reads=["yy", "ydg"], writes=["yy"])
            S.dve(lambda e, p=p: e.tensor_tensor(yy[p], yy[p], tmpd[p], ALU.add), reads=["yy", "tmpd"], writes=["yy"])
            S.dve(lambda e, p=p: e.tensor_tensor(yo[p], yy[p], zs[p], ALU.mult), reads=["yy", "zs" + P], writes=["yo" + P])
            S.dma("sp", T["mixo"][tok, 0:256], yo[p], reads=["yo" + P], writes=["mixo_s%d" % tt])

    if int(os.environ.get('MIX_STOP', '9')) <= 1:
        return
    bar()
    A.reset(base_persist)
    attO = A.alloc([NTT, 256], F32)
    nmT = A.alloc([SEQ], BF16)
    w1 = A.alloc([32, 256], BF16)
    S.dma("pool", w1[0:64], T["w1k"].rearrange("d (l h) -> d l h", l=32), writes=["w1k"])
    S.dma("pool", w1[64:128], T["w1v"].rearrange("d (l h) -> d l h", l=32), writes=["w1v"])
    pe_ = A.alloc([32], BF16)
    S.dma("pool", pe_[0:64], T["pek"], writes=["pek"])
    S.dma("pool", pe_[64:128], T["pev"], writes=["pev"])
    w2 = A.alloc([2, 2, 64], BF16)
    S.dma("pool", w2[:, 0], T["w2k"].rearrange("(c p) d -> p c d", p=128), writes=["w2k"])
    S.dma("pool", w2[:, 1], T["w2v"].rearrange("(c p) d -> p c d", p=128), writes=["w2v"])
    cbias = A.alloc([4], F32)
    hsb = A.alloc([4, 256], BF16)
    KcT = A.alloc([256], BF16)
    VcA = A.alloc([2, 129], BF16)
    S.pool(lambda e: e.memset(hsb, 0.0), writes=["hsb"])
    S.pool(lambda e: e.memset(VcA, 0.0), writes=["VcA"])
    S.pool(lambda e: e.memset(KcT, 0.0), writes=["KcT"])
    S.pool(lambda e: e.memset(VcA[:, :, 64:65], 1.0), reads=["VcA"], writes=["VcA"])
    S.dma("pool", VcA[:, :, 65:129], T["ovl"].rearrange("(c p) j -> p c j", p=128), reads=["VcA"], writes=["VcA"])
    for kv in range(2):
        rows = slice(kv * 64, (kv + 1) * 64)
        for hc in range(2):
            idx = kv * 2 + hc
            pb_ = pbig[:, idx, 0:255]
            pbias = pbig[:, 4, idx:idx + 1]
            for l in range(32):
                S.pe(lambda e, rows=rows, hc=hc, l=l, pbias=pbias: e.matmul(pbias, w1[rows, l, hc * 128:(hc + 1) * 128], pe_[rows, l:l + 1], start=(l == 0), stop=(l == 31)),
                     reads=["w1k", "w1v", "pek", "pev"], writes=["pbias%d" % idx])
            S.act(lambda e, idx=idx, pbias=pbias: e.copy(cbias[:, idx:idx + 1], pbias), reads=["pbias%d" % idx], writes=["cbias%d" % idx])
            for l in range(32):
                S.pe(lambda e, rows=rows, hc=hc, l=l, pb_=pb_: e.matmul(pb_, w1[rows, l, hc * 128:(hc + 1) * 128], kcvcT[rows, l:l + 16 * 254 + 1:16], start=(l == 0), stop=(l == 31)),
                     reads=["w1k", "w1v", "kcvcT"], writes=["phid%d" % idx])
            S.act(lambda e, idx=idx, pb_=pb_: e.activation(hsb[:, idx, 0:255], pb_, AF.Silu, bias=cbias[:, idx:idx + 1]), reads=["phid%d" % idx, "cbias%d" % idx, "hsb"], writes=["hsb%d" % idx])
    pk = pbig[0:64, 5, 0:255]
    for hc in range(2):
        S.pe(lambda e, hc=hc: e.matmul(pk, w2[:, 0, hc, :], hsb[:, hc, 0:255], start=(hc == 0), stop=(hc == 1)), reads=["w2k", "hsb0", "hsb1"], writes=["pk"])
    S.act(lambda e: e.copy(KcT[0:64, 0:255], pk), reads=["pk", "KcT"], writes=["KcT"])
    for it in range(2):
        m = 128 if it == 0 else 127
        pv = pbig[0:m, 6 + it, 0:64]
        for hc in range(2):
            S.pe(lambda e, it=it, hc=hc, m=m, pv=pv: e.matmul(pv, hsb[:, 2 + hc, it * 128:it * 128 + m], w2[:, 1, hc, :], start=(hc == 0), stop=(hc == 1)),
                 reads=["w2v", "hsb2", "hsb3"], writes=["pv%d" % it])
        S.act(lambda e, it=it, m=m, pv=pv: e.copy(VcA[0:m, it, 0:64], pv), reads=["pv%d" % it, "VcA"], writes=["VcA"])

    if int(os.environ.get('MIX_STOP', '9')) <= 2:
        return
    bar()
    base3 = A.off
    qu = [A.alloc([4, 512], BF16) for _ in range(2)]
    cmk = [A.alloc([2, 512], BF16) for _ in range(2)]
    PcT = [A.alloc([2, 512], BF16) for _ in range(2)]
    imp = A.alloc([4, 64], F32)
    imp2 = A.alloc([64], F32)
    m8 = A.alloc([16], F32)
    thr = A.alloc([1], F32)
    rr = A.alloc([2], F32)
    nmb = A.alloc([128], BF16)
    S.pool(lambda e: e.memset(nmb, 0.0), writes=["nmb"])
    pnt = pbf(pbig, 7, 8)[:, 0:128]
    for Q in range(8):
        qb = Q % 2
        qs = slice(Q * 512, (Q + 1) * 512)
        S.dma("sp", qu[qb][0:64], T["qu_d"][:, :, qs], writes=["qu%d" % qb])
        S.dma("pool", cmk[qb], T["cmask"][:, qs].rearrange("(c p) t -> p c t", p=128), writes=["cmk%d" % qb])
        for h in range(4):
            pb2 = h % 2
            for it in range(2):
                ps_ = pbig[:, pb2 * 2 + it, :]
                S.pe(lambda e, it=it, h=h, qb=qb, ps_=ps_: e.matmul(ps_, KcT[0:64, it * 128:(it + 1) * 128], qu[qb][0:64, h, :], start=True, stop=False),
                     reads=["KcT", "qu%d" % qb], writes=["psc%d_%d" % (pb2, it)])
                S.pe(lambda e, it=it, qb=qb, ps_=ps_: e.matmul(ps_, identb, cmk[qb][:, it, :], start=False, stop=True),
                     reads=["ident", "cmk%d" % qb], writes=["psc%d_%d" % (pb2, it)])
                S.act(lambda e, it=it, pb2=pb2, ps_=ps_: e.activation(PcT[pb2][:, it, :], ps_, AF.Exp, scale=SCALE), reads=["psc%d_%d" % (pb2, it)], writes=["PcT%d_%d" % (pb2, it)])
            for sub in range(4):
                tt = Q * 4 + sub
                po = pbig[:, 4 + (sub % 2), 0:129]
                for it in range(2):
                    S.pe(lambda e, it=it, pb2=pb2, sub=sub, po=po: e.matmul(po, PcT[pb2][:, it, sub * 128:(sub + 1) * 128], VcA[:, it, :], start=(it == 0), stop=(it == 1)),
                         reads=["PcT%d_0" % pb2, "PcT%d_1" % pb2, "VcA"], writes=["po%d" % (sub % 2)])
                pr = ["po%d" % (sub % 2)]
                S.dve(lambda e, po=po: e.tensor_scalar(rr[:, 0:1], po[:, 64:65], 1e-30, None, ALU.add), reads=pr, writes=["rr0"])
                S.dve(lambda e: e.reciprocal(rr[:, 0:1], rr[:, 0:1]), reads=["rr0"], writes=["rr0"])
                if h == 0:
                    S.dve(lambda e, po=po, sub=sub: e.tensor_scalar(imp[:, sub, :], po[:, 65:129], rr[:, 0:1], None, ALU.mult), reads=pr + ["rr0"], writes=["imp%d" % sub])
                else:
                    S.dve(lambda e, po=po, sub=sub: e.scalar_tensor_tensor(imp[:, sub, :], po[:, 65:129], rr[:, 0:1], imp[:, sub, :], ALU.mult, ALU.add),
                          reads=pr + ["rr0", "imp%d" % sub], writes=["imp%d" % sub])
                S.dve(lambda e, tt=tt, h=h: e.tensor_tensor(rr[:, 1:2], rr[:, 0:1], gate[:, tt, h * 3:h * 3 + 1], ALU.mult), reads=["rr0", "gate"], writes=["rr1"])
                S.dve(lambda e, po=po, tt=tt, h=h: e.tensor_scalar(attO[:, tt, h * 64:(h + 1) * 64], po[:, 0:64], rr[:, 1:2], None, ALU.mult), reads=pr + ["rr1"], writes=["attO%d" % tt])
        for sub in range(4):
            tt = Q * 4 + sub
            ir = ["imp%d" % sub]
            im = imp[:, sub, :]
            S.pool(lambda e, im=im: e.memset(im[:, 0:1], 1e4), reads=ir, writes=ir)
            lo = max(2 * tt - 1, 0)
            S.pool(lambda e, im=im, lo=lo, tt=tt: e.memset(im[0:64, lo:2 * tt + 1], 1e4), reads=ir, writes=ir)
            S.pool(lambda e, im=im, tt=tt: e.memset(im[64:128, 2 * tt:2 * tt + 2], 1e4), reads=ir, writes=ir)
            if 2 * tt + 1 < 64:
                S.pool(lambda e, im=im, tt=tt: e.memset(im[0:64, 2 * tt + 1:64], -1.0), reads=ir, writes=ir)
            if 2 * tt + 2 < 64:
                S.pool(lambda e, im=im, tt=tt: e.memset(im[64:128, 2 * tt + 2:64], -1.0), reads=ir, writes=ir)
            S.dve(lambda e, im=im: e.max(m8[:, 0:8], im), reads=ir, writes=["m8a"])
            S.dve(lambda e, im=im: e.match_replace(imp2, m8[:, 0:8], im, -1e30), reads=ir + ["m8a"], writes=["imp2"])
            S.dve(lambda e: e.max(m8[:, 8:16], imp2), reads=["imp2"], writes=["m8b"])
            S.dve(lambda e: e.tensor_scalar(thr, m8[:, 15:16], 0.0, None, ALU.max), reads=["m8b"], writes=["thr"])
            S.dve(lambda e, im=im: e.tensor_scalar(nmb[:, 64:128], im, thr, NEG, ALU.is_lt, ALU.mult), reads=ir + ["thr", "nmb"], writes=["nmb"])
            S.pe(lambda e: e.transpose(pnt, nmb, identb), reads=["nmb", "ident"], writes=["pnt"])
            S.act(lambda e, tt=tt: e.copy(nmT[64:128, tt * 128:(tt + 1) * 128], pnt[64:128, :]), reads=["pnt"], writes=["nmT"])

    if int(os.environ.get('MIX_STOP', '9')) <= 3:
        return
    bar()
    A.reset(base3)
    Qa = [A.alloc([4, 512], BF16) for _ in range(2)]
    PT = [A.alloc([512], BF16) for _ in range(3)]
    PW = [A.alloc([128], BF16) for _ in range(3)]
    r4 = A.alloc([2], F32)
    pti = 0
    pwi = 0
    for Q in range(8):
        qb = Q % 2
        qs = slice(Q * 512, (Q + 1) * 512)
        S.dma("sp", Qa[qb][0:64], T["qr_d"][:, :, qs], writes=["Qa%d" % qb])
        for h in range(4):
            S.pool(lambda e, qb=qb, h=h, qs=qs: e.tensor_copy(Qa[qb][64:128, h, :], nmT[64:128, qs]), reads=["nmT"], writes=["Qm%d_%d" % (qb, h)])
        for h in range(4):
            qr_ = ["Qa%d" % qb, "Qm%d_%d" % (qb, h)]
            posel = pbig[:, 6, 0:260].rearrange("p (s c) -> p s c", s=4)
            powin = pbig[:, 7, 0:260].rearrange("p (s c) -> p s c", s=4)
            S.dve(lambda e: e.memset(pbig[:, 6, 0:260], 0.0), writes=["posel"])
            for kt in range(4 * Q + 4):
                ks_ = slice(kt * 128, (kt + 1) * 128)
                sb_ = kt % 3
                ps_ = pbig[:, sb_, :]
                pres = "pss%d" % sb_
                o = kt - 4 * Q
                if o < 0:
                    S.pe(lambda e, ks_=ks_, qb=qb, h=h, ps_=ps_: e.matmul(ps_, KsA[:, ks_], Qa[qb][:, h, :], start=True, stop=True),
                         reads=["KsA", "KsE"] + qr_, writes=[pres])
                    lo = 0
                else:
                    lo = o * 128
                    S.pe(lambda e, ks_=ks_, qb=qb, h=h, ps_=ps_, lo=lo: e.matmul(ps_[:, lo:lo + 128], KsA[:, ks_], Qa[qb][:, h, lo:lo + 128], start=True, stop=False),
                         reads=["KsA", "KsE"] + qr_, writes=[pres])
                    S.pe(lambda e, ps_=ps_, lo=lo: e.matmul(ps_[:, lo:lo + 128], identb, tricb, start=False, stop=True), reads=["ident", "tric"], writes=[pres])
                    if o < 3:
                        S.pe(lambda e, ks_=ks_, qb=qb, h=h, ps_=ps_, lo=lo: e.matmul(ps_[:, lo + 128:512], KsA[:, ks_], Qa[qb][:, h, lo + 128:512], start=True, stop=True),
                             reads=["KsA", "KsE"] + qr_, writes=[pres])
                pt_ = PT[pti % 3]
                ptres = "PT%d" % (pti % 3)
                pti += 1
                S.act(lambda e, ps_=ps_, pt_=pt_, lo=lo: e.activation(pt_[:, lo:512], ps_[:, lo:512], AF.Exp, scale=SCALE), reads=[pres], writes=[ptres])
                for sub in range(max(o, 0), 4):
                    S.pe(lambda e, pt_=pt_, sub=sub, kt=kt, Q=Q, posel=posel: e.matmul(posel[:, sub, :], pt_[:, sub * 128:(sub + 1) * 128], VsA[:, kt, :],
                                                                                 start=False, stop=False, skip_group_check=True),
                         reads=[ptres, "VsA"], writes=["posel"])
            for sub in range(4):
                tt = 4 * Q + sub
                kts = [k for k in range(tt - 4, tt + 1) if k >= 0]
                for kt in kts:
                    ks_ = slice(kt * 128, (kt + 1) * 128)
                    wsl = pwi % 4
                    psw = pbig[:, 3 + wsl // 2, (wsl % 2) * 128:(wsl % 2) * 128 + 128]
                    pwres = "psw%d" % wsl
                    msk = triab if kt == tt - 4 else (tricb if kt == tt else None)
                    S.pe(lambda e, ks_=ks_, qb=qb, h=h, sub=sub, psw=psw, msk=msk: e.matmul(psw, KwT[0:64, ks_], Qa[qb][0:64, h, sub * 128:(sub + 1) * 128], start=True, stop=(msk is None)),
                         reads=["KwT", "Qa%d" % qb], writes=[pwres])
                    if msk is not None:
                        S.pe(lambda e, psw=psw, msk=msk: e.matmul(psw, identb, msk, start=False, stop=True), reads=["ident", "tric", "tria"], writes=[pwres])
                    pw_ = PW[pwi % 3]
                    pwr = "PW%d" % (pwi % 3)
                    pwi += 1
                    S.act(lambda e, psw=psw, pw_=pw_: e.activation(pw_, psw, AF.Exp, scale=SCALE), reads=[pwres], writes=[pwr])
                    S.pe(lambda e, pw_=pw_, sub=sub, kt=kt, kts=kts, powin=powin: e.matmul(powin[:, sub, :], pw_, VwA[:, kt, :], start=(kt == kts[0]), stop=(kt == kts[-1])),
                         reads=[pwr, "VwA"], writes=["powin"])
            for sub in range(4):
                tt = 4 * Q + sub
                for br, (po_, pres) in enumerate(((posel, "posel"), (powin, "powin"))):
                    S.dve(lambda e, po_=po_, sub=sub, br=br: e.reciprocal(r4[:, br:br + 1], po_[:, sub, 64:65]), reads=[pres], writes=["r4_%d" % br])
                    S.dve(lambda e, tt=tt, h=h, br=br: e.tensor_tensor(r4[:, br:br + 1], r4[:, br:br + 1], gate[:, tt, h * 3 + 1 + br:h * 3 + 2 + br], ALU.mult),
                          reads=["r4_%d" % br, "gate"], writes=["r4_%d" % br])
                    S.dve(lambda e, po_=po_, sub=sub, tt=tt, h=h, br=br: e.scalar_tensor_tensor(attO[:, tt, h * 64:(h + 1) * 64], po_[:, sub, 0:64], r4[:, br:br + 1],
                                                                                                  attO[:, tt, h * 64:(h + 1) * 64], ALU.mult, ALU.add),
                          reads=[pres, "r4_%d" % br, "attO%d" % tt], writes=["attO%d" % tt])
        for sub in range(4):
            tt = 4 * Q + sub
            S.dma("sp", T["mixo"][tt * 128:(tt + 1) * 128, 256:512], attO[:, tt, :], reads=["attO%d" % tt])


def _perm_cols():
    return None


def run_mix(inputs):
    if "mix" not in _CACHE:
        _CACHE["mix"] = build_mix()
    nc = _CACHE["mix"]
    x = inputs["x"]
    w_in = inputs["w_in"][0]
    offs = np.cumsum([0, 1024, 1536, 16, 1024, 256, 256, 256, 256, 256, 256, 48])
    oz, oxbc, odt, oq, okc, ovc, oks, ovs, okw, ovw, ogate = offs[:11]
    conv_w = inputs["conv_w"][0]
    conv_b = inputs["conv_b"][0]
    t = np.arange(SEQ, dtype=np.float32)
    inv = (1.0 / (500000.0 ** (np.arange(0, 16, 2, dtype=np.float32) / np.float32(16)))).astype(np.float32)
    ang = (t[:, None] * inv[None, :]).astype(np.float32)
    rope = np.concatenate([np.cos(ang), np.sin(ang)], 1).astype(np.float32)
    rope = np.ascontiguousarray(rope.reshape(NTT, 128, 16).transpose(1, 0, 2).reshape(128, NTT * 16))
    ident = np.eye(128, dtype=np.float32)
    kk = np.arange(128)[:, None]
    qq = np.arange(128)[None, :]
    tric = np.where(kk <= qq, 0.0, NEG).astype(np.float32)
    tria = np.where(kk > qq, 0.0, NEG).astype(np.float32)
    utri = (kk <= qq).astype(np.float32)
    emat = (np.arange(SEQ)[None, :] // 64 == np.arange(64)[:, None]).astype(np.float32)
    ii = np.arange(256)[:, None]
    cmask = np.where((16 * ii + 31 <= np.arange(SEQ)[None, :]) & (ii < 255), 0.0, NEG).astype(np.float32)
    cs = np.arange(255)[:, None] * 16
    ss_ = np.arange(64)[None, :] * 64
    ov = np.clip(np.minimum(cs + 32, ss_ + 64) - np.maximum(cs, ss_), 0, None) / 32.0
    ovl = np.zeros((256, 64), np.float32)
    ovl[:255] = ov
    in_maps = []
    for c in range(8):
        b, g = c // 4, c % 4
        grp = g // 2
        ar = np.arange
        tm_cols = np.concatenate([oz + 256 * g + ar(256), odt + 4 * g + ar(4), ogate + 12 * g + ar(12), ovs + 64 * g + ar(64), ovw + 64 * g + ar(64),
                                  oq + 256 * g + ar(256), oks + 64 * g + ar(64), okw + 64 * g + ar(64)])
        xcols = np.concatenate([256 * g + ar(256), 1024 + 128 * grp + ar(128), 1280 + 128 * grp + ar(128)])
        fm_cols = np.concatenate([oxbc + xcols, okc + 64 * g + ar(64), ovc + 64 * g + ar(64)])
        convw = np.ascontiguousarray(conv_w[:, xcols].T.reshape(4, 128, 4).transpose(1, 0, 2).reshape(128, 16))
        convb = np.ascontiguousarray(conv_b[xcols].reshape(4, 128).T)
        hp = np.concatenate([inputs["dt_bias"][0][4 * g:4 * g + 4], inputs["a_log"][0][4 * g:4 * g + 4], inputs["d_skip"][0][4 * g:4 * g + 4]]).astype(np.float32)
        in_maps.append(dict(
            xb=np.ascontiguousarray(x[b]), anw=inputs["attn_norm_w"][0], w_tm=np.ascontiguousarray(w_in[:, tm_cols]), w_fm=np.ascontiguousarray(w_in[:, fm_cols]),
            convw=convw, convb=convb, hp=hp, rope=rope,
            w1k=np.ascontiguousarray(inputs["cmp_w1_k"][0].reshape(32, 64, 256).transpose(1, 0, 2).reshape(64, 32 * 256)),
            w1v=np.ascontiguousarray(inputs["cmp_w1_v"][0].reshape(32, 64, 256).transpose(1, 0, 2).reshape(64, 32 * 256)),
            w2k=inputs["cmp_w2_k"][0], w2v=inputs["cmp_w2_v"][0],
            pek=np.ascontiguousarray(inputs["cmp_pe_k"][0].T), pev=np.ascontiguousarray(inputs["cmp_pe_v"][0].T),
            ident=ident, emat=emat, tric=tric, tria=tria, cmask=cmask, ovl=ovl, utri=utri,
        ))
    res = run_bass_kernel_spmd(nc, in_maps, core_ids=list(range(8)))
    mixed = np.empty((2, SEQ, D), np.float32)
    for c in range(8):
        b, g = c // 4, c % 4
        m = res.results[c]["mixo"]
        mixed[b, :, 256 * g:256 * (g + 1)] = m[:, 0:256]
        mixed[b, :, 1024 + 256 * g:1024 + 256 * (g + 1)] = m[:, 256:512]
    return mixed


def kernel(**inputs):
    inputs = {k: np.asarray(v) for k, v in inputs.items()}
    mixed = run_mix(inputs)
    return run_tail(inputs["x"], mixed, inputs["ssd_norm_w"][0], inputs["w_out"][0], inputs["ffn_norm_w"][0],
                    inputs["w_gate"][0], inputs["w_up"][0], inputs["w_down"][0], inputs["final_norm_w"])
```

```python
import contextlib
import os
import numpy as np
import ml_dtypes
import concourse.bass as bass
import concourse.mybir as mybir
from concourse.bass_utils import run_bass_kernel_spmd

F32 = mybir.dt.float32
BF16 = mybir.dt.bfloat16
U8 = mybir.dt.uint8
ALU = mybir.AluOpType
AF = mybir.ActivationFunctionType
AX = mybir.AxisListType

ENGS = ("pe", "act", "dve", "pool", "sp")
EPS = 1e-6


class _Op:
    __slots__ = ("eng", "fn", "dma", "deps", "idx", "signal", "val", "sem", "cc", "cost")


class Sched:
    def __init__(self, nc, n_dma_sems=8):
        self.nc = nc
        self.ops = []
        self.last_w = {}
        self.readers = {}
        self.n_dma_sems = n_dma_sems
        self.bar = None
        self.bank_of = {}

    def op(self, eng, fn, reads=(), writes=(), dma=False, c=None):
        o = _Op()
        o.cost = c
        o.eng, o.fn, o.dma = eng, fn, dma
        o.cc = False
        o.idx = len(self.ops)
        o.signal = False
        deps = {}
        if self.bar is not None:
            deps[self.bar] = "raw"
        for r in reads:
            w = self.last_w.get(r)
            if w is not None:
                deps[w] = "raw"
        for w_ in writes:
            w = self.last_w.get(w_)
            if w is not None:
                deps[w] = "raw"
            for r in self.readers.get(w_, ()):
                if r not in deps:
                    deps[r] = "war"
        for r in reads:
            self.readers.setdefault(r, []).append(o.idx)
        for w_ in writes:
            self.last_w[w_] = o.idx
            self.readers[w_] = []
        banks = set()
        for r in tuple(reads) + tuple(writes):
            banks.update(self.bank_of.get(r, ()))
        for b in banks:
            key = ("bank", b)
            w = self.last_w.get(key)
            if w is not None and w not in deps:
                deps[w] = "bank"
            self.last_w[key] = o.idx
        deps.pop(o.idx, None)
        o.deps = deps
        self.ops.append(o)
        return o

    def pe(self, fn, reads=(), writes=(), c=None):
        return self.op("pe", fn, reads, writes, c=c)

    def act(self, fn, reads=(), writes=(), c=None):
        return self.op("act", fn, reads, writes, c=c)

    def dve(self, fn, reads=(), writes=(), c=None):
        return self.op("dve", fn, reads, writes, c=c)

    def pool(self, fn, reads=(), writes=(), c=None):
        return self.op("pool", fn, reads, writes, c=c)

    DEF_COST = {"pe": 0.12, "act": 0.35, "dve": 0.25, "pool": 0.35}

    def reorder(self, window=int(os.environ.get("RWIN", "120"))):
        ops = self.ops
        n = len(ops)
        queues = {e: [] for e in ENGS}
        for o in ops:
            queues[o.eng].append(o.idx)
        head = {e: 0 for e in ENGS}
        sched = [False] * n
        fin = [0.0] * n
        etime = {e: 0.0 for e in ENGS}
        order = []
        left = n
        while left:
            best = None
            for e in ENGS:
                q = queues[e]
                h = head[e]
                while h < len(q) and sched[q[h]]:
                    h += 1
                head[e] = h
                if h >= len(q):
                    continue
                seen = 0
                i = h
                et = etime[e]
                while i < len(q) and seen < window:
                    k = q[i]
                    i += 1
                    if sched[k]:
                        continue
                    seen += 1
                    o = ops[k]
                    rdy = 0.0
                    ok = True
                    for d in o.deps:
                        if not sched[d]:
                            ok = False
                            break
                        f = fin[d] + (0.0 if ops[d].eng == e else 0.15)
                        if f > rdy:
                            rdy = f
                    if not ok:
                        continue
                    st = rdy if rdy > et else et
                    key = (st, k)
                    if best is None or key < best[0]:
                        best = (key, e, k)
                    if st <= et:
                        break
            assert best is not None, "scheduler stuck"
            (st, k), e, _ = best
            o = ops[k]
            if o.dma:
                etime[e] = st + 0.06
                fin[k] = st + (o.cost if o.cost is not None else 3.0)
            else:
                c = o.cost if o.cost is not None else self.DEF_COST[e]
                etime[e] = st + c
                fin[k] = st + c
            sched[k] = True
            order.append(k)
            left -= 1
        self.order = order
        self.est_time = max(fin) if fin else 0.0

    def dma(self, q, out, in_, reads=(), writes=(), c=None):
        return self.op(q, lambda e: e.dma_start(out=out, in_=in_), reads, writes, dma=True, c=c)

    def cc(self, fn, reads=(), writes=()):
        o = self.op("pool", fn, reads, writes, dma=True)
        o.cc = True
        return o

    def barrier(self, out, in_):
        allres = set(self.last_w.keys()) | set(self.readers.keys())
        o = self.op("sp", lambda e: e.dma_start(out=out, in_=in_), reads=(), writes=tuple(allres), dma=True)
        self.bar = o.idx
        self.last_w = {}
        self.readers = {}
        return o

    def emit(self, sems, block, final_wait_eng="sp"):
        ops = self.ops
        need = [False] * len(ops)
        for o in ops:
            for d, kind in o.deps.items():
                do = ops[d]
                if do.dma:
                    continue
                if do.eng == o.eng and not o.dma:
                    if do.eng == "pe" or kind == "bank":
                        continue
                need[d] = True
        cnt = {e: 0 for e in ENGS}
        dcnt = {}
        dval = {}
        per_eng = {e: [] for e in ENGS}
        order = getattr(self, "order", None) or list(range(len(ops)))
        for k_ in order:
            o = ops[k_]
            per_eng[o.eng].append(o)
            if o.dma and o.cc:
                o.sem = "cc"
                dval["cc"] = dval.get("cc", 0) + 1
                o.val = dval["cc"]
            elif o.dma:
                k = dcnt.get(o.eng, 0)
                dcnt[o.eng] = k + 1
                key = ("dma", o.eng, k % self.n_dma_sems)
                o.sem = key
                dval[key] = dval.get(key, 0) + 16
                o.val = dval[key]
            elif need[o.idx]:
                cnt[o.eng] += 1
                o.val = cnt[o.eng]
                o.sem = o.eng
                o.signal = True
        self.stats = {e: len(per_eng[e]) for e in ENGS}
        self.stats["signals"] = dict(cnt)

        def run(engname, e):
            waited = {}
            for o in per_eng[engname]:
                wl = {}
                for d, kind in o.deps.items():
                    do = ops[d]
                    if do.dma:
                        wl[do.sem] = max(wl.get(do.sem, 0), do.val)
                        continue
                    if do.eng == o.eng and not o.dma:
                        if do.eng == "pe" or kind == "bank":
                            continue
                    wl[do.sem] = max(wl.get(do.sem, 0), do.val)
                if o.dma and o.cc and o.val > 1:
                    wl[o.sem] = max(wl.get(o.sem, 0), o.val - 1)
                elif o.dma and not o.cc and o.val > 16:
                    wl[o.sem] = max(wl.get(o.sem, 0), o.val - 16)
                for s, v in wl.items():
                    if waited.get(s, 0) >= v:
                        continue
                    waited[s] = v
                    e.wait_ge(sems[s], v)
                ins = o.fn(e)
                if o.dma and o.cc:
                    ins.then_inc(sems[o.sem])
                elif o.dma:
                    ins.then_inc(sems[o.sem], 16)
                elif o.signal:
                    ins.then_inc(sems[o.sem], 1)
            if engname == final_wait_eng:
                for key, v in dval.items():
                    if waited.get(key, 0) < v:
                        e.wait_ge(sems[key], v)
                for en in ("pe", "act", "dve", "pool"):
                    if cnt[en] > 0 and waited.get(en, 0) < cnt[en]:
                        e.wait_ge(sems[en], cnt[en])

        @block.tensor
        def _(e):
            run("pe", e)

        @block.scalar
        def _(e):
            run("act", e)

        @block.vector
        def _(e):
            run("dve", e)

        @block.gpsimd
        def _(e):
            run("pool", e)

        @block.sync
        def _(e):
            run("sp", e)


def make_sems(nc, stack, n_dma_sems=8, queues=("sp", "pool", "act")):
    sems = {}
    for e in ("pe", "act", "dve", "pool", "cc"):
        sems[e] = stack.enter_context(nc.semaphore("s_" + e))
    for q in queues:
        for i in range(n_dma_sems):
            sems[("dma", q, i)] = stack.enter_context(nc.semaphore("d_%s_%d" % (q, i)))
    return sems


_DTSZ = {F32: 4, BF16: 2, U8: 1}


class Arena:
    def __init__(self, ar, size):
        self.ar, self.size, self.off = ar, size, 0

    def reset(self, off=0):
        self.off = off

    def alloc(self, shape, dtype, parts=128):
        n = int(np.prod(shape)) * _DTSZ[dtype]
        off = (self.off + 63) // 64 * 64
        assert off + n <= self.size, ("arena overflow", off, n, self.size)
        self.off = off + n
        ap = self.ar[0:parts, off:off + n].bitcast(dtype)
        if len(shape) > 1:
            names = [chr(ord("a") + i) for i in range(len(shape))]
            pat = "p (%s) -> p %s" % (" ".join(names), " ".join(names))
            ap = ap.rearrange(pat, **{nm: int(s) for nm, s in zip(names, shape)})
        return ap


D = 2048
FF = 5632
TOK = 1024
NT = TOK // 128
KC = D // 128
FC = FF // 128
SSDW = 1024


def build_tail():
    nc = bass.Bass("TRN2", target_bir_lowering=False)
    x = nc.dram_tensor("x_own", [TOK, D], F32, kind="ExternalInput").ap()
    mix = nc.dram_tensor("mix", [TOK, D], F32, kind="ExternalInput").ap()
    w_out = nc.dram_tensor("w_out", [D, D], F32, kind="ExternalInput").ap()
    w_gate = nc.dram_tensor("w_gate", [D, FF], F32, kind="ExternalInput").ap()
    w_up = nc.dram_tensor("w_up", [D, FF], F32, kind="ExternalInput").ap()
    w_down = nc.dram_tensor("w_down", [FF, D], F32, kind="ExternalInput").ap()
    nw = nc.dram_tensor("nw", [3, D], F32, kind="ExternalInput").ap()
    ident = nc.dram_tensor("ident", [128, 128], F32, kind="ExternalInput").ap()
    out = nc.dram_tensor("out", [TOK, D], F32, kind="ExternalOutput").ap()
    h_d = nc.dram_tensor("h_d", [TOK, D], F32, kind="Internal").ap()
    dummy = nc.dram_tensor("dummy_bar", [2, 64], F32, kind="Internal").ap()
    with contextlib.ExitStack() as st:
        ASZ = 207 * 1024
        ar = st.enter_context(nc.sbuf_tensor("arena", [128, ASZ], U8))
        A = Arena(ar, ASZ)
        pbig = st.enter_context(nc.psum_tensor("pbig", [128, 8, 512], F32))
        sems = make_sems(nc, st)
        block = st.enter_context(nc.Block())
        S = Sched(nc)
        tail_body(nc, S, A, pbig, x, mix, w_out, w_gate, w_up, w_down, nw, ident, out, h_d, dummy)
        if os.environ.get('NO_REORDER') is None:
            S.reorder()
        S.emit(sems, block)
    return nc


def rms_rstd(S, src, n, ss, sq, tag, rd, wr_extra=(), c=None):
    S.act(lambda e: e.activation(sq, src, AF.Square, accum_out=ss), reads=rd, writes=[tag + "ss", tag + "sq"], c=c)
    S.act(lambda e: e.activation(ss, ss, AF.Sqrt, scale=1.0 / n, bias=EPS_AP[0]), reads=[tag + "ss"], writes=[tag + "ss"])
    S.dve(lambda e: e.reciprocal(ss, ss), reads=[tag + "ss"], writes=[tag + "ss"])


EPS_AP = [None]


def tail_body(nc, S, A, pbig, x, mix, w_out, w_gate, w_up, w_down, nw, ident, out, h_d, dummy):
    bk = {"ptr": (4, 5)}
    for i_ in range(8):
        bk["pacc%d" % i_] = (i_,)
        bk["pd%d" % i_] = (i_,)
    for i_ in range(2):
        bk["pg%d" % i_] = (i_ * 4, i_ * 4 + 1)
        bk["pu%d" % i_] = (i_ * 4 + 2, i_ * 4 + 3)
    S.bank_of = bk
    identb = A.alloc([128], BF16)
    nwb_flat = A.alloc([2 * D + SSDW], F32)
    epsb = A.alloc([1], F32)
    ss = A.alloc([4], F32)
    EPS_AP[0] = epsb
    S.pool(lambda e: e.memset(epsb, EPS), writes=["eps"])
    S.dma("pool", identb, ident, writes=["ident"])
    S.dma("sp", nwb_flat, nw.rearrange("a b -> (a b)")[0:2 * D + SSDW].partition_broadcast(128), writes=["nwb"])

    class _NW:
        def __getitem__(self, key):
            _, row, cols = key
            base = {1: 0, 2: D, 0: 2 * D}[row]
            lo = cols.start or 0
            hi = cols.stop if cols.stop is not None else (SSDW if row == 0 else D)
            return nwb_flat[:, base + lo:base + hi]
    nwb = _NW()
    vT = A.alloc([KC, TOK], BF16)
    base_persist = A.off

    wo = A.alloc([KC, D], BF16)
    for cb in range(4):
        S.dma("pool", wo[:, :, cb * 512:(cb + 1) * 512],
              w_out[:, cb * 512:(cb + 1) * 512].rearrange("(k p) n -> p k n", p=128), writes=["wo%d" % cb], c=25.0)
    xt = [A.alloc([D], F32) for _ in range(2)]
    mt = [A.alloc([D], F32) for _ in range(2)]
    sq = A.alloc([D], BF16)
    mb = A.alloc([D], BF16)
    mT = A.alloc([KC, 128], BF16)
    hs = [A.alloc([D], F32) for _ in range(2)]
    vb = A.alloc([D], BF16)
    WB = 256
    A.alloc([1024], F32)
    pre_lo = (A.off + 63) // 64 * 64
    wg_pre = A.alloc([KC, WB], BF16)
    wu_pre = A.alloc([KC, WB], BF16)
    S.dma("pool", wg_pre, w_gate[:, 0:WB].rearrange("(k p) n -> p k n", p=128), writes=["wg0"], c=15.0)
    S.dma("pool", wu_pre, w_up[:, 0:WB].rearrange("(k p) n -> p k n", p=128), writes=["wu0"], c=15.0)
    pacc = pbig[:, 0:4, :]
    ptr_all = pbig[:, 4:6, :].rearrange("p a b -> p (a b)").bitcast(BF16)
    ptr = ptr_all.rearrange("p (k n) -> p k n", k=KC)

    def loads(tt):
        b = tt % 2
        S.dma("sp", xt[b], x[tt * 128:(tt + 1) * 128, :], writes=["xt%d" % b])
        S.dma("sp", mt[b], mix[tt * 128:(tt + 1) * 128, :], writes=["mt%d" % b])

    loads(0)
    for tt in range(NT):
        b = tt % 2
        if tt + 1 < NT:
            loads(tt + 1)
        rms_rstd(S, mt[b][:, 0:SSDW], SSDW, ss[:, 0:1], sq[:, 0:SSDW], "a", ["mt%d" % b, "eps"])
        S.dve(lambda e, b=b: e.scalar_tensor_tensor(mb[:, 0:SSDW], mt[b][:, 0:SSDW], ss[:, 0:1], nwb[:, 0, 0:SSDW], ALU.mult, ALU.mult),
              reads=["mt%d" % b, "ass", "nwb"], writes=["mb0"])
        S.pool(lambda e, b=b: e.tensor_copy(mb[:, SSDW:D], mt[b][:, SSDW:D]), reads=["mt%d" % b], writes=["mb1"])
        for kc in range(KC):
            S.pe(lambda e, kc=kc: e.transpose(ptr[:, kc, :], mb[:, kc * 128:(kc + 1) * 128], identb),
                 reads=["mb0", "mb1", "ident"], writes=["ptr"])
        S.act(lambda e: e.copy(mT[:, 0:8, :], ptr[:, 0:8, :]), reads=["ptr"], writes=["mTa"])
        S.dve(lambda e: e.tensor_copy(mT[:, 8:16, :], ptr[:, 8:16, :]), reads=["ptr"], writes=["mTb"])
        for cb in range(4):
            for kc in range(KC):
                S.pe(lambda e, cb=cb, kc=kc: e.matmul(pacc[:, cb, :], mT[:, kc, :], wo[:, kc, cb * 512:(cb + 1) * 512],
                                                       start=(kc == 0), stop=(kc == KC - 1)),
                     reads=["mTa", "mTb", "wo%d" % cb], writes=["pacc%d" % cb], c=0.22)
            S.dve(lambda e, cb=cb, b=b: e.tensor_tensor(hs[b][:, cb * 512:(cb + 1) * 512], pacc[:, cb, :], xt[b][:, cb * 512:(cb + 1) * 512], ALU.add),
                  reads=["pacc%d" % cb, "xt%d" % b], writes=["hs%d_%d" % (b, cb)])
        hres = ["hs%d_%d" % (b, cb) for cb in range(4)]
        S.dma("sp", h_d[tt * 128:(tt + 1) * 128, :], hs[b], reads=hres, writes=["h_d%d" % tt])
        rms_rstd(S, hs[b], D, ss[:, 1:2], sq, "b", hres + ["eps"])
        S.dve(lambda e, b=b: e.scalar_tensor_tensor(vb, hs[b], ss[:, 1:2], nwb[:, 1, :], ALU.mult, ALU.mult),
              reads=hres + ["bss", "nwb"], writes=["vb"])
        for kc in range(KC):
            S.pe(lambda e, kc=kc: e.transpose(ptr[:, kc, :], vb[:, kc * 128:(kc + 1) * 128], identb),
                 reads=["vb", "ident"], writes=["ptr"])
        S.act(lambda e, tt=tt: e.copy(vT[:, 0:8, tt * 128:(tt + 1) * 128], ptr[:, 0:8, :]), reads=["ptr"], writes=["vT%da" % tt])
        S.dve(lambda e, tt=tt: e.tensor_copy(vT[:, 8:16, tt * 128:(tt + 1) * 128], ptr[:, 8:16, :]), reads=["ptr"], writes=["vT%db" % tt])

    S.barrier(dummy[1:2, :], ident[0:1, 0:64])
    A.reset(base_persist)
    hT = A.alloc([FC, TOK], BF16)
    HK = 22
    wd_pre = A.alloc([HK, 512], BF16)
    base_b = A.off
    NB = FF // WB
    wg = [wg_pre, A.alloc([KC, WB], BF16)]
    wu = [wu_pre, A.alloc([KC, WB], BF16)]
    sg = [A.alloc([TOK], BF16) for _ in range(2)]
    assert A.off <= pre_lo, (A.off, pre_lo)
    for blk in range(NB):
        b = blk % 2
        if blk > 0:
            S.dma("pool", wg[b], w_gate[:, blk * WB:(blk + 1) * WB].rearrange("(k p) n -> p k n", p=128), writes=["wg%d" % b], c=15.0)
            S.dma("pool", wu[b], w_up[:, blk * WB:(blk + 1) * WB].rearrange("(k p) n -> p k n", p=128), writes=["wu%d" % b], c=15.0)
        if blk == NB - 2:
            S.dma("pool", wd_pre, w_down[0:HK * 128, 0:512].rearrange("(k p) n -> p k n", p=128), writes=["wd0"], c=15.0)
        for j in range(WB // 128):
            fc = blk * (WB // 128) + j
            pb = fc % 2
            pg = pbig[:, pb * 4:pb * 4 + 2, :]
            pu = pbig[:, pb * 4 + 2:pb * 4 + 4, :]
            for hf in range(2):
                for kc in range(KC):
                    S.pe(lambda e, b=b, j=j, hf=hf, kc=kc, pg=pg: e.matmul(pg[:, hf, :], wg[b][:, kc, j * 128:(j + 1) * 128], vT[:, kc, hf * 512:(hf + 1) * 512],
                                                                         start=(kc == 0), stop=(kc == KC - 1)),
                         reads=["wg%d" % b, "vT"], writes=["pg%d" % pb], c=0.22)
            for hf in range(2):
                for kc in range(KC):
                    S.pe(lambda e, b=b, j=j, hf=hf, kc=kc, pu=pu: e.matmul(pu[:, hf, :], wu[b][:, kc, j * 128:(j + 1) * 128], vT[:, kc, hf * 512:(hf + 1) * 512],
                                                                         start=(kc == 0), stop=(kc == KC - 1)),
                         reads=["wu%d" % b, "vT"], writes=["pu%d" % pb], c=0.22)
            S.act(lambda e, pb=pb, pg=pg: e.activation(sg[pb], pg.rearrange("p a b -> p (a b)"), AF.Silu), reads=["pg%d" % pb], writes=["sg%d" % pb])
            S.dve(lambda e, pb=pb, pu=pu, fc=fc: e.tensor_tensor(hT[:, fc, :], sg[pb], pu.rearrange("p a b -> p (a b)"), ALU.mult),
                  reads=["sg%d" % pb, "pu%d" % pb], writes=["hT%d" % fc])

    S.barrier(dummy[1:2, :], ident[0:1, 0:64])
    A.reset(base_b)
    wd = [wd_pre, A.alloc([HK, 512], BF16)]
    hl = [A.alloc([512], F32) for _ in range(2)]
    ys = [A.alloc([512], F32) for _ in range(2)]
    it = 0
    for r in range(4):
        for hf in range(2):
            b = (r * 2 + hf) % 2
            if r * 2 + hf > 0:
                S.dma("pool", wd[b], w_down[hf * HK * 128:(hf + 1) * HK * 128, r * 512:(r + 1) * 512].rearrange("(k p) n -> p k n", p=128), writes=["wd%d" % b], c=15.0)
            for tt in range(NT):
                for k in range(HK):
                    kk = hf * HK + k
                    S.pe(lambda e, b=b, tt=tt, k=k, kk=kk: e.matmul(pbig[:, tt, :], hT[:, kk, tt * 128:(tt + 1) * 128], wd[b][:, k, :],
                                                                    start=(kk == 0), stop=(kk == FC - 1)),
                         reads=["wd%d" % b, "hT"], writes=["pd%d" % tt], c=0.22)
        for tt in range(NT):
            b = it % 2
            it += 1
            S.dma("sp", hl[b], h_d[tt * 128:(tt + 1) * 128, r * 512:(r + 1) * 512], reads=["h_d%d_%d" % (tt, r)], writes=["hl%d" % b])
            S.dve(lambda e, b=b, tt=tt: e.tensor_tensor(ys[b], pbig[:, tt, :], hl[b], ALU.add), reads=["pd%d" % tt, "hl%d" % b], writes=["ys%d" % b])
            S.dma("sp", h_d[tt * 128:(tt + 1) * 128, r * 512:(r + 1) * 512], ys[b], reads=["ys%d" % b], writes=["h_d%d_%d" % (tt, r)])

    S.barrier(dummy[1:2, :], ident[0:1, 0:64])
    A.reset(base_persist)
    yt = [A.alloc([D], F32) for _ in range(2)]
    ot = [A.alloc([D], F32) for _ in range(2)]
    sq2 = A.alloc([D], F32)
    for tt in range(NT):
        b = tt % 2
        S.dma("sp", yt[b], h_d[tt * 128:(tt + 1) * 128, :], writes=["yt%d" % b])
        rms_rstd(S, yt[b], D, ss[:, 2:3], sq2, "c", ["yt%d" % b, "eps"])
        S.dve(lambda e, b=b: e.scalar_tensor_tensor(ot[b], yt[b], ss[:, 2:3], nwb[:, 2, :], ALU.mult, ALU.mult),
              reads=["yt%d" % b, "css", "nwb"], writes=["ot%d" % b])
        S.dma("sp", out[tt * 128:(tt + 1) * 128, :], ot[b], reads=["ot%d" % b])


_CACHE = {}


def run_tail(x, mixed, ssd_norm_w, w_out, ffn_norm_w, w_gate, w_up, w_down, final_norm_w):
    if "tail" not in _CACHE:
        _CACHE["tail"] = build_tail()
    nc = _CACHE["tail"]
    nwv = np.ones((3, D), np.float32)
    nwv[0] = ffn_norm_w
    nwv[1] = final_norm_w
    nwv[2, :SSDW] = ssd_norm_w
    ident = np.eye(128, dtype=np.float32)
    in_maps = []
    for c in range(8):
        b, g = c // 4, c % 4
        in_maps.append({
            "x_own": np.ascontiguousarray(x[b, g * TOK:(g + 1) * TOK]),
            "mix": np.ascontiguousarray(mixed[b, g * TOK:(g + 1) * TOK]),
            "w_out": w_out, "w_gate": w_gate, "w_up": w_up, "w_down": w_down,
            "nw": nwv, "ident": ident,
        })
    res = run_bass_kernel_spmd(nc, in_maps, core_ids=list(range(8)))
    outp = np.empty((2, 4096, D), np.float32)
    for c in range(8):
        b, g = c // 4, c % 4
        outp[b, g * TOK:(g + 1) * TOK] = res.results[c]["out"]
    return outp


SEQ = 4096
NTT = SEQ // 128
NEG = -30000.0
NA = 400
NB_ = 384
NFM = 640
SCALE = 0.125


def build_mix():
    nc = bass.Bass("TRN2", target_bir_lowering=False)
    di = lambda n, s, d=F32: nc.dram_tensor(n, s, d, kind="ExternalInput").ap()
    T = dict(
        xb=di("xb", [SEQ, D]), anw=di("anw", [D]), w_tm=di("w_tm", [D, NA + NB_]), w_fm=di("w_fm", [D, NFM]),
        convw=di("convw", [128, 16]), convb=di("convb", [128, 4]), hp=di("hp", [12]), rope=di("rope", [128, NTT * 16]),
        w1k=di("w1k", [64, 32 * 256]), w1v=di("w1v", [64, 32 * 256]), w2k=di("w2k", [256, 64]), w2v=di("w2v", [256, 64]),
        pek=di("pek", [64, 32]), pev=di("pev", [64, 32]), ident=di("ident", [128, 128]), emat=di("emat", [64, SEQ]),
        tric=di("tric", [128, 128]), tria=di("tria", [128, 128]), cmask=di("cmask", [256, SEQ]), ovl=di("ovl", [256, 64]),
        utri=di("utri", [128, 128]),
    )
    T["mixo"] = nc.dram_tensor("mixo", [SEQ, 512], F32, kind="ExternalOutput").ap()
    T["qr_d"] = nc.dram_tensor("qr_d", [64, 4, SEQ], BF16, kind="Internal").ap()
    T["qu_d"] = nc.dram_tensor("qu_d", [64, 4, SEQ], BF16, kind="Internal").ap()
    T["dummy"] = nc.dram_tensor("dummy_bar", [2, 64], F32, kind="Internal").ap()
    with contextlib.ExitStack() as st:
        ASZ = 207 * 1024
        ar = st.enter_context(nc.sbuf_tensor("arena", [128, ASZ], U8))
        A = Arena(ar, ASZ)
        pbig = st.enter_context(nc.psum_tensor("pbig", [128, 8, 512], F32))
        sems = make_sems(nc, st)
        block = st.enter_context(nc.Block())
        S = Sched(nc)
        mix_body(nc, S, A, pbig, T)
        if os.environ.get('NO_REORDER') is None:
            S.reorder()
        S.emit(sems, block)
    return nc


def pbf(pb, lo, hi):
    return pb[:, lo:hi, :].rearrange("p a b -> p (a b)").bitcast(BF16)


def mix_body(nc, S, A, pbig, T):
    bar = lambda: S.barrier(T["dummy"][1:2, :], T["ident"][0:1, 0:64])
    bk = {"ptr": (0,), "ptr2": (6,), "psA": (1,), "psB": (2,), "psF0": (3,), "psR": (4,), "psS_a": (5,), "psS_c": (5,),
          "psN": (6,), "psY_o": (7,), "psY_d": (7,), "pk": (5,), "pv0": (6,), "pv1": (7,), "pnt": (7,), "posel": (6,), "powin": (7,)}
    for i_ in range(4):
        bk["pbias%d" % i_] = (4,)
        bk["phid%d" % i_] = (i_,)
        bk["psw%d" % i_] = (3 + i_ % 2,)
    for a_ in range(2):
        bk["po%d" % a_] = (4 + a_,)
        for b_ in range(2):
            bk["psc%d_%d" % (a_, b_)] = (a_ * 2 + b_,)
    for i_ in range(4):
        bk["pss%d" % i_] = ((0, 1, 2, 5)[i_],)
    S.bank_of = bk
    identb = A.alloc([128], BF16)
    utri = A.alloc([128], F32)
    tricb = A.alloc([128], BF16)
    triab = A.alloc([128], BF16)
    epsb = A.alloc([1], F32)
    oneb = A.alloc([1], F32)
    EPS_AP[0] = epsb
    KsA = A.alloc([SEQ], BF16)
    KwT = A.alloc([SEQ], BF16)
    kcvcT = A.alloc([SEQ], BF16)
    VsA = A.alloc([NTT, 65], BF16)
    VwA = A.alloc([NTT, 65], BF16)
    gate = A.alloc([NTT, 12], F32)
    hpb = A.alloc([12], F32)
    base_persist = A.off
    S.pool(lambda e: e.memset(epsb, EPS), writes=["eps"])
    S.pool(lambda e: e.memset(oneb, 1.0), writes=["one"])
    S.pool(lambda e: e.memset(VsA, 1.0), writes=["VsA"])
    S.pool(lambda e: e.memset(VwA, 1.0), writes=["VwA"])
    S.dma("pool", identb, T["ident"], writes=["ident"])
    S.dma("sp", utri, T["utri"], writes=["utri"])
    S.dma("pool", tricb, T["tric"], writes=["tric"])
    S.dma("pool", triab, T["tria"], writes=["tria"])
    S.dma("pool", KsA[64:128, :], T["emat"], writes=["KsE"])
    S.dma("sp", hpb, T["hp"].partition_broadcast(128), writes=["hpb"])

    wtm = A.alloc([KC, NA + NB_], BF16)
    wfm = A.alloc([KC, NFM], BF16)
    S.dma("pool", wfm, T["w_fm"].rearrange("(k p) n -> p k n", p=128), writes=["wfm"], c=30.0)
    S.dma("pool", wtm, T["w_tm"].rearrange("(k p) n -> p k n", p=128), writes=["wtm"], c=35.0)
    anwb = A.alloc([D], F32)
    S.dma("sp", anwb, T["anw"].partition_broadcast(128), writes=["anwb"])
    ropet = A.alloc([NTT, 16], F32)
    S.dma("sp", ropet, T["rope"].rearrange("p (t c) -> p t c", c=16), writes=["ropet"])
    convw = A.alloc([16], F32)
    convb = A.alloc([4], F32)
    S.dma("sp", convw, T["convw"], writes=["convw"])
    S.dma("sp", convb, T["convb"], writes=["convb"])
    onesf = A.alloc([128], F32)
    S.pool(lambda e: e.memset(onesf, 1.0), writes=["onesf"])
    two = lambda shape, dt: [A.alloc(shape, dt) for _ in range(2)]
    xt = two([D], F32)
    sq1 = A.alloc([D], BF16)
    sq = [sq1, sq1]
    ss = A.alloc([8], F32)
    ub = two([D], BF16)
    uT2 = two([KC, 512], BF16)
    cbuf = A.alloc([4, 515], F32)
    cacc_1 = A.alloc([512], F32)
    cacc = [cacc_1, cacc_1]
    xbcT = two([4, 512], BF16)
    zs = two([256], BF16)
    ez_1 = A.alloc([256], F32)
    ez = [ez_1, ez_1]
    ec = two([512], F32)
    dtt = two([4], F32)
    qk = two([6, 64], F32)
    qkr = two([6, 64], BF16)
    qkb = two([4, 64], BF16)
    rt = two([4, 6, 8], F32)
    qst = two([4, 128], BF16)
    qut = two([4, 128], BF16)
    xtm = two([256], BF16)
    btm = two([128], BF16)
    hst = A.alloc([256], F32)
    hstb = two([256], BF16)
    aneg = A.alloc([4], F32)
    adt = two([4], F32)
    acol = two([4], F32)
    nacol = two([4], F32)
    eac = two([4], F32)
    rhs4_1 = A.alloc([4, 128], F32)
    rhs4 = [rhs4_1, rhs4_1]
    seg4_1 = A.alloc([4, 128], F32)
    seg4 = [seg4_1, seg4_1]
    cbm = two([128], F32)
    MT = two([4, 128], BF16)
    alast = two([4], F32)
    dsv = two([4], F32)
    cdv = two([4], F32)
    wsc = two([4], F32)
    xw = two([256], BF16)
    xdt = two([256], BF16)
    ydg_1 = A.alloc([256], F32)
    ydg = [ydg_1, ydg_1]
    yy_1 = A.alloc([256], F32)
    yy = [yy_1, yy_1]
    tmpd_1 = A.alloc([256], F32)
    tmpd = [tmpd_1, tmpd_1]
    yo = two([256], F32)
    S.pool(lambda e: e.memset(cbuf, 0.0), writes=["cbuf", "cbuf0", "cbuf1", "cbuf2", "cbuf3"])
    S.pool(lambda e: e.memset(hst, 0.0), writes=["hst"])
    S.act(lambda e: e.activation(aneg, hpb[:, 4:8], AF.Exp), reads=["hpb"], writes=["aneg"])
    S.dve(lambda e: e.tensor_scalar(aneg, aneg, -1.0, None, ALU.mult), reads=["aneg"], writes=["aneg"])

    ptr = pbf(pbig, 0, 1)
    psA = pbig[:, 1, 0:NA]
    psB = pbig[:, 2, 0:NB_]
    psF = pbig[:, 3, :]
    psR = pbig[:, 4, :]
    psS = pbig[:, 5, :]
    psN = pbig[:, 6, 0:256]
    psY = pbig[:, 7, :]

    def load_x(tt):
        S.dma("sp", xt[tt % 2], T["xb"][tt * 128:(tt + 1) * 128, :], writes=["xt%d" % (tt % 2)], c=5.0)

    def b4(ap, n):
        return ap.unsqueeze(2).to_broadcast([128, 4, n])

    load_x(0)
    for G in range(SEQ // 512):
        gp = G % 2
        uT = uT2[gp]
        for j in range(4):
            tt = G * 4 + j
            b = tt % 2
            if tt + 1 < NTT:
                load_x(tt + 1)
            ssb = ss[:, b:b + 1]
            S.act(lambda e, b=b, ssb=ssb: e.activation(sq[b], xt[b], AF.Square, accum_out=ssb), reads=["xt%d" % b], writes=["n%dss" % b, "nsq"], c=1.9)
            S.act(lambda e, ssb=ssb: e.activation(ssb, ssb, AF.Ln, scale=1.0 / D, bias=epsb), reads=["n%dss" % b, "eps"], writes=["n%dss" % b])
            S.act(lambda e, ssb=ssb: e.activation(ssb, ssb, AF.Exp, scale=-0.5), reads=["n%dss" % b], writes=["n%dss" % b])
            S.dve(lambda e, b=b: e.scalar_tensor_tensor(ub[b], xt[b], ss[:, b:b + 1], anwb, ALU.mult, ALU.mult),
                  reads=["xt%d" % b, "n%dss" % b, "anwb"], writes=["ub%d" % b], c=2.2)
            for half in range(2):
                for k8 in range(8):
                    kc = half * 8 + k8
                    S.pe(lambda e, kc=kc, k8=k8, b=b: e.transpose(ptr[:, k8 * 128:(k8 + 1) * 128], ub[b][:, kc * 128:(kc + 1) * 128], identb),
                         reads=["ub%d" % b, "ident"], writes=["ptr"])
                if half == 0:
                    S.act(lambda e, j=j, uT=uT: e.copy(uT[:, 0:8, j * 128:(j + 1) * 128], ptr.rearrange("p (k n) -> p k n", k=8)), reads=["ptr"], writes=["uT%d_%d_0" % (gp, j)], c=0.9)
                else:
                    S.dve(lambda e, j=j, uT=uT: e.tensor_copy(uT[:, 8:16, j * 128:(j + 1) * 128], ptr.rearrange("p (k n) -> p k n", k=8)), reads=["ptr"], writes=["uT%d_%d_1" % (gp, j)], c=0.7)
        uTr = ["uT%d_%d_%d" % (gp, j, h) for j in range(4) for h in range(2)]
        for c in range(5):
            for kc in range(KC):
                S.pe(lambda e, c=c, kc=kc, uT=uT: e.matmul(psF, wfm[:, kc, c * 128:(c + 1) * 128], uT[:, kc, :], start=(kc == 0), stop=(kc == KC - 1)),
                     reads=uTr + ["wfm"], writes=["psF0"], c=0.22)
            if c < 4:
                ca = cacc[c % 2]
                car = "cacc"
                S.act(lambda e, c=c: e.copy(cbuf[:, c, 3:515], psF), reads=["psF0"], writes=["cbuf%d" % c], c=0.6)
                S.dve(lambda e, c=c, ca=ca: e.tensor_scalar(ca, cbuf[:, c, 0:512], convw[:, c * 4:c * 4 + 1], convb[:, c:c + 1], ALU.mult, ALU.add), reads=["cbuf%d" % c, "convw", "convb"], writes=[car], c=0.4)
                for k in range(1, 4):
                    S.dve(lambda e, c=c, k=k, ca=ca: e.scalar_tensor_tensor(ca, cbuf[:, c, k:k + 512], convw[:, c * 4 + k:c * 4 + k + 1], ca, ALU.mult, ALU.add),
                          reads=["cbuf%d" % c, car, "convw"], writes=[car], c=0.65)
                ece = ec[c % 2]
                ecr = "ec%d" % (c % 2)
                S.act(lambda e, ca=ca, ece=ece: e.activation(ece, ca, AF.Exp, scale=-1.0), reads=[car], writes=[ecr], c=0.6)
                S.act(lambda e, ece=ece: e.activation(ece, ece, AF.Ln, bias=oneb), reads=[ecr, "one"], writes=[ecr], c=0.6)
                S.act(lambda e, ece=ece: e.activation(ece, ece, AF.Exp, scale=-1.0), reads=[ecr], writes=[ecr], c=0.6)
                S.pool(lambda e, c=c, ca=ca, ece=ece, gp=gp: e.tensor_tensor(xbcT[gp][:, c, :], ca, ece, ALU.mult), reads=[car, ecr], writes=["xbcT%d_%d" % (gp, c)], c=2.0)
                S.pool(lambda e, c=c: e.tensor_copy(cbuf[:, c, 0:3], cbuf[:, c, 512:515]), reads=["cbuf%d" % c], writes=["cbuf%d" % c])
            else:
                S.act(lambda e, G=G: e.copy(kcvcT[:, G * 512:(G + 1) * 512], psF), reads=["psF0"], writes=["kcvcT"], c=0.6)
        xr = lambda c: "xbcT%d_%d" % (gp, c)
        for j in range(4):
            tt = G * 4 + j
            p = tt % 2
            P = str(p)
            tok = slice(tt * 128, (tt + 1) * 128)
            js = slice(j * 128, (j + 1) * 128)
            for kc in range(KC):
                S.pe(lambda e, js=js, kc=kc, uT=uT: e.matmul(psA, uT[:, kc, js], wtm[:, kc, 0:NA], start=(kc == 0), stop=(kc == KC - 1)),
                     reads=uTr + ["wtm"], writes=["psA"], c=0.19)
            for kc in range(KC):
                S.pe(lambda e, js=js, kc=kc, uT=uT: e.matmul(psB, uT[:, kc, js], wtm[:, kc, NA:NA + NB_], start=(kc == 0), stop=(kc == KC - 1)),
                     reads=uTr + ["wtm"], writes=["psB"], c=0.18)
            S.act(lambda e, p=p: e.activation(ez[p], psA[:, 0:256], AF.Exp, scale=-1.0), reads=["psA"], writes=["ez"])
            S.act(lambda e, p=p: e.activation(ez[p], ez[p], AF.Ln, bias=oneb), reads=["ez", "one"], writes=["ez"])
            S.act(lambda e, p=p: e.activation(ez[p], ez[p], AF.Exp, scale=-1.0), reads=["ez"], writes=["ez"])
            S.dve(lambda e, p=p: e.tensor_tensor(zs[p], psA[:, 0:256], ez[p], ALU.mult), reads=["psA", "ez"], writes=["zs" + P])
            S.dve(lambda e, p=p: e.tensor_tensor(dtt[p], psA[:, 256:260], hpb[:, 0:4], ALU.add), reads=["psA", "hpb"], writes=["dtt" + P])
            S.act(lambda e, p=p: e.activation(dtt[p], dtt[p], AF.Exp), reads=["dtt" + P], writes=["dtt" + P])
            S.act(lambda e, p=p: e.activation(dtt[p], dtt[p], AF.Ln, bias=oneb), reads=["dtt" + P, "one"], writes=["dtt" + P])
            S.act(lambda e, tt=tt: e.activation(gate[:, tt, :], psA[:, 260:272], AF.Exp, scale=-1.0), reads=["psA"], writes=["gate%d" % tt])
            S.dve(lambda e, tt=tt: e.tensor_scalar(gate[:, tt, :], gate[:, tt, :], 1.0, None, ALU.add), reads=["gate%d" % tt], writes=["gate%d" % tt])
            S.dve(lambda e, tt=tt: e.reciprocal(gate[:, tt, :], gate[:, tt, :]), reads=["gate%d" % tt], writes=["gate%d" % tt])
            S.dve(lambda e, tt=tt: e.tensor_copy(VsA[:, tt, 0:64], psA[:, 272:336]), reads=["psA", "VsA"], writes=["VsA%d" % tt])
            S.dve(lambda e, tt=tt: e.tensor_copy(VwA[:, tt, 0:64], psA[:, 336:400]), reads=["psA", "VwA"], writes=["VwA%d" % tt])
            S.act(lambda e, p=p: e.copy(qk[p], psB.rearrange("p (a b) -> p a b", a=6)), reads=["psB"], writes=["qk" + P], c=0.5)
            S.pool(lambda e, p=p: e.tensor_copy(qkb[p], qk[p][:, 0:4, :]), reads=["qk" + P], writes=["qkb" + P])
            S.pool(lambda e, p=p: e.tensor_copy(qkr[p], qk[p]), reads=["qk" + P], writes=["qkr" + P])
            cosb = ropet[:, tt, 0:8].unsqueeze(1).to_broadcast([128, 6, 8])
            sinb = ropet[:, tt, 8:16].unsqueeze(1).to_broadcast([128, 6, 8])
            S.dve(lambda e, cosb=cosb, p=p: e.tensor_tensor(rt[p][:, 0], qk[p][:, :, 0:8], cosb, ALU.mult), reads=["qk" + P, "ropet"], writes=["rt0" + P])
            S.dve(lambda e, sinb=sinb, p=p: e.tensor_tensor(rt[p][:, 1], qk[p][:, :, 8:16], sinb, ALU.mult), reads=["qk" + P, "ropet"], writes=["rt1" + P])
            S.dve(lambda e, cosb=cosb, p=p: e.tensor_tensor(rt[p][:, 2], qk[p][:, :, 8:16], cosb, ALU.mult), reads=["qk" + P, "ropet"], writes=["rt2" + P])
            S.dve(lambda e, sinb=sinb, p=p: e.tensor_tensor(rt[p][:, 3], qk[p][:, :, 0:8], sinb, ALU.mult), reads=["qk" + P, "ropet"], writes=["rt3" + P])
            S.dve(lambda e, p=p: e.tensor_tensor(qkr[p][:, :, 0:8], rt[p][:, 0], rt[p][:, 1], ALU.subtract), reads=["rt0" + P, "rt1" + P, "qkr" + P], writes=["qkr" + P])
            S.dve(lambda e, p=p: e.tensor_tensor(qkr[p][:, :, 8:16], rt[p][:, 2], rt[p][:, 3], ALU.add), reads=["rt2" + P, "rt3" + P, "qkr" + P], writes=["qkr" + P])
            ptq = ptr[0:64, 0:768].rearrange("p (a b) -> p a b", a=6)
            ptu = pbf(pbig, 6, 7)[0:64, 512:1024].rearrange("p (a b) -> p a b", a=4)
            for a in range(6):
                S.pe(lambda e, a=a, p=p: e.transpose(ptq[:, a, :], qkr[p][:, a, :], identb), reads=["qkr" + P, "ident"], writes=["ptr"])
            for a in range(4):
                S.pe(lambda e, a=a, p=p: e.transpose(ptu[:, a, :], qkb[p][:, a, :], identb), reads=["qkb" + P, "ident"], writes=["ptr2"])
            S.act(lambda e, p=p: e.copy(qst[p][0:64], ptq[:, 0:4, :]), reads=["ptr"], writes=["qst" + P])
            S.dve(lambda e, p=p: e.tensor_copy(qut[p][0:64], ptu), reads=["ptr2"], writes=["qut" + P])
            S.act(lambda e, tok=tok: e.copy(KsA[0:64, tok], ptq[:, 4, :]), reads=["ptr"], writes=["KsA%d" % tt])
            S.dve(lambda e, tok=tok: e.tensor_copy(KwT[0:64, tok], ptq[:, 5, :]), reads=["ptr"], writes=["KwT%d" % tt])
            S.dma("sp", T["qr_d"][:, :, tok], qst[p][0:64], reads=["qst" + P], writes=["qr_d%d" % tt])
            S.dma("sp", T["qu_d"][:, :, tok], qut[p][0:64], reads=["qut" + P], writes=["qu_d%d" % tt])
            ptx = ptr[:, 0:384]
            for c in range(3):
                S.pe(lambda e, c=c, js=js, gp=gp: e.transpose(ptx[:, c * 128:(c + 1) * 128], xbcT[gp][:, c, js], identb),
                     reads=[xr(c), "ident"], writes=["ptr"])
            S.act(lambda e, p=p: e.copy(xtm[p], ptx[:, 0:256]), reads=["ptr"], writes=["xtm" + P])
            S.act(lambda e, p=p: e.copy(btm[p], ptx[:, 256:384]), reads=["ptr"], writes=["btm" + P])
            S.dve(lambda e, p=p: e.tensor_tensor(adt[p], dtt[p], aneg, ALU.mult), reads=["dtt" + P, "aneg"], writes=["adt" + P])
            S.pe(lambda e, p=p: e.matmul(psS[:, 0:4], utri, adt[p], start=True, stop=True), reads=["utri", "adt" + P], writes=["psS_a"])
            S.act(lambda e, p=p: e.copy(acol[p], psS[:, 0:4]), reads=["psS_a"], writes=["acol" + P])
            S.dve(lambda e, p=p: e.tensor_scalar(nacol[p], psS[:, 0:4], -1.0, None, ALU.mult), reads=["psS_a"], writes=["nacol" + P])
            S.act(lambda e, p=p: e.activation(eac[p], acol[p], AF.Exp), reads=["acol" + P], writes=["eac" + P])
            S.pe(lambda e, js=js, gp=gp: e.matmul(psS[:, 256:384], xbcT[gp][:, 2, js], xbcT[gp][:, 3, js], start=True, stop=True),
                 reads=[xr(2), xr(3)], writes=["psS_c"])
            S.dve(lambda e, p=p: e.tensor_tensor(cbm[p], psS[:, 256:384], utri, ALU.mult), reads=["psS_c", "utri"], writes=["cbm" + P])
            S.dve(lambda e, p=p: e.tensor_tensor(rhs4[p], utri.unsqueeze(1).to_broadcast([128, 4, 128]), b4(adt[p], 128), ALU.mult),
                  reads=["utri", "adt" + P], writes=["rhs4"], c=0.6)
            S.pe(lambda e, p=p: e.matmul(psR, onesf, rhs4[p].rearrange("p a b -> p (a b)"), start=True, stop=True), reads=["onesf", "rhs4"], writes=["psR"], c=0.9)
            psR4 = psR.rearrange("p (a b) -> p a b", a=4)
            S.dve(lambda e, p=p: e.tensor_tensor(seg4[p], psR4, b4(acol[p], 128), ALU.subtract), reads=["psR", "acol" + P], writes=["seg4"], c=0.7)
            S.dve(lambda e, p=p: e.tensor_scalar(seg4[p], seg4[p], 0.0, None, ALU.min), reads=["seg4"], writes=["seg4"], c=0.35)
            S.act(lambda e, p=p: e.activation(seg4[p], seg4[p], AF.Exp), reads=["seg4"], writes=["seg4"], c=0.6)
            S.dve(lambda e, p=p: e.tensor_tensor(MT[p], seg4[p], cbm[p].unsqueeze(1).to_broadcast([128, 4, 128]), ALU.mult),
                  reads=["seg4", "cbm" + P], writes=["MT" + P], c=0.6)
            S.dve(lambda e, p=p: e.tensor_copy(alast[p], psR4[:, :, 127]), reads=["psR"], writes=["alast" + P])
            S.dve(lambda e, p=p: e.tensor_tensor(dsv[p], nacol[p], alast[p], ALU.add), reads=["nacol" + P, "alast" + P], writes=["dsv" + P])
            S.act(lambda e, p=p: e.activation(dsv[p], dsv[p], AF.Exp), reads=["dsv" + P], writes=["dsv" + P])
            S.act(lambda e, p=p: e.activation(cdv[p], alast[p], AF.Exp), reads=["alast" + P], writes=["cdv" + P])
            S.dve(lambda e, p=p: e.tensor_tensor(wsc[p], dtt[p], dsv[p], ALU.mult), reads=["dtt" + P, "dsv" + P], writes=["wsc" + P])
            v4 = lambda ap: ap.rearrange("p (h d) -> p h d", h=4)
            S.dve(lambda e, p=p: e.tensor_tensor(v4(xw[p]), v4(xtm[p]), b4(wsc[p], 64), ALU.mult), reads=["xtm" + P, "wsc" + P], writes=["xw" + P])
            S.dve(lambda e, p=p: e.tensor_tensor(v4(xdt[p]), v4(xtm[p]), b4(dtt[p], 64), ALU.mult), reads=["xtm" + P, "dtt" + P], writes=["xdt" + P])
            S.pool(lambda e, p=p: e.tensor_copy(hstb[p], hst), reads=["hst"], writes=["hstb" + P])
            S.pe(lambda e, js=js, gp=gp, p=p: e.matmul(psY[:, 0:256], xbcT[gp][:, 3, js], hstb[p], start=True, stop=True), reads=[xr(3), "hstb" + P], writes=["psY_o"])
            for h in range(4):
                S.pe(lambda e, h=h, p=p: e.matmul(psY[:, 256 + h * 64:256 + (h + 1) * 64], MT[p][:, h, :], xdt[p][:, h * 64:(h + 1) * 64], start=True, stop=True),
                     reads=["MT" + P, "xdt" + P], writes=["psY_d"])
            S.pe(lambda e, p=p: e.matmul(psN, btm[p], xw[p], start=True, stop=True), reads=["btm" + P, "xw" + P], writes=["psN"])
            S.dve(lambda e, p=p: e.tensor_tensor(v4(hst), v4(hst), b4(cdv[p], 64), ALU.mult), reads=["hst", "cdv" + P], writes=["hst"])
            S.dve(lambda e: e.tensor_tensor(hst, hst, psN, ALU.add), reads=["hst", "psN"], writes=["hst"])
            S.act(lambda e, p=p: e.copy(ydg[p], psY[:, 256:512]), reads=["psY_d"], writes=["ydg"])
            S.dve(lambda e, p=p: e.tensor_tensor(v4(yy[p]), v4(psY[:, 0:256]), b4(eac[p], 64), ALU.mult), reads=["psY_o", "eac" + P], writes=["yy"])
            S.pool(lambda e, p=p: e.tensor_tensor(v4(tmpd[p]), v4(xtm[p]), b4(hpb[:, 8:12], 64), ALU.mult), reads=["xtm" + P, "hpb"], writes=["tmpd"])
            S.dve(lambda e, p=p: e.tensor_tensor(yy[p], yy[p], ydg[p], ALU.add), reads=["yy", "ydg"], writes=["yy"])
            S.dve(lambda e, p=p: e.tensor_tensor(yy[p], yy[p], tmpd[p], ALU.add), reads=["yy", "tmpd"], writes=["yy"])
            S.dve(lambda e, p=p: e.tensor_tensor(yo[p], yy[p], zs[p], ALU.mult), reads=["yy", "zs" + P], writes=["yo" + P])
            S.dma("sp", T["mixo"][tok, 0:256], yo[p], reads=["yo" + P], writes=["mixo_s%d" % tt])

    if int(os.environ.get('MIX_STOP', '9')) <= 1:
        return
    bar()
    A.reset(base_persist)
    attO = A.alloc([NTT, 256], F32)
    nmT = A.alloc([SEQ], BF16)
    w1 = A.alloc([32, 256], BF16)
    S.dma("pool", w1[0:64], T["w1k"].rearrange("d (l h) -> d l h", l=32), writes=["w1k"])
    S.dma("pool", w1[64:128], T["w1v"].rearrange("d (l h) -> d l h", l=32), writes=["w1v"])
    pe_ = A.alloc([32], BF16)
    S.dma("pool", pe_[0:64], T["pek"], writes=["pek"])
    S.dma("pool", pe_[64:128], T["pev"], writes=["pev"])
    w2 = A.alloc([2, 2, 64], BF16)
    S.dma("pool", w2[:, 0], T["w2k"].rearrange("(c p) d -> p c d", p=128), writes=["w2k"])
    S.dma("pool", w2[:, 1], T["w2v"].rearrange("(c p) d -> p c d", p=128), writes=["w2v"])
    cbias = A.alloc([4], F32)
    hsb = A.alloc([4, 256], BF16)
    KcT = A.alloc([256], BF16)
    VcA = A.alloc([2, 129], BF16)
    S.pool(lambda e: e.memset(hsb, 0.0), writes=["hsb"])
    S.pool(lambda e: e.memset(VcA, 0.0), writes=["VcA"])
    S.pool(lambda e: e.memset(KcT, 0.0), writes=["KcT"])
    S.pool(lambda e: e.memset(VcA[:, :, 64:65], 1.0), reads=["VcA"], writes=["VcA"])
    S.dma("pool", VcA[:, :, 65:129], T["ovl"].rearrange("(c p) j -> p c j", p=128), reads=["VcA"], writes=["VcA"])
    for kv in range(2):
        rows = slice(kv * 64, (kv + 1) * 64)
        for hc in range(2):
            idx = kv * 2 + hc
            pb_ = pbig[:, idx, 0:255]
            pbias = pbig[:, 4, idx:idx + 1]
            for l in range(32):
                S.pe(lambda e, rows=rows, hc=hc, l=l, pbias=pbias: e.matmul(pbias, w1[rows, l, hc * 128:(hc + 1) * 128], pe_[rows, l:l + 1], start=(l == 0), stop=(l == 31)),
                     reads=["w1k", "w1v", "pek", "pev"], writes=["pbias%d" % idx])
            S.act(lambda e, idx=idx, pbias=pbias: e.copy(cbias[:, idx:idx + 1], pbias), reads=["pbias%d" % idx], writes=["cbias%d" % idx])
            for l in range(32):
                S.pe(lambda e, rows=rows, hc=hc, l=l, pb_=pb_: e.matmul(pb_, w1[rows, l, hc * 128:(hc + 1) * 128], kcvcT[rows, l:l + 16 * 254 + 1:16], start=(l == 0), stop=(l == 31)),
                     reads=["w1k", "w1v", "kcvcT"], writes=["phid%d" % idx])
            S.act(lambda e, idx=idx, pb_=pb_: e.activation(hsb[:, idx, 0:255], pb_, AF.Silu, bias=cbias[:, idx:idx + 1]), reads=["phid%d" % idx, "cbias%d" % idx, "hsb"], writes=["hsb%d" % idx])
    pk = pbig[0:64, 5, 0:255]
    for hc in range(2):
        S.pe(lambda e, hc=hc: e.matmul(pk, w2[:, 0, hc, :], hsb[:, hc, 0:255], start=(hc == 0), stop=(hc == 1)), reads=["w2k", "hsb0", "hsb1"], writes=["pk"])
    S.act(lambda e: e.copy(KcT[0:64, 0:255], pk), reads=["pk", "KcT"], writes=["KcT"])
    for it in range(2):
        m = 128 if it == 0 else 127
        pv = pbig[0:m, 6 + it, 0:64]
        for hc in range(2):
            S.pe(lambda e, it=it, hc=hc, m=m, pv=pv: e.matmul(pv, hsb[:, 2 + hc, it * 128:it * 128 + m], w2[:, 1, hc, :], start=(hc == 0), stop=(hc == 1)),
                 reads=["w2v", "hsb2", "hsb3"], writes=["pv%d" % it])
        S.act(lambda e, it=it, m=m, pv=pv: e.copy(VcA[0:m, it, 0:64], pv), reads=["pv%d" % it, "VcA"], writes=["VcA"])

    if int(os.environ.get('MIX_STOP', '9')) <= 2:
        return
    bar()
    base3 = A.off
    qu = [A.alloc([4, 512], BF16) for _ in range(2)]
    cmk = [A.alloc([2, 512], BF16) for _ in range(2)]
    PcT = [A.alloc([2, 512], BF16) for _ in range(2)]
    imp = A.alloc([4, 64], F32)
    imp2 = A.alloc([64], F32)
    m8 = A.alloc([16], F32)
    thr = A.alloc([1], F32)
    rr = A.alloc([2], F32)
    nmb = A.alloc([128], BF16)
    S.pool(lambda e: e.memset(nmb, 0.0), writes=["nmb"])
    pnt = pbf(pbig, 7, 8)[:, 0:128]
    for Q in range(8):
        qb = Q % 2
        qs = slice(Q * 512, (Q + 1) * 512)
        S.dma("sp", qu[qb][0:64], T["qu_d"][:, :, qs], writes=["qu%d" % qb])
        S.dma("pool", cmk[qb], T["cmask"][:, qs].rearrange("(c p) t -> p c t", p=128), writes=["cmk%d" % qb])
        for h in range(4):
            pb2 = h % 2
            for it in range(2):
                ps_ = pbig[:, pb2 * 2 + it, :]
                S.pe(lambda e, it=it, h=h, qb=qb, ps_=ps_: e.matmul(ps_, KcT[0:64, it * 128:(it + 1) * 128], qu[qb][0:64, h, :], start=True, stop=False),
                     reads=["KcT", "qu%d" % qb], writes=["psc%d_%d" % (pb2, it)])
                S.pe(lambda e, it=it, qb=qb, ps_=ps_: e.matmul(ps_, identb, cmk[qb][:, it, :], start=False, stop=True),
                     reads=["ident", "cmk%d" % qb], writes=["psc%d_%d" % (pb2, it)])
                S.act(lambda e, it=it, pb2=pb2, ps_=ps_: e.activation(PcT[pb2][:, it, :], ps_, AF.Exp, scale=SCALE), reads=["psc%d_%d" % (pb2, it)], writes=["PcT%d_%d" % (pb2, it)])
            for sub in range(4):
                tt = Q * 4 + sub
                po = pbig[:, 4 + (sub % 2), 0:129]
                for it in range(2):
                    S.pe(lambda e, it=it, pb2=pb2, sub=sub, po=po: e.matmul(po, PcT[pb2][:, it, sub * 128:(sub + 1) * 128], VcA[:, it, :], start=(it == 0), stop=(it == 1)),
                         reads=["PcT%d_0" % pb2, "PcT%d_1" % pb2, "VcA"], writes=["po%d" % (sub % 2)])
                pr = ["po%d" % (sub % 2)]
                S.dve(lambda e, po=po: e.tensor_scalar(rr[:, 0:1], po[:, 64:65], 1e-30, None, ALU.add), reads=pr, writes=["rr0"])
                S.dve(lambda e: e.reciprocal(rr[:, 0:1], rr[:, 0:1]), reads=["rr0"], writes=["rr0"])
                if h == 0:
                    S.dve(lambda e, po=po, sub=sub: e.tensor_scalar(imp[:, sub, :], po[:, 65:129], rr[:, 0:1], None, ALU.mult), reads=pr + ["rr0"], writes=["imp%d" % sub])
                else:
                    S.dve(lambda e, po=po, sub=sub: e.scalar_tensor_tensor(imp[:, sub, :], po[:, 65:129], rr[:, 0:1], imp[:, sub, :], ALU.mult, ALU.add),
                          reads=pr + ["rr0", "imp%d" % sub], writes=["imp%d" % sub])
                S.dve(lambda e, tt=tt, h=h: e.tensor_tensor(rr[:, 1:2], rr[:, 0:1], gate[:, tt, h * 3:h * 3 + 1], ALU.mult), reads=["rr0", "gate"], writes=["rr1"])
                S.dve(lambda e, po=po, tt=tt, h=h: e.tensor_scalar(attO[:, tt, h * 64:(h + 1) * 64], po[:, 0:64], rr[:, 1:2], None, ALU.mult), reads=pr + ["rr1"], writes=["attO%d" % tt])
        for sub in range(4):
            tt = Q * 4 + sub
            ir = ["imp%d" % sub]
            im = imp[:, sub, :]
            S.pool(lambda e, im=im: e.memset(im[:, 0:1], 1e4), reads=ir, writes=ir)
            lo = max(2 * tt - 1, 0)
            S.pool(lambda e, im=im, lo=lo, tt=tt: e.memset(im[0:64, lo:2 * tt + 1], 1e4), reads=ir, writes=ir)
            S.pool(lambda e, im=im, tt=tt: e.memset(im[64:128, 2 * tt:2 * tt + 2], 1e4), reads=ir, writes=ir)
            if 2 * tt + 1 < 64:
                S.pool(lambda e, im=im, tt=tt: e.memset(im[0:64, 2 * tt + 1:64], -1.0), reads=ir, writes=ir)
            if 2 * tt + 2 < 64:
                S.pool(lambda e, im=im, tt=tt: e.memset(im[64:128, 2 * tt + 2:64], -1.0), reads=ir, writes=ir)
            S.dve(lambda e, im=im: e.max(m8[:, 0:8], im), reads=ir, writes=["m8a"])
            S.dve(lambda e, im=im: e.match_replace(imp2, m8[:, 0:8], im, -1e30), reads=ir + ["m8a"], writes=["imp2"])
            S.dve(lambda e: e.max(m8[:, 8:16], imp2), reads=["imp2"], writes=["m8b"])
            S.dve(lambda e: e.tensor_scalar(thr, m8[:, 15:16], 0.0, None, ALU.max), reads=["m8b"], writes=["thr"])
            S.dve(lambda e, im=im: e.tensor_scalar(nmb[:, 64:128], im, thr, NEG, ALU.is_lt, ALU.mult), reads=ir + ["thr", "nmb"], writes=["nmb"])
            S.pe(lambda e: e.transpose(pnt, nmb, identb), reads=["nmb", "ident"], writes=["pnt"])
            S.act(lambda e, tt=tt: e.copy(nmT[64:128, tt * 128:(tt + 1) * 128], pnt[64:128, :]), reads=["pnt"], writes=["nmT"])

    if int(os.environ.get('MIX_STOP', '9')) <= 3:
        return
    bar()
    A.reset(base3)
    Qa = [A.alloc([4, 512], BF16) for _ in range(2)]
    PT = [A.alloc([512], BF16) for _ in range(4)]
    PW = [A.alloc([512], BF16) for _ in range(3)]
    r4 = A.alloc([2], F32)
    pti = 0
    pwi = 0
    for Q in range(8):
        qb = Q % 2
        qs = slice(Q * 512, (Q + 1) * 512)
        S.dma("sp", Qa[qb][0:64], T["qr_d"][:, :, qs], writes=["Qa%d" % qb])
        for h in range(4):
            S.pool(lambda e, qb=qb, h=h, qs=qs: e.tensor_copy(Qa[qb][64:128, h, :], nmT[64:128, qs]), reads=["nmT"], writes=["Qm%d_%d" % (qb, h)])
        for h in range(4):
            qr_ = ["Qa%d" % qb, "Qm%d_%d" % (qb, h)]
            posel = pbig[:, 6, 0:260].rearrange("p (s c) -> p s c", s=4)
            powin = pbig[:, 7, 0:260].rearrange("p (s c) -> p s c", s=4)
            S.dve(lambda e: e.memset(pbig[:, 6, 0:260], 0.0), writes=["posel"])
            for kt in range(4 * Q + 4):
                ks_ = slice(kt * 128, (kt + 1) * 128)
                sb_ = kt % 4
                ps_ = pbig[:, (0, 1, 2, 5)[sb_], :]
                pres = "pss%d" % sb_
                o = kt - 4 * Q
                if o < 0:
                    S.pe(lambda e, ks_=ks_, qb=qb, h=h, ps_=ps_: e.matmul(ps_, KsA[:, ks_], Qa[qb][:, h, :], start=True, stop=True),
                         reads=["KsA", "KsE"] + qr_, writes=[pres])
                    lo = 0
                else:
                    lo = o * 128
                    S.pe(lambda e, ks_=ks_, qb=qb, h=h, ps_=ps_, lo=lo: e.matmul(ps_[:, lo:lo + 128], KsA[:, ks_], Qa[qb][:, h, lo:lo + 128], start=True, stop=False),
                         reads=["KsA", "KsE"] + qr_, writes=[pres])
                    S.pe(lambda e, ps_=ps_, lo=lo: e.matmul(ps_[:, lo:lo + 128], identb, tricb, start=False, stop=True), reads=["ident", "tric"], writes=[pres])
                    if o < 3:
                        S.pe(lambda e, ks_=ks_, qb=qb, h=h, ps_=ps_, lo=lo: e.matmul(ps_[:, lo + 128:512], KsA[:, ks_], Qa[qb][:, h, lo + 128:512], start=True, stop=True),
                             reads=["KsA", "KsE"] + qr_, writes=[pres])
                pt_ = PT[pti % 4]
                ptres = "PT%d" % (pti % 4)
                pti += 1
                S.act(lambda e, ps_=ps_, pt_=pt_, lo=lo: e.activation(pt_[:, lo:512], ps_[:, lo:512], AF.Exp, scale=SCALE), reads=[pres], writes=[ptres])
                for sub in range(max(o, 0), 4):
                    S.pe(lambda e, pt_=pt_, sub=sub, kt=kt, Q=Q, posel=posel: e.matmul(posel[:, sub, :], pt_[:, sub * 128:(sub + 1) * 128], VsA[:, kt, :],
                                                                                 start=False, stop=False, skip_group_check=True),
                         reads=[ptres, "VsA"], writes=["posel"])
            S.dve(lambda e: e.memset(pbig[:, 7, 0:260], 0.0), writes=["powin"])
            for r in range(-4, 4):
                kt = 4 * Q + r
                if kt < 0:
                    continue
                ks_ = slice(kt * 128, (kt + 1) * 128)
                s_lo, s_hi = max(r, 0), min(r + 4, 3)
                wsl = pwi % 2
                psw = pbig[:, 3 + wsl, :]
                pwres = "psw%d" % wsl
                pw_ = PW[pwi % 3]
                pwr = "PW%d" % (pwi % 3)
                pwi += 1
                plain = [s_ for s_ in range(s_lo, s_hi + 1) if s_ != r and s_ != r + 4]
                for s_, msk in ((r, tricb), (r + 4, triab)):
                    if s_lo <= s_ <= s_hi:
                        cs = slice(s_ * 128, (s_ + 1) * 128)
                        S.pe(lambda e, ks_=ks_, qb=qb, h=h, cs=cs, psw=psw: e.matmul(psw[:, cs], KwT[0:64, ks_], Qa[qb][0:64, h, cs], start=True, stop=False),
                             reads=["KwT", "Qa%d" % qb], writes=[pwres])
                        S.pe(lambda e, cs=cs, psw=psw, msk=msk: e.matmul(psw[:, cs], identb, msk, start=False, stop=True), reads=["ident", "tric", "tria"], writes=[pwres])
                if plain:
                    cs = slice(plain[0] * 128, (plain[-1] + 1) * 128)
                    S.pe(lambda e, ks_=ks_, qb=qb, h=h, cs=cs, psw=psw: e.matmul(psw[:, cs], KwT[0:64, ks_], Qa[qb][0:64, h, cs], start=True, stop=True),
                         reads=["KwT", "Qa%d" % qb], writes=[pwres], c=0.2)
                ca = slice(s_lo * 128, (s_hi + 1) * 128)
                S.act(lambda e, psw=psw, pw_=pw_, ca=ca: e.activation(pw_[:, ca], psw[:, ca], AF.Exp, scale=SCALE), reads=[pwres], writes=[pwr], c=0.5)
                for s_ in range(s_lo, s_hi + 1):
                    S.pe(lambda e, pw_=pw_, s_=s_, kt=kt, powin=powin: e.matmul(powin[:, s_, :], pw_[:, s_ * 128:(s_ + 1) * 128], VwA[:, kt, :],
                                                                                start=False, stop=False, skip_group_check=True),
                         reads=[pwr, "VwA"], writes=["powin"])
            for sub in range(4):
                tt = 4 * Q + sub
                for br, (po_, pres) in enumerate(((posel, "posel"), (powin, "powin"))):
                    S.dve(lambda e, po_=po_, sub=sub, br=br: e.reciprocal(r4[:, br:br + 1], po_[:, sub, 64:65]), reads=[pres], writes=["r4_%d" % br])
                    S.dve(lambda e, tt=tt, h=h, br=br: e.tensor_tensor(r4[:, br:br + 1], r4[:, br:br + 1], gate[:, tt, h * 3 + 1 + br:h * 3 + 2 + br], ALU.mult),
                          reads=["r4_%d" % br, "gate"], writes=["r4_%d" % br])
                    S.dve(lambda e, po_=po_, sub=sub, tt=tt, h=h, br=br: e.scalar_tensor_tensor(attO[:, tt, h * 64:(h + 1) * 64], po_[:, sub, 0:64], r4[:, br:br + 1],
                                                                                                  attO[:, tt, h * 64:(h + 1) * 64], ALU.mult, ALU.add),
                          reads=[pres, "r4_%d" % br, "attO%d" % tt], writes=["attO%d" % tt])
        for sub in range(4):
            tt = 4 * Q + sub
            S.dma("sp", T["mixo"][tt * 128:(tt + 1) * 128, 256:512], attO[:, tt, :], reads=["attO%d" % tt])


def _perm_cols():
    return None


def run_mix(inputs):
    if "mix" not in _CACHE:
        _CACHE["mix"] = build_mix()
    nc = _CACHE["mix"]
    x = inputs["x"]
    w_in = inputs["w_in"][0]
    offs = np.cumsum([0, 1024, 1536, 16, 1024, 256, 256, 256, 256, 256, 256, 48])
    oz, oxbc, odt, oq, okc, ovc, oks, ovs, okw, ovw, ogate = offs[:11]
    conv_w = inputs["conv_w"][0]
    conv_b = inputs["conv_b"][0]
    t = np.arange(SEQ, dtype=np.float32)
    inv = (1.0 / (500000.0 ** (np.arange(0, 16, 2, dtype=np.float32) / np.float32(16)))).astype(np.float32)
    ang = (t[:, None] * inv[None, :]).astype(np.float32)
    rope = np.concatenate([np.cos(ang), np.sin(ang)], 1).astype(np.float32)
    rope = np.ascontiguousarray(rope.reshape(NTT, 128, 16).transpose(1, 0, 2).reshape(128, NTT * 16))
    ident = np.eye(128, dtype=np.float32)
    kk = np.arange(128)[:, None]
    qq = np.arange(128)[None, :]
    tric = np.where(kk <= qq, 0.0, NEG).astype(np.float32)
    tria = np.where(kk > qq, 0.0, NEG).astype(np.float32)
    utri = (kk <= qq).astype(np.float32)
    emat = (np.arange(SEQ)[None, :] // 64 == np.arange(64)[:, None]).astype(np.float32)
    ii = np.arange(256)[:, None]
    cmask = np.where((16 * ii + 31 <= np.arange(SEQ)[None, :]) & (ii < 255), 0.0, NEG).astype(np.float32)
    cs = np.arange(255)[:, None] * 16
    ss_ = np.arange(64)[None, :] * 64
    ov = np.clip(np.minimum(cs + 32, ss_ + 64) - np.maximum(cs, ss_), 0, None) / 32.0
    ovl = np.zeros((256, 64), np.float32)
    ovl[:255] = ov
    in_maps = []
    for c in range(8):
        b, g = c // 4, c % 4
        grp = g // 2
        ar = np.arange
        tm_cols = np.concatenate([oz + 256 * g + ar(256), odt + 4 * g + ar(4), ogate + 12 * g + ar(12), ovs + 64 * g + ar(64), ovw + 64 * g + ar(64),
                                  oq + 256 * g + ar(256), oks + 64 * g + ar(64), okw + 64 * g + ar(64)])
        xcols = np.concatenate([256 * g + ar(256), 1024 + 128 * grp + ar(128), 1280 + 128 * grp + ar(128)])
        fm_cols = np.concatenate([oxbc + xcols, okc + 64 * g + ar(64), ovc + 64 * g + ar(64)])
        convw = np.ascontiguousarray(conv_w[:, xcols].T.reshape(4, 128, 4).transpose(1, 0, 2).reshape(128, 16))
        convb = np.ascontiguousarray(conv_b[xcols].reshape(4, 128).T)
        hp = np.concatenate([inputs["dt_bias"][0][4 * g:4 * g + 4], inputs["a_log"][0][4 * g:4 * g + 4], inputs["d_skip"][0][4 * g:4 * g + 4]]).astype(np.float32)
        in_maps.append(dict(
            xb=np.ascontiguousarray(x[b]), anw=inputs["attn_norm_w"][0], w_tm=np.ascontiguousarray(w_in[:, tm_cols]), w_fm=np.ascontiguousarray(w_in[:, fm_cols]),
            convw=convw, convb=convb, hp=hp, rope=rope,
            w1k=np.ascontiguousarray(inputs["cmp_w1_k"][0].reshape(32, 64, 256).transpose(1, 0, 2).reshape(64, 32 * 256)),
            w1v=np.ascontiguousarray(inputs["cmp_w1_v"][0].reshape(32, 64, 256).transpose(1, 0, 2).reshape(64, 32 * 256)),
            w2k=inputs["cmp_w2_k"][0], w2v=inputs["cmp_w2_v"][0],
            pek=np.ascontiguousarray(inputs["cmp_pe_k"][0].T), pev=np.ascontiguousarray(inputs["cmp_pe_v"][0].T),
            ident=ident, emat=emat, tric=tric, tria=tria, cmask=cmask, ovl=ovl, utri=utri,
        ))
    res = run_bass_kernel_spmd(nc, in_maps, core_ids=list(range(8)))
    mixed = np.empty((2, SEQ, D), np.float32)
    for c in range(8):
        b, g = c // 4, c % 4
        m = res.results[c]["mixo"]
        mixed[b, :, 256 * g:256 * (g + 1)] = m[:, 0:256]
        mixed[b, :, 1024 + 256 * g:1024 + 256 * (g + 1)] = m[:, 256:512]
    return mixed


def kernel(**inputs):
    inputs = {k: np.asarray(v) for k, v in inputs.items()}
    mixed = run_mix(inputs)
    return run_tail(inputs["x"], mixed, inputs["ssd_norm_w"][0], inputs["w_out"][0], inputs["ffn_norm_w"][0],
                    inputs["w_gate"][0], inputs["w_up"][0], inputs["w_down"][0], inputs["final_norm_w"])
```

```python
import contextlib
import os
import numpy as np
import ml_dtypes
import concourse.bass as bass
import concourse.mybir as mybir
from concourse.bass_utils import run_bass_kernel_spmd

F32 = mybir.dt.float32
BF16 = mybir.dt.bfloat16
U8 = mybir.dt.uint8
ALU = mybir.AluOpType
AF = mybir.ActivationFunctionType
AX = mybir.AxisListType

ENGS = ("pe", "act", "dve", "pool", "sp")
EPS = 1e-6


class _Op:
    __slots__ = ("eng", "fn", "dma", "deps", "idx", "signal", "val", "sem", "cc", "cost")


class Sched:
    def __init__(self, nc, n_dma_sems=8):
        self.nc = nc
        self.ops = []
        self.last_w = {}
        self.readers = {}
        self.n_dma_sems = n_dma_sems
        self.bar = None
        self.bank_of = {}

    def op(self, eng, fn, reads=(), writes=(), dma=False, c=None):
        o = _Op()
        o.cost = c
        o.eng, o.fn, o.dma = eng, fn, dma
        o.cc = False
        o.idx = len(self.ops)
        o.signal = False
        deps = {}
        if self.bar is not None:
            deps[self.bar] = "raw"
        for r in reads:
            w = self.last_w.get(r)
            if w is not None:
                deps[w] = "raw"
        for w_ in writes:
            w = self.last_w.get(w_)
            if w is not None:
                deps[w] = "raw"
            for r in self.readers.get(w_, ()):
                if r not in deps:
                    deps[r] = "war"
        for r in reads:
            self.readers.setdefault(r, []).append(o.idx)
        for w_ in writes:
            self.last_w[w_] = o.idx
            self.readers[w_] = []
        banks = set()
        for r in tuple(reads) + tuple(writes):
            banks.update(self.bank_of.get(r, ()))
        for b in banks:
            key = ("bank", b)
            w = self.last_w.get(key)
            if w is not None and w not in deps:
                deps[w] = "bank"
            self.last_w[key] = o.idx
        deps.pop(o.idx, None)
        o.deps = deps
        self.ops.append(o)
        return o

    def pe(self, fn, reads=(), writes=(), c=None):
        return self.op("pe", fn, reads, writes, c=c)

    def act(self, fn, reads=(), writes=(), c=None):
        return self.op("act", fn, reads, writes, c=c)

    def dve(self, fn, reads=(), writes=(), c=None):
        return self.op("dve", fn, reads, writes, c=c)

    def pool(self, fn, reads=(), writes=(), c=None):
        return self.op("pool", fn, reads, writes, c=c)

    DEF_COST = {"pe": 0.12, "act": 0.35, "dve": 0.25, "pool": 0.35}

    def reorder(self, window=int(os.environ.get("RWIN", "120"))):
        ops = self.ops
        n = len(ops)
        queues = {e: [] for e in ENGS}
        for o in ops:
            queues[o.eng].append(o.idx)
        head = {e: 0 for e in ENGS}
        sched = [False] * n
        fin = [0.0] * n
        etime = {e: 0.0 for e in ENGS}
        order = []
        left = n
        while left:
            best = None
            for e in ENGS:
                q = queues[e]
                h = head[e]
                while h < len(q) and sched[q[h]]:
                    h += 1
                head[e] = h
                if h >= len(q):
                    continue
                seen = 0
                i = h
                et = etime[e]
                while i < len(q) and seen < window:
                    k = q[i]
                    i += 1
                    if sched[k]:
                        continue
                    seen += 1
                    o = ops[k]
                    rdy = 0.0
                    ok = True
                    for d in o.deps:
                        if not sched[d]:
                            ok = False
                            break
                        f = fin[d] + (0.0 if ops[d].eng == e else 0.15)
                        if f > rdy:
                            rdy = f
                    if not ok:
                        continue
                    st = rdy if rdy > et else et
                    key = (st, k)
                    if best is None or key < best[0]:
                        best = (key, e, k)
                    if st <= et:
                        break
            assert best is not None, "scheduler stuck"
            (st, k), e, _ = best
            o = ops[k]
            if o.dma:
                etime[e] = st + 0.06
                fin[k] = st + (o.cost if o.cost is not None else 3.0)
            else:
                c = o.cost if o.cost is not None else self.DEF_COST[e]
                etime[e] = st + c
                fin[k] = st + c
            sched[k] = True
            order.append(k)
            left -= 1
        self.order = order
        self.est_time = max(fin) if fin else 0.0

    def dma(self, q, out, in_, reads=(), writes=(), c=None):
        return self.op(q, lambda e: e.dma_start(out=out, in_=in_), reads, writes, dma=True, c=c)

    def cc(self, fn, reads=(), writes=()):
        o = self.op("pool", fn, reads, writes, dma=True)
        o.cc = True
        return o

    def barrier(self, out, in_):
        allres = set(self.last_w.keys()) | set(self.readers.keys())
        o = self.op("sp", lambda e: e.dma_start(out=out, in_=in_), reads=(), writes=tuple(allres), dma=True)
        self.bar = o.idx
        self.last_w = {}
        self.readers = {}
        return o

    def emit(self, sems, block, final_wait_eng="sp"):
        ops = self.ops
        need = [False] * len(ops)
        for o in ops:
            for d, kind in o.deps.items():
                do = ops[d]
                if do.dma:
                    continue
                if do.eng == o.eng and not o.dma:
                    if do.eng == "pe" or kind == "bank":
                        continue
                need[d] = True
        cnt = {e: 0 for e in ENGS}
        dcnt = {}
        dval = {}
        per_eng = {e: [] for e in ENGS}
        order = getattr(self, "order", None) or list(range(len(ops)))
        for k_ in order:
            o = ops[k_]
            per_eng[o.eng].append(o)
            if o.dma and o.cc:
                o.sem = "cc"
                dval["cc"] = dval.get("cc", 0) + 1
                o.val = dval["cc"]
            elif o.dma:
                k = dcnt.get(o.eng, 0)
                dcnt[o.eng] = k + 1
                key = ("dma", o.eng, k % self.n_dma_sems)
                o.sem = key
                dval[key] = dval.get(key, 0) + 16
                o.val = dval[key]
            elif need[o.idx]:
                cnt[o.eng] += 1
                o.val = cnt[o.eng]
                o.sem = o.eng
                o.signal = True
        self.stats = {e: len(per_eng[e]) for e in ENGS}
        self.stats["signals"] = dict(cnt)

        def run(engname, e):
            waited = {}
            for o in per_eng[engname]:
                wl = {}
                for d, kind in o.deps.items():
                    do = ops[d]
                    if do.dma:
                        wl[do.sem] = max(wl.get(do.sem, 0), do.val)
                        continue
                    if do.eng == o.eng and not o.dma:
                        if do.eng == "pe" or kind == "bank":
                            continue
                    wl[do.sem] = max(wl.get(do.sem, 0), do.val)
                if o.dma and o.cc and o.val > 1:
                    wl[o.sem] = max(wl.get(o.sem, 0), o.val - 1)
                elif o.dma and not o.cc and o.val > 16:
                    wl[o.sem] = max(wl.get(o.sem, 0), o.val - 16)
                for s, v in wl.items():
                    if waited.get(s, 0) >= v:
                        continue
                    waited[s] = v
                    e.wait_ge(sems[s], v)
                ins = o.fn(e)
                if o.dma and o.cc:
                    ins.then_inc(sems[o.sem])
                elif o.dma:
                    ins.then_inc(sems[o.sem], 16)
                elif o.signal:
                    ins.then_inc(sems[o.sem], 1)
            if engname == final_wait_eng:
                for key, v in dval.items():
                    if waited.get(key, 0) < v:
                        e.wait_ge(sems[key], v)
                for en in ("pe", "act", "dve", "pool"):
                    if cnt[en] > 0 and waited.get(en, 0) < cnt[en]:
                        e.wait_ge(sems[en], cnt[en])

        @block.tensor
        def _(e):
            run("pe", e)

        @block.scalar
        def _(e):
            run("act", e)

        @block.vector
        def _(e):
            run("dve", e)

        @block.gpsimd
        def _(e):
            run("pool", e)

        @block.sync
        def _(e):
            run("sp", e)


def make_sems(nc, stack, n_dma_sems=8, queues=("sp", "pool", "act")):
    sems = {}
    for e in ("pe", "act", "dve", "pool", "cc"):
        sems[e] = stack.enter_context(nc.semaphore("s_" + e))
    for q in queues:
        for i in range(n_dma_sems):
            sems[("dma", q, i)] = stack.enter_context(nc.semaphore("d_%s_%d" % (q, i)))
    return sems


_DTSZ = {F32: 4, BF16: 2, U8: 1}


class Arena:
    def __init__(self, ar, size):
        self.ar, self.size, self.off = ar, size, 0

    def reset(self, off=0):
        self.off = off

    def alloc(self, shape, dtype, parts=128):
        n = int(np.prod(shape)) * _DTSZ[dtype]
        off = (self.off + 63) // 64 * 64
        assert off + n <= self.size, ("arena overflow", off, n, self.size)
        self.off = off + n
        ap = self.ar[0:parts, off:off + n].bitcast(dtype)
        if len(shape) > 1:
            names = [chr(ord("a") + i) for i in range(len(shape))]
            pat = "p (%s) -> p %s" % (" ".join(names), " ".join(names))
            ap = ap.rearrange(pat, **{nm: int(s) for nm, s in zip(names, shape)})
        return ap


D = 2048
FF = 5632
TOK = 1024
NT = TOK // 128
KC = D // 128
FC = FF // 128
SSDW = 1024


def build_tail():
    nc = bass.Bass("TRN2", target_bir_lowering=False)
    x = nc.dram_tensor("x_own", [TOK, D], F32, kind="ExternalInput").ap()
    mix = nc.dram_tensor("mix", [TOK, D], F32, kind="ExternalInput").ap()
    w_out = nc.dram_tensor("w_out", [D, D], F32, kind="ExternalInput").ap()
    w_gate = nc.dram_tensor("w_gate", [D, FF], F32, kind="ExternalInput").ap()
    w_up = nc.dram_tensor("w_up", [D, FF], F32, kind="ExternalInput").ap()
    w_down = nc.dram_tensor("w_down", [FF, D], F32, kind="ExternalInput").ap()
    nw = nc.dram_tensor("nw", [3, D], F32, kind="ExternalInput").ap()
    ident = nc.dram_tensor("ident", [128, 128], F32, kind="ExternalInput").ap()
    out = nc.dram_tensor("out", [TOK, D], F32, kind="ExternalOutput").ap()
    h_d = nc.dram_tensor("h_d", [TOK, D], F32, kind="Internal").ap()
    dummy = nc.dram_tensor("dummy_bar", [2, 64], F32, kind="Internal").ap()
    with contextlib.ExitStack() as st:
        ASZ = 207 * 1024
        ar = st.enter_context(nc.sbuf_tensor("arena", [128, ASZ], U8))
        A = Arena(ar, ASZ)
        pbig = st.enter_context(nc.psum_tensor("pbig", [128, 8, 512], F32))
        sems = make_sems(nc, st)
        block = st.enter_context(nc.Block())
        S = Sched(nc)
        tail_body(nc, S, A, pbig, x, mix, w_out, w_gate, w_up, w_down, nw, ident, out, h_d, dummy)
        if os.environ.get('NO_REORDER') is None:
            S.reorder()
        S.emit(sems, block)
    return nc


def rms_rstd(S, src, n, ss, sq, tag, rd, wr_extra=(), c=None):
    S.act(lambda e: e.activation(sq, src, AF.Square, accum_out=ss), reads=rd, writes=[tag + "ss", tag + "sq"], c=c)
    S.act(lambda e: e.activation(ss, ss, AF.Sqrt, scale=1.0 / n, bias=EPS_AP[0]), reads=[tag + "ss"], writes=[tag + "ss"])
    S.dve(lambda e: e.reciprocal(ss, ss), reads=[tag + "ss"], writes=[tag + "ss"])


EPS_AP = [None]


def tail_body(nc, S, A, pbig, x, mix, w_out, w_gate, w_up, w_down, nw, ident, out, h_d, dummy):
    bk = {"ptr": (4, 5)}
    for i_ in range(8):
        bk["pacc%d" % i_] = (i_,)
        bk["pd%d" % i_] = (i_,)
    for i_ in range(2):
        bk["pg%d" % i_] = (i_ * 4, i_ * 4 + 1)
        bk["pu%d" % i_] = (i_ * 4 + 2, i_ * 4 + 3)
    S.bank_of = bk
    identb = A.alloc([128], BF16)
    nwb_flat = A.alloc([2 * D + SSDW], F32)
    epsb = A.alloc([1], F32)
    ss = A.alloc([4], F32)
    EPS_AP[0] = epsb
    S.pool(lambda e: e.memset(epsb, EPS), writes=["eps"])
    S.dma("pool", identb, ident, writes=["ident"])
    S.dma("sp", nwb_flat, nw.rearrange("a b -> (a b)")[0:2 * D + SSDW].partition_broadcast(128), writes=["nwb"])

    class _NW:
        def __getitem__(self, key):
            _, row, cols = key
            base = {1: 0, 2: D, 0: 2 * D}[row]
            lo = cols.start or 0
            hi = cols.stop if cols.stop is not None else (SSDW if row == 0 else D)
            return nwb_flat[:, base + lo:base + hi]
    nwb = _NW()
    vT = A.alloc([KC, TOK], BF16)
    base_persist = A.off

    wo = A.alloc([KC, D], BF16)
    for cb in range(4):
        S.dma("pool", wo[:, :, cb * 512:(cb + 1) * 512],
              w_out[:, cb * 512:(cb + 1) * 512].rearrange("(k p) n -> p k n", p=128), writes=["wo%d" % cb], c=25.0)
    xt = [A.alloc([D], F32) for _ in range(2)]
    mt = [A.alloc([D], F32) for _ in range(2)]
    sq = A.alloc([D], BF16)
    mb = A.alloc([D], BF16)
    mT = A.alloc([KC, 128], BF16)
    hs = [A.alloc([D], F32) for _ in range(2)]
    vb = A.alloc([D], BF16)
    WB = 256
    A.alloc([1024], F32)
    pre_lo = (A.off + 63) // 64 * 64
    wg_pre = A.alloc([KC, WB], BF16)
    wu_pre = A.alloc([KC, WB], BF16)
    S.dma("pool", wg_pre, w_gate[:, 0:WB].rearrange("(k p) n -> p k n", p=128), writes=["wg0"], c=15.0)
    S.dma("pool", wu_pre, w_up[:, 0:WB].rearrange("(k p) n -> p k n", p=128), writes=["wu0"], c=15.0)
    pacc = pbig[:, 0:4, :]
    ptr_all = pbig[:, 4:6, :].rearrange("p a b -> p (a b)").bitcast(BF16)
    ptr = ptr_all.rearrange("p (k n) -> p k n", k=KC)

    def loads(tt):
        b = tt % 2
        S.dma("sp", xt[b], x[tt * 128:(tt + 1) * 128, :], writes=["xt%d" % b])
        S.dma("sp", mt[b], mix[tt * 128:(tt + 1) * 128, :], writes=["mt%d" % b])

    loads(0)
    for tt in range(NT):
        b = tt % 2
        if tt + 1 < NT:
            loads(tt + 1)
        rms_rstd(S, mt[b][:, 0:SSDW], SSDW, ss[:, 0:1], sq[:, 0:SSDW], "a", ["mt%d" % b, "eps"])
        S.dve(lambda e, b=b: e.scalar_tensor_tensor(mb[:, 0:SSDW], mt[b][:, 0:SSDW], ss[:, 0:1], nwb[:, 0, 0:SSDW], ALU.mult, ALU.mult),
              reads=["mt%d" % b, "ass", "nwb"], writes=["mb0"])
        S.pool(lambda e, b=b: e.tensor_copy(mb[:, SSDW:D], mt[b][:, SSDW:D]), reads=["mt%d" % b], writes=["mb1"])
        for kc in range(KC):
            S.pe(lambda e, kc=kc: e.transpose(ptr[:, kc, :], mb[:, kc * 128:(kc + 1) * 128], identb),
                 reads=["mb0", "mb1", "ident"], writes=["ptr"])
        S.act(lambda e: e.copy(mT[:, 0:8, :], ptr[:, 0:8, :]), reads=["ptr"], writes=["mTa"])
        S.dve(lambda e: e.tensor_copy(mT[:, 8:16, :], ptr[:, 8:16, :]), reads=["ptr"], writes=["mTb"])
        for cb in range(4):
            for kc in range(KC):
                S.pe(lambda e, cb=cb, kc=kc: e.matmul(pacc[:, cb, :], mT[:, kc, :], wo[:, kc, cb * 512:(cb + 1) * 512],
                                                       start=(kc == 0), stop=(kc == KC - 1)),
                     reads=["mTa", "mTb", "wo%d" % cb], writes=["pacc%d" % cb], c=0.22)
            S.dve(lambda e, cb=cb, b=b: e.tensor_tensor(hs[b][:, cb * 512:(cb + 1) * 512], pacc[:, cb, :], xt[b][:, cb * 512:(cb + 1) * 512], ALU.add),
                  reads=["pacc%d" % cb, "xt%d" % b], writes=["hs%d_%d" % (b, cb)])
        hres = ["hs%d_%d" % (b, cb) for cb in range(4)]
        S.dma("sp", h_d[tt * 128:(tt + 1) * 128, :], hs[b], reads=hres, writes=["h_d%d" % tt])
        rms_rstd(S, hs[b], D, ss[:, 1:2], sq, "b", hres + ["eps"])
        S.dve(lambda e, b=b: e.scalar_tensor_tensor(vb, hs[b], ss[:, 1:2], nwb[:, 1, :], ALU.mult, ALU.mult),
              reads=hres + ["bss", "nwb"], writes=["vb"])
        for kc in range(KC):
            S.pe(lambda e, kc=kc: e.transpose(ptr[:, kc, :], vb[:, kc * 128:(kc + 1) * 128], identb),
                 reads=["vb", "ident"], writes=["ptr"])
        S.act(lambda e, tt=tt: e.copy(vT[:, 0:8, tt * 128:(tt + 1) * 128], ptr[:, 0:8, :]), reads=["ptr"], writes=["vT%da" % tt])
        S.dve(lambda e, tt=tt: e.tensor_copy(vT[:, 8:16, tt * 128:(tt + 1) * 128], ptr[:, 8:16, :]), reads=["ptr"], writes=["vT%db" % tt])

    S.barrier(dummy[1:2, :], ident[0:1, 0:64])
    A.reset(base_persist)
    hT = A.alloc([FC, TOK], BF16)
    HK = 22
    wd_pre = A.alloc([HK, 512], BF16)
    base_b = A.off
    NB = FF // WB
    wg = [wg_pre, A.alloc([KC, WB], BF16)]
    wu = [wu_pre, A.alloc([KC, WB], BF16)]
    sg = [A.alloc([TOK], BF16) for _ in range(2)]
    assert A.off <= pre_lo, (A.off, pre_lo)
    for blk in range(NB):
        b = blk % 2
        if blk > 0:
            S.dma("pool", wg[b], w_gate[:, blk * WB:(blk + 1) * WB].rearrange("(k p) n -> p k n", p=128), writes=["wg%d" % b], c=15.0)
            S.dma("pool", wu[b], w_up[:, blk * WB:(blk + 1) * WB].rearrange("(k p) n -> p k n", p=128), writes=["wu%d" % b], c=15.0)
        if blk == NB - 2:
            S.dma("pool", wd_pre, w_down[0:HK * 128, 0:512].rearrange("(k p) n -> p k n", p=128), writes=["wd0"], c=15.0)
        for j in range(WB // 128):
            fc = blk * (WB // 128) + j
            pb = fc % 2
            pg = pbig[:, pb * 4:pb * 4 + 2, :]
            pu = pbig[:, pb * 4 + 2:pb * 4 + 4, :]
            for hf in range(2):
                for kc in range(KC):
                    S.pe(lambda e, b=b, j=j, hf=hf, kc=kc, pg=pg: e.matmul(pg[:, hf, :], wg[b][:, kc, j * 128:(j + 1) * 128], vT[:, kc, hf * 512:(hf + 1) * 512],
                                                                         start=(kc == 0), stop=(kc == KC - 1)),
                         reads=["wg%d" % b, "vT"], writes=["pg%d" % pb], c=0.22)
            for hf in range(2):
                for kc in range(KC):
                    S.pe(lambda e, b=b, j=j, hf=hf, kc=kc, pu=pu: e.matmul(pu[:, hf, :], wu[b][:, kc, j * 128:(j + 1) * 128], vT[:, kc, hf * 512:(hf + 1) * 512],
                                                                         start=(kc == 0), stop=(kc == KC - 1)),
                         reads=["wu%d" % b, "vT"], writes=["pu%d" % pb], c=0.22)
            S.act(lambda e, pb=pb, pg=pg: e.activation(sg[pb], pg.rearrange("p a b -> p (a b)"), AF.Silu), reads=["pg%d" % pb], writes=["sg%d" % pb])
            S.dve(lambda e, pb=pb, pu=pu, fc=fc: e.tensor_tensor(hT[:, fc, :], sg[pb], pu.rearrange("p a b -> p (a b)"), ALU.mult),
                  reads=["sg%d" % pb, "pu%d" % pb], writes=["hT%d" % fc])

    S.barrier(dummy[1:2, :], ident[0:1, 0:64])
    A.reset(base_b)
    wd = [wd_pre, A.alloc([HK, 512], BF16)]
    hl = [A.alloc([512], F32) for _ in range(2)]
    ys = [A.alloc([512], F32) for _ in range(2)]
    it = 0
    for r in range(4):
        for hf in range(2):
            b = (r * 2 + hf) % 2
            if r * 2 + hf > 0:
                S.dma("pool", wd[b], w_down[hf * HK * 128:(hf + 1) * HK * 128, r * 512:(r + 1) * 512].rearrange("(k p) n -> p k n", p=128), writes=["wd%d" % b], c=15.0)
            for tt in range(NT):
                for k in range(HK):
                    kk = hf * HK + k
                    S.pe(lambda e, b=b, tt=tt, k=k, kk=kk: e.matmul(pbig[:, tt, :], hT[:, kk, tt * 128:(tt + 1) * 128], wd[b][:, k, :],
                                                                    start=(kk == 0), stop=(kk == FC - 1)),
                         reads=["wd%d" % b, "hT"], writes=["pd%d" % tt], c=0.22)
        for tt in range(NT):
            b = it % 2
            it += 1
            S.dma("sp", hl[b], h_d[tt * 128:(tt + 1) * 128, r * 512:(r + 1) * 512], reads=["h_d%d_%d" % (tt, r)], writes=["hl%d" % b])
            S.dve(lambda e, b=b, tt=tt: e.tensor_tensor(ys[b], pbig[:, tt, :], hl[b], ALU.add), reads=["pd%d" % tt, "hl%d" % b], writes=["ys%d" % b])
            S.dma("sp", h_d[tt * 128:(tt + 1) * 128, r * 512:(r + 1) * 512], ys[b], reads=["ys%d" % b], writes=["h_d%d_%d" % (tt, r)])

    S.barrier(dummy[1:2, :], ident[0:1, 0:64])
    A.reset(base_persist)
    yt = [A.alloc([D], F32) for _ in range(2)]
    ot = [A.alloc([D], F32) for _ in range(2)]
    sq2 = A.alloc([D], F32)
    for tt in range(NT):
        b = tt % 2
        S.dma("sp", yt[b], h_d[tt * 128:(tt + 1) * 128, :], writes=["yt%d" % b])
        rms_rstd(S, yt[b], D, ss[:, 2:3], sq2, "c", ["yt%d" % b, "eps"])
        S.dve(lambda e, b=b: e.scalar_tensor_tensor(ot[b], yt[b], ss[:, 2:3], nwb[:, 2, :], ALU.mult, ALU.mult),
              reads=["yt%d" % b, "css", "nwb"], writes=["ot%d" % b])
        S.dma("sp", out[tt * 128:(tt + 1) * 128, :], ot[b], reads=["ot%d" % b])


_CACHE = {}


def run_tail(x, mixed, ssd_norm_w, w_out, ffn_norm_w, w_gate, w_up, w_down, final_norm_w):
    if "tail" not in _CACHE:
        _CACHE["tail"] = build_tail()
    nc = _CACHE["tail"]
    nwv = np.ones((3, D), np.float32)
    nwv[0] = ffn_norm_w
    nwv[1] = final_norm_w
    nwv[2, :SSDW] = ssd_norm_w
    ident = np.eye(128, dtype=np.float32)
    in_maps = []
    for c in range(8):
        b, g = c // 4, c % 4
        in_maps.append({
            "x_own": np.ascontiguousarray(x[b, g * TOK:(g + 1) * TOK]),
            "mix": np.ascontiguousarray(mixed[b, g * TOK:(g + 1) * TOK]),
            "w_out": w_out, "w_gate": w_gate, "w_up": w_up, "w_down": w_down,
            "nw": nwv, "ident": ident,
        })
    res = run_bass_kernel_spmd(nc, in_maps, core_ids=list(range(8)))
    outp = np.empty((2, 4096, D), np.float32)
    for c in range(8):
        b, g = c // 4, c % 4
        outp[b, g * TOK:(g + 1) * TOK] = res.results[c]["out"]
    return outp


SEQ = 4096
NTT = SEQ // 128
NEG = -30000.0
NA = 400
NB_ = 384
NFM = 640
SCALE = 0.125


def build_mix():
    nc = bass.Bass("TRN2", target_bir_lowering=False)
    di = lambda n, s, d=F32: nc.dram_tensor(n, s, d, kind="ExternalInput").ap()
    T = dict(
        xb=di("xb", [SEQ, D]), anw=di("anw", [D]), w_tm=di("w_tm", [D, NA + NB_]), w_fm=di("w_fm", [D, NFM]),
        convw=di("convw", [128, 16]), convb=di("convb", [128, 4]), hp=di("hp", [12]), rope=di("rope", [128, NTT * 16]),
        w1k=di("w1k", [64, 32 * 256]), w1v=di("w1v", [64, 32 * 256]), w2k=di("w2k", [256, 64]), w2v=di("w2v", [256, 64]),
        pek=di("pek", [64, 32]), pev=di("pev", [64, 32]), ident=di("ident", [128, 128]), emat=di("emat", [64, SEQ]),
        tric=di("tric", [128, 128]), tria=di("tria", [128, 128]), cmask=di("cmask", [256, SEQ]), ovl=di("ovl", [256, 64]),
        utri=di("utri", [128, 128]),
    )
    T["mixo"] = nc.dram_tensor("mixo", [SEQ, 512], F32, kind="ExternalOutput").ap()
    T["qr_d"] = nc.dram_tensor("qr_d", [64, 4, SEQ], BF16, kind="Internal").ap()
    T["qu_d"] = nc.dram_tensor("qu_d", [64, 4, SEQ], BF16, kind="Internal").ap()
    T["dummy"] = nc.dram_tensor("dummy_bar", [2, 64], F32, kind="Internal").ap()
    with contextlib.ExitStack() as st:
        ASZ = 207 * 1024
        ar = st.enter_context(nc.sbuf_tensor("arena", [128, ASZ], U8))
        A = Arena(ar, ASZ)
        pbig = st.enter_context(nc.psum_tensor("pbig", [128, 8, 512], F32))
        sems = make_sems(nc, st)
        block = st.enter_context(nc.Block())
        S = Sched(nc)
        mix_body(nc, S, A, pbig, T)
        if os.environ.get('NO_REORDER') is None:
            S.reorder()
        S.emit(sems, block)
    return nc


def pbf(pb, lo, hi):
    return pb[:, lo:hi, :].rearrange("p a b -> p (a b)").bitcast(BF16)


def mix_body(nc, S, A, pbig, T):
    bar = lambda: S.barrier(T["dummy"][1:2, :], T["ident"][0:1, 0:64])
    bk = {"ptr": (0,), "ptr2": (6,), "psA": (1,), "psB": (2,), "psF0": (3,), "psR": (4,), "psS_a": (5,), "psS_c": (5,),
          "psN": (6,), "psY_o": (7,), "psY_d": (7,), "pk": (5,), "pv0": (6,), "pv1": (7,), "pnt": (7,), "posel": (6,), "powin": (7,)}
    for i_ in range(4):
        bk["pbias%d" % i_] = (4,)
        bk["phid%d" % i_] = (i_,)
        bk["psw%d" % i_] = (3 + i_ % 2,)
    for a_ in range(2):
        bk["po%d" % a_] = (4 + a_,)
        for b_ in range(2):
            bk["psc%d_%d" % (a_, b_)] = (a_ * 2 + b_,)
    for i_ in range(3):
        bk["pss%d" % i_] = (i_,)
    bk["poselT"] = (5,)
    S.bank_of = bk
    identb = A.alloc([128], BF16)
    utri = A.alloc([128], F32)
    identf = A.alloc([128], F32)
    tricb = A.alloc([128], BF16)
    triab = A.alloc([128], BF16)
    epsb = A.alloc([1], F32)
    oneb = A.alloc([1], F32)
    EPS_AP[0] = epsb
    KsA = A.alloc([SEQ], BF16)
    KwT = A.alloc([SEQ], BF16)
    kcvcT = A.alloc([SEQ], BF16)
    VsA = A.alloc([NTT, 65], BF16)
    VwA = A.alloc([NTT, 65], BF16)
    gate = A.alloc([NTT, 12], F32)
    hpb = A.alloc([12], F32)
    base_persist = A.off
    S.pool(lambda e: e.memset(epsb, EPS), writes=["eps"])
    S.pool(lambda e: e.memset(oneb, 1.0), writes=["one"])
    S.pool(lambda e: e.memset(VsA, 1.0), writes=["VsA"])
    S.pool(lambda e: e.memset(VwA, 1.0), writes=["VwA"])
    S.dma("pool", identb, T["ident"], writes=["ident"])
    S.dma("sp", utri, T["utri"], writes=["utri"])
    S.dma("sp", identf, T["ident"], writes=["identf"])
    S.dma("pool", tricb, T["tric"], writes=["tric"])
    S.dma("pool", triab, T["tria"], writes=["tria"])
    S.dma("pool", KsA[64:128, :], T["emat"], writes=["KsE"])
    S.dma("sp", hpb, T["hp"].partition_broadcast(128), writes=["hpb"])

    wtm = A.alloc([KC, NA + NB_], BF16)
    wfm = A.alloc([KC, NFM], BF16)
    S.dma("pool", wfm, T["w_fm"].rearrange("(k p) n -> p k n", p=128), writes=["wfm"], c=30.0)
    S.dma("pool", wtm, T["w_tm"].rearrange("(k p) n -> p k n", p=128), writes=["wtm"], c=35.0)
    anwb = A.alloc([D], F32)
    S.dma("sp", anwb, T["anw"].partition_broadcast(128), writes=["anwb"])
    ropet = A.alloc([NTT, 16], F32)
    S.dma("sp", ropet, T["rope"].rearrange("p (t c) -> p t c", c=16), writes=["ropet"])
    convw = A.alloc([16], F32)
    convb = A.alloc([4], F32)
    S.dma("sp", convw, T["convw"], writes=["convw"])
    S.dma("sp", convb, T["convb"], writes=["convb"])
    onesf = A.alloc([128], F32)
    S.pool(lambda e: e.memset(onesf, 1.0), writes=["onesf"])
    two = lambda shape, dt: [A.alloc(shape, dt) for _ in range(2)]
    xt = two([D], F32)
    sq1 = A.alloc([D], BF16)
    sq = [sq1, sq1]
    ss = A.alloc([8], F32)
    ub = two([D], BF16)
    uT2 = two([KC, 512], BF16)
    cbuf = A.alloc([4, 515], F32)
    cacc_1 = A.alloc([512], F32)
    cacc = [cacc_1, cacc_1]
    xbcT = two([4, 512], BF16)
    zs = two([256], BF16)
    ez_1 = A.alloc([256], F32)
    ez = [ez_1, ez_1]
    ec = two([512], F32)
    dtt = two([4], F32)
    qk = two([6, 64], F32)
    qkr = two([6, 64], BF16)
    qkb = two([4, 64], BF16)
    rt = two([4, 6, 8], F32)
    qst = two([4, 128], BF16)
    qut = two([4, 128], BF16)
    xtm = two([256], BF16)
    btm = two([128], BF16)
    hst = A.alloc([256], F32)
    hstb = two([256], BF16)
    aneg = A.alloc([4], F32)
    adt = two([4], F32)
    acol = two([4], F32)
    nacol = two([4], F32)
    eac = two([4], F32)
    rhs4_1 = A.alloc([4, 128], F32)
    rhs4 = [rhs4_1, rhs4_1]
    seg4_1 = A.alloc([4, 128], F32)
    seg4 = [seg4_1, seg4_1]
    cbm = two([128], F32)
    MT = two([4, 128], BF16)
    alast = two([4], F32)
    dsv = two([4], F32)
    cdv = two([4], F32)
    wsc = two([4], F32)
    xw = two([256], BF16)
    xdt = two([256], BF16)
    ydg_1 = A.alloc([256], F32)
    ydg = [ydg_1, ydg_1]
    yy_1 = A.alloc([256], F32)
    yy = [yy_1, yy_1]
    tmpd_1 = A.alloc([256], F32)
    tmpd = [tmpd_1, tmpd_1]
    yo = two([256], F32)
    S.pool(lambda e: e.memset(cbuf, 0.0), writes=["cbuf", "cbuf0", "cbuf1", "cbuf2", "cbuf3"])
    S.pool(lambda e: e.memset(hst, 0.0), writes=["hst"])
    S.act(lambda e: e.activation(aneg, hpb[:, 4:8], AF.Exp), reads=["hpb"], writes=["aneg"])
    S.dve(lambda e: e.tensor_scalar(aneg, aneg, -1.0, None, ALU.mult), reads=["aneg"], writes=["aneg"])

    ptr = pbf(pbig, 0, 1)
    psA = pbig[:, 1, 0:NA]
    psB = pbig[:, 2, 0:NB_]
    psF = pbig[:, 3, :]
    psR = pbig[:, 4, :]
    psS = pbig[:, 5, :]
    psN = pbig[:, 6, 0:256]
    psY = pbig[:, 7, :]

    def load_x(tt):
        S.dma("sp", xt[tt % 2], T["xb"][tt * 128:(tt + 1) * 128, :], writes=["xt%d" % (tt % 2)], c=5.0)

    def b4(ap, n):
        return ap.unsqueeze(2).to_broadcast([128, 4, n])

    load_x(0)
    for G in range(SEQ // 512):
        gp = G % 2
        uT = uT2[gp]
        for j in range(4):
            tt = G * 4 + j
            b = tt % 2
            if tt + 1 < NTT:
                load_x(tt + 1)
            ssb = ss[:, b:b + 1]
            S.act(lambda e, b=b, ssb=ssb: e.activation(sq[b], xt[b], AF.Square, accum_out=ssb), reads=["xt%d" % b], writes=["n%dss" % b, "nsq"], c=1.9)
            S.act(lambda e, ssb=ssb: e.activation(ssb, ssb, AF.Ln, scale=1.0 / D, bias=epsb), reads=["n%dss" % b, "eps"], writes=["n%dss" % b])
            S.act(lambda e, ssb=ssb: e.activation(ssb, ssb, AF.Exp, scale=-0.5), reads=["n%dss" % b], writes=["n%dss" % b])
            S.dve(lambda e, b=b: e.scalar_tensor_tensor(ub[b], xt[b], ss[:, b:b + 1], anwb, ALU.mult, ALU.mult),
                  reads=["xt%d" % b, "n%dss" % b, "anwb"], writes=["ub%d" % b], c=2.2)
            for half in range(2):
                for k8 in range(8):
                    kc = half * 8 + k8
                    S.pe(lambda e, kc=kc, k8=k8, b=b: e.transpose(ptr[:, k8 * 128:(k8 + 1) * 128], ub[b][:, kc * 128:(kc + 1) * 128], identb),
                         reads=["ub%d" % b, "ident"], writes=["ptr"])
                if half == 0:
                    S.act(lambda e, j=j, uT=uT: e.copy(uT[:, 0:8, j * 128:(j + 1) * 128], ptr.rearrange("p (k n) -> p k n", k=8)), reads=["ptr"], writes=["uT%d_%d_0" % (gp, j)], c=0.9)
                else:
                    S.dve(lambda e, j=j, uT=uT: e.tensor_copy(uT[:, 8:16, j * 128:(j + 1) * 128], ptr.rearrange("p (k n) -> p k n", k=8)), reads=["ptr"], writes=["uT%d_%d_1" % (gp, j)], c=0.7)
        uTr = ["uT%d_%d_%d" % (gp, j, h) for j in range(4) for h in range(2)]
        for c in range(5):
            for kc in range(KC):
                S.pe(lambda e, c=c, kc=kc, uT=uT: e.matmul(psF, wfm[:, kc, c * 128:(c + 1) * 128], uT[:, kc, :], start=(kc == 0), stop=(kc == KC - 1)),
                     reads=uTr + ["wfm"], writes=["psF0"], c=0.22)
            if c < 4:
                ca = cacc[c % 2]
                car = "cacc"
                S.act(lambda e, c=c: e.copy(cbuf[:, c, 3:515], psF), reads=["psF0"], writes=["cbuf%d" % c], c=0.6)
                S.dve(lambda e, c=c, ca=ca: e.tensor_scalar(ca, cbuf[:, c, 0:512], convw[:, c * 4:c * 4 + 1], convb[:, c:c + 1], ALU.mult, ALU.add), reads=["cbuf%d" % c, "convw", "convb"], writes=[car], c=0.4)
                for k in range(1, 4):
                    S.dve(lambda e, c=c, k=k, ca=ca: e.scalar_tensor_tensor(ca, cbuf[:, c, k:k + 512], convw[:, c * 4 + k:c * 4 + k + 1], ca, ALU.mult, ALU.add),
                          reads=["cbuf%d" % c, car, "convw"], writes=[car], c=0.65)
                ece = ec[c % 2]
                ecr = "ec%d" % (c % 2)
                S.act(lambda e, ca=ca, ece=ece: e.activation(ece, ca, AF.Exp, scale=-1.0), reads=[car], writes=[ecr], c=0.6)
                S.act(lambda e, ece=ece: e.activation(ece, ece, AF.Ln, bias=oneb), reads=[ecr, "one"], writes=[ecr], c=0.6)
                S.act(lambda e, ece=ece: e.activation(ece, ece, AF.Exp, scale=-1.0), reads=[ecr], writes=[ecr], c=0.6)
                S.pool(lambda e, c=c, ca=ca, ece=ece, gp=gp: e.tensor_tensor(xbcT[gp][:, c, :], ca, ece, ALU.mult), reads=[car, ecr], writes=["xbcT%d_%d" % (gp, c)], c=2.0)
                S.pool(lambda e, c=c: e.tensor_copy(cbuf[:, c, 0:3], cbuf[:, c, 512:515]), reads=["cbuf%d" % c], writes=["cbuf%d" % c])
            else:
                S.act(lambda e, G=G: e.copy(kcvcT[:, G * 512:(G + 1) * 512], psF), reads=["psF0"], writes=["kcvcT"], c=0.6)
        xr = lambda c: "xbcT%d_%d" % (gp, c)
        for j in range(4):
            tt = G * 4 + j
            p = tt % 2
            P = str(p)
            tok = slice(tt * 128, (tt + 1) * 128)
            js = slice(j * 128, (j + 1) * 128)
            for kc in range(KC):
                S.pe(lambda e, js=js, kc=kc, uT=uT: e.matmul(psA, uT[:, kc, js], wtm[:, kc, 0:NA], start=(kc == 0), stop=(kc == KC - 1)),
                     reads=uTr + ["wtm"], writes=["psA"], c=0.19)
            for kc in range(KC):
                S.pe(lambda e, js=js, kc=kc, uT=uT: e.matmul(psB, uT[:, kc, js], wtm[:, kc, NA:NA + NB_], start=(kc == 0), stop=(kc == KC - 1)),
                     reads=uTr + ["wtm"], writes=["psB"], c=0.18)
            S.act(lambda e, p=p: e.activation(ez[p], psA[:, 0:256], AF.Exp, scale=-1.0), reads=["psA"], writes=["ez"])
            S.act(lambda e, p=p: e.activation(ez[p], ez[p], AF.Ln, bias=oneb), reads=["ez", "one"], writes=["ez"])
            S.act(lambda e, p=p: e.activation(ez[p], ez[p], AF.Exp, scale=-1.0), reads=["ez"], writes=["ez"])
            S.dve(lambda e, p=p: e.tensor_tensor(zs[p], psA[:, 0:256], ez[p], ALU.mult), reads=["psA", "ez"], writes=["zs" + P])
            S.dve(lambda e, p=p: e.tensor_tensor(dtt[p], psA[:, 256:260], hpb[:, 0:4], ALU.add), reads=["psA", "hpb"], writes=["dtt" + P])
            S.act(lambda e, p=p: e.activation(dtt[p], dtt[p], AF.Exp), reads=["dtt" + P], writes=["dtt" + P])
            S.act(lambda e, p=p: e.activation(dtt[p], dtt[p], AF.Ln, bias=oneb), reads=["dtt" + P, "one"], writes=["dtt" + P])
            S.act(lambda e, tt=tt: e.activation(gate[:, tt, :], psA[:, 260:272], AF.Exp, scale=-1.0), reads=["psA"], writes=["gate%d" % tt])
            S.dve(lambda e, tt=tt: e.tensor_scalar(gate[:, tt, :], gate[:, tt, :], 1.0, None, ALU.add), reads=["gate%d" % tt], writes=["gate%d" % tt])
            S.dve(lambda e, tt=tt: e.reciprocal(gate[:, tt, :], gate[:, tt, :]), reads=["gate%d" % tt], writes=["gate%d" % tt])
            S.dve(lambda e, tt=tt: e.tensor_copy(VsA[:, tt, 0:64], psA[:, 272:336]), reads=["psA", "VsA"], writes=["VsA%d" % tt])
            S.dve(lambda e, tt=tt: e.tensor_copy(VwA[:, tt, 0:64], psA[:, 336:400]), reads=["psA", "VwA"], writes=["VwA%d" % tt])
            S.act(lambda e, p=p: e.copy(qk[p], psB.rearrange("p (a b) -> p a b", a=6)), reads=["psB"], writes=["qk" + P], c=0.5)
            S.pool(lambda e, p=p: e.tensor_copy(qkb[p], qk[p][:, 0:4, :]), reads=["qk" + P], writes=["qkb" + P])
            S.pool(lambda e, p=p: e.tensor_copy(qkr[p], qk[p]), reads=["qk" + P], writes=["qkr" + P])
            cosb = ropet[:, tt, 0:8].unsqueeze(1).to_broadcast([128, 6, 8])
            sinb = ropet[:, tt, 8:16].unsqueeze(1).to_broadcast([128, 6, 8])
            S.dve(lambda e, cosb=cosb, p=p: e.tensor_tensor(rt[p][:, 0], qk[p][:, :, 0:8], cosb, ALU.mult), reads=["qk" + P, "ropet"], writes=["rt0" + P])
            S.dve(lambda e, sinb=sinb, p=p: e.tensor_tensor(rt[p][:, 1], qk[p][:, :, 8:16], sinb, ALU.mult), reads=["qk" + P, "ropet"], writes=["rt1" + P])
            S.dve(lambda e, cosb=cosb, p=p: e.tensor_tensor(rt[p][:, 2], qk[p][:, :, 8:16], cosb, ALU.mult), reads=["qk" + P, "ropet"], writes=["rt2" + P])
            S.dve(lambda e, sinb=sinb, p=p: e.tensor_tensor(rt[p][:, 3], qk[p][:, :, 0:8], sinb, ALU.mult), reads=["qk" + P, "ropet"], writes=["rt3" + P])
            S.dve(lambda e, p=p: e.tensor_tensor(qkr[p][:, :, 0:8], rt[p][:, 0], rt[p][:, 1], ALU.subtract), reads=["rt0" + P, "rt1" + P, "qkr" + P], writes=["qkr" + P])
            S.dve(lambda e, p=p: e.tensor_tensor(qkr[p][:, :, 8:16], rt[p][:, 2], rt[p][:, 3], ALU.add), reads=["rt2" + P, "rt3" + P, "qkr" + P], writes=["qkr" + P])
            ptq = ptr[0:64, 0:768].rearrange("p (a b) -> p a b", a=6)
            ptu = pbf(pbig, 6, 7)[0:64, 512:1024].rearrange("p (a b) -> p a b", a=4)
            for a in range(6):
                S.pe(lambda e, a=a, p=p: e.transpose(ptq[:, a, :], qkr[p][:, a, :], identb), reads=["qkr" + P, "ident"], writes=["ptr"])
            for a in range(4):
                S.pe(lambda e, a=a, p=p: e.transpose(ptu[:, a, :], qkb[p][:, a, :], identb), reads=["qkb" + P, "ident"], writes=["ptr2"])
            S.act(lambda e, p=p: e.copy(qst[p][0:64], ptq[:, 0:4, :]), reads=["ptr"], writes=["qst" + P])
            S.dve(lambda e, p=p: e.tensor_copy(qut[p][0:64], ptu), reads=["ptr2"], writes=["qut" + P])
            S.act(lambda e, tok=tok: e.copy(KsA[0:64, tok], ptq[:, 4, :]), reads=["ptr"], writes=["KsA%d" % tt])
            S.dve(lambda e, tok=tok: e.tensor_copy(KwT[0:64, tok], ptq[:, 5, :]), reads=["ptr"], writes=["KwT%d" % tt])
            S.dma("sp", T["qr_d"][:, :, tok], qst[p][0:64], reads=["qst" + P], writes=["qr_d%d" % tt])
            S.dma("sp", T["qu_d"][:, :, tok], qut[p][0:64], reads=["qut" + P], writes=["qu_d%d" % tt])
            ptx = ptr[:, 0:384]
            for c in range(3):
                S.pe(lambda e, c=c, js=js, gp=gp: e.transpose(ptx[:, c * 128:(c + 1) * 128], xbcT[gp][:, c, js], identb),
                     reads=[xr(c), "ident"], writes=["ptr"])
            S.act(lambda e, p=p: e.copy(xtm[p], ptx[:, 0:256]), reads=["ptr"], writes=["xtm" + P])
            S.act(lambda e, p=p: e.copy(btm[p], ptx[:, 256:384]), reads=["ptr"], writes=["btm" + P])
            S.dve(lambda e, p=p: e.tensor_tensor(adt[p], dtt[p], aneg, ALU.mult), reads=["dtt" + P, "aneg"], writes=["adt" + P])
            S.pe(lambda e, p=p: e.matmul(psS[:, 0:4], utri, adt[p], start=True, stop=True), reads=["utri", "adt" + P], writes=["psS_a"])
            S.act(lambda e, p=p: e.copy(acol[p], psS[:, 0:4]), reads=["psS_a"], writes=["acol" + P])
            S.dve(lambda e, p=p: e.tensor_scalar(nacol[p], psS[:, 0:4], -1.0, None, ALU.mult), reads=["psS_a"], writes=["nacol" + P])
            S.act(lambda e, p=p: e.activation(eac[p], acol[p], AF.Exp), reads=["acol" + P], writes=["eac" + P])
            S.pe(lambda e, js=js, gp=gp: e.matmul(psS[:, 256:384], xbcT[gp][:, 2, js], xbcT[gp][:, 3, js], start=True, stop=True),
                 reads=[xr(2), xr(3)], writes=["psS_c"])
            S.dve(lambda e, p=p: e.tensor_tensor(cbm[p], psS[:, 256:384], utri, ALU.mult), reads=["psS_c", "utri"], writes=["cbm" + P])
            S.dve(lambda e, p=p: e.tensor_tensor(rhs4[p], utri.unsqueeze(1).to_broadcast([128, 4, 128]), b4(adt[p], 128), ALU.mult),
                  reads=["utri", "adt" + P], writes=["rhs4"], c=0.6)
            S.pe(lambda e, p=p: e.matmul(psR, onesf, rhs4[p].rearrange("p a b -> p (a b)"), start=True, stop=True), reads=["onesf", "rhs4"], writes=["psR"], c=0.9)
            psR4 = psR.rearrange("p (a b) -> p a b", a=4)
            S.dve(lambda e, p=p: e.tensor_tensor(seg4[p], psR4, b4(acol[p], 128), ALU.subtract), reads=["psR", "acol" + P], writes=["seg4"], c=0.7)
            S.dve(lambda e, p=p: e.tensor_scalar(seg4[p], seg4[p], 0.0, None, ALU.min), reads=["seg4"], writes=["seg4"], c=0.35)
            S.act(lambda e, p=p: e.activation(seg4[p], seg4[p], AF.Exp), reads=["seg4"], writes=["seg4"], c=0.6)
            S.dve(lambda e, p=p: e.tensor_tensor(MT[p], seg4[p], cbm[p].unsqueeze(1).to_broadcast([128, 4, 128]), ALU.mult),
                  reads=["seg4", "cbm" + P], writes=["MT" + P], c=0.6)
            S.dve(lambda e, p=p: e.tensor_copy(alast[p], psR4[:, :, 127]), reads=["psR"], writes=["alast" + P])
            S.dve(lambda e, p=p: e.tensor_tensor(dsv[p], nacol[p], alast[p], ALU.add), reads=["nacol" + P, "alast" + P], writes=["dsv" + P])
            S.act(lambda e, p=p: e.activation(dsv[p], dsv[p], AF.Exp), reads=["dsv" + P], writes=["dsv" + P])
            S.act(lambda e, p=p: e.activation(cdv[p], alast[p], AF.Exp), reads=["alast" + P], writes=["cdv" + P])
            S.dve(lambda e, p=p: e.tensor_tensor(wsc[p], dtt[p], dsv[p], ALU.mult), reads=["dtt" + P, "dsv" + P], writes=["wsc" + P])
            v4 = lambda ap: ap.rearrange("p (h d) -> p h d", h=4)
            S.dve(lambda e, p=p: e.tensor_tensor(v4(xw[p]), v4(xtm[p]), b4(wsc[p], 64), ALU.mult), reads=["xtm" + P, "wsc" + P], writes=["xw" + P])
            S.dve(lambda e, p=p: e.tensor_tensor(v4(xdt[p]), v4(xtm[p]), b4(dtt[p], 64), ALU.mult), reads=["xtm" + P, "dtt" + P], writes=["xdt" + P])
            S.pool(lambda e, p=p: e.tensor_copy(hstb[p], hst), reads=["hst"], writes=["hstb" + P])
            S.pe(lambda e, js=js, gp=gp, p=p: e.matmul(psY[:, 0:256], xbcT[gp][:, 3, js], hstb[p], start=True, stop=True), reads=[xr(3), "hstb" + P], writes=["psY_o"])
            for h in range(4):
                S.pe(lambda e, h=h, p=p: e.matmul(psY[:, 256 + h * 64:256 + (h + 1) * 64], MT[p][:, h, :], xdt[p][:, h * 64:(h + 1) * 64], start=True, stop=True),
                     reads=["MT" + P, "xdt" + P], writes=["psY_d"])
            S.pe(lambda e, p=p: e.matmul(psN, btm[p], xw[p], start=True, stop=True), reads=["btm" + P, "xw" + P], writes=["psN"])
            S.dve(lambda e, p=p: e.tensor_tensor(v4(hst), v4(hst), b4(cdv[p], 64), ALU.mult), reads=["hst", "cdv" + P], writes=["hst"])
            S.dve(lambda e: e.tensor_tensor(hst, hst, psN, ALU.add), reads=["hst", "psN"], writes=["hst"])
            S.act(lambda e, p=p: e.copy(ydg[p], psY[:, 256:512]), reads=["psY_d"], writes=["ydg"])
            S.dve(lambda e, p=p: e.tensor_tensor(v4(yy[p]), v4(psY[:, 0:256]), b4(eac[p], 64), ALU.mult), reads=["psY_o", "eac" + P], writes=["yy"])
            S.pool(lambda e, p=p: e.tensor_tensor(v4(tmpd[p]), v4(xtm[p]), b4(hpb[:, 8:12], 64), ALU.mult), reads=["xtm" + P, "hpb"], writes=["tmpd"])
            S.dve(lambda e, p=p: e.tensor_tensor(yy[p], yy[p], ydg[p], ALU.add), reads=["yy", "ydg"], writes=["yy"])
            S.dve(lambda e, p=p: e.tensor_tensor(yy[p], yy[p], tmpd[p], ALU.add), reads=["yy", "tmpd"], writes=["yy"])
            S.dve(lambda e, p=p: e.tensor_tensor(yo[p], yy[p], zs[p], ALU.mult), reads=["yy", "zs" + P], writes=["yo" + P])
            S.dma("sp", T["mixo"][tok, 0:256], yo[p], reads=["yo" + P], writes=["mixo_s%d" % tt])

    if int(os.environ.get('MIX_STOP', '9')) <= 1:
        return
    bar()
    A.reset(base_persist)
    attO = A.alloc([NTT, 256], F32)
    nmT = A.alloc([SEQ], BF16)
    w1 = A.alloc([32, 256], BF16)
    S.dma("pool", w1[0:64], T["w1k"].rearrange("d (l h) -> d l h", l=32), writes=["w1k"])
    S.dma("pool", w1[64:128], T["w1v"].rearrange("d (l h) -> d l h", l=32), writes=["w1v"])
    pe_ = A.alloc([32], BF16)
    S.dma("pool", pe_[0:64], T["pek"], writes=["pek"])
    S.dma("pool", pe_[64:128], T["pev"], writes=["pev"])
    w2 = A.alloc([2, 2, 64], BF16)
    S.dma("pool", w2[:, 0], T["w2k"].rearrange("(c p) d -> p c d", p=128), writes=["w2k"])
    S.dma("pool", w2[:, 1], T["w2v"].rearrange("(c p) d -> p c d", p=128), writes=["w2v"])
    cbias = A.alloc([4], F32)
    hsb = A.alloc([4, 256], BF16)
    KcT = A.alloc([256], BF16)
    VcA = A.alloc([2, 129], BF16)
    S.pool(lambda e: e.memset(hsb, 0.0), writes=["hsb"])
    S.pool(lambda e: e.memset(VcA, 0.0), writes=["VcA"])
    S.pool(lambda e: e.memset(KcT, 0.0), writes=["KcT"])
    S.pool(lambda e: e.memset(VcA[:, :, 64:65], 1.0), reads=["VcA"], writes=["VcA"])
    S.dma("pool", VcA[:, :, 65:129], T["ovl"].rearrange("(c p) j -> p c j", p=128), reads=["VcA"], writes=["VcA"])
    for kv in range(2):
        rows = slice(kv * 64, (kv + 1) * 64)
        for hc in range(2):
            idx = kv * 2 + hc
            pb_ = pbig[:, idx, 0:255]
            pbias = pbig[:, 4, idx:idx + 1]
            for l in range(32):
                S.pe(lambda e, rows=rows, hc=hc, l=l, pbias=pbias: e.matmul(pbias, w1[rows, l, hc * 128:(hc + 1) * 128], pe_[rows, l:l + 1], start=(l == 0), stop=(l == 31)),
                     reads=["w1k", "w1v", "pek", "pev"], writes=["pbias%d" % idx])
            S.act(lambda e, idx=idx, pbias=pbias: e.copy(cbias[:, idx:idx + 1], pbias), reads=["pbias%d" % idx], writes=["cbias%d" % idx])
            for l in range(32):
                S.pe(lambda e, rows=rows, hc=hc, l=l, pb_=pb_: e.matmul(pb_, w1[rows, l, hc * 128:(hc + 1) * 128], kcvcT[rows, l:l + 16 * 254 + 1:16], start=(l == 0), stop=(l == 31)),
                     reads=["w1k", "w1v", "kcvcT"], writes=["phid%d" % idx])
            S.act(lambda e, idx=idx, pb_=pb_: e.activation(hsb[:, idx, 0:255], pb_, AF.Silu, bias=cbias[:, idx:idx + 1]), reads=["phid%d" % idx, "cbias%d" % idx, "hsb"], writes=["hsb%d" % idx])
    pk = pbig[0:64, 5, 0:255]
    for hc in range(2):
        S.pe(lambda e, hc=hc: e.matmul(pk, w2[:, 0, hc, :], hsb[:, hc, 0:255], start=(hc == 0), stop=(hc == 1)), reads=["w2k", "hsb0", "hsb1"], writes=["pk"])
    S.act(lambda e: e.copy(KcT[0:64, 0:255], pk), reads=["pk", "KcT"], writes=["KcT"])
    for it in range(2):
        m = 128 if it == 0 else 127
        pv = pbig[0:m, 6 + it, 0:64]
        for hc in range(2):
            S.pe(lambda e, it=it, hc=hc, m=m, pv=pv: e.matmul(pv, hsb[:, 2 + hc, it * 128:it * 128 + m], w2[:, 1, hc, :], start=(hc == 0), stop=(hc == 1)),
                 reads=["w2v", "hsb2", "hsb3"], writes=["pv%d" % it])
        S.act(lambda e, it=it, m=m, pv=pv: e.copy(VcA[0:m, it, 0:64], pv), reads=["pv%d" % it, "VcA"], writes=["VcA"])

    if int(os.environ.get('MIX_STOP', '9')) <= 2:
        return
    bar()
    base3 = A.off
    qu = [A.alloc([4, 512], BF16) for _ in range(2)]
    cmk = [A.alloc([2, 512], BF16) for _ in range(2)]
    PcT = [A.alloc([2, 512], BF16) for _ in range(2)]
    imp = A.alloc([4, 64], F32)
    imp2 = A.alloc([64], F32)
    m8 = A.alloc([16], F32)
    thr = A.alloc([1], F32)
    rr = A.alloc([32], F32)
    nmb = A.alloc([128], BF16)
    S.pool(lambda e: e.memset(nmb, 0.0), writes=["nmb"])
    pnt = pbf(pbig, 7, 8)[:, 0:128]
    for Q in range(8):
        qb = Q % 2
        qs = slice(Q * 512, (Q + 1) * 512)
        S.dma("sp", qu[qb][0:64], T["qu_d"][:, :, qs], writes=["qu%d" % qb])
        S.dma("pool", cmk[qb], T["cmask"][:, qs].rearrange("(c p) t -> p c t", p=128), writes=["cmk%d" % qb])
        for h in range(4):
            pb2 = h % 2
            for it in range(2):
                ps_ = pbig[:, pb2 * 2 + it, :]
                S.pe(lambda e, it=it, h=h, qb=qb, ps_=ps_: e.matmul(ps_, KcT[0:64, it * 128:(it + 1) * 128], qu[qb][0:64, h, :], start=True, stop=False),
                     reads=["KcT", "qu%d" % qb], writes=["psc%d_%d" % (pb2, it)])
                S.pe(lambda e, it=it, qb=qb, ps_=ps_: e.matmul(ps_, identb, cmk[qb][:, it, :], start=False, stop=True),
                     reads=["ident", "cmk%d" % qb], writes=["psc%d_%d" % (pb2, it)])
                S.act(lambda e, it=it, pb2=pb2, ps_=ps_: e.activation(PcT[pb2][:, it, :], ps_, AF.Exp, scale=SCALE), reads=["psc%d_%d" % (pb2, it)], writes=["PcT%d_%d" % (pb2, it)])
            for sub in range(4):
                tt = Q * 4 + sub
                po = pbig[:, 4 + (sub % 2), 0:129]
                for it in range(2):
                    S.pe(lambda e, it=it, pb2=pb2, sub=sub, po=po: e.matmul(po, PcT[pb2][:, it, sub * 128:(sub + 1) * 128], VcA[:, it, :], start=(it == 0), stop=(it == 1)),
                         reads=["PcT%d_0" % pb2, "PcT%d_1" % pb2, "VcA"], writes=["po%d" % (sub % 2)])
                pr = ["po%d" % (sub % 2)]
                ri = ((h % 2) * 4 + sub) * 2
                S.dve(lambda e, ri=ri, po=po: e.tensor_scalar(rr[:, ri:ri + 1], po[:, 64:65], 1e-30, None, ALU.add), reads=pr, writes=["rr0_%d" % ri])
                S.dve(lambda e, ri=ri: e.reciprocal(rr[:, ri:ri + 1], rr[:, ri:ri + 1]), reads=["rr0_%d" % ri], writes=["rr0_%d" % ri])
                if h == 0:
                    S.dve(lambda e, ri=ri, po=po, sub=sub: e.tensor_scalar(imp[:, sub, :], po[:, 65:129], rr[:, ri:ri + 1], None, ALU.mult), reads=pr + ["rr0_%d" % ri], writes=["imp%d" % sub])
                else:
                    S.dve(lambda e, ri=ri, po=po, sub=sub: e.scalar_tensor_tensor(imp[:, sub, :], po[:, 65:129], rr[:, ri:ri + 1], imp[:, sub, :], ALU.mult, ALU.add),
                          reads=pr + ["rr0_%d" % ri, "imp%d" % sub], writes=["imp%d" % sub])
                S.dve(lambda e, ri=ri, tt=tt, h=h: e.tensor_tensor(rr[:, ri + 1:ri + 2], rr[:, ri:ri + 1], gate[:, tt, h * 3:h * 3 + 1], ALU.mult), reads=["rr0_%d" % ri, "gate"], writes=["rr1_%d" % ri])
                S.dve(lambda e, ri=ri, po=po, tt=tt, h=h: e.tensor_scalar(attO[:, tt, h * 64:(h + 1) * 64], po[:, 0:64], rr[:, ri + 1:ri + 2], None, ALU.mult), reads=pr + ["rr1_%d" % ri], writes=["attO%d" % tt])
        for sub in range(4):
            tt = Q * 4 + sub
            ir = ["imp%d" % sub]
            im = imp[:, sub, :]
            S.pool(lambda e, im=im: e.memset(im[:, 0:1], 1e4), reads=ir, writes=ir)
            lo = max(2 * tt - 1, 0)
            S.pool(lambda e, im=im, lo=lo, tt=tt: e.memset(im[0:64, lo:2 * tt + 1], 1e4), reads=ir, writes=ir)
            S.pool(lambda e, im=im, tt=tt: e.memset(im[64:128, 2 * tt:2 * tt + 2], 1e4), reads=ir, writes=ir)
            if 2 * tt + 1 < 64:
                S.pool(lambda e, im=im, tt=tt: e.memset(im[0:64, 2 * tt + 1:64], -1.0), reads=ir, writes=ir)
            if 2 * tt + 2 < 64:
                S.pool(lambda e, im=im, tt=tt: e.memset(im[64:128, 2 * tt + 2:64], -1.0), reads=ir, writes=ir)
            S.dve(lambda e, im=im: e.max(m8[:, 0:8], im), reads=ir, writes=["m8a"])
            S.dve(lambda e, im=im: e.match_replace(imp2, m8[:, 0:8], im, -1e30), reads=ir + ["m8a"], writes=["imp2"])
            S.dve(lambda e: e.max(m8[:, 8:16], imp2), reads=["imp2"], writes=["m8b"])
            S.dve(lambda e: e.tensor_scalar(thr, m8[:, 15:16], 0.0, None, ALU.max), reads=["m8b"], writes=["thr"])
            S.dve(lambda e, im=im: e.tensor_scalar(nmb[:, 64:128], im, thr, NEG, ALU.is_lt, ALU.mult), reads=ir + ["thr", "nmb"], writes=["nmb"])
            S.pe(lambda e: e.transpose(pnt, nmb, identb), reads=["nmb", "ident"], writes=["pnt"])
            S.act(lambda e, tt=tt: e.copy(nmT[64:128, tt * 128:(tt + 1) * 128], pnt[64:128, :]), reads=["pnt"], writes=["nmT"])

    if int(os.environ.get('MIX_STOP', '9')) <= 3:
        return
    bar()
    A.reset(base3)
    Qa = [A.alloc([4, 512], BF16) for _ in range(2)]
    PT = [A.alloc([512], BF16) for _ in range(4)]
    PW = [A.alloc([512], BF16) for _ in range(3)]
    r4 = A.alloc([8], F32)
    oTs = [A.alloc([512], F32) for _ in range(2)]
    pti = 0
    pwi = 0
    for Q in range(8):
        qb = Q % 2
        qs = slice(Q * 512, (Q + 1) * 512)
        S.dma("sp", Qa[qb][0:64], T["qr_d"][:, :, qs], writes=["Qa%d" % qb])
        for h in range(4):
            S.pool(lambda e, qb=qb, h=h, qs=qs: e.tensor_copy(Qa[qb][64:128, h, :], nmT[64:128, qs]), reads=["nmT"], writes=["Qm%d_%d" % (qb, h)])
        for h in range(4):
            qr_ = ["Qa%d" % qb, "Qm%d_%d" % (qb, h)]
            posel = pbig[:, 6, 0:260].rearrange("p (s c) -> p s c", s=4)
            powin = pbig[:, 7, 0:260].rearrange("p (s c) -> p s c", s=4)
            for kt in range(4 * Q + 4):
                ks_ = slice(kt * 128, (kt + 1) * 128)
                sb_ = kt % 3
                ps_ = pbig[:, sb_, :]
                pres = "pss%d" % sb_
                o = kt - 4 * Q
                if o < 0:
                    S.pe(lambda e, ks_=ks_, qb=qb, h=h, ps_=ps_: e.matmul(ps_, KsA[:, ks_], Qa[qb][:, h, :], start=True, stop=True),
                         reads=["KsA", "KsE"] + qr_, writes=[pres])
                    lo = 0
                else:
                    lo = o * 128
                    S.pe(lambda e, ks_=ks_, qb=qb, h=h, ps_=ps_, lo=lo: e.matmul(ps_[:, lo:lo + 128], KsA[:, ks_], Qa[qb][:, h, lo:lo + 128], start=True, stop=False),
                         reads=["KsA", "KsE"] + qr_, writes=[pres])
                    S.pe(lambda e, ps_=ps_, lo=lo: e.matmul(ps_[:, lo:lo + 128], identb, tricb, start=False, stop=True), reads=["ident", "tric"], writes=[pres])
                    if o < 3:
                        S.pe(lambda e, ks_=ks_, qb=qb, h=h, ps_=ps_, lo=lo: e.matmul(ps_[:, lo + 128:512], KsA[:, ks_], Qa[qb][:, h, lo + 128:512], start=True, stop=True),
                             reads=["KsA", "KsE"] + qr_, writes=[pres])
                pt_ = PT[pti % 4]
                ptres = "PT%d" % (pti % 4)
                pti += 1
                S.act(lambda e, ps_=ps_, pt_=pt_, lo=lo: e.activation(pt_[:, lo:512], ps_[:, lo:512], AF.Exp, scale=SCALE), reads=[pres], writes=[ptres])
                S.pe(lambda e, pt_=pt_, kt=kt, Q=Q, lo=lo: e.matmul(pbig[0:65, 6, lo:512], VsA[:, kt, :], pt_[:, lo:512], start=(kt == 0), stop=(kt == 4 * Q + 3)),
                     reads=[ptres, "VsA"], writes=["posel"], c=0.22)
            ob_ = (Q * 4 + h) % 2
            S.act(lambda e, ob_=ob_: e.copy(oTs[ob_][0:65, :], pbig[0:65, 6, :]), reads=["posel"], writes=["oTs%d" % ob_], c=0.6)
            poselT = pbig[:, 5, 0:260].rearrange("p (s c) -> p s c", s=4)
            for sub in range(4):
                S.pe(lambda e, ob_=ob_, sub=sub, poselT=poselT: e.transpose(poselT[:, sub, :], oTs[ob_][0:65, sub * 128:(sub + 1) * 128], identf[0:65, 0:65]),
                     reads=["oTs%d" % ob_, "identf"], writes=["poselT"])
            S.dve(lambda e: e.memset(pbig[:, 7, 0:260], 0.0), writes=["powin"])
            for r in range(-4, 4):
                kt = 4 * Q + r
                if kt < 0:
                    continue
                ks_ = slice(kt * 128, (kt + 1) * 128)
                s_lo, s_hi = max(r, 0), min(r + 4, 3)
                wsl = pwi % 2
                psw = pbig[:, 3 + wsl, :]
                pwres = "psw%d" % wsl
                pw_ = PW[pwi % 3]
                pwr = "PW%d" % (pwi % 3)
                pwi += 1
                plain = [s_ for s_ in range(s_lo, s_hi + 1) if s_ != r and s_ != r + 4]
                for s_, msk in ((r, tricb), (r + 4, triab)):
                    if s_lo <= s_ <= s_hi:
                        cs = slice(s_ * 128, (s_ + 1) * 128)
                        S.pe(lambda e, ks_=ks_, qb=qb, h=h, cs=cs, psw=psw: e.matmul(psw[:, cs], KwT[0:64, ks_], Qa[qb][0:64, h, cs], start=True, stop=False),
                             reads=["KwT", "Qa%d" % qb], writes=[pwres])
                        S.pe(lambda e, cs=cs, psw=psw, msk=msk: e.matmul(psw[:, cs], identb, msk, start=False, stop=True), reads=["ident", "tric", "tria"], writes=[pwres])
                if plain:
                    cs = slice(plain[0] * 128, (plain[-1] + 1) * 128)
                    S.pe(lambda e, ks_=ks_, qb=qb, h=h, cs=cs, psw=psw: e.matmul(psw[:, cs], KwT[0:64, ks_], Qa[qb][0:64, h, cs], start=True, stop=True),
                         reads=["KwT", "Qa%d" % qb], writes=[pwres], c=0.2)
                ca = slice(s_lo * 128, (s_hi + 1) * 128)
                S.act(lambda e, psw=psw, pw_=pw_, ca=ca: e.activation(pw_[:, ca], psw[:, ca], AF.Exp, scale=SCALE), reads=[pwres], writes=[pwr], c=0.5)
                for s_ in range(s_lo, s_hi + 1):
                    S.pe(lambda e, pw_=pw_, s_=s_, kt=kt, powin=powin: e.matmul(powin[:, s_, :], pw_[:, s_ * 128:(s_ + 1) * 128], VwA[:, kt, :],
                                                                                start=False, stop=False, skip_group_check=True),
                         reads=[pwr, "VwA"], writes=["powin"])
            for sub in range(4):
                tt = 4 * Q + sub
                for br, (po_, pres) in enumerate(((poselT, "poselT"), (powin, "powin"))):
                    S.dve(lambda e, po_=po_, sub=sub, br=br: e.reciprocal(r4[:, sub * 2 + br:sub * 2 + br + 1], po_[:, sub, 64:65]), reads=[pres], writes=["r4_%d_%d" % (sub, br)])
                    S.dve(lambda e, sub=sub, tt=tt, h=h, br=br: e.tensor_tensor(r4[:, sub * 2 + br:sub * 2 + br + 1], r4[:, sub * 2 + br:sub * 2 + br + 1], gate[:, tt, h * 3 + 1 + br:h * 3 + 2 + br], ALU.mult),
                          reads=["r4_%d_%d" % (sub, br), "gate"], writes=["r4_%d_%d" % (sub, br)])
                    S.dve(lambda e, po_=po_, sub=sub, tt=tt, h=h, br=br: e.scalar_tensor_tensor(attO[:, tt, h * 64:(h + 1) * 64], po_[:, sub, 0:64], r4[:, sub * 2 + br:sub * 2 + br + 1],
                                                                                                  attO[:, tt, h * 64:(h + 1) * 64], ALU.mult, ALU.add),
                          reads=[pres, "r4_%d_%d" % (sub, br), "attO%d" % tt], writes=["attO%d" % tt])
        for sub in range(4):
            tt = 4 * Q + sub
            S.dma("sp", T["mixo"][tt * 128:(tt + 1) * 128, 256:512], attO[:, tt, :], reads=["attO%d" % tt])


def _perm_cols():
    return None


def run_mix(inputs):
    if "mix" not in _CACHE:
        _CACHE["mix"] = build_mix()
    nc = _CACHE["mix"]
    x = inputs["x"]
    w_in = inputs["w_in"][0]
    offs = np.cumsum([0, 1024, 1536, 16, 1024, 256, 256, 256, 256, 256, 256, 48])
    oz, oxbc, odt, oq, okc, ovc, oks, ovs, okw, ovw, ogate = offs[:11]
    conv_w = inputs["conv_w"][0]
    conv_b = inputs["conv_b"][0]
    t = np.arange(SEQ, dtype=np.float32)
    inv = (1.0 / (500000.0 ** (np.arange(0, 16, 2, dtype=np.float32) / np.float32(16)))).astype(np.float32)
    ang = (t[:, None] * inv[None, :]).astype(np.float32)
    rope = np.concatenate([np.cos(ang), np.sin(ang)], 1).astype(np.float32)
    rope = np.ascontiguousarray(rope.reshape(NTT, 128, 16).transpose(1, 0, 2).reshape(128, NTT * 16))
    ident = np.eye(128, dtype=np.float32)
    kk = np.arange(128)[:, None]
    qq = np.arange(128)[None, :]
    tric = np.where(kk <= qq, 0.0, NEG).astype(np.float32)
    tria = np.where(kk > qq, 0.0, NEG).astype(np.float32)
    utri = (kk <= qq).astype(np.float32)
    emat = (np.arange(SEQ)[None, :] // 64 == np.arange(64)[:, None]).astype(np.float32)
    ii = np.arange(256)[:, None]
    cmask = np.where((16 * ii + 31 <= np.arange(SEQ)[None, :]) & (ii < 255), 0.0, NEG).astype(np.float32)
    cs = np.arange(255)[:, None] * 16
    ss_ = np.arange(64)[None, :] * 64
    ov = np.clip(np.minimum(cs + 32, ss_ + 64) - np.maximum(cs, ss_), 0, None) / 32.0
    ovl = np.zeros((256, 64), np.float32)
    ovl[:255] = ov
    in_maps = []
    for c in range(8):
        b, g = c // 4, c % 4
        grp = g // 2
        ar = np.arange
        tm_cols = np.concatenate([oz + 256 * g + ar(256), odt + 4 * g + ar(4), ogate + 12 * g + ar(12), ovs + 64 * g + ar(64), ovw + 64 * g + ar(64),
                                  oq + 256 * g + ar(256), oks + 64 * g + ar(64), okw + 64 * g + ar(64)])
        xcols = np.concatenate([256 * g + ar(256), 1024 + 128 * grp + ar(128), 1280 + 128 * grp + ar(128)])
        fm_cols = np.concatenate([oxbc + xcols, okc + 64 * g + ar(64), ovc + 64 * g + ar(64)])
        convw = np.ascontiguousarray(conv_w[:, xcols].T.reshape(4, 128, 4).transpose(1, 0, 2).reshape(128, 16))
        convb = np.ascontiguousarray(conv_b[xcols].reshape(4, 128).T)
        hp = np.concatenate([inputs["dt_bias"][0][4 * g:4 * g + 4], inputs["a_log"][0][4 * g:4 * g + 4], inputs["d_skip"][0][4 * g:4 * g + 4]]).astype(np.float32)
        in_maps.append(dict(
            xb=np.ascontiguousarray(x[b]), anw=inputs["attn_norm_w"][0], w_tm=np.ascontiguousarray(w_in[:, tm_cols]), w_fm=np.ascontiguousarray(w_in[:, fm_cols]),
            convw=convw, convb=convb, hp=hp, rope=rope,
            w1k=np.ascontiguousarray(inputs["cmp_w1_k"][0].reshape(32, 64, 256).transpose(1, 0, 2).reshape(64, 32 * 256)),
            w1v=np.ascontiguousarray(inputs["cmp_w1_v"][0].reshape(32, 64, 256).transpose(1, 0, 2).reshape(64, 32 * 256)),
            w2k=inputs["cmp_w2_k"][0], w2v=inputs["cmp_w2_v"][0],
            pek=np.ascontiguousarray(inputs["cmp_pe_k"][0].T), pev=np.ascontiguousarray(inputs["cmp_pe_v"][0].T),
            ident=ident, emat=emat, tric=tric, tria=tria, cmask=cmask, ovl=ovl, utri=utri,
        ))
    res = run_bass_kernel_spmd(nc, in_maps, core_ids=list(range(8)))
    mixed = np.empty((2, SEQ, D), np.float32)
    for c in range(8):
        b, g = c // 4, c % 4
        m = res.results[c]["mixo"]
        mixed[b, :, 256 * g:256 * (g + 1)] = m[:, 0:256]
        mixed[b, :, 1024 + 256 * g:1024 + 256 * (g + 1)] = m[:, 256:512]
    return mixed


def kernel(**inputs):
    inputs = {k: np.asarray(v) for k, v in inputs.items()}
    mixed = run_mix(inputs)
    return run_tail(inputs["x"], mixed, inputs["ssd_norm_w"][0], inputs["w_out"][0], inputs["ffn_norm_w"][0],
                    inputs["w_gate"][0], inputs["w_up"][0], inputs["w_down"][0], inputs["final_norm_w"])
```

```python
import contextlib
import os
import numpy as np
import ml_dtypes
import concourse.bass as bass
import concourse.mybir as mybir
from concourse.bass_utils import run_bass_kernel_spmd

F32 = mybir.dt.float32
BF16 = mybir.dt.bfloat16
U8 = mybir.dt.uint8
ALU = mybir.AluOpType
AF = mybir.ActivationFunctionType
AX = mybir.AxisListType

ENGS = ("pe", "act", "dve", "pool", "sp")
EPS = 1e-6


class _Op:
    __slots__ = ("eng", "fn", "dma", "deps", "idx", "signal", "val", "sem", "cc", "cost")


class Sched:
    def __init__(self, nc, n_dma_sems=8):
        self.nc = nc
        self.ops = []
        self.last_w = {}
        self.readers = {}
        self.n_dma_sems = n_dma_sems
        self.bar = None
        self.bank_of = {}

    def op(self, eng, fn, reads=(), writes=(), dma=False, c=None):
        o = _Op()
        o.cost = c
        o.eng, o.fn, o.dma = eng, fn, dma
        o.cc = False
        o.idx = len(self.ops)
        o.signal = False
        deps = {}
        if self.bar is not None:
            deps[self.bar] = "raw"
        for r in reads:
            w = self.last_w.get(r)
            if w is not None:
                deps[w] = "raw"
        for w_ in writes:
            w = self.last_w.get(w_)
            if w is not None:
                deps[w] = "raw"
            for r in self.readers.get(w_, ()):
                if r not in deps:
                    deps[r] = "war"
        for r in reads:
            self.readers.setdefault(r, []).append(o.idx)
        for w_ in writes:
            self.last_w[w_] = o.idx
            self.readers[w_] = []
        banks = set()
        for r in tuple(reads) + tuple(writes):
            banks.update(self.bank_of.get(r, ()))
        for b in banks:
            key = ("bank", b)
            w = self.last_w.get(key)
            if w is not None and w not in deps:
                deps[w] = "bank"
            self.last_w[key] = o.idx
        deps.pop(o.idx, None)
        o.deps = deps
        self.ops.append(o)
        return o

    def pe(self, fn, reads=(), writes=(), c=None):
        return self.op("pe", fn, reads, writes, c=c)

    def act(self, fn, reads=(), writes=(), c=None):
        return self.op("act", fn, reads, writes, c=c)

    def dve(self, fn, reads=(), writes=(), c=None):
        return self.op("dve", fn, reads, writes, c=c)

    def pool(self, fn, reads=(), writes=(), c=None):
        return self.op("pool", fn, reads, writes, c=c)

    DEF_COST = {"pe": 0.12, "act": 0.35, "dve": 0.25, "pool": 0.35}

    def reorder(self, window=int(os.environ.get("RWIN", "120"))):
        ops = self.ops
        n = len(ops)
        queues = {e: [] for e in ENGS}
        for o in ops:
            queues[o.eng].append(o.idx)
        head = {e: 0 for e in ENGS}
        sched = [False] * n
        fin = [0.0] * n
        etime = {e: 0.0 for e in ENGS}
        order = []
        left = n
        while left:
            best = None
            for e in ENGS:
                q = queues[e]
                h = head[e]
                while h < len(q) and sched[q[h]]:
                    h += 1
                head[e] = h
                if h >= len(q):
                    continue
                seen = 0
                i = h
                et = etime[e]
                while i < len(q) and seen < window:
                    k = q[i]
                    i += 1
                    if sched[k]:
                        continue
                    seen += 1
                    o = ops[k]
                    rdy = 0.0
                    ok = True
                    for d in o.deps:
                        if not sched[d]:
                            ok = False
                            break
                        f = fin[d] + (0.0 if ops[d].eng == e else 0.15)
                        if f > rdy:
                            rdy = f
                    if not ok:
                        continue
                    st = rdy if rdy > et else et
                    key = (st, k)
                    if best is None or key < best[0]:
                        best = (key, e, k)
                    if st <= et:
                        break
            assert best is not None, "scheduler stuck"
            (st, k), e, _ = best
            o = ops[k]
            if o.dma:
                etime[e] = st + 0.06
                fin[k] = st + (o.cost if o.cost is not None else 3.0)
            else:
                c = o.cost if o.cost is not None else self.DEF_COST[e]
                etime[e] = st + c
                fin[k] = st + c
            sched[k] = True
            order.append(k)
            left -= 1
        self.order = order
        self.est_time = max(fin) if fin else 0.0

    def dma(self, q, out, in_, reads=(), writes=(), c=None):
        return self.op(q, lambda e: e.dma_start(out=out, in_=in_), reads, writes, dma=True, c=c)

    def cc(self, fn, reads=(), writes=()):
        o = self.op("pool", fn, reads, writes, dma=True)
        o.cc = True
        return o

    def barrier(self, out, in_):
        allres = set(self.last_w.keys()) | set(self.readers.keys())
        o = self.op("sp", lambda e: e.dma_start(out=out, in_=in_), reads=(), writes=tuple(allres), dma=True)
        self.bar = o.idx
        self.last_w = {}
        self.readers = {}
        return o

    def emit(self, sems, block, final_wait_eng="sp"):
        ops = self.ops
        need = [False] * len(ops)
        for o in ops:
            for d, kind in o.deps.items():
                do = ops[d]
                if do.dma:
                    continue
                if do.eng == o.eng and not o.dma:
                    if do.eng == "pe" or kind == "bank":
                        continue
                need[d] = True
        cnt = {e: 0 for e in ENGS}
        dcnt = {}
        dval = {}
        per_eng = {e: [] for e in ENGS}
        order = getattr(self, "order", None) or list(range(len(ops)))
        for k_ in order:
            o = ops[k_]
            per_eng[o.eng].append(o)
            if o.dma and o.cc:
                o.sem = "cc"
                dval["cc"] = dval.get("cc", 0) + 1
                o.val = dval["cc"]
            elif o.dma:
                k = dcnt.get(o.eng, 0)
                dcnt[o.eng] = k + 1
                key = ("dma", o.eng, k % self.n_dma_sems)
                o.sem = key
                dval[key] = dval.get(key, 0) + 16
                o.val = dval[key]
            elif need[o.idx]:
                cnt[o.eng] += 1
                o.val = cnt[o.eng]
                o.sem = o.eng
                o.signal = True
        self.stats = {e: len(per_eng[e]) for e in ENGS}
        self.stats["signals"] = dict(cnt)

        def run(engname, e):
            waited = {}
            for o in per_eng[engname]:
                wl = {}
                for d, kind in o.deps.items():
                    do = ops[d]
                    if do.dma:
                        wl[do.sem] = max(wl.get(do.sem, 0), do.val)
                        continue
                    if do.eng == o.eng and not o.dma:
                        if do.eng == "pe" or kind == "bank":
                            continue
                    wl[do.sem] = max(wl.get(do.sem, 0), do.val)
                if o.dma and o.cc and o.val > 1:
                    wl[o.sem] = max(wl.get(o.sem, 0), o.val - 1)
                elif o.dma and not o.cc and o.val > 16:
                    wl[o.sem] = max(wl.get(o.sem, 0), o.val - 16)
                for s, v in wl.items():
                    if waited.get(s, 0) >= v:
                        continue
                    waited[s] = v
                    e.wait_ge(sems[s], v)
                ins = o.fn(e)
                if o.dma and o.cc:
                    ins.then_inc(sems[o.sem])
                elif o.dma:
                    ins.then_inc(sems[o.sem], 16)
                elif o.signal:
                    ins.then_inc(sems[o.sem], 1)
            if engname == final_wait_eng:
                for key, v in dval.items():
                    if waited.get(key, 0) < v:
                        e.wait_ge(sems[key], v)
                for en in ("pe", "act", "dve", "pool"):
                    if cnt[en] > 0 and waited.get(en, 0) < cnt[en]:
                        e.wait_ge(sems[en], cnt[en])

        @block.tensor
        def _(e):
            run("pe", e)

        @block.scalar
        def _(e):
            run("act", e)

        @block.vector
        def _(e):
            run("dve", e)

        @block.gpsimd
        def _(e):
            run("pool", e)

        @block.sync
        def _(e):
            run("sp", e)


def make_sems(nc, stack, n_dma_sems=8, queues=("sp", "pool", "act")):
    sems = {}
    for e in ("pe", "act", "dve", "pool", "cc"):
        sems[e] = stack.enter_context(nc.semaphore("s_" + e))
    for q in queues:
        for i in range(n_dma_sems):
            sems[("dma", q, i)] = stack.enter_context(nc.semaphore("d_%s_%d" % (q, i)))
    return sems


_DTSZ = {F32: 4, BF16: 2, U8: 1}


class Arena:
    def __init__(self, ar, size):
        self.ar, self.size, self.off = ar, size, 0

    def reset(self, off=0):
        self.off = off

    def alloc(self, shape, dtype, parts=128):
        n = int(np.prod(shape)) * _DTSZ[dtype]
        off = (self.off + 63) // 64 * 64
        assert off + n <= self.size, ("arena overflow", off, n, self.size)
        self.off = off + n
        ap = self.ar[0:parts, off:off + n].bitcast(dtype)
        if len(shape) > 1:
            names = [chr(ord("a") + i) for i in range(len(shape))]
            pat = "p (%s) -> p %s" % (" ".join(names), " ".join(names))
            ap = ap.rearrange(pat, **{nm: int(s) for nm, s in zip(names, shape)})
        return ap


D = 2048
FF = 5632
TOK = 1024
NT = TOK // 128
KC = D // 128
FC = FF // 128
SSDW = 1024


def build_tail():
    nc = bass.Bass("TRN2", target_bir_lowering=False)
    x = nc.dram_tensor("x_own", [TOK, D], F32, kind="ExternalInput").ap()
    mix = nc.dram_tensor("mix", [TOK, D], F32, kind="ExternalInput").ap()
    w_out = nc.dram_tensor("w_out", [D, D], F32, kind="ExternalInput").ap()
    w_gate = nc.dram_tensor("w_gate", [D, FF], F32, kind="ExternalInput").ap()
    w_up = nc.dram_tensor("w_up", [D, FF], F32, kind="ExternalInput").ap()
    w_down = nc.dram_tensor("w_down", [FF, D], F32, kind="ExternalInput").ap()
    nw = nc.dram_tensor("nw", [3, D], F32, kind="ExternalInput").ap()
    ident = nc.dram_tensor("ident", [128, 128], F32, kind="ExternalInput").ap()
    out = nc.dram_tensor("out", [TOK, D], F32, kind="ExternalOutput").ap()
    h_d = nc.dram_tensor("h_d", [TOK, D], F32, kind="Internal").ap()
    dummy = nc.dram_tensor("dummy_bar", [2, 64], F32, kind="Internal").ap()
    with contextlib.ExitStack() as st:
        ASZ = 207 * 1024
        ar = st.enter_context(nc.sbuf_tensor("arena", [128, ASZ], U8))
        A = Arena(ar, ASZ)
        pbig = st.enter_context(nc.psum_tensor("pbig", [128, 8, 512], F32))
        sems = make_sems(nc, st)
        block = st.enter_context(nc.Block())
        S = Sched(nc)
        tail_body(nc, S, A, pbig, x, mix, w_out, w_gate, w_up, w_down, nw, ident, out, h_d, dummy)
        if os.environ.get('NO_REORDER') is None:
            S.reorder()
        S.emit(sems, block)
    return nc


def rms_rstd(S, src, n, ss, sq, tag, rd, wr_extra=(), c=None):
    S.act(lambda e: e.activation(sq, src, AF.Square, accum_out=ss), reads=rd, writes=[tag + "ss", tag + "sq"], c=c)
    S.act(lambda e: e.activation(ss, ss, AF.Sqrt, scale=1.0 / n, bias=EPS_AP[0]), reads=[tag + "ss"], writes=[tag + "ss"])
    S.dve(lambda e: e.reciprocal(ss, ss), reads=[tag + "ss"], writes=[tag + "ss"])


EPS_AP = [None]


def tail_body(nc, S, A, pbig, x, mix, w_out, w_gate, w_up, w_down, nw, ident, out, h_d, dummy):
    bk = {"ptr": (4, 5)}
    for i_ in range(8):
        bk["pacc%d" % i_] = (i_,)
        bk["pd%d" % i_] = (i_,)
    for i_ in range(2):
        bk["pg%d" % i_] = (i_ * 4, i_ * 4 + 1)
        bk["pu%d" % i_] = (i_ * 4 + 2, i_ * 4 + 3)
    S.bank_of = bk
    identb = A.alloc([128], BF16)
    nwb_flat = A.alloc([2 * D + SSDW], F32)
    epsb = A.alloc([1], F32)
    ss = A.alloc([4], F32)
    EPS_AP[0] = epsb
    S.pool(lambda e: e.memset(epsb, EPS), writes=["eps"])
    S.dma("pool", identb, ident, writes=["ident"])
    S.dma("sp", nwb_flat, nw.rearrange("a b -> (a b)")[0:2 * D + SSDW].partition_broadcast(128), writes=["nwb"])

    class _NW:
        def __getitem__(self, key):
            _, row, cols = key
            base = {1: 0, 2: D, 0: 2 * D}[row]
            lo = cols.start or 0
            hi = cols.stop if cols.stop is not None else (SSDW if row == 0 else D)
            return nwb_flat[:, base + lo:base + hi]
    nwb = _NW()
    vT = A.alloc([KC, TOK], BF16)
    base_persist = A.off

    wo = A.alloc([KC, D], BF16)
    for cb in range(4):
        S.dma("pool", wo[:, :, cb * 512:(cb + 1) * 512],
              w_out[:, cb * 512:(cb + 1) * 512].rearrange("(k p) n -> p k n", p=128), writes=["wo%d" % cb], c=25.0)
    xt = [A.alloc([D], F32) for _ in range(2)]
    mt = [A.alloc([D], F32) for _ in range(2)]
    sq = A.alloc([D], BF16)
    mb = A.alloc([D], BF16)
    mT = A.alloc([KC, 128], BF16)
    hs = [A.alloc([D], F32) for _ in range(2)]
    vb = A.alloc([D], BF16)
    WB = 256
    A.alloc([1024], F32)
    pre_lo = (A.off + 63) // 64 * 64
    wg_pre = A.alloc([KC, WB], BF16)
    wu_pre = A.alloc([KC, WB], BF16)
    S.dma("pool", wg_pre, w_gate[:, 0:WB].rearrange("(k p) n -> p k n", p=128), writes=["wg0"], c=15.0)
    S.dma("pool", wu_pre, w_up[:, 0:WB].rearrange("(k p) n -> p k n", p=128), writes=["wu0"], c=15.0)
    pacc = pbig[:, 0:4, :]
    ptr_all = pbig[:, 4:6, :].rearrange("p a b -> p (a b)").bitcast(BF16)
    ptr = ptr_all.rearrange("p (k n) -> p k n", k=KC)

    def loads(tt):
        b = tt % 2
        S.dma("sp", xt[b], x[tt * 128:(tt + 1) * 128, :], writes=["xt%d" % b])
        S.dma("sp", mt[b], mix[tt * 128:(tt + 1) * 128, :], writes=["mt%d" % b])

    loads(0)
    for tt in range(NT):
        b = tt % 2
        if tt + 1 < NT:
            loads(tt + 1)
        rms_rstd(S, mt[b][:, 0:SSDW], SSDW, ss[:, 0:1], sq[:, 0:SSDW], "a", ["mt%d" % b, "eps"])
        S.dve(lambda e, b=b: e.scalar_tensor_tensor(mb[:, 0:SSDW], mt[b][:, 0:SSDW], ss[:, 0:1], nwb[:, 0, 0:SSDW], ALU.mult, ALU.mult),
              reads=["mt%d" % b, "ass", "nwb"], writes=["mb0"])
        S.pool(lambda e, b=b: e.tensor_copy(mb[:, SSDW:D], mt[b][:, SSDW:D]), reads=["mt%d" % b], writes=["mb1"])
        for kc in range(KC):
            S.pe(lambda e, kc=kc: e.transpose(ptr[:, kc, :], mb[:, kc * 128:(kc + 1) * 128], identb),
                 reads=["mb0", "mb1", "ident"], writes=["ptr"])
        S.act(lambda e: e.copy(mT[:, 0:8, :], ptr[:, 0:8, :]), reads=["ptr"], writes=["mTa"])
        S.dve(lambda e: e.tensor_copy(mT[:, 8:16, :], ptr[:, 8:16, :]), reads=["ptr"], writes=["mTb"])
        for cb in range(4):
            for kc in range(KC):
                S.pe(lambda e, cb=cb, kc=kc: e.matmul(pacc[:, cb, :], mT[:, kc, :], wo[:, kc, cb * 512:(cb + 1) * 512],
                                                       start=(kc == 0), stop=(kc == KC - 1)),
                     reads=["mTa", "mTb", "wo%d" % cb], writes=["pacc%d" % cb], c=0.22)
            S.dve(lambda e, cb=cb, b=b: e.tensor_tensor(hs[b][:, cb * 512:(cb + 1) * 512], pacc[:, cb, :], xt[b][:, cb * 512:(cb + 1) * 512], ALU.add),
                  reads=["pacc%d" % cb, "xt%d" % b], writes=["hs%d_%d" % (b, cb)])
        hres = ["hs%d_%d" % (b, cb) for cb in range(4)]
        S.dma("sp", h_d[tt * 128:(tt + 1) * 128, :], hs[b], reads=hres, writes=["h_d%d" % tt])
        rms_rstd(S, hs[b], D, ss[:, 1:2], sq, "b", hres + ["eps"])
        S.dve(lambda e, b=b: e.scalar_tensor_tensor(vb, hs[b], ss[:, 1:2], nwb[:, 1, :], ALU.mult, ALU.mult),
              reads=hres + ["bss", "nwb"], writes=["vb"])
        for kc in range(KC):
            S.pe(lambda e, kc=kc: e.transpose(ptr[:, kc, :], vb[:, kc * 128:(kc + 1) * 128], identb),
                 reads=["vb", "ident"], writes=["ptr"])
        S.act(lambda e, tt=tt: e.copy(vT[:, 0:8, tt * 128:(tt + 1) * 128], ptr[:, 0:8, :]), reads=["ptr"], writes=["vT%da" % tt])
        S.dve(lambda e, tt=tt: e.tensor_copy(vT[:, 8:16, tt * 128:(tt + 1) * 128], ptr[:, 8:16, :]), reads=["ptr"], writes=["vT%db" % tt])

    S.barrier(dummy[1:2, :], ident[0:1, 0:64])
    A.reset(base_persist)
    hT = A.alloc([FC, TOK], BF16)
    HK = 22
    wd_pre = A.alloc([HK, 512], BF16)
    base_b = A.off
    NB = FF // WB
    wg = [wg_pre, A.alloc([KC, WB], BF16)]
    wu = [wu_pre, A.alloc([KC, WB], BF16)]
    sg = [A.alloc([TOK], BF16) for _ in range(2)]
    assert A.off <= pre_lo, (A.off, pre_lo)
    for blk in range(NB):
        b = blk % 2
        if blk > 0:
            S.dma("pool", wg[b], w_gate[:, blk * WB:(blk + 1) * WB].rearrange("(k p) n -> p k n", p=128), writes=["wg%d" % b], c=15.0)
            S.dma("pool", wu[b], w_up[:, blk * WB:(blk + 1) * WB].rearrange("(k p) n -> p k n", p=128), writes=["wu%d" % b], c=15.0)
        if blk == NB - 2:
            S.dma("pool", wd_pre, w_down[0:HK * 128, 0:512].rearrange("(k p) n -> p k n", p=128), writes=["wd0"], c=15.0)
        for j in range(WB // 128):
            fc = blk * (WB // 128) + j
            pb = fc % 2
            pg = pbig[:, pb * 4:pb * 4 + 2, :]
            pu = pbig[:, pb * 4 + 2:pb * 4 + 4, :]
            for hf in range(2):
                for kc in range(KC):
                    S.pe(lambda e, b=b, j=j, hf=hf, kc=kc, pg=pg: e.matmul(pg[:, hf, :], wg[b][:, kc, j * 128:(j + 1) * 128], vT[:, kc, hf * 512:(hf + 1) * 512],
                                                                         start=(kc == 0), stop=(kc == KC - 1)),
                         reads=["wg%d" % b, "vT"], writes=["pg%d" % pb], c=0.22)
            for hf in range(2):
                for kc in range(KC):
                    S.pe(lambda e, b=b, j=j, hf=hf, kc=kc, pu=pu: e.matmul(pu[:, hf, :], wu[b][:, kc, j * 128:(j + 1) * 128], vT[:, kc, hf * 512:(hf + 1) * 512],
                                                                         start=(kc == 0), stop=(kc == KC - 1)),
                         reads=["wu%d" % b, "vT"], writes=["pu%d" % pb], c=0.22)
            S.act(lambda e, pb=pb, pg=pg: e.activation(sg[pb], pg.rearrange("p a b -> p (a b)"), AF.Silu), reads=["pg%d" % pb], writes=["sg%d" % pb])
            S.dve(lambda e, pb=pb, pu=pu, fc=fc: e.tensor_tensor(hT[:, fc, :], sg[pb], pu.rearrange("p a b -> p (a b)"), ALU.mult),
                  reads=["sg%d" % pb, "pu%d" % pb], writes=["hT%d" % fc])

    S.barrier(dummy[1:2, :], ident[0:1, 0:64])
    A.reset(base_b)
    wd = [wd_pre, A.alloc([HK, 512], BF16)]
    hl = [A.alloc([512], F32) for _ in range(2)]
    ys = [A.alloc([512], F32) for _ in range(2)]
    it = 0
    for r in range(4):
        for hf in range(2):
            b = (r * 2 + hf) % 2
            if r * 2 + hf > 0:
                S.dma("pool", wd[b], w_down[hf * HK * 128:(hf + 1) * HK * 128, r * 512:(r + 1) * 512].rearrange("(k p) n -> p k n", p=128), writes=["wd%d" % b], c=15.0)
            for tt in range(NT):
                for k in range(HK):
                    kk = hf * HK + k
                    S.pe(lambda e, b=b, tt=tt, k=k, kk=kk: e.matmul(pbig[:, tt, :], hT[:, kk, tt * 128:(tt + 1) * 128], wd[b][:, k, :],
                                                                    start=(kk == 0), stop=(kk == FC - 1)),
                         reads=["wd%d" % b, "hT"], writes=["pd%d" % tt], c=0.22)
        for tt in range(NT):
            b = it % 2
            it += 1
            S.dma("sp", hl[b], h_d[tt * 128:(tt + 1) * 128, r * 512:(r + 1) * 512], reads=["h_d%d_%d" % (tt, r)], writes=["hl%d" % b])
            S.dve(lambda e, b=b, tt=tt: e.tensor_tensor(ys[b], pbig[:, tt, :], hl[b], ALU.add), reads=["pd%d" % tt, "hl%d" % b], writes=["ys%d" % b])
            S.dma("sp", h_d[tt * 128:(tt + 1) * 128, r * 512:(r + 1) * 512], ys[b], reads=["ys%d" % b], writes=["h_d%d_%d" % (tt, r)])

    S.barrier(dummy[1:2, :], ident[0:1, 0:64])
    A.reset(base_persist)
    yt = [A.alloc([D], F32) for _ in range(2)]
    ot = [A.alloc([D], F32) for _ in range(2)]
    sq2 = A.alloc([D], F32)
    for tt in range(NT):
        b = tt % 2
        S.dma("sp", yt[b], h_d[tt * 128:(tt + 1) * 128, :], writes=["yt%d" % b])
        rms_rstd(S, yt[b], D, ss[:, 2:3], sq2, "c", ["yt%d" % b, "eps"])
        S.dve(lambda e, b=b: e.scalar_tensor_tensor(ot[b], yt[b], ss[:, 2:3], nwb[:, 2, :], ALU.mult, ALU.mult),
              reads=["yt%d" % b, "css", "nwb"], writes=["ot%d" % b])
        S.dma("sp", out[tt * 128:(tt + 1) * 128, :], ot[b], reads=["ot%d" % b])


_CACHE = {}


def run_tail(x, mixed, ssd_norm_w, w_out, ffn_norm_w, w_gate, w_up, w_down, final_norm_w):
    if "tail" not in _CACHE:
        _CACHE["tail"] = build_tail()
    nc = _CACHE["tail"]
    nwv = np.ones((3, D), np.float32)
    nwv[0] = ffn_norm_w
    nwv[1] = final_norm_w
    nwv[2, :SSDW] = ssd_norm_w
    ident = np.eye(128, dtype=np.float32)
    in_maps = []
    for c in range(8):
        b, g = c // 4, c % 4
        in_maps.append({
            "x_own": np.ascontiguousarray(x[b, g * TOK:(g + 1) * TOK]),
            "mix": np.ascontiguousarray(mixed[b, g * TOK:(g + 1) * TOK]),
            "w_out": w_out, "w_gate": w_gate, "w_up": w_up, "w_down": w_down,
            "nw": nwv, "ident": ident,
        })
    res = run_bass_kernel_spmd(nc, in_maps, core_ids=list(range(8)))
    outp = np.empty((2, 4096, D), np.float32)
    for c in range(8):
        b, g = c // 4, c % 4
        outp[b, g * TOK:(g + 1) * TOK] = res.results[c]["out"]
    return outp


SEQ = 4096
NTT = SEQ // 128
NEG = -30000.0
NA = 400
NB_ = 384
NFM = 640
SCALE = 0.125


def build_mix():
    nc = bass.Bass("TRN2", target_bir_lowering=False)
    di = lambda n, s, d=F32: nc.dram_tensor(n, s, d, kind="ExternalInput").ap()
    T = dict(
        xb=di("xb", [SEQ, D]), anw=di("anw", [D]), w_tm=di("w_tm", [D, NA + NB_]), w_fm=di("w_fm", [D, NFM]),
        convw=di("convw", [128, 16]), convb=di("convb", [128, 4]), hp=di("hp", [12]), rope=di("rope", [128, NTT * 16]),
        w1k=di("w1k", [64, 32 * 256]), w1v=di("w1v", [64, 32 * 256]), w2k=di("w2k", [256, 64]), w2v=di("w2v", [256, 64]),
        pek=di("pek", [64, 32]), pev=di("pev", [64, 32]), ident=di("ident", [128, 128]), emat=di("emat", [64, SEQ]),
        tric=di("tric", [128, 128]), tria=di("tria", [128, 128]), cmask=di("cmask", [256, SEQ]), ovl=di("ovl", [256, 64]),
        utri=di("utri", [128, 128]),
    )
    T["mixo"] = nc.dram_tensor("mixo", [SEQ, 512], F32, kind="ExternalOutput").ap()
    T["qr_d"] = nc.dram_tensor("qr_d", [64, 4, SEQ], BF16, kind="Internal").ap()
    T["qu_d"] = nc.dram_tensor("qu_d", [64, 4, SEQ], BF16, kind="Internal").ap()
    T["dummy"] = nc.dram_tensor("dummy_bar", [2, 64], F32, kind="Internal").ap()
    with contextlib.ExitStack() as st:
        ASZ = 207 * 1024
        ar = st.enter_context(nc.sbuf_tensor("arena", [128, ASZ], U8))
        A = Arena(ar, ASZ)
        pbig = st.enter_context(nc.psum_tensor("pbig", [128, 8, 512], F32))
        sems = make_sems(nc, st)
        block = st.enter_context(nc.Block())
        S = Sched(nc)
        mix_body(nc, S, A, pbig, T)
        if os.environ.get('NO_REORDER') is None:
            S.reorder()
        S.emit(sems, block)
    return nc


def pbf(pb, lo, hi):
    return pb[:, lo:hi, :].rearrange("p a b -> p (a b)").bitcast(BF16)


def mix_body(nc, S, A, pbig, T):
    bar = lambda: S.barrier(T["dummy"][1:2, :], T["ident"][0:1, 0:64])
    bk = {"ptr": (0,), "ptr2": (6,), "psA": (1,), "psB": (2,), "psF0": (3,), "psR": (4,), "psS_a": (5,), "psS_c": (5,),
          "psN": (6,), "psY_o": (7,), "psY_d": (7,), "pk": (5,), "pv0": (6,), "pv1": (7,), "pnt": (7,), "posel": (6,), "powin": (7,)}
    for i_ in range(4):
        bk["pbias%d" % i_] = (4,)
        bk["phid%d" % i_] = (i_,)
        bk["psw%d" % i_] = (3 + i_ % 2,)
    for a_ in range(2):
        bk["po%d" % a_] = (4 + a_,)
        for b_ in range(2):
            bk["psc%d_%d" % (a_, b_)] = (a_ * 2 + b_,)
    for i_ in range(3):
        bk["pss%d" % i_] = (i_,)
    bk["poselT"] = (5,)
    S.bank_of = bk
    identb = A.alloc([128], BF16)
    utri = A.alloc([128], F32)
    identf = A.alloc([128], F32)
    tricb = A.alloc([128], BF16)
    triab = A.alloc([128], BF16)
    epsb = A.alloc([1], F32)
    oneb = A.alloc([1], F32)
    EPS_AP[0] = epsb
    KsA = A.alloc([SEQ], BF16)
    KwT = A.alloc([SEQ], BF16)
    kcvcT = A.alloc([SEQ], BF16)
    VsA = A.alloc([NTT, 65], BF16)
    VwA = A.alloc([NTT, 65], BF16)
    gate = A.alloc([NTT, 12], F32)
    hpb = A.alloc([12], F32)
    base_persist = A.off
    S.pool(lambda e: e.memset(epsb, EPS), writes=["eps"])
    S.pool(lambda e: e.memset(oneb, 1.0), writes=["one"])
    S.pool(lambda e: e.memset(VsA, 1.0), writes=["VsA"])
    S.pool(lambda e: e.memset(VwA, 1.0), writes=["VwA"])
    S.dma("pool", identb, T["ident"], writes=["ident"])
    S.dma("sp", utri, T["utri"], writes=["utri"])
    S.dma("sp", identf, T["ident"], writes=["identf"])
    S.dma("pool", tricb, T["tric"], writes=["tric"])
    S.dma("pool", triab, T["tria"], writes=["tria"])
    S.dma("pool", KsA[64:128, :], T["emat"], writes=["KsE"])
    S.dma("sp", hpb, T["hp"].partition_broadcast(128), writes=["hpb"])

    wtm = A.alloc([KC, NA + NB_], BF16)
    wfm = A.alloc([KC, NFM], BF16)
    S.dma("pool", wfm, T["w_fm"].rearrange("(k p) n -> p k n", p=128), writes=["wfm"], c=30.0)
    S.dma("pool", wtm, T["w_tm"].rearrange("(k p) n -> p k n", p=128), writes=["wtm"], c=35.0)
    anwb = A.alloc([D], F32)
    S.dma("sp", anwb, T["anw"].partition_broadcast(128), writes=["anwb"])
    ropet = A.alloc([NTT, 16], F32)
    S.dma("sp", ropet, T["rope"].rearrange("p (t c) -> p t c", c=16), writes=["ropet"])
    convw = A.alloc([16], F32)
    convb = A.alloc([4], F32)
    S.dma("sp", convw, T["convw"], writes=["convw"])
    S.dma("sp", convb, T["convb"], writes=["convb"])
    onesf = A.alloc([128], F32)
    S.pool(lambda e: e.memset(onesf, 1.0), writes=["onesf"])
    two = lambda shape, dt: [A.alloc(shape, dt) for _ in range(2)]
    xt = two([D], F32)
    sq1 = A.alloc([D], BF16)
    sq = [sq1, sq1]
    ss = A.alloc([8], F32)
    ub = two([D], BF16)
    uT2 = two([KC, 512], BF16)
    cbuf = A.alloc([4, 515], F32)
    cacc_1 = A.alloc([512], F32)
    cacc = [cacc_1, cacc_1]
    xbcT = two([4, 512], BF16)
    zs = two([256], BF16)
    ez_1 = A.alloc([256], F32)
    ez = [ez_1, ez_1]
    ec = two([512], F32)
    dtt = two([4], F32)
    qk = two([6, 64], F32)
    qkr = two([6, 64], BF16)
    qkb = two([4, 64], BF16)
    rt = two([4, 6, 8], F32)
    qst = two([4, 128], BF16)
    qut = two([4, 128], BF16)
    xtm = two([256], BF16)
    btm = two([128], BF16)
    hst = A.alloc([256], F32)
    hstb = two([256], BF16)
    aneg = A.alloc([4], F32)
    adt = two([4], F32)
    acol = two([4], F32)
    nacol = two([4], F32)
    eac = two([4], F32)
    rhs4_1 = A.alloc([4, 128], F32)
    rhs4 = [rhs4_1, rhs4_1]
    seg4_1 = A.alloc([4, 128], F32)
    seg4 = [seg4_1, seg4_1]
    cbm = two([128], F32)
    MT = two([4, 128], BF16)
    alast = two([4], F32)
    dsv = two([4], F32)
    cdv = two([4], F32)
    wsc = two([4], F32)
    xw = two([256], BF16)
    xdt = two([256], BF16)
    ydg_1 = A.alloc([256], F32)
    ydg = [ydg_1, ydg_1]
    yy_1 = A.alloc([256], F32)
    yy = [yy_1, yy_1]
    tmpd_1 = A.alloc([256], F32)
    tmpd = [tmpd_1, tmpd_1]
    yo = two([256], F32)
    S.pool(lambda e: e.memset(cbuf, 0.0), writes=["cbuf", "cbuf0", "cbuf1", "cbuf2", "cbuf3"])
    S.pool(lambda e: e.memset(hst, 0.0), writes=["hst"])
    S.act(lambda e: e.activation(aneg, hpb[:, 4:8], AF.Exp), reads=["hpb"], writes=["aneg"])
    S.dve(lambda e: e.tensor_scalar(aneg, aneg, -1.0, None, ALU.mult), reads=["aneg"], writes=["aneg"])

    ptr = pbf(pbig, 0, 1)
    psA = pbig[:, 1, 0:NA]
    psB = pbig[:, 2, 0:NB_]
    psF = pbig[:, 3, :]
    psR = pbig[:, 4, :]
    psS = pbig[:, 5, :]
    psN = pbig[:, 6, 0:256]
    psY = pbig[:, 7, :]

    def load_x(tt):
        S.dma("sp", xt[tt % 2], T["xb"][tt * 128:(tt + 1) * 128, :], writes=["xt%d" % (tt % 2)], c=5.0)

    def b4(ap, n):
        return ap.unsqueeze(2).to_broadcast([128, 4, n])

    load_x(0)
    for G in range(SEQ // 512):
        gp = G % 2
        uT = uT2[gp]
        for j in range(4):
            tt = G * 4 + j
            b = tt % 2
            if tt + 1 < NTT:
                load_x(tt + 1)
            ssb = ss[:, b:b + 1]
            S.act(lambda e, b=b, ssb=ssb: e.activation(sq[b], xt[b], AF.Square, accum_out=ssb), reads=["xt%d" % b], writes=["n%dss" % b, "nsq"], c=1.9)
            S.act(lambda e, ssb=ssb: e.activation(ssb, ssb, AF.Ln, scale=1.0 / D, bias=epsb), reads=["n%dss" % b, "eps"], writes=["n%dss" % b])
            S.act(lambda e, ssb=ssb: e.activation(ssb, ssb, AF.Exp, scale=-0.5), reads=["n%dss" % b], writes=["n%dss" % b])
            S.dve(lambda e, b=b: e.scalar_tensor_tensor(ub[b], xt[b], ss[:, b:b + 1], anwb, ALU.mult, ALU.mult),
                  reads=["xt%d" % b, "n%dss" % b, "anwb"], writes=["ub%d" % b], c=2.2)
            for half in range(2):
                for k8 in range(8):
                    kc = half * 8 + k8
                    S.pe(lambda e, kc=kc, k8=k8, b=b: e.transpose(ptr[:, k8 * 128:(k8 + 1) * 128], ub[b][:, kc * 128:(kc + 1) * 128], identb),
                         reads=["ub%d" % b, "ident"], writes=["ptr"])
                if half == 0:
                    S.act(lambda e, j=j, uT=uT: e.copy(uT[:, 0:8, j * 128:(j + 1) * 128], ptr.rearrange("p (k n) -> p k n", k=8)), reads=["ptr"], writes=["uT%d_%d_0" % (gp, j)], c=0.9)
                else:
                    S.dve(lambda e, j=j, uT=uT: e.tensor_copy(uT[:, 8:16, j * 128:(j + 1) * 128], ptr.rearrange("p (k n) -> p k n", k=8)), reads=["ptr"], writes=["uT%d_%d_1" % (gp, j)], c=0.7)
        uTr = ["uT%d_%d_%d" % (gp, j, h) for j in range(4) for h in range(2)]
        for c in range(5):
            for kc in range(KC):
                S.pe(lambda e, c=c, kc=kc, uT=uT: e.matmul(psF, wfm[:, kc, c * 128:(c + 1) * 128], uT[:, kc, :], start=(kc == 0), stop=(kc == KC - 1)),
                     reads=uTr + ["wfm"], writes=["psF0"], c=0.22)
            if c < 4:
                ca = cacc[c % 2]
                car = "cacc"
                S.act(lambda e, c=c: e.copy(cbuf[:, c, 3:515], psF), reads=["psF0"], writes=["cbuf%d" % c], c=0.6)
                S.dve(lambda e, c=c, ca=ca: e.tensor_scalar(ca, cbuf[:, c, 0:512], convw[:, c * 4:c * 4 + 1], convb[:, c:c + 1], ALU.mult, ALU.add), reads=["cbuf%d" % c, "convw", "convb"], writes=[car], c=0.4)
                for k in range(1, 4):
                    S.dve(lambda e, c=c, k=k, ca=ca: e.scalar_tensor_tensor(ca, cbuf[:, c, k:k + 512], convw[:, c * 4 + k:c * 4 + k + 1], ca, ALU.mult, ALU.add),
                          reads=["cbuf%d" % c, car, "convw"], writes=[car], c=0.65)
                ece = ec[c % 2]
                ecr = "ec%d" % (c % 2)
                S.act(lambda e, ca=ca, ece=ece: e.activation(ece, ca, AF.Exp, scale=-1.0), reads=[car], writes=[ecr], c=0.6)
                S.act(lambda e, ece=ece: e.activation(ece, ece, AF.Ln, bias=oneb), reads=[ecr, "one"], writes=[ecr], c=0.6)
                S.act(lambda e, ece=ece: e.activation(ece, ece, AF.Exp, scale=-1.0), reads=[ecr], writes=[ecr], c=0.6)
                S.pool(lambda e, c=c, ca=ca, ece=ece, gp=gp: e.tensor_tensor(xbcT[gp][:, c, :], ca, ece, ALU.mult), reads=[car, ecr], writes=["xbcT%d_%d" % (gp, c)], c=2.0)
                S.pool(lambda e, c=c: e.tensor_copy(cbuf[:, c, 0:3], cbuf[:, c, 512:515]), reads=["cbuf%d" % c], writes=["cbuf%d" % c])
            else:
                S.act(lambda e, G=G: e.copy(kcvcT[:, G * 512:(G + 1) * 512], psF), reads=["psF0"], writes=["kcvcT"], c=0.6)
        xr = lambda c: "xbcT%d_%d" % (gp, c)
        for j in range(4):
            tt = G * 4 + j
            p = tt % 2
            P = str(p)
            tok = slice(tt * 128, (tt + 1) * 128)
            js = slice(j * 128, (j + 1) * 128)
            for kc in range(KC):
                S.pe(lambda e, js=js, kc=kc, uT=uT: e.matmul(psA, uT[:, kc, js], wtm[:, kc, 0:NA], start=(kc == 0), stop=(kc == KC - 1)),
                     reads=uTr + ["wtm"], writes=["psA"], c=0.19)
            for kc in range(KC):
                S.pe(lambda e, js=js, kc=kc, uT=uT: e.matmul(psB, uT[:, kc, js], wtm[:, kc, NA:NA + NB_], start=(kc == 0), stop=(kc == KC - 1)),
                     reads=uTr + ["wtm"], writes=["psB"], c=0.18)
            S.act(lambda e, p=p: e.activation(ez[p], psA[:, 0:256], AF.Exp, scale=-1.0), reads=["psA"], writes=["ez"])
            S.act(lambda e, p=p: e.activation(ez[p], ez[p], AF.Ln, bias=oneb), reads=["ez", "one"], writes=["ez"])
            S.act(lambda e, p=p: e.activation(ez[p], ez[p], AF.Exp, scale=-1.0), reads=["ez"], writes=["ez"])
            S.dve(lambda e, p=p: e.tensor_tensor(zs[p], psA[:, 0:256], ez[p], ALU.mult), reads=["psA", "ez"], writes=["zs" + P])
            S.dve(lambda e, p=p: e.tensor_tensor(dtt[p], psA[:, 256:260], hpb[:, 0:4], ALU.add), reads=["psA", "hpb"], writes=["dtt" + P])
            S.act(lambda e, p=p: e.activation(dtt[p], dtt[p], AF.Exp), reads=["dtt" + P], writes=["dtt" + P])
            S.act(lambda e, p=p: e.activation(dtt[p], dtt[p], AF.Ln, bias=oneb), reads=["dtt" + P, "one"], writes=["dtt" + P])
            S.act(lambda e, tt=tt: e.activation(gate[:, tt, :], psA[:, 260:272], AF.Exp, scale=-1.0), reads=["psA"], writes=["gate%d" % tt])
            S.dve(lambda e, tt=tt: e.tensor_scalar(gate[:, tt, :], gate[:, tt, :], 1.0, None, ALU.add), reads=["gate%d" % tt], writes=["gate%d" % tt])
            S.dve(lambda e, tt=tt: e.reciprocal(gate[:, tt, :], gate[:, tt, :]), reads=["gate%d" % tt], writes=["gate%d" % tt])
            S.dve(lambda e, tt=tt: e.tensor_copy(VsA[:, tt, 0:64], psA[:, 272:336]), reads=["psA", "VsA"], writes=["VsA%d" % tt])
            S.dve(lambda e, tt=tt: e.tensor_copy(VwA[:, tt, 0:64], psA[:, 336:400]), reads=["psA", "VwA"], writes=["VwA%d" % tt])
            S.act(lambda e, p=p: e.copy(qk[p], psB.rearrange("p (a b) -> p a b", a=6)), reads=["psB"], writes=["qk" + P], c=0.5)
            S.pool(lambda e, p=p: e.tensor_copy(qkb[p], qk[p][:, 0:4, :]), reads=["qk" + P], writes=["qkb" + P])
            S.pool(lambda e, p=p: e.tensor_copy(qkr[p], qk[p]), reads=["qk" + P], writes=["qkr" + P])
            cosb = ropet[:, tt, 0:8].unsqueeze(1).to_broadcast([128, 6, 8])
            sinb = ropet[:, tt, 8:16].unsqueeze(1).to_broadcast([128, 6, 8])
            S.dve(lambda e, cosb=cosb, p=p: e.tensor_tensor(rt[p][:, 0], qk[p][:, :, 0:8], cosb, ALU.mult), reads=["qk" + P, "ropet"], writes=["rt0" + P])
            S.dve(lambda e, sinb=sinb, p=p: e.tensor_tensor(rt[p][:, 1], qk[p][:, :, 8:16], sinb, ALU.mult), reads=["qk" + P, "ropet"], writes=["rt1" + P])
            S.dve(lambda e, cosb=cosb, p=p: e.tensor_tensor(rt[p][:, 2], qk[p][:, :, 8:16], cosb, ALU.mult), reads=["qk" + P, "ropet"], writes=["rt2" + P])
            S.dve(lambda e, sinb=sinb, p=p: e.tensor_tensor(rt[p][:, 3], qk[p][:, :, 0:8], sinb, ALU.mult), reads=["qk" + P, "ropet"], writes=["rt3" + P])
            S.dve(lambda e, p=p: e.tensor_tensor(qkr[p][:, :, 0:8], rt[p][:, 0], rt[p][:, 1], ALU.subtract), reads=["rt0" + P, "rt1" + P, "qkr" + P], writes=["qkr" + P])
            S.dve(lambda e, p=p: e.tensor_tensor(qkr[p][:, :, 8:16], rt[p][:, 2], rt[p][:, 3], ALU.add), reads=["rt2" + P, "rt3" + P, "qkr" + P], writes=["qkr" + P])
            ptq = ptr[0:64, 0:768].rearrange("p (a b) -> p a b", a=6)
            ptu = pbf(pbig, 6, 7)[0:64, 512:1024].rearrange("p (a b) -> p a b", a=4)
            for a in range(6):
                S.pe(lambda e, a=a, p=p: e.transpose(ptq[:, a, :], qkr[p][:, a, :], identb), reads=["qkr" + P, "ident"], writes=["ptr"])
            for a in range(4):
                S.pe(lambda e, a=a, p=p: e.transpose(ptu[:, a, :], qkb[p][:, a, :], identb), reads=["qkb" + P, "ident"], writes=["ptr2"])
            S.act(lambda e, p=p: e.copy(qst[p][0:64], ptq[:, 0:4, :]), reads=["ptr"], writes=["qst" + P])
            S.dve(lambda e, p=p: e.tensor_copy(qut[p][0:64], ptu), reads=["ptr2"], writes=["qut" + P])
            S.act(lambda e, tok=tok: e.copy(KsA[0:64, tok], ptq[:, 4, :]), reads=["ptr"], writes=["KsA%d" % tt])
            S.dve(lambda e, tok=tok: e.tensor_copy(KwT[0:64, tok], ptq[:, 5, :]), reads=["ptr"], writes=["KwT%d" % tt])
            S.dma("sp", T["qr_d"][:, :, tok], qst[p][0:64], reads=["qst" + P], writes=["qr_d%d" % tt])
            S.dma("sp", T["qu_d"][:, :, tok], qut[p][0:64], reads=["qut" + P], writes=["qu_d%d" % tt])
            ptx = ptr[:, 0:384]
            for c in range(3):
                S.pe(lambda e, c=c, js=js, gp=gp: e.transpose(ptx[:, c * 128:(c + 1) * 128], xbcT[gp][:, c, js], identb),
                     reads=[xr(c), "ident"], writes=["ptr"])
            S.act(lambda e, p=p: e.copy(xtm[p], ptx[:, 0:256]), reads=["ptr"], writes=["xtm" + P])
            S.act(lambda e, p=p: e.copy(btm[p], ptx[:, 256:384]), reads=["ptr"], writes=["btm" + P])
            S.dve(lambda e, p=p: e.tensor_tensor(adt[p], dtt[p], aneg, ALU.mult), reads=["dtt" + P, "aneg"], writes=["adt" + P])
            S.pe(lambda e, p=p: e.matmul(psS[:, 0:4], utri, adt[p], start=True, stop=True), reads=["utri", "adt" + P], writes=["psS_a"])
            S.act(lambda e, p=p: e.copy(acol[p], psS[:, 0:4]), reads=["psS_a"], writes=["acol" + P])
            S.dve(lambda e, p=p: e.tensor_scalar(nacol[p], psS[:, 0:4], -1.0, None, ALU.mult), reads=["psS_a"], writes=["nacol" + P])
            S.act(lambda e, p=p: e.activation(eac[p], acol[p], AF.Exp), reads=["acol" + P], writes=["eac" + P])
            S.pe(lambda e, js=js, gp=gp: e.matmul(psS[:, 256:384], xbcT[gp][:, 2, js], xbcT[gp][:, 3, js], start=True, stop=True),
                 reads=[xr(2), xr(3)], writes=["psS_c"])
            S.dve(lambda e, p=p: e.tensor_tensor(cbm[p], psS[:, 256:384], utri, ALU.mult), reads=["psS_c", "utri"], writes=["cbm" + P])
            S.dve(lambda e, p=p: e.tensor_tensor(rhs4[p], utri.unsqueeze(1).to_broadcast([128, 4, 128]), b4(adt[p], 128), ALU.mult),
                  reads=["utri", "adt" + P], writes=["rhs4"], c=0.6)
            S.pe(lambda e, p=p: e.matmul(psR, onesf, rhs4[p].rearrange("p a b -> p (a b)"), start=True, stop=True), reads=["onesf", "rhs4"], writes=["psR"], c=0.9)
            psR4 = psR.rearrange("p (a b) -> p a b", a=4)
            S.dve(lambda e, p=p: e.tensor_tensor(seg4[p], psR4, b4(acol[p], 128), ALU.subtract), reads=["psR", "acol" + P], writes=["seg4"], c=0.7)
            S.dve(lambda e, p=p: e.tensor_scalar(seg4[p], seg4[p], 0.0, None, ALU.min), reads=["seg4"], writes=["seg4"], c=0.35)
            S.act(lambda e, p=p: e.activation(seg4[p], seg4[p], AF.Exp), reads=["seg4"], writes=["seg4"], c=0.6)
            S.dve(lambda e, p=p: e.tensor_tensor(MT[p], seg4[p], cbm[p].unsqueeze(1).to_broadcast([128, 4, 128]), ALU.mult),
                  reads=["seg4", "cbm" + P], writes=["MT" + P], c=0.6)
            S.dve(lambda e, p=p: e.tensor_copy(alast[p], psR4[:, :, 127]), reads=["psR"], writes=["alast" + P])
            S.dve(lambda e, p=p: e.tensor_tensor(dsv[p], nacol[p], alast[p], ALU.add), reads=["nacol" + P, "alast" + P], writes=["dsv" + P])
            S.act(lambda e, p=p: e.activation(dsv[p], dsv[p], AF.Exp), reads=["dsv" + P], writes=["dsv" + P])
            S.act(lambda e, p=p: e.activation(cdv[p], alast[p], AF.Exp), reads=["alast" + P], writes=["cdv" + P])
            S.dve(lambda e, p=p: e.tensor_tensor(wsc[p], dtt[p], dsv[p], ALU.mult), reads=["dtt" + P, "dsv" + P], writes=["wsc" + P])
            v4 = lambda ap: ap.rearrange("p (h d) -> p h d", h=4)
            S.dve(lambda e, p=p: e.tensor_tensor(v4(xw[p]), v4(xtm[p]), b4(wsc[p], 64), ALU.mult), reads=["xtm" + P, "wsc" + P], writes=["xw" + P])
            S.dve(lambda e, p=p: e.tensor_tensor(v4(xdt[p]), v4(xtm[p]), b4(dtt[p], 64), ALU.mult), reads=["xtm" + P, "dtt" + P], writes=["xdt" + P])
            S.pool(lambda e, p=p: e.tensor_copy(hstb[p], hst), reads=["hst"], writes=["hstb" + P])
            S.pe(lambda e, js=js, gp=gp, p=p: e.matmul(psY[:, 0:256], xbcT[gp][:, 3, js], hstb[p], start=True, stop=True), reads=[xr(3), "hstb" + P], writes=["psY_o"])
            for h in range(4):
                S.pe(lambda e, h=h, p=p: e.matmul(psY[:, 256 + h * 64:256 + (h + 1) * 64], MT[p][:, h, :], xdt[p][:, h * 64:(h + 1) * 64], start=True, stop=True),
                     reads=["MT" + P, "xdt" + P], writes=["psY_d"])
            S.pe(lambda e, p=p: e.matmul(psN, btm[p], xw[p], start=True, stop=True), reads=["btm" + P, "xw" + P], writes=["psN"])
            S.dve(lambda e, p=p: e.tensor_tensor(v4(hst), v4(hst), b4(cdv[p], 64), ALU.mult), reads=["hst", "cdv" + P], writes=["hst"])
            S.dve(lambda e: e.tensor_tensor(hst, hst, psN, ALU.add), reads=["hst", "psN"], writes=["hst"])
            S.act(lambda e, p=p: e.copy(ydg[p], psY[:, 256:512]), reads=["psY_d"], writes=["ydg"])
            S.dve(lambda e, p=p: e.tensor_tensor(v4(yy[p]), v4(psY[:, 0:256]), b4(eac[p], 64), ALU.mult), reads=["psY_o", "eac" + P], writes=["yy"])
            S.pool(lambda e, p=p: e.tensor_tensor(v4(tmpd[p]), v4(xtm[p]), b4(hpb[:, 8:12], 64), ALU.mult), reads=["xtm" + P, "hpb"], writes=["tmpd"])
            S.dve(lambda e, p=p: e.tensor_tensor(yy[p], yy[p], ydg[p], ALU.add), reads=["yy", "ydg"], writes=["yy"])
            S.dve(lambda e, p=p: e.tensor_tensor(yy[p], yy[p], tmpd[p], ALU.add), reads=["yy", "tmpd"], writes=["yy"])
            S.dve(lambda e, p=p: e.tensor_tensor(yo[p], yy[p], zs[p], ALU.mult), reads=["yy", "zs" + P], writes=["yo" + P])
            S.dma("sp", T["mixo"][tok, 0:256], yo[p], reads=["yo" + P], writes=["mixo_s%d" % tt])

    if int(os.environ.get('MIX_STOP', '9')) <= 1:
        return
    bar()
    A.reset(base_persist)
    attO = A.alloc([NTT, 256], F32)
    nmT = A.alloc([SEQ], BF16)
    w1 = A.alloc([32, 256], BF16)
    S.dma("pool", w1[0:64], T["w1k"].rearrange("d (l h) -> d l h", l=32), writes=["w1k"])
    S.dma("pool", w1[64:128], T["w1v"].rearrange("d (l h) -> d l h", l=32), writes=["w1v"])
    pe_ = A.alloc([32], BF16)
    S.dma("pool", pe_[0:64], T["pek"], writes=["pek"])
    S.dma("pool", pe_[64:128], T["pev"], writes=["pev"])
    w2 = A.alloc([2, 2, 64], BF16)
    S.dma("pool", w2[:, 0], T["w2k"].rearrange("(c p) d -> p c d", p=128), writes=["w2k"])
    S.dma("pool", w2[:, 1], T["w2v"].rearrange("(c p) d -> p c d", p=128), writes=["w2v"])
    cbias = A.alloc([4], F32)
    hsb = A.alloc([4, 256], BF16)
    KcT = A.alloc([256], BF16)
    VcA = A.alloc([2, 129], BF16)
    S.pool(lambda e: e.memset(hsb, 0.0), writes=["hsb"])
    S.pool(lambda e: e.memset(VcA, 0.0), writes=["VcA"])
    S.pool(lambda e: e.memset(KcT, 0.0), writes=["KcT"])
    S.pool(lambda e: e.memset(VcA[:, :, 64:65], 1.0), reads=["VcA"], writes=["VcA"])
    S.dma("pool", VcA[:, :, 65:129], T["ovl"].rearrange("(c p) j -> p c j", p=128), reads=["VcA"], writes=["VcA"])
    for kv in range(2):
        rows = slice(kv * 64, (kv + 1) * 64)
        for hc in range(2):
            idx = kv * 2 + hc
            pb_ = pbig[:, idx, 0:255]
            pbias = pbig[:, 4, idx:idx + 1]
            for l in range(32):
                S.pe(lambda e, rows=rows, hc=hc, l=l, pbias=pbias: e.matmul(pbias, w1[rows, l, hc * 128:(hc + 1) * 128], pe_[rows, l:l + 1], start=(l == 0), stop=(l == 31)),
                     reads=["w1k", "w1v", "pek", "pev"], writes=["pbias%d" % idx])
            S.act(lambda e, idx=idx, pbias=pbias: e.copy(cbias[:, idx:idx + 1], pbias), reads=["pbias%d" % idx], writes=["cbias%d" % idx])
            for l in range(32):
                S.pe(lambda e, rows=rows, hc=hc, l=l, pb_=pb_: e.matmul(pb_, w1[rows, l, hc * 128:(hc + 1) * 128], kcvcT[rows, l:l + 16 * 254 + 1:16], start=(l == 0), stop=(l == 31)),
                     reads=["w1k", "w1v", "kcvcT"], writes=["phid%d" % idx])
            S.act(lambda e, idx=idx, pb_=pb_: e.activation(hsb[:, idx, 0:255], pb_, AF.Silu, bias=cbias[:, idx:idx + 1]), reads=["phid%d" % idx, "cbias%d" % idx, "hsb"], writes=["hsb%d" % idx])
    pk = pbig[0:64, 5, 0:255]
    for hc in range(2):
        S.pe(lambda e, hc=hc: e.matmul(pk, w2[:, 0, hc, :], hsb[:, hc, 0:255], start=(hc == 0), stop=(hc == 1)), reads=["w2k", "hsb0", "hsb1"], writes=["pk"])
    S.act(lambda e: e.copy(KcT[0:64, 0:255], pk), reads=["pk", "KcT"], writes=["KcT"])
    for it in range(2):
        m = 128 if it == 0 else 127
        pv = pbig[0:m, 6 + it, 0:64]
        for hc in range(2):
            S.pe(lambda e, it=it, hc=hc, m=m, pv=pv: e.matmul(pv, hsb[:, 2 + hc, it * 128:it * 128 + m], w2[:, 1, hc, :], start=(hc == 0), stop=(hc == 1)),
                 reads=["w2v", "hsb2", "hsb3"], writes=["pv%d" % it])
        S.act(lambda e, it=it, m=m, pv=pv: e.copy(VcA[0:m, it, 0:64], pv), reads=["pv%d" % it, "VcA"], writes=["VcA"])

    if int(os.environ.get('MIX_STOP', '9')) <= 2:
        return
    bar()
    base3 = A.off
    qu = [A.alloc([4, 512], BF16) for _ in range(2)]
    cmk = [A.alloc([2, 512], BF16) for _ in range(2)]
    PcT = [A.alloc([2, 512], BF16) for _ in range(2)]
    imp = A.alloc([4, 64], F32)
    imp2 = A.alloc([64], F32)
    m8 = A.alloc([16], F32)
    thr = A.alloc([1], F32)
    rr = A.alloc([32], F32)
    nmb = A.alloc([128], BF16)
    S.pool(lambda e: e.memset(nmb, 0.0), writes=["nmb"])
    pnt = pbf(pbig, 7, 8)[:, 0:128]
    for Q in range(8):
        qb = Q % 2
        qs = slice(Q * 512, (Q + 1) * 512)
        S.dma("sp", qu[qb][0:64], T["qu_d"][:, :, qs], writes=["qu%d" % qb])
        S.dma("pool", cmk[qb], T["cmask"][:, qs].rearrange("(c p) t -> p c t", p=128), writes=["cmk%d" % qb])
        for h in range(4):
            pb2 = h % 2
            for it in range(2):
                ps_ = pbig[:, pb2 * 2 + it, :]
                S.pe(lambda e, it=it, h=h, qb=qb, ps_=ps_: e.matmul(ps_, KcT[0:64, it * 128:(it + 1) * 128], qu[qb][0:64, h, :], start=True, stop=False),
                     reads=["KcT", "qu%d" % qb], writes=["psc%d_%d" % (pb2, it)])
                S.pe(lambda e, it=it, qb=qb, ps_=ps_: e.matmul(ps_, identb, cmk[qb][:, it, :], start=False, stop=True),
                     reads=["ident", "cmk%d" % qb], writes=["psc%d_%d" % (pb2, it)])
                S.act(lambda e, it=it, pb2=pb2, ps_=ps_: e.activation(PcT[pb2][:, it, :], ps_, AF.Exp, scale=SCALE), reads=["psc%d_%d" % (pb2, it)], writes=["PcT%d_%d" % (pb2, it)])
            for sub in range(4):
                tt = Q * 4 + sub
                po = pbig[:, 4 + (sub % 2), 0:129]
                for it in range(2):
                    S.pe(lambda e, it=it, pb2=pb2, sub=sub, po=po: e.matmul(po, PcT[pb2][:, it, sub * 128:(sub + 1) * 128], VcA[:, it, :], start=(it == 0), stop=(it == 1)),
                         reads=["PcT%d_0" % pb2, "PcT%d_1" % pb2, "VcA"], writes=["po%d" % (sub % 2)])
                pr = ["po%d" % (sub % 2)]
                ri = ((h % 2) * 4 + sub) * 2
                S.dve(lambda e, ri=ri, po=po: e.tensor_scalar(rr[:, ri:ri + 1], po[:, 64:65], 1e-30, None, ALU.add), reads=pr, writes=["rr0_%d" % ri])
                S.dve(lambda e, ri=ri: e.reciprocal(rr[:, ri:ri + 1], rr[:, ri:ri + 1]), reads=["rr0_%d" % ri], writes=["rr0_%d" % ri])
                if h == 0:
                    S.act(lambda e, ri=ri, po=po, sub=sub: e.activation(imp[:, sub, :], po[:, 65:129], AF.Copy, scale=rr[:, ri:ri + 1]), reads=pr + ["rr0_%d" % ri], writes=["imp%d" % sub])
                else:
                    S.dve(lambda e, ri=ri, po=po, sub=sub: e.scalar_tensor_tensor(imp[:, sub, :], po[:, 65:129], rr[:, ri:ri + 1], imp[:, sub, :], ALU.mult, ALU.add),
                          reads=pr + ["rr0_%d" % ri, "imp%d" % sub], writes=["imp%d" % sub])
                S.dve(lambda e, ri=ri, tt=tt, h=h: e.tensor_tensor(rr[:, ri + 1:ri + 2], rr[:, ri:ri + 1], gate[:, tt, h * 3:h * 3 + 1], ALU.mult), reads=["rr0_%d" % ri, "gate"], writes=["rr1_%d" % ri])
                S.act(lambda e, ri=ri, po=po, tt=tt, h=h: e.activation(attO[:, tt, h * 64:(h + 1) * 64], po[:, 0:64], AF.Copy, scale=rr[:, ri + 1:ri + 2]), reads=pr + ["rr1_%d" % ri], writes=["attO%d" % tt])
        for sub in range(4):
            tt = Q * 4 + sub
            ir = ["imp%d" % sub]
            im = imp[:, sub, :]
            S.pool(lambda e, im=im: e.memset(im[:, 0:1], 1e4), reads=ir, writes=ir)
            lo = max(2 * tt - 1, 0)
            S.pool(lambda e, im=im, lo=lo, tt=tt: e.memset(im[0:64, lo:2 * tt + 1], 1e4), reads=ir, writes=ir)
            S.pool(lambda e, im=im, tt=tt: e.memset(im[64:128, 2 * tt:2 * tt + 2], 1e4), reads=ir, writes=ir)
            if 2 * tt + 1 < 64:
                S.pool(lambda e, im=im, tt=tt: e.memset(im[0:64, 2 * tt + 1:64], -1.0), reads=ir, writes=ir)
            if 2 * tt + 2 < 64:
                S.pool(lambda e, im=im, tt=tt: e.memset(im[64:128, 2 * tt + 2:64], -1.0), reads=ir, writes=ir)
            S.dve(lambda e, im=im: e.max(m8[:, 0:8], im), reads=ir, writes=["m8a"])
            S.dve(lambda e, im=im: e.match_replace(imp2, m8[:, 0:8], im, -1e30), reads=ir + ["m8a"], writes=["imp2"])
            S.dve(lambda e: e.max(m8[:, 8:16], imp2), reads=["imp2"], writes=["m8b"])
            S.dve(lambda e: e.tensor_scalar(thr, m8[:, 15:16], 0.0, None, ALU.max), reads=["m8b"], writes=["thr"])
            S.dve(lambda e, im=im: e.tensor_scalar(nmb[:, 64:128], im, thr, NEG, ALU.is_lt, ALU.mult), reads=ir + ["thr", "nmb"], writes=["nmb"])
            S.pe(lambda e: e.transpose(pnt, nmb, identb), reads=["nmb", "ident"], writes=["pnt"])
            S.act(lambda e, tt=tt: e.copy(nmT[64:128, tt * 128:(tt + 1) * 128], pnt[64:128, :]), reads=["pnt"], writes=["nmT"])

    if int(os.environ.get('MIX_STOP', '9')) <= 3:
        return
    bar()
    A.reset(base3)
    Qa = [A.alloc([4, 512], BF16) for _ in range(2)]
    PT = [A.alloc([512], BF16) for _ in range(4)]
    PW = [A.alloc([512], BF16) for _ in range(3)]
    r4 = A.alloc([8], F32)
    oTs = [A.alloc([512], F32) for _ in range(2)]
    pti = 0
    pwi = 0
    for Q in range(8):
        qb = Q % 2
        qs = slice(Q * 512, (Q + 1) * 512)
        S.dma("sp", Qa[qb][0:64], T["qr_d"][:, :, qs], writes=["Qa%d" % qb])
        for h in range(4):
            S.pool(lambda e, qb=qb, h=h, qs=qs: e.tensor_copy(Qa[qb][64:128, h, :], nmT[64:128, qs]), reads=["nmT"], writes=["Qm%d_%d" % (qb, h)])
        for h in range(4):
            qr_ = ["Qa%d" % qb, "Qm%d_%d" % (qb, h)]
            posel = pbig[:, 6, 0:260].rearrange("p (s c) -> p s c", s=4)
            powin = pbig[:, 7, 0:260].rearrange("p (s c) -> p s c", s=4)
            for kt in range(4 * Q + 4):
                ks_ = slice(kt * 128, (kt + 1) * 128)
                sb_ = kt % 3
                ps_ = pbig[:, sb_, :]
                pres = "pss%d" % sb_
                o = kt - 4 * Q
                if o < 0:
                    S.pe(lambda e, ks_=ks_, qb=qb, h=h, ps_=ps_: e.matmul(ps_, KsA[:, ks_], Qa[qb][:, h, :], start=True, stop=True),
                         reads=["KsA", "KsE"] + qr_, writes=[pres])
                    lo = 0
                else:
                    lo = o * 128
                    S.pe(lambda e, ks_=ks_, qb=qb, h=h, ps_=ps_, lo=lo: e.matmul(ps_[:, lo:lo + 128], KsA[:, ks_], Qa[qb][:, h, lo:lo + 128], start=True, stop=False),
                         reads=["KsA", "KsE"] + qr_, writes=[pres])
                    S.pe(lambda e, ps_=ps_, lo=lo: e.matmul(ps_[:, lo:lo + 128], identb, tricb, start=False, stop=True), reads=["ident", "tric"], writes=[pres])
                    if o < 3:
                        S.pe(lambda e, ks_=ks_, qb=qb, h=h, ps_=ps_, lo=lo: e.matmul(ps_[:, lo + 128:512], KsA[:, ks_], Qa[qb][:, h, lo + 128:512], start=True, stop=True),
                             reads=["KsA", "KsE"] + qr_, writes=[pres])
                pt_ = PT[pti % 4]
                ptres = "PT%d" % (pti % 4)
                pti += 1
                S.act(lambda e, ps_=ps_, pt_=pt_, lo=lo: e.activation(pt_[:, lo:512], ps_[:, lo:512], AF.Exp, scale=SCALE), reads=[pres], writes=[ptres])
                S.pe(lambda e, pt_=pt_, kt=kt, Q=Q, lo=lo: e.matmul(pbig[0:65, 6, lo:512], VsA[:, kt, :], pt_[:, lo:512], start=(kt == 0), stop=(kt == 4 * Q + 3)),
                     reads=[ptres, "VsA"], writes=["posel"], c=0.22)
            ob_ = (Q * 4 + h) % 2
            S.act(lambda e, ob_=ob_: e.copy(oTs[ob_][0:65, :], pbig[0:65, 6, :]), reads=["posel"], writes=["oTs%d" % ob_], c=0.6)
            poselT = pbig[:, 5, 0:260].rearrange("p (s c) -> p s c", s=4)
            for sub in range(4):
                S.pe(lambda e, ob_=ob_, sub=sub, poselT=poselT: e.transpose(poselT[:, sub, :], oTs[ob_][0:65, sub * 128:(sub + 1) * 128], identf[0:65, 0:65]),
                     reads=["oTs%d" % ob_, "identf"], writes=["poselT"])
            S.dve(lambda e: e.memset(pbig[:, 7, 0:260], 0.0), writes=["powin"])
            for r in range(-4, 4):
                kt = 4 * Q + r
                if kt < 0:
                    continue
                ks_ = slice(kt * 128, (kt + 1) * 128)
                s_lo, s_hi = max(r, 0), min(r + 4, 3)
                wsl = pwi % 2
                psw = pbig[:, 3 + wsl, :]
                pwres = "psw%d" % wsl
                pw_ = PW[pwi % 3]
                pwr = "PW%d" % (pwi % 3)
                pwi += 1
                plain = [s_ for s_ in range(s_lo, s_hi + 1) if s_ != r and s_ != r + 4]
                for s_, msk in ((r, tricb), (r + 4, triab)):
                    if s_lo <= s_ <= s_hi:
                        cs = slice(s_ * 128, (s_ + 1) * 128)
                        S.pe(lambda e, ks_=ks_, qb=qb, h=h, cs=cs, psw=psw: e.matmul(psw[:, cs], KwT[0:64, ks_], Qa[qb][0:64, h, cs], start=True, stop=False),
                             reads=["KwT", "Qa%d" % qb], writes=[pwres])
                        S.pe(lambda e, cs=cs, psw=psw, msk=msk: e.matmul(psw[:, cs], identb, msk, start=False, stop=True), reads=["ident", "tric", "tria"], writes=[pwres])
                if plain:
                    cs = slice(plain[0] * 128, (plain[-1] + 1) * 128)
                    S.pe(lambda e, ks_=ks_, qb=qb, h=h, cs=cs, psw=psw: e.matmul(psw[:, cs], KwT[0:64, ks_], Qa[qb][0:64, h, cs], start=True, stop=True),
                         reads=["KwT", "Qa%d" % qb], writes=[pwres], c=0.2)
                ca = slice(s_lo * 128, (s_hi + 1) * 128)
                S.act(lambda e, psw=psw, pw_=pw_, ca=ca: e.activation(pw_[:, ca], psw[:, ca], AF.Exp, scale=SCALE), reads=[pwres], writes=[pwr], c=0.5)
                for s_ in range(s_lo, s_hi + 1):
                    S.pe(lambda e, pw_=pw_, s_=s_, kt=kt, powin=powin: e.matmul(powin[:, s_, :], pw_[:, s_ * 128:(s_ + 1) * 128], VwA[:, kt, :],
                                                                                start=False, stop=False, skip_group_check=True),
                         reads=[pwr, "VwA"], writes=["powin"])
            for sub in range(4):
                tt = 4 * Q + sub
                for br, (po_, pres) in enumerate(((poselT, "poselT"), (powin, "powin"))):
                    S.dve(lambda e, po_=po_, sub=sub, br=br: e.reciprocal(r4[:, sub * 2 + br:sub * 2 + br + 1], po_[:, sub, 64:65]), reads=[pres], writes=["r4_%d_%d" % (sub, br)])
                    S.dve(lambda e, sub=sub, tt=tt, h=h, br=br: e.tensor_tensor(r4[:, sub * 2 + br:sub * 2 + br + 1], r4[:, sub * 2 + br:sub * 2 + br + 1], gate[:, tt, h * 3 + 1 + br:h * 3 + 2 + br], ALU.mult),
                          reads=["r4_%d_%d" % (sub, br), "gate"], writes=["r4_%d_%d" % (sub, br)])
                    S.dve(lambda e, po_=po_, sub=sub, tt=tt, h=h, br=br: e.scalar_tensor_tensor(attO[:, tt, h * 64:(h + 1) * 64], po_[:, sub, 0:64], r4[:, sub * 2 + br:sub * 2 + br + 1],
                                                                                                  attO[:, tt, h * 64:(h + 1) * 64], ALU.mult, ALU.add),
                          reads=[pres, "r4_%d_%d" % (sub, br), "attO%d" % tt], writes=["attO%d" % tt])
        for sub in range(4):
            tt = 4 * Q + sub
            S.dma("sp", T["mixo"][tt * 128:(tt + 1) * 128, 256:512], attO[:, tt, :], reads=["attO%d" % tt])


def _perm_cols():
    return None


def run_mix(inputs):
    if "mix" not in _CACHE:
        _CACHE["mix"] = build_mix()
    nc = _CACHE["mix"]
    x = inputs["x"]
    w_in = inputs["w_in"][0]
    offs = np.cumsum([0, 1024, 1536, 16, 1024, 256, 256, 256, 256, 256, 256, 48])
    oz, oxbc, odt, oq, okc, ovc, oks, ovs, okw, ovw, ogate = offs[:11]
    conv_w = inputs["conv_w"][0]
    conv_b = inputs["conv_b"][0]
    t = np.arange(SEQ, dtype=np.float32)
    inv = (1.0 / (500000.0 ** (np.arange(0, 16, 2, dtype=np.float32) / np.float32(16)))).astype(np.float32)
    ang = (t[:, None] * inv[None, :]).astype(np.float32)
    rope = np.concatenate([np.cos(ang), np.sin(ang)], 1).astype(np.float32)
    rope = np.ascontiguousarray(rope.reshape(NTT, 128, 16).transpose(1, 0, 2).reshape(128, NTT * 16))
    ident = np.eye(128, dtype=np.float32)
    kk = np.arange(128)[:, None]
    qq = np.arange(128)[None, :]
    tric = np.where(kk <= qq, 0.0, NEG).astype(np.float32)
    tria = np.where(kk > qq, 0.0, NEG).astype(np.float32)
    utri = (kk <= qq).astype(np.float32)
    emat = (np.arange(SEQ)[None, :] // 64 == np.arange(64)[:, None]).astype(np.float32)
    ii = np.arange(256)[:, None]
    cmask = np.where((16 * ii + 31 <= np.arange(SEQ)[None, :]) & (ii < 255), 0.0, NEG).astype(np.float32)
    cs = np.arange(255)[:, None] * 16
    ss_ = np.arange(64)[None, :] * 64
    ov = np.clip(np.minimum(cs + 32, ss_ + 64) - np.maximum(cs, ss_), 0, None) / 32.0
    ovl = np.zeros((256, 64), np.float32)
    ovl[:255] = ov
    in_maps = []
    for c in range(8):
        b, g = c // 4, c % 4
        grp = g // 2
        ar = np.arange
        tm_cols = np.concatenate([oz + 256 * g + ar(256), odt + 4 * g + ar(4), ogate + 12 * g + ar(12), ovs + 64 * g + ar(64), ovw + 64 * g + ar(64),
                                  oq + 256 * g + ar(256), oks + 64 * g + ar(64), okw + 64 * g + ar(64)])
        xcols = np.concatenate([256 * g + ar(256), 1024 + 128 * grp + ar(128), 1280 + 128 * grp + ar(128)])
        fm_cols = np.concatenate([oxbc + xcols, okc + 64 * g + ar(64), ovc + 64 * g + ar(64)])
        convw = np.ascontiguousarray(conv_w[:, xcols].T.reshape(4, 128, 4).transpose(1, 0, 2).reshape(128, 16))
        convb = np.ascontiguousarray(conv_b[xcols].reshape(4, 128).T)
        hp = np.concatenate([inputs["dt_bias"][0][4 * g:4 * g + 4], inputs["a_log"][0][4 * g:4 * g + 4], inputs["d_skip"][0][4 * g:4 * g + 4]]).astype(np.float32)
        in_maps.append(dict(
            xb=np.ascontiguousarray(x[b]), anw=inputs["attn_norm_w"][0], w_tm=np.ascontiguousarray(w_in[:, tm_cols]), w_fm=np.ascontiguousarray(w_in[:, fm_cols]),
            convw=convw, convb=convb, hp=hp, rope=rope,
            w1k=np.ascontiguousarray(inputs["cmp_w1_k"][0].reshape(32, 64, 256).transpose(1, 0, 2).reshape(64, 32 * 256)),
            w1v=np.ascontiguousarray(inputs["cmp_w1_v"][0].reshape(32, 64, 256).transpose(1, 0, 2).reshape(64, 32 * 256)),
            w2k=inputs["cmp_w2_k"][0], w2v=inputs["cmp_w2_v"][0],
            pek=np.ascontiguousarray(inputs["cmp_pe_k"][0].T), pev=np.ascontiguousarray(inputs["cmp_pe_v"][0].T),
            ident=ident, emat=emat, tric=tric, tria=tria, cmask=cmask, ovl=ovl, utri=utri,
        ))
    res = run_bass_kernel_spmd(nc, in_maps, core_ids=list(range(8)))
    mixed = np.empty((2, SEQ, D), np.float32)
    for c in range(8):
        b, g = c // 4, c % 4
        m = res.results[c]["mixo"]
        mixed[b, :, 256 * g:256 * (g + 1)] = m[:, 0:256]
        mixed[b, :, 1024 + 256 * g:1024 + 256 * (g + 1)] = m[:, 256:512]
    return mixed


def kernel(**inputs):
    inputs = {k: np.asarray(v) for k, v in inputs.items()}
    mixed = run_mix(inputs)
    return run_tail(inputs["x"], mixed, inputs["ssd_norm_w"][0], inputs["w_out"][0], inputs["ffn_norm_w"][0],
                    inputs["w_gate"][0], inputs["w_up"][0], inputs["w_down"][0], inputs["final_norm_w"])
```

```python
import contextlib
import os
import numpy as np
import ml_dtypes
import concourse.bass as bass
import concourse.mybir as mybir
from concourse.bass_utils import run_bass_kernel_spmd

F32 = mybir.dt.float32
BF16 = mybir.dt.bfloat16
U8 = mybir.dt.uint8
ALU = mybir.AluOpType
AF = mybir.ActivationFunctionType
AX = mybir.AxisListType

ENGS = ("pe", "act", "dve", "pool", "sp")
EPS = 1e-6


class _Op:
    __slots__ = ("eng", "fn", "dma", "deps", "idx", "signal", "val", "sem", "cc", "cost")


class Sched:
    def __init__(self, nc, n_dma_sems=8):
        self.nc = nc
        self.ops = []
        self.last_w = {}
        self.readers = {}
        self.n_dma_sems = n_dma_sems
        self.bar = None
        self.bank_of = {}

    def op(self, eng, fn, reads=(), writes=(), dma=False, c=None):
        o = _Op()
        o.cost = c
        o.eng, o.fn, o.dma = eng, fn, dma
        o.cc = False
        o.idx = len(self.ops)
        o.signal = False
        deps = {}
        if self.bar is not None:
            deps[self.bar] = "raw"
        for r in reads:
            w = self.last_w.get(r)
            if w is not None:
                deps[w] = "raw"
        for w_ in writes:
            w = self.last_w.get(w_)
            if w is not None:
                deps[w] = "raw"
            for r in self.readers.get(w_, ()):
                if r not in deps:
                    deps[r] = "war"
        for r in reads:
            self.readers.setdefault(r, []).append(o.idx)
        for w_ in writes:
            self.last_w[w_] = o.idx
            self.readers[w_] = []
        banks = set()
        for r in tuple(reads) + tuple(writes):
            banks.update(self.bank_of.get(r, ()))
        for b in banks:
            key = ("bank", b)
            w = self.last_w.get(key)
            if w is not None and w not in deps:
                deps[w] = "bank"
            self.last_w[key] = o.idx
        deps.pop(o.idx, None)
        o.deps = deps
        self.ops.append(o)
        return o

    def pe(self, fn, reads=(), writes=(), c=None):
        return self.op("pe", fn, reads, writes, c=c)

    def act(self, fn, reads=(), writes=(), c=None):
        return self.op("act", fn, reads, writes, c=c)

    def dve(self, fn, reads=(), writes=(), c=None):
        return self.op("dve", fn, reads, writes, c=c)

    def pool(self, fn, reads=(), writes=(), c=None):
        return self.op("pool", fn, reads, writes, c=c)

    DEF_COST = {"pe": 0.12, "act": 0.35, "dve": 0.25, "pool": 0.35}

    def reorder(self, window=int(os.environ.get("RWIN", "120"))):
        ops = self.ops
        n = len(ops)
        queues = {e: [] for e in ENGS}
        for o in ops:
            queues[o.eng].append(o.idx)
        head = {e: 0 for e in ENGS}
        sched = [False] * n
        fin = [0.0] * n
        etime = {e: 0.0 for e in ENGS}
        order = []
        left = n
        while left:
            best = None
            for e in ENGS:
                q = queues[e]
                h = head[e]
                while h < len(q) and sched[q[h]]:
                    h += 1
                head[e] = h
                if h >= len(q):
                    continue
                seen = 0
                i = h
                et = etime[e]
                while i < len(q) and seen < window:
                    k = q[i]
                    i += 1
                    if sched[k]:
                        continue
                    seen += 1
                    o = ops[k]
                    rdy = 0.0
                    ok = True
                    for d in o.deps:
                        if not sched[d]:
                            ok = False
                            break
                        f = fin[d] + (0.0 if ops[d].eng == e else 0.15)
                        if f > rdy:
                            rdy = f
                    if not ok:
                        continue
                    st = rdy if rdy > et else et
                    key = (st, k)
                    if best is None or key < best[0]:
                        best = (key, e, k)
                    if st <= et:
                        break
            assert best is not None, "scheduler stuck"
            (st, k), e, _ = best
            o = ops[k]
            if o.dma:
                etime[e] = st + 0.06
                fin[k] = st + (o.cost if o.cost is not None else 3.0)
            else:
                c = o.cost if o.cost is not None else self.DEF_COST[e]
                etime[e] = st + c
                fin[k] = st + c
            sched[k] = True
            order.append(k)
            left -= 1
        self.order = order
        self.est_time = max(fin) if fin else 0.0

    def dma(self, q, out, in_, reads=(), writes=(), c=None):
        return self.op(q, lambda e: e.dma_start(out=out, in_=in_), reads, writes, dma=True, c=c)

    def cc(self, fn, reads=(), writes=()):
        o = self.op("pool", fn, reads, writes, dma=True)
        o.cc = True
        return o

    def barrier(self, out, in_):
        allres = set(self.last_w.keys()) | set(self.readers.keys())
        o = self.op("sp", lambda e: e.dma_start(out=out, in_=in_), reads=(), writes=tuple(allres), dma=True)
        self.bar = o.idx
        self.last_w = {}
        self.readers = {}
        return o

    def emit(self, sems, block, final_wait_eng="sp"):
        ops = self.ops
        need = [False] * len(ops)
        for o in ops:
            for d, kind in o.deps.items():
                do = ops[d]
                if do.dma:
                    continue
                if do.eng == o.eng and not o.dma:
                    if do.eng == "pe" or kind == "bank":
                        continue
                need[d] = True
        cnt = {e: 0 for e in ENGS}
        dcnt = {}
        dval = {}
        per_eng = {e: [] for e in ENGS}
        order = getattr(self, "order", None) or list(range(len(ops)))
        for k_ in order:
            o = ops[k_]
            per_eng[o.eng].append(o)
            if o.dma and o.cc:
                o.sem = "cc"
                dval["cc"] = dval.get("cc", 0) + 1
                o.val = dval["cc"]
            elif o.dma:
                k = dcnt.get(o.eng, 0)
                dcnt[o.eng] = k + 1
                key = ("dma", o.eng, k % self.n_dma_sems)
                o.sem = key
                dval[key] = dval.get(key, 0) + 16
                o.val = dval[key]
            elif need[o.idx]:
                cnt[o.eng] += 1
                o.val = cnt[o.eng]
                o.sem = o.eng
                o.signal = True
        self.stats = {e: len(per_eng[e]) for e in ENGS}
        self.stats["signals"] = dict(cnt)

        def run(engname, e):
            waited = {}
            for o in per_eng[engname]:
                wl = {}
                for d, kind in o.deps.items():
                    do = ops[d]
                    if do.dma:
                        wl[do.sem] = max(wl.get(do.sem, 0), do.val)
                        continue
                    if do.eng == o.eng and not o.dma:
                        if do.eng == "pe" or kind == "bank":
                            continue
                    wl[do.sem] = max(wl.get(do.sem, 0), do.val)
                if o.dma and o.cc and o.val > 1:
                    wl[o.sem] = max(wl.get(o.sem, 0), o.val - 1)
                elif o.dma and not o.cc and o.val > 16:
                    wl[o.sem] = max(wl.get(o.sem, 0), o.val - 16)
                for s, v in wl.items():
                    if waited.get(s, 0) >= v:
                        continue
                    waited[s] = v
                    e.wait_ge(sems[s], v)
                ins = o.fn(e)
                if o.dma and o.cc:
                    ins.then_inc(sems[o.sem])
                elif o.dma:
                    ins.then_inc(sems[o.sem], 16)
                elif o.signal:
                    ins.then_inc(sems[o.sem], 1)
            if engname == final_wait_eng:
                for key, v in dval.items():
                    if waited.get(key, 0) < v:
                        e.wait_ge(sems[key], v)
                for en in ("pe", "act", "dve", "pool"):
                    if cnt[en] > 0 and waited.get(en, 0) < cnt[en]:
                        e.wait_ge(sems[en], cnt[en])

        @block.tensor
        def _(e):
            run("pe", e)

        @block.scalar
        def _(e):
            run("act", e)

        @block.vector
        def _(e):
            run("dve", e)

        @block.gpsimd
        def _(e):
            run("pool", e)

        @block.sync
        def _(e):
            run("sp", e)


def make_sems(nc, stack, n_dma_sems=8, queues=("sp", "pool", "act")):
    sems = {}
    for e in ("pe", "act", "dve", "pool", "cc"):
        sems[e] = stack.enter_context(nc.semaphore("s_" + e))
    for q in queues:
        for i in range(n_dma_sems):
            sems[("dma", q, i)] = stack.enter_context(nc.semaphore("d_%s_%d" % (q, i)))
    return sems


_DTSZ = {F32: 4, BF16: 2, U8: 1}


class Arena:
    def __init__(self, ar, size):
        self.ar, self.size, self.off = ar, size, 0

    def reset(self, off=0):
        self.off = off

    def alloc(self, shape, dtype, parts=128):
        n = int(np.prod(shape)) * _DTSZ[dtype]
        off = (self.off + 63) // 64 * 64
        assert off + n <= self.size, ("arena overflow", off, n, self.size)
        self.off = off + n
        ap = self.ar[0:parts, off:off + n].bitcast(dtype)
        if len(shape) > 1:
            names = [chr(ord("a") + i) for i in range(len(shape))]
            pat = "p (%s) -> p %s" % (" ".join(names), " ".join(names))
            ap = ap.rearrange(pat, **{nm: int(s) for nm, s in zip(names, shape)})
        return ap


D = 2048
FF = 5632
TOK = 1024
NT = TOK // 128
KC = D // 128
FC = FF // 128
SSDW = 1024


def build_tail():
    nc = bass.Bass("TRN2", target_bir_lowering=False)
    x = nc.dram_tensor("x_own", [TOK, D], F32, kind="ExternalInput").ap()
    mix = nc.dram_tensor("mix", [TOK, D], F32, kind="ExternalInput").ap()
    w_out = nc.dram_tensor("w_out", [D, D], F32, kind="ExternalInput").ap()
    w_gate = nc.dram_tensor("w_gate", [D, FF], F32, kind="ExternalInput").ap()
    w_up = nc.dram_tensor("w_up", [D, FF], F32, kind="ExternalInput").ap()
    w_down = nc.dram_tensor("w_down", [FF, D], F32, kind="ExternalInput").ap()
    nw = nc.dram_tensor("nw", [3, D], F32, kind="ExternalInput").ap()
    ident = nc.dram_tensor("ident", [128, 128], F32, kind="ExternalInput").ap()
    out = nc.dram_tensor("out", [TOK, D], F32, kind="ExternalOutput").ap()
    h_d = nc.dram_tensor("h_d", [TOK, D], F32, kind="Internal").ap()
    dummy = nc.dram_tensor("dummy_bar", [2, 64], F32, kind="Internal").ap()
    with contextlib.ExitStack() as st:
        ASZ = 207 * 1024
        ar = st.enter_context(nc.sbuf_tensor("arena", [128, ASZ], U8))
        A = Arena(ar, ASZ)
        pbig = st.enter_context(nc.psum_tensor("pbig", [128, 8, 512], F32))
        sems = make_sems(nc, st)
        block = st.enter_context(nc.Block())
        S = Sched(nc)
        tail_body(nc, S, A, pbig, x, mix, w_out, w_gate, w_up, w_down, nw, ident, out, h_d, dummy)
        if os.environ.get('NO_REORDER') is None:
            S.reorder()
        S.emit(sems, block)
    return nc


def rms_rstd(S, src, n, ss, sq, tag, rd, wr_extra=(), c=None):
    S.act(lambda e: e.activation(sq, src, AF.Square, accum_out=ss), reads=rd, writes=[tag + "ss", tag + "sq"], c=c)
    S.act(lambda e: e.activation(ss, ss, AF.Sqrt, scale=1.0 / n, bias=EPS_AP[0]), reads=[tag + "ss"], writes=[tag + "ss"])
    S.dve(lambda e: e.reciprocal(ss, ss), reads=[tag + "ss"], writes=[tag + "ss"])


EPS_AP = [None]


def tail_body(nc, S, A, pbig, x, mix, w_out, w_gate, w_up, w_down, nw, ident, out, h_d, dummy):
    bk = {"ptr": (4, 5), "ptrv": (6, 7)}
    for i_ in range(8):
        bk["pacc%d" % i_] = (i_,)
        bk["pd%d" % i_] = (i_,)
    for i_ in range(2):
        bk["pg%d" % i_] = (i_ * 4, i_ * 4 + 1)
        bk["pu%d" % i_] = (i_ * 4 + 2, i_ * 4 + 3)
    S.bank_of = bk
    identb = A.alloc([128], BF16)
    nwb_flat = A.alloc([2 * D + SSDW], F32)
    epsb = A.alloc([1], F32)
    ss = A.alloc([4], F32)
    EPS_AP[0] = epsb
    S.pool(lambda e: e.memset(epsb, EPS), writes=["eps"])
    S.dma("pool", identb, ident, writes=["ident"])
    S.dma("sp", nwb_flat, nw.rearrange("a b -> (a b)")[0:2 * D + SSDW].partition_broadcast(128), writes=["nwb"])

    class _NW:
        def __getitem__(self, key):
            _, row, cols = key
            base = {1: 0, 2: D, 0: 2 * D}[row]
            lo = cols.start or 0
            hi = cols.stop if cols.stop is not None else (SSDW if row == 0 else D)
            return nwb_flat[:, base + lo:base + hi]
    nwb = _NW()
    vT = A.alloc([KC, TOK], BF16)
    base_persist = A.off

    wo = A.alloc([KC, D], BF16)
    for cb in range(4):
        S.dma("pool", wo[:, :, cb * 512:(cb + 1) * 512],
              w_out[:, cb * 512:(cb + 1) * 512].rearrange("(k p) n -> p k n", p=128), writes=["wo%d" % cb], c=25.0)
    xt = [A.alloc([D], F32) for _ in range(2)]
    mt = [A.alloc([D], F32) for _ in range(2)]
    sq = A.alloc([D], BF16)
    mb2 = [A.alloc([D], BF16) for _ in range(2)]
    mT2 = [A.alloc([KC, 128], BF16) for _ in range(2)]
    hs = [A.alloc([D], F32) for _ in range(2)]
    vb = A.alloc([D], BF16)
    WB = 256
    pre_lo = (A.off + 63) // 64 * 64
    wg_pre = A.alloc([KC, WB], BF16)
    wu_pre = A.alloc([KC, WB], BF16)
    S.dma("pool", wg_pre, w_gate[:, 0:WB].rearrange("(k p) n -> p k n", p=128), writes=["wg0"], c=15.0)
    S.dma("pool", wu_pre, w_up[:, 0:WB].rearrange("(k p) n -> p k n", p=128), writes=["wu0"], c=15.0)
    pacc = pbig[:, 0:4, :]
    ptr_all = pbig[:, 4:6, :].rearrange("p a b -> p (a b)").bitcast(BF16)
    ptr = ptr_all.rearrange("p (k n) -> p k n", k=KC)

    def loads(tt):
        b = tt % 2
        S.dma("sp", xt[b], x[tt * 128:(tt + 1) * 128, :], writes=["xt%d" % b])
        S.dma("sp", mt[b], mix[tt * 128:(tt + 1) * 128, :], writes=["mt%d" % b])

    loads(0)
    ptr2_all = pbig[:, 6:8, :].rearrange("p a b -> p (a b)").bitcast(BF16)
    ptr2 = ptr2_all.rearrange("p (k n) -> p k n", k=KC)
    for tt in range(NT):
        b = tt % 2
        mb = mb2[b]
        mT = mT2[b]
        if tt + 1 < NT:
            loads(tt + 1)
        rms_rstd(S, mt[b][:, 0:SSDW], SSDW, ss[:, 0:1], sq[:, 0:SSDW], "a", ["mt%d" % b, "eps"])
        S.dve(lambda e, b=b, mb=mb: e.scalar_tensor_tensor(mb[:, 0:SSDW], mt[b][:, 0:SSDW], ss[:, 0:1], nwb[:, 0, 0:SSDW], ALU.mult, ALU.mult),
              reads=["mt%d" % b, "ass", "nwb"], writes=["mb0_%d" % b])
        S.act(lambda e, b=b, mb=mb: e.copy(mb[:, SSDW:D], mt[b][:, SSDW:D]), reads=["mt%d" % b], writes=["mb1_%d" % b], c=0.9)
        for kc in range(KC):
            S.pe(lambda e, kc=kc, mb=mb: e.transpose(ptr[:, kc, :], mb[:, kc * 128:(kc + 1) * 128], identb),
                 reads=["mb0_%d" % b, "mb1_%d" % b, "ident"], writes=["ptr"])
        S.act(lambda e, mT=mT: e.copy(mT[:, 0:8, :], ptr[:, 0:8, :]), reads=["ptr"], writes=["mTa%d" % b], c=0.6)
        S.dve(lambda e, mT=mT: e.tensor_copy(mT[:, 8:16, :], ptr[:, 8:16, :]), reads=["ptr"], writes=["mTb%d" % b], c=0.5)
        for cb in range(4):
            for kc in range(KC):
                S.pe(lambda e, cb=cb, kc=kc, mT=mT: e.matmul(pacc[:, cb, :], mT[:, kc, :], wo[:, kc, cb * 512:(cb + 1) * 512],
                                                              start=(kc == 0), stop=(kc == KC - 1)),
                     reads=["mTa%d" % b, "mTb%d" % b, "wo%d" % cb], writes=["pacc%d" % cb], c=0.22)
            S.dve(lambda e, cb=cb, b=b: e.tensor_tensor(hs[b][:, cb * 512:(cb + 1) * 512], pacc[:, cb, :], xt[b][:, cb * 512:(cb + 1) * 512], ALU.add),
                  reads=["pacc%d" % cb, "xt%d" % b], writes=["hs%d_%d" % (b, cb)])
        hres = ["hs%d_%d" % (b, cb) for cb in range(4)]
        S.dma("sp", h_d[tt * 128:(tt + 1) * 128, :], hs[b], reads=hres, writes=["h_d%d" % tt])
        rms_rstd(S, hs[b], D, ss[:, 1:2], sq, "b", hres + ["eps"])
        S.dve(lambda e, b=b: e.scalar_tensor_tensor(vb, hs[b], ss[:, 1:2], nwb[:, 1, :], ALU.mult, ALU.mult),
              reads=hres + ["bss", "nwb"], writes=["vb"])
        for kc in range(KC):
            S.pe(lambda e, kc=kc: e.transpose(ptr2[:, kc, :], vb[:, kc * 128:(kc + 1) * 128], identb),
                 reads=["vb", "ident"], writes=["ptrv"])
        S.act(lambda e, tt=tt: e.copy(vT[:, 0:8, tt * 128:(tt + 1) * 128], ptr2[:, 0:8, :]), reads=["ptrv"], writes=["vT%da" % tt], c=0.6)
        S.dve(lambda e, tt=tt: e.tensor_copy(vT[:, 8:16, tt * 128:(tt + 1) * 128], ptr2[:, 8:16, :]), reads=["ptrv"], writes=["vT%db" % tt], c=0.5)

    S.barrier(dummy[1:2, :], ident[0:1, 0:64])
    A.reset(base_persist)
    hT = A.alloc([FC, TOK], BF16)
    HK = 22
    wd_pre = A.alloc([HK, 512], BF16)
    base_b = A.off
    NB = FF // WB
    wg = [wg_pre, A.alloc([KC, WB], BF16)]
    wu = [wu_pre, A.alloc([KC, WB], BF16)]
    sg = [A.alloc([TOK], BF16) for _ in range(2)]
    assert A.off <= pre_lo, (A.off, pre_lo)
    for blk in range(NB):
        b = blk % 2
        if blk > 0:
            S.dma("pool", wg[b], w_gate[:, blk * WB:(blk + 1) * WB].rearrange("(k p) n -> p k n", p=128), writes=["wg%d" % b], c=15.0)
            S.dma("pool", wu[b], w_up[:, blk * WB:(blk + 1) * WB].rearrange("(k p) n -> p k n", p=128), writes=["wu%d" % b], c=15.0)
        if blk == NB - 2:
            S.dma("pool", wd_pre, w_down[0:HK * 128, 0:512].rearrange("(k p) n -> p k n", p=128), writes=["wd0"], c=15.0)
        for j in range(WB // 128):
            fc = blk * (WB // 128) + j
            pb = fc % 2
            pg = pbig[:, pb * 4:pb * 4 + 2, :]
            pu = pbig[:, pb * 4 + 2:pb * 4 + 4, :]
            for hf in range(2):
                for kc in range(KC):
                    S.pe(lambda e, b=b, j=j, hf=hf, kc=kc, pg=pg: e.matmul(pg[:, hf, :], wg[b][:, kc, j * 128:(j + 1) * 128], vT[:, kc, hf * 512:(hf + 1) * 512],
                                                                         start=(kc == 0), stop=(kc == KC - 1)),
                         reads=["wg%d" % b, "vT"], writes=["pg%d" % pb], c=0.22)
            for hf in range(2):
                for kc in range(KC):
                    S.pe(lambda e, b=b, j=j, hf=hf, kc=kc, pu=pu: e.matmul(pu[:, hf, :], wu[b][:, kc, j * 128:(j + 1) * 128], vT[:, kc, hf * 512:(hf + 1) * 512],
                                                                         start=(kc == 0), stop=(kc == KC - 1)),
                         reads=["wu%d" % b, "vT"], writes=["pu%d" % pb], c=0.22)
            S.act(lambda e, pb=pb, pg=pg: e.activation(sg[pb], pg.rearrange("p a b -> p (a b)"), AF.Silu), reads=["pg%d" % pb], writes=["sg%d" % pb])
            S.dve(lambda e, pb=pb, pu=pu, fc=fc: e.tensor_tensor(hT[:, fc, :], sg[pb], pu.rearrange("p a b -> p (a b)"), ALU.mult),
                  reads=["sg%d" % pb, "pu%d" % pb], writes=["hT%d" % fc])

    S.barrier(dummy[1:2, :], ident[0:1, 0:64])
    A.reset(base_b)
    wd = [wd_pre, A.alloc([HK, 512], BF16)]
    hl = [A.alloc([512], F32) for _ in range(2)]
    ys = [A.alloc([512], F32) for _ in range(2)]
    it = 0
    for r in range(4):
        for hf in range(2):
            b = (r * 2 + hf) % 2
            if r * 2 + hf > 0:
                S.dma("pool", wd[b], w_down[hf * HK * 128:(hf + 1) * HK * 128, r * 512:(r + 1) * 512].rearrange("(k p) n -> p k n", p=128), writes=["wd%d" % b], c=15.0)
            for tt in range(NT):
                for k in range(HK):
                    kk = hf * HK + k
                    S.pe(lambda e, b=b, tt=tt, k=k, kk=kk: e.matmul(pbig[:, tt, :], hT[:, kk, tt * 128:(tt + 1) * 128], wd[b][:, k, :],
                                                                    start=(kk == 0), stop=(kk == FC - 1)),
                         reads=["wd%d" % b, "hT"], writes=["pd%d" % tt], c=0.22)
        for tt in range(NT):
            b = it % 2
            it += 1
            S.dma("sp", hl[b], h_d[tt * 128:(tt + 1) * 128, r * 512:(r + 1) * 512], reads=["h_d%d_%d" % (tt, r)], writes=["hl%d" % b])
            S.dve(lambda e, b=b, tt=tt: e.tensor_tensor(ys[b], pbig[:, tt, :], hl[b], ALU.add), reads=["pd%d" % tt, "hl%d" % b], writes=["ys%d" % b])
            S.dma("sp", h_d[tt * 128:(tt + 1) * 128, r * 512:(r + 1) * 512], ys[b], reads=["ys%d" % b], writes=["h_d%d_%d" % (tt, r)])

    S.barrier(dummy[1:2, :], ident[0:1, 0:64])
    A.reset(base_persist)
    yt = [A.alloc([D], F32) for _ in range(2)]
    ot = [A.alloc([D], F32) for _ in range(2)]
    sq2 = A.alloc([D], F32)
    for tt in range(NT):
        b = tt % 2
        S.dma("sp", yt[b], h_d[tt * 128:(tt + 1) * 128, :], writes=["yt%d" % b])
        rms_rstd(S, yt[b], D, ss[:, 2:3], sq2, "c", ["yt%d" % b, "eps"])
        S.dve(lambda e, b=b: e.scalar_tensor_tensor(ot[b], yt[b], ss[:, 2:3], nwb[:, 2, :], ALU.mult, ALU.mult),
              reads=["yt%d" % b, "css", "nwb"], writes=["ot%d" % b])
        S.dma("sp", out[tt * 128:(tt + 1) * 128, :], ot[b], reads=["ot%d" % b])


_CACHE = {}


def run_tail(x, mixed, ssd_norm_w, w_out, ffn_norm_w, w_gate, w_up, w_down, final_norm_w):
    if "tail" not in _CACHE:
        _CACHE["tail"] = build_tail()
    nc = _CACHE["tail"]
    nwv = np.ones((3, D), np.float32)
    nwv[0] = ffn_norm_w
    nwv[1] = final_norm_w
    nwv[2, :SSDW] = ssd_norm_w
    ident = np.eye(128, dtype=np.float32)
    in_maps = []
    for c in range(8):
        b, g = c // 4, c % 4
        in_maps.append({
            "x_own": np.ascontiguousarray(x[b, g * TOK:(g + 1) * TOK]),
            "mix": np.ascontiguousarray(mixed[b, g * TOK:(g + 1) * TOK]),
            "w_out": w_out, "w_gate": w_gate, "w_up": w_up, "w_down": w_down,
            "nw": nwv, "ident": ident,
        })
    res = run_bass_kernel_spmd(nc, in_maps, core_ids=list(range(8)))
    outp = np.empty((2, 4096, D), np.float32)
    for c in range(8):
        b, g = c // 4, c % 4
        outp[b, g * TOK:(g + 1) * TOK] = res.results[c]["out"]
    return outp


SEQ = 4096
NTT = SEQ // 128
NEG = -30000.0
NA = 400
NB_ = 384
NFM = 640
SCALE = 0.125


def build_mix():
    nc = bass.Bass("TRN2", target_bir_lowering=False)
    di = lambda n, s, d=F32: nc.dram_tensor(n, s, d, kind="ExternalInput").ap()
    T = dict(
        xb=di("xb", [SEQ, D]), anw=di("anw", [D]), w_tm=di("w_tm", [D, NA + NB_]), w_fm=di("w_fm", [D, NFM]),
        convw=di("convw", [128, 16]), convb=di("convb", [128, 4]), hp=di("hp", [12]), rope=di("rope", [128, NTT * 16]),
        w1k=di("w1k", [64, 32 * 256]), w1v=di("w1v", [64, 32 * 256]), w2k=di("w2k", [256, 64]), w2v=di("w2v", [256, 64]),
        pek=di("pek", [64, 32]), pev=di("pev", [64, 32]), ident=di("ident", [128, 128]), emat=di("emat", [64, SEQ]),
        tric=di("tric", [128, 128]), tria=di("tria", [128, 128]), cmask=di("cmask", [256, SEQ]), ovl=di("ovl", [256, 64]),
        utri=di("utri", [128, 128]),
    )
    T["mixo"] = nc.dram_tensor("mixo", [SEQ, 512], F32, kind="ExternalOutput").ap()
    T["qr_d"] = nc.dram_tensor("qr_d", [64, 4, SEQ], BF16, kind="Internal").ap()
    T["qu_d"] = nc.dram_tensor("qu_d", [64, 4, SEQ], BF16, kind="Internal").ap()
    T["dummy"] = nc.dram_tensor("dummy_bar", [2, 64], F32, kind="Internal").ap()
    with contextlib.ExitStack() as st:
        ASZ = 207 * 1024
        ar = st.enter_context(nc.sbuf_tensor("arena", [128, ASZ], U8))
        A = Arena(ar, ASZ)
        pbig = st.enter_context(nc.psum_tensor("pbig", [128, 8, 512], F32))
        sems = make_sems(nc, st)
        block = st.enter_context(nc.Block())
        S = Sched(nc)
        mix_body(nc, S, A, pbig, T)
        if os.environ.get('NO_REORDER') is None:
            S.reorder()
        S.emit(sems, block)
    return nc


def pbf(pb, lo, hi):
    return pb[:, lo:hi, :].rearrange("p a b -> p (a b)").bitcast(BF16)


def mix_body(nc, S, A, pbig, T):
    bar = lambda: S.barrier(T["dummy"][1:2, :], T["ident"][0:1, 0:64])
    bk = {"ptr": (0,), "ptr2": (6,), "psA": (1,), "psB": (2,), "psF0": (3,), "psR": (4,), "psS_a": (5,), "psS_c": (5,),
          "psN": (6,), "psY_o": (7,), "psY_d": (7,), "pk": (5,), "pv0": (6,), "pv1": (7,), "pnt": (7,), "posel": (6,), "powin": (7,)}
    for i_ in range(4):
        bk["pbias%d" % i_] = (4,)
        bk["phid%d" % i_] = (i_,)
        bk["psw%d" % i_] = (3 + i_ % 2,)
    for a_ in range(2):
        bk["po%d" % a_] = (4 + a_,)
        for b_ in range(2):
            bk["psc%d_%d" % (a_, b_)] = (a_ * 2 + b_,)
    for i_ in range(3):
        bk["pss%d" % i_] = (i_,)
    bk["poselT"] = (5,)
    S.bank_of = bk
    identb = A.alloc([128], BF16)
    utri = A.alloc([128], F32)
    identf = A.alloc([128], F32)
    tricb = A.alloc([128], BF16)
    triab = A.alloc([128], BF16)
    epsb = A.alloc([1], F32)
    oneb = A.alloc([1], F32)
    EPS_AP[0] = epsb
    KsA = A.alloc([SEQ], BF16)
    KwT = A.alloc([SEQ], BF16)
    kcvcT = A.alloc([SEQ], BF16)
    VsA = A.alloc([NTT, 65], BF16)
    VwA = A.alloc([NTT, 65], BF16)
    gate = A.alloc([NTT, 12], F32)
    hpb = A.alloc([12], F32)
    base_persist = A.off
    S.pool(lambda e: e.memset(epsb, EPS), writes=["eps"])
    S.pool(lambda e: e.memset(oneb, 1.0), writes=["one"])
    S.pool(lambda e: e.memset(VsA, 1.0), writes=["VsA"])
    S.pool(lambda e: e.memset(VwA, 1.0), writes=["VwA"])
    S.dma("pool", identb, T["ident"], writes=["ident"])
    S.dma("sp", utri, T["utri"], writes=["utri"])
    S.dma("sp", identf, T["ident"], writes=["identf"])
    S.dma("pool", tricb, T["tric"], writes=["tric"])
    S.dma("pool", triab, T["tria"], writes=["tria"])
    S.dma("pool", KsA[64:128, :], T["emat"], writes=["KsE"])
    S.dma("sp", hpb, T["hp"].partition_broadcast(128), writes=["hpb"])

    wtm = A.alloc([KC, NA + NB_], BF16)
    wfm = A.alloc([KC, NFM], BF16)
    S.dma("pool", wfm, T["w_fm"].rearrange("(k p) n -> p k n", p=128), writes=["wfm"], c=30.0)
    S.dma("pool", wtm, T["w_tm"].rearrange("(k p) n -> p k n", p=128), writes=["wtm"], c=35.0)
    anwb = A.alloc([D], F32)
    S.dma("sp", anwb, T["anw"].partition_broadcast(128), writes=["anwb"])
    ropet = A.alloc([NTT, 16], F32)
    S.dma("sp", ropet, T["rope"].rearrange("p (t c) -> p t c", c=16), writes=["ropet"])
    convw = A.alloc([16], F32)
    convb = A.alloc([4], F32)
    S.dma("sp", convw, T["convw"], writes=["convw"])
    S.dma("sp", convb, T["convb"], writes=["convb"])
    onesf = A.alloc([128], F32)
    S.pool(lambda e: e.memset(onesf, 1.0), writes=["onesf"])
    two = lambda shape, dt: [A.alloc(shape, dt) for _ in range(2)]
    xt = two([D], F32)
    sq1 = A.alloc([D], BF16)
    sq = [sq1, sq1]
    ss = A.alloc([8], F32)
    ub = two([D], BF16)
    uT2 = two([KC, 512], BF16)
    cbuf = A.alloc([4, 515], F32)
    cacc_1 = A.alloc([512], F32)
    cacc = [cacc_1, cacc_1]
    xbcT = two([4, 512], BF16)
    zs = two([256], BF16)
    ez_1 = A.alloc([256], F32)
    ez = [ez_1, ez_1]
    ec = two([512], F32)
    dtt = two([4], F32)
    qk = two([6, 64], F32)
    qkr = two([6, 64], BF16)
    qkb = two([4, 64], BF16)
    rt = two([4, 6, 8], F32)
    qst = two([4, 128], BF16)
    qut = two([4, 128], BF16)
    xtm = two([256], BF16)
    btm = two([128], BF16)
    hst = A.alloc([256], F32)
    hstb = two([256], BF16)
    aneg = A.alloc([4], F32)
    adt = two([4], F32)
    acol = two([4], F32)
    nacol = two([4], F32)
    eac = two([4], F32)
    rhs4_1 = A.alloc([4, 128], F32)
    rhs4 = [rhs4_1, rhs4_1]
    seg4_1 = A.alloc([4, 128], F32)
    seg4 = [seg4_1, seg4_1]
    cbm = two([128], F32)
    MT = two([4, 128], BF16)
    alast = two([4], F32)
    dsv = two([4], F32)
    cdv = two([4], F32)
    wsc = two([4], F32)
    xw = two([256], BF16)
    xdt = two([256], BF16)
    ydg_1 = A.alloc([256], F32)
    ydg = [ydg_1, ydg_1]
    yy_1 = A.alloc([256], F32)
    yy = [yy_1, yy_1]
    tmpd_1 = A.alloc([256], F32)
    tmpd = [tmpd_1, tmpd_1]
    yo = two([256], F32)
    S.pool(lambda e: e.memset(cbuf, 0.0), writes=["cbuf", "cbuf0", "cbuf1", "cbuf2", "cbuf3"])
    S.pool(lambda e: e.memset(hst, 0.0), writes=["hst"])
    S.act(lambda e: e.activation(aneg, hpb[:, 4:8], AF.Exp), reads=["hpb"], writes=["aneg"])
    S.dve(lambda e: e.tensor_scalar(aneg, aneg, -1.0, None, ALU.mult), reads=["aneg"], writes=["aneg"])

    ptr = pbf(pbig, 0, 1)
    psA = pbig[:, 1, 0:NA]
    psB = pbig[:, 2, 0:NB_]
    psF = pbig[:, 3, :]
    psR = pbig[:, 4, :]
    psS = pbig[:, 5, :]
    psN = pbig[:, 6, 0:256]
    psY = pbig[:, 7, :]

    def load_x(tt):
        S.dma("sp", xt[tt % 2], T["xb"][tt * 128:(tt + 1) * 128, :], writes=["xt%d" % (tt % 2)], c=5.0)

    def b4(ap, n):
        return ap.unsqueeze(2).to_broadcast([128, 4, n])

    load_x(0)
    for G in range(SEQ // 512):
        gp = G % 2
        uT = uT2[gp]
        for j in range(4):
            tt = G * 4 + j
            b = tt % 2
            if tt + 1 < NTT:
                load_x(tt + 1)
            ssb = ss[:, b:b + 1]
            S.act(lambda e, b=b, ssb=ssb: e.activation(sq[b], xt[b], AF.Square, accum_out=ssb), reads=["xt%d" % b], writes=["n%dss" % b, "nsq"], c=1.9)
            S.act(lambda e, ssb=ssb: e.activation(ssb, ssb, AF.Ln, scale=1.0 / D, bias=epsb), reads=["n%dss" % b, "eps"], writes=["n%dss" % b])
            S.act(lambda e, ssb=ssb: e.activation(ssb, ssb, AF.Exp, scale=-0.5), reads=["n%dss" % b], writes=["n%dss" % b])
            S.dve(lambda e, b=b: e.scalar_tensor_tensor(ub[b], xt[b], ss[:, b:b + 1], anwb, ALU.mult, ALU.mult),
                  reads=["xt%d" % b, "n%dss" % b, "anwb"], writes=["ub%d" % b], c=2.2)
            for half in range(2):
                for k8 in range(8):
                    kc = half * 8 + k8
                    S.pe(lambda e, kc=kc, k8=k8, b=b: e.transpose(ptr[:, k8 * 128:(k8 + 1) * 128], ub[b][:, kc * 128:(kc + 1) * 128], identb),
                         reads=["ub%d" % b, "ident"], writes=["ptr"])
                if half == 0:
                    S.act(lambda e, j=j, uT=uT: e.copy(uT[:, 0:8, j * 128:(j + 1) * 128], ptr.rearrange("p (k n) -> p k n", k=8)), reads=["ptr"], writes=["uT%d_%d_0" % (gp, j)], c=0.9)
                else:
                    S.dve(lambda e, j=j, uT=uT: e.tensor_copy(uT[:, 8:16, j * 128:(j + 1) * 128], ptr.rearrange("p (k n) -> p k n", k=8)), reads=["ptr"], writes=["uT%d_%d_1" % (gp, j)], c=0.7)
        uTr = ["uT%d_%d_%d" % (gp, j, h) for j in range(4) for h in range(2)]
        for c in range(5):
            for kc in range(KC):
                S.pe(lambda e, c=c, kc=kc, uT=uT: e.matmul(psF, wfm[:, kc, c * 128:(c + 1) * 128], uT[:, kc, :], start=(kc == 0), stop=(kc == KC - 1)),
                     reads=uTr + ["wfm"], writes=["psF0"], c=0.22)
            if c < 4:
                ca = cacc[c % 2]
                car = "cacc"
                S.act(lambda e, c=c: e.copy(cbuf[:, c, 3:515], psF), reads=["psF0"], writes=["cbuf%d" % c], c=0.6)
                S.dve(lambda e, c=c, ca=ca: e.tensor_scalar(ca, cbuf[:, c, 0:512], convw[:, c * 4:c * 4 + 1], convb[:, c:c + 1], ALU.mult, ALU.add), reads=["cbuf%d" % c, "convw", "convb"], writes=[car], c=0.4)
                for k in range(1, 4):
                    S.dve(lambda e, c=c, k=k, ca=ca: e.scalar_tensor_tensor(ca, cbuf[:, c, k:k + 512], convw[:, c * 4 + k:c * 4 + k + 1], ca, ALU.mult, ALU.add),
                          reads=["cbuf%d" % c, car, "convw"], writes=[car], c=0.65)
                ece = ec[c % 2]
                ecr = "ec%d" % (c % 2)
                S.act(lambda e, ca=ca, ece=ece: e.activation(ece, ca, AF.Exp, scale=-1.0), reads=[car], writes=[ecr], c=0.6)
                S.act(lambda e, ece=ece: e.activation(ece, ece, AF.Ln, bias=oneb), reads=[ecr, "one"], writes=[ecr], c=0.6)
                S.act(lambda e, ece=ece: e.activation(ece, ece, AF.Exp, scale=-1.0), reads=[ecr], writes=[ecr], c=0.6)
                S.pool(lambda e, c=c, ca=ca, ece=ece, gp=gp: e.tensor_tensor(xbcT[gp][:, c, :], ca, ece, ALU.mult), reads=[car, ecr], writes=["xbcT%d_%d" % (gp, c)], c=2.0)
                S.pool(lambda e, c=c: e.tensor_copy(cbuf[:, c, 0:3], cbuf[:, c, 512:515]), reads=["cbuf%d" % c], writes=["cbuf%d" % c])
            else:
                S.act(lambda e, G=G: e.copy(kcvcT[:, G * 512:(G + 1) * 512], psF), reads=["psF0"], writes=["kcvcT"], c=0.6)
        xr = lambda c: "xbcT%d_%d" % (gp, c)
        for j in range(4):
            tt = G * 4 + j
            p = tt % 2
            P = str(p)
            tok = slice(tt * 128, (tt + 1) * 128)
            js = slice(j * 128, (j + 1) * 128)
            for kc in range(KC):
                S.pe(lambda e, js=js, kc=kc, uT=uT: e.matmul(psA, uT[:, kc, js], wtm[:, kc, 0:NA], start=(kc == 0), stop=(kc == KC - 1)),
                     reads=uTr + ["wtm"], writes=["psA"], c=0.19)
            for kc in range(KC):
                S.pe(lambda e, js=js, kc=kc, uT=uT: e.matmul(psB, uT[:, kc, js], wtm[:, kc, NA:NA + NB_], start=(kc == 0), stop=(kc == KC - 1)),
                     reads=uTr + ["wtm"], writes=["psB"], c=0.18)
            S.act(lambda e, p=p: e.activation(ez[p], psA[:, 0:256], AF.Exp, scale=-1.0), reads=["psA"], writes=["ez"])
            S.act(lambda e, p=p: e.activation(ez[p], ez[p], AF.Ln, bias=oneb), reads=["ez", "one"], writes=["ez"])
            S.act(lambda e, p=p: e.activation(ez[p], ez[p], AF.Exp, scale=-1.0), reads=["ez"], writes=["ez"])
            S.dve(lambda e, p=p: e.tensor_tensor(zs[p], psA[:, 0:256], ez[p], ALU.mult), reads=["psA", "ez"], writes=["zs" + P])
            S.dve(lambda e, p=p: e.tensor_tensor(dtt[p], psA[:, 256:260], hpb[:, 0:4], ALU.add), reads=["psA", "hpb"], writes=["dtt" + P])
            S.act(lambda e, p=p: e.activation(dtt[p], dtt[p], AF.Exp), reads=["dtt" + P], writes=["dtt" + P])
            S.act(lambda e, p=p: e.activation(dtt[p], dtt[p], AF.Ln, bias=oneb), reads=["dtt" + P, "one"], writes=["dtt" + P])
            S.act(lambda e, tt=tt: e.activation(gate[:, tt, :], psA[:, 260:272], AF.Exp, scale=-1.0), reads=["psA"], writes=["gate%d" % tt])
            S.dve(lambda e, tt=tt: e.tensor_scalar(gate[:, tt, :], gate[:, tt, :], 1.0, None, ALU.add), reads=["gate%d" % tt], writes=["gate%d" % tt])
            S.dve(lambda e, tt=tt: e.reciprocal(gate[:, tt, :], gate[:, tt, :]), reads=["gate%d" % tt], writes=["gate%d" % tt])
            S.dve(lambda e, tt=tt: e.tensor_copy(VsA[:, tt, 0:64], psA[:, 272:336]), reads=["psA", "VsA"], writes=["VsA%d" % tt])
            S.dve(lambda e, tt=tt: e.tensor_copy(VwA[:, tt, 0:64], psA[:, 336:400]), reads=["psA", "VwA"], writes=["VwA%d" % tt])
            S.act(lambda e, p=p: e.copy(qk[p], psB.rearrange("p (a b) -> p a b", a=6)), reads=["psB"], writes=["qk" + P], c=0.5)
            S.pool(lambda e, p=p: e.tensor_copy(qkb[p], qk[p][:, 0:4, :]), reads=["qk" + P], writes=["qkb" + P])
            S.pool(lambda e, p=p: e.tensor_copy(qkr[p], qk[p]), reads=["qk" + P], writes=["qkr" + P])
            cosb = ropet[:, tt, 0:8].unsqueeze(1).to_broadcast([128, 6, 8])
            sinb = ropet[:, tt, 8:16].unsqueeze(1).to_broadcast([128, 6, 8])
            S.dve(lambda e, cosb=cosb, p=p: e.tensor_tensor(rt[p][:, 0], qk[p][:, :, 0:8], cosb, ALU.mult), reads=["qk" + P, "ropet"], writes=["rt0" + P])
            S.dve(lambda e, sinb=sinb, p=p: e.tensor_tensor(rt[p][:, 1], qk[p][:, :, 8:16], sinb, ALU.mult), reads=["qk" + P, "ropet"], writes=["rt1" + P])
            S.dve(lambda e, cosb=cosb, p=p: e.tensor_tensor(rt[p][:, 2], qk[p][:, :, 8:16], cosb, ALU.mult), reads=["qk" + P, "ropet"], writes=["rt2" + P])
            S.dve(lambda e, sinb=sinb, p=p: e.tensor_tensor(rt[p][:, 3], qk[p][:, :, 0:8], sinb, ALU.mult), reads=["qk" + P, "ropet"], writes=["rt3" + P])
            S.dve(lambda e, p=p: e.tensor_tensor(qkr[p][:, :, 0:8], rt[p][:, 0], rt[p][:, 1], ALU.subtract), reads=["rt0" + P, "rt1" + P, "qkr" + P], writes=["qkr" + P])
            S.dve(lambda e, p=p: e.tensor_tensor(qkr[p][:, :, 8:16], rt[p][:, 2], rt[p][:, 3], ALU.add), reads=["rt2" + P, "rt3" + P, "qkr" + P], writes=["qkr" + P])
            ptq = ptr[0:64, 0:768].rearrange("p (a b) -> p a b", a=6)
            ptu = pbf(pbig, 6, 7)[0:64, 512:1024].rearrange("p (a b) -> p a b", a=4)
            for a in range(6):
                S.pe(lambda e, a=a, p=p: e.transpose(ptq[:, a, :], qkr[p][:, a, :], identb), reads=["qkr" + P, "ident"], writes=["ptr"])
            for a in range(4):
                S.pe(lambda e, a=a, p=p: e.transpose(ptu[:, a, :], qkb[p][:, a, :], identb), reads=["qkb" + P, "ident"], writes=["ptr2"])
            S.act(lambda e, p=p: e.copy(qst[p][0:64], ptq[:, 0:4, :]), reads=["ptr"], writes=["qst" + P])
            S.dve(lambda e, p=p: e.tensor_copy(qut[p][0:64], ptu), reads=["ptr2"], writes=["qut" + P])
            S.act(lambda e, tok=tok: e.copy(KsA[0:64, tok], ptq[:, 4, :]), reads=["ptr"], writes=["KsA%d" % tt])
            S.dve(lambda e, tok=tok: e.tensor_copy(KwT[0:64, tok], ptq[:, 5, :]), reads=["ptr"], writes=["KwT%d" % tt])
            S.dma("sp", T["qr_d"][:, :, tok], qst[p][0:64], reads=["qst" + P], writes=["qr_d%d" % tt])
            S.dma("sp", T["qu_d"][:, :, tok], qut[p][0:64], reads=["qut" + P], writes=["qu_d%d" % tt])
            ptx = ptr[:, 0:384]
            for c in range(3):
                S.pe(lambda e, c=c, js=js, gp=gp: e.transpose(ptx[:, c * 128:(c + 1) * 128], xbcT[gp][:, c, js], identb),
                     reads=[xr(c), "ident"], writes=["ptr"])
            S.act(lambda e, p=p: e.copy(xtm[p], ptx[:, 0:256]), reads=["ptr"], writes=["xtm" + P])
            S.act(lambda e, p=p: e.copy(btm[p], ptx[:, 256:384]), reads=["ptr"], writes=["btm" + P])
            S.dve(lambda e, p=p: e.tensor_tensor(adt[p], dtt[p], aneg, ALU.mult), reads=["dtt" + P, "aneg"], writes=["adt" + P])
            S.pe(lambda e, p=p: e.matmul(psS[:, 0:4], utri, adt[p], start=True, stop=True), reads=["utri", "adt" + P], writes=["psS_a"])
            S.act(lambda e, p=p: e.copy(acol[p], psS[:, 0:4]), reads=["psS_a"], writes=["acol" + P])
            S.dve(lambda e, p=p: e.tensor_scalar(nacol[p], psS[:, 0:4], -1.0, None, ALU.mult), reads=["psS_a"], writes=["nacol" + P])
            S.act(lambda e, p=p: e.activation(eac[p], acol[p], AF.Exp), reads=["acol" + P], writes=["eac" + P])
            S.pe(lambda e, js=js, gp=gp: e.matmul(psS[:, 256:384], xbcT[gp][:, 2, js], xbcT[gp][:, 3, js], start=True, stop=True),
                 reads=[xr(2), xr(3)], writes=["psS_c"])
            S.dve(lambda e, p=p: e.tensor_tensor(cbm[p], psS[:, 256:384], utri, ALU.mult), reads=["psS_c", "utri"], writes=["cbm" + P])
            S.dve(lambda e, p=p: e.tensor_tensor(rhs4[p], utri.unsqueeze(1).to_broadcast([128, 4, 128]), b4(adt[p], 128), ALU.mult),
                  reads=["utri", "adt" + P], writes=["rhs4"], c=0.6)
            S.pe(lambda e, p=p: e.matmul(psR, onesf, rhs4[p].rearrange("p a b -> p (a b)"), start=True, stop=True), reads=["onesf", "rhs4"], writes=["psR"], c=0.9)
            psR4 = psR.rearrange("p (a b) -> p a b", a=4)
            S.dve(lambda e, p=p: e.tensor_tensor(seg4[p], psR4, b4(acol[p], 128), ALU.subtract), reads=["psR", "acol" + P], writes=["seg4"], c=0.7)
            S.dve(lambda e, p=p: e.tensor_scalar(seg4[p], seg4[p], 0.0, None, ALU.min), reads=["seg4"], writes=["seg4"], c=0.35)
            S.act(lambda e, p=p: e.activation(seg4[p], seg4[p], AF.Exp), reads=["seg4"], writes=["seg4"], c=0.6)
            S.dve(lambda e, p=p: e.tensor_tensor(MT[p], seg4[p], cbm[p].unsqueeze(1).to_broadcast([128, 4, 128]), ALU.mult),
                  reads=["seg4", "cbm" + P], writes=["MT" + P], c=0.6)
            S.dve(lambda e, p=p: e.tensor_copy(alast[p], psR4[:, :, 127]), reads=["psR"], writes=["alast" + P])
            S.dve(lambda e, p=p: e.tensor_tensor(dsv[p], nacol[p], alast[p], ALU.add), reads=["nacol" + P, "alast" + P], writes=["dsv" + P])
            S.act(lambda e, p=p: e.activation(dsv[p], dsv[p], AF.Exp), reads=["dsv" + P], writes=["dsv" + P])
            S.act(lambda e, p=p: e.activation(cdv[p], alast[p], AF.Exp), reads=["alast" + P], writes=["cdv" + P])
            S.dve(lambda e, p=p: e.tensor_tensor(wsc[p], dtt[p], dsv[p], ALU.mult), reads=["dtt" + P, "dsv" + P], writes=["wsc" + P])
            v4 = lambda ap: ap.rearrange("p (h d) -> p h d", h=4)
            S.dve(lambda e, p=p: e.tensor_tensor(v4(xw[p]), v4(xtm[p]), b4(wsc[p], 64), ALU.mult), reads=["xtm" + P, "wsc" + P], writes=["xw" + P])
            S.dve(lambda e, p=p: e.tensor_tensor(v4(xdt[p]), v4(xtm[p]), b4(dtt[p], 64), ALU.mult), reads=["xtm" + P, "dtt" + P], writes=["xdt" + P])
            S.pool(lambda e, p=p: e.tensor_copy(hstb[p], hst), reads=["hst"], writes=["hstb" + P])
            S.pe(lambda e, js=js, gp=gp, p=p: e.matmul(psY[:, 0:256], xbcT[gp][:, 3, js], hstb[p], start=True, stop=True), reads=[xr(3), "hstb" + P], writes=["psY_o"])
            for h in range(4):
                S.pe(lambda e, h=h, p=p: e.matmul(psY[:, 256 + h * 64:256 + (h + 1) * 64], MT[p][:, h, :], xdt[p][:, h * 64:(h + 1) * 64], start=True, stop=True),
                     reads=["MT" + P, "xdt" + P], writes=["psY_d"])
            S.pe(lambda e, p=p: e.matmul(psN, btm[p], xw[p], start=True, stop=True), reads=["btm" + P, "xw" + P], writes=["psN"])
            S.dve(lambda e, p=p: e.tensor_tensor(v4(hst), v4(hst), b4(cdv[p], 64), ALU.mult), reads=["hst", "cdv" + P], writes=["hst"])
            S.dve(lambda e: e.tensor_tensor(hst, hst, psN, ALU.add), reads=["hst", "psN"], writes=["hst"])
            S.act(lambda e, p=p: e.copy(ydg[p], psY[:, 256:512]), reads=["psY_d"], writes=["ydg"])
            S.dve(lambda e, p=p: e.tensor_tensor(v4(yy[p]), v4(psY[:, 0:256]), b4(eac[p], 64), ALU.mult), reads=["psY_o", "eac" + P], writes=["yy"])
            S.pool(lambda e, p=p: e.tensor_tensor(v4(tmpd[p]), v4(xtm[p]), b4(hpb[:, 8:12], 64), ALU.mult), reads=["xtm" + P, "hpb"], writes=["tmpd"])
            S.dve(lambda e, p=p: e.tensor_tensor(yy[p], yy[p], ydg[p], ALU.add), reads=["yy", "ydg"], writes=["yy"])
            S.dve(lambda e, p=p: e.tensor_tensor(yy[p], yy[p], tmpd[p], ALU.add), reads=["yy", "tmpd"], writes=["yy"])
            S.dve(lambda e, p=p: e.tensor_tensor(yo[p], yy[p], zs[p], ALU.mult), reads=["yy", "zs" + P], writes=["yo" + P])
            S.dma("sp", T["mixo"][tok, 0:256], yo[p], reads=["yo" + P], writes=["mixo_s%d" % tt])

    if int(os.environ.get('MIX_STOP', '9')) <= 1:
        return
    bar()
    A.reset(base_persist)
    attO = A.alloc([NTT, 256], F32)
    nmT = A.alloc([SEQ], BF16)
    w1 = A.alloc([32, 256], BF16)
    S.dma("pool", w1[0:64], T["w1k"].rearrange("d (l h) -> d l h", l=32), writes=["w1k"])
    S.dma("pool", w1[64:128], T["w1v"].rearrange("d (l h) -> d l h", l=32), writes=["w1v"])
    pe_ = A.alloc([32], BF16)
    S.dma("pool", pe_[0:64], T["pek"], writes=["pek"])
    S.dma("pool", pe_[64:128], T["pev"], writes=["pev"])
    w2 = A.alloc([2, 2, 64], BF16)
    S.dma("pool", w2[:, 0], T["w2k"].rearrange("(c p) d -> p c d", p=128), writes=["w2k"])
    S.dma("pool", w2[:, 1], T["w2v"].rearrange("(c p) d -> p c d", p=128), writes=["w2v"])
    cbias = A.alloc([4], F32)
    hsb = A.alloc([4, 256], BF16)
    KcT = A.alloc([256], BF16)
    VcA = A.alloc([2, 129], BF16)
    S.pool(lambda e: e.memset(hsb, 0.0), writes=["hsb"])
    S.pool(lambda e: e.memset(VcA, 0.0), writes=["VcA"])
    S.pool(lambda e: e.memset(KcT, 0.0), writes=["KcT"])
    S.pool(lambda e: e.memset(VcA[:, :, 64:65], 1.0), reads=["VcA"], writes=["VcA"])
    S.dma("pool", VcA[:, :, 65:129], T["ovl"].rearrange("(c p) j -> p c j", p=128), reads=["VcA"], writes=["VcA"])
    for kv in range(2):
        rows = slice(kv * 64, (kv + 1) * 64)
        for hc in range(2):
            idx = kv * 2 + hc
            pb_ = pbig[:, idx, 0:255]
            pbias = pbig[:, 4, idx:idx + 1]
            for l in range(32):
                S.pe(lambda e, rows=rows, hc=hc, l=l, pbias=pbias: e.matmul(pbias, w1[rows, l, hc * 128:(hc + 1) * 128], pe_[rows, l:l + 1], start=(l == 0), stop=(l == 31)),
                     reads=["w1k", "w1v", "pek", "pev"], writes=["pbias%d" % idx])
            S.act(lambda e, idx=idx, pbias=pbias: e.copy(cbias[:, idx:idx + 1], pbias), reads=["pbias%d" % idx], writes=["cbias%d" % idx])
            for l in range(32):
                S.pe(lambda e, rows=rows, hc=hc, l=l, pb_=pb_: e.matmul(pb_, w1[rows, l, hc * 128:(hc + 1) * 128], kcvcT[rows, l:l + 16 * 254 + 1:16], start=(l == 0), stop=(l == 31)),
                     reads=["w1k", "w1v", "kcvcT"], writes=["phid%d" % idx])
            S.act(lambda e, idx=idx, pb_=pb_: e.activation(hsb[:, idx, 0:255], pb_, AF.Silu, bias=cbias[:, idx:idx + 1]), reads=["phid%d" % idx, "cbias%d" % idx, "hsb"], writes=["hsb%d" % idx])
    pk = pbig[0:64, 5, 0:255]
    for hc in range(2):
        S.pe(lambda e, hc=hc: e.matmul(pk, w2[:, 0, hc, :], hsb[:, hc, 0:255], start=(hc == 0), stop=(hc == 1)), reads=["w2k", "hsb0", "hsb1"], writes=["pk"])
    S.act(lambda e: e.copy(KcT[0:64, 0:255], pk), reads=["pk", "KcT"], writes=["KcT"])
    for it in range(2):
        m = 128 if it == 0 else 127
        pv = pbig[0:m, 6 + it, 0:64]
        for hc in range(2):
            S.pe(lambda e, it=it, hc=hc, m=m, pv=pv: e.matmul(pv, hsb[:, 2 + hc, it * 128:it * 128 + m], w2[:, 1, hc, :], start=(hc == 0), stop=(hc == 1)),
                 reads=["w2v", "hsb2", "hsb3"], writes=["pv%d" % it])
        S.act(lambda e, it=it, m=m, pv=pv: e.copy(VcA[0:m, it, 0:64], pv), reads=["pv%d" % it, "VcA"], writes=["VcA"])

    if int(os.environ.get('MIX_STOP', '9')) <= 2:
        return
    bar()
    base3 = A.off
    qu = [A.alloc([4, 512], BF16) for _ in range(2)]
    cmk = [A.alloc([2, 512], BF16) for _ in range(2)]
    PcT = [A.alloc([2, 512], BF16) for _ in range(2)]
    imp = A.alloc([4, 64], F32)
    imp2 = A.alloc([64], F32)
    m8 = A.alloc([16], F32)
    thr = A.alloc([1], F32)
    rr = A.alloc([32], F32)
    nmb = A.alloc([128], BF16)
    S.pool(lambda e: e.memset(nmb, 0.0), writes=["nmb"])
    pnt = pbf(pbig, 7, 8)[:, 0:128]
    for Q in range(8):
        qb = Q % 2
        qs = slice(Q * 512, (Q + 1) * 512)
        S.dma("sp", qu[qb][0:64], T["qu_d"][:, :, qs], writes=["qu%d" % qb])
        S.dma("pool", cmk[qb], T["cmask"][:, qs].rearrange("(c p) t -> p c t", p=128), writes=["cmk%d" % qb])
        for h in range(4):
            pb2 = h % 2
            for it in range(2):
                ps_ = pbig[:, pb2 * 2 + it, :]
                S.pe(lambda e, it=it, h=h, qb=qb, ps_=ps_: e.matmul(ps_, KcT[0:64, it * 128:(it + 1) * 128], qu[qb][0:64, h, :], start=True, stop=False),
                     reads=["KcT", "qu%d" % qb], writes=["psc%d_%d" % (pb2, it)])
                S.pe(lambda e, it=it, qb=qb, ps_=ps_: e.matmul(ps_, identb, cmk[qb][:, it, :], start=False, stop=True),
                     reads=["ident", "cmk%d" % qb], writes=["psc%d_%d" % (pb2, it)])
                S.act(lambda e, it=it, pb2=pb2, ps_=ps_: e.activation(PcT[pb2][:, it, :], ps_, AF.Exp, scale=SCALE), reads=["psc%d_%d" % (pb2, it)], writes=["PcT%d_%d" % (pb2, it)])
            for sub in range(4):
                tt = Q * 4 + sub
                po = pbig[:, 4 + (sub % 2), 0:129]
                for it in range(2):
                    S.pe(lambda e, it=it, pb2=pb2, sub=sub, po=po: e.matmul(po, PcT[pb2][:, it, sub * 128:(sub + 1) * 128], VcA[:, it, :], start=(it == 0), stop=(it == 1)),
                         reads=["PcT%d_0" % pb2, "PcT%d_1" % pb2, "VcA"], writes=["po%d" % (sub % 2)])
                pr = ["po%d" % (sub % 2)]
                ri = ((h % 2) * 4 + sub) * 2
                S.dve(lambda e, ri=ri, po=po: e.tensor_scalar(rr[:, ri:ri + 1], po[:, 64:65], 1e-30, None, ALU.add), reads=pr, writes=["rr0_%d" % ri])
                S.dve(lambda e, ri=ri: e.reciprocal(rr[:, ri:ri + 1], rr[:, ri:ri + 1]), reads=["rr0_%d" % ri], writes=["rr0_%d" % ri])
                if h == 0:
                    S.act(lambda e, ri=ri, po=po, sub=sub: e.activation(imp[:, sub, :], po[:, 65:129], AF.Copy, scale=rr[:, ri:ri + 1]), reads=pr + ["rr0_%d" % ri], writes=["imp%d" % sub])
                else:
                    S.dve(lambda e, ri=ri, po=po, sub=sub: e.scalar_tensor_tensor(imp[:, sub, :], po[:, 65:129], rr[:, ri:ri + 1], imp[:, sub, :], ALU.mult, ALU.add),
                          reads=pr + ["rr0_%d" % ri, "imp%d" % sub], writes=["imp%d" % sub])
                S.dve(lambda e, ri=ri, tt=tt, h=h: e.tensor_tensor(rr[:, ri + 1:ri + 2], rr[:, ri:ri + 1], gate[:, tt, h * 3:h * 3 + 1], ALU.mult), reads=["rr0_%d" % ri, "gate"], writes=["rr1_%d" % ri])
                S.act(lambda e, ri=ri, po=po, tt=tt, h=h: e.activation(attO[:, tt, h * 64:(h + 1) * 64], po[:, 0:64], AF.Copy, scale=rr[:, ri + 1:ri + 2]), reads=pr + ["rr1_%d" % ri], writes=["attO%d" % tt])
        for sub in range(4):
            tt = Q * 4 + sub
            ir = ["imp%d" % sub]
            im = imp[:, sub, :]
            S.pool(lambda e, im=im: e.memset(im[:, 0:1], 1e4), reads=ir, writes=ir)
            lo = max(2 * tt - 1, 0)
            S.pool(lambda e, im=im, lo=lo, tt=tt: e.memset(im[0:64, lo:2 * tt + 1], 1e4), reads=ir, writes=ir)
            S.pool(lambda e, im=im, tt=tt: e.memset(im[64:128, 2 * tt:2 * tt + 2], 1e4), reads=ir, writes=ir)
            if 2 * tt + 1 < 64:
                S.pool(lambda e, im=im, tt=tt: e.memset(im[0:64, 2 * tt + 1:64], -1.0), reads=ir, writes=ir)
            if 2 * tt + 2 < 64:
                S.pool(lambda e, im=im, tt=tt: e.memset(im[64:128, 2 * tt + 2:64], -1.0), reads=ir, writes=ir)
            S.dve(lambda e, im=im: e.max(m8[:, 0:8], im), reads=ir, writes=["m8a"])
            S.dve(lambda e, im=im: e.match_replace(imp2, m8[:, 0:8], im, -1e30), reads=ir + ["m8a"], writes=["imp2"])
            S.dve(lambda e: e.max(m8[:, 8:16], imp2), reads=["imp2"], writes=["m8b"])
            S.dve(lambda e: e.tensor_scalar(thr, m8[:, 15:16], 0.0, None, ALU.max), reads=["m8b"], writes=["thr"])
            S.dve(lambda e, im=im: e.tensor_scalar(nmb[:, 64:128], im, thr, NEG, ALU.is_lt, ALU.mult), reads=ir + ["thr", "nmb"], writes=["nmb"])
            S.pe(lambda e: e.transpose(pnt, nmb, identb), reads=["nmb", "ident"], writes=["pnt"])
            S.act(lambda e, tt=tt: e.copy(nmT[64:128, tt * 128:(tt + 1) * 128], pnt[64:128, :]), reads=["pnt"], writes=["nmT"])

    if int(os.environ.get('MIX_STOP', '9')) <= 3:
        return
    bar()
    A.reset(base3)
    Qa = [A.alloc([4, 512], BF16) for _ in range(2)]
    PT = [A.alloc([512], BF16) for _ in range(4)]
    PW = [A.alloc([512], BF16) for _ in range(3)]
    r4 = A.alloc([8], F32)
    oTs = [A.alloc([512], F32) for _ in range(2)]
    pti = 0
    pwi = 0
    for Q in range(8):
        qb = Q % 2
        qs = slice(Q * 512, (Q + 1) * 512)
        S.dma("sp", Qa[qb][0:64], T["qr_d"][:, :, qs], writes=["Qa%d" % qb])
        for h in range(4):
            S.pool(lambda e, qb=qb, h=h, qs=qs: e.tensor_copy(Qa[qb][64:128, h, :], nmT[64:128, qs]), reads=["nmT"], writes=["Qm%d_%d" % (qb, h)])
        for h in range(4):
            qr_ = ["Qa%d" % qb, "Qm%d_%d" % (qb, h)]
            posel = pbig[:, 6, 0:260].rearrange("p (s c) -> p s c", s=4)
            powin = pbig[:, 7, 0:260].rearrange("p (s c) -> p s c", s=4)
            for kt in range(4 * Q + 4):
                ks_ = slice(kt * 128, (kt + 1) * 128)
                sb_ = kt % 3
                ps_ = pbig[:, sb_, :]
                pres = "pss%d" % sb_
                o = kt - 4 * Q
                if o < 0:
                    S.pe(lambda e, ks_=ks_, qb=qb, h=h, ps_=ps_: e.matmul(ps_, KsA[:, ks_], Qa[qb][:, h, :], start=True, stop=True),
                         reads=["KsA", "KsE"] + qr_, writes=[pres])
                    lo = 0
                else:
                    lo = o * 128
                    S.pe(lambda e, ks_=ks_, qb=qb, h=h, ps_=ps_, lo=lo: e.matmul(ps_[:, lo:lo + 128], KsA[:, ks_], Qa[qb][:, h, lo:lo + 128], start=True, stop=False),
                         reads=["KsA", "KsE"] + qr_, writes=[pres])
                    S.pe(lambda e, ps_=ps_, lo=lo: e.matmul(ps_[:, lo:lo + 128], identb, tricb, start=False, stop=True), reads=["ident", "tric"], writes=[pres])
                    if o < 3:
                        S.pe(lambda e, ks_=ks_, qb=qb, h=h, ps_=ps_, lo=lo: e.matmul(ps_[:, lo + 128:512], KsA[:, ks_], Qa[qb][:, h, lo + 128:512], start=True, stop=True),
                             reads=["KsA", "KsE"] + qr_, writes=[pres])
                pt_ = PT[pti % 4]
                ptres = "PT%d" % (pti % 4)
                pti += 1
                S.act(lambda e, ps_=ps_, pt_=pt_, lo=lo: e.activation(pt_[:, lo:512], ps_[:, lo:512], AF.Exp, scale=SCALE), reads=[pres], writes=[ptres])
                S.pe(lambda e, pt_=pt_, kt=kt, Q=Q, lo=lo: e.matmul(pbig[0:65, 6, lo:512], VsA[:, kt, :], pt_[:, lo:512], start=(kt == 0), stop=(kt == 4 * Q + 3)),
                     reads=[ptres, "VsA"], writes=["posel"], c=0.22)
            ob_ = (Q * 4 + h) % 2
            S.act(lambda e, ob_=ob_: e.copy(oTs[ob_][0:65, :], pbig[0:65, 6, :]), reads=["posel"], writes=["oTs%d" % ob_], c=0.6)
            poselT = pbig[:, 5, 0:260].rearrange("p (s c) -> p s c", s=4)
            for sub in range(4):
                S.pe(lambda e, ob_=ob_, sub=sub, poselT=poselT: e.transpose(poselT[:, sub, :], oTs[ob_][0:65, sub * 128:(sub + 1) * 128], identf[0:65, 0:65]),
                     reads=["oTs%d" % ob_, "identf"], writes=["poselT"])
            S.dve(lambda e: e.memset(pbig[:, 7, 0:260], 0.0), writes=["powin"])
            for r in range(-4, 4):
                kt = 4 * Q + r
                if kt < 0:
                    continue
                ks_ = slice(kt * 128, (kt + 1) * 128)
                s_lo, s_hi = max(r, 0), min(r + 4, 3)
                wsl = pwi % 2
                psw = pbig[:, 3 + wsl, :]
                pwres = "psw%d" % wsl
                pw_ = PW[pwi % 3]
                pwr = "PW%d" % (pwi % 3)
                pwi += 1
                plain = [s_ for s_ in range(s_lo, s_hi + 1) if s_ != r and s_ != r + 4]
                for s_, msk in ((r, tricb), (r + 4, triab)):
                    if s_lo <= s_ <= s_hi:
                        cs = slice(s_ * 128, (s_ + 1) * 128)
                        S.pe(lambda e, ks_=ks_, qb=qb, h=h, cs=cs, psw=psw: e.matmul(psw[:, cs], KwT[0:64, ks_], Qa[qb][0:64, h, cs], start=True, stop=False),
                             reads=["KwT", "Qa%d" % qb], writes=[pwres])
                        S.pe(lambda e, cs=cs, psw=psw, msk=msk: e.matmul(psw[:, cs], identb, msk, start=False, stop=True), reads=["ident", "tric", "tria"], writes=[pwres])
                if plain:
                    cs = slice(plain[0] * 128, (plain[-1] + 1) * 128)
                    S.pe(lambda e, ks_=ks_, qb=qb, h=h, cs=cs, psw=psw: e.matmul(psw[:, cs], KwT[0:64, ks_], Qa[qb][0:64, h, cs], start=True, stop=True),
                         reads=["KwT", "Qa%d" % qb], writes=[pwres], c=0.2)
                ca = slice(s_lo * 128, (s_hi + 1) * 128)
                S.act(lambda e, psw=psw, pw_=pw_, ca=ca: e.activation(pw_[:, ca], psw[:, ca], AF.Exp, scale=SCALE), reads=[pwres], writes=[pwr], c=0.5)
                for s_ in range(s_lo, s_hi + 1):
                    S.pe(lambda e, pw_=pw_, s_=s_, kt=kt, powin=powin: e.matmul(powin[:, s_, :], pw_[:, s_ * 128:(s_ + 1) * 128], VwA[:, kt, :],
                                                                                start=False, stop=False, skip_group_check=True),
                         reads=[pwr, "VwA"], writes=["powin"])
            for sub in range(4):
                tt = 4 * Q + sub
                for br, (po_, pres) in enumerate(((poselT, "poselT"), (powin, "powin"))):
                    S.dve(lambda e, po_=po_, sub=sub, br=br: e.reciprocal(r4[:, sub * 2 + br:sub * 2 + br + 1], po_[:, sub, 64:65]), reads=[pres], writes=["r4_%d_%d" % (sub, br)])
                    S.dve(lambda e, sub=sub, tt=tt, h=h, br=br: e.tensor_tensor(r4[:, sub * 2 + br:sub * 2 + br + 1], r4[:, sub * 2 + br:sub * 2 + br + 1], gate[:, tt, h * 3 + 1 + br:h * 3 + 2 + br], ALU.mult),
                          reads=["r4_%d_%d" % (sub, br), "gate"], writes=["r4_%d_%d" % (sub, br)])
                    S.dve(lambda e, po_=po_, sub=sub, tt=tt, h=h, br=br: e.scalar_tensor_tensor(attO[:, tt, h * 64:(h + 1) * 64], po_[:, sub, 0:64], r4[:, sub * 2 + br:sub * 2 + br + 1],
                                                                                                  attO[:, tt, h * 64:(h + 1) * 64], ALU.mult, ALU.add),
                          reads=[pres, "r4_%d_%d" % (sub, br), "attO%d" % tt], writes=["attO%d" % tt])
        for sub in range(4):
            tt = 4 * Q + sub
            S.dma("sp", T["mixo"][tt * 128:(tt + 1) * 128, 256:512], attO[:, tt, :], reads=["attO%d" % tt])


def _perm_cols():
    return None


def run_mix(inputs):
    if "mix" not in _CACHE:
        _CACHE["mix"] = build_mix()
    nc = _CACHE["mix"]
    x = inputs["x"]
    w_in = inputs["w_in"][0]
    offs = np.cumsum([0, 1024, 1536, 16, 1024, 256, 256, 256, 256, 256, 256, 48])
    oz, oxbc, odt, oq, okc, ovc, oks, ovs, okw, ovw, ogate = offs[:11]
    conv_w = inputs["conv_w"][0]
    conv_b = inputs["conv_b"][0]
    t = np.arange(SEQ, dtype=np.float32)
    inv = (1.0 / (500000.0 ** (np.arange(0, 16, 2, dtype=np.float32) / np.float32(16)))).astype(np.float32)
    ang = (t[:, None] * inv[None, :]).astype(np.float32)
    rope = np.concatenate([np.cos(ang), np.sin(ang)], 1).astype(np.float32)
    rope = np.ascontiguousarray(rope.reshape(NTT, 128, 16).transpose(1, 0, 2).reshape(128, NTT * 16))
    ident = np.eye(128, dtype=np.float32)
    kk = np.arange(128)[:, None]
    qq = np.arange(128)[None, :]
    tric = np.where(kk <= qq, 0.0, NEG).astype(np.float32)
    tria = np.where(kk > qq, 0.0, NEG).astype(np.float32)
    utri = (kk <= qq).astype(np.float32)
    emat = (np.arange(SEQ)[None, :] // 64 == np.arange(64)[:, None]).astype(np.float32)
    ii = np.arange(256)[:, None]
    cmask = np.where((16 * ii + 31 <= np.arange(SEQ)[None, :]) & (ii < 255), 0.0, NEG).astype(np.float32)
    cs = np.arange(255)[:, None] * 16
    ss_ = np.arange(64)[None, :] * 64
    ov = np.clip(np.minimum(cs + 32, ss_ + 64) - np.maximum(cs, ss_), 0, None) / 32.0
    ovl = np.zeros((256, 64), np.float32)
    ovl[:255] = ov
    in_maps = []
    for c in range(8):
        b, g = c // 4, c % 4
        grp = g // 2
        ar = np.arange
        tm_cols = np.concatenate([oz + 256 * g + ar(256), odt + 4 * g + ar(4), ogate + 12 * g + ar(12), ovs + 64 * g + ar(64), ovw + 64 * g + ar(64),
                                  oq + 256 * g + ar(256), oks + 64 * g + ar(64), okw + 64 * g + ar(64)])
        xcols = np.concatenate([256 * g + ar(256), 1024 + 128 * grp + ar(128), 1280 + 128 * grp + ar(128)])
        fm_cols = np.concatenate([oxbc + xcols, okc + 64 * g + ar(64), ovc + 64 * g + ar(64)])
        convw = np.ascontiguousarray(conv_w[:, xcols].T.reshape(4, 128, 4).transpose(1, 0, 2).reshape(128, 16))
        convb = np.ascontiguousarray(conv_b[xcols].reshape(4, 128).T)
        hp = np.concatenate([inputs["dt_bias"][0][4 * g:4 * g + 4], inputs["a_log"][0][4 * g:4 * g + 4], inputs["d_skip"][0][4 * g:4 * g + 4]]).astype(np.float32)
        in_maps.append(dict(
            xb=np.ascontiguousarray(x[b]), anw=inputs["attn_norm_w"][0], w_tm=np.ascontiguousarray(w_in[:, tm_cols]), w_fm=np.ascontiguousarray(w_in[:, fm_cols]),
            convw=convw, convb=convb, hp=hp, rope=rope,
            w1k=np.ascontiguousarray(inputs["cmp_w1_k"][0].reshape(32, 64, 256).transpose(1, 0, 2).reshape(64, 32 * 256)),
            w1v=np.ascontiguousarray(inputs["cmp_w1_v"][0].reshape(32, 64, 256).transpose(1, 0, 2).reshape(64, 32 * 256)),
            w2k=inputs["cmp_w2_k"][0], w2v=inputs["cmp_w2_v"][0],
            pek=np.ascontiguousarray(inputs["cmp_pe_k"][0].T), pev=np.ascontiguousarray(inputs["cmp_pe_v"][0].T),
            ident=ident, emat=emat, tric=tric, tria=tria, cmask=cmask, ovl=ovl, utri=utri,
        ))
    res = run_bass_kernel_spmd(nc, in_maps, core_ids=list(range(8)))
    mixed = np.empty((2, SEQ, D), np.float32)
    for c in range(8):
        b, g = c // 4, c % 4
        m = res.results[c]["mixo"]
        mixed[b, :, 256 * g:256 * (g + 1)] = m[:, 0:256]
        mixed[b, :, 1024 + 256 * g:1024 + 256 * (g + 1)] = m[:, 256:512]
    return mixed


def kernel(**inputs):
    inputs = {k: np.asarray(v) for k, v in inputs.items()}
    mixed = run_mix(inputs)
    return run_tail(inputs["x"], mixed, inputs["ssd_norm_w"][0], inputs["w_out"][0], inputs["ffn_norm_w"][0],
                    inputs["w_gate"][0], inputs["w_up"][0], inputs["w_down"][0], inputs["final_norm_w"])
```

```python
import contextlib
import os
import numpy as np
import ml_dtypes
import concourse.bass as bass
import concourse.mybir as mybir
from concourse.bass_utils import run_bass_kernel_spmd

F32 = mybir.dt.float32
BF16 = mybir.dt.bfloat16
U8 = mybir.dt.uint8
ALU = mybir.AluOpType
AF = mybir.ActivationFunctionType
AX = mybir.AxisListType

ENGS = ("pe", "act", "dve", "pool", "sp")
EPS = 1e-6


class _Op:
    __slots__ = ("eng", "fn", "dma", "deps", "idx", "signal", "val", "sem", "cc", "cost")


class Sched:
    def __init__(self, nc, n_dma_sems=8):
        self.nc = nc
        self.ops = []
        self.last_w = {}
        self.readers = {}
        self.n_dma_sems = n_dma_sems
        self.bar = None
        self.bank_of = {}

    def op(self, eng, fn, reads=(), writes=(), dma=False, c=None):
        o = _Op()
        o.cost = c
        o.eng, o.fn, o.dma = eng, fn, dma
        o.cc = False
        o.idx = len(self.ops)
        o.signal = False
        deps = {}
        if self.bar is not None:
            deps[self.bar] = "raw"
        for r in reads:
            w = self.last_w.get(r)
            if w is not None:
                deps[w] = "raw"
        for w_ in writes:
            w = self.last_w.get(w_)
            if w is not None:
                deps[w] = "raw"
            for r in self.readers.get(w_, ()):
                if r not in deps:
                    deps[r] = "war"
        for r in reads:
            self.readers.setdefault(r, []).append(o.idx)
        for w_ in writes:
            self.last_w[w_] = o.idx
            self.readers[w_] = []
        banks = set()
        for r in tuple(reads) + tuple(writes):
            banks.update(self.bank_of.get(r, ()))
        for b in banks:
            key = ("bank", b)
            w = self.last_w.get(key)
            if w is not None and w not in deps:
                deps[w] = "bank"
            self.last_w[key] = o.idx
        deps.pop(o.idx, None)
        o.deps = deps
        self.ops.append(o)
        return o

    def pe(self, fn, reads=(), writes=(), c=None):
        return self.op("pe", fn, reads, writes, c=c)

    def act(self, fn, reads=(), writes=(), c=None):
        return self.op("act", fn, reads, writes, c=c)

    def dve(self, fn, reads=(), writes=(), c=None):
        return self.op("dve", fn, reads, writes, c=c)

    def pool(self, fn, reads=(), writes=(), c=None):
        return self.op("pool", fn, reads, writes, c=c)

    DEF_COST = {"pe": 0.12, "act": 0.35, "dve": 0.25, "pool": 0.35}

    def reorder(self, window=int(os.environ.get("RWIN", "120"))):
        ops = self.ops
        n = len(ops)
        queues = {e: [] for e in ENGS}
        for o in ops:
            queues[o.eng].append(o.idx)
        head = {e: 0 for e in ENGS}
        sched = [False] * n
        fin = [0.0] * n
        etime = {e: 0.0 for e in ENGS}
        order = []
        left = n
        while left:
            best = None
            for e in ENGS:
                q = queues[e]
                h = head[e]
                while h < len(q) and sched[q[h]]:
                    h += 1
                head[e] = h
                if h >= len(q):
                    continue
                seen = 0
                i = h
                et = etime[e]
                while i < len(q) and seen < window:
                    k = q[i]
                    i += 1
                    if sched[k]:
                        continue
                    seen += 1
                    o = ops[k]
                    rdy = 0.0
                    ok = True
                    for d in o.deps:
                        if not sched[d]:
                            ok = False
                            break
                        f = fin[d] + (0.0 if ops[d].eng == e else 0.15)
                        if f > rdy:
                            rdy = f
                    if not ok:
                        continue
                    st = rdy if rdy > et else et
                    key = (st, k)
                    if best is None or key < best[0]:
                        best = (key, e, k)
                    if st <= et:
                        break
            assert best is not None, "scheduler stuck"
            (st, k), e, _ = best
            o = ops[k]
            if o.dma:
                etime[e] = st + 0.06
                fin[k] = st + (o.cost if o.cost is not None else 3.0)
            else:
                c = o.cost if o.cost is not None else self.DEF_COST[e]
                etime[e] = st + c
                fin[k] = st + c
            sched[k] = True
            order.append(k)
            left -= 1
        self.order = order
        self.est_time = max(fin) if fin else 0.0

    def dma(self, q, out, in_, reads=(), writes=(), c=None):
        return self.op(q, lambda e: e.dma_start(out=out, in_=in_), reads, writes, dma=True, c=c)

    def cc(self, fn, reads=(), writes=()):
        o = self.op("pool", fn, reads, writes, dma=True)
        o.cc = True
        return o

    def barrier(self, out, in_):
        allres = set(self.last_w.keys()) | set(self.readers.keys())
        o = self.op("sp", lambda e: e.dma_start(out=out, in_=in_), reads=(), writes=tuple(allres), dma=True)
        self.bar = o.idx
        self.last_w = {}
        self.readers = {}
        return o

    def emit(self, sems, block, final_wait_eng="sp"):
        ops = self.ops
        need = [False] * len(ops)
        for o in ops:
            for d, kind in o.deps.items():
                do = ops[d]
                if do.dma:
                    continue
                if do.eng == o.eng and not o.dma:
                    if do.eng == "pe" or kind == "bank":
                        continue
                need[d] = True
        cnt = {e: 0 for e in ENGS}
        dcnt = {}
        dval = {}
        per_eng = {e: [] for e in ENGS}
        order = getattr(self, "order", None) or list(range(len(ops)))
        for k_ in order:
            o = ops[k_]
            per_eng[o.eng].append(o)
            if o.dma and o.cc:
                o.sem = "cc"
                dval["cc"] = dval.get("cc", 0) + 1
                o.val = dval["cc"]
            elif o.dma:
                k = dcnt.get(o.eng, 0)
                dcnt[o.eng] = k + 1
                key = ("dma", o.eng, k % self.n_dma_sems)
                o.sem = key
                dval[key] = dval.get(key, 0) + 16
                o.val = dval[key]
            elif need[o.idx]:
                cnt[o.eng] += 1
                o.val = cnt[o.eng]
                o.sem = o.eng
                o.signal = True
        self.stats = {e: len(per_eng[e]) for e in ENGS}
        self.stats["signals"] = dict(cnt)

        def run(engname, e):
            waited = {}
            for o in per_eng[engname]:
                wl = {}
                for d, kind in o.deps.items():
                    do = ops[d]
                    if do.dma:
                        wl[do.sem] = max(wl.get(do.sem, 0), do.val)
                        continue
                    if do.eng == o.eng and not o.dma:
                        if do.eng == "pe" or kind == "bank":
                            continue
                    wl[do.sem] = max(wl.get(do.sem, 0), do.val)
                if o.dma and o.cc and o.val > 1:
                    wl[o.sem] = max(wl.get(o.sem, 0), o.val - 1)
                elif o.dma and not o.cc and o.val > 16:
                    wl[o.sem] = max(wl.get(o.sem, 0), o.val - 16)
                for s, v in wl.items():
                    if waited.get(s, 0) >= v:
                        continue
                    waited[s] = v
                    e.wait_ge(sems[s], v)
                ins = o.fn(e)
                if o.dma and o.cc:
                    ins.then_inc(sems[o.sem])
                elif o.dma:
                    ins.then_inc(sems[o.sem], 16)
                elif o.signal:
                    ins.then_inc(sems[o.sem], 1)
            if engname == final_wait_eng:
                for key, v in dval.items():
                    if waited.get(key, 0) < v:
                        e.wait_ge(sems[key], v)
                for en in ("pe", "act", "dve", "pool"):
                    if cnt[en] > 0 and waited.get(en, 0) < cnt[en]:
                        e.wait_ge(sems[en], cnt[en])

        @block.tensor
        def _(e):
            run("pe", e)

        @block.scalar
        def _(e):
            run("act", e)

        @block.vector
        def _(e):
            run("dve", e)

        @block.gpsimd
        def _(e):
            run("pool", e)

        @block.sync
        def _(e):
            run("sp", e)


def make_sems(nc, stack, n_dma_sems=8, queues=("sp", "pool", "act")):
    sems = {}
    for e in ("pe", "act", "dve", "pool", "cc"):
        sems[e] = stack.enter_context(nc.semaphore("s_" + e))
    for q in queues:
        for i in range(n_dma_sems):
            sems[("dma", q, i)] = stack.enter_context(nc.semaphore("d_%s_%d" % (q, i)))
    return sems


_DTSZ = {F32: 4, BF16: 2, U8: 1}


class Arena:
    def __init__(self, ar, size):
        self.ar, self.size, self.off = ar, size, 0

    def reset(self, off=0):
        self.off = off

    def alloc(self, shape, dtype, parts=128):
        n = int(np.prod(shape)) * _DTSZ[dtype]
        off = (self.off + 63) // 64 * 64
        assert off + n <= self.size, ("arena overflow", off, n, self.size)
        self.off = off + n
        ap = self.ar[0:parts, off:off + n].bitcast(dtype)
        if len(shape) > 1:
            names = [chr(ord("a") + i) for i in range(len(shape))]
            pat = "p (%s) -> p %s" % (" ".join(names), " ".join(names))
            ap = ap.rearrange(pat, **{nm: int(s) for nm, s in zip(names, shape)})
        return ap


D = 2048
FF = 5632
TOK = 1024
NT = TOK // 128
KC = D // 128
FC = FF // 128
SSDW = 1024


def build_tail():
    nc = bass.Bass("TRN2", target_bir_lowering=False)
    x = nc.dram_tensor("x_own", [TOK, D], F32, kind="ExternalInput").ap()
    mix = nc.dram_tensor("mix", [TOK, D], F32, kind="ExternalInput").ap()
    w_out = nc.dram_tensor("w_out", [D, D], F32, kind="ExternalInput").ap()
    w_gate = nc.dram_tensor("w_gate", [D, FF], F32, kind="ExternalInput").ap()
    w_up = nc.dram_tensor("w_up", [D, FF], F32, kind="ExternalInput").ap()
    w_down = nc.dram_tensor("w_down", [FF, D], F32, kind="ExternalInput").ap()
    nw = nc.dram_tensor("nw", [3, D], F32, kind="ExternalInput").ap()
    ident = nc.dram_tensor("ident", [128, 128], F32, kind="ExternalInput").ap()
    out = nc.dram_tensor("out", [TOK, D], F32, kind="ExternalOutput").ap()
    h_d = nc.dram_tensor("h_d", [TOK, D], F32, kind="Internal").ap()
    dummy = nc.dram_tensor("dummy_bar", [2, 64], F32, kind="Internal").ap()
    with contextlib.ExitStack() as st:
        ASZ = 207 * 1024
        ar = st.enter_context(nc.sbuf_tensor("arena", [128, ASZ], U8))
        A = Arena(ar, ASZ)
        pbig = st.enter_context(nc.psum_tensor("pbig", [128, 8, 512], F32))
        sems = make_sems(nc, st)
        block = st.enter_context(nc.Block())
        S = Sched(nc)
        tail_body(nc, S, A, pbig, x, mix, w_out, w_gate, w_up, w_down, nw, ident, out, h_d, dummy)
        if os.environ.get('NO_REORDER') is None:
            S.reorder()
        S.emit(sems, block)
    return nc


def rms_rstd(S, src, n, ss, sq, tag, rd, wr_extra=(), c=None):
    S.act(lambda e: e.activation(sq, src, AF.Square, accum_out=ss), reads=rd, writes=[tag + "ss", tag + "sq"], c=c)
    S.act(lambda e: e.activation(ss, ss, AF.Sqrt, scale=1.0 / n, bias=EPS_AP[0]), reads=[tag + "ss"], writes=[tag + "ss"])
    S.dve(lambda e: e.reciprocal(ss, ss), reads=[tag + "ss"], writes=[tag + "ss"])


EPS_AP = [None]


def tail_body(nc, S, A, pbig, x, mix, w_out, w_gate, w_up, w_down, nw, ident, out, h_d, dummy):
    bk = {"ptr": (4, 5), "ptrv": (6, 7)}
    for i_ in range(8):
        bk["pacc%d" % i_] = (i_,)
        bk["pd%d" % i_] = (i_,)
    for i_ in range(2):
        bk["pg%d" % i_] = (i_ * 4, i_ * 4 + 1)
        bk["pu%d" % i_] = (i_ * 4 + 2, i_ * 4 + 3)
    S.bank_of = bk
    identb = A.alloc([128], BF16)
    nwb_flat = A.alloc([2 * D + SSDW], F32)
    epsb = A.alloc([1], F32)
    ss = A.alloc([4], F32)
    EPS_AP[0] = epsb
    S.pool(lambda e: e.memset(epsb, EPS), writes=["eps"])
    S.dma("pool", identb, ident, writes=["ident"])
    S.dma("sp", nwb_flat, nw.rearrange("a b -> (a b)")[0:2 * D + SSDW].partition_broadcast(128), writes=["nwb"])

    class _NW:
        def __getitem__(self, key):
            _, row, cols = key
            base = {1: 0, 2: D, 0: 2 * D}[row]
            lo = cols.start or 0
            hi = cols.stop if cols.stop is not None else (SSDW if row == 0 else D)
            return nwb_flat[:, base + lo:base + hi]
    nwb = _NW()
    vT = A.alloc([KC, TOK], BF16)
    base_persist = A.off

    wo = A.alloc([KC, D], BF16)
    for cb in range(4):
        S.dma("pool", wo[:, :, cb * 512:(cb + 1) * 512],
              w_out[:, cb * 512:(cb + 1) * 512].rearrange("(k p) n -> p k n", p=128), writes=["wo%d" % cb], c=25.0)
    xt = [A.alloc([D], F32) for _ in range(2)]
    mt = [A.alloc([D], F32) for _ in range(2)]
    sq = A.alloc([D], BF16)
    mb2 = [A.alloc([D], BF16) for _ in range(2)]
    mT2 = [A.alloc([KC, 128], BF16) for _ in range(2)]
    hs = [A.alloc([D], F32) for _ in range(2)]
    vb = A.alloc([D], BF16)
    WB = 256
    pre_lo = (A.off + 63) // 64 * 64
    wg_pre = A.alloc([KC, WB], BF16)
    wu_pre = A.alloc([KC, WB], BF16)
    S.dma("pool", wg_pre, w_gate[:, 0:WB].rearrange("(k p) n -> p k n", p=128), writes=["wg0"], c=15.0)
    S.dma("pool", wu_pre, w_up[:, 0:WB].rearrange("(k p) n -> p k n", p=128), writes=["wu0"], c=15.0)
    pacc = pbig[:, 0:4, :]
    ptr_all = pbig[:, 4:6, :].rearrange("p a b -> p (a b)").bitcast(BF16)
    ptr = ptr_all.rearrange("p (k n) -> p k n", k=KC)

    def loads(tt):
        b = tt % 2
        S.dma("sp", xt[b], x[tt * 128:(tt + 1) * 128, :], writes=["xt%d" % b])
        S.dma("sp", mt[b], mix[tt * 128:(tt + 1) * 128, :], writes=["mt%d" % b])

    loads(0)
    ptr2_all = pbig[:, 6:8, :].rearrange("p a b -> p (a b)").bitcast(BF16)
    ptr2 = ptr2_all.rearrange("p (k n) -> p k n", k=KC)
    for tt in range(NT):
        b = tt % 2
        mb = mb2[b]
        mT = mT2[b]
        if tt + 1 < NT:
            loads(tt + 1)
        rms_rstd(S, mt[b][:, 0:SSDW], SSDW, ss[:, 0:1], sq[:, 0:SSDW], "a", ["mt%d" % b, "eps"])
        S.dve(lambda e, b=b, mb=mb: e.scalar_tensor_tensor(mb[:, 0:SSDW], mt[b][:, 0:SSDW], ss[:, 0:1], nwb[:, 0, 0:SSDW], ALU.mult, ALU.mult),
              reads=["mt%d" % b, "ass", "nwb"], writes=["mb0_%d" % b])
        S.act(lambda e, b=b, mb=mb: e.copy(mb[:, SSDW:D], mt[b][:, SSDW:D]), reads=["mt%d" % b], writes=["mb1_%d" % b], c=0.9)
        for kc in range(KC):
            S.pe(lambda e, kc=kc, mb=mb: e.transpose(ptr[:, kc, :], mb[:, kc * 128:(kc + 1) * 128], identb),
                 reads=["mb0_%d" % b, "mb1_%d" % b, "ident"], writes=["ptr"])
        S.act(lambda e, mT=mT: e.copy(mT[:, 0:8, :], ptr[:, 0:8, :]), reads=["ptr"], writes=["mTa%d" % b], c=0.6)
        S.dve(lambda e, mT=mT: e.tensor_copy(mT[:, 8:16, :], ptr[:, 8:16, :]), reads=["ptr"], writes=["mTb%d" % b], c=0.5)
        for cb in range(4):
            for kc in range(KC):
                S.pe(lambda e, cb=cb, kc=kc, mT=mT: e.matmul(pacc[:, cb, :], mT[:, kc, :], wo[:, kc, cb * 512:(cb + 1) * 512],
                                                              start=(kc == 0), stop=(kc == KC - 1)),
                     reads=["mTa%d" % b, "mTb%d" % b, "wo%d" % cb], writes=["pacc%d" % cb], c=0.22)
            S.dve(lambda e, cb=cb, b=b: e.tensor_tensor(hs[b][:, cb * 512:(cb + 1) * 512], pacc[:, cb, :], xt[b][:, cb * 512:(cb + 1) * 512], ALU.add),
                  reads=["pacc%d" % cb, "xt%d" % b], writes=["hs%d_%d" % (b, cb)])
        hres = ["hs%d_%d" % (b, cb) for cb in range(4)]
        S.dma("sp", h_d[tt * 128:(tt + 1) * 128, :], hs[b], reads=hres, writes=["h_d%d" % tt])
        rms_rstd(S, hs[b], D, ss[:, 1:2], sq, "b", hres + ["eps"])
        S.dve(lambda e, b=b: e.scalar_tensor_tensor(vb, hs[b], ss[:, 1:2], nwb[:, 1, :], ALU.mult, ALU.mult),
              reads=hres + ["bss", "nwb"], writes=["vb"])
        for kc in range(KC):
            S.pe(lambda e, kc=kc: e.transpose(ptr2[:, kc, :], vb[:, kc * 128:(kc + 1) * 128], identb),
                 reads=["vb", "ident"], writes=["ptrv"])
        S.act(lambda e, tt=tt: e.copy(vT[:, 0:8, tt * 128:(tt + 1) * 128], ptr2[:, 0:8, :]), reads=["ptrv"], writes=["vT%da" % tt], c=0.6)
        S.dve(lambda e, tt=tt: e.tensor_copy(vT[:, 8:16, tt * 128:(tt + 1) * 128], ptr2[:, 8:16, :]), reads=["ptrv"], writes=["vT%db" % tt], c=0.5)

    S.barrier(dummy[1:2, :], ident[0:1, 0:64])
    A.reset(base_persist)
    hT = A.alloc([FC, TOK], BF16)
    HK = 22
    wd_pre = A.alloc([HK, 512], BF16)
    base_b = A.off
    NB = FF // WB
    wg = [wg_pre, A.alloc([KC, WB], BF16)]
    wu = [wu_pre, A.alloc([KC, WB], BF16)]
    sg = [A.alloc([TOK], BF16) for _ in range(2)]
    assert A.off <= pre_lo, (A.off, pre_lo)
    for blk in range(NB):
        b = blk % 2
        if blk > 0:
            S.dma("pool", wg[b], w_gate[:, blk * WB:(blk + 1) * WB].rearrange("(k p) n -> p k n", p=128), writes=["wg%d" % b], c=15.0)
            S.dma("pool", wu[b], w_up[:, blk * WB:(blk + 1) * WB].rearrange("(k p) n -> p k n", p=128), writes=["wu%d" % b], c=15.0)
        if blk == NB - 2:
            S.dma("pool", wd_pre, w_down[0:HK * 128, 0:512].rearrange("(k p) n -> p k n", p=128), writes=["wd0"], c=15.0)
        for j in range(WB // 128):
            fc = blk * (WB // 128) + j
            pb = fc % 2
            pg = pbig[:, pb * 4:pb * 4 + 2, :]
            pu = pbig[:, pb * 4 + 2:pb * 4 + 4, :]
            for hf in range(2):
                for kc in range(KC):
                    S.pe(lambda e, b=b, j=j, hf=hf, kc=kc, pg=pg: e.matmul(pg[:, hf, :], wg[b][:, kc, j * 128:(j + 1) * 128], vT[:, kc, hf * 512:(hf + 1) * 512],
                                                                         start=(kc == 0), stop=(kc == KC - 1)),
                         reads=["wg%d" % b, "vT"], writes=["pg%d" % pb], c=0.22)
            for hf in range(2):
                for kc in range(KC):
                    S.pe(lambda e, b=b, j=j, hf=hf, kc=kc, pu=pu: e.matmul(pu[:, hf, :], wu[b][:, kc, j * 128:(j + 1) * 128], vT[:, kc, hf * 512:(hf + 1) * 512],
                                                                         start=(kc == 0), stop=(kc == KC - 1)),
                         reads=["wu%d" % b, "vT"], writes=["pu%d" % pb], c=0.22)
            S.act(lambda e, pb=pb, pg=pg: e.activation(sg[pb], pg.rearrange("p a b -> p (a b)"), AF.Silu), reads=["pg%d" % pb], writes=["sg%d" % pb])
            S.dve(lambda e, pb=pb, pu=pu, fc=fc: e.tensor_tensor(hT[:, fc, :], sg[pb], pu.rearrange("p a b -> p (a b)"), ALU.mult),
                  reads=["sg%d" % pb, "pu%d" % pb], writes=["hT%d" % fc])

    S.barrier(dummy[1:2, :], ident[0:1, 0:64])
    A.reset(base_b)
    wd = [wd_pre, A.alloc([HK, 512], BF16)]
    hl = [A.alloc([512], F32) for _ in range(2)]
    ys = [A.alloc([512], F32) for _ in range(2)]
    it = 0
    for r in range(4):
        for hf in range(2):
            b = (r * 2 + hf) % 2
            if r * 2 + hf > 0:
                S.dma("pool", wd[b], w_down[hf * HK * 128:(hf + 1) * HK * 128, r * 512:(r + 1) * 512].rearrange("(k p) n -> p k n", p=128), writes=["wd%d" % b], c=15.0)
            for tt in range(NT):
                for k in range(HK):
                    kk = hf * HK + k
                    S.pe(lambda e, b=b, tt=tt, k=k, kk=kk: e.matmul(pbig[:, tt, :], hT[:, kk, tt * 128:(tt + 1) * 128], wd[b][:, k, :],
                                                                    start=(kk == 0), stop=(kk == FC - 1)),
                         reads=["wd%d" % b, "hT"], writes=["pd%d" % tt], c=0.22)
        for tt in range(NT):
            b = it % 2
            it += 1
            S.dma("sp", hl[b], h_d[tt * 128:(tt + 1) * 128, r * 512:(r + 1) * 512], reads=["h_d%d_%d" % (tt, r)], writes=["hl%d" % b])
            S.dve(lambda e, b=b, tt=tt: e.tensor_tensor(ys[b], pbig[:, tt, :], hl[b], ALU.add), reads=["pd%d" % tt, "hl%d" % b], writes=["ys%d" % b])
            S.dma("sp", h_d[tt * 128:(tt + 1) * 128, r * 512:(r + 1) * 512], ys[b], reads=["ys%d" % b], writes=["h_d%d_%d" % (tt, r)])

    S.barrier(dummy[1:2, :], ident[0:1, 0:64])
    A.reset(base_persist)
    yt = [A.alloc([D], F32) for _ in range(2)]
    ot = [A.alloc([D], F32) for _ in range(2)]
    sq2 = A.alloc([D], F32)
    for tt in range(NT):
        b = tt % 2
        S.dma("sp", yt[b], h_d[tt * 128:(tt + 1) * 128, :], writes=["yt%d" % b])
        rms_rstd(S, yt[b], D, ss[:, 2:3], sq2, "c", ["yt%d" % b, "eps"])
        S.dve(lambda e, b=b: e.scalar_tensor_tensor(ot[b], yt[b], ss[:, 2:3], nwb[:, 2, :], ALU.mult, ALU.mult),
              reads=["yt%d" % b, "css", "nwb"], writes=["ot%d" % b])
        S.dma("sp", out[tt * 128:(tt + 1) * 128, :], ot[b], reads=["ot%d" % b])


_CACHE = {}


def run_tail(x, mixed, ssd_norm_w, w_out, ffn_norm_w, w_gate, w_up, w_down, final_norm_w):
    if "tail" not in _CACHE:
        _CACHE["tail"] = build_tail()
    nc = _CACHE["tail"]
    nwv = np.ones((3, D), np.float32)
    nwv[0] = ffn_norm_w
    nwv[1] = final_norm_w
    nwv[2, :SSDW] = ssd_norm_w
    ident = np.eye(128, dtype=np.float32)
    in_maps = []
    for c in range(8):
        b, g = c // 4, c % 4
        in_maps.append({
            "x_own": np.ascontiguousarray(x[b, g * TOK:(g + 1) * TOK]),
            "mix": np.ascontiguousarray(mixed[b, g * TOK:(g + 1) * TOK]),
            "w_out": w_out, "w_gate": w_gate, "w_up": w_up, "w_down": w_down,
            "nw": nwv, "ident": ident,
        })
    res = run_bass_kernel_spmd(nc, in_maps, core_ids=list(range(8)))
    outp = np.empty((2, 4096, D), np.float32)
    for c in range(8):
        b, g = c // 4, c % 4
        outp[b, g * TOK:(g + 1) * TOK] = res.results[c]["out"]
    return outp


SEQ = 4096
NTT = SEQ // 128
NEG = -30000.0
NA = 400
NB_ = 384
NFM = 640
SCALE = 0.125


def build_mix():
    nc = bass.Bass("TRN2", target_bir_lowering=False)
    di = lambda n, s, d=F32: nc.dram_tensor(n, s, d, kind="ExternalInput").ap()
    T = dict(
        xb=di("xb", [SEQ, D]), anw=di("anw", [D]), w_tm=di("w_tm", [D, NA + NB_]), w_fm=di("w_fm", [D, NFM]),
        convw=di("convw", [128, 16]), convb=di("convb", [128, 4]), hp=di("hp", [12]), rope=di("rope", [128, NTT * 16]),
        w1k=di("w1k", [64, 32 * 256]), w1v=di("w1v", [64, 32 * 256]), w2k=di("w2k", [256, 64]), w2v=di("w2v", [256, 64]),
        pek=di("pek", [64, 32]), pev=di("pev", [64, 32]), ident=di("ident", [128, 128]), emat=di("emat", [64, SEQ]),
        tric=di("tric", [128, 128]), tria=di("tria", [128, 128]), cmask=di("cmask", [256, SEQ]), ovl=di("ovl", [256, 64]),
        utri=di("utri", [128, 128]),
    )
    T["mixo"] = nc.dram_tensor("mixo", [SEQ, 512], F32, kind="ExternalOutput").ap()
    T["qr_d"] = nc.dram_tensor("qr_d", [64, 4, SEQ], BF16, kind="Internal").ap()
    T["qu_d"] = nc.dram_tensor("qu_d", [64, 4, SEQ], BF16, kind="Internal").ap()
    T["dummy"] = nc.dram_tensor("dummy_bar", [2, 64], F32, kind="Internal").ap()
    with contextlib.ExitStack() as st:
        ASZ = 207 * 1024
        ar = st.enter_context(nc.sbuf_tensor("arena", [128, ASZ], U8))
        A = Arena(ar, ASZ)
        pbig = st.enter_context(nc.psum_tensor("pbig", [128, 8, 512], F32))
        sems = make_sems(nc, st)
        block = st.enter_context(nc.Block())
        S = Sched(nc)
        mix_body(nc, S, A, pbig, T)
        if os.environ.get('NO_REORDER') is None:
            S.reorder()
        S.emit(sems, block)
    return nc


def pbf(pb, lo, hi):
    return pb[:, lo:hi, :].rearrange("p a b -> p (a b)").bitcast(BF16)


def mix_body(nc, S, A, pbig, T):
    bar = lambda: S.barrier(T["dummy"][1:2, :], T["ident"][0:1, 0:64])
    bk = {"ptr": (0,), "ptr2": (6,), "psA": (1,), "psB": (2,), "psF0": (3,), "psR": (4,), "psS_a": (5,), "psS_c": (5,),
          "psN": (6,), "psY_o": (7,), "psY_d": (7,), "pk": (5,), "pv0": (6,), "pv1": (7,), "pnt": (7,), "posel": (6,), "powin": (7,)}
    for i_ in range(4):
        bk["pbias%d" % i_] = (4,)
        bk["phid%d" % i_] = (i_,)
        bk["psw%d" % i_] = (3 + i_ % 2,)
    for a_ in range(2):
        bk["po%d" % a_] = (4 + a_,)
        for b_ in range(2):
            bk["psc%d_%d" % (a_, b_)] = (a_ * 2 + b_,)
    for i_ in range(3):
        bk["pss%d" % i_] = (i_,)
    bk["poselT"] = (5,)
    S.bank_of = bk
    identb = A.alloc([128], BF16)
    utri = A.alloc([128], F32)
    identf = A.alloc([128], F32)
    tricb = A.alloc([128], BF16)
    triab = A.alloc([128], BF16)
    epsb = A.alloc([1], F32)
    oneb = A.alloc([1], F32)
    EPS_AP[0] = epsb
    KsA = A.alloc([SEQ], BF16)
    KwT = A.alloc([SEQ], BF16)
    kcvcT = A.alloc([SEQ], BF16)
    VsA = A.alloc([NTT, 65], BF16)
    VwA = A.alloc([NTT, 65], BF16)
    gate = A.alloc([NTT, 12], F32)
    hpb = A.alloc([12], F32)
    base_persist = A.off
    S.pool(lambda e: e.memset(epsb, EPS), writes=["eps"])
    S.pool(lambda e: e.memset(oneb, 1.0), writes=["one"])
    S.pool(lambda e: e.memset(VsA, 1.0), writes=["VsA"])
    S.pool(lambda e: e.memset(VwA, 1.0), writes=["VwA"])
    S.dma("pool", identb, T["ident"], writes=["ident"])
    S.dma("sp", utri, T["utri"], writes=["utri"])
    S.dma("sp", identf, T["ident"], writes=["identf"])
    S.dma("pool", tricb, T["tric"], writes=["tric"])
    S.dma("pool", triab, T["tria"], writes=["tria"])
    S.dma("pool", KsA[64:128, :], T["emat"], writes=["KsE"])
    S.dma("sp", hpb, T["hp"].partition_broadcast(128), writes=["hpb"])

    wtm = A.alloc([KC, NA + NB_], BF16)
    wfm = A.alloc([KC, NFM], BF16)
    S.dma("pool", wfm, T["w_fm"].rearrange("(k p) n -> p k n", p=128), writes=["wfm"], c=30.0)
    S.dma("pool", wtm, T["w_tm"].rearrange("(k p) n -> p k n", p=128), writes=["wtm"], c=35.0)
    anwb = A.alloc([D], F32)
    S.dma("sp", anwb, T["anw"].partition_broadcast(128), writes=["anwb"])
    ropet = A.alloc([NTT, 16], F32)
    S.dma("sp", ropet, T["rope"].rearrange("p (t c) -> p t c", c=16), writes=["ropet"])
    convw = A.alloc([16], F32)
    convb = A.alloc([4], F32)
    S.dma("sp", convw, T["convw"], writes=["convw"])
    S.dma("sp", convb, T["convb"], writes=["convb"])
    onesf = A.alloc([128], F32)
    S.pool(lambda e: e.memset(onesf, 1.0), writes=["onesf"])
    two = lambda shape, dt: [A.alloc(shape, dt) for _ in range(2)]
    xt = two([D], F32)
    sq1 = A.alloc([D], BF16)
    sq = [sq1, sq1]
    ss = A.alloc([8], F32)
    ub = two([D], BF16)
    uT2 = two([KC, 512], BF16)
    cbuf = A.alloc([4, 515], F32)
    cacc_1 = A.alloc([512], F32)
    cacc = [cacc_1, cacc_1]
    xbcT = two([4, 512], BF16)
    zs = two([256], BF16)
    ez_1 = A.alloc([256], F32)
    ez = [ez_1, ez_1]
    ec = two([512], F32)
    dtt = two([4], F32)
    qk = two([6, 64], F32)
    qkr = two([6, 64], BF16)
    qkb = two([4, 64], BF16)
    rt = two([4, 6, 8], F32)
    qst = two([4, 128], BF16)
    qut = two([4, 128], BF16)
    xtm = two([256], BF16)
    btm = two([128], BF16)
    hst = A.alloc([256], F32)
    hstb = two([256], BF16)
    aneg = A.alloc([4], F32)
    adt = two([4], F32)
    acol = two([4], F32)
    nacol = two([4], F32)
    eac = two([4], F32)
    rhs4_1 = A.alloc([4, 128], F32)
    rhs4 = [rhs4_1, rhs4_1]
    seg4_1 = A.alloc([4, 128], F32)
    seg4 = [seg4_1, seg4_1]
    cbm = two([128], F32)
    MT = two([4, 128], BF16)
    alast = two([4], F32)
    dsv = two([4], F32)
    cdv = two([4], F32)
    wsc = two([4], F32)
    xw = two([256], BF16)
    xdt = two([256], BF16)
    ydg_1 = A.alloc([256], F32)
    ydg = [ydg_1, ydg_1]
    yy_1 = A.alloc([256], F32)
    yy = [yy_1, yy_1]
    tmpd_1 = A.alloc([256], F32)
    tmpd = [tmpd_1, tmpd_1]
    yo = two([256], F32)
    S.pool(lambda e: e.memset(cbuf, 0.0), writes=["cbuf", "cbuf0", "cbuf1", "cbuf2", "cbuf3"])
    S.pool(lambda e: e.memset(hst, 0.0), writes=["hst"])
    S.act(lambda e: e.activation(aneg, hpb[:, 4:8], AF.Exp), reads=["hpb"], writes=["aneg"])
    S.dve(lambda e: e.tensor_scalar(aneg, aneg, -1.0, None, ALU.mult), reads=["aneg"], writes=["aneg"])

    ptr = pbf(pbig, 0, 1)
    psA = pbig[:, 1, 0:NA]
    psB = pbig[:, 2, 0:NB_]
    psF = pbig[:, 3, :]
    psR = pbig[:, 4, :]
    psS = pbig[:, 5, :]
    psN = pbig[:, 6, 0:256]
    psY = pbig[:, 7, :]

    def load_x(tt):
        S.dma("sp", xt[tt % 2], T["xb"][tt * 128:(tt + 1) * 128, :], writes=["xt%d" % (tt % 2)], c=5.0)

    def b4(ap, n):
        return ap.unsqueeze(2).to_broadcast([128, 4, n])

    load_x(0)
    for G in range(SEQ // 512):
        gp = G % 2
        uT = uT2[gp]
        for j in range(4):
            tt = G * 4 + j
            b = tt % 2
            if tt + 1 < NTT:
                load_x(tt + 1)
            ssb = ss[:, b:b + 1]
            S.act(lambda e, b=b, ssb=ssb: e.activation(sq[b], xt[b], AF.Square, accum_out=ssb), reads=["xt%d" % b], writes=["n%dss" % b, "nsq"], c=1.9)
            S.act(lambda e, ssb=ssb: e.activation(ssb, ssb, AF.Ln, scale=1.0 / D, bias=epsb), reads=["n%dss" % b, "eps"], writes=["n%dss" % b])
            S.act(lambda e, ssb=ssb: e.activation(ssb, ssb, AF.Exp, scale=-0.5), reads=["n%dss" % b], writes=["n%dss" % b])
            S.dve(lambda e, b=b: e.scalar_tensor_tensor(ub[b], xt[b], ss[:, b:b + 1], anwb, ALU.mult, ALU.mult),
                  reads=["xt%d" % b, "n%dss" % b, "anwb"], writes=["ub%d" % b], c=2.2)
            for half in range(2):
                for k8 in range(8):
                    kc = half * 8 + k8
                    S.pe(lambda e, kc=kc, k8=k8, b=b: e.transpose(ptr[:, k8 * 128:(k8 + 1) * 128], ub[b][:, kc * 128:(kc + 1) * 128], identb),
                         reads=["ub%d" % b, "ident"], writes=["ptr"])
                if half == 0:
                    S.act(lambda e, j=j, uT=uT: e.copy(uT[:, 0:8, j * 128:(j + 1) * 128], ptr.rearrange("p (k n) -> p k n", k=8)), reads=["ptr"], writes=["uT%d_%d_0" % (gp, j)], c=0.9)
                else:
                    S.dve(lambda e, j=j, uT=uT: e.tensor_copy(uT[:, 8:16, j * 128:(j + 1) * 128], ptr.rearrange("p (k n) -> p k n", k=8)), reads=["ptr"], writes=["uT%d_%d_1" % (gp, j)], c=0.7)
        uTr = ["uT%d_%d_%d" % (gp, j, h) for j in range(4) for h in range(2)]
        for c in range(5):
            for kc in range(KC):
                S.pe(lambda e, c=c, kc=kc, uT=uT: e.matmul(psF, wfm[:, kc, c * 128:(c + 1) * 128], uT[:, kc, :], start=(kc == 0), stop=(kc == KC - 1)),
                     reads=uTr + ["wfm"], writes=["psF0"], c=0.22)
            if c < 4:
                ca = cacc[c % 2]
                car = "cacc"
                S.act(lambda e, c=c: e.copy(cbuf[:, c, 3:515], psF), reads=["psF0"], writes=["cbuf%d" % c], c=0.6)
                S.dve(lambda e, c=c, ca=ca: e.tensor_scalar(ca, cbuf[:, c, 0:512], convw[:, c * 4:c * 4 + 1], convb[:, c:c + 1], ALU.mult, ALU.add), reads=["cbuf%d" % c, "convw", "convb"], writes=[car], c=0.4)
                for k in range(1, 4):
                    S.dve(lambda e, c=c, k=k, ca=ca: e.scalar_tensor_tensor(ca, cbuf[:, c, k:k + 512], convw[:, c * 4 + k:c * 4 + k + 1], ca, ALU.mult, ALU.add),
                          reads=["cbuf%d" % c, car, "convw"], writes=[car], c=0.65)
                ece = ec[c % 2]
                ecr = "ec%d" % (c % 2)
                S.act(lambda e, ca=ca, ece=ece: e.activation(ece, ca, AF.Exp, scale=-1.0), reads=[car], writes=[ecr], c=0.6)
                S.act(lambda e, ece=ece: e.activation(ece, ece, AF.Ln, bias=oneb), reads=[ecr, "one"], writes=[ecr], c=0.6)
                S.act(lambda e, ece=ece: e.activation(ece, ece, AF.Exp, scale=-1.0), reads=[ecr], writes=[ecr], c=0.6)
                S.pool(lambda e, c=c, ca=ca, ece=ece, gp=gp: e.tensor_tensor(xbcT[gp][:, c, :], ca, ece, ALU.mult), reads=[car, ecr], writes=["xbcT%d_%d" % (gp, c)], c=2.0)
                S.pool(lambda e, c=c: e.tensor_copy(cbuf[:, c, 0:3], cbuf[:, c, 512:515]), reads=["cbuf%d" % c], writes=["cbuf%d" % c])
            else:
                S.act(lambda e, G=G: e.copy(kcvcT[:, G * 512:(G + 1) * 512], psF), reads=["psF0"], writes=["kcvcT"], c=0.6)
        xr = lambda c: "xbcT%d_%d" % (gp, c)
        for j in range(4):
            tt = G * 4 + j
            p = tt % 2
            P = str(p)
            tok = slice(tt * 128, (tt + 1) * 128)
            js = slice(j * 128, (j + 1) * 128)
            for kc in range(KC):
                S.pe(lambda e, js=js, kc=kc, uT=uT: e.matmul(psA, uT[:, kc, js], wtm[:, kc, 0:NA], start=(kc == 0), stop=(kc == KC - 1)),
                     reads=uTr + ["wtm"], writes=["psA"], c=0.19)
            for kc in range(KC):
                S.pe(lambda e, js=js, kc=kc, uT=uT: e.matmul(psB, uT[:, kc, js], wtm[:, kc, NA:NA + NB_], start=(kc == 0), stop=(kc == KC - 1)),
                     reads=uTr + ["wtm"], writes=["psB"], c=0.18)
            S.act(lambda e, p=p: e.activation(ez[p], psA[:, 0:256], AF.Exp, scale=-1.0), reads=["psA"], writes=["ez"])
            S.act(lambda e, p=p: e.activation(ez[p], ez[p], AF.Ln, bias=oneb), reads=["ez", "one"], writes=["ez"])
            S.act(lambda e, p=p: e.activation(ez[p], ez[p], AF.Exp, scale=-1.0), reads=["ez"], writes=["ez"])
            S.dve(lambda e, p=p: e.tensor_tensor(zs[p], psA[:, 0:256], ez[p], ALU.mult), reads=["psA", "ez"], writes=["zs" + P])
            S.dve(lambda e, p=p: e.tensor_tensor(dtt[p], psA[:, 256:260], hpb[:, 0:4], ALU.add), reads=["psA", "hpb"], writes=["dtt" + P])
            S.act(lambda e, p=p: e.activation(dtt[p], dtt[p], AF.Exp), reads=["dtt" + P], writes=["dtt" + P])
            S.act(lambda e, p=p: e.activation(dtt[p], dtt[p], AF.Ln, bias=oneb), reads=["dtt" + P, "one"], writes=["dtt" + P])
            S.act(lambda e, tt=tt: e.activation(gate[:, tt, :], psA[:, 260:272], AF.Exp, scale=-1.0), reads=["psA"], writes=["gate%d" % tt])
            S.dve(lambda e, tt=tt: e.tensor_scalar(gate[:, tt, :], gate[:, tt, :], 1.0, None, ALU.add), reads=["gate%d" % tt], writes=["gate%d" % tt])
            S.dve(lambda e, tt=tt: e.reciprocal(gate[:, tt, :], gate[:, tt, :]), reads=["gate%d" % tt], writes=["gate%d" % tt])
            S.dve(lambda e, tt=tt: e.tensor_copy(VsA[:, tt, 0:64], psA[:, 272:336]), reads=["psA", "VsA"], writes=["VsA%d" % tt])
            S.dve(lambda e, tt=tt: e.tensor_copy(VwA[:, tt, 0:64], psA[:, 336:400]), reads=["psA", "VwA"], writes=["VwA%d" % tt])
            S.act(lambda e, p=p: e.copy(qk[p], psB.rearrange("p (a b) -> p a b", a=6)), reads=["psB"], writes=["qk" + P], c=0.5)
            S.act(lambda e, p=p: e.copy(qkb[p], qk[p][:, 0:4, :]), reads=["qk" + P], writes=["qkb" + P])
            S.act(lambda e, p=p: e.copy(qkr[p], qk[p]), reads=["qk" + P], writes=["qkr" + P], c=0.45)
            cosb = ropet[:, tt, 0:8].unsqueeze(1).to_broadcast([128, 6, 8])
            sinb = ropet[:, tt, 8:16].unsqueeze(1).to_broadcast([128, 6, 8])
            S.dve(lambda e, cosb=cosb, p=p: e.tensor_tensor(rt[p][:, 0], qk[p][:, :, 0:8], cosb, ALU.mult), reads=["qk" + P, "ropet"], writes=["rt0" + P])
            S.dve(lambda e, sinb=sinb, p=p: e.tensor_tensor(rt[p][:, 1], qk[p][:, :, 8:16], sinb, ALU.mult), reads=["qk" + P, "ropet"], writes=["rt1" + P])
            S.dve(lambda e, cosb=cosb, p=p: e.tensor_tensor(rt[p][:, 2], qk[p][:, :, 8:16], cosb, ALU.mult), reads=["qk" + P, "ropet"], writes=["rt2" + P])
            S.dve(lambda e, sinb=sinb, p=p: e.tensor_tensor(rt[p][:, 3], qk[p][:, :, 0:8], sinb, ALU.mult), reads=["qk" + P, "ropet"], writes=["rt3" + P])
            S.dve(lambda e, p=p: e.tensor_tensor(qkr[p][:, :, 0:8], rt[p][:, 0], rt[p][:, 1], ALU.subtract), reads=["rt0" + P, "rt1" + P, "qkr" + P], writes=["qkr" + P])
            S.dve(lambda e, p=p: e.tensor_tensor(qkr[p][:, :, 8:16], rt[p][:, 2], rt[p][:, 3], ALU.add), reads=["rt2" + P, "rt3" + P, "qkr" + P], writes=["qkr" + P])
            ptq = ptr[0:64, 0:768].rearrange("p (a b) -> p a b", a=6)
            ptu = pbf(pbig, 6, 7)[0:64, 512:1024].rearrange("p (a b) -> p a b", a=4)
            for a in range(6):
                S.pe(lambda e, a=a, p=p: e.transpose(ptq[:, a, :], qkr[p][:, a, :], identb), reads=["qkr" + P, "ident"], writes=["ptr"])
            for a in range(4):
                S.pe(lambda e, a=a, p=p: e.transpose(ptu[:, a, :], qkb[p][:, a, :], identb), reads=["qkb" + P, "ident"], writes=["ptr2"])
            S.act(lambda e, p=p: e.copy(qst[p][0:64], ptq[:, 0:4, :]), reads=["ptr"], writes=["qst" + P])
            S.dve(lambda e, p=p: e.tensor_copy(qut[p][0:64], ptu), reads=["ptr2"], writes=["qut" + P])
            S.act(lambda e, tok=tok: e.copy(KsA[0:64, tok], ptq[:, 4, :]), reads=["ptr"], writes=["KsA%d" % tt])
            S.dve(lambda e, tok=tok: e.tensor_copy(KwT[0:64, tok], ptq[:, 5, :]), reads=["ptr"], writes=["KwT%d" % tt])
            S.dma("sp", T["qr_d"][:, :, tok], qst[p][0:64], reads=["qst" + P], writes=["qr_d%d" % tt])
            S.dma("sp", T["qu_d"][:, :, tok], qut[p][0:64], reads=["qut" + P], writes=["qu_d%d" % tt])
            ptx = ptr[:, 0:384]
            for c in range(3):
                S.pe(lambda e, c=c, js=js, gp=gp: e.transpose(ptx[:, c * 128:(c + 1) * 128], xbcT[gp][:, c, js], identb),
                     reads=[xr(c), "ident"], writes=["ptr"])
            S.act(lambda e, p=p: e.copy(xtm[p], ptx[:, 0:256]), reads=["ptr"], writes=["xtm" + P])
            S.act(lambda e, p=p: e.copy(btm[p], ptx[:, 256:384]), reads=["ptr"], writes=["btm" + P])
            S.dve(lambda e, p=p: e.tensor_tensor(adt[p], dtt[p], aneg, ALU.mult), reads=["dtt" + P, "aneg"], writes=["adt" + P])
            S.pe(lambda e, p=p: e.matmul(psS[:, 0:4], utri, adt[p], start=True, stop=True), reads=["utri", "adt" + P], writes=["psS_a"])
            S.act(lambda e, p=p: e.copy(acol[p], psS[:, 0:4]), reads=["psS_a"], writes=["acol" + P])
            S.dve(lambda e, p=p: e.tensor_scalar(nacol[p], psS[:, 0:4], -1.0, None, ALU.mult), reads=["psS_a"], writes=["nacol" + P])
            S.act(lambda e, p=p: e.activation(eac[p], acol[p], AF.Exp), reads=["acol" + P], writes=["eac" + P])
            S.pe(lambda e, js=js, gp=gp: e.matmul(psS[:, 256:384], xbcT[gp][:, 2, js], xbcT[gp][:, 3, js], start=True, stop=True),
                 reads=[xr(2), xr(3)], writes=["psS_c"])
            S.dve(lambda e, p=p: e.tensor_tensor(cbm[p], psS[:, 256:384], utri, ALU.mult), reads=["psS_c", "utri"], writes=["cbm" + P])
            S.dve(lambda e, p=p: e.tensor_tensor(rhs4[p], utri.unsqueeze(1).to_broadcast([128, 4, 128]), b4(adt[p], 128), ALU.mult),
                  reads=["utri", "adt" + P], writes=["rhs4"], c=0.6)
            S.pe(lambda e, p=p: e.matmul(psR, onesf, rhs4[p].rearrange("p a b -> p (a b)"), start=True, stop=True), reads=["onesf", "rhs4"], writes=["psR"], c=0.9)
            psR4 = psR.rearrange("p (a b) -> p a b", a=4)
            S.dve(lambda e, p=p: e.tensor_tensor(seg4[p], psR4, b4(acol[p], 128), ALU.subtract), reads=["psR", "acol" + P], writes=["seg4"], c=0.7)
            S.dve(lambda e, p=p: e.tensor_scalar(seg4[p], seg4[p], 0.0, None, ALU.min), reads=["seg4"], writes=["seg4"], c=0.35)
            S.act(lambda e, p=p: e.activation(seg4[p], seg4[p], AF.Exp), reads=["seg4"], writes=["seg4"], c=0.6)
            S.dve(lambda e, p=p: e.tensor_tensor(MT[p], seg4[p], cbm[p].unsqueeze(1).to_broadcast([128, 4, 128]), ALU.mult),
                  reads=["seg4", "cbm" + P], writes=["MT" + P], c=0.6)
            S.dve(lambda e, p=p: e.tensor_copy(alast[p], psR4[:, :, 127]), reads=["psR"], writes=["alast" + P])
            S.dve(lambda e, p=p: e.tensor_tensor(dsv[p], nacol[p], alast[p], ALU.add), reads=["nacol" + P, "alast" + P], writes=["dsv" + P])
            S.act(lambda e, p=p: e.activation(dsv[p], dsv[p], AF.Exp), reads=["dsv" + P], writes=["dsv" + P])
            S.act(lambda e, p=p: e.activation(cdv[p], alast[p], AF.Exp), reads=["alast" + P], writes=["cdv" + P])
            S.dve(lambda e, p=p: e.tensor_tensor(wsc[p], dtt[p], dsv[p], ALU.mult), reads=["dtt" + P, "dsv" + P], writes=["wsc" + P])
            v4 = lambda ap: ap.rearrange("p (h d) -> p h d", h=4)
            S.dve(lambda e, p=p: e.tensor_tensor(v4(xw[p]), v4(xtm[p]), b4(wsc[p], 64), ALU.mult), reads=["xtm" + P, "wsc" + P], writes=["xw" + P])
            S.dve(lambda e, p=p: e.tensor_tensor(v4(xdt[p]), v4(xtm[p]), b4(dtt[p], 64), ALU.mult), reads=["xtm" + P, "dtt" + P], writes=["xdt" + P])
            S.act(lambda e, p=p: e.copy(hstb[p], hst), reads=["hst"], writes=["hstb" + P])
            S.pe(lambda e, js=js, gp=gp, p=p: e.matmul(psY[:, 0:256], xbcT[gp][:, 3, js], hstb[p], start=True, stop=True), reads=[xr(3), "hstb" + P], writes=["psY_o"])
            for h in range(4):
                S.pe(lambda e, h=h, p=p: e.matmul(psY[:, 256 + h * 64:256 + (h + 1) * 64], MT[p][:, h, :], xdt[p][:, h * 64:(h + 1) * 64], start=True, stop=True),
                     reads=["MT" + P, "xdt" + P], writes=["psY_d"])
            S.pe(lambda e, p=p: e.matmul(psN, btm[p], xw[p], start=True, stop=True), reads=["btm" + P, "xw" + P], writes=["psN"])
            S.dve(lambda e, p=p: e.tensor_tensor(v4(hst), v4(hst), b4(cdv[p], 64), ALU.mult), reads=["hst", "cdv" + P], writes=["hst"])
            S.dve(lambda e: e.tensor_tensor(hst, hst, psN, ALU.add), reads=["hst", "psN"], writes=["hst"])
            S.act(lambda e, p=p: e.copy(ydg[p], psY[:, 256:512]), reads=["psY_d"], writes=["ydg"])
            S.dve(lambda e, p=p: e.tensor_tensor(v4(yy[p]), v4(psY[:, 0:256]), b4(eac[p], 64), ALU.mult), reads=["psY_o", "eac" + P], writes=["yy"])
            S.pool(lambda e, p=p: e.tensor_tensor(v4(tmpd[p]), v4(xtm[p]), b4(hpb[:, 8:12], 64), ALU.mult), reads=["xtm" + P, "hpb"], writes=["tmpd"])
            S.dve(lambda e, p=p: e.tensor_tensor(yy[p], yy[p], ydg[p], ALU.add), reads=["yy", "ydg"], writes=["yy"])
            S.dve(lambda e, p=p: e.tensor_tensor(yy[p], yy[p], tmpd[p], ALU.add), reads=["yy", "tmpd"], writes=["yy"])
            S.dve(lambda e, p=p: e.tensor_tensor(yo[p], yy[p], zs[p], ALU.mult), reads=["yy", "zs" + P], writes=["yo" + P])
            S.dma("sp", T["mixo"][tok, 0:256], yo[p], reads=["yo" + P], writes=["mixo_s%d" % tt])

    if int(os.environ.get('MIX_STOP', '9')) <= 1:
        return
    bar()
    A.reset(base_persist)
    attO = A.alloc([NTT, 256], F32)
    nmT = A.alloc([SEQ], BF16)
    w1 = A.alloc([32, 256], BF16)
    S.dma("pool", w1[0:64], T["w1k"].rearrange("d (l h) -> d l h", l=32), writes=["w1k"])
    S.dma("pool", w1[64:128], T["w1v"].rearrange("d (l h) -> d l h", l=32), writes=["w1v"])
    pe_ = A.alloc([32], BF16)
    S.dma("pool", pe_[0:64], T["pek"], writes=["pek"])
    S.dma("pool", pe_[64:128], T["pev"], writes=["pev"])
    w2 = A.alloc([2, 2, 64], BF16)
    S.dma("pool", w2[:, 0], T["w2k"].rearrange("(c p) d -> p c d", p=128), writes=["w2k"])
    S.dma("pool", w2[:, 1], T["w2v"].rearrange("(c p) d -> p c d", p=128), writes=["w2v"])
    cbias = A.alloc([4], F32)
    hsb = A.alloc([4, 256], BF16)
    KcT = A.alloc([256], BF16)
    VcA = A.alloc([2, 129], BF16)
    S.pool(lambda e: e.memset(hsb, 0.0), writes=["hsb"])
    S.pool(lambda e: e.memset(VcA, 0.0), writes=["VcA"])
    S.pool(lambda e: e.memset(KcT, 0.0), writes=["KcT"])
    S.pool(lambda e: e.memset(VcA[:, :, 64:65], 1.0), reads=["VcA"], writes=["VcA"])
    S.dma("pool", VcA[:, :, 65:129], T["ovl"].rearrange("(c p) j -> p c j", p=128), reads=["VcA"], writes=["VcA"])
    for kv in range(2):
        rows = slice(kv * 64, (kv + 1) * 64)
        for hc in range(2):
            idx = kv * 2 + hc
            pb_ = pbig[:, idx, 0:255]
            pbias = pbig[:, 4, idx:idx + 1]
            for l in range(32):
                S.pe(lambda e, rows=rows, hc=hc, l=l, pbias=pbias: e.matmul(pbias, w1[rows, l, hc * 128:(hc + 1) * 128], pe_[rows, l:l + 1], start=(l == 0), stop=(l == 31)),
                     reads=["w1k", "w1v", "pek", "pev"], writes=["pbias%d" % idx])
            S.act(lambda e, idx=idx, pbias=pbias: e.copy(cbias[:, idx:idx + 1], pbias), reads=["pbias%d" % idx], writes=["cbias%d" % idx])
            for l in range(32):
                S.pe(lambda e, rows=rows, hc=hc, l=l, pb_=pb_: e.matmul(pb_, w1[rows, l, hc * 128:(hc + 1) * 128], kcvcT[rows, l:l + 16 * 254 + 1:16], start=(l == 0), stop=(l == 31)),
                     reads=["w1k", "w1v", "kcvcT"], writes=["phid%d" % idx])
            S.act(lambda e, idx=idx, pb_=pb_: e.activation(hsb[:, idx, 0:255], pb_, AF.Silu, bias=cbias[:, idx:idx + 1]), reads=["phid%d" % idx, "cbias%d" % idx, "hsb"], writes=["hsb%d" % idx])
    pk = pbig[0:64, 5, 0:255]
    for hc in range(2):
        S.pe(lambda e, hc=hc: e.matmul(pk, w2[:, 0, hc, :], hsb[:, hc, 0:255], start=(hc == 0), stop=(hc == 1)), reads=["w2k", "hsb0", "hsb1"], writes=["pk"])
    S.act(lambda e: e.copy(KcT[0:64, 0:255], pk), reads=["pk", "KcT"], writes=["KcT"])
    for it in range(2):
        m = 128 if it == 0 else 127
        pv = pbig[0:m, 6 + it, 0:64]
        for hc in range(2):
            S.pe(lambda e, it=it, hc=hc, m=m, pv=pv: e.matmul(pv, hsb[:, 2 + hc, it * 128:it * 128 + m], w2[:, 1, hc, :], start=(hc == 0), stop=(hc == 1)),
                 reads=["w2v", "hsb2", "hsb3"], writes=["pv%d" % it])
        S.act(lambda e, it=it, m=m, pv=pv: e.copy(VcA[0:m, it, 0:64], pv), reads=["pv%d" % it, "VcA"], writes=["VcA"])

    if int(os.environ.get('MIX_STOP', '9')) <= 2:
        return
    bar()
    base3 = A.off
    qu = [A.alloc([4, 512], BF16) for _ in range(2)]
    cmk = [A.alloc([2, 512], BF16) for _ in range(2)]
    PcT = [A.alloc([2, 512], BF16) for _ in range(2)]
    imp = A.alloc([4, 64], F32)
    imp2 = A.alloc([64], F32)
    m8 = A.alloc([16], F32)
    thr = A.alloc([1], F32)
    rr = A.alloc([32], F32)
    nmb = A.alloc([128], BF16)
    S.pool(lambda e: e.memset(nmb, 0.0), writes=["nmb"])
    pnt = pbf(pbig, 7, 8)[:, 0:128]
    for Q in range(8):
        qb = Q % 2
        qs = slice(Q * 512, (Q + 1) * 512)
        S.dma("sp", qu[qb][0:64], T["qu_d"][:, :, qs], writes=["qu%d" % qb])
        S.dma("pool", cmk[qb], T["cmask"][:, qs].rearrange("(c p) t -> p c t", p=128), writes=["cmk%d" % qb])
        for h in range(4):
            pb2 = h % 2
            for it in range(2):
                ps_ = pbig[:, pb2 * 2 + it, :]
                S.pe(lambda e, it=it, h=h, qb=qb, ps_=ps_: e.matmul(ps_, KcT[0:64, it * 128:(it + 1) * 128], qu[qb][0:64, h, :], start=True, stop=False),
                     reads=["KcT", "qu%d" % qb], writes=["psc%d_%d" % (pb2, it)])
                S.pe(lambda e, it=it, qb=qb, ps_=ps_: e.matmul(ps_, identb, cmk[qb][:, it, :], start=False, stop=True),
                     reads=["ident", "cmk%d" % qb], writes=["psc%d_%d" % (pb2, it)])
                S.act(lambda e, it=it, pb2=pb2, ps_=ps_: e.activation(PcT[pb2][:, it, :], ps_, AF.Exp, scale=SCALE), reads=["psc%d_%d" % (pb2, it)], writes=["PcT%d_%d" % (pb2, it)])
            for sub in range(4):
                tt = Q * 4 + sub
                po = pbig[:, 4 + (sub % 2), 0:129]
                for it in range(2):
                    S.pe(lambda e, it=it, pb2=pb2, sub=sub, po=po: e.matmul(po, PcT[pb2][:, it, sub * 128:(sub + 1) * 128], VcA[:, it, :], start=(it == 0), stop=(it == 1)),
                         reads=["PcT%d_0" % pb2, "PcT%d_1" % pb2, "VcA"], writes=["po%d" % (sub % 2)])
                pr = ["po%d" % (sub % 2)]
                ri = ((h % 2) * 4 + sub) * 2
                S.dve(lambda e, ri=ri, po=po: e.tensor_scalar(rr[:, ri:ri + 1], po[:, 64:65], 1e-30, None, ALU.add), reads=pr, writes=["rr0_%d" % ri])
                S.dve(lambda e, ri=ri: e.reciprocal(rr[:, ri:ri + 1], rr[:, ri:ri + 1]), reads=["rr0_%d" % ri], writes=["rr0_%d" % ri])
                if h == 0:
                    S.act(lambda e, ri=ri, po=po, sub=sub: e.activation(imp[:, sub, :], po[:, 65:129], AF.Copy, scale=rr[:, ri:ri + 1]), reads=pr + ["rr0_%d" % ri], writes=["imp%d" % sub])
                else:
                    S.dve(lambda e, ri=ri, po=po, sub=sub: e.scalar_tensor_tensor(imp[:, sub, :], po[:, 65:129], rr[:, ri:ri + 1], imp[:, sub, :], ALU.mult, ALU.add),
                          reads=pr + ["rr0_%d" % ri, "imp%d" % sub], writes=["imp%d" % sub])
                S.dve(lambda e, ri=ri, tt=tt, h=h: e.tensor_tensor(rr[:, ri + 1:ri + 2], rr[:, ri:ri + 1], gate[:, tt, h * 3:h * 3 + 1], ALU.mult), reads=["rr0_%d" % ri, "gate"], writes=["rr1_%d" % ri])
                S.act(lambda e, ri=ri, po=po, tt=tt, h=h: e.activation(attO[:, tt, h * 64:(h + 1) * 64], po[:, 0:64], AF.Copy, scale=rr[:, ri + 1:ri + 2]), reads=pr + ["rr1_%d" % ri], writes=["attO%d" % tt])
        for sub in range(4):
            tt = Q * 4 + sub
            ir = ["imp%d" % sub]
            im = imp[:, sub, :]
            S.pool(lambda e, im=im: e.memset(im[:, 0:1], 1e4), reads=ir, writes=ir)
            lo = max(2 * tt - 1, 0)
            S.pool(lambda e, im=im, lo=lo, tt=tt: e.memset(im[0:64, lo:2 * tt + 1], 1e4), reads=ir, writes=ir)
            S.pool(lambda e, im=im, tt=tt: e.memset(im[64:128, 2 * tt:2 * tt + 2], 1e4), reads=ir, writes=ir)
            if 2 * tt + 1 < 64:
                S.pool(lambda e, im=im, tt=tt: e.memset(im[0:64, 2 * tt + 1:64], -1.0), reads=ir, writes=ir)
            if 2 * tt + 2 < 64:
                S.pool(lambda e, im=im, tt=tt: e.memset(im[64:128, 2 * tt + 2:64], -1.0), reads=ir, writes=ir)
            S.dve(lambda e, im=im: e.max(m8[:, 0:8], im), reads=ir, writes=["m8a"])
            S.dve(lambda e, im=im: e.match_replace(imp2, m8[:, 0:8], im, -1e30), reads=ir + ["m8a"], writes=["imp2"])
            S.dve(lambda e: e.max(m8[:, 8:16], imp2), reads=["imp2"], writes=["m8b"])
            S.dve(lambda e: e.tensor_scalar(thr, m8[:, 15:16], 0.0, None, ALU.max), reads=["m8b"], writes=["thr"])
            S.dve(lambda e, im=im: e.tensor_scalar(nmb[:, 64:128], im, thr, NEG, ALU.is_lt, ALU.mult), reads=ir + ["thr", "nmb"], writes=["nmb"])
            S.pe(lambda e: e.transpose(pnt, nmb, identb), reads=["nmb", "ident"], writes=["pnt"])
            S.act(lambda e, tt=tt: e.copy(nmT[64:128, tt * 128:(tt + 1) * 128], pnt[64:128, :]), reads=["pnt"], writes=["nmT"])

    if int(os.environ.get('MIX_STOP', '9')) <= 3:
        return
    bar()
    A.reset(base3)
    Qa = [A.alloc([4, 512], BF16) for _ in range(2)]
    PT = [A.alloc([512], BF16) for _ in range(4)]
    PW = [A.alloc([512], BF16) for _ in range(3)]
    r4 = A.alloc([8], F32)
    oTs = [A.alloc([512], F32) for _ in range(2)]
    pti = 0
    pwi = 0
    for Q in range(8):
        qb = Q % 2
        qs = slice(Q * 512, (Q + 1) * 512)
        S.dma("sp", Qa[qb][0:64], T["qr_d"][:, :, qs], writes=["Qa%d" % qb])
        for h in range(4):
            S.pool(lambda e, qb=qb, h=h, qs=qs: e.tensor_copy(Qa[qb][64:128, h, :], nmT[64:128, qs]), reads=["nmT"], writes=["Qm%d_%d" % (qb, h)])
        for h in range(4):
            qr_ = ["Qa%d" % qb, "Qm%d_%d" % (qb, h)]
            posel = pbig[:, 6, 0:260].rearrange("p (s c) -> p s c", s=4)
            powin = pbig[:, 7, 0:260].rearrange("p (s c) -> p s c", s=4)
            for kt in range(4 * Q + 4):
                ks_ = slice(kt * 128, (kt + 1) * 128)
                sb_ = kt % 3
                ps_ = pbig[:, sb_, :]
                pres = "pss%d" % sb_
                o = kt - 4 * Q
                if o < 0:
                    S.pe(lambda e, ks_=ks_, qb=qb, h=h, ps_=ps_: e.matmul(ps_, KsA[:, ks_], Qa[qb][:, h, :], start=True, stop=True),
                         reads=["KsA", "KsE"] + qr_, writes=[pres])
                    lo = 0
                else:
                    lo = o * 128
                    S.pe(lambda e, ks_=ks_, qb=qb, h=h, ps_=ps_, lo=lo: e.matmul(ps_[:, lo:lo + 128], KsA[:, ks_], Qa[qb][:, h, lo:lo + 128], start=True, stop=False),
                         reads=["KsA", "KsE"] + qr_, writes=[pres])
                    S.pe(lambda e, ps_=ps_, lo=lo: e.matmul(ps_[:, lo:lo + 128], identb, tricb, start=False, stop=True), reads=["ident", "tric"], writes=[pres])
                    if o < 3:
                        S.pe(lambda e, ks_=ks_, qb=qb, h=h, ps_=ps_, lo=lo: e.matmul(ps_[:, lo + 128:512], KsA[:, ks_], Qa[qb][:, h, lo + 128:512], start=True, stop=True),
                             reads=["KsA", "KsE"] + qr_, writes=[pres])
                pt_ = PT[pti % 4]
                ptres = "PT%d" % (pti % 4)
                pti += 1
                S.act(lambda e, ps_=ps_, pt_=pt_, lo=lo: e.activation(pt_[:, lo:512], ps_[:, lo:512], AF.Exp, scale=SCALE), reads=[pres], writes=[ptres])
                S.pe(lambda e, pt_=pt_, kt=kt, Q=Q, lo=lo: e.matmul(pbig[0:65, 6, lo:512], VsA[:, kt, :], pt_[:, lo:512], start=(kt == 0), stop=(kt == 4 * Q + 3)),
                     reads=[ptres, "VsA"], writes=["posel"], c=0.22)
            ob_ = (Q * 4 + h) % 2
            S.act(lambda e, ob_=ob_: e.copy(oTs[ob_][0:65, :], pbig[0:65, 6, :]), reads=["posel"], writes=["oTs%d" % ob_], c=0.6)
            poselT = pbig[:, 5, 0:260].rearrange("p (s c) -> p s c", s=4)
            for sub in range(4):
                S.pe(lambda e, ob_=ob_, sub=sub, poselT=poselT: e.transpose(poselT[:, sub, :], oTs[ob_][0:65, sub * 128:(sub + 1) * 128], identf[0:65, 0:65]),
                     reads=["oTs%d" % ob_, "identf"], writes=["poselT"])
            S.dve(lambda e: e.memset(pbig[:, 7, 0:260], 0.0), writes=["powin"])
            for r in range(-4, 4):
                kt = 4 * Q + r
                if kt < 0:
                    continue
                ks_ = slice(kt * 128, (kt + 1) * 128)
                s_lo, s_hi = max(r, 0), min(r + 4, 3)
                wsl = pwi % 2
                psw = pbig[:, 3 + wsl, :]
                pwres = "psw%d" % wsl
                pw_ = PW[pwi % 3]
                pwr = "PW%d" % (pwi % 3)
                pwi += 1
                plain = [s_ for s_ in range(s_lo, s_hi + 1) if s_ != r and s_ != r + 4]
                for s_, msk in ((r, tricb), (r + 4, triab)):
                    if s_lo <= s_ <= s_hi:
                        cs = slice(s_ * 128, (s_ + 1) * 128)
                        S.pe(lambda e, ks_=ks_, qb=qb, h=h, cs=cs, psw=psw: e.matmul(psw[:, cs], KwT[0:64, ks_], Qa[qb][0:64, h, cs], start=True, stop=False),
                             reads=["KwT", "Qa%d" % qb], writes=[pwres])
                        S.pe(lambda e, cs=cs, psw=psw, msk=msk: e.matmul(psw[:, cs], identb, msk, start=False, stop=True), reads=["ident", "tric", "tria"], writes=[pwres])
                if plain:
                    cs = slice(plain[0] * 128, (plain[-1] + 1) * 128)
                    S.pe(lambda e, ks_=ks_, qb=qb, h=h, cs=cs, psw=psw: e.matmul(psw[:, cs], KwT[0:64, ks_], Qa[qb][0:64, h, cs], start=True, stop=True),
                         reads=["KwT", "Qa%d" % qb], writes=[pwres], c=0.2)
                ca = slice(s_lo * 128, (s_hi + 1) * 128)
                S.act(lambda e, psw=psw, pw_=pw_, ca=ca: e.activation(pw_[:, ca], psw[:, ca], AF.Exp, scale=SCALE), reads=[pwres], writes=[pwr], c=0.5)
                for s_ in range(s_lo, s_hi + 1):
                    S.pe(lambda e, pw_=pw_, s_=s_, kt=kt, powin=powin: e.matmul(powin[:, s_, :], pw_[:, s_ * 128:(s_ + 1) * 128], VwA[:, kt, :],
                                                                                start=False, stop=False, skip_group_check=True),
                         reads=[pwr, "VwA"], writes=["powin"])
            for sub in range(4):
                tt = 4 * Q + sub
                for br, (po_, pres) in enumerate(((poselT, "poselT"), (powin, "powin"))):
                    S.dve(lambda e, po_=po_, sub=sub, br=br: e.reciprocal(r4[:, sub * 2 + br:sub * 2 + br + 1], po_[:, sub, 64:65]), reads=[pres], writes=["r4_%d_%d" % (sub, br)])
                    S.dve(lambda e, sub=sub, tt=tt, h=h, br=br: e.tensor_tensor(r4[:, sub * 2 + br:sub * 2 + br + 1], r4[:, sub * 2 + br:sub * 2 + br + 1], gate[:, tt, h * 3 + 1 + br:h * 3 + 2 + br], ALU.mult),
                          reads=["r4_%d_%d" % (sub, br), "gate"], writes=["r4_%d_%d" % (sub, br)])
                    S.dve(lambda e, po_=po_, sub=sub, tt=tt, h=h, br=br: e.scalar_tensor_tensor(attO[:, tt, h * 64:(h + 1) * 64], po_[:, sub, 0:64], r4[:, sub * 2 + br:sub * 2 + br + 1],
                                                                                                  attO[:, tt, h * 64:(h + 1) * 64], ALU.mult, ALU.add),
                          reads=[pres, "r4_%d_%d" % (sub, br), "attO%d" % tt], writes=["attO%d" % tt])
        for sub in range(4):
            tt = 4 * Q + sub
            S.dma("sp", T["mixo"][tt * 128:(tt + 1) * 128, 256:512], attO[:, tt, :], reads=["attO%d" % tt])


def _perm_cols():
    return None


def run_mix(inputs):
    if "mix" not in _CACHE:
        _CACHE["mix"] = build_mix()
    nc = _CACHE["mix"]
    x = inputs["x"]
    w_in = inputs["w_in"][0]
    offs = np.cumsum([0, 1024, 1536, 16, 1024, 256, 256, 256, 256, 256, 256, 48])
    oz, oxbc, odt, oq, okc, ovc, oks, ovs, okw, ovw, ogate = offs[:11]
    conv_w = inputs["conv_w"][0]
    conv_b = inputs["conv_b"][0]
    t = np.arange(SEQ, dtype=np.float32)
    inv = (1.0 / (500000.0 ** (np.arange(0, 16, 2, dtype=np.float32) / np.float32(16)))).astype(np.float32)
    ang = (t[:, None] * inv[None, :]).astype(np.float32)
    rope = np.concatenate([np.cos(ang), np.sin(ang)], 1).astype(np.float32)
    rope = np.ascontiguousarray(rope.reshape(NTT, 128, 16).transpose(1, 0, 2).reshape(128, NTT * 16))
    ident = np.eye(128, dtype=np.float32)
    kk = np.arange(128)[:, None]
    qq = np.arange(128)[None, :]
    tric = np.where(kk <= qq, 0.0, NEG).astype(np.float32)
    tria = np.where(kk > qq, 0.0, NEG).astype(np.float32)
    utri = (kk <= qq).astype(np.float32)
    emat = (np.arange(SEQ)[None, :] // 64 == np.arange(64)[:, None]).astype(np.float32)
    ii = np.arange(256)[:, None]
    cmask = np.where((16 * ii + 31 <= np.arange(SEQ)[None, :]) & (ii < 255), 0.0, NEG).astype(np.float32)
    cs = np.arange(255)[:, None] * 16
    ss_ = np.arange(64)[None, :] * 64
    ov = np.clip(np.minimum(cs + 32, ss_ + 64) - np.maximum(cs, ss_), 0, None) / 32.0
    ovl = np.zeros((256, 64), np.float32)
    ovl[:255] = ov
    in_maps = []
    for c in range(8):
        b, g = c // 4, c % 4
        grp = g // 2
        ar = np.arange
        tm_cols = np.concatenate([oz + 256 * g + ar(256), odt + 4 * g + ar(4), ogate + 12 * g + ar(12), ovs + 64 * g + ar(64), ovw + 64 * g + ar(64),
                                  oq + 256 * g + ar(256), oks + 64 * g + ar(64), okw + 64 * g + ar(64)])
        xcols = np.concatenate([256 * g + ar(256), 1024 + 128 * grp + ar(128), 1280 + 128 * grp + ar(128)])
        fm_cols = np.concatenate([oxbc + xcols, okc + 64 * g + ar(64), ovc + 64 * g + ar(64)])
        convw = np.ascontiguousarray(conv_w[:, xcols].T.reshape(4, 128, 4).transpose(1, 0, 2).reshape(128, 16))
        convb = np.ascontiguousarray(conv_b[xcols].reshape(4, 128).T)
        hp = np.concatenate([inputs["dt_bias"][0][4 * g:4 * g + 4], inputs["a_log"][0][4 * g:4 * g + 4], inputs["d_skip"][0][4 * g:4 * g + 4]]).astype(np.float32)
        in_maps.append(dict(
            xb=np.ascontiguousarray(x[b]), anw=inputs["attn_norm_w"][0], w_tm=np.ascontiguousarray(w_in[:, tm_cols]), w_fm=np.ascontiguousarray(w_in[:, fm_cols]),
            convw=convw, convb=convb, hp=hp, rope=rope,
            w1k=np.ascontiguousarray(inputs["cmp_w1_k"][0].reshape(32, 64, 256).transpose(1, 0, 2).reshape(64, 32 * 256)),
            w1v=np.ascontiguousarray(inputs["cmp_w1_v"][0].reshape(32, 64, 256).transpose(1, 0, 2).reshape(64, 32 * 256)),
            w2k=inputs["cmp_w2_k"][0], w2v=inputs["cmp_w2_v"][0],
            pek=np.ascontiguousarray(inputs["cmp_pe_k"][0].T), pev=np.ascontiguousarray(inputs["cmp_pe_v"][0].T),
            ident=ident, emat=emat, tric=tric, tria=tria, cmask=cmask, ovl=ovl, utri=utri,
        ))
    res = run_bass_kernel_spmd(nc, in_maps, core_ids=list(range(8)))
    mixed = np.empty((2, SEQ, D), np.float32)
    for c in range(8):
        b, g = c // 4, c % 4
        m = res.results[c]["mixo"]
        mixed[b, :, 256 * g:256 * (g + 1)] = m[:, 0:256]
        mixed[b, :, 1024 + 256 * g:1024 + 256 * (g + 1)] = m[:, 256:512]
    return mixed


def kernel(**inputs):
    inputs = {k: np.asarray(v) for k, v in inputs.items()}
    mixed = run_mix(inputs)
    return run_tail(inputs["x"], mixed, inputs["ssd_norm_w"][0], inputs["w_out"][0], inputs["ffn_norm_w"][0],
                    inputs["w_gate"][0], inputs["w_up"][0], inputs["w_down"][0], inputs["final_norm_w"])
```

```python
import contextlib
import os
import numpy as np
import ml_dtypes
import concourse.bass as bass
import concourse.mybir as mybir
from concourse.bass_utils import run_bass_kernel_spmd

F32 = mybir.dt.float32
BF16 = mybir.dt.bfloat16
U8 = mybir.dt.uint8
ALU = mybir.AluOpType
AF = mybir.ActivationFunctionType
AX = mybir.AxisListType

ENGS = ("pe", "act", "dve", "pool", "sp")
EPS = 1e-6


class _Op:
    __slots__ = ("eng", "fn", "dma", "deps", "idx", "signal", "val", "sem", "cc", "cost")


class Sched:
    def __init__(self, nc, n_dma_sems=8):
        self.nc = nc
        self.ops = []
        self.last_w = {}
        self.readers = {}
        self.n_dma_sems = n_dma_sems
        self.bar = None
        self.bank_of = {}

    def op(self, eng, fn, reads=(), writes=(), dma=False, c=None):
        o = _Op()
        o.cost = c
        o.eng, o.fn, o.dma = eng, fn, dma
        o.cc = False
        o.idx = len(self.ops)
        o.signal = False
        deps = {}
        if self.bar is not None:
            deps[self.bar] = "raw"
        for r in reads:
            w = self.last_w.get(r)
            if w is not None:
                deps[w] = "raw"
        for w_ in writes:
            w = self.last_w.get(w_)
            if w is not None:
                deps[w] = "raw"
            for r in self.readers.get(w_, ()):
                if r not in deps:
                    deps[r] = "war"
        for r in reads:
            self.readers.setdefault(r, []).append(o.idx)
        for w_ in writes:
            self.last_w[w_] = o.idx
            self.readers[w_] = []
        banks = set()
        for r in tuple(reads) + tuple(writes):
            banks.update(self.bank_of.get(r, ()))
        for b in banks:
            key = ("bank", b)
            w = self.last_w.get(key)
            if w is not None and w not in deps:
                deps[w] = "bank"
            self.last_w[key] = o.idx
        deps.pop(o.idx, None)
        o.deps = deps
        self.ops.append(o)
        return o

    def pe(self, fn, reads=(), writes=(), c=None):
        return self.op("pe", fn, reads, writes, c=c)

    def act(self, fn, reads=(), writes=(), c=None):
        return self.op("act", fn, reads, writes, c=c)

    def dve(self, fn, reads=(), writes=(), c=None):
        return self.op("dve", fn, reads, writes, c=c)

    def pool(self, fn, reads=(), writes=(), c=None):
        return self.op("pool", fn, reads, writes, c=c)

    DEF_COST = {"pe": 0.12, "act": 0.35, "dve": 0.25, "pool": 0.35}

    def reorder(self, window=int(os.environ.get("RWIN", "120"))):
        ops = self.ops
        n = len(ops)
        queues = {e: [] for e in ENGS}
        for o in ops:
            queues[o.eng].append(o.idx)
        head = {e: 0 for e in ENGS}
        sched = [False] * n
        fin = [0.0] * n
        etime = {e: 0.0 for e in ENGS}
        order = []
        left = n
        while left:
            best = None
            for e in ENGS:
                q = queues[e]
                h = head[e]
                while h < len(q) and sched[q[h]]:
                    h += 1
                head[e] = h
                if h >= len(q):
                    continue
                seen = 0
                i = h
                et = etime[e]
                while i < len(q) and seen < window:
                    k = q[i]
                    i += 1
                    if sched[k]:
                        continue
                    seen += 1
                    o = ops[k]
                    rdy = 0.0
                    ok = True
                    for d in o.deps:
                        if not sched[d]:
                            ok = False
                            break
                        f = fin[d] + (0.0 if ops[d].eng == e else 0.15)
                        if f > rdy:
                            rdy = f
                    if not ok:
                        continue
                    st = rdy if rdy > et else et
                    key = (st, k)
                    if best is None or key < best[0]:
                        best = (key, e, k)
                    if st <= et:
                        break
            assert best is not None, "scheduler stuck"
            (st, k), e, _ = best
            o = ops[k]
            if o.dma:
                etime[e] = st + 0.06
                fin[k] = st + (o.cost if o.cost is not None else 3.0)
            else:
                c = o.cost if o.cost is not None else self.DEF_COST[e]
                etime[e] = st + c
                fin[k] = st + c
            sched[k] = True
            order.append(k)
            left -= 1
        self.order = order
        self.est_time = max(fin) if fin else 0.0

    def dma(self, q, out, in_, reads=(), writes=(), c=None):
        return self.op(q, lambda e: e.dma_start(out=out, in_=in_), reads, writes, dma=True, c=c)

    def cc(self, fn, reads=(), writes=()):
        o = self.op("pool", fn, reads, writes, dma=True)
        o.cc = True
        return o

    def barrier(self, out, in_):
        allres = set(self.last_w.keys()) | set(self.readers.keys())
        o = self.op("sp", lambda e: e.dma_start(out=out, in_=in_), reads=(), writes=tuple(allres), dma=True)
        self.bar = o.idx
        self.last_w = {}
        self.readers = {}
        return o

    def emit(self, sems, block, final_wait_eng="sp"):
        ops = self.ops
        need = [False] * len(ops)
        for o in ops:
            for d, kind in o.deps.items():
                do = ops[d]
                if do.dma:
                    continue
                if do.eng == o.eng and not o.dma:
                    if do.eng == "pe" or kind == "bank":
                        continue
                need[d] = True
        cnt = {e: 0 for e in ENGS}
        dcnt = {}
        dval = {}
        per_eng = {e: [] for e in ENGS}
        order = getattr(self, "order", None) or list(range(len(ops)))
        for k_ in order:
            o = ops[k_]
            per_eng[o.eng].append(o)
            if o.dma and o.cc:
                o.sem = "cc"
                dval["cc"] = dval.get("cc", 0) + 1
                o.val = dval["cc"]
            elif o.dma:
                k = dcnt.get(o.eng, 0)
                dcnt[o.eng] = k + 1
                key = ("dma", o.eng, k % self.n_dma_sems)
                o.sem = key
                dval[key] = dval.get(key, 0) + 16
                o.val = dval[key]
            elif need[o.idx]:
                cnt[o.eng] += 1
                o.val = cnt[o.eng]
                o.sem = o.eng
                o.signal = True
        self.stats = {e: len(per_eng[e]) for e in ENGS}
        self.stats["signals"] = dict(cnt)

        def run(engname, e):
            waited = {}
            for o in per_eng[engname]:
                wl = {}
                for d, kind in o.deps.items():
                    do = ops[d]
                    if do.dma:
                        wl[do.sem] = max(wl.get(do.sem, 0), do.val)
                        continue
                    if do.eng == o.eng and not o.dma:
                        if do.eng == "pe" or kind == "bank":
                            continue
                    wl[do.sem] = max(wl.get(do.sem, 0), do.val)
                if o.dma and o.cc and o.val > 1:
                    wl[o.sem] = max(wl.get(o.sem, 0), o.val - 1)
                elif o.dma and not o.cc and o.val > 16:
                    wl[o.sem] = max(wl.get(o.sem, 0), o.val - 16)
                for s, v in wl.items():
                    if waited.get(s, 0) >= v:
                        continue
                    waited[s] = v
                    e.wait_ge(sems[s], v)
                ins = o.fn(e)
                if o.dma and o.cc:
                    ins.then_inc(sems[o.sem])
                elif o.dma:
                    ins.then_inc(sems[o.sem], 16)
                elif o.signal:
                    ins.then_inc(sems[o.sem], 1)
            if engname == final_wait_eng:
                for key, v in dval.items():
                    if waited.get(key, 0) < v:
                        e.wait_ge(sems[key], v)
                for en in ("pe", "act", "dve", "pool"):
                    if cnt[en] > 0 and waited.get(en, 0) < cnt[en]:
                        e.wait_ge(sems[en], cnt[en])

        @block.tensor
        def _(e):
            run("pe", e)

        @block.scalar
        def _(e):
            run("act", e)

        @block.vector
        def _(e):
            run("dve", e)

        @block.gpsimd
        def _(e):
            run("pool", e)

        @block.sync
        def _(e):
            run("sp", e)


def make_sems(nc, stack, n_dma_sems=8, queues=("sp", "pool", "act")):
    sems = {}
    for e in ("pe", "act", "dve", "pool", "cc"):
        sems[e] = stack.enter_context(nc.semaphore("s_" + e))
    for q in queues:
        for i in range(n_dma_sems):
            sems[("dma", q, i)] = stack.enter_context(nc.semaphore("d_%s_%d" % (q, i)))
    return sems


_DTSZ = {F32: 4, BF16: 2, U8: 1}


class Arena:
    def __init__(self, ar, size):
        self.ar, self.size, self.off = ar, size, 0

    def reset(self, off=0):
        self.off = off

    def alloc(self, shape, dtype, parts=128):
        n = int(np.prod(shape)) * _DTSZ[dtype]
        off = (self.off + 63) // 64 * 64
        assert off + n <= self.size, ("arena overflow", off, n, self.size)
        self.off = off + n
        ap = self.ar[0:parts, off:off + n].bitcast(dtype)
        if len(shape) > 1:
            names = [chr(ord("a") + i) for i in range(len(shape))]
            pat = "p (%s) -> p %s" % (" ".join(names), " ".join(names))
            ap = ap.rearrange(pat, **{nm: int(s) for nm, s in zip(names, shape)})
        return ap


D = 2048
FF = 5632
TOK = 1024
NT = TOK // 128
KC = D // 128
FC = FF // 128
SSDW = 1024


def build_tail():
    nc = bass.Bass("TRN2", target_bir_lowering=False)
    x = nc.dram_tensor("x_own", [TOK, D], F32, kind="ExternalInput").ap()
    mix = nc.dram_tensor("mix", [TOK, D], F32, kind="ExternalInput").ap()
    w_out = nc.dram_tensor("w_out", [D, D], F32, kind="ExternalInput").ap()
    w_gate = nc.dram_tensor("w_gate", [D, FF], F32, kind="ExternalInput").ap()
    w_up = nc.dram_tensor("w_up", [D, FF], F32, kind="ExternalInput").ap()
    w_down = nc.dram_tensor("w_down", [FF, D], F32, kind="ExternalInput").ap()
    nw = nc.dram_tensor("nw", [3, D], F32, kind="ExternalInput").ap()
    ident = nc.dram_tensor("ident", [128, 128], F32, kind="ExternalInput").ap()
    out = nc.dram_tensor("out", [TOK, D], F32, kind="ExternalOutput").ap()
    h_d = nc.dram_tensor("h_d", [TOK, D], F32, kind="Internal").ap()
    dummy = nc.dram_tensor("dummy_bar", [2, 64], F32, kind="Internal").ap()
    with contextlib.ExitStack() as st:
        ASZ = 207 * 1024
        ar = st.enter_context(nc.sbuf_tensor("arena", [128, ASZ], U8))
        A = Arena(ar, ASZ)
        pbig = st.enter_context(nc.psum_tensor("pbig", [128, 8, 512], F32))
        sems = make_sems(nc, st)
        block = st.enter_context(nc.Block())
        S = Sched(nc)
        tail_body(nc, S, A, pbig, x, mix, w_out, w_gate, w_up, w_down, nw, ident, out, h_d, dummy)
        if os.environ.get('NO_REORDER') is None:
            S.reorder()
        S.emit(sems, block)
    return nc


def rms_rstd(S, src, n, ss, sq, tag, rd, wr_extra=(), c=None):
    S.act(lambda e: e.activation(sq, src, AF.Square, accum_out=ss), reads=rd, writes=[tag + "ss", tag + "sq"], c=c)
    S.act(lambda e: e.activation(ss, ss, AF.Sqrt, scale=1.0 / n, bias=EPS_AP[0]), reads=[tag + "ss"], writes=[tag + "ss"])
    S.dve(lambda e: e.reciprocal(ss, ss), reads=[tag + "ss"], writes=[tag + "ss"])


EPS_AP = [None]


def tail_body(nc, S, A, pbig, x, mix, w_out, w_gate, w_up, w_down, nw, ident, out, h_d, dummy):
    bk = {"ptr": (4, 5), "ptrv": (6, 7)}
    for i_ in range(8):
        bk["pacc%d" % i_] = (i_,)
        bk["pd%d" % i_] = (i_,)
    for i_ in range(2):
        bk["pg%d" % i_] = (i_ * 4, i_ * 4 + 1)
        bk["pu%d" % i_] = (i_ * 4 + 2, i_ * 4 + 3)
    S.bank_of = bk
    identb = A.alloc([128], BF16)
    nwb_flat = A.alloc([2 * D + SSDW], F32)
    epsb = A.alloc([1], F32)
    ss = A.alloc([4], F32)
    EPS_AP[0] = epsb
    S.pool(lambda e: e.memset(epsb, EPS), writes=["eps"])
    S.dma("pool", identb, ident, writes=["ident"])
    S.dma("sp", nwb_flat, nw.rearrange("a b -> (a b)")[0:2 * D + SSDW].partition_broadcast(128), writes=["nwb"])

    class _NW:
        def __getitem__(self, key):
            _, row, cols = key
            base = {1: 0, 2: D, 0: 2 * D}[row]
            lo = cols.start or 0
            hi = cols.stop if cols.stop is not None else (SSDW if row == 0 else D)
            return nwb_flat[:, base + lo:base + hi]
    nwb = _NW()
    vT = A.alloc([KC, TOK], BF16)
    base_persist = A.off

    wo = A.alloc([KC, D], BF16)
    for cb in range(4):
        S.dma("pool", wo[:, :, cb * 512:(cb + 1) * 512],
              w_out[:, cb * 512:(cb + 1) * 512].rearrange("(k p) n -> p k n", p=128), writes=["wo%d" % cb], c=25.0)
    xt = [A.alloc([D], F32) for _ in range(2)]
    mt = [A.alloc([D], F32) for _ in range(2)]
    sq = A.alloc([D], BF16)
    mb2 = [A.alloc([D], BF16) for _ in range(2)]
    mT2 = [A.alloc([KC, 128], BF16) for _ in range(2)]
    hs = [A.alloc([D], F32) for _ in range(2)]
    vb = A.alloc([D], BF16)
    WB = 256
    pre_lo = (A.off + 63) // 64 * 64
    wg_pre = A.alloc([KC, WB], BF16)
    wu_pre = A.alloc([KC, WB], BF16)
    S.dma("pool", wg_pre, w_gate[:, 0:WB].rearrange("(k p) n -> p k n", p=128), writes=["wg0"], c=15.0)
    S.dma("pool", wu_pre, w_up[:, 0:WB].rearrange("(k p) n -> p k n", p=128), writes=["wu0"], c=15.0)
    pacc = pbig[:, 0:4, :]
    ptr_all = pbig[:, 4:6, :].rearrange("p a b -> p (a b)").bitcast(BF16)
    ptr = ptr_all.rearrange("p (k n) -> p k n", k=KC)

    def loads(tt):
        b = tt % 2
        S.dma("sp", xt[b], x[tt * 128:(tt + 1) * 128, :], writes=["xt%d" % b])
        S.dma("sp", mt[b], mix[tt * 128:(tt + 1) * 128, :], writes=["mt%d" % b])

    loads(0)
    ptr2_all = pbig[:, 6:8, :].rearrange("p a b -> p (a b)").bitcast(BF16)
    ptr2 = ptr2_all.rearrange("p (k n) -> p k n", k=KC)
    for tt in range(NT):
        b = tt % 2
        mb = mb2[b]
        mT = mT2[b]
        if tt + 1 < NT:
            loads(tt + 1)
        rms_rstd(S, mt[b][:, 0:SSDW], SSDW, ss[:, 0:1], sq[:, 0:SSDW], "a", ["mt%d" % b, "eps"])
        S.dve(lambda e, b=b, mb=mb: e.scalar_tensor_tensor(mb[:, 0:SSDW], mt[b][:, 0:SSDW], ss[:, 0:1], nwb[:, 0, 0:SSDW], ALU.mult, ALU.mult),
              reads=["mt%d" % b, "ass", "nwb"], writes=["mb0_%d" % b])
        S.act(lambda e, b=b, mb=mb: e.copy(mb[:, SSDW:D], mt[b][:, SSDW:D]), reads=["mt%d" % b], writes=["mb1_%d" % b], c=0.9)
        for kc in range(KC):
            S.pe(lambda e, kc=kc, mb=mb: e.transpose(ptr[:, kc, :], mb[:, kc * 128:(kc + 1) * 128], identb),
                 reads=["mb0_%d" % b, "mb1_%d" % b, "ident"], writes=["ptr"])
        S.act(lambda e, mT=mT: e.copy(mT[:, 0:8, :], ptr[:, 0:8, :]), reads=["ptr"], writes=["mTa%d" % b], c=0.6)
        S.dve(lambda e, mT=mT: e.tensor_copy(mT[:, 8:16, :], ptr[:, 8:16, :]), reads=["ptr"], writes=["mTb%d" % b], c=0.5)
        for cb in range(4):
            for kc in range(KC):
                S.pe(lambda e, cb=cb, kc=kc, mT=mT: e.matmul(pacc[:, cb, :], mT[:, kc, :], wo[:, kc, cb * 512:(cb + 1) * 512],
                                                              start=(kc == 0), stop=(kc == KC - 1)),
                     reads=["mTa%d" % b, "mTb%d" % b, "wo%d" % cb], writes=["pacc%d" % cb], c=0.22)
            S.dve(lambda e, cb=cb, b=b: e.tensor_tensor(hs[b][:, cb * 512:(cb + 1) * 512], pacc[:, cb, :], xt[b][:, cb * 512:(cb + 1) * 512], ALU.add),
                  reads=["pacc%d" % cb, "xt%d" % b], writes=["hs%d_%d" % (b, cb)])
        hres = ["hs%d_%d" % (b, cb) for cb in range(4)]
        S.dma("sp", h_d[tt * 128:(tt + 1) * 128, :], hs[b], reads=hres, writes=["h_d%d" % tt])
        rms_rstd(S, hs[b], D, ss[:, 1:2], sq, "b", hres + ["eps"])
        S.dve(lambda e, b=b: e.scalar_tensor_tensor(vb, hs[b], ss[:, 1:2], nwb[:, 1, :], ALU.mult, ALU.mult),
              reads=hres + ["bss", "nwb"], writes=["vb"])
        for kc in range(KC):
            S.pe(lambda e, kc=kc: e.transpose(ptr2[:, kc, :], vb[:, kc * 128:(kc + 1) * 128], identb),
                 reads=["vb", "ident"], writes=["ptrv"])
        S.act(lambda e, tt=tt: e.copy(vT[:, 0:8, tt * 128:(tt + 1) * 128], ptr2[:, 0:8, :]), reads=["ptrv"], writes=["vT%da" % tt], c=0.6)
        S.dve(lambda e, tt=tt: e.tensor_copy(vT[:, 8:16, tt * 128:(tt + 1) * 128], ptr2[:, 8:16, :]), reads=["ptrv"], writes=["vT%db" % tt], c=0.5)

    S.barrier(dummy[1:2, :], ident[0:1, 0:64])
    A.reset(base_persist)
    hT = A.alloc([FC, TOK], BF16)
    HK = 22
    wd_pre = A.alloc([HK, 512], BF16)
    base_b = A.off
    NB = FF // WB
    wg = [wg_pre, A.alloc([KC, WB], BF16)]
    wu = [wu_pre, A.alloc([KC, WB], BF16)]
    sg = [A.alloc([TOK], BF16) for _ in range(2)]
    assert A.off <= pre_lo, (A.off, pre_lo)
    for blk in range(NB):
        b = blk % 2
        if blk > 0:
            S.dma("pool", wg[b], w_gate[:, blk * WB:(blk + 1) * WB].rearrange("(k p) n -> p k n", p=128), writes=["wg%d" % b], c=15.0)
            S.dma("pool", wu[b], w_up[:, blk * WB:(blk + 1) * WB].rearrange("(k p) n -> p k n", p=128), writes=["wu%d" % b], c=15.0)
        if blk == NB - 2:
            S.dma("pool", wd_pre, w_down[0:HK * 128, 0:512].rearrange("(k p) n -> p k n", p=128), writes=["wd0"], c=15.0)
        for j in range(WB // 128):
            fc = blk * (WB // 128) + j
            pb = fc % 2
            pg = pbig[:, pb * 4:pb * 4 + 2, :]
            pu = pbig[:, pb * 4 + 2:pb * 4 + 4, :]
            for hf in range(2):
                for kc in range(KC):
                    S.pe(lambda e, b=b, j=j, hf=hf, kc=kc, pg=pg: e.matmul(pg[:, hf, :], wg[b][:, kc, j * 128:(j + 1) * 128], vT[:, kc, hf * 512:(hf + 1) * 512],
                                                                         start=(kc == 0), stop=(kc == KC - 1)),
                         reads=["wg%d" % b, "vT"], writes=["pg%d" % pb], c=0.22)
            for hf in range(2):
                for kc in range(KC):
                    S.pe(lambda e, b=b, j=j, hf=hf, kc=kc, pu=pu: e.matmul(pu[:, hf, :], wu[b][:, kc, j * 128:(j + 1) * 128], vT[:, kc, hf * 512:(hf + 1) * 512],
                                                                         start=(kc == 0), stop=(kc == KC - 1)),
                         reads=["wu%d" % b, "vT"], writes=["pu%d" % pb], c=0.22)
            S.act(lambda e, pb=pb, pg=pg: e.activation(sg[pb], pg.rearrange("p a b -> p (a b)"), AF.Silu), reads=["pg%d" % pb], writes=["sg%d" % pb])
            S.dve(lambda e, pb=pb, pu=pu, fc=fc: e.tensor_tensor(hT[:, fc, :], sg[pb], pu.rearrange("p a b -> p (a b)"), ALU.mult),
                  reads=["sg%d" % pb, "pu%d" % pb], writes=["hT%d" % fc])

    S.barrier(dummy[1:2, :], ident[0:1, 0:64])
    A.reset(base_b)
    wd = [wd_pre, A.alloc([HK, 512], BF16)]
    hl = [A.alloc([512], F32) for _ in range(2)]
    ys = [A.alloc([512], F32) for _ in range(2)]
    it = 0
    for r in range(4):
        for hf in range(2):
            b = (r * 2 + hf) % 2
            if r * 2 + hf > 0:
                S.dma("pool", wd[b], w_down[hf * HK * 128:(hf + 1) * HK * 128, r * 512:(r + 1) * 512].rearrange("(k p) n -> p k n", p=128), writes=["wd%d" % b], c=15.0)
            for tt in range(NT):
                for k in range(HK):
                    kk = hf * HK + k
                    S.pe(lambda e, b=b, tt=tt, k=k, kk=kk: e.matmul(pbig[:, tt, :], hT[:, kk, tt * 128:(tt + 1) * 128], wd[b][:, k, :],
                                                                    start=(kk == 0), stop=(kk == FC - 1)),
                         reads=["wd%d" % b, "hT"], writes=["pd%d" % tt], c=0.22)
        for tt in range(NT):
            b = it % 2
            it += 1
            S.dma("sp", hl[b], h_d[tt * 128:(tt + 1) * 128, r * 512:(r + 1) * 512], reads=["h_d%d_%d" % (tt, r)], writes=["hl%d" % b])
            S.dve(lambda e, b=b, tt=tt: e.tensor_tensor(ys[b], pbig[:, tt, :], hl[b], ALU.add), reads=["pd%d" % tt, "hl%d" % b], writes=["ys%d" % b])
            S.dma("sp", h_d[tt * 128:(tt + 1) * 128, r * 512:(r + 1) * 512], ys[b], reads=["ys%d" % b], writes=["h_d%d_%d" % (tt, r)])

    S.barrier(dummy[1:2, :], ident[0:1, 0:64])
    A.reset(base_persist)
    yt = [A.alloc([D], F32) for _ in range(2)]
    ot = [A.alloc([D], F32) for _ in range(2)]
    sq2 = A.alloc([D], F32)
    for tt in range(NT):
        b = tt % 2
        S.dma("sp", yt[b], h_d[tt * 128:(tt + 1) * 128, :], writes=["yt%d" % b])
        rms_rstd(S, yt[b], D, ss[:, 2:3], sq2, "c", ["yt%d" % b, "eps"])
        S.dve(lambda e, b=b: e.scalar_tensor_tensor(ot[b], yt[b], ss[:, 2:3], nwb[:, 2, :], ALU.mult, ALU.mult),
              reads=["yt%d" % b, "css", "nwb"], writes=["ot%d" % b])
        S.dma("sp", out[tt * 128:(tt + 1) * 128, :], ot[b], reads=["ot%d" % b])


_CACHE = {}


def run_tail(x, mixed, ssd_norm_w, w_out, ffn_norm_w, w_gate, w_up, w_down, final_norm_w):
    if "tail" not in _CACHE:
        _CACHE["tail"] = build_tail()
    nc = _CACHE["tail"]
    nwv = np.ones((3, D), np.float32)
    nwv[0] = ffn_norm_w
    nwv[1] = final_norm_w
    nwv[2, :SSDW] = ssd_norm_w
    ident = np.eye(128, dtype=np.float32)
    in_maps = []
    for c in range(8):
        b, g = c // 4, c % 4
        in_maps.append({
            "x_own": np.ascontiguousarray(x[b, g * TOK:(g + 1) * TOK]),
            "mix": np.ascontiguousarray(mixed[b, g * TOK:(g + 1) * TOK]),
            "w_out": w_out, "w_gate": w_gate, "w_up": w_up, "w_down": w_down,
            "nw": nwv, "ident": ident,
        })
    res = run_bass_kernel_spmd(nc, in_maps, core_ids=list(range(8)))
    outp = np.empty((2, 4096, D), np.float32)
    for c in range(8):
        b, g = c // 4, c % 4
        outp[b, g * TOK:(g + 1) * TOK] = res.results[c]["out"]
    return outp


SEQ = 4096
NTT = SEQ // 128
NEG = -30000.0
NA = 400
NB_ = 384
NFM = 640
SCALE = 0.125


def build_mix():
    nc = bass.Bass("TRN2", target_bir_lowering=False)
    di = lambda n, s, d=F32: nc.dram_tensor(n, s, d, kind="ExternalInput").ap()
    T = dict(
        xb=di("xb", [SEQ, D]), anw=di("anw", [D]), w_tm=di("w_tm", [D, NA + NB_]), w_fm=di("w_fm", [D, NFM]),
        convw=di("convw", [128, 16]), convb=di("convb", [128, 4]), hp=di("hp", [12]), rope=di("rope", [128, NTT * 16]),
        w1k=di("w1k", [64, 32 * 256]), w1v=di("w1v", [64, 32 * 256]), w2k=di("w2k", [256, 64]), w2v=di("w2v", [256, 64]),
        pek=di("pek", [64, 32]), pev=di("pev", [64, 32]), ident=di("ident", [128, 128]), emat=di("emat", [64, SEQ]),
        tric=di("tric", [128, 128]), tria=di("tria", [128, 128]), cmask=di("cmask", [256, SEQ]), ovl=di("ovl", [256, 64]),
        utri=di("utri", [128, 128]),
    )
    T["mixo"] = nc.dram_tensor("mixo", [SEQ, 512], F32, kind="ExternalOutput").ap()
    T["qr_d"] = nc.dram_tensor("qr_d", [64, 4, SEQ], BF16, kind="Internal").ap()
    T["qu_d"] = nc.dram_tensor("qu_d", [64, 4, SEQ], BF16, kind="Internal").ap()
    T["dummy"] = nc.dram_tensor("dummy_bar", [2, 64], F32, kind="Internal").ap()
    with contextlib.ExitStack() as st:
        ASZ = 207 * 1024
        ar = st.enter_context(nc.sbuf_tensor("arena", [128, ASZ], U8))
        A = Arena(ar, ASZ)
        pbig = st.enter_context(nc.psum_tensor("pbig", [128, 8, 512], F32))
        sems = make_sems(nc, st)
        block = st.enter_context(nc.Block())
        S = Sched(nc)
        mix_body(nc, S, A, pbig, T)
        if os.environ.get('NO_REORDER') is None:
            S.reorder()
        S.emit(sems, block)
    return nc


def pbf(pb, lo, hi):
    return pb[:, lo:hi, :].rearrange("p a b -> p (a b)").bitcast(BF16)


def mix_body(nc, S, A, pbig, T):
    bar = lambda: S.barrier(T["dummy"][1:2, :], T["ident"][0:1, 0:64])
    bk = {"ptr": (0,), "ptr2": (6,), "psA": (1,), "psB": (2,), "psF0": (3,), "psR": (4,), "psS_a": (5,), "psS_c": (5,),
          "psN": (6,), "psY_o": (7,), "psY_d": (7,), "pk": (5,), "pv0": (6,), "pv1": (7,), "pnt": (7,), "posel": (6,), "powin": (7,)}
    for i_ in range(4):
        bk["pbias%d" % i_] = (4,)
        bk["phid%d" % i_] = (i_,)
        bk["psw%d" % i_] = (3 + i_ % 2,)
    for a_ in range(2):
        bk["po%d" % a_] = (4 + a_,)
        for b_ in range(2):
            bk["psc%d_%d" % (a_, b_)] = (a_ * 2 + b_,)
    for i_ in range(3):
        bk["pss%d" % i_] = (i_,)
    bk["poselT"] = (5,)
    S.bank_of = bk
    identb = A.alloc([128], BF16)
    utri = A.alloc([128], F32)
    identf = A.alloc([128], F32)
    tricb = A.alloc([128], BF16)
    triab = A.alloc([128], BF16)
    epsb = A.alloc([1], F32)
    oneb = A.alloc([1], F32)
    EPS_AP[0] = epsb
    KsA = A.alloc([SEQ], BF16)
    KwT = A.alloc([SEQ], BF16)
    kcvcT = A.alloc([SEQ], BF16)
    VsA = A.alloc([NTT, 65], BF16)
    VwA = A.alloc([NTT, 65], BF16)
    gate = A.alloc([NTT, 12], F32)
    hpb = A.alloc([12], F32)
    base_persist = A.off
    S.pool(lambda e: e.memset(epsb, EPS), writes=["eps"])
    S.pool(lambda e: e.memset(oneb, 1.0), writes=["one"])
    S.pool(lambda e: e.memset(VsA, 1.0), writes=["VsA"])
    S.pool(lambda e: e.memset(VwA, 1.0), writes=["VwA"])
    S.dma("pool", identb, T["ident"], writes=["ident"])
    S.dma("sp", utri, T["utri"], writes=["utri"])
    S.dma("sp", identf, T["ident"], writes=["identf"])
    S.dma("pool", tricb, T["tric"], writes=["tric"])
    S.dma("pool", triab, T["tria"], writes=["tria"])
    S.dma("pool", KsA[64:128, :], T["emat"], writes=["KsE"])
    S.dma("sp", hpb, T["hp"].partition_broadcast(128), writes=["hpb"])

    wtm = A.alloc([KC, NA + NB_], BF16)
    wfm = A.alloc([KC, NFM], BF16)
    S.dma("pool", wfm, T["w_fm"].rearrange("(k p) n -> p k n", p=128), writes=["wfm"], c=30.0)
    S.dma("pool", wtm, T["w_tm"].rearrange("(k p) n -> p k n", p=128), writes=["wtm"], c=35.0)
    anwb = A.alloc([D], F32)
    S.dma("sp", anwb, T["anw"].partition_broadcast(128), writes=["anwb"])
    ropet = A.alloc([NTT, 16], F32)
    S.dma("sp", ropet, T["rope"].rearrange("p (t c) -> p t c", c=16), writes=["ropet"])
    convw = A.alloc([16], F32)
    convb = A.alloc([4], F32)
    S.dma("sp", convw, T["convw"], writes=["convw"])
    S.dma("sp", convb, T["convb"], writes=["convb"])
    onesf = A.alloc([128], F32)
    S.pool(lambda e: e.memset(onesf, 1.0), writes=["onesf"])
    two = lambda shape, dt: [A.alloc(shape, dt) for _ in range(2)]
    xt = two([D], F32)
    sq1 = A.alloc([D], BF16)
    sq = [sq1, sq1]
    ss = A.alloc([8], F32)
    ub = two([D], BF16)
    uT2 = two([KC, 512], BF16)
    cbuf = A.alloc([4, 515], F32)
    cacc_1 = A.alloc([512], F32)
    cacc = [cacc_1, cacc_1]
    xbcT = two([4, 512], BF16)
    zs = two([256], BF16)
    ez_1 = A.alloc([256], F32)
    ez = [ez_1, ez_1]
    ec = two([512], F32)
    dtt = two([4], F32)
    qk = two([6, 64], F32)
    qkr = two([6, 64], BF16)
    qkb = two([4, 64], BF16)
    rt = two([4, 6, 8], F32)
    qst = two([4, 128], BF16)
    qut = two([4, 128], BF16)
    xtm = two([256], BF16)
    btm = two([128], BF16)
    hst = A.alloc([256], F32)
    hstb = two([256], BF16)
    aneg = A.alloc([4], F32)
    adt = two([4], F32)
    acol = two([4], F32)
    nacol = two([4], F32)
    eac = two([4], F32)
    rhs4_1 = A.alloc([4, 128], F32)
    rhs4 = [rhs4_1, rhs4_1]
    seg4_1 = A.alloc([4, 128], F32)
    seg4 = [seg4_1, seg4_1]
    cbm = two([128], F32)
    MT = two([4, 128], BF16)
    alast = two([4], F32)
    dsv = two([4], F32)
    cdv = two([4], F32)
    wsc = two([4], F32)
    xw = two([256], BF16)
    xdt = two([256], BF16)
    ydg_1 = A.alloc([256], F32)
    ydg = [ydg_1, ydg_1]
    yy_1 = A.alloc([256], F32)
    yy = [yy_1, yy_1]
    tmpd_1 = A.alloc([256], F32)
    tmpd = [tmpd_1, tmpd_1]
    yo = two([256], F32)
    S.pool(lambda e: e.memset(cbuf, 0.0), writes=["cbuf", "cbuf0", "cbuf1", "cbuf2", "cbuf3"])
    S.pool(lambda e: e.memset(hst, 0.0), writes=["hst"])
    S.act(lambda e: e.activation(aneg, hpb[:, 4:8], AF.Exp), reads=["hpb"], writes=["aneg"])
    S.dve(lambda e: e.tensor_scalar(aneg, aneg, -1.0, None, ALU.mult), reads=["aneg"], writes=["aneg"])

    ptr = pbf(pbig, 0, 1)
    psA = pbig[:, 1, 0:NA]
    psB = pbig[:, 2, 0:NB_]
    psF = pbig[:, 3, :]
    psR = pbig[:, 4, :]
    psS = pbig[:, 5, :]
    psN = pbig[:, 6, 0:256]
    psY = pbig[:, 7, :]

    def load_x(tt):
        S.dma("sp", xt[tt % 2], T["xb"][tt * 128:(tt + 1) * 128, :], writes=["xt%d" % (tt % 2)], c=5.0)

    def b4(ap, n):
        return ap.unsqueeze(2).to_broadcast([128, 4, n])

    load_x(0)
    for G in range(SEQ // 512):
        gp = G % 2
        uT = uT2[gp]
        for j in range(4):
            tt = G * 4 + j
            b = tt % 2
            if tt + 1 < NTT:
                load_x(tt + 1)
            ssb = ss[:, b:b + 1]
            S.act(lambda e, b=b, ssb=ssb: e.activation(sq[b], xt[b], AF.Square, accum_out=ssb), reads=["xt%d" % b], writes=["n%dss" % b, "nsq"], c=1.9)
            S.act(lambda e, ssb=ssb: e.activation(ssb, ssb, AF.Ln, scale=1.0 / D, bias=epsb), reads=["n%dss" % b, "eps"], writes=["n%dss" % b])
            S.act(lambda e, ssb=ssb: e.activation(ssb, ssb, AF.Exp, scale=-0.5), reads=["n%dss" % b], writes=["n%dss" % b])
            S.dve(lambda e, b=b: e.scalar_tensor_tensor(ub[b], xt[b], ss[:, b:b + 1], anwb, ALU.mult, ALU.mult),
                  reads=["xt%d" % b, "n%dss" % b, "anwb"], writes=["ub%d" % b], c=2.2)
            for half in range(2):
                for k8 in range(8):
                    kc = half * 8 + k8
                    S.pe(lambda e, kc=kc, k8=k8, b=b: e.transpose(ptr[:, k8 * 128:(k8 + 1) * 128], ub[b][:, kc * 128:(kc + 1) * 128], identb),
                         reads=["ub%d" % b, "ident"], writes=["ptr"])
                if half == 0:
                    S.act(lambda e, j=j, uT=uT: e.copy(uT[:, 0:8, j * 128:(j + 1) * 128], ptr.rearrange("p (k n) -> p k n", k=8)), reads=["ptr"], writes=["uT%d_%d_0" % (gp, j)], c=0.9)
                else:
                    S.dve(lambda e, j=j, uT=uT: e.tensor_copy(uT[:, 8:16, j * 128:(j + 1) * 128], ptr.rearrange("p (k n) -> p k n", k=8)), reads=["ptr"], writes=["uT%d_%d_1" % (gp, j)], c=0.7)
        uTr = ["uT%d_%d_%d" % (gp, j, h) for j in range(4) for h in range(2)]
        for c in range(5):
            for kc in range(KC):
                S.pe(lambda e, c=c, kc=kc, uT=uT: e.matmul(psF, wfm[:, kc, c * 128:(c + 1) * 128], uT[:, kc, :], start=(kc == 0), stop=(kc == KC - 1)),
                     reads=uTr + ["wfm"], writes=["psF0"], c=0.22)
            if c < 4:
                ca = cacc[c % 2]
                car = "cacc"
                S.act(lambda e, c=c: e.copy(cbuf[:, c, 3:515], psF), reads=["psF0"], writes=["cbuf%d" % c], c=0.6)
                S.dve(lambda e, c=c, ca=ca: e.tensor_scalar(ca, cbuf[:, c, 0:512], convw[:, c * 4:c * 4 + 1], convb[:, c:c + 1], ALU.mult, ALU.add), reads=["cbuf%d" % c, "convw", "convb"], writes=[car], c=0.4)
                for k in range(1, 4):
                    S.dve(lambda e, c=c, k=k, ca=ca: e.scalar_tensor_tensor(ca, cbuf[:, c, k:k + 512], convw[:, c * 4 + k:c * 4 + k + 1], ca, ALU.mult, ALU.add),
                          reads=["cbuf%d" % c, car, "convw"], writes=[car], c=0.65)
                ece = ec[c % 2]
                ecr = "ec%d" % (c % 2)
                S.act(lambda e, ca=ca, ece=ece: e.activation(ece, ca, AF.Exp, scale=-1.0), reads=[car], writes=[ecr], c=0.6)
                S.act(lambda e, ece=ece: e.activation(ece, ece, AF.Ln, bias=oneb), reads=[ecr, "one"], writes=[ecr], c=0.6)
                S.act(lambda e, ece=ece: e.activation(ece, ece, AF.Exp, scale=-1.0), reads=[ecr], writes=[ecr], c=0.6)
                S.pool(lambda e, c=c, ca=ca, ece=ece, gp=gp: e.tensor_tensor(xbcT[gp][:, c, :], ca, ece, ALU.mult), reads=[car, ecr], writes=["xbcT%d_%d" % (gp, c)], c=2.0)
                S.pool(lambda e, c=c: e.tensor_copy(cbuf[:, c, 0:3], cbuf[:, c, 512:515]), reads=["cbuf%d" % c], writes=["cbuf%d" % c])
            else:
                S.act(lambda e, G=G: e.copy(kcvcT[:, G * 512:(G + 1) * 512], psF), reads=["psF0"], writes=["kcvcT"], c=0.6)
        xr = lambda c: "xbcT%d_%d" % (gp, c)
        for j in range(4):
            tt = G * 4 + j
            p = tt % 2
            P = str(p)
            tok = slice(tt * 128, (tt + 1) * 128)
            js = slice(j * 128, (j + 1) * 128)
            for kc in range(KC):
                S.pe(lambda e, js=js, kc=kc, uT=uT: e.matmul(psA, uT[:, kc, js], wtm[:, kc, 0:NA], start=(kc == 0), stop=(kc == KC - 1)),
                     reads=uTr + ["wtm"], writes=["psA"], c=0.19)
            for kc in range(KC):
                S.pe(lambda e, js=js, kc=kc, uT=uT: e.matmul(psB, uT[:, kc, js], wtm[:, kc, NA:NA + NB_], start=(kc == 0), stop=(kc == KC - 1)),
                     reads=uTr + ["wtm"], writes=["psB"], c=0.18)
            S.act(lambda e, p=p: e.activation(ez[p], psA[:, 0:256], AF.Exp, scale=-1.0), reads=["psA"], writes=["ez"])
            S.act(lambda e, p=p: e.activation(ez[p], ez[p], AF.Ln, bias=oneb), reads=["ez", "one"], writes=["ez"])
            S.act(lambda e, p=p: e.activation(ez[p], ez[p], AF.Exp, scale=-1.0), reads=["ez"], writes=["ez"])
            S.dve(lambda e, p=p: e.tensor_tensor(zs[p], psA[:, 0:256], ez[p], ALU.mult), reads=["psA", "ez"], writes=["zs" + P])
            S.dve(lambda e, p=p: e.tensor_tensor(dtt[p], psA[:, 256:260], hpb[:, 0:4], ALU.add), reads=["psA", "hpb"], writes=["dtt" + P])
            S.act(lambda e, p=p: e.activation(dtt[p], dtt[p], AF.Exp), reads=["dtt" + P], writes=["dtt" + P])
            S.act(lambda e, p=p: e.activation(dtt[p], dtt[p], AF.Ln, bias=oneb), reads=["dtt" + P, "one"], writes=["dtt" + P])
            S.act(lambda e, tt=tt: e.activation(gate[:, tt, :], psA[:, 260:272], AF.Exp, scale=-1.0), reads=["psA"], writes=["gate%d" % tt])
            S.dve(lambda e, tt=tt: e.tensor_scalar(gate[:, tt, :], gate[:, tt, :], 1.0, None, ALU.add), reads=["gate%d" % tt], writes=["gate%d" % tt])
            S.dve(lambda e, tt=tt: e.reciprocal(gate[:, tt, :], gate[:, tt, :]), reads=["gate%d" % tt], writes=["gate%d" % tt])
            S.dve(lambda e, tt=tt: e.tensor_copy(VsA[:, tt, 0:64], psA[:, 272:336]), reads=["psA", "VsA"], writes=["VsA%d" % tt])
            S.dve(lambda e, tt=tt: e.tensor_copy(VwA[:, tt, 0:64], psA[:, 336:400]), reads=["psA", "VwA"], writes=["VwA%d" % tt])
            S.act(lambda e, p=p: e.copy(qk[p], psB.rearrange("p (a b) -> p a b", a=6)), reads=["psB"], writes=["qk" + P], c=0.5)
            S.act(lambda e, p=p: e.copy(qkb[p], qk[p][:, 0:4, :]), reads=["qk" + P], writes=["qkb" + P])
            S.act(lambda e, p=p: e.copy(qkr[p], qk[p]), reads=["qk" + P], writes=["qkr" + P], c=0.45)
            cosb = ropet[:, tt, 0:8].unsqueeze(1).to_broadcast([128, 6, 8])
            sinb = ropet[:, tt, 8:16].unsqueeze(1).to_broadcast([128, 6, 8])
            S.dve(lambda e, cosb=cosb, p=p: e.tensor_tensor(rt[p][:, 0], qk[p][:, :, 0:8], cosb, ALU.mult), reads=["qk" + P, "ropet"], writes=["rt0" + P])
            S.dve(lambda e, sinb=sinb, p=p: e.tensor_tensor(rt[p][:, 1], qk[p][:, :, 8:16], sinb, ALU.mult), reads=["qk" + P, "ropet"], writes=["rt1" + P])
            S.dve(lambda e, cosb=cosb, p=p: e.tensor_tensor(rt[p][:, 2], qk[p][:, :, 8:16], cosb, ALU.mult), reads=["qk" + P, "ropet"], writes=["rt2" + P])
            S.dve(lambda e, sinb=sinb, p=p: e.tensor_tensor(rt[p][:, 3], qk[p][:, :, 0:8], sinb, ALU.mult), reads=["qk" + P, "ropet"], writes=["rt3" + P])
            S.dve(lambda e, p=p: e.tensor_tensor(qkr[p][:, :, 0:8], rt[p][:, 0], rt[p][:, 1], ALU.subtract), reads=["rt0" + P, "rt1" + P, "qkr" + P], writes=["qkr" + P])
            S.dve(lambda e, p=p: e.tensor_tensor(qkr[p][:, :, 8:16], rt[p][:, 2], rt[p][:, 3], ALU.add), reads=["rt2" + P, "rt3" + P, "qkr" + P], writes=["qkr" + P])
            ptq = ptr[0:64, 0:768].rearrange("p (a b) -> p a b", a=6)
            ptu = pbf(pbig, 6, 7)[0:64, 512:1024].rearrange("p (a b) -> p a b", a=4)
            for a in range(6):
                S.pe(lambda e, a=a, p=p: e.transpose(ptq[:, a, :], qkr[p][:, a, :], identb), reads=["qkr" + P, "ident"], writes=["ptr"])
            for a in range(4):
                S.pe(lambda e, a=a, p=p: e.transpose(ptu[:, a, :], qkb[p][:, a, :], identb), reads=["qkb" + P, "ident"], writes=["ptr2"])
            S.act(lambda e, p=p: e.copy(qst[p][0:64], ptq[:, 0:4, :]), reads=["ptr"], writes=["qst" + P])
            S.dve(lambda e, p=p: e.tensor_copy(qut[p][0:64], ptu), reads=["ptr2"], writes=["qut" + P])
            S.act(lambda e, tok=tok: e.copy(KsA[0:64, tok], ptq[:, 4, :]), reads=["ptr"], writes=["KsA%d" % tt])
            S.dve(lambda e, tok=tok: e.tensor_copy(KwT[0:64, tok], ptq[:, 5, :]), reads=["ptr"], writes=["KwT%d" % tt])
            S.dma("sp", T["qr_d"][:, :, tok], qst[p][0:64], reads=["qst" + P], writes=["qr_d%d" % tt])
            S.dma("sp", T["qu_d"][:, :, tok], qut[p][0:64], reads=["qut" + P], writes=["qu_d%d" % tt])
            ptx = ptr[:, 0:384]
            for c in range(3):
                S.pe(lambda e, c=c, js=js, gp=gp: e.transpose(ptx[:, c * 128:(c + 1) * 128], xbcT[gp][:, c, js], identb),
                     reads=[xr(c), "ident"], writes=["ptr"])
            S.act(lambda e, p=p: e.copy(xtm[p], ptx[:, 0:256]), reads=["ptr"], writes=["xtm" + P])
            S.act(lambda e, p=p: e.copy(btm[p], ptx[:, 256:384]), reads=["ptr"], writes=["btm" + P])
            S.dve(lambda e, p=p: e.tensor_tensor(adt[p], dtt[p], aneg, ALU.mult), reads=["dtt" + P, "aneg"], writes=["adt" + P])
            S.pe(lambda e, p=p: e.matmul(psS[:, 0:4], utri, adt[p], start=True, stop=True), reads=["utri", "adt" + P], writes=["psS_a"])
            S.act(lambda e, p=p: e.copy(acol[p], psS[:, 0:4]), reads=["psS_a"], writes=["acol" + P])
            S.dve(lambda e, p=p: e.tensor_scalar(nacol[p], psS[:, 0:4], -1.0, None, ALU.mult), reads=["psS_a"], writes=["nacol" + P])
            S.act(lambda e, p=p: e.activation(eac[p], acol[p], AF.Exp), reads=["acol" + P], writes=["eac" + P])
            S.pe(lambda e, js=js, gp=gp: e.matmul(psS[:, 256:384], xbcT[gp][:, 2, js], xbcT[gp][:, 3, js], start=True, stop=True),
                 reads=[xr(2), xr(3)], writes=["psS_c"])
            S.dve(lambda e, p=p: e.tensor_tensor(cbm[p], psS[:, 256:384], utri, ALU.mult), reads=["psS_c", "utri"], writes=["cbm" + P])
            S.dve(lambda e, p=p: e.tensor_tensor(rhs4[p], utri.unsqueeze(1).to_broadcast([128, 4, 128]), b4(adt[p], 128), ALU.mult),
                  reads=["utri", "adt" + P], writes=["rhs4"], c=0.6)
            S.pe(lambda e, p=p: e.matmul(psR, onesf, rhs4[p].rearrange("p a b -> p (a b)"), start=True, stop=True), reads=["onesf", "rhs4"], writes=["psR"], c=0.9)
            psR4 = psR.rearrange("p (a b) -> p a b", a=4)
            S.dve(lambda e, p=p: e.tensor_tensor(seg4[p], psR4, b4(acol[p], 128), ALU.subtract), reads=["psR", "acol" + P], writes=["seg4"], c=0.7)
            S.dve(lambda e, p=p: e.tensor_scalar(seg4[p], seg4[p], 0.0, None, ALU.min), reads=["seg4"], writes=["seg4"], c=0.35)
            S.act(lambda e, p=p: e.activation(seg4[p], seg4[p], AF.Exp), reads=["seg4"], writes=["seg4"], c=0.6)
            S.dve(lambda e, p=p: e.tensor_tensor(MT[p], seg4[p], cbm[p].unsqueeze(1).to_broadcast([128, 4, 128]), ALU.mult),
                  reads=["seg4", "cbm" + P], writes=["MT" + P], c=0.6)
            S.dve(lambda e, p=p: e.tensor_copy(alast[p], psR4[:, :, 127]), reads=["psR"], writes=["alast" + P])
            S.dve(lambda e, p=p: e.tensor_tensor(dsv[p], nacol[p], alast[p], ALU.add), reads=["nacol" + P, "alast" + P], writes=["dsv" + P])
            S.act(lambda e, p=p: e.activation(dsv[p], dsv[p], AF.Exp), reads=["dsv" + P], writes=["dsv" + P])
            S.act(lambda e, p=p: e.activation(cdv[p], alast[p], AF.Exp), reads=["alast" + P], writes=["cdv" + P])
            S.dve(lambda e, p=p: e.tensor_tensor(wsc[p], dtt[p], dsv[p], ALU.mult), reads=["dtt" + P, "dsv" + P], writes=["wsc" + P])
            v4 = lambda ap: ap.rearrange("p (h d) -> p h d", h=4)
            S.dve(lambda e, p=p: e.tensor_tensor(v4(xw[p]), v4(xtm[p]), b4(wsc[p], 64), ALU.mult), reads=["xtm" + P, "wsc" + P], writes=["xw" + P])
            S.dve(lambda e, p=p: e.tensor_tensor(v4(xdt[p]), v4(xtm[p]), b4(dtt[p], 64), ALU.mult), reads=["xtm" + P, "dtt" + P], writes=["xdt" + P])
            S.act(lambda e, p=p: e.copy(hstb[p], hst), reads=["hst"], writes=["hstb" + P])
            S.pe(lambda e, js=js, gp=gp, p=p: e.matmul(psY[:, 0:256], xbcT[gp][:, 3, js], hstb[p], start=True, stop=True), reads=[xr(3), "hstb" + P], writes=["psY_o"])
            for h in range(4):
                S.pe(lambda e, h=h, p=p: e.matmul(psY[:, 256 + h * 64:256 + (h + 1) * 64], MT[p][:, h, :], xdt[p][:, h * 64:(h + 1) * 64], start=True, stop=True),
                     reads=["MT" + P, "xdt" + P], writes=["psY_d"])
            S.pe(lambda e, p=p: e.matmul(psN, btm[p], xw[p], start=True, stop=True), reads=["btm" + P, "xw" + P], writes=["psN"])
            S.dve(lambda e, p=p: e.tensor_tensor(v4(hst), v4(hst), b4(cdv[p], 64), ALU.mult), reads=["hst", "cdv" + P], writes=["hst"])
            S.dve(lambda e: e.tensor_tensor(hst, hst, psN, ALU.add), reads=["hst", "psN"], writes=["hst"])
            S.act(lambda e, p=p: e.copy(ydg[p], psY[:, 256:512]), reads=["psY_d"], writes=["ydg"])
            S.dve(lambda e, p=p: e.tensor_tensor(v4(yy[p]), v4(psY[:, 0:256]), b4(eac[p], 64), ALU.mult), reads=["psY_o", "eac" + P], writes=["yy"])
            S.pool(lambda e, p=p: e.tensor_tensor(v4(tmpd[p]), v4(xtm[p]), b4(hpb[:, 8:12], 64), ALU.mult), reads=["xtm" + P, "hpb"], writes=["tmpd"])
            S.dve(lambda e, p=p: e.tensor_tensor(yy[p], yy[p], ydg[p], ALU.add), reads=["yy", "ydg"], writes=["yy"])
            S.dve(lambda e, p=p: e.tensor_tensor(yy[p], yy[p], tmpd[p], ALU.add), reads=["yy", "tmpd"], writes=["yy"])
            S.dve(lambda e, p=p: e.tensor_tensor(yo[p], yy[p], zs[p], ALU.mult), reads=["yy", "zs" + P], writes=["yo" + P])
            S.dma("sp", T["mixo"][tok, 0:256], yo[p], reads=["yo" + P], writes=["mixo_s%d" % tt])

    if int(os.environ.get('MIX_STOP', '9')) <= 1:
        return
    bar()
    A.reset(base_persist)
    attO = A.alloc([NTT, 256], F32)
    nmT = A.alloc([SEQ], BF16)
    w1 = A.alloc([32, 256], BF16)
    S.dma("pool", w1[0:64], T["w1k"].rearrange("d (l h) -> d l h", l=32), writes=["w1k"])
    S.dma("pool", w1[64:128], T["w1v"].rearrange("d (l h) -> d l h", l=32), writes=["w1v"])
    pe_ = A.alloc([32], BF16)
    S.dma("pool", pe_[0:64], T["pek"], writes=["pek"])
    S.dma("pool", pe_[64:128], T["pev"], writes=["pev"])
    w2 = A.alloc([2, 2, 64], BF16)
    S.dma("pool", w2[:, 0], T["w2k"].rearrange("(c p) d -> p c d", p=128), writes=["w2k"])
    S.dma("pool", w2[:, 1], T["w2v"].rearrange("(c p) d -> p c d", p=128), writes=["w2v"])
    cbias = A.alloc([4], F32)
    hsb = A.alloc([4, 256], BF16)
    KcT = A.alloc([256], BF16)
    VcA = A.alloc([2, 129], BF16)
    S.pool(lambda e: e.memset(hsb, 0.0), writes=["hsb"])
    S.pool(lambda e: e.memset(VcA, 0.0), writes=["VcA"])
    S.pool(lambda e: e.memset(KcT, 0.0), writes=["KcT"])
    S.pool(lambda e: e.memset(VcA[:, :, 64:65], 1.0), reads=["VcA"], writes=["VcA"])
    S.dma("pool", VcA[:, :, 65:129], T["ovl"].rearrange("(c p) j -> p c j", p=128), reads=["VcA"], writes=["VcA"])
    for kv in range(2):
        rows = slice(kv * 64, (kv + 1) * 64)
        for hc in range(2):
            idx = kv * 2 + hc
            pb_ = pbig[:, idx, 0:255]
            pbias = pbig[:, 4, idx:idx + 1]
            for l in range(32):
                S.pe(lambda e, rows=rows, hc=hc, l=l, pbias=pbias: e.matmul(pbias, w1[rows, l, hc * 128:(hc + 1) * 128], pe_[rows, l:l + 1], start=(l == 0), stop=(l == 31)),
                     reads=["w1k", "w1v", "pek", "pev"], writes=["pbias%d" % idx])
            S.act(lambda e, idx=idx, pbias=pbias: e.copy(cbias[:, idx:idx + 1], pbias), reads=["pbias%d" % idx], writes=["cbias%d" % idx])
            for l in range(32):
                S.pe(lambda e, rows=rows, hc=hc, l=l, pb_=pb_: e.matmul(pb_, w1[rows, l, hc * 128:(hc + 1) * 128], kcvcT[rows, l:l + 16 * 254 + 1:16], start=(l == 0), stop=(l == 31)),
                     reads=["w1k", "w1v", "kcvcT"], writes=["phid%d" % idx])
            S.act(lambda e, idx=idx, pb_=pb_: e.activation(hsb[:, idx, 0:255], pb_, AF.Silu, bias=cbias[:, idx:idx + 1]), reads=["phid%d" % idx, "cbias%d" % idx, "hsb"], writes=["hsb%d" % idx])
    pk = pbig[0:64, 5, 0:255]
    for hc in range(2):
        S.pe(lambda e, hc=hc: e.matmul(pk, w2[:, 0, hc, :], hsb[:, hc, 0:255], start=(hc == 0), stop=(hc == 1)), reads=["w2k", "hsb0", "hsb1"], writes=["pk"])
    S.act(lambda e: e.copy(KcT[0:64, 0:255], pk), reads=["pk", "KcT"], writes=["KcT"])
    for it in range(2):
        m = 128 if it == 0 else 127
        pv = pbig[0:m, 6 + it, 0:64]
        for hc in range(2):
            S.pe(lambda e, it=it, hc=hc, m=m, pv=pv: e.matmul(pv, hsb[:, 2 + hc, it * 128:it * 128 + m], w2[:, 1, hc, :], start=(hc == 0), stop=(hc == 1)),
                 reads=["w2v", "hsb2", "hsb3"], writes=["pv%d" % it])
        S.act(lambda e, it=it, m=m, pv=pv: e.copy(VcA[0:m, it, 0:64], pv), reads=["pv%d" % it, "VcA"], writes=["VcA"])

    if int(os.environ.get('MIX_STOP', '9')) <= 2:
        return
    bar()
    base3 = A.off
    qu = [A.alloc([4, 512], BF16) for _ in range(2)]
    cmk = [A.alloc([2, 512], BF16) for _ in range(2)]
    PcT = [A.alloc([2, 512], BF16) for _ in range(2)]
    imp = A.alloc([4, 64], F32)
    imp2 = A.alloc([64], F32)
    m8 = A.alloc([16], F32)
    thr = A.alloc([1], F32)
    rr = A.alloc([32], F32)
    nmb = A.alloc([128], BF16)
    S.pool(lambda e: e.memset(nmb, 0.0), writes=["nmb"])
    pnt = pbf(pbig, 7, 8)[:, 0:128]
    def p3(Q):
        qb = Q % 2
        qs = slice(Q * 512, (Q + 1) * 512)
        S.dma("sp", qu[qb][0:64], T["qu_d"][:, :, qs], writes=["qu%d" % qb])
        S.dma("pool", cmk[qb], T["cmask"][:, qs].rearrange("(c p) t -> p c t", p=128), writes=["cmk%d" % qb])
        for h in range(4):
            pb2 = h % 2
            for it in range(2):
                ps_ = pbig[:, pb2 * 2 + it, :]
                S.pe(lambda e, it=it, h=h, qb=qb, ps_=ps_: e.matmul(ps_, KcT[0:64, it * 128:(it + 1) * 128], qu[qb][0:64, h, :], start=True, stop=False),
                     reads=["KcT", "qu%d" % qb], writes=["psc%d_%d" % (pb2, it)])
                S.pe(lambda e, it=it, qb=qb, ps_=ps_: e.matmul(ps_, identb, cmk[qb][:, it, :], start=False, stop=True),
                     reads=["ident", "cmk%d" % qb], writes=["psc%d_%d" % (pb2, it)])
                S.act(lambda e, it=it, pb2=pb2, ps_=ps_: e.activation(PcT[pb2][:, it, :], ps_, AF.Exp, scale=SCALE), reads=["psc%d_%d" % (pb2, it)], writes=["PcT%d_%d" % (pb2, it)])
            for sub in range(4):
                tt = Q * 4 + sub
                po = pbig[:, 4 + (sub % 2), 0:129]
                for it in range(2):
                    S.pe(lambda e, it=it, pb2=pb2, sub=sub, po=po: e.matmul(po, PcT[pb2][:, it, sub * 128:(sub + 1) * 128], VcA[:, it, :], start=(it == 0), stop=(it == 1)),
                         reads=["PcT%d_0" % pb2, "PcT%d_1" % pb2, "VcA"], writes=["po%d" % (sub % 2)])
                pr = ["po%d" % (sub % 2)]
                ri = ((h % 2) * 4 + sub) * 2
                S.dve(lambda e, ri=ri, po=po: e.tensor_scalar(rr[:, ri:ri + 1], po[:, 64:65], 1e-30, None, ALU.add), reads=pr, writes=["rr0_%d" % ri])
                S.dve(lambda e, ri=ri: e.reciprocal(rr[:, ri:ri + 1], rr[:, ri:ri + 1]), reads=["rr0_%d" % ri], writes=["rr0_%d" % ri])
                if h == 0:
                    S.act(lambda e, ri=ri, po=po, sub=sub: e.activation(imp[:, sub, :], po[:, 65:129], AF.Copy, scale=rr[:, ri:ri + 1]), reads=pr + ["rr0_%d" % ri], writes=["imp%d" % sub])
                else:
                    S.dve(lambda e, ri=ri, po=po, sub=sub: e.scalar_tensor_tensor(imp[:, sub, :], po[:, 65:129], rr[:, ri:ri + 1], imp[:, sub, :], ALU.mult, ALU.add),
                          reads=pr + ["rr0_%d" % ri, "imp%d" % sub], writes=["imp%d" % sub])
                S.dve(lambda e, ri=ri, tt=tt, h=h: e.tensor_tensor(rr[:, ri + 1:ri + 2], rr[:, ri:ri + 1], gate[:, tt, h * 3:h * 3 + 1], ALU.mult), reads=["rr0_%d" % ri, "gate"], writes=["rr1_%d" % ri])
                S.act(lambda e, ri=ri, po=po, tt=tt, h=h: e.activation(attO[:, tt, h * 64:(h + 1) * 64], po[:, 0:64], AF.Copy, scale=rr[:, ri + 1:ri + 2]), reads=pr + ["rr1_%d" % ri], writes=["attO%d" % tt])
        for sub in range(4):
            tt = Q * 4 + sub
            ir = ["imp%d" % sub]
            im = imp[:, sub, :]
            S.pool(lambda e, im=im: e.memset(im[:, 0:1], 1e4), reads=ir, writes=ir)
            lo = max(2 * tt - 1, 0)
            S.pool(lambda e, im=im, lo=lo, tt=tt: e.memset(im[0:64, lo:2 * tt + 1], 1e4), reads=ir, writes=ir)
            S.pool(lambda e, im=im, tt=tt: e.memset(im[64:128, 2 * tt:2 * tt + 2], 1e4), reads=ir, writes=ir)
            if 2 * tt + 1 < 64:
                S.pool(lambda e, im=im, tt=tt: e.memset(im[0:64, 2 * tt + 1:64], -1.0), reads=ir, writes=ir)
            if 2 * tt + 2 < 64:
                S.pool(lambda e, im=im, tt=tt: e.memset(im[64:128, 2 * tt + 2:64], -1.0), reads=ir, writes=ir)
            S.dve(lambda e, im=im: e.max(m8[:, 0:8], im), reads=ir, writes=["m8a"])
            S.dve(lambda e, im=im: e.match_replace(imp2, m8[:, 0:8], im, -1e30), reads=ir + ["m8a"], writes=["imp2"])
            S.dve(lambda e: e.max(m8[:, 8:16], imp2), reads=["imp2"], writes=["m8b"])
            S.dve(lambda e: e.tensor_scalar(thr, m8[:, 15:16], 0.0, None, ALU.max), reads=["m8b"], writes=["thr"])
            S.dve(lambda e, im=im: e.tensor_scalar(nmb[:, 64:128], im, thr, NEG, ALU.is_lt, ALU.mult), reads=ir + ["thr", "nmb"], writes=["nmb"])
            S.pe(lambda e: e.transpose(pnt, nmb, identb), reads=["nmb", "ident"], writes=["pnt"])
            S.act(lambda e, tt=tt: e.copy(nmT[64:128, tt * 128:(tt + 1) * 128], pnt[64:128, :]), reads=["pnt"], writes=["nmT%d" % tt])


    Qa = [A.alloc([4, 512], BF16) for _ in range(2)]
    PT = [A.alloc([512], BF16) for _ in range(4)]
    PW = [A.alloc([512], BF16) for _ in range(3)]
    r4 = A.alloc([8], F32)
    oTs = [A.alloc([512], F32) for _ in range(2)]
    pti = 0
    pwi = 0
    def p4(Q):
        nonlocal pti, pwi
        qb = Q % 2
        qs = slice(Q * 512, (Q + 1) * 512)
        S.dma("sp", Qa[qb][0:64], T["qr_d"][:, :, qs], writes=["Qa%d" % qb])
        for h in range(4):
            S.pool(lambda e, qb=qb, h=h, qs=qs: e.tensor_copy(Qa[qb][64:128, h, :], nmT[64:128, qs]), reads=["nmT%d" % (4 * Q + i_) for i_ in range(4)], writes=["Qm%d_%d" % (qb, h)])
        for h in range(4):
            qr_ = ["Qa%d" % qb, "Qm%d_%d" % (qb, h)]
            posel = pbig[:, 6, 0:260].rearrange("p (s c) -> p s c", s=4)
            powin = pbig[:, 7, 0:260].rearrange("p (s c) -> p s c", s=4)
            for kt in range(4 * Q + 4):
                ks_ = slice(kt * 128, (kt + 1) * 128)
                sb_ = kt % 3
                ps_ = pbig[:, sb_, :]
                pres = "pss%d" % sb_
                o = kt - 4 * Q
                if o < 0:
                    S.pe(lambda e, ks_=ks_, qb=qb, h=h, ps_=ps_: e.matmul(ps_, KsA[:, ks_], Qa[qb][:, h, :], start=True, stop=True),
                         reads=["KsA", "KsE"] + qr_, writes=[pres])
                    lo = 0
                else:
                    lo = o * 128
                    S.pe(lambda e, ks_=ks_, qb=qb, h=h, ps_=ps_, lo=lo: e.matmul(ps_[:, lo:lo + 128], KsA[:, ks_], Qa[qb][:, h, lo:lo + 128], start=True, stop=False),
                         reads=["KsA", "KsE"] + qr_, writes=[pres])
                    S.pe(lambda e, ps_=ps_, lo=lo: e.matmul(ps_[:, lo:lo + 128], identb, tricb, start=False, stop=True), reads=["ident", "tric"], writes=[pres])
                    if o < 3:
                        S.pe(lambda e, ks_=ks_, qb=qb, h=h, ps_=ps_, lo=lo: e.matmul(ps_[:, lo + 128:512], KsA[:, ks_], Qa[qb][:, h, lo + 128:512], start=True, stop=True),
                             reads=["KsA", "KsE"] + qr_, writes=[pres])
                pt_ = PT[pti % 4]
                ptres = "PT%d" % (pti % 4)
                pti += 1
                S.act(lambda e, ps_=ps_, pt_=pt_, lo=lo: e.activation(pt_[:, lo:512], ps_[:, lo:512], AF.Exp, scale=SCALE), reads=[pres], writes=[ptres])
                S.pe(lambda e, pt_=pt_, kt=kt, Q=Q, lo=lo: e.matmul(pbig[0:65, 6, lo:512], VsA[:, kt, :], pt_[:, lo:512], start=(kt == 0), stop=(kt == 4 * Q + 3)),
                     reads=[ptres, "VsA"], writes=["posel"], c=0.22)
            ob_ = (Q * 4 + h) % 2
            S.act(lambda e, ob_=ob_: e.copy(oTs[ob_][0:65, :], pbig[0:65, 6, :]), reads=["posel"], writes=["oTs%d" % ob_], c=0.6)
            poselT = pbig[:, 5, 0:260].rearrange("p (s c) -> p s c", s=4)
            for sub in range(4):
                S.pe(lambda e, ob_=ob_, sub=sub, poselT=poselT: e.transpose(poselT[:, sub, :], oTs[ob_][0:65, sub * 128:(sub + 1) * 128], identf[0:65, 0:65]),
                     reads=["oTs%d" % ob_, "identf"], writes=["poselT"])
            S.dve(lambda e: e.memset(pbig[:, 7, 0:260], 0.0), writes=["powin"])
            for r in range(-4, 4):
                kt = 4 * Q + r
                if kt < 0:
                    continue
                ks_ = slice(kt * 128, (kt + 1) * 128)
                s_lo, s_hi = max(r, 0), min(r + 4, 3)
                wsl = pwi % 2
                psw = pbig[:, 3 + wsl, :]
                pwres = "psw%d" % wsl
                pw_ = PW[pwi % 3]
                pwr = "PW%d" % (pwi % 3)
                pwi += 1
                plain = [s_ for s_ in range(s_lo, s_hi + 1) if s_ != r and s_ != r + 4]
                for s_, msk in ((r, tricb), (r + 4, triab)):
                    if s_lo <= s_ <= s_hi:
                        cs = slice(s_ * 128, (s_ + 1) * 128)
                        S.pe(lambda e, ks_=ks_, qb=qb, h=h, cs=cs, psw=psw: e.matmul(psw[:, cs], KwT[0:64, ks_], Qa[qb][0:64, h, cs], start=True, stop=False),
                             reads=["KwT", "Qa%d" % qb], writes=[pwres])
                        S.pe(lambda e, cs=cs, psw=psw, msk=msk: e.matmul(psw[:, cs], identb, msk, start=False, stop=True), reads=["ident", "tric", "tria"], writes=[pwres])
                if plain:
                    cs = slice(plain[0] * 128, (plain[-1] + 1) * 128)
                    S.pe(lambda e, ks_=ks_, qb=qb, h=h, cs=cs, psw=psw: e.matmul(psw[:, cs], KwT[0:64, ks_], Qa[qb][0:64, h, cs], start=True, stop=True),
                         reads=["KwT", "Qa%d" % qb], writes=[pwres], c=0.2)
                ca = slice(s_lo * 128, (s_hi + 1) * 128)
                S.act(lambda e, psw=psw, pw_=pw_, ca=ca: e.activation(pw_[:, ca], psw[:, ca], AF.Exp, scale=SCALE), reads=[pwres], writes=[pwr], c=0.5)
                for s_ in range(s_lo, s_hi + 1):
                    S.pe(lambda e, pw_=pw_, s_=s_, kt=kt, powin=powin: e.matmul(powin[:, s_, :], pw_[:, s_ * 128:(s_ + 1) * 128], VwA[:, kt, :],
                                                                                start=False, stop=False, skip_group_check=True),
                         reads=[pwr, "VwA"], writes=["powin"])
            for sub in range(4):
                tt = 4 * Q + sub
                for br, (po_, pres) in enumerate(((poselT, "poselT"), (powin, "powin"))):
                    S.dve(lambda e, po_=po_, sub=sub, br=br: e.reciprocal(r4[:, sub * 2 + br:sub * 2 + br + 1], po_[:, sub, 64:65]), reads=[pres], writes=["r4_%d_%d" % (sub, br)])
                    S.dve(lambda e, sub=sub, tt=tt, h=h, br=br: e.tensor_tensor(r4[:, sub * 2 + br:sub * 2 + br + 1], r4[:, sub * 2 + br:sub * 2 + br + 1], gate[:, tt, h * 3 + 1 + br:h * 3 + 2 + br], ALU.mult),
                          reads=["r4_%d_%d" % (sub, br), "gate"], writes=["r4_%d_%d" % (sub, br)])
                    S.dve(lambda e, po_=po_, sub=sub, tt=tt, h=h, br=br: e.scalar_tensor_tensor(attO[:, tt, h * 64:(h + 1) * 64], po_[:, sub, 0:64], r4[:, sub * 2 + br:sub * 2 + br + 1],
                                                                                                  attO[:, tt, h * 64:(h + 1) * 64], ALU.mult, ALU.add),
                          reads=[pres, "r4_%d_%d" % (sub, br), "attO%d" % tt], writes=["attO%d" % tt])
        for sub in range(4):
            tt = 4 * Q + sub
            S.dma("sp", T["mixo"][tt * 128:(tt + 1) * 128, 256:512], attO[:, tt, :], reads=["attO%d" % tt])
    for Q in range(8):
        p3(Q)
        if Q >= 1:
            p4(Q - 1)
    p4(7)


def _perm_cols():
    return None


def run_mix(inputs):
    if "mix" not in _CACHE:
        _CACHE["mix"] = build_mix()
    nc = _CACHE["mix"]
    x = inputs["x"]
    w_in = inputs["w_in"][0]
    offs = np.cumsum([0, 1024, 1536, 16, 1024, 256, 256, 256, 256, 256, 256, 48])
    oz, oxbc, odt, oq, okc, ovc, oks, ovs, okw, ovw, ogate = offs[:11]
    conv_w = inputs["conv_w"][0]
    conv_b = inputs["conv_b"][0]
    t = np.arange(SEQ, dtype=np.float32)
    inv = (1.0 / (500000.0 ** (np.arange(0, 16, 2, dtype=np.float32) / np.float32(16)))).astype(np.float32)
    ang = (t[:, None] * inv[None, :]).astype(np.float32)
    rope = np.concatenate([np.cos(ang), np.sin(ang)], 1).astype(np.float32)
    rope = np.ascontiguousarray(rope.reshape(NTT, 128, 16).transpose(1, 0, 2).reshape(128, NTT * 16))
    ident = np.eye(128, dtype=np.float32)
    kk = np.arange(128)[:, None]
    qq = np.arange(128)[None, :]
    tric = np.where(kk <= qq, 0.0, NEG).astype(np.float32)
    tria = np.where(kk > qq, 0.0, NEG).astype(np.float32)
    utri = (kk <= qq).astype(np.float32)
    emat = (np.arange(SEQ)[None, :] // 64 == np.arange(64)[:, None]).astype(np.float32)
    ii = np.arange(256)[:, None]
    cmask = np.where((16 * ii + 31 <= np.arange(SEQ)[None, :]) & (ii < 255), 0.0, NEG).astype(np.float32)
    cs = np.arange(255)[:, None] * 16
    ss_ = np.arange(64)[None, :] * 64
    ov = np.clip(np.minimum(cs + 32, ss_ + 64) - np.maximum(cs, ss_), 0, None) / 32.0
    ovl = np.zeros((256, 64), np.float32)
    ovl[:255] = ov
    in_maps = []
    for c in range(8):
        b, g = c // 4, c % 4
        grp = g // 2
        ar = np.arange
        tm_cols = np.concatenate([oz + 256 * g + ar(256), odt + 4 * g + ar(4), ogate + 12 * g + ar(12), ovs + 64 * g + ar(64), ovw + 64 * g + ar(64),
                                  oq + 256 * g + ar(256), oks + 64 * g + ar(64), okw + 64 * g + ar(64)])
        xcols = np.concatenate([256 * g + ar(256), 1024 + 128 * grp + ar(128), 1280 + 128 * grp + ar(128)])
        fm_cols = np.concatenate([oxbc + xcols, okc + 64 * g + ar(64), ovc + 64 * g + ar(64)])
        convw = np.ascontiguousarray(conv_w[:, xcols].T.reshape(4, 128, 4).transpose(1, 0, 2).reshape(128, 16))
        convb = np.ascontiguousarray(conv_b[xcols].reshape(4, 128).T)
        hp = np.concatenate([inputs["dt_bias"][0][4 * g:4 * g + 4], inputs["a_log"][0][4 * g:4 * g + 4], inputs["d_skip"][0][4 * g:4 * g + 4]]).astype(np.float32)
        in_maps.append(dict(
            xb=np.ascontiguousarray(x[b]), anw=inputs["attn_norm_w"][0], w_tm=np.ascontiguousarray(w_in[:, tm_cols]), w_fm=np.ascontiguousarray(w_in[:, fm_cols]),
            convw=convw, convb=convb, hp=hp, rope=rope,
            w1k=np.ascontiguousarray(inputs["cmp_w1_k"][0].reshape(32, 64, 256).transpose(1, 0, 2).reshape(64, 32 * 256)),
            w1v=np.ascontiguousarray(inputs["cmp_w1_v"][0].reshape(32, 64, 256).transpose(1, 0, 2).reshape(64, 32 * 256)),
            w2k=inputs["cmp_w2_k"][0], w2v=inputs["cmp_w2_v"][0],
            pek=np.ascontiguousarray(inputs["cmp_pe_k"][0].T), pev=np.ascontiguousarray(inputs["cmp_pe_v"][0].T),
            ident=ident, emat=emat, tric=tric, tria=tria, cmask=cmask, ovl=ovl, utri=utri,
        ))
    res = run_bass_kernel_spmd(nc, in_maps, core_ids=list(range(8)))
    mixed = np.empty((2, SEQ, D), np.float32)
    for c in range(8):
        b, g = c // 4, c % 4
        m = res.results[c]["mixo"]
        mixed[b, :, 256 * g:256 * (g + 1)] = m[:, 0:256]
        mixed[b, :, 1024 + 256 * g:1024 + 256 * (g + 1)] = m[:, 256:512]
    return mixed


def kernel(**inputs):
    inputs = {k: np.asarray(v) for k, v in inputs.items()}
    mixed = run_mix(inputs)
    return run_tail(inputs["x"], mixed, inputs["ssd_norm_w"][0], inputs["w_out"][0], inputs["ffn_norm_w"][0],
                    inputs["w_gate"][0], inputs["w_up"][0], inputs["w_down"][0], inputs["final_norm_w"])
```
